# Optimizing a Trainium2 kernel written in Bass

```python
import jax, jax.numpy as jnp
from jax import lax
import numpy as np

D_MODEL = 1024
BATCH = 8
SEQ = 4096
DEPTH = 2

HEAD_DIM = 64
ROT_DIM = HEAD_DIM // 4
ROPE_THETA = 500000.0
NORM_EPS = 1e-6
NEG_INF = -1e30

MOBA_HEADS = 8
MOBA_BLOCK = 256
MOBA_TOPK = 3
MOBA_Q_CHUNK = 32

NSA_HEADS = 8
NSA_KV_HEADS = 2
NSA_GROUP = NSA_HEADS // NSA_KV_HEADS
NSA_CMP_LEN = 32
NSA_CMP_STRIDE = 16
NSA_CMP_HIDDEN = 256
NSA_SEL_BLOCK = 64
NSA_SEL_TOPK = 16
NSA_WINDOW = 512
NSA_Q_CHUNK = 32
NSA_FORCE = 1e4

DIFF_HEADS = 8
DIFF_QK_DIM = HEAD_DIM
DIFF_V_DIM = 2 * HEAD_DIM
DIFF_Q_CHUNK = 128

FFN_HIDDEN = -(-8 * D_MODEL // (3 * 256)) * 256

MOBA_W = MOBA_HEADS * HEAD_DIM
NSA_Q_W = NSA_HEADS * HEAD_DIM
NSA_KV_W = 3 * 2 * NSA_KV_HEADS * HEAD_DIM
NSA_GATE_W = 3 * NSA_HEADS
SPARSE_IN = 3 * MOBA_W + NSA_Q_W + NSA_KV_W + NSA_GATE_W
SPARSE_SPLITS = [MOBA_W, 2 * MOBA_W, 3 * MOBA_W, 3 * MOBA_W + NSA_Q_W, 3 * MOBA_W + NSA_Q_W + NSA_KV_W]
SPARSE_OUT = MOBA_W + NSA_Q_W
DIFF_IN = 2 * DIFF_HEADS * 2 * DIFF_QK_DIM + DIFF_HEADS * DIFF_V_DIM
DIFF_OUT = DIFF_HEADS * DIFF_V_DIM

kernel_name = 'hybrid_moba_nsa_diffattn_adaln_block'


def rms_norm(t, gain):
    tf = t.astype(jnp.float32)
    y = tf * lax.rsqrt(jnp.mean(tf * tf, axis=-1, keepdims=True) + NORM_EPS)
    return (y * gain.astype(jnp.float32)).astype(t.dtype)


def rope_tables(positions):
    inv_freq = 1.0 / (ROPE_THETA ** (jnp.arange(0, ROT_DIM, 2, dtype=jnp.float32) / ROT_DIM))
    ang = positions.astype(jnp.float32)[:, None, :, None] * inv_freq
    return jnp.cos(ang), jnp.sin(ang)


def partial_rope(t, cos, sin):
    half = ROT_DIM // 2
    cos = cos.astype(t.dtype)
    sin = sin.astype(t.dtype)
    t1, t2, rest = t[..., :half], t[..., half:ROT_DIM], t[..., ROT_DIM:]
    return jnp.concatenate([t1 * cos - t2 * sin, t2 * cos + t1 * sin, rest], axis=-1)


def masked_softmax(s, valid):
    s = jnp.where(valid, s.astype(jnp.float32), NEG_INF)
    return jax.nn.softmax(s, axis=-1) * valid


def moba_attention(q, k, v):
    B, H, S, dh = q.shape
    Qc = MOBA_Q_CHUNK
    nb = -(-S // MOBA_BLOCK)
    pad = nb * MOBA_BLOCK - S
    kb = jnp.pad(k, ((0, 0), (0, 0), (0, pad), (0, 0))).reshape(B, H, nb, MOBA_BLOCK, dh)
    vb = jnp.pad(v, ((0, 0), (0, 0), (0, pad), (0, 0))).reshape(B, H, nb, MOBA_BLOCK, dh)
    kmean = jnp.mean(kb.astype(jnp.float32), axis=3)
    topk = min(MOBA_TOPK, nb)
    scale = dh ** -0.5
    bi = jnp.arange(B)[:, None, None, None]
    hi = jnp.arange(H)[None, :, None, None]
    blk = jnp.arange(nb)
    off = jnp.arange(MOBA_BLOCK)

    def one_chunk(ci):
        t0 = ci * Qc
        pos = t0 + jnp.arange(Qc)
        own = t0 // MOBA_BLOCK
        qc = lax.dynamic_slice_in_dim(q, t0, Qc, axis=2)
        gate = jnp.einsum('bhqd,bhnd->bhqn', qc.astype(jnp.float32), kmean)
        gate = jnp.where(blk < own, gate, NEG_INF)
        gval, idx = lax.top_k(gate, topk)
        sel_ok = gval > 0.5 * NEG_INF
        k_sel = kb[bi, hi, idx]
        v_sel = vb[bi, hi, idx]
        k_own = lax.dynamic_index_in_dim(kb, own, axis=2, keepdims=False)
        v_own = lax.dynamic_index_in_dim(vb, own, axis=2, keepdims=False)
        s_sel = jnp.einsum('bhqd,bhqnjd->bhqnj', qc, k_sel).reshape(B, H, Qc, topk * MOBA_BLOCK)
        s_own = jnp.einsum('bhqd,bhjd->bhqj', qc, k_own)
        ok_sel = jnp.broadcast_to(sel_ok[..., None], (B, H, Qc, topk, MOBA_BLOCK)).reshape(B, H, Qc, topk * MOBA_BLOCK)
        ok_own = jnp.broadcast_to((own * MOBA_BLOCK + off)[None, :] <= pos[:, None], (B, H, Qc, MOBA_BLOCK))
        p = masked_softmax(jnp.concatenate([s_sel, s_own], axis=-1) * scale,
                           jnp.concatenate([ok_sel, ok_own], axis=-1)).astype(v.dtype)
        p_sel = p[..., :topk * MOBA_BLOCK].reshape(B, H, Qc, topk, MOBA_BLOCK)
        p_own = p[..., topk * MOBA_BLOCK:]
        return (jnp.einsum('bhqnj,bhqnjd->bhqd', p_sel, v_sel)
                + jnp.einsum('bhqj,bhjd->bhqd', p_own, v_own))

    out = lax.map(one_chunk, jnp.arange(S // Qc))
    return out.transpose(1, 2, 0, 3, 4).reshape(B, H, S, dh)


def nsa_compress(t, pos_emb, w1, w2):
    S = t.shape[2]
    ncmp = (S - NSA_CMP_LEN) // NSA_CMP_STRIDE + 1
    idx = np.arange(ncmp)[:, None] * NSA_CMP_STRIDE + np.arange(NSA_CMP_LEN)[None, :]
    blocks = t[:, :, idx] + pos_emb
    hid = jax.nn.silu(jnp.einsum('bgnld,lde->bgne', blocks, w1))
    return jnp.einsum('bgne,ed->bgnd', hid, w2)


def nsa_overlap(ncmp, nsel):
    cs = np.arange(ncmp) * NSA_CMP_STRIDE
    ss = np.arange(nsel) * NSA_SEL_BLOCK
    ov = np.minimum(cs[:, None] + NSA_CMP_LEN, ss[None, :] + NSA_SEL_BLOCK) - np.maximum(cs[:, None], ss[None, :])
    return jnp.asarray(np.clip(ov, 0, None) / NSA_CMP_LEN, dtype=jnp.float32)


def nsa_attention(q, kc, vc, ks, vs, kw, vw, gates):
    B, H, S, dh = q.shape
    G, R, Qc, SB, W = NSA_KV_HEADS, NSA_GROUP, NSA_Q_CHUNK, NSA_SEL_BLOCK, NSA_WINDOW
    ncmp = kc.shape[2]
    nsel = S // SB
    topk = min(NSA_SEL_TOPK, nsel)
    scale = dh ** -0.5
    cmp_end = jnp.arange(ncmp) * NSA_CMP_STRIDE + NSA_CMP_LEN - 1
    overlap = nsa_overlap(ncmp, nsel)
    ksb = ks.reshape(B, G, nsel, SB, dh)
    vsb = vs.reshape(B, G, nsel, SB, dh)
    kwp = jnp.pad(kw, ((0, 0), (0, 0), (W, 0), (0, 0)))
    vwp = jnp.pad(vw, ((0, 0), (0, 0), (W, 0), (0, 0)))
    bi = jnp.arange(B)[:, None, None, None]
    gi = jnp.arange(G)[None, :, None, None]
    sel_blk = jnp.arange(nsel)
    sel_off = jnp.arange(SB)
    win_off = jnp.arange(W + Qc)

    def one_chunk(ci):
        t0 = ci * Qc
        pos = t0 + jnp.arange(Qc)
        qg = lax.dynamic_slice_in_dim(q, t0, Qc, axis=2).reshape(B, G, R, Qc, dh)
        g = lax.dynamic_slice_in_dim(gates, t0, Qc, axis=2).reshape(B, G, R, Qc, 3)
        ok_c = cmp_end[None, :] <= pos[:, None]
        p_c = masked_softmax(jnp.einsum('bgrqd,bgnd->bgrqn', qg, kc) * scale, ok_c)
        o_c = jnp.einsum('bgrqn,bgnd->bgrqd', p_c.astype(vc.dtype), vc)
        imp = jnp.einsum('bgqn,nm->bgqm', p_c.sum(axis=2), overlap)
        cur = pos // SB
        ok_blk = sel_blk[None, :] <= cur[:, None]
        forced = ok_blk & ((sel_blk[None, :] == 0) | (sel_blk[None, :] >= cur[:, None] - 1))
        score = jnp.where(ok_blk, jnp.where(forced, NSA_FORCE, imp), NEG_INF)
        sval, idx = lax.top_k(score, topk)
        sel_ok = sval > 0.5 * NEG_INF
        k_sel = ksb[bi, gi, idx]
        v_sel = vsb[bi, gi, idx]
        tok = idx[..., None] * SB + sel_off
        ok_s = (sel_ok[..., None] & (tok <= pos[None, None, :, None, None])).reshape(B, G, 1, Qc, topk * SB)
        s_s = jnp.einsum('bgrqd,bgqnjd->bgrqnj', qg, k_sel).reshape(B, G, R, Qc, topk * SB) * scale
        p_s = masked_softmax(s_s, ok_s).astype(vs.dtype).reshape(B, G, R, Qc, topk, SB)
        o_s = jnp.einsum('bgrqnj,bgqnjd->bgrqd', p_s, v_sel)
        k_w = lax.dynamic_slice_in_dim(kwp, t0, W + Qc, axis=2)
        v_w = lax.dynamic_slice_in_dim(vwp, t0, W + Qc, axis=2)
        kpos = t0 - W + win_off
        ok_w = (kpos[None, :] <= pos[:, None]) & (kpos[None, :] > pos[:, None] - W) & (kpos[None, :] >= 0)
        p_w = masked_softmax(jnp.einsum('bgrqd,bgkd->bgrqk', qg, k_w) * scale, ok_w)
        o_w = jnp.einsum('bgrqk,bgkd->bgrqd', p_w.astype(vw.dtype), v_w)
        o = g[..., 0:1] * o_c + g[..., 1:2] * o_s + g[..., 2:3] * o_w
        return o.reshape(B, H, Qc, dh)

    out = lax.map(one_chunk, jnp.arange(S // Qc))
    return out.transpose(1, 2, 0, 3, 4).reshape(B, H, S, dh)


def to_heads(t, n_heads):
    B, S, _ = t.shape
    return t.reshape(B, S, n_heads, HEAD_DIM).transpose(0, 2, 1, 3)


def qk_prep(t, gain, cos, sin):
    return partial_rope(rms_norm(t, gain), cos, sin)


def sparse_mixer(h, cos, sin, w_in, w_out, moba_qn, moba_kn, nsa_qn, nsa_kn, cmp_pos, cmp_w1, cmp_w2):
    B, S, _ = h.shape
    mq, mk, mv, nq, nkv, ng = jnp.split(h @ w_in, SPARSE_SPLITS, axis=-1)
    o_moba = moba_attention(qk_prep(to_heads(mq, MOBA_HEADS), moba_qn, cos, sin),
                            qk_prep(to_heads(mk, MOBA_HEADS), moba_kn, cos, sin),
                            to_heads(mv, MOBA_HEADS))
    kv = nkv.reshape(B, S, 3, 2, NSA_KV_HEADS, HEAD_DIM).transpose(2, 3, 0, 4, 1, 5)
    kc = nsa_compress(qk_prep(kv[0, 0], nsa_kn[0], cos, sin), cmp_pos[0], cmp_w1[0], cmp_w2[0])
    vc = nsa_compress(kv[0, 1], cmp_pos[1], cmp_w1[1], cmp_w2[1])
    gates = jax.nn.sigmoid(ng).reshape(B, S, 3, NSA_HEADS).transpose(0, 3, 1, 2)
    o_nsa = nsa_attention(qk_prep(to_heads(nq, NSA_HEADS), nsa_qn, cos, sin), kc, vc,
                          qk_prep(kv[1, 0], nsa_kn[1], cos, sin), kv[1, 1],
                          qk_prep(kv[2, 0], nsa_kn[2], cos, sin), kv[2, 1], gates)
    o = jnp.concatenate([o_moba, o_nsa], axis=1).transpose(0, 2, 1, 3).reshape(B, S, SPARSE_OUT)
    return o @ w_out


def diff_mixer(h, cos, sin, w_in, w_out, q_norm, k_norm, lam_params, out_norm, lam_init):
    B, S, _ = h.shape
    H, Qc = DIFF_HEADS, DIFF_Q_CHUNK
    qk_w = H * 2 * DIFF_QK_DIM
    q, k, v = jnp.split(h @ w_in, [qk_w, 2 * qk_w], axis=-1)
    cos5, sin5 = cos[:, :, None], sin[:, :, None]
    q = partial_rope(rms_norm(q.reshape(B, S, H, 2, DIFF_QK_DIM).transpose(0, 2, 3, 1, 4), q_norm), cos5, sin5)
    k = partial_rope(rms_norm(k.reshape(B, S, H, 2, DIFF_QK_DIM).transpose(0, 2, 3, 1, 4), k_norm), cos5, sin5)
    v = v.reshape(B, S, H, DIFF_V_DIM).transpose(0, 2, 1, 3)
    lp = lam_params.astype(jnp.float32)
    lam = jnp.exp(jnp.sum(lp[0] * lp[1])) - jnp.exp(jnp.sum(lp[2] * lp[3])) + lam_init
    scale = DIFF_QK_DIM ** -0.5
    key_idx = jnp.arange(S)

    def one_chunk(ci):
        t0 = ci * Qc
        pos = t0 + jnp.arange(Qc)
        qc = lax.dynamic_slice_in_dim(q, t0, Qc, axis=3)
        s = jnp.einsum('bhcqd,bhckd->bhcqk', qc, k) * scale
        p = masked_softmax(s, key_idx[None, :] <= pos[:, None])
        a = p[:, :, 0] - lam * p[:, :, 1]
        return jnp.einsum('bhqk,bhkd->bhqd', a.astype(v.dtype), v)

    o = lax.map(one_chunk, jnp.arange(S // Qc)).transpose(1, 2, 0, 3, 4).reshape(B, H, S, DIFF_V_DIM)
    o = rms_norm(o, out_norm) * (1.0 - lam_init)
    return o.transpose(0, 2, 1, 3).reshape(B, S, DIFF_OUT) @ w_out


def swiglu(h, w_gate, w_up, w_down):
    return (jax.nn.silu(h @ w_gate) * (h @ w_up)) @ w_down


def setup_inputs(seed: int = 0) -> dict:
    key = jax.random.key(seed)
    keys = iter(jax.random.split(key, 40))
    n_even = (DEPTH + 1) // 2
    n_odd = DEPTH // 2

    def nrm(shape, scale):
        return jax.random.normal(next(keys), shape, jnp.float32) * scale

    def gain(shape):
        return 1.0 + nrm(shape, 0.05)

    offsets = jax.random.randint(next(keys), (BATCH, 1), 0, 1024, dtype=jnp.int32)
    positions = offsets + jnp.arange(SEQ, dtype=jnp.int32)[None, :]
    return {
        'x': nrm((BATCH, SEQ, D_MODEL), 1.0),
        'c': nrm((BATCH, D_MODEL), 1.0),
        'positions': positions,
        'ada_w': nrm((DEPTH, D_MODEL, 6 * D_MODEL), 0.5 * D_MODEL ** -0.5),
        'ada_b': nrm((DEPTH, 6 * D_MODEL), 0.02),
        'attn_norm': gain((DEPTH, D_MODEL)),
        'ffn_norm': gain((DEPTH, D_MODEL)),
        'ffn_w_gate': nrm((DEPTH, D_MODEL, FFN_HIDDEN), D_MODEL ** -0.5),
        'ffn_w_up': nrm((DEPTH, D_MODEL, FFN_HIDDEN), D_MODEL ** -0.5),
        'ffn_w_down': nrm((DEPTH, FFN_HIDDEN, D_MODEL), FFN_HIDDEN ** -0.5),
        'sp_w_in': nrm((n_even, D_MODEL, SPARSE_IN), D_MODEL ** -0.5),
        'sp_w_out': nrm((n_even, SPARSE_OUT, D_MODEL), SPARSE_OUT ** -0.5),
        'moba_q_norm': gain((n_even, HEAD_DIM)),
        'moba_k_norm': gain((n_even, HEAD_DIM)),
        'nsa_q_norm': gain((n_even, HEAD_DIM)),
        'nsa_k_norm': gain((n_even, 3, HEAD_DIM)),
        'nsa_cmp_pos': nrm((n_even, 2, NSA_CMP_LEN, HEAD_DIM), 0.1),
        'nsa_cmp_w1': nrm((n_even, 2, NSA_CMP_LEN, HEAD_DIM, NSA_CMP_HIDDEN), (NSA_CMP_LEN * HEAD_DIM) ** -0.5),
        'nsa_cmp_w2': nrm((n_even, 2, NSA_CMP_HIDDEN, HEAD_DIM), NSA_CMP_HIDDEN ** -0.5),
        'diff_w_in': nrm((n_odd, D_MODEL, DIFF_IN), D_MODEL ** -0.5),
        'diff_w_out': nrm((n_odd, DIFF_OUT, D_MODEL), DIFF_OUT ** -0.5),
        'diff_q_norm': gain((n_odd, DIFF_QK_DIM)),
        'diff_k_norm': gain((n_odd, DIFF_QK_DIM)),
        'diff_lambda': nrm((n_odd, 4, DIFF_QK_DIM), 0.1),
        'diff_out_norm': gain((n_odd, DIFF_V_DIM)),
    }


def reference(x, c, positions, ada_w, ada_b, attn_norm, ffn_norm, ffn_w_gate, ffn_w_up, ffn_w_down,
              sp_w_in, sp_w_out, moba_q_norm, moba_k_norm, nsa_q_norm, nsa_k_norm, nsa_cmp_pos,
              nsa_cmp_w1, nsa_cmp_w2, diff_w_in, diff_w_out, diff_q_norm, diff_k_norm, diff_lambda,
              diff_out_norm):
    cos, sin = rope_tables(positions)
    for i in range(DEPTH):
        mod = jax.nn.silu(c) @ ada_w[i] + ada_b[i]
        sh_a, sc_a, g_a, sh_f, sc_f, g_f = jnp.split(mod[:, None, :], 6, axis=-1)
        h = rms_norm(x, attn_norm[i]) * (1.0 + sc_a) + sh_a
        j = i // 2
        if i % 2 == 0:
            y = sparse_mixer(h, cos, sin, sp_w_in[j], sp_w_out[j], moba_q_norm[j], moba_k_norm[j],
                             nsa_q_norm[j], nsa_k_norm[j], nsa_cmp_pos[j], nsa_cmp_w1[j], nsa_cmp_w2[j])
        else:
            lam_init = 0.8 - 0.6 * float(np.exp(-0.3 * i))
            y = diff_mixer(h, cos, sin, diff_w_in[j], diff_w_out[j], diff_q_norm[j], diff_k_norm[j],
                           diff_lambda[j], diff_out_norm[j], lam_init)
        x = x + g_a * y
        h = rms_norm(x, ffn_norm[i]) * (1.0 + sc_f) + sh_f
        x = x + g_f * swiglu(h, ffn_w_gate[i], ffn_w_up[i], ffn_w_down[i])
    return x
```

```python
from contextlib import ExitStack
import numpy as np
import concourse.bass as bass
import concourse.mybir as mybir
from concourse.bass_utils import run_bass_kernel_spmd

F32 = mybir.dt.float32
BF16 = mybir.dt.bfloat16
I32 = mybir.dt.int32
AF = mybir.ActivationFunctionType
ALU = mybir.AluOpType
AX = mybir.AxisListType

S = 4096
D = 1024
NT = S // 128
FFN = 2816
NJ = FFN // 128
EPS = 1e-6
MASKV = -30000.0
SP_IN = 2840
N_DMA_SEMS = 24


class Buf:
    __slots__ = ("w", "r", "name")

    def __init__(self, name=""):
        self.w = None
        self.r = {}
        self.name = name


class Rot:
    def __init__(self, items):
        self.items = list(items)
        self.i = 0

    def next(self):
        it = self.items[self.i]
        self.i = (self.i + 1) % len(self.items)
        return it


class KB:
    def __init__(self, nc):
        self.nc = nc
        self.es = ExitStack()
        self.engs = {"PE": nc.tensor, "ACT": nc.scalar, "DVE": nc.vector, "POOL": nc.gpsimd, "SP": nc.sync}
        self.sem = {}
        self.cnt = {}
        for e in ("PE", "ACT", "DVE", "POOL"):
            self.sem[e] = self.es.enter_context(nc.semaphore("s_" + e))
            self.cnt[e] = 0
        self.dsem = [self.es.enter_context(nc.semaphore("d_%d" % i)) for i in range(N_DMA_SEMS)]
        self.dcnt = [0] * N_DMA_SEMS
        self.dnext = 0
        self.dnext2 = [0, 0]
        self.waited = {e: {} for e in self.engs}
        self.n_inst = 0
        self.uid = 0
        self.limit = None
        self.log = None

    def name(self, p):
        self.uid += 1
        return "%s_%d" % (p, self.uid)

    def _wait(self, E, key, c, raw=False):
        if key == E and E == "PE":
            return
        w = self.waited[E]
        if w.get(key, 0) >= c:
            return
        w[key] = c
        if isinstance(key, int):
            self.engs[E].wait_ge(self.dsem[key], c)
        else:
            self.engs[E].wait_ge(self.sem[key], c)

    def _deps(self, E, reads, writes):
        for b in reads:
            if b.w is not None:
                self._wait(E, b.w[0], b.w[1], raw=True)
        for b in writes:
            if b.w is not None:
                self._wait(E, b.w[0], b.w[1])
            for k, c in b.r.items():
                self._wait(E, k, c)

    def _mark(self, tok, reads, writes):
        for b in reads:
            if b.r.get(tok[0], 0) < tok[1]:
                b.r[tok[0]] = tok[1]
        for b in writes:
            b.w = tok
            b.r = {}

    def op(self, E, fn, reads=(), writes=()):
        if self.limit is not None and self.n_inst >= self.limit:
            return None
        self._deps(E, reads, writes)
        if self.log is not None:
            import sys as _sys
            self.log.append((self.n_inst, E, _sys._getframe(1).f_lineno))
        ins = fn(self.engs[E])
        self.cnt[E] += 1
        ins.then_inc(self.sem[E], 1)
        tok = (E, self.cnt[E])
        self._mark(tok, reads, writes)
        self.n_inst += 1
        return tok

    def dma(self, Q, out_ap, in_ap, reads=(), writes=(), **kw):
        if self.limit is not None and self.n_inst >= self.limit:
            return None
        self._deps(Q, reads, writes)
        half = N_DMA_SEMS // 2
        qi = 0 if Q == "SP" else 1
        k = qi * half + self.dnext2[qi]
        self.dnext2[qi] = (self.dnext2[qi] + 1) % half
        if self.dcnt[k] > 0:
            self._wait(Q, k, self.dcnt[k])
        if self.log is not None:
            import sys as _sys
            self.log.append((self.n_inst, "DMA-" + Q, _sys._getframe(1).f_lineno))
        ins = self.engs[Q].dma_start(out=out_ap, in_=in_ap, **kw)
        self.dcnt[k] += 16
        ins.then_inc(self.dsem[k], 16)
        tok = (k, self.dcnt[k])
        self._mark(tok, reads, writes)
        self.n_inst += 1
        return tok

    def barrier(self, engines=("PE", "ACT", "DVE", "POOL", "SP")):
        for E in engines:
            for e2 in ("PE", "ACT", "DVE", "POOL"):
                if self.cnt[e2] > 0:
                    self._wait(E, e2, self.cnt[e2])
            for k in range(N_DMA_SEMS):
                if self.dcnt[k] > 0:
                    self._wait(E, k, self.dcnt[k])


class Scope:
    def __init__(self, kb):
        self.kb = kb
        self.es = ExitStack()

    def sb(self, name, shape, dt):
        t = self.es.enter_context(self.kb.nc.sbuf_tensor(self.kb.name(name), list(shape), dt))
        return t, Buf(name)

    def ps(self, name, shape, dt):
        t = self.es.enter_context(self.kb.nc.psum_tensor(self.kb.name(name), list(shape), dt))
        return t, Buf(name)

    def close(self):
        self.kb.barrier()
        self.es.close()


def bc_row(ap_row, n):
    return ap_row.to_broadcast([128, n])


class Prog:
    def __init__(self, cfg):
        self.cfg = cfg
        self.nc = bass.Bass("TRN2", target_bir_lowering=False)
        self.kb = KB(self.nc)
        self.kb.limit = cfg.get("limit")
        self.I = {}
        self.taps = {}

    def din(self, name, shape, dt=F32):
        self.I[name] = self.nc.dram_tensor(name, list(shape), dt, kind="ExternalInput").ap()
        return self.I[name]

    def dscr(self, name, shape, dt):
        if self.cfg.get("debug"):
            return self.nc.dram_tensor(name, list(shape), dt, kind="ExternalOutput").ap()
        return self.nc.dram_tensor(name, list(shape), dt).ap()

    def tap(self, name, shape, dt=F32):
        self.taps[name] = self.nc.dram_tensor(name, list(shape), dt, kind="ExternalOutput").ap()
        return self.taps[name]

    def declare(self):
        d = self.din
        d("x", [S, D]); d("cT", [128, 8]); d("posT", [128, NT], I32)
        d("ada_w", [2, D, 6 * D]); d("ada_b", [2, 6 * D])
        d("attn_norm", [2, D]); d("ffn_norm", [2, D])
        d("ffn_w_gate", [2, D, FFN]); d("ffn_w_up", [2, D, FFN]); d("ffn_w_down", [2, FFN, D])
        d("sp_w_in", [D, SP_IN]); d("sp_w_out", [D, D])
        d("moba_q_norm", [1, 64]); d("moba_k_norm", [1, 64]); d("nsa_q_norm", [1, 64]); d("nsa_k_norm", [3, 64])
        d("cmp_posT", [64, 2, 32]); d("cmp_w1", [64, 2, 32, 256]); d("cmp_w2", [2, 256, 64])
        d("diff_w_in", [D, 3072]); d("diff_w_out", [D, D])
        d("diff_q_norm", [1, 64]); d("diff_k_norm", [1, 64]); d("diff_lambda", [1, 256]); d("diff_out_norm", [1, 128])
        d("c_ident", [128, 128]); d("c_tri", [128, 128]); d("c_atri", [128, 128])
        d("c_e16", [16, S]); d("c_e64", [64, S]); d("c_ov", [256, 65])
        d("c_t1", [1, 256]); d("c_t2", [1, 256])
        d("c_nfv", [128, NT, 64]); d("c_cst", [128, NT, 64]); d("c_v01", [128, NT, 64])
        self.out = self.nc.dram_tensor("out", [S, D], F32, kind="ExternalOutput").ap()
        self.xmid = self.dscr("xmid", [S, D], F32)
        self.x1s = self.dscr("x1s", [S, D], F32)
        self.FT = self.dscr("FT", [2048, S], BF16)
        self.TM = self.dscr("TM", [S, D], BF16)
        self.OS = self.dscr("OS", [S, D], BF16)
        self.H2T = self.dscr("H2T", [D, S], BF16)

    def setup(self):
        kb, I = self.kb, self.I
        self.G = Scope(kb)
        G = self.G
        self.ident, self.Bident = G.sb("ident", [128, 128], BF16)
        kb.dma("POOL", self.ident[:], I["c_ident"][:, :], writes=[self.Bident])
        self.tri, self.Btri = G.sb("tri", [128, 128], BF16)
        kb.dma("POOL", self.tri[:], I["c_tri"][:, :], writes=[self.Btri])
        self.atri, self.Batri = G.sb("atri", [128, 128], BF16)
        kb.dma("POOL", self.atri[:], I["c_atri"][:, :], writes=[self.Batri])
        self.cosT, self.Bcos = G.sb("cosT", [128, NT, 8], F32)
        self.sinT, self.Bsin = G.sb("sinT", [128, NT, 8], F32)
        self.mod, self.Bmod = G.sb("mod", [128, 6, D], F32)
        self.silc, self.Bsilc = G.sb("silc", [128, 8, 128], F32)
        with_scope = Scope(kb)
        T = with_scope
        pi, Bpi = T.sb("pi", [128, NT], I32)
        pf, Bpf = T.sb("pf", [128, NT], F32)
        ang, Bang = T.sb("ang", [128, NT, 8], F32)
        tmp, Btmp = T.sb("tmp", [128, NT, 8], F32)
        tmp2, Btmp2 = T.sb("tmp2", [128, NT, 8], F32)
        kb.dma("SP", pi[:], I["posT"][:, :], writes=[Bpi])
        kb.op("DVE", lambda e: e.tensor_copy(pf[:], pi[:]), reads=[Bpi], writes=[Bpf])
        inv = (1.0 / (np.float32(500000.0) ** (np.arange(0, 16, 2, dtype=np.float32) / np.float32(16)))).astype(np.float32)
        for j in range(8):
            kb.op("DVE", lambda e, j=j: e.tensor_scalar(ang[:, :, j], pf[:], float(inv[j]), None, ALU.mult),
                  reads=[Bpf], writes=[Bang])
        TWO_PI = float(2 * np.pi)
        MAG = 12582912.0

        def sin_of(dst, Bdst, shift):
            if shift != 0.0:
                kb.op("DVE", lambda e: e.tensor_scalar(tmp2[:], ang[:], shift, None, ALU.add), reads=[Bang], writes=[Btmp2])
                src, Bsrc = tmp2, Btmp2
            else:
                src, Bsrc = ang, Bang
            kb.op("DVE", lambda e: e.tensor_scalar(tmp[:], src[:], 1.0 / TWO_PI, MAG, ALU.mult, ALU.add), reads=[Bsrc], writes=[Btmp])
            kb.op("DVE", lambda e: e.tensor_scalar(tmp[:], tmp[:], -MAG, -TWO_PI, ALU.add, ALU.mult), reads=[Btmp], writes=[Btmp])
            kb.op("DVE", lambda e: e.tensor_tensor(tmp[:], src[:], tmp[:], ALU.add), reads=[Bsrc, Btmp], writes=[Btmp])
            kb.op("DVE", lambda e: e.tensor_scalar(tmp[:], tmp[:], float(np.pi), float(-np.pi), ALU.min, ALU.max), reads=[Btmp], writes=[Btmp])
            kb.op("ACT", lambda e: e.activation(dst[:], tmp[:], AF.Sin), reads=[Btmp], writes=[Bdst])

        sin_of(self.sinT, self.Bsin, 0.0)
        sin_of(self.cosT, self.Bcos, float(np.pi / 2))
        ct, Bct = T.sb("ct", [128, 8], F32)
        kb.dma("SP", ct[:], I["cT"][:, :], writes=[Bct])
        kb.op("ACT", lambda e: e.activation(ct[:], ct[:], AF.Silu), reads=[Bct], writes=[Bct])
        kb.op("DVE", lambda e: e.tensor_copy(self.silc[:], ct[:].unsqueeze(2).to_broadcast([128, 8, 128])),
              reads=[Bct], writes=[self.Bsilc])
        T.close()

    def compute_mod(self, li):
        kb, I = self.kb, self.I
        T = Scope(kb)
        wch = [T.sb("adaw", [128, 8, 512], F32) for _ in range(2)]
        bch = [T.sb("adab", [128, 512], F32) for _ in range(2)]
        pm = [T.ps("pmod", [128, 512], F32) for _ in range(2)]
        nrm, Bnrm = T.sb("nrm", [128, 2, D], F32)
        kb.dma("SP", nrm[:, 0, :], bc_row(I["attn_norm"][li:li + 1, :], D), writes=[Bnrm])
        kb.dma("SP", nrm[:, 1, :], bc_row(I["ffn_norm"][li:li + 1, :], D), writes=[Bnrm])
        wv = I["ada_w"][li].rearrange("(kc p) n -> p kc n", p=128)
        for g in range(12):
            w, Bw = wch[g % 2]
            b, Bb = bch[g % 2]
            p, Bp = pm[g % 2]
            kb.dma("SP", w[:], wv[:, :, g * 512:(g + 1) * 512], writes=[Bw])
            kb.dma("SP", b[:], bc_row(I["ada_b"][li:li + 1, g * 512:(g + 1) * 512], 512), writes=[Bb])
            for kc in range(8):
                kb.op("PE", lambda e, kc=kc: e.matmul(p[:], self.silc[:, kc, :], w[:, kc, :], start=(kc == 0), stop=(kc == 7)),
                      reads=[Bw, self.Bsilc], writes=[Bp])
            sl = self.mod[:, g // 2, (g % 2) * 512:(g % 2 + 1) * 512]
            kb.op("DVE", lambda e: e.tensor_tensor(sl, p[:], b[:], ALU.add), reads=[Bp, Bb], writes=[self.Bmod])
        kb.op("DVE", lambda e: e.scalar_tensor_tensor(self.mod[:, 1, :], self.mod[:, 1, :], 1.0, nrm[:, 0, :], ALU.add, ALU.mult),
              reads=[self.Bmod, Bnrm], writes=[self.Bmod])
        kb.op("DVE", lambda e: e.scalar_tensor_tensor(self.mod[:, 4, :], self.mod[:, 4, :], 1.0, nrm[:, 1, :], ALU.add, ALU.mult),
              reads=[self.Bmod, Bnrm], writes=[self.Bmod])
        T.close()

    def rms_mod_tile(self, T, xt, Bxt, hb, Bhb, gi, si, scr):
        kb = self.kb
        junk, Bjunk, ss, Bss, hn, Bhn = scr
        kb.op("ACT", lambda e: e.activation(junk[:], xt[:], AF.Square, accum_out=ss[:]), reads=[Bxt], writes=[Bjunk, Bss])
        kb.op("ACT", lambda e: e.activation(ss[:], ss[:], AF.Sqrt, bias=self.epsb[:], scale=1.0 / D), reads=[Bss, self.Bepsb], writes=[Bss])
        kb.op("DVE", lambda e: e.reciprocal(ss[:], ss[:]), reads=[Bss], writes=[Bss])
        kb.op("DVE", lambda e: e.scalar_tensor_tensor(hn[:], xt[:], ss[:, 0:1], self.mod[:, gi, :], ALU.mult, ALU.mult),
              reads=[Bxt, Bss, self.Bmod], writes=[Bhn])
        kb.op("POOL", lambda e: e.tensor_tensor(hb[:], hn[:], self.mod[:, si, :], ALU.add), reads=[Bhn, self.Bmod], writes=[Bhb])

    def transpose8(self, src, Bsrc, pt, Bpt, dst_ap, Bdst, eng="ACT"):
        kb = self.kb
        for j in range(8):
            kb.op("PE", lambda e, j=j: e.transpose(pt[:, j * 128:(j + 1) * 128], src[:, j * 128:(j + 1) * 128], self.ident[:]),
                  reads=[Bsrc, self.Bident], writes=[Bpt])
        if eng == "ACT":
            kb.op("ACT", lambda e: e.copy(dst_ap, pt[:].rearrange("p (j t) -> p j t", j=8)), reads=[Bpt], writes=[Bdst])
        else:
            kb.op("DVE", lambda e: e.tensor_copy(dst_ap, pt[:].rearrange("p (j t) -> p j t", j=8)), reads=[Bpt], writes=[Bdst])

    def load_w_bf16(self, dst, Bdst, w_ap, nk):
        for kc in range(nk):
            self.kb.dma("POOL", dst[:, kc, :], w_ap[kc * 128:(kc + 1) * 128, :], writes=[Bdst])

    def phase_a(self, li, x_src):
        kb, I = self.kb, self.I
        T = Scope(kb)
        if li == 1:
            w_ap, ncols = I["diff_w_in"], 3072
            groups = []
            for g in range(6):
                if g < 4:
                    gi = 0 if g < 2 else 1
                    blocks = [("T", g * 512 + b * 128) for b in range(4)]
                    groups.append(dict(c0=g * 512, w=512, qk=[(0, 8)], gain=gi, blocks=blocks, gate=None))
                else:
                    blocks = [("V", (g - 4) * 512 + b * 128) for b in range(4)]
                    groups.append(dict(c0=g * 512, w=512, qk=[], gain=None, blocks=blocks, gate=None))
            gain_srcs = [[(I["diff_q_norm"][0:1, :], 0, 8)], [(I["diff_k_norm"][0:1, :], 0, 8)]]
        else:
            w_ap, ncols = I["sp_w_in"], SP_IN
            kn = I["nsa_k_norm"]
            groups = [
                dict(c0=0, w=512, qk=[(0, 8)], gain=0, blocks=[("T", b * 128) for b in range(4)], gate=None),
                dict(c0=512, w=512, qk=[(0, 8)], gain=1, blocks=[("T", 512 + b * 128) for b in range(4)], gate=None),
                dict(c0=1024, w=512, qk=[], gain=None, blocks=[("V", b * 128) for b in range(4)], gate=None),
                dict(c0=1536, w=512, qk=[(0, 8)], gain=2, blocks=[("T", 1024 + b * 128) for b in range(4)], gate=None),
                dict(c0=2048, w=512, qk=[(0, 2), (4, 6)], gain=3,
                     blocks=[("T", 1536), ("T", 1664), ("T", 1792), ("V", 512)], gate=None),
                dict(c0=2560, w=280, qk=[(0, 2)], gain=4, blocks=[("T", 1920), ("V", 640)], gate=(256, 24)),
            ]
            gain_srcs = [
                [(I["moba_q_norm"][0:1, :], 0, 8)], [(I["moba_k_norm"][0:1, :], 0, 8)], [(I["nsa_q_norm"][0:1, :], 0, 8)],
                [(kn[0:1, :], 0, 2), (kn[1:2, :], 4, 6)], [(kn[2:3, :], 0, 2)],
            ]
        nk = 8
        W, BW = T.sb("w_in", [128, nk, ncols], BF16)
        self.load_w_bf16(W, BW, w_ap, nk)
        gains = []
        for gs in gain_srcs:
            gr, Bgr = T.sb("gainrow", [128, 8, 64], F32)
            kb.op("POOL", lambda e: e.memset(gr[:], 1.0), writes=[Bgr])
            for (src, u0, u1) in gs:
                for u in range(u0, u1):
                    kb.dma("SP", gr[:, u, :], bc_row(src, 64), writes=[Bgr])
            gains.append((gr, Bgr))
        xts = Rot([T.sb("xt", [128, D], F32) for _ in range(2)])
        hbs = Rot([T.sb("hb", [128, D], BF16) for _ in range(2)])
        hTs = Rot([T.sb("hT", [128, 8, 128], BF16) for _ in range(2)])
        junk, Bjunk = T.sb("junk", [128, D], BF16)
        sss = Rot([T.sb("ss", [128, 1], F32) for _ in range(2)])
        hn, Bhn = T.sb("hn", [128, D], F32)
        sq, Bsq = T.sb("sq", [128, 512], F32)
        ss8s = Rot([T.sb("ss8", [128, 8], F32) for _ in range(2)])
        qns = Rot([T.sb("qn", [128, 8, 64], F32) for _ in range(2)])
        posts = Rot([T.sb("post", [128, 512], BF16) for _ in range(3)])
        rts = Rot([T.sb("rt", [128, 4, 8, 8], F32) for _ in range(2)])
        pts = Rot([T.ps("ptA", [128, 1024], BF16) for _ in range(2)])
        pgs = Rot([T.ps("pgA", [128, 512], F32) for _ in range(3)])
        ptb = Rot([T.ps("ptB", [128, 1024], BF16) for _ in range(2)])
        stages = {}
        for gi_, g in enumerate(groups):
            if any(k == "T" for k, _ in g["blocks"]):
                stages[gi_] = [T.sb("stage", [128, 4, 512], BF16) for _ in range(2)]
        def load_x(t):
            xt, Bxt = xts.next()
            kb.dma("SP", xt[:], x_src[t * 128:(t + 1) * 128, :], writes=[Bxt])
            return xt, Bxt

        nxt = load_x(0)
        for t in range(NT):
            xt, Bxt = nxt
            if t + 1 < NT:
                nxt = load_x(t + 1)
            hb, Bhb = hbs.next()
            ss, Bss = sss.next()
            self.rms_mod_tile(T, xt, Bxt, hb, Bhb, 1, 0, (junk, Bjunk, ss, Bss, hn, Bhn))
            hT, BhT = hTs.next()
            pt, Bpt = pts.next()
            self.transpose8(hb, Bhb, pt, Bpt, hT[:], BhT, eng="ACT")
            tt = t % 4
            half = (t // 4) % 2
            for gi_, g in enumerate(groups):
                w = g["w"]
                pg, Bpg = pgs.next()
                for kc in range(8):
                    kb.op("PE", lambda e, kc=kc: e.matmul(pg[:, 0:w], hT[:, kc, :], W[:, kc, g["c0"]:g["c0"] + w],
                                                          start=(kc == 0), stop=(kc == 7)),
                          reads=[BhT, BW], writes=[Bpg])
                post, Bpost = posts.next()
                wq = (w // 64) * 64
                nu = wq // 64
                if g["qk"]:
                    ss8, Bss8 = ss8s.next()
                    qn, Bqn = qns.next()
                    gr, Bgr = gains[g["gain"]]
                    kb.op("ACT", lambda e: e.activation(sq[:, 0:wq], pg[:, 0:wq], AF.Square), reads=[Bpg], writes=[Bsq])
                    kb.op("DVE", lambda e: e.tensor_reduce(ss8[:, 0:nu], sq[:, 0:wq].rearrange("p (u d) -> p u d", d=64), AX.X, ALU.add),
                          reads=[Bsq], writes=[Bss8])
                    kb.op("ACT", lambda e: e.activation(ss8[:, 0:nu], ss8[:, 0:nu], AF.Sqrt, bias=self.epsb[:], scale=1.0 / 64),
                          reads=[Bss8, self.Bepsb], writes=[Bss8])
                    kb.op("DVE", lambda e: e.reciprocal(ss8[:, 0:nu], ss8[:, 0:nu]), reads=[Bss8], writes=[Bss8])
                    qk_units = set()
                    for (u0, u1) in g["qk"]:
                        qk_units.update(range(u0, u1))
                    raw_units = [u for u in range(nu) if u not in qk_units]
                    for u in raw_units:
                        kb.op("DVE", lambda e, u=u: e.memset(ss8[:, u:u + 1], 1.0), writes=[Bss8])
                    kb.op("DVE", lambda e: e.tensor_tensor(qn[:, 0:nu, :], pg[:, 0:wq].rearrange("p (u d) -> p u d", d=64),
                                                           ss8[:, 0:nu].unsqueeze(2).to_broadcast([128, nu, 64]), ALU.mult),
                          reads=[Bpg, Bss8], writes=[Bqn])
                    kb.op("DVE", lambda e: e.tensor_tensor(qn[:, 0:nu, :], qn[:, 0:nu, :], gr[:, 0:nu, :], ALU.mult),
                          reads=[Bqn, Bgr], writes=[Bqn])
                    kb.op("ACT", lambda e: e.copy(post[:, 0:wq], qn[:, 0:nu, :].rearrange("p u d -> p (u d)")), reads=[Bqn], writes=[Bpost])
                    postv = post[:, 0:wq].rearrange("p (u d) -> p u d", d=64)
                    rt, Brt = rts.next()
                    for (u0, u1) in g["qk"]:
                        n_ = u1 - u0
                        cosb = self.cosT[:, t, :].unsqueeze(1).to_broadcast([128, n_, 8])
                        sinb = self.sinT[:, t, :].unsqueeze(1).to_broadcast([128, n_, 8])
                        t1 = qn[:, u0:u1, 0:8]
                        t2 = qn[:, u0:u1, 8:16]
                        kb.op("DVE", lambda e: e.tensor_tensor(rt[:, 0, 0:n_, :], t1, cosb, ALU.mult), reads=[Bqn, self.Bcos], writes=[Brt])
                        kb.op("DVE", lambda e: e.tensor_tensor(rt[:, 1, 0:n_, :], t2, sinb, ALU.mult), reads=[Bqn, self.Bsin], writes=[Brt])
                        kb.op("DVE", lambda e: e.tensor_tensor(rt[:, 2, 0:n_, :], t2, cosb, ALU.mult), reads=[Bqn, self.Bcos], writes=[Brt])
                        kb.op("DVE", lambda e: e.tensor_tensor(rt[:, 3, 0:n_, :], t1, sinb, ALU.mult), reads=[Bqn, self.Bsin], writes=[Brt])
                        kb.op("DVE", lambda e: e.tensor_tensor(postv[:, u0:u1, 0:8], rt[:, 0, 0:n_, :], rt[:, 1, 0:n_, :], ALU.subtract),
                              reads=[Brt], writes=[Bpost])
                        kb.op("DVE", lambda e: e.tensor_tensor(postv[:, u0:u1, 8:16], rt[:, 2, 0:n_, :], rt[:, 3, 0:n_, :], ALU.add),
                              reads=[Brt], writes=[Bpost])
                else:
                    kb.op("ACT", lambda e: e.copy(post[:, 0:wq], pg[:, 0:wq]), reads=[Bpg], writes=[Bpost])
                if g["gate"] is not None:
                    gc0, gw = g["gate"]
                    kb.op("ACT", lambda e: e.activation(self.gate_sb[:, t, :], pg[:, gc0:gc0 + gw], AF.Sigmoid),
                          reads=[Bpg], writes=[self.Bgate])
                tblocks = [(bi, dst) for bi, (k, dst) in enumerate(g["blocks"]) if k == "T"]
                if tblocks:
                    pb, Bpb = ptb.next()
                    st, Bst = stages[gi_][half]
                    for bi, dst in tblocks:
                        kb.op("PE", lambda e, bi=bi: e.transpose(pb[:, bi * 128:(bi + 1) * 128], post[:, bi * 128:(bi + 1) * 128], self.ident[:]),
                              reads=[Bpost, self.Bident], writes=[Bpb])
                    b0, b1 = tblocks[0][0], tblocks[-1][0] + 1
                    kb.op("ACT", lambda e: e.copy(st[:, b0:b1, tt * 128:(tt + 1) * 128],
                                                  pb[:, b0 * 128:b1 * 128].rearrange("p (b t) -> p b t", t=128)),
                          reads=[Bpb], writes=[Bst])
                    if tt == 3:
                        t0 = (t - 3) * 128
                        for bi, dst in tblocks:
                            kb.dma("POOL", self.FT[dst:dst + 128, t0:t0 + 512], st[:, bi, :], reads=[Bst])
                for bi, (k, dst) in enumerate(g["blocks"]):
                    if k == "V":
                        kb.dma("POOL", self.TM[t * 128:(t + 1) * 128, dst:dst + 128], post[:, bi * 128:(bi + 1) * 128], reads=[Bpost])
        T.close()

    def attn_qtile(self, R, q_rhs, Bq, k_lhsT, Bk, v_rhs, Bv, vw, pairs, O, BO, o_off, post_exp=None, bank_of=lambda s: 0):
        kb = self.kb
        first = {}
        last = {}
        started = set()
        for kc, subs in pairs:
            for s, kind in subs:
                first.setdefault(s, kc)
                last[s] = kc
        for kc, subs in pairs:
            pS, BpS = R["pS"].next()
            kl, krows = k_lhsT(kc)
            kb.op("PE", lambda e: e.matmul(pS[0:krows, :], kl, q_rhs, start=True, stop=True), reads=[Bk, Bq], writes=[BpS])
            pT, BpT = R["pT"].next()
            kb.op("ACT", lambda e: e.activation(pT[0:krows, :], pS[0:krows, :], AF.Exp, scale=0.125), reads=[BpS], writes=[BpT])
            if post_exp is not None:
                post_exp(kc, pT, BpT, krows)
            for s, kind in subs:
                if kind == "tri":
                    kb.op("DVE", lambda e, s=s: e.tensor_tensor(pT[:, s * 128:(s + 1) * 128], pT[:, s * 128:(s + 1) * 128], self.tri[:], ALU.mult),
                          reads=[BpT, self.Btri], writes=[BpT])
                elif kind == "atri":
                    kb.op("DVE", lambda e, s=s: e.tensor_tensor(pT[:, s * 128:(s + 1) * 128], pT[:, s * 128:(s + 1) * 128], self.atri[:], ALU.mult),
                          reads=[BpT, self.Batri], writes=[BpT])
            for s, kind in subs:
                bk = bank_of(s)
                st_flag = bk not in started
                started.add(bk)
                kb.op("PE", lambda e, s=s, st_flag=st_flag: e.matmul(o_off(O, s), pT[0:krows, s * 128:(s + 1) * 128], v_rhs(kc)[0:krows, :],
                                                                     start=st_flag, stop=(last[s] == kc), skip_group_check=True),
                      reads=[BpT, Bv], writes=[BO])

    @staticmethod
    def causal_pairs(qt):
        pairs = []
        for kc in range(4 * qt + 4):
            j = kc - 4 * qt
            if j < 0:
                pairs.append((kc, [(s, "full") for s in range(4)]))
            else:
                pairs.append((kc, [(s, "tri" if s == j else "full") for s in range(j, 4)]))
        return pairs

    def phase_b_diff(self, li_odd_index):
        kb, I = self.kb, self.I
        T = Scope(kb)
        lam_init = 0.8 - 0.6 * float(np.exp(-0.3 * 1))
        lp, Blp = T.sb("lp", [128, 4, 64], F32)
        kb.dma("SP", lp[:].rearrange("p a d -> p (a d)"), bc_row(I["diff_lambda"][0:1, :], 256), writes=[Blp])
        l2, Bl2 = T.sb("l2", [128, 2, 64], F32)
        kb.op("DVE", lambda e: e.tensor_tensor(l2[:, 0, :], lp[:, 0, :], lp[:, 1, :], ALU.mult), reads=[Blp], writes=[Bl2])
        kb.op("DVE", lambda e: e.tensor_tensor(l2[:, 1, :], lp[:, 2, :], lp[:, 3, :], ALU.mult), reads=[Blp], writes=[Bl2])
        ls, Bls = T.sb("ls", [128, 2], F32)
        kb.op("DVE", lambda e: e.tensor_reduce(ls[:], l2[:], AX.X, ALU.add), reads=[Bl2], writes=[Bls])
        kb.op("ACT", lambda e: e.activation(ls[:], ls[:], AF.Exp), reads=[Bls], writes=[Bls])
        nlam, Bnlam = T.sb("nlam", [128, 1], F32)
        kb.op("DVE", lambda e: e.tensor_tensor(nlam[:], ls[:, 1:2], ls[:, 0:1], ALU.subtract), reads=[Bls], writes=[Bnlam])
        kb.op("DVE", lambda e: e.tensor_scalar(nlam[:], nlam[:], -lam_init, None, ALU.add), reads=[Bnlam], writes=[Bnlam])
        go, Bgo = T.sb("go", [128, 128], F32)
        kb.dma("SP", go[:], bc_row(I["diff_out_norm"][0:1, :], 128), writes=[Bgo])
        kb.op("DVE", lambda e: e.tensor_scalar(go[:], go[:], 1.0 - lam_init, None, ALU.mult), reads=[Bgo], writes=[Bgo])

        KTs = Rot([T.sb("KT", [128, S], BF16) for _ in range(2)])
        QTs = Rot([T.sb("QT", [128, S], BF16) for _ in range(2)])
        Vs = Rot([T.sb("V", [128, NT, 129], BF16) for _ in range(2)])
        for (v, Bv) in Vs.items:
            kb.op("POOL", lambda e, v=v: e.memset(v[:, :, 128:129], 1.0), writes=[Bv])
        R = dict(pS=Rot([T.ps("pS", [128, 512], F32) for _ in range(2)]),
                 pT=Rot([T.sb("pT", [128, 512], BF16) for _ in range(4)]))
        Os = [[T.ps("O", [128, 512], F32) for _ in range(2)] for _ in range(2)]
        osts = Rot([T.sb("ost", [128, 4, 128], BF16) for _ in range(2)])
        a0s = Rot([T.sb("a0", [128, 128], F32) for _ in range(2)])
        a1s = Rot([T.sb("a1", [128, 128], F32) for _ in range(2)])
        rds = Rot([T.sb("rd", [128, 4], F32) for _ in range(4)])
        junk, Bjunk = T.sb("junkb", [128, 128], BF16)

        def load_head(h):
            KT, BKT = KTs.next()
            QT, BQT = QTs.next()
            V, BV = Vs.next()
            kb.dma("SP", QT[:], self.FT[h * 128:(h + 1) * 128, :], writes=[BQT])
            kb.dma("SP", KT[:], self.FT[1024 + h * 128:1024 + (h + 1) * 128, :], writes=[BKT])
            tmv = self.TM[:, h * 128:(h + 1) * 128].rearrange("(c p) d -> p c d", p=128)
            for c4 in range(4):
                kb.dma("SP", V[:, c4 * 8:(c4 + 1) * 8, 0:128], tmv[:, c4 * 8:(c4 + 1) * 8, :], writes=[BV])
            return (KT, BKT, QT, BQT, V, BV)

        nxt = load_head(0)
        for h in range(8):
            KT, BKT, QT, BQT, V, BV = nxt
            if h + 1 < 8:
                nxt = load_head(h + 1)
            for qt in range(8):
                pairs = self.causal_pairs(qt)
                for c in range(2):
                    def o_off(O, s, c=c):
                        return Os[c][s // 2][0][:, (s % 2) * 256:(s % 2) * 256 + 129]
                    self.attn_qtile(R, QT[64 * c:64 * c + 64, qt * 512:(qt + 1) * 512], BQT,
                                    lambda kc, c=c: (KT[64 * c:64 * c + 64, kc * 128:(kc + 1) * 128], 128), BKT,
                                    lambda kc: V[:, kc, :], BV, 129, pairs, None, Os[c][0][1], o_off, bank_of=lambda s: s // 2)
                ost, Bost = osts.next()
                for s in range(4):
                    O0 = Os[0][s // 2][0][:, (s % 2) * 256:(s % 2) * 256 + 129]
                    O1 = Os[1][s // 2][0][:, (s % 2) * 256:(s % 2) * 256 + 129]
                    BO0, BO1 = Os[0][0][1], Os[1][0][1]
                    rd, Brd = rds.next()
                    a0, Ba0 = a0s.next()
                    a1, Ba1 = a1s.next()
                    kb.op("DVE", lambda e: e.reciprocal(rd[:, 0:1], O0[:, 128:129]), reads=[BO0], writes=[Brd])
                    kb.op("DVE", lambda e: e.reciprocal(rd[:, 1:2], O1[:, 128:129]), reads=[BO1], writes=[Brd])
                    kb.op("DVE", lambda e: e.tensor_tensor(rd[:, 1:2], rd[:, 1:2], nlam[:], ALU.mult), reads=[Brd, Bnlam], writes=[Brd])
                    kb.op("DVE", lambda e: e.tensor_scalar(a0[:], O0[:, 0:128], rd[:, 0:1], None, ALU.mult), reads=[BO0, Brd], writes=[Ba0])
                    kb.op("DVE", lambda e: e.scalar_tensor_tensor(a1[:], O1[:, 0:128], rd[:, 1:2], a0[:], ALU.mult, ALU.add),
                          reads=[BO1, Brd, Ba0], writes=[Ba1])
                    kb.op("ACT", lambda e: e.activation(junk[:], a1[:], AF.Square, accum_out=rd[:, 2:3]), reads=[Ba1], writes=[Bjunk, Brd])
                    kb.op("ACT", lambda e: e.activation(rd[:, 2:3], rd[:, 2:3], AF.Sqrt, bias=self.epsb[:], scale=1.0 / 128),
                          reads=[Brd, self.Bepsb], writes=[Brd])
                    kb.op("DVE", lambda e: e.reciprocal(rd[:, 3:4], rd[:, 2:3]), reads=[Brd], writes=[Brd])
                    kb.op("DVE", lambda e, s=s: e.scalar_tensor_tensor(ost[:, s, :], a1[:], rd[:, 3:4], go[:], ALU.mult, ALU.mult),
                          reads=[Ba1, Brd, Bgo], writes=[Bost])
                osv = self.OS[qt * 512:(qt + 1) * 512, h * 128:(h + 1) * 128].rearrange("(s p) d -> p s d", p=128)
                kb.dma("POOL", osv, ost[:], reads=[Bost])
        T.close()

    def phase_c1(self, li, x_src, w_out_ap):
        kb, I = self.kb, self.I
        T = Scope(kb)
        W, BW = T.sb("w_out", [128, 8, D], BF16)
        self.load_w_bf16(W, BW, w_out_ap, 8)
        ots = Rot([T.sb("ot", [128, D], BF16) for _ in range(2)])
        xts = Rot([T.sb("xt", [128, D], F32) for _ in range(2)])
        oTs = Rot([T.sb("oT", [128, 8, 128], BF16) for _ in range(2)])
        x1s = Rot([T.sb("x1", [128, D], F32) for _ in range(2)])
        hbs = Rot([T.sb("hb", [128, D], BF16) for _ in range(2)])
        ytmp, Bytmp = T.sb("ytmp", [128, D], F32)
        junk, Bjunk = T.sb("junk", [128, D], BF16)
        sss = Rot([T.sb("ss", [128, 1], F32) for _ in range(2)])
        hn, Bhn = T.sb("hn", [128, D], F32)
        stg = [T.sb("stage", [128, 8, 512], BF16) for _ in range(2)]
        pts = Rot([T.ps("ptA", [128, 1024], BF16) for _ in range(2)])
        pys = Rot([T.ps("py", [128, 512], F32) for _ in range(4)])

        def load(t):
            ot, Bot = ots.next()
            xt, Bxt = xts.next()
            kb.dma("SP", ot[:], self.OS[t * 128:(t + 1) * 128, :], writes=[Bot])
            kb.dma("SP", xt[:], x_src[t * 128:(t + 1) * 128, :], writes=[Bxt])
            return ot, Bot, xt, Bxt

        nxt = load(0)
        for t in range(NT):
            ot, Bot, xt, Bxt = nxt
            if t + 1 < NT:
                nxt = load(t + 1)
            oT, BoT = oTs.next()
            pt, Bpt = pts.next()
            self.transpose8(ot, Bot, pt, Bpt, oT[:], BoT, eng="ACT")
            x1, Bx1 = x1s.next()
            for g in range(2):
                py, Bpy = pys.next()
                for kc in range(8):
                    kb.op("PE", lambda e, kc=kc: e.matmul(py[:], oT[:, kc, :], W[:, kc, g * 512:(g + 1) * 512], start=(kc == 0), stop=(kc == 7)),
                          reads=[BoT, BW], writes=[Bpy])
                sl = slice(g * 512, (g + 1) * 512)
                kb.op("DVE", lambda e: e.tensor_tensor(ytmp[:, sl], py[:], self.mod[:, 2, sl], ALU.mult), reads=[Bpy, self.Bmod], writes=[Bytmp])
                kb.op("POOL", lambda e: e.tensor_tensor(x1[:, sl], ytmp[:, sl], xt[:, sl], ALU.add), reads=[Bytmp, Bxt], writes=[Bx1])
            kb.dma("POOL", self.x1s[t * 128:(t + 1) * 128, :], x1[:], reads=[Bx1])
            hb, Bhb = hbs.next()
            ss, Bss = sss.next()
            self.rms_mod_tile(T, x1, Bx1, hb, Bhb, 4, 3, (junk, Bjunk, ss, Bss, hn, Bhn))
            pt, Bpt = pts.next()
            tt = t % 4
            st, Bst = stg[(t // 4) % 2]
            self.transpose8(hb, Bhb, pt, Bpt, st[:, :, tt * 128:(tt + 1) * 128], Bst, eng="ACT")
            if tt == 3:
                t0 = (t - 3) * 128
                h2v = self.H2T.rearrange("(j p) t -> p j t", p=128)
                kb.dma("POOL", h2v[:, :, t0:t0 + 512], st[:], reads=[Bst])
        T.close()

    def phase_c2(self, li, x_dst):
        kb, I = self.kb, self.I
        T = Scope(kb)
        Wg, BWg = T.sb("wg", [128, 8, FFN], BF16)
        Wu, BWu = T.sb("wu", [128, 8, FFN], BF16)
        Wd, BWd = T.sb("wd", [128, NJ, D], BF16)
        self.load_w_bf16(Wg, BWg, I["ffn_w_gate"][li], 8)
        self.load_w_bf16(Wu, BWu, I["ffn_w_up"][li], 8)
        self.load_w_bf16(Wd, BWd, I["ffn_w_down"][li], NJ)
        TT = 256
        h2s = Rot([T.sb("h2T", [128, 8, TT], BF16) for _ in range(2)])
        act, Bact = T.sb("act", [128, NJ, TT], BF16)
        sgs = Rot([T.sb("sg", [128, TT], F32) for _ in range(2)])
        xts = Rot([T.sb("x1t", [128, D], F32) for _ in range(2)])
        ytmp, Bytmp = T.sb("ytmp", [128, 512], F32)
        pgs = Rot([T.ps("pg", [128, 512], F32) for _ in range(2)])
        pus = Rot([T.ps("pu", [128, 512], F32) for _ in range(2)])
        pys = Rot([T.ps("py", [128, 512], F32) for _ in range(3)])
        h2v = self.H2T.rearrange("(j p) t -> p j t", p=128)

        def load(st):
            h2, Bh2 = h2s.next()
            kb.dma("SP", h2[:], h2v[:, :, st * TT:(st + 1) * TT], writes=[Bh2])
            return h2, Bh2

        nxt = load(0)
        for st in range(S // TT):
            h2, Bh2 = nxt
            if st + 1 < S // TT:
                nxt = load(st + 1)
            for j in range(NJ):
                pg, Bpg = pgs.next()
                pu, Bpu = pus.next()
                for kc in range(8):
                    kb.op("PE", lambda e, kc=kc: e.matmul(pg[:, 0:TT], Wg[:, kc, j * 128:(j + 1) * 128], h2[:, kc, :], start=(kc == 0), stop=(kc == 7)),
                          reads=[BWg, Bh2], writes=[Bpg])
                for kc in range(8):
                    kb.op("PE", lambda e, kc=kc: e.matmul(pu[:, 0:TT], Wu[:, kc, j * 128:(j + 1) * 128], h2[:, kc, :], start=(kc == 0), stop=(kc == 7)),
                          reads=[BWu, Bh2], writes=[Bpu])
                sg, Bsg = sgs.next()
                kb.op("ACT", lambda e: e.activation(sg[:], pg[:, 0:TT], AF.Silu), reads=[Bpg], writes=[Bsg])
                kb.op("DVE", lambda e, j=j: e.tensor_tensor(act[:, j, :], sg[:], pu[:, 0:TT], ALU.mult), reads=[Bsg, Bpu], writes=[Bact])
            for q in range(TT // 128):
                t = st * (TT // 128) + q
                xt, Bxt = xts.next()
                kb.dma("SP", xt[:], self.x1s[t * 128:(t + 1) * 128, :], writes=[Bxt])
                for g in range(2):
                    py, Bpy = pys.next()
                    for j in range(NJ):
                        kb.op("PE", lambda e, j=j: e.matmul(py[:], act[:, j, q * 128:(q + 1) * 128], Wd[:, j, g * 512:(g + 1) * 512],
                                                            start=(j == 0), stop=(j == NJ - 1)),
                              reads=[Bact, BWd], writes=[Bpy])
                    sl = slice(g * 512, (g + 1) * 512)
                    kb.op("DVE", lambda e: e.tensor_tensor(ytmp[:], py[:], self.mod[:, 5, sl], ALU.mult), reads=[Bpy, self.Bmod], writes=[Bytmp])
                    kb.op("POOL", lambda e: e.tensor_tensor(xt[:, sl], ytmp[:], xt[:, sl], ALU.add), reads=[Bytmp, Bxt], writes=[Bxt])
                kb.dma("POOL", x_dst[t * 128:(t + 1) * 128, :], xt[:], reads=[Bxt])
        T.close()

    def build(self):
        kb = self.kb
        self.declare()
        self.setup()
        G = self.G
        self.epsb, self.Bepsb = G.sb("epsb", [128, 1], F32)
        kb.op("POOL", lambda e: e.memset(self.epsb[:], EPS), writes=[self.Bepsb])
        layers = self.cfg.get("layers", [0, 1])
        x_src = self.I["x"]
        for n, li in enumerate(layers):
            x_dst = self.out if n == len(layers) - 1 else self.xmid
            self.L = Scope(kb)
            if li == 0:
                self.gate_sb, self.Bgate = self.L.sb("gates", [128, NT, 24], F32)
            self.compute_mod(li)
            stop = self.cfg.get("stop")
            if li == 0:
                self.phase_a(0, x_src)
                if stop == "A":
                    self.L.close()
                    break
                if not self.cfg.get("skip_moba"):
                    self.moba_part()
                if stop == "moba":
                    self.L.close()
                    break
                self.nsa_part()
                if stop in ("nsa", "cmpmlp"):
                    self.L.close()
                    break
                self.phase_c1(0, x_src, self.I["sp_w_out"])
            else:
                self.phase_a(1, x_src)
                self.phase_b_diff(0)
                self.phase_c1(1, x_src, self.I["diff_w_out"])
            self.phase_c2(li, x_dst)
            self.L.close()
            x_src = x_dst
        kb.barrier(engines=("POOL",))
        G.es.close()
        kb.es.close()
        return self.nc

    def epi_norm(self, T, O_ap, BO, vcol, rd_ap, Brd):
        kb = self.kb
        kb.op("DVE", lambda e: e.tensor_scalar(rd_ap, O_ap[:, vcol:vcol + 1], 1e-30, None, ALU.max), reads=[BO], writes=[Brd])
        kb.op("DVE", lambda e: e.reciprocal(rd_ap, rd_ap), reads=[Brd], writes=[Brd])

    def phase_b_sparse(self):
        self.moba_part()
        self.nsa_part()

    def moba_part(self):
        kb, I = self.kb, self.I
        T = Scope(kb)
        QAs = Rot([T.sb("QA", [128, 2, S], BF16) for _ in range(2)])
        KAs = Rot([T.sb("KA", [128, 2, S], BF16) for _ in range(2)])
        Vps = Rot([T.sb("Vp", [128, NT, 2, 65], BF16) for _ in range(2)])
        for (ka, Bka) in KAs.items:
            for hh in range(2):
                kb.dma("POOL", ka[64:80, hh, :], I["c_e16"][:, :], writes=[Bka])
        for (v, Bv) in Vps.items:
            kb.op("POOL", lambda e, v=v: e.memset(v[:, :, :, 64:65], 1.0), writes=[Bv])
        t1, Bt1 = T.sb("t1", [128, 16, 16], F32)
        t2, Bt2 = T.sb("t2", [128, 16, 16], F32)
        kb.dma("SP", t1[:].rearrange("p a b -> p (a b)"), bc_row(I["c_t1"][0:1, :], 256), writes=[Bt1])
        kb.dma("SP", t2[:].rearrange("p a b -> p (a b)"), bc_row(I["c_t2"][0:1, :], 256), writes=[Bt2])
        augs = Rot([T.sb("aug", [128, 128], BF16) for _ in range(2)])
        for (a, Ba) in augs.items:
            kb.op("POOL", lambda e, a=a: e.memset(a[:], 0.0), writes=[Ba])
        km, Bkm = T.sb("km", [64, 16], F32)
        kmb, Bkmb = T.sb("kmb", [64, 16], BF16)
        gms = Rot([T.sb("gm", [128, 16], F32) for _ in range(2)])
        m8s = Rot([T.sb("m8", [128, 8], F32) for _ in range(2)])
        sels = Rot([T.sb("sel", [128, 16], F32) for _ in range(2)])
        R = dict(pS=Rot([T.ps("pS", [128, 512], F32) for _ in range(2)]),
                 pT=Rot([T.sb("pT", [128, 512], BF16) for _ in range(4)]))
        Obs = Rot([T.ps("Om", [128, 512], F32) for _ in range(2)])
        pgs = Rot([T.ps("pgate", [128, 512], F32) for _ in range(2)])
        ptr = Rot([T.ps("ptr", [128, 1024], BF16) for _ in range(2)])
        osts = Rot([T.sb("ost", [128, 4, 128], BF16) for _ in range(2)])
        rds = Rot([T.sb("rd", [128, 1], F32) for _ in range(4)])

        def load_pair(hp):
            QA, BQA = QAs.next()
            KA, BKA = KAs.next()
            Vp, BVp = Vps.next()
            for hh in range(2):
                h = 2 * hp + hh
                kb.dma("SP", QA[0:64, hh, :], self.FT[h * 64:(h + 1) * 64, :], writes=[BQA])
                kb.dma("SP", KA[0:64, hh, :], self.FT[512 + h * 64:512 + (h + 1) * 64, :], writes=[BKA])
            for hh in range(2):
                h = 2 * hp + hh
                tmv = self.TM[:, h * 64:(h + 1) * 64].rearrange("(c p) d -> p c d", p=128)
                for c4 in range(4):
                    kb.dma("SP", Vp[:, c4 * 8:(c4 + 1) * 8, hh, 0:64], tmv[:, c4 * 8:(c4 + 1) * 8, :], writes=[BVp])
            return QA, BQA, KA, BKA, Vp, BVp

        nxt = load_pair(0)
        for hp in range(4):
            QA, BQA, KA, BKA, Vp, BVp = nxt
            if hp + 1 < 4:
                nxt = load_pair(hp + 1)
            for hh in range(2):
                kb.op("DVE", lambda e: e.tensor_reduce(km[:], KA[0:64, hh, :].rearrange("p (b j) -> p b j", j=256), AX.X, ALU.add),
                      reads=[BKA], writes=[Bkm])
                kb.op("DVE", lambda e: e.tensor_scalar(kmb[:], km[:], 1.0 / 256, None, ALU.mult), reads=[Bkm], writes=[Bkmb])
                for t in range(NT):
                    own = t // 2
                    pg, Bpg = pgs.next()
                    kb.op("PE", lambda e: e.matmul(pg[:, 0:16], QA[0:64, hh, t * 128:(t + 1) * 128], kmb[:], start=True, stop=True),
                          reads=[BQA, Bkmb], writes=[Bpg])
                    gm, Bgm = gms.next()
                    m8, Bm8 = m8s.next()
                    sel, Bsel = sels.next()
                    aug, Baug = augs.next()
                    kb.op("DVE", lambda e: e.tensor_tensor(gm[:], pg[:, 0:16], t1[:, own, :], ALU.add), reads=[Bpg, Bt1], writes=[Bgm])
                    kb.op("DVE", lambda e: e.max(m8[:], gm[:]), reads=[Bgm], writes=[Bm8])
                    kb.op("DVE", lambda e: e.tensor_scalar(m8[:, 2:3], m8[:, 2:3], -1e29, None, ALU.max), reads=[Bm8], writes=[Bm8])
                    kb.op("DVE", lambda e: e.tensor_scalar(sel[:], gm[:], m8[:, 2:3], None, ALU.is_ge), reads=[Bgm, Bm8], writes=[Bsel])
                    kb.op("DVE", lambda e: e.tensor_tensor(sel[:], sel[:], t2[:, own, :], ALU.max), reads=[Bsel, Bt2], writes=[Bsel])
                    kb.op("DVE", lambda e: e.tensor_scalar(aug[:, 64:80], sel[:], -1.0, -MASKV, ALU.add, ALU.mult), reads=[Bsel], writes=[Baug])
                    pt, Bpt = ptr.next()
                    kb.op("PE", lambda e: e.transpose(pt[:, 0:128], aug[:], self.ident[:]), reads=[Baug, self.Bident], writes=[Bpt])
                    kb.op("ACT", lambda e: e.copy(QA[64:80, hh, t * 128:(t + 1) * 128], pt[64:80, 0:128]), reads=[Bpt], writes=[BQA])
            for qt in range(8):
                pairs = self.causal_pairs(qt)
                ost, Bost = osts.next()
                for hh in range(2):
                    Om, BOm = Obs.next()
                    self.attn_qtile(R, QA[0:80, hh, qt * 512:(qt + 1) * 512], BQA,
                                    lambda kc: (KA[0:80, hh, kc * 128:(kc + 1) * 128], 128), BKA,
                                    lambda kc: Vp[:, kc, hh, :], BVp, 65, pairs, Om, BOm,
                                    lambda O, s: O[:, s * 128:s * 128 + 65])
                    for s in range(4):
                        rd, Brd = rds.next()
                        Os_ = Om[:, s * 128:s * 128 + 65]
                        self.epi_norm(T, Os_, BOm, 64, rd[:], Brd)
                        kb.op("DVE", lambda e, s=s: e.tensor_scalar(ost[:, s, hh * 64:(hh + 1) * 64], Os_[:, 0:64], rd[:, 0:1], None, ALU.mult),
                              reads=[BOm, Brd], writes=[Bost])
                osv = self.OS[qt * 512:(qt + 1) * 512, hp * 128:(hp + 1) * 128].rearrange("(s p) d -> p s d", p=128)
                kb.dma("POOL", osv, ost[:], reads=[Bost])
        T.close()

    def nsa_part(self):
        kb, I = self.kb, self.I
        for _ in range(self.cfg.get("pad_dve", 0)):
            kb.op("DVE", lambda e: e.memset(self.epsb[:], EPS), writes=[self.Bepsb])
        P = Scope(kb)
        CKc, BCKc = P.sb("CKc", [64, 2, 256], BF16)
        VCs = [P.sb("VC", [128, 2, 129], BF16) for _ in range(2)]
        kb.op("POOL", lambda e: e.memset(CKc[:], 0.0), writes=[BCKc])
        for g in range(2):
            vc, Bvc = VCs[g]
            for c in range(2):
                kb.dma("POOL", vc[:, c, 64:129], I["c_ov"][c * 128:(c + 1) * 128, :], writes=[Bvc])
        T = Scope(kb)
        CX = [T.sb("CX", [128, S], BF16) for _ in range(2)]
        kb.dma("SP", CX[0][0][:], self.FT[1536:1664, :], writes=[CX[0][1]])
        kb.dma("SP", CX[1][0][:], self.FT[1664:1792, :], writes=[CX[1][1]])
        w1, Bw1 = T.sb("w1", [128, 2 * 32 * 256], BF16)
        w1src = I["cmp_w1"].rearrange("d a l e -> d (a l e)")
        for half in range(2):
            for a in range(2):
                kb.dma("POOL", w1[half * 64:(half + 1) * 64, a * 8192:(a + 1) * 8192], w1src[:, a * 8192:(a + 1) * 8192], writes=[Bw1])
        w1v = w1[:].rearrange("p (a l e) -> p a l e", a=2, l=32)
        posb, Bposb = T.sb("posb", [64, 2, 34], BF16)
        kb.op("POOL", lambda e: e.memset(posb[:], 0.0), writes=[Bposb])
        kb.dma("POOL", posb[:, :, 0:32], I["cmp_posT"][:, :, :], writes=[Bposb])
        w2, Bw2 = T.sb("w2", [128, 2, 2, 64], BF16)
        for kv in range(2):
            for eh in range(2):
                kb.dma("POOL", w2[:, kv, eh, :], I["cmp_w2"][kv, eh * 128:(eh + 1) * 128, :], writes=[Bw2])
        b1, Bb1 = T.sb("b1", [128, 4], F32)
        pbs = Rot([T.ps("pb", [128, 512], F32) for _ in range(2)])
        phs = Rot([T.ps("ph", [128, 512], F32) for _ in range(3)])
        for kv in range(2):
            for eh in range(2):
                pb, Bpb = pbs.next()
                for l in range(32):
                    kb.op("PE", lambda e, l=l: e.matmul(pb[:, 0:2], w1v[0:64, kv, l, eh * 128:(eh + 1) * 128],
                                                        posb[0:64, kv, l:l + 2], start=(l == 0), stop=(l == 31)),
                          reads=[Bw1, Bposb], writes=[Bpb])
                kb.op("DVE", lambda e: e.tensor_copy(b1[:, kv * 2 + eh:kv * 2 + eh + 1], pb[:, 0:1]), reads=[Bpb], writes=[Bb1])
        hid = {}
        for kv in range(2):
            cx, Bcx = CX[kv]
            cxv = cx[:].rearrange("p (n j) -> p n j", j=16)
            for g in range(2):
                for eh in range(2):
                    ph, Bph = phs.next()
                    for l in range(32):
                        a = l // 16
                        kb.op("PE", lambda e, l=l, a=a: e.matmul(ph[:, 0:255], w1v[64 * g:64 * g + 64, kv, l, eh * 128:(eh + 1) * 128],
                                                                 cxv[64 * g:64 * g + 64, a:a + 255, l % 16], start=(l == 0), stop=(l == 31)),
                              reads=[Bw1, Bcx], writes=[Bph])
                    ht, Bht = T.sb("hid", [128, 256], BF16)
                    kb.op("POOL", lambda e: e.memset(ht[:], 0.0), writes=[Bht])
                    kb.op("ACT", lambda e: e.activation(ht[:, 0:255], ph[:, 0:255], AF.Silu, bias=b1[:, kv * 2 + eh:kv * 2 + eh + 1]),
                          reads=[Bph, Bb1], writes=[Bht])
                    hid[(kv, g, eh)] = (ht, Bht)
        for g in range(2):
            ph, Bph = phs.next()
            for eh in range(2):
                ht, Bht = hid[(0, g, eh)]
                kb.op("PE", lambda e: e.matmul(ph[0:64, 0:255], w2[:, 0, eh, :], ht[:, 0:255], start=(eh == 0), stop=(eh == 1)),
                      reads=[Bw2, Bht], writes=[Bph])
            kb.op("ACT", lambda e: e.copy(CKc[:, g, 0:255], ph[0:64, 0:255]), reads=[Bph], writes=[BCKc])
            vc, Bvc = VCs[g]
            for c in range(2):
                ph, Bph = phs.next()
                for eh in range(2):
                    ht, Bht = hid[(1, g, eh)]
                    kb.op("PE", lambda e: e.matmul(ph[:, 0:64], ht[:, c * 128:(c + 1) * 128], w2[:, 1, eh, :], start=(eh == 0), stop=(eh == 1)),
                          reads=[Bw2, Bht], writes=[Bph])
                kb.op("ACT", lambda e: e.copy(vc[:, c, 0:64], ph[:, 0:64]), reads=[Bph], writes=[Bvc])
        T.close()
        if self.cfg.get("stop") == "cmpmlp":
            if self.cfg.get("debug"):
                dck = self.tap("d_ckc", [64, 512], BF16)
                kb.dma("SP", dck[:, :], CKc[:].rearrange("p g n -> p (g n)"), reads=[BCKc])
                for g in range(2):
                    dvc = self.tap("d_vc%d" % g, [128, 258], BF16)
                    kb.dma("SP", dvc[:, :], VCs[g][0][:].rearrange("p c n -> p (c n)"), reads=[VCs[g][1]])
            P.close()
            return

        T = Scope(kb)
        nfv, Bnfv = T.sb("nfv", [128, NT, 64], F32)
        cst, Bcst = T.sb("cst", [128, NT, 64], F32)
        v01, Bv01 = T.sb("v01", [128, NT, 64], F32)
        kb.dma("SP", nfv[:], I["c_nfv"][:, :, :], writes=[Bnfv])
        kb.dma("SP", cst[:], I["c_cst"][:, :, :], writes=[Bcst])
        kb.dma("SP", v01[:], I["c_v01"][:, :, :], writes=[Bv01])
        QA4, BQA4 = T.sb("QA4", [128, 4, S], BF16)
        KS, BKS = T.sb("KS", [128, S], BF16)
        KW, BKW = T.sb("KW", [64, S], BF16)
        VS, BVS = T.sb("VS", [128, NT, 65], BF16)
        VW, BVW = T.sb("VW", [128, NT, 65], BF16)
        kb.op("POOL", lambda e: e.memset(VS[:, :, 64:65], 1.0), writes=[BVS])
        kb.op("POOL", lambda e: e.memset(VW[:, :, 64:65], 1.0), writes=[BVW])
        kb.dma("POOL", KS[64:128, :], I["c_e64"][:, :], writes=[BKS])
        augs = Rot([T.sb("aug", [128, 128], BF16) for _ in range(2)])
        for (a, Ba) in augs.items:
            kb.op("POOL", lambda e, a=a: e.memset(a[:], 0.0), writes=[Ba])
        R = dict(pS=Rot([T.ps("pS", [128, 512], F32) for _ in range(2)]),
                 pT=Rot([T.sb("pT", [128, 512], BF16) for _ in range(4)]))
        Oc = [T.ps("Oc", [128, 512], F32) for _ in range(2)]
        BOc = Oc[0][1]
        Osw = Rot([T.ps("Osw", [128, 512], F32) for _ in range(3)])
        ptr = Rot([T.ps("ptr", [128, 1024], BF16) for _ in range(1)])
        occ, Bocc = T.sb("occ", [128, 4, 4, 64], F32)
        imp, Bimp = T.sb("imp", [128, 4, 64], F32)
        itmp, Bitmp = T.sb("itmp", [128, 64], F32)
        sc, Bsc = T.sb("sc", [128, 64], F32)
        sc2, Bsc2 = T.sb("sc2", [128, 64], F32)
        m8a, Bm8a = T.sb("m8a", [128, 8], F32)
        m8b, Bm8b = T.sb("m8b", [128, 8], F32)
        selm, Bselm = T.sb("selm", [128, 64], F32)
        rds = Rot([T.sb("rd", [128, 2], F32) for _ in range(6)])
        osts = Rot([T.sb("ost", [128, 4, 256], BF16) for _ in range(2)])
        gsb = self.gate_sb

        for g in range(2):
            for r in range(4):
                h = 4 * g + r
                kb.dma("SP", QA4[0:64, r, :], self.FT[1024 + h * 64:1024 + (h + 1) * 64, :], writes=[BQA4])
            kb.dma("SP", KS[0:64, :], self.FT[1792 + 64 * g:1792 + 64 * g + 64, :], writes=[BKS])
            kb.dma("SP", KW[:], self.FT[1920 + 64 * g:1920 + 64 * g + 64, :], writes=[BKW])
            tms = self.TM[:, 512 + 64 * g:512 + 64 * g + 64].rearrange("(c p) d -> p c d", p=128)
            tmw = self.TM[:, 640 + 64 * g:640 + 64 * g + 64].rearrange("(c p) d -> p c d", p=128)
            for c4 in range(4):
                kb.dma("SP", VS[:, c4 * 8:(c4 + 1) * 8, 0:64], tms[:, c4 * 8:(c4 + 1) * 8, :], writes=[BVS])
                kb.dma("SP", VW[:, c4 * 8:(c4 + 1) * 8, 0:64], tmw[:, c4 * 8:(c4 + 1) * 8, :], writes=[BVW])
            vc, Bvc = VCs[g]
            parts = self.cfg.get("nsa_parts", ("cmp", "select", "sw"))
            for qt in self.cfg.get("nsa_qts", range(8)):
                ost, Bost = osts.next()
                cchunks = [0] + ([1] if qt >= 4 else [])
                cpairs = [(c, [(s, "full") for s in range(4)]) for c in cchunks]

                def cmp_mask(c, pT, BpT, krows):
                    kb.op("POOL", lambda e: e.affine_select(pT[:], pT[:], [[1, 512]], ALU.is_ge, 0.0,
                                                            base=512 * qt - 2048 * c - 31, channel_multiplier=-16),
                          reads=[BpT], writes=[BpT])

                for r in (range(4) if "cmp" in parts else ()):
                    h = 4 * g + r
                    self.attn_qtile(R, QA4[0:64, r, qt * 512:(qt + 1) * 512], BQA4,
                                    lambda c: (CKc[:, g, c * 128:(c + 1) * 128], 128), BCKc,
                                    lambda c: vc[:, c, :], Bvc, 129, cpairs, None, BOc,
                                    lambda O, s: Oc[s // 2][0][:, (s % 2) * 256:(s % 2) * 256 + 129], post_exp=cmp_mask, bank_of=lambda s: s // 2)
                    for s in range(4):
                        t = 4 * qt + s
                        O_ = Oc[s // 2][0][:, (s % 2) * 256:(s % 2) * 256 + 129]
                        rd, Brd = rds.next()
                        self.epi_norm(T, O_, BOc, 64, rd[:, 0:1], Brd)
                        if r == 0:
                            kb.op("DVE", lambda e, s=s: e.tensor_scalar(imp[:, s, :], O_[:, 65:129], rd[:, 0:1], None, ALU.mult),
                                  reads=[BOc, Brd], writes=[Bimp])
                        else:
                            kb.op("DVE", lambda e: e.tensor_scalar(itmp[:], O_[:, 65:129], rd[:, 0:1], None, ALU.mult),
                                  reads=[BOc, Brd], writes=[Bitmp])
                            kb.op("DVE", lambda e, s=s: e.tensor_tensor(imp[:, s, :], imp[:, s, :], itmp[:], ALU.add),
                                  reads=[Bitmp, Bimp], writes=[Bimp])
                        kb.op("DVE", lambda e: e.tensor_tensor(rd[:, 1:2], rd[:, 0:1], gsb[:, t, h:h + 1], ALU.mult), reads=[Brd, self.Bgate], writes=[Brd])
                        kb.op("DVE", lambda e, s=s, r=r: e.tensor_scalar(occ[:, r, s, :], O_[:, 0:64], rd[:, 1:2], None, ALU.mult),
                              reads=[BOc, Brd], writes=[Bocc])
                for s in (range(4) if "select" in parts else ()):
                    t = 4 * qt + s
                    aug, Baug = augs.next()
                    kb.op("DVE", lambda e: e.tensor_tensor(sc[:], imp[:, s, :], nfv[:, t, :], ALU.mult), reads=[Bimp, Bnfv], writes=[Bsc])
                    kb.op("DVE", lambda e: e.tensor_tensor(sc[:], sc[:], cst[:, t, :], ALU.add), reads=[Bsc, Bcst], writes=[Bsc])
                    kb.op("DVE", lambda e: e.max(m8a[:], sc[:]), reads=[Bsc], writes=[Bm8a])
                    kb.op("DVE", lambda e: e.match_replace(sc2[:], m8a[:], sc[:], -1e30), reads=[Bsc, Bm8a], writes=[Bsc2])
                    kb.op("DVE", lambda e: e.max(m8b[:], sc2[:]), reads=[Bsc2], writes=[Bm8b])
                    kb.op("DVE", lambda e: e.tensor_scalar(selm[:], sc[:], m8b[:, 7:8], None, ALU.is_ge), reads=[Bsc, Bm8b], writes=[Bselm])
                    kb.op("DVE", lambda e: e.tensor_tensor(selm[:], selm[:], v01[:, t, :], ALU.mult), reads=[Bselm, Bv01], writes=[Bselm])
                    kb.op("DVE", lambda e: e.tensor_scalar(aug[:, 64:128], selm[:], -1.0, -MASKV, ALU.add, ALU.mult), reads=[Bselm], writes=[Baug])
                    pt, Bpt = ptr.next()
                    kb.op("PE", lambda e: e.transpose(pt[:, 0:128], aug[:], self.ident[:]), reads=[Baug, self.Bident], writes=[Bpt])
                    for r in range(4):
                        kb.op("ACT", lambda e, r=r: e.copy(QA4[64:128, r, t * 128:(t + 1) * 128], pt[64:128, 0:128]), reads=[Bpt], writes=[BQA4])
                spairs = self.causal_pairs(qt)
                wpairs = []
                for kc in range(max(0, 4 * qt - 4), 4 * qt + 4):
                    subs = []
                    for s in range(4):
                        dlt = 4 * qt + s - kc
                        if dlt == 0:
                            subs.append((s, "tri"))
                        elif 1 <= dlt <= 3:
                            subs.append((s, "full"))
                        elif dlt == 4:
                            subs.append((s, "atri"))
                    if subs:
                        wpairs.append((kc, subs))
                for r in (range(4) if "sw" in parts else ()):
                    h = 4 * g + r
                    for br, (pairs, qrows, kl, Bkl, vt, Bvt) in enumerate((
                            (spairs, 128, KS, BKS, VS, BVS), (wpairs, 64, KW, BKW, VW, BVW))):
                        if br not in self.cfg.get("nsa_br", (0, 1)):
                            continue
                        Ob, BOb = Osw.next()
                        self.attn_qtile(R, QA4[0:qrows, r, qt * 512:(qt + 1) * 512], BQA4,
                                        lambda kc, kl=kl, qrows=qrows: (kl[0:qrows, kc * 128:(kc + 1) * 128], 128), Bkl,
                                        lambda kc, vt=vt: vt[:, kc, :], Bvt, 65, pairs, Ob, BOb,
                                        lambda O, s: O[:, s * 128:s * 128 + 65])
                        for s in range(4):
                            t = 4 * qt + s
                            O_ = Ob[:, s * 128:s * 128 + 65]
                            rd, Brd = rds.next()
                            self.epi_norm(T, O_, BOb, 64, rd[:, 0:1], Brd)
                            gcol = (br + 1) * 8 + h
                            kb.op("DVE", lambda e: e.tensor_tensor(rd[:, 1:2], rd[:, 0:1], gsb[:, t, gcol:gcol + 1], ALU.mult),
                                  reads=[Brd, self.Bgate], writes=[Brd])
                            if br == 0:
                                kb.op("DVE", lambda e: e.tensor_scalar(itmp[:], O_[:, 0:64], rd[:, 1:2], None, ALU.mult),
                                      reads=[BOb, Brd], writes=[Bitmp])
                                kb.op("DVE", lambda e, s=s, r=r: e.tensor_tensor(occ[:, r, s, :], occ[:, r, s, :], itmp[:], ALU.add),
                                      reads=[Bitmp, Bocc], writes=[Bocc])
                            else:
                                kb.op("DVE", lambda e, s=s, r=r: e.scalar_tensor_tensor(ost[:, s, r * 64:(r + 1) * 64], O_[:, 0:64], rd[:, 1:2], occ[:, r, s, :], ALU.mult, ALU.add),
                                      reads=[BOb, Brd, Bocc], writes=[Bost])
                osv = self.OS[qt * 512:(qt + 1) * 512, 512 + g * 256:512 + (g + 1) * 256].rearrange("(s p) d -> p s d", p=128)
                kb.dma("POOL", osv, ost[:], reads=[Bost])
                if self.cfg.get("nsa_barrier"):
                    kb.barrier()
        T.close()
        P.close()


def _consts():
    c = {}
    c["c_ident"] = np.eye(128, dtype=np.float32)
    k = np.arange(128)[:, None]
    q = np.arange(128)[None, :]
    c["c_tri"] = (q >= k).astype(np.float32)
    c["c_atri"] = (k > q).astype(np.float32)
    key = np.arange(S)[None, :]
    c["c_e16"] = (key // 256 == np.arange(16)[:, None]).astype(np.float32)
    c["c_e64"] = (key // 64 == np.arange(64)[:, None]).astype(np.float32)
    ncmp = 255
    cs = np.arange(ncmp) * 16
    ss = np.arange(64) * 64
    ov = np.minimum(cs[:, None] + 32, ss[None, :] + 64) - np.maximum(cs[:, None], ss[None, :])
    ovp = np.zeros((256, 65), np.float32)
    ovp[:255, 0] = 1.0
    ovp[:255, 1:] = np.clip(ov, 0, None) / 32.0
    c["c_ov"] = ovp
    own = np.arange(16)[:, None]
    blk = np.arange(16)[None, :]
    c["c_t1"] = np.where(blk < own, 0.0, -1e30).astype(np.float32).reshape(1, 256)
    c["c_t2"] = (blk == own).astype(np.float32).reshape(1, 256)
    p = np.arange(128)[:, None, None]
    t = np.arange(NT)[None, :, None]
    m = np.arange(64)[None, None, :]
    cur = (t * 128 + p) // 64
    ok = m <= cur
    forced = ok & ((m == 0) | (m >= cur - 1))
    c["c_nfv"] = (ok & ~forced).astype(np.float32)
    c["c_cst"] = np.where(ok, np.where(forced, 1e4 + m, 0.0), -1e30).astype(np.float32)
    c["c_v01"] = ok.astype(np.float32)
    return c


def _core_inputs(b, inp, consts):
    f = lambda a: np.ascontiguousarray(np.asarray(a), dtype=np.float32)
    m = {}
    m["x"] = f(inp["x"][b])
    m["cT"] = f(np.asarray(inp["c"][b]).reshape(8, 128).T)
    m["posT"] = np.ascontiguousarray(np.asarray(inp["positions"][b]).reshape(NT, 128).T.astype(np.int32))
    for k in ("ada_w", "ada_b", "attn_norm", "ffn_norm", "ffn_w_gate", "ffn_w_up", "ffn_w_down"):
        m[k] = f(inp[k])
    m["sp_w_in"] = f(inp["sp_w_in"][0]); m["sp_w_out"] = f(inp["sp_w_out"][0])
    m["moba_q_norm"] = f(inp["moba_q_norm"]); m["moba_k_norm"] = f(inp["moba_k_norm"])
    m["nsa_q_norm"] = f(inp["nsa_q_norm"]); m["nsa_k_norm"] = f(inp["nsa_k_norm"][0])
    m["cmp_posT"] = f(np.asarray(inp["nsa_cmp_pos"][0]).transpose(2, 0, 1))
    m["cmp_w1"] = f(np.asarray(inp["nsa_cmp_w1"][0]).transpose(2, 0, 1, 3))
    m["cmp_w2"] = f(inp["nsa_cmp_w2"][0])
    m["diff_w_in"] = f(inp["diff_w_in"][0]); m["diff_w_out"] = f(inp["diff_w_out"][0])
    m["diff_q_norm"] = f(inp["diff_q_norm"]); m["diff_k_norm"] = f(inp["diff_k_norm"])
    m["diff_lambda"] = f(np.asarray(inp["diff_lambda"][0]).reshape(1, 256))
    m["diff_out_norm"] = f(inp["diff_out_norm"])
    m.update(consts)
    return m


_NC_CACHE = {}


def kernel(**inputs):
    consts = _consts()
    if "nc" not in _NC_CACHE:
        _NC_CACHE["nc"] = Prog({}).build()
    nc = _NC_CACHE["nc"]
    in_maps = [_core_inputs(b, inputs, consts) for b in range(8)]
    res = run_bass_kernel_spmd(nc, in_maps, core_ids=list(range(8)))
    return np.stack([np.asarray(r["out"], dtype=np.float32) for r in res.results], axis=0)
```

```python
from contextlib import ExitStack
import numpy as np
import concourse.bass as bass
import concourse.mybir as mybir
from concourse.bass_utils import run_bass_kernel_spmd

F32 = mybir.dt.float32
BF16 = mybir.dt.bfloat16
I32 = mybir.dt.int32
AF = mybir.ActivationFunctionType
ALU = mybir.AluOpType
AX = mybir.AxisListType

S = 4096
D = 1024
NT = S // 128
FFN = 2816
NJ = FFN // 128
EPS = 1e-6
MASKV = -30000.0
SP_IN = 2840
N_DMA_SEMS = 24


class Buf:
    __slots__ = ("w", "r", "name")

    def __init__(self, name=""):
        self.w = None
        self.r = {}
        self.name = name


class Rot:
    def __init__(self, items):
        self.items = list(items)
        self.i = 0

    def next(self):
        it = self.items[self.i]
        self.i = (self.i + 1) % len(self.items)
        return it


class KB:
    def __init__(self, nc):
        self.nc = nc
        self.es = ExitStack()
        self.engs = {"PE": nc.tensor, "ACT": nc.scalar, "DVE": nc.vector, "POOL": nc.gpsimd, "SP": nc.sync}
        self.sem = {}
        self.cnt = {}
        for e in ("PE", "ACT", "DVE", "POOL"):
            self.sem[e] = self.es.enter_context(nc.semaphore("s_" + e))
            self.cnt[e] = 0
        self.dsem = [self.es.enter_context(nc.semaphore("d_%d" % i)) for i in range(N_DMA_SEMS)]
        self.dcnt = [0] * N_DMA_SEMS
        self.dnext = 0
        self.dnext2 = [0, 0]
        self.waited = {e: {} for e in self.engs}
        self.n_inst = 0
        self.uid = 0
        self.limit = None
        self.log = None

    def name(self, p):
        self.uid += 1
        return "%s_%d" % (p, self.uid)

    def _wait(self, E, key, c, raw=False):
        if key == E and E == "PE":
            return
        w = self.waited[E]
        if w.get(key, 0) >= c:
            return
        w[key] = c
        if isinstance(key, int):
            self.engs[E].wait_ge(self.dsem[key], c)
        else:
            self.engs[E].wait_ge(self.sem[key], c)

    def _deps(self, E, reads, writes):
        for b in reads:
            if b.w is not None:
                self._wait(E, b.w[0], b.w[1], raw=True)
        for b in writes:
            if b.w is not None:
                self._wait(E, b.w[0], b.w[1])
            for k, c in b.r.items():
                self._wait(E, k, c)

    def _mark(self, tok, reads, writes):
        for b in reads:
            if b.r.get(tok[0], 0) < tok[1]:
                b.r[tok[0]] = tok[1]
        for b in writes:
            b.w = tok
            b.r = {}

    def op(self, E, fn, reads=(), writes=()):
        if self.limit is not None and self.n_inst >= self.limit:
            return None
        self._deps(E, reads, writes)
        if self.log is not None:
            import sys as _sys
            self.log.append((self.n_inst, E, _sys._getframe(1).f_lineno))
        ins = fn(self.engs[E])
        self.cnt[E] += 1
        ins.then_inc(self.sem[E], 1)
        tok = (E, self.cnt[E])
        self._mark(tok, reads, writes)
        self.n_inst += 1
        return tok

    def dma(self, Q, out_ap, in_ap, reads=(), writes=(), **kw):
        if self.limit is not None and self.n_inst >= self.limit:
            return None
        self._deps(Q, reads, writes)
        half = N_DMA_SEMS // 2
        qi = 0 if Q == "SP" else 1
        k = qi * half + self.dnext2[qi]
        self.dnext2[qi] = (self.dnext2[qi] + 1) % half
        if self.dcnt[k] > 0:
            self._wait(Q, k, self.dcnt[k])
        if self.log is not None:
            import sys as _sys
            self.log.append((self.n_inst, "DMA-" + Q, _sys._getframe(1).f_lineno))
        ins = self.engs[Q].dma_start(out=out_ap, in_=in_ap, **kw)
        self.dcnt[k] += 16
        ins.then_inc(self.dsem[k], 16)
        tok = (k, self.dcnt[k])
        self._mark(tok, reads, writes)
        self.n_inst += 1
        return tok

    def barrier(self, engines=("PE", "ACT", "DVE", "POOL", "SP")):
        for E in engines:
            for e2 in ("PE", "ACT", "DVE", "POOL"):
                if self.cnt[e2] > 0:
                    self._wait(E, e2, self.cnt[e2])
            for k in range(N_DMA_SEMS):
                if self.dcnt[k] > 0:
                    self._wait(E, k, self.dcnt[k])


class Scope:
    def __init__(self, kb):
        self.kb = kb
        self.es = ExitStack()

    def sb(self, name, shape, dt):
        t = self.es.enter_context(self.kb.nc.sbuf_tensor(self.kb.name(name), list(shape), dt))
        return t, Buf(name)

    def ps(self, name, shape, dt):
        t = self.es.enter_context(self.kb.nc.psum_tensor(self.kb.name(name), list(shape), dt))
        return t, Buf(name)

    def close(self):
        self.kb.barrier()
        self.es.close()


def bc_row(ap_row, n):
    return ap_row.to_broadcast([128, n])


class Prog:
    def __init__(self, cfg):
        self.cfg = cfg
        self.nc = bass.Bass("TRN2", target_bir_lowering=False)
        self.kb = KB(self.nc)
        self.kb.limit = cfg.get("limit")
        self.I = {}
        self.taps = {}

    def din(self, name, shape, dt=F32):
        self.I[name] = self.nc.dram_tensor(name, list(shape), dt, kind="ExternalInput").ap()
        return self.I[name]

    def dscr(self, name, shape, dt):
        if self.cfg.get("debug"):
            return self.nc.dram_tensor(name, list(shape), dt, kind="ExternalOutput").ap()
        return self.nc.dram_tensor(name, list(shape), dt).ap()

    def tap(self, name, shape, dt=F32):
        self.taps[name] = self.nc.dram_tensor(name, list(shape), dt, kind="ExternalOutput").ap()
        return self.taps[name]

    def declare(self):
        d = self.din
        d("x", [S, D]); d("cT", [128, 8]); d("posT", [128, NT], I32)
        d("ada_w", [2, D, 6 * D]); d("ada_b", [2, 6 * D])
        d("attn_norm", [2, D]); d("ffn_norm", [2, D])
        d("ffn_w_gate", [2, D, FFN]); d("ffn_w_up", [2, D, FFN]); d("ffn_w_down", [2, FFN, D])
        d("sp_w_in", [D, SP_IN]); d("sp_w_out", [D, D])
        d("moba_q_norm", [1, 64]); d("moba_k_norm", [1, 64]); d("nsa_q_norm", [1, 64]); d("nsa_k_norm", [3, 64])
        d("cmp_posT", [64, 2, 32]); d("cmp_w1", [64, 2, 32, 256]); d("cmp_w2", [2, 256, 64])
        d("diff_w_in", [D, 3072]); d("diff_w_out", [D, D])
        d("diff_q_norm", [1, 64]); d("diff_k_norm", [1, 64]); d("diff_lambda", [1, 256]); d("diff_out_norm", [1, 128])
        d("c_ident", [128, 128]); d("c_tri", [128, 128]); d("c_atri", [128, 128])
        d("c_e16", [16, S]); d("c_e64", [64, S]); d("c_ov", [256, 65])
        d("c_t1", [1, 256]); d("c_t2", [1, 256])
        d("c_nfv", [128, NT, 64]); d("c_cst", [128, NT, 64]); d("c_v01", [128, NT, 64])
        self.out = self.nc.dram_tensor("out", [S, D], F32, kind="ExternalOutput").ap()
        self.xmid = self.dscr("xmid", [S, D], F32)
        self.x1s = self.dscr("x1s", [S, D], F32)
        self.FT = self.dscr("FT", [2048, S], BF16)
        self.TM = self.dscr("TM", [S, D], BF16)
        self.OS = self.dscr("OS", [S, D], BF16)
        self.H2T = self.dscr("H2T", [D, S], BF16)

    def setup(self):
        kb, I = self.kb, self.I
        self.G = Scope(kb)
        G = self.G
        self.ident, self.Bident = G.sb("ident", [128, 128], BF16)
        kb.dma("POOL", self.ident[:], I["c_ident"][:, :], writes=[self.Bident])
        self.tri, self.Btri = G.sb("tri", [128, 128], BF16)
        kb.dma("POOL", self.tri[:], I["c_tri"][:, :], writes=[self.Btri])
        self.atri, self.Batri = G.sb("atri", [128, 128], BF16)
        kb.dma("POOL", self.atri[:], I["c_atri"][:, :], writes=[self.Batri])
        self.cosT, self.Bcos = G.sb("cosT", [128, NT, 8], F32)
        self.sinT, self.Bsin = G.sb("sinT", [128, NT, 8], F32)
        self.mod, self.Bmod = G.sb("mod", [128, 6, D], F32)
        self.silc, self.Bsilc = G.sb("silc", [128, 8, 128], F32)
        with_scope = Scope(kb)
        T = with_scope
        pi, Bpi = T.sb("pi", [128, NT], I32)
        pf, Bpf = T.sb("pf", [128, NT], F32)
        ang, Bang = T.sb("ang", [128, NT, 8], F32)
        tmp, Btmp = T.sb("tmp", [128, NT, 8], F32)
        tmp2, Btmp2 = T.sb("tmp2", [128, NT, 8], F32)
        kb.dma("SP", pi[:], I["posT"][:, :], writes=[Bpi])
        kb.op("DVE", lambda e: e.tensor_copy(pf[:], pi[:]), reads=[Bpi], writes=[Bpf])
        inv = (1.0 / (np.float32(500000.0) ** (np.arange(0, 16, 2, dtype=np.float32) / np.float32(16)))).astype(np.float32)
        for j in range(8):
            kb.op("DVE", lambda e, j=j: e.tensor_scalar(ang[:, :, j], pf[:], float(inv[j]), None, ALU.mult),
                  reads=[Bpf], writes=[Bang])
        TWO_PI = float(2 * np.pi)
        MAG = 12582912.0

        def sin_of(dst, Bdst, shift):
            if shift != 0.0:
                kb.op("DVE", lambda e: e.tensor_scalar(tmp2[:], ang[:], shift, None, ALU.add), reads=[Bang], writes=[Btmp2])
                src, Bsrc = tmp2, Btmp2
            else:
                src, Bsrc = ang, Bang
            kb.op("DVE", lambda e: e.tensor_scalar(tmp[:], src[:], 1.0 / TWO_PI, MAG, ALU.mult, ALU.add), reads=[Bsrc], writes=[Btmp])
            kb.op("DVE", lambda e: e.tensor_scalar(tmp[:], tmp[:], -MAG, -TWO_PI, ALU.add, ALU.mult), reads=[Btmp], writes=[Btmp])
            kb.op("DVE", lambda e: e.tensor_tensor(tmp[:], src[:], tmp[:], ALU.add), reads=[Bsrc, Btmp], writes=[Btmp])
            kb.op("DVE", lambda e: e.tensor_scalar(tmp[:], tmp[:], float(np.pi), float(-np.pi), ALU.min, ALU.max), reads=[Btmp], writes=[Btmp])
            kb.op("ACT", lambda e: e.activation(dst[:], tmp[:], AF.Sin), reads=[Btmp], writes=[Bdst])

        sin_of(self.sinT, self.Bsin, 0.0)
        sin_of(self.cosT, self.Bcos, float(np.pi / 2))
        ct, Bct = T.sb("ct", [128, 8], F32)
        kb.dma("SP", ct[:], I["cT"][:, :], writes=[Bct])
        kb.op("ACT", lambda e: e.activation(ct[:], ct[:], AF.Silu), reads=[Bct], writes=[Bct])
        kb.op("DVE", lambda e: e.tensor_copy(self.silc[:], ct[:].unsqueeze(2).to_broadcast([128, 8, 128])),
              reads=[Bct], writes=[self.Bsilc])
        T.close()

    def compute_mod(self, li):
        kb, I = self.kb, self.I
        T = Scope(kb)
        wch = [T.sb("adaw", [128, 8, 512], F32) for _ in range(2)]
        bch = [T.sb("adab", [128, 512], F32) for _ in range(2)]
        pm = [T.ps("pmod", [128, 512], F32) for _ in range(2)]
        nrm, Bnrm = T.sb("nrm", [128, 2, D], F32)
        kb.dma("SP", nrm[:, 0, :], bc_row(I["attn_norm"][li:li + 1, :], D), writes=[Bnrm])
        kb.dma("SP", nrm[:, 1, :], bc_row(I["ffn_norm"][li:li + 1, :], D), writes=[Bnrm])
        wv = I["ada_w"][li].rearrange("(kc p) n -> p kc n", p=128)
        for g in range(12):
            w, Bw = wch[g % 2]
            b, Bb = bch[g % 2]
            p, Bp = pm[g % 2]
            kb.dma("SP", w[:], wv[:, :, g * 512:(g + 1) * 512], writes=[Bw])
            kb.dma("SP", b[:], bc_row(I["ada_b"][li:li + 1, g * 512:(g + 1) * 512], 512), writes=[Bb])
            for kc in range(8):
                kb.op("PE", lambda e, kc=kc: e.matmul(p[:], self.silc[:, kc, :], w[:, kc, :], start=(kc == 0), stop=(kc == 7)),
                      reads=[Bw, self.Bsilc], writes=[Bp])
            sl = self.mod[:, g // 2, (g % 2) * 512:(g % 2 + 1) * 512]
            kb.op("DVE", lambda e: e.tensor_tensor(sl, p[:], b[:], ALU.add), reads=[Bp, Bb], writes=[self.Bmod])
        kb.op("DVE", lambda e: e.scalar_tensor_tensor(self.mod[:, 1, :], self.mod[:, 1, :], 1.0, nrm[:, 0, :], ALU.add, ALU.mult),
              reads=[self.Bmod, Bnrm], writes=[self.Bmod])
        kb.op("DVE", lambda e: e.scalar_tensor_tensor(self.mod[:, 4, :], self.mod[:, 4, :], 1.0, nrm[:, 1, :], ALU.add, ALU.mult),
              reads=[self.Bmod, Bnrm], writes=[self.Bmod])
        T.close()

    def rms_mod_tile(self, T, xt, Bxt, hb, Bhb, gi, si, scr):
        kb = self.kb
        junk, Bjunk, ss, Bss, hn, Bhn = scr
        kb.op("ACT", lambda e: e.activation(junk[:], xt[:], AF.Square, accum_out=ss[:]), reads=[Bxt], writes=[Bjunk, Bss])
        kb.op("ACT", lambda e: e.activation(ss[:], ss[:], AF.Sqrt, bias=self.epsb[:], scale=1.0 / D), reads=[Bss, self.Bepsb], writes=[Bss])
        kb.op("DVE", lambda e: e.reciprocal(ss[:], ss[:]), reads=[Bss], writes=[Bss])
        kb.op("DVE", lambda e: e.scalar_tensor_tensor(hn[:], xt[:], ss[:, 0:1], self.mod[:, gi, :], ALU.mult, ALU.mult),
              reads=[Bxt, Bss, self.Bmod], writes=[Bhn])
        kb.op("POOL", lambda e: e.tensor_tensor(hb[:], hn[:], self.mod[:, si, :], ALU.add), reads=[Bhn, self.Bmod], writes=[Bhb])

    def transpose8(self, src, Bsrc, pt, Bpt, dst_ap, Bdst, eng="ACT"):
        kb = self.kb
        for j in range(8):
            kb.op("PE", lambda e, j=j: e.transpose(pt[:, j * 128:(j + 1) * 128], src[:, j * 128:(j + 1) * 128], self.ident[:]),
                  reads=[Bsrc, self.Bident], writes=[Bpt])
        if eng == "ACT":
            kb.op("ACT", lambda e: e.copy(dst_ap, pt[:].rearrange("p (j t) -> p j t", j=8)), reads=[Bpt], writes=[Bdst])
        else:
            kb.op("DVE", lambda e: e.tensor_copy(dst_ap, pt[:].rearrange("p (j t) -> p j t", j=8)), reads=[Bpt], writes=[Bdst])

    def load_w_bf16(self, dst, Bdst, w_ap, nk):
        for kc in range(nk):
            self.kb.dma("POOL", dst[:, kc, :], w_ap[kc * 128:(kc + 1) * 128, :], writes=[Bdst])

    def phase_a(self, li, x_src):
        kb, I = self.kb, self.I
        T = Scope(kb)
        if li == 1:
            w_ap, ncols = I["diff_w_in"], 3072
            groups = []
            for g in range(6):
                if g < 4:
                    gi = 0 if g < 2 else 1
                    blocks = [("T", g * 512 + b * 128) for b in range(4)]
                    groups.append(dict(c0=g * 512, w=512, qk=[(0, 8)], gain=gi, blocks=blocks, gate=None))
                else:
                    blocks = [("V", (g - 4) * 512 + b * 128) for b in range(4)]
                    groups.append(dict(c0=g * 512, w=512, qk=[], gain=None, blocks=blocks, gate=None))
            gain_srcs = [[(I["diff_q_norm"][0:1, :], 0, 8)], [(I["diff_k_norm"][0:1, :], 0, 8)]]
        else:
            w_ap, ncols = I["sp_w_in"], SP_IN
            kn = I["nsa_k_norm"]
            groups = [
                dict(c0=0, w=512, qk=[(0, 8)], gain=0, blocks=[("T", b * 128) for b in range(4)], gate=None),
                dict(c0=512, w=512, qk=[(0, 8)], gain=1, blocks=[("T", 512 + b * 128) for b in range(4)], gate=None),
                dict(c0=1024, w=512, qk=[], gain=None, blocks=[("V", b * 128) for b in range(4)], gate=None),
                dict(c0=1536, w=512, qk=[(0, 8)], gain=2, blocks=[("T", 1024 + b * 128) for b in range(4)], gate=None),
                dict(c0=2048, w=512, qk=[(0, 2), (4, 6)], gain=3,
                     blocks=[("T", 1536), ("T", 1664), ("T", 1792), ("V", 512)], gate=None),
                dict(c0=2560, w=280, qk=[(0, 2)], gain=4, blocks=[("T", 1920), ("V", 640)], gate=(256, 24)),
            ]
            gain_srcs = [
                [(I["moba_q_norm"][0:1, :], 0, 8)], [(I["moba_k_norm"][0:1, :], 0, 8)], [(I["nsa_q_norm"][0:1, :], 0, 8)],
                [(kn[0:1, :], 0, 2), (kn[1:2, :], 4, 6)], [(kn[2:3, :], 0, 2)],
            ]
        nk = 8
        W, BW = T.sb("w_in", [128, nk, ncols], BF16)
        self.load_w_bf16(W, BW, w_ap, nk)
        gains = []
        for gs in gain_srcs:
            gr, Bgr = T.sb("gainrow", [128, 8, 64], F32)
            kb.op("POOL", lambda e: e.memset(gr[:], 1.0), writes=[Bgr])
            for (src, u0, u1) in gs:
                for u in range(u0, u1):
                    kb.dma("SP", gr[:, u, :], bc_row(src, 64), writes=[Bgr])
            gains.append((gr, Bgr))
        xts = Rot([T.sb("xt", [128, D], F32) for _ in range(2)])
        hbs = Rot([T.sb("hb", [128, D], BF16) for _ in range(2)])
        hTs = Rot([T.sb("hT", [128, 8, 128], BF16) for _ in range(2)])
        junk, Bjunk = T.sb("junk", [128, D], BF16)
        sss = Rot([T.sb("ss", [128, 1], F32) for _ in range(2)])
        hn, Bhn = T.sb("hn", [128, D], F32)
        sq, Bsq = T.sb("sq", [128, 512], F32)
        ss8s = Rot([T.sb("ss8", [128, 8], F32) for _ in range(2)])
        qns = Rot([T.sb("qn", [128, 8, 64], F32) for _ in range(2)])
        posts = Rot([T.sb("post", [128, 512], BF16) for _ in range(3)])
        rts = Rot([T.sb("rt", [128, 4, 8, 8], F32) for _ in range(2)])
        pts = Rot([T.ps("ptA", [128, 1024], BF16) for _ in range(2)])
        pgs = Rot([T.ps("pgA", [128, 512], F32) for _ in range(3)])
        ptb = Rot([T.ps("ptB", [128, 1024], BF16) for _ in range(2)])
        stages = {}
        for gi_, g in enumerate(groups):
            if any(k == "T" for k, _ in g["blocks"]):
                stages[gi_] = [T.sb("stage", [128, 4, 512], BF16) for _ in range(2)]
        def load_x(t):
            xt, Bxt = xts.next()
            kb.dma("SP", xt[:], x_src[t * 128:(t + 1) * 128, :], writes=[Bxt])
            return xt, Bxt

        nxt = load_x(0)
        for t in range(NT):
            xt, Bxt = nxt
            if t + 1 < NT:
                nxt = load_x(t + 1)
            hb, Bhb = hbs.next()
            ss, Bss = sss.next()
            self.rms_mod_tile(T, xt, Bxt, hb, Bhb, 1, 0, (junk, Bjunk, ss, Bss, hn, Bhn))
            hT, BhT = hTs.next()
            pt, Bpt = pts.next()
            self.transpose8(hb, Bhb, pt, Bpt, hT[:], BhT, eng="ACT")
            tt = t % 4
            half = (t // 4) % 2
            for gi_, g in enumerate(groups):
                w = g["w"]
                pg, Bpg = pgs.next()
                for kc in range(8):
                    kb.op("PE", lambda e, kc=kc: e.matmul(pg[:, 0:w], hT[:, kc, :], W[:, kc, g["c0"]:g["c0"] + w],
                                                          start=(kc == 0), stop=(kc == 7)),
                          reads=[BhT, BW], writes=[Bpg])
                post, Bpost = posts.next()
                wq = (w // 64) * 64
                nu = wq // 64
                if g["qk"]:
                    ss8, Bss8 = ss8s.next()
                    qn, Bqn = qns.next()
                    gr, Bgr = gains[g["gain"]]
                    kb.op("ACT", lambda e: e.activation(sq[:, 0:wq], pg[:, 0:wq], AF.Square), reads=[Bpg], writes=[Bsq])
                    kb.op("DVE", lambda e: e.tensor_reduce(ss8[:, 0:nu], sq[:, 0:wq].rearrange("p (u d) -> p u d", d=64), AX.X, ALU.add),
                          reads=[Bsq], writes=[Bss8])
                    kb.op("ACT", lambda e: e.activation(ss8[:, 0:nu], ss8[:, 0:nu], AF.Sqrt, bias=self.epsb[:], scale=1.0 / 64),
                          reads=[Bss8, self.Bepsb], writes=[Bss8])
                    kb.op("DVE", lambda e: e.reciprocal(ss8[:, 0:nu], ss8[:, 0:nu]), reads=[Bss8], writes=[Bss8])
                    qk_units = set()
                    for (u0, u1) in g["qk"]:
                        qk_units.update(range(u0, u1))
                    raw_units = [u for u in range(nu) if u not in qk_units]
                    for u in raw_units:
                        kb.op("DVE", lambda e, u=u: e.memset(ss8[:, u:u + 1], 1.0), writes=[Bss8])
                    kb.op("DVE", lambda e: e.tensor_tensor(qn[:, 0:nu, :], pg[:, 0:wq].rearrange("p (u d) -> p u d", d=64),
                                                           ss8[:, 0:nu].unsqueeze(2).to_broadcast([128, nu, 64]), ALU.mult),
                          reads=[Bpg, Bss8], writes=[Bqn])
                    kb.op("DVE", lambda e: e.tensor_tensor(qn[:, 0:nu, :], qn[:, 0:nu, :], gr[:, 0:nu, :], ALU.mult),
                          reads=[Bqn, Bgr], writes=[Bqn])
                    kb.op("ACT", lambda e: e.copy(post[:, 0:wq], qn[:, 0:nu, :].rearrange("p u d -> p (u d)")), reads=[Bqn], writes=[Bpost])
                    postv = post[:, 0:wq].rearrange("p (u d) -> p u d", d=64)
                    rt, Brt = rts.next()
                    for (u0, u1) in g["qk"]:
                        n_ = u1 - u0
                        cosb = self.cosT[:, t, :].unsqueeze(1).to_broadcast([128, n_, 8])
                        sinb = self.sinT[:, t, :].unsqueeze(1).to_broadcast([128, n_, 8])
                        t1 = qn[:, u0:u1, 0:8]
                        t2 = qn[:, u0:u1, 8:16]
                        kb.op("DVE", lambda e: e.tensor_tensor(rt[:, 0, 0:n_, :], t1, cosb, ALU.mult), reads=[Bqn, self.Bcos], writes=[Brt])
                        kb.op("DVE", lambda e: e.tensor_tensor(rt[:, 1, 0:n_, :], t2, sinb, ALU.mult), reads=[Bqn, self.Bsin], writes=[Brt])
                        kb.op("DVE", lambda e: e.tensor_tensor(rt[:, 2, 0:n_, :], t2, cosb, ALU.mult), reads=[Bqn, self.Bcos], writes=[Brt])
                        kb.op("DVE", lambda e: e.tensor_tensor(rt[:, 3, 0:n_, :], t1, sinb, ALU.mult), reads=[Bqn, self.Bsin], writes=[Brt])
                        kb.op("DVE", lambda e: e.tensor_tensor(postv[:, u0:u1, 0:8], rt[:, 0, 0:n_, :], rt[:, 1, 0:n_, :], ALU.subtract),
                              reads=[Brt], writes=[Bpost])
                        kb.op("DVE", lambda e: e.tensor_tensor(postv[:, u0:u1, 8:16], rt[:, 2, 0:n_, :], rt[:, 3, 0:n_, :], ALU.add),
                              reads=[Brt], writes=[Bpost])
                else:
                    kb.op("ACT", lambda e: e.copy(post[:, 0:wq], pg[:, 0:wq]), reads=[Bpg], writes=[Bpost])
                if g["gate"] is not None:
                    gc0, gw = g["gate"]
                    kb.op("ACT", lambda e: e.activation(self.gate_sb[:, t, :], pg[:, gc0:gc0 + gw], AF.Sigmoid),
                          reads=[Bpg], writes=[self.Bgate])
                tblocks = [(bi, dst) for bi, (k, dst) in enumerate(g["blocks"]) if k == "T"]
                if tblocks:
                    pb, Bpb = ptb.next()
                    st, Bst = stages[gi_][half]
                    for bi, dst in tblocks:
                        kb.op("PE", lambda e, bi=bi: e.transpose(pb[:, bi * 128:(bi + 1) * 128], post[:, bi * 128:(bi + 1) * 128], self.ident[:]),
                              reads=[Bpost, self.Bident], writes=[Bpb])
                    b0, b1 = tblocks[0][0], tblocks[-1][0] + 1
                    kb.op("ACT", lambda e: e.copy(st[:, b0:b1, tt * 128:(tt + 1) * 128],
                                                  pb[:, b0 * 128:b1 * 128].rearrange("p (b t) -> p b t", t=128)),
                          reads=[Bpb], writes=[Bst])
                    if tt == 3:
                        t0 = (t - 3) * 128
                        for bi, dst in tblocks:
                            kb.dma("POOL", self.FT[dst:dst + 128, t0:t0 + 512], st[:, bi, :], reads=[Bst])
                for bi, (k, dst) in enumerate(g["blocks"]):
                    if k == "V":
                        kb.dma("POOL", self.TM[t * 128:(t + 1) * 128, dst:dst + 128], post[:, bi * 128:(bi + 1) * 128], reads=[Bpost])
        T.close()

    def attn_qtile(self, R, q_rhs, Bq, k_lhsT, Bk, v_rhs, Bv, vw, pairs, O, BO, o_off, post_exp=None, bank_of=lambda s: 0):
        kb = self.kb
        LOOK = self.cfg.get("look", 2)
        last = {}
        started = set()
        for kc, subs in pairs:
            for s, kind in subs:
                last[s] = kc
        live = {}

        def stage1(i):
            kc, subs = pairs[i]
            pS, BpS = R["pS"].next()
            kl, krows = k_lhsT(kc)
            kb.op("PE", lambda e: e.matmul(pS[0:krows, :], kl, q_rhs, start=True, stop=True), reads=[Bk, Bq], writes=[BpS])
            pT, BpT = R["pT"].next()
            kb.op("ACT", lambda e: e.activation(pT[0:krows, :], pS[0:krows, :], AF.Exp, scale=0.125), reads=[BpS], writes=[BpT])
            if post_exp is not None:
                post_exp(kc, pT, BpT, krows)
            for s, kind in subs:
                if kind == "tri":
                    kb.op("DVE", lambda e, s=s: e.tensor_tensor(pT[:, s * 128:(s + 1) * 128], pT[:, s * 128:(s + 1) * 128], self.tri[:], ALU.mult),
                          reads=[BpT, self.Btri], writes=[BpT])
                elif kind == "atri":
                    kb.op("DVE", lambda e, s=s: e.tensor_tensor(pT[:, s * 128:(s + 1) * 128], pT[:, s * 128:(s + 1) * 128], self.atri[:], ALU.mult),
                          reads=[BpT, self.Batri], writes=[BpT])
            live[i] = (pT, BpT, krows)

        def stage2(i):
            kc, subs = pairs[i]
            pT, BpT, krows = live.pop(i)
            for s, kind in subs:
                bk = bank_of(s)
                st_flag = bk not in started
                started.add(bk)
                kb.op("PE", lambda e, s=s, st_flag=st_flag: e.matmul(o_off(O, s), pT[0:krows, s * 128:(s + 1) * 128], v_rhs(kc)[0:krows, :],
                                                                     start=st_flag, stop=(last[s] == kc), skip_group_check=True),
                      reads=[BpT, Bv], writes=[BO])

        n = len(pairs)
        for i in range(n + LOOK):
            if i < n:
                stage1(i)
            if i - LOOK >= 0:
                stage2(i - LOOK)

    @staticmethod
    def causal_pairs(qt):
        pairs = []
        for kc in range(4 * qt + 4):
            j = kc - 4 * qt
            if j < 0:
                pairs.append((kc, [(s, "full") for s in range(4)]))
            else:
                pairs.append((kc, [(s, "tri" if s == j else "full") for s in range(j, 4)]))
        return pairs

    def phase_b_diff(self, li_odd_index):
        kb, I = self.kb, self.I
        T = Scope(kb)
        lam_init = 0.8 - 0.6 * float(np.exp(-0.3 * 1))
        lp, Blp = T.sb("lp", [128, 4, 64], F32)
        kb.dma("SP", lp[:].rearrange("p a d -> p (a d)"), bc_row(I["diff_lambda"][0:1, :], 256), writes=[Blp])
        l2, Bl2 = T.sb("l2", [128, 2, 64], F32)
        kb.op("DVE", lambda e: e.tensor_tensor(l2[:, 0, :], lp[:, 0, :], lp[:, 1, :], ALU.mult), reads=[Blp], writes=[Bl2])
        kb.op("DVE", lambda e: e.tensor_tensor(l2[:, 1, :], lp[:, 2, :], lp[:, 3, :], ALU.mult), reads=[Blp], writes=[Bl2])
        ls, Bls = T.sb("ls", [128, 2], F32)
        kb.op("DVE", lambda e: e.tensor_reduce(ls[:], l2[:], AX.X, ALU.add), reads=[Bl2], writes=[Bls])
        kb.op("ACT", lambda e: e.activation(ls[:], ls[:], AF.Exp), reads=[Bls], writes=[Bls])
        nlam, Bnlam = T.sb("nlam", [128, 1], F32)
        kb.op("DVE", lambda e: e.tensor_tensor(nlam[:], ls[:, 1:2], ls[:, 0:1], ALU.subtract), reads=[Bls], writes=[Bnlam])
        kb.op("DVE", lambda e: e.tensor_scalar(nlam[:], nlam[:], -lam_init, None, ALU.add), reads=[Bnlam], writes=[Bnlam])
        go, Bgo = T.sb("go", [128, 128], F32)
        kb.dma("SP", go[:], bc_row(I["diff_out_norm"][0:1, :], 128), writes=[Bgo])
        kb.op("DVE", lambda e: e.tensor_scalar(go[:], go[:], 1.0 - lam_init, None, ALU.mult), reads=[Bgo], writes=[Bgo])

        KTs = Rot([T.sb("KT", [128, S], BF16) for _ in range(2)])
        QTs = Rot([T.sb("QT", [128, S], BF16) for _ in range(2)])
        Vs = Rot([T.sb("V", [128, NT, 129], BF16) for _ in range(2)])
        for (v, Bv) in Vs.items:
            kb.op("POOL", lambda e, v=v: e.memset(v[:, :, 128:129], 1.0), writes=[Bv])
        R = dict(pS=Rot([T.ps("pS", [128, 512], F32) for _ in range(3)]),
                 pT=Rot([T.sb("pT", [128, 512], BF16) for _ in range(4)]))
        Os = [[T.ps("O", [128, 512], F32) for _ in range(2)] for _ in range(2)]
        osts = Rot([T.sb("ost", [128, 4, 128], BF16) for _ in range(2)])
        a0s = Rot([T.sb("a0", [128, 128], F32) for _ in range(2)])
        a1s = Rot([T.sb("a1", [128, 128], F32) for _ in range(2)])
        rds = Rot([T.sb("rd", [128, 4], F32) for _ in range(4)])
        junk, Bjunk = T.sb("junkb", [128, 128], BF16)

        def load_head(h):
            KT, BKT = KTs.next()
            QT, BQT = QTs.next()
            V, BV = Vs.next()
            kb.dma("SP", QT[:], self.FT[h * 128:(h + 1) * 128, :], writes=[BQT])
            kb.dma("SP", KT[:], self.FT[1024 + h * 128:1024 + (h + 1) * 128, :], writes=[BKT])
            tmv = self.TM[:, h * 128:(h + 1) * 128].rearrange("(c p) d -> p c d", p=128)
            for c4 in range(4):
                kb.dma("SP", V[:, c4 * 8:(c4 + 1) * 8, 0:128], tmv[:, c4 * 8:(c4 + 1) * 8, :], writes=[BV])
            return (KT, BKT, QT, BQT, V, BV)

        nxt = load_head(0)
        for h in range(8):
            KT, BKT, QT, BQT, V, BV = nxt
            if h + 1 < 8:
                nxt = load_head(h + 1)
            for qt in range(8):
                pairs = self.causal_pairs(qt)
                for c in range(2):
                    def o_off(O, s, c=c):
                        return Os[c][s // 2][0][:, (s % 2) * 256:(s % 2) * 256 + 129]
                    self.attn_qtile(R, QT[64 * c:64 * c + 64, qt * 512:(qt + 1) * 512], BQT,
                                    lambda kc, c=c: (KT[64 * c:64 * c + 64, kc * 128:(kc + 1) * 128], 128), BKT,
                                    lambda kc: V[:, kc, :], BV, 129, pairs, None, Os[c][0][1], o_off, bank_of=lambda s: s // 2)
                ost, Bost = osts.next()
                for s in range(4):
                    O0 = Os[0][s // 2][0][:, (s % 2) * 256:(s % 2) * 256 + 129]
                    O1 = Os[1][s // 2][0][:, (s % 2) * 256:(s % 2) * 256 + 129]
                    BO0, BO1 = Os[0][0][1], Os[1][0][1]
                    rd, Brd = rds.next()
                    a0, Ba0 = a0s.next()
                    a1, Ba1 = a1s.next()
                    kb.op("DVE", lambda e: e.reciprocal(rd[:, 0:1], O0[:, 128:129]), reads=[BO0], writes=[Brd])
                    kb.op("DVE", lambda e: e.reciprocal(rd[:, 1:2], O1[:, 128:129]), reads=[BO1], writes=[Brd])
                    kb.op("DVE", lambda e: e.tensor_tensor(rd[:, 1:2], rd[:, 1:2], nlam[:], ALU.mult), reads=[Brd, Bnlam], writes=[Brd])
                    kb.op("DVE", lambda e: e.tensor_scalar(a0[:], O0[:, 0:128], rd[:, 0:1], None, ALU.mult), reads=[BO0, Brd], writes=[Ba0])
                    kb.op("DVE", lambda e: e.scalar_tensor_tensor(a1[:], O1[:, 0:128], rd[:, 1:2], a0[:], ALU.mult, ALU.add),
                          reads=[BO1, Brd, Ba0], writes=[Ba1])
                    kb.op("ACT", lambda e: e.activation(junk[:], a1[:], AF.Square, accum_out=rd[:, 2:3]), reads=[Ba1], writes=[Bjunk, Brd])
                    kb.op("ACT", lambda e: e.activation(rd[:, 2:3], rd[:, 2:3], AF.Sqrt, bias=self.epsb[:], scale=1.0 / 128),
                          reads=[Brd, self.Bepsb], writes=[Brd])
                    kb.op("DVE", lambda e: e.reciprocal(rd[:, 3:4], rd[:, 2:3]), reads=[Brd], writes=[Brd])
                    kb.op("DVE", lambda e, s=s: e.scalar_tensor_tensor(ost[:, s, :], a1[:], rd[:, 3:4], go[:], ALU.mult, ALU.mult),
                          reads=[Ba1, Brd, Bgo], writes=[Bost])
                osv = self.OS[qt * 512:(qt + 1) * 512, h * 128:(h + 1) * 128].rearrange("(s p) d -> p s d", p=128)
                kb.dma("POOL", osv, ost[:], reads=[Bost])
        T.close()

    def phase_c1(self, li, x_src, w_out_ap):
        kb, I = self.kb, self.I
        T = Scope(kb)
        W, BW = T.sb("w_out", [128, 8, D], BF16)
        self.load_w_bf16(W, BW, w_out_ap, 8)
        ots = Rot([T.sb("ot", [128, D], BF16) for _ in range(2)])
        xts = Rot([T.sb("xt", [128, D], F32) for _ in range(2)])
        oTs = Rot([T.sb("oT", [128, 8, 128], BF16) for _ in range(2)])
        x1s = Rot([T.sb("x1", [128, D], F32) for _ in range(2)])
        hbs = Rot([T.sb("hb", [128, D], BF16) for _ in range(2)])
        ytmp, Bytmp = T.sb("ytmp", [128, D], F32)
        junk, Bjunk = T.sb("junk", [128, D], BF16)
        sss = Rot([T.sb("ss", [128, 1], F32) for _ in range(2)])
        hn, Bhn = T.sb("hn", [128, D], F32)
        stg = [T.sb("stage", [128, 8, 512], BF16) for _ in range(2)]
        pts = Rot([T.ps("ptA", [128, 1024], BF16) for _ in range(2)])
        pys = Rot([T.ps("py", [128, 512], F32) for _ in range(4)])

        def load(t):
            ot, Bot = ots.next()
            xt, Bxt = xts.next()
            kb.dma("SP", ot[:], self.OS[t * 128:(t + 1) * 128, :], writes=[Bot])
            kb.dma("SP", xt[:], x_src[t * 128:(t + 1) * 128, :], writes=[Bxt])
            return ot, Bot, xt, Bxt

        nxt = load(0)
        for t in range(NT):
            ot, Bot, xt, Bxt = nxt
            if t + 1 < NT:
                nxt = load(t + 1)
            oT, BoT = oTs.next()
            pt, Bpt = pts.next()
            self.transpose8(ot, Bot, pt, Bpt, oT[:], BoT, eng="ACT")
            x1, Bx1 = x1s.next()
            for g in range(2):
                py, Bpy = pys.next()
                for kc in range(8):
                    kb.op("PE", lambda e, kc=kc: e.matmul(py[:], oT[:, kc, :], W[:, kc, g * 512:(g + 1) * 512], start=(kc == 0), stop=(kc == 7)),
                          reads=[BoT, BW], writes=[Bpy])
                sl = slice(g * 512, (g + 1) * 512)
                kb.op("DVE", lambda e: e.tensor_tensor(ytmp[:, sl], py[:], self.mod[:, 2, sl], ALU.mult), reads=[Bpy, self.Bmod], writes=[Bytmp])
                kb.op("POOL", lambda e: e.tensor_tensor(x1[:, sl], ytmp[:, sl], xt[:, sl], ALU.add), reads=[Bytmp, Bxt], writes=[Bx1])
            kb.dma("POOL", self.x1s[t * 128:(t + 1) * 128, :], x1[:], reads=[Bx1])
            hb, Bhb = hbs.next()
            ss, Bss = sss.next()
            self.rms_mod_tile(T, x1, Bx1, hb, Bhb, 4, 3, (junk, Bjunk, ss, Bss, hn, Bhn))
            pt, Bpt = pts.next()
            tt = t % 4
            st, Bst = stg[(t // 4) % 2]
            self.transpose8(hb, Bhb, pt, Bpt, st[:, :, tt * 128:(tt + 1) * 128], Bst, eng="ACT")
            if tt == 3:
                t0 = (t - 3) * 128
                h2v = self.H2T.rearrange("(j p) t -> p j t", p=128)
                kb.dma("POOL", h2v[:, :, t0:t0 + 512], st[:], reads=[Bst])
        T.close()

    def phase_c2(self, li, x_dst):
        kb, I = self.kb, self.I
        T = Scope(kb)
        Wg, BWg = T.sb("wg", [128, 8, FFN], BF16)
        Wu, BWu = T.sb("wu", [128, 8, FFN], BF16)
        Wd, BWd = T.sb("wd", [128, NJ, D], BF16)
        self.load_w_bf16(Wg, BWg, I["ffn_w_gate"][li], 8)
        self.load_w_bf16(Wu, BWu, I["ffn_w_up"][li], 8)
        self.load_w_bf16(Wd, BWd, I["ffn_w_down"][li], NJ)
        TT = 256
        h2s = Rot([T.sb("h2T", [128, 8, TT], BF16) for _ in range(2)])
        act, Bact = T.sb("act", [128, NJ, TT], BF16)
        sgs = Rot([T.sb("sg", [128, TT], F32) for _ in range(2)])
        xts = Rot([T.sb("x1t", [128, D], F32) for _ in range(2)])
        ytmp, Bytmp = T.sb("ytmp", [128, 512], F32)
        pgs = Rot([T.ps("pg", [128, 512], F32) for _ in range(2)])
        pus = Rot([T.ps("pu", [128, 512], F32) for _ in range(2)])
        pys = Rot([T.ps("py", [128, 512], F32) for _ in range(3)])
        h2v = self.H2T.rearrange("(j p) t -> p j t", p=128)

        def load(st):
            h2, Bh2 = h2s.next()
            kb.dma("SP", h2[:], h2v[:, :, st * TT:(st + 1) * TT], writes=[Bh2])
            return h2, Bh2

        nxt = load(0)
        for st in range(S // TT):
            h2, Bh2 = nxt
            if st + 1 < S // TT:
                nxt = load(st + 1)
            for j in range(NJ):
                pg, Bpg = pgs.next()
                pu, Bpu = pus.next()
                for kc in range(8):
                    kb.op("PE", lambda e, kc=kc: e.matmul(pg[:, 0:TT], Wg[:, kc, j * 128:(j + 1) * 128], h2[:, kc, :], start=(kc == 0), stop=(kc == 7)),
                          reads=[BWg, Bh2], writes=[Bpg])
                for kc in range(8):
                    kb.op("PE", lambda e, kc=kc: e.matmul(pu[:, 0:TT], Wu[:, kc, j * 128:(j + 1) * 128], h2[:, kc, :], start=(kc == 0), stop=(kc == 7)),
                          reads=[BWu, Bh2], writes=[Bpu])
                sg, Bsg = sgs.next()
                kb.op("ACT", lambda e: e.activation(sg[:], pg[:, 0:TT], AF.Silu), reads=[Bpg], writes=[Bsg])
                kb.op("DVE", lambda e, j=j: e.tensor_tensor(act[:, j, :], sg[:], pu[:, 0:TT], ALU.mult), reads=[Bsg, Bpu], writes=[Bact])
            for q in range(TT // 128):
                t = st * (TT // 128) + q
                xt, Bxt = xts.next()
                kb.dma("SP", xt[:], self.x1s[t * 128:(t + 1) * 128, :], writes=[Bxt])
                for g in range(2):
                    py, Bpy = pys.next()
                    for j in range(NJ):
                        kb.op("PE", lambda e, j=j: e.matmul(py[:], act[:, j, q * 128:(q + 1) * 128], Wd[:, j, g * 512:(g + 1) * 512],
                                                            start=(j == 0), stop=(j == NJ - 1)),
                              reads=[Bact, BWd], writes=[Bpy])
                    sl = slice(g * 512, (g + 1) * 512)
                    kb.op("DVE", lambda e: e.tensor_tensor(ytmp[:], py[:], self.mod[:, 5, sl], ALU.mult), reads=[Bpy, self.Bmod], writes=[Bytmp])
                    kb.op("POOL", lambda e: e.tensor_tensor(xt[:, sl], ytmp[:], xt[:, sl], ALU.add), reads=[Bytmp, Bxt], writes=[Bxt])
                kb.dma("POOL", x_dst[t * 128:(t + 1) * 128, :], xt[:], reads=[Bxt])
        T.close()

    def build(self):
        kb = self.kb
        self.declare()
        self.setup()
        G = self.G
        self.epsb, self.Bepsb = G.sb("epsb", [128, 1], F32)
        kb.op("POOL", lambda e: e.memset(self.epsb[:], EPS), writes=[self.Bepsb])
        layers = self.cfg.get("layers", [0, 1])
        x_src = self.I["x"]
        for n, li in enumerate(layers):
            x_dst = self.out if n == len(layers) - 1 else self.xmid
            self.L = Scope(kb)
            if li == 0:
                self.gate_sb, self.Bgate = self.L.sb("gates", [128, NT, 24], F32)
            self.compute_mod(li)
            stop = self.cfg.get("stop")
            if stop == "mod":
                self.L.close()
                break
            if li == 0:
                self.phase_a(0, x_src)
                if stop == "A":
                    self.L.close()
                    break
                if not self.cfg.get("skip_moba"):
                    self.moba_part()
                if stop == "moba":
                    self.L.close()
                    break
                self.nsa_part()
                if stop in ("nsa", "cmpmlp"):
                    self.L.close()
                    break
                self.phase_c1(0, x_src, self.I["sp_w_out"])
            else:
                self.phase_a(1, x_src)
                if stop == "A":
                    self.L.close()
                    break
                self.phase_b_diff(0)
                if stop == "B":
                    self.L.close()
                    break
                self.phase_c1(1, x_src, self.I["diff_w_out"])
            if stop == "C1":
                self.L.close()
                break
            self.phase_c2(li, x_dst)
            self.L.close()
            x_src = x_dst
        kb.barrier(engines=("POOL",))
        G.es.close()
        kb.es.close()
        return self.nc

    def epi_norm(self, T, O_ap, BO, vcol, rd_ap, Brd):
        kb = self.kb
        kb.op("DVE", lambda e: e.tensor_scalar(rd_ap, O_ap[:, vcol:vcol + 1], 1e-30, None, ALU.max), reads=[BO], writes=[Brd])
        kb.op("DVE", lambda e: e.reciprocal(rd_ap, rd_ap), reads=[Brd], writes=[Brd])

    def phase_b_sparse(self):
        self.moba_part()
        self.nsa_part()

    def moba_part(self):
        kb, I = self.kb, self.I
        T = Scope(kb)
        QAs = Rot([T.sb("QA", [128, 2, S], BF16) for _ in range(2)])
        KAs = Rot([T.sb("KA", [128, 2, S], BF16) for _ in range(2)])
        Vps = Rot([T.sb("Vp", [128, NT, 2, 65], BF16) for _ in range(2)])
        for (ka, Bka) in KAs.items:
            for hh in range(2):
                kb.dma("POOL", ka[64:80, hh, :], I["c_e16"][:, :], writes=[Bka])
        for (v, Bv) in Vps.items:
            kb.op("POOL", lambda e, v=v: e.memset(v[:, :, :, 64:65], 1.0), writes=[Bv])
        t1, Bt1 = T.sb("t1", [128, 16, 16], F32)
        t2, Bt2 = T.sb("t2", [128, 16, 16], F32)
        kb.dma("SP", t1[:].rearrange("p a b -> p (a b)"), bc_row(I["c_t1"][0:1, :], 256), writes=[Bt1])
        kb.dma("SP", t2[:].rearrange("p a b -> p (a b)"), bc_row(I["c_t2"][0:1, :], 256), writes=[Bt2])
        augs = Rot([T.sb("aug", [128, 128], BF16) for _ in range(2)])
        for (a, Ba) in augs.items:
            kb.op("POOL", lambda e, a=a: e.memset(a[:], 0.0), writes=[Ba])
        km, Bkm = T.sb("km", [64, 16], F32)
        kmb, Bkmb = T.sb("kmb", [64, 16], BF16)
        gms = Rot([T.sb("gm", [128, 16], F32) for _ in range(2)])
        m8s = Rot([T.sb("m8", [128, 8], F32) for _ in range(2)])
        sels = Rot([T.sb("sel", [128, 16], F32) for _ in range(2)])
        R = dict(pS=Rot([T.ps("pS", [128, 512], F32) for _ in range(3)]),
                 pT=Rot([T.sb("pT", [128, 512], BF16) for _ in range(4)]))
        Obs = Rot([T.ps("Om", [128, 512], F32) for _ in range(2)])
        pgs = Rot([T.ps("pgate", [128, 512], F32) for _ in range(2)])
        ptr = Rot([T.ps("ptr", [128, 1024], BF16) for _ in range(1)])
        osts = Rot([T.sb("ost", [128, 4, 128], BF16) for _ in range(2)])
        rds = Rot([T.sb("rd", [128, 1], F32) for _ in range(4)])

        def load_pair(hp):
            QA, BQA = QAs.next()
            KA, BKA = KAs.next()
            Vp, BVp = Vps.next()
            for hh in range(2):
                h = 2 * hp + hh
                kb.dma("SP", QA[0:64, hh, :], self.FT[h * 64:(h + 1) * 64, :], writes=[BQA])
                kb.dma("SP", KA[0:64, hh, :], self.FT[512 + h * 64:512 + (h + 1) * 64, :], writes=[BKA])
            for hh in range(2):
                h = 2 * hp + hh
                tmv = self.TM[:, h * 64:(h + 1) * 64].rearrange("(c p) d -> p c d", p=128)
                for c4 in range(4):
                    kb.dma("SP", Vp[:, c4 * 8:(c4 + 1) * 8, hh, 0:64], tmv[:, c4 * 8:(c4 + 1) * 8, :], writes=[BVp])
            return QA, BQA, KA, BKA, Vp, BVp

        nxt = load_pair(0)
        for hp in range(4):
            QA, BQA, KA, BKA, Vp, BVp = nxt
            if hp + 1 < 4:
                nxt = load_pair(hp + 1)
            for hh in range(2):
                kb.op("DVE", lambda e: e.tensor_reduce(km[:], KA[0:64, hh, :].rearrange("p (b j) -> p b j", j=256), AX.X, ALU.add),
                      reads=[BKA], writes=[Bkm])
                kb.op("DVE", lambda e: e.tensor_scalar(kmb[:], km[:], 1.0 / 256, None, ALU.mult), reads=[Bkm], writes=[Bkmb])
                for t in range(NT):
                    own = t // 2
                    pg, Bpg = pgs.next()
                    kb.op("PE", lambda e: e.matmul(pg[:, 0:16], QA[0:64, hh, t * 128:(t + 1) * 128], kmb[:], start=True, stop=True),
                          reads=[BQA, Bkmb], writes=[Bpg])
                    gm, Bgm = gms.next()
                    m8, Bm8 = m8s.next()
                    sel, Bsel = sels.next()
                    aug, Baug = augs.next()
                    kb.op("DVE", lambda e: e.tensor_tensor(gm[:], pg[:, 0:16], t1[:, own, :], ALU.add), reads=[Bpg, Bt1], writes=[Bgm])
                    kb.op("DVE", lambda e: e.max(m8[:], gm[:]), reads=[Bgm], writes=[Bm8])
                    kb.op("DVE", lambda e: e.tensor_scalar(m8[:, 2:3], m8[:, 2:3], -1e29, None, ALU.max), reads=[Bm8], writes=[Bm8])
                    kb.op("DVE", lambda e: e.tensor_scalar(sel[:], gm[:], m8[:, 2:3], None, ALU.is_ge), reads=[Bgm, Bm8], writes=[Bsel])
                    kb.op("DVE", lambda e: e.tensor_tensor(sel[:], sel[:], t2[:, own, :], ALU.max), reads=[Bsel, Bt2], writes=[Bsel])
                    kb.op("DVE", lambda e: e.tensor_scalar(aug[:, 64:80], sel[:], -1.0, -MASKV, ALU.add, ALU.mult), reads=[Bsel], writes=[Baug])
                    pt, Bpt = ptr.next()
                    kb.op("PE", lambda e: e.transpose(pt[:, 0:128], aug[:], self.ident[:]), reads=[Baug, self.Bident], writes=[Bpt])
                    kb.op("ACT", lambda e: e.copy(QA[64:80, hh, t * 128:(t + 1) * 128], pt[64:80, 0:128]), reads=[Bpt], writes=[BQA])
            for qt in range(8):
                pairs = self.causal_pairs(qt)
                ost, Bost = osts.next()
                for hh in range(2):
                    Om, BOm = Obs.next()
                    self.attn_qtile(R, QA[0:80, hh, qt * 512:(qt + 1) * 512], BQA,
                                    lambda kc: (KA[0:80, hh, kc * 128:(kc + 1) * 128], 128), BKA,
                                    lambda kc: Vp[:, kc, hh, :], BVp, 65, pairs, Om, BOm,
                                    lambda O, s: O[:, s * 128:s * 128 + 65])
                    for s in range(4):
                        rd, Brd = rds.next()
                        Os_ = Om[:, s * 128:s * 128 + 65]
                        self.epi_norm(T, Os_, BOm, 64, rd[:], Brd)
                        kb.op("DVE", lambda e, s=s: e.tensor_scalar(ost[:, s, hh * 64:(hh + 1) * 64], Os_[:, 0:64], rd[:, 0:1], None, ALU.mult),
                              reads=[BOm, Brd], writes=[Bost])
                osv = self.OS[qt * 512:(qt + 1) * 512, hp * 128:(hp + 1) * 128].rearrange("(s p) d -> p s d", p=128)
                kb.dma("POOL", osv, ost[:], reads=[Bost])
        T.close()

    def nsa_part(self):
        kb, I = self.kb, self.I
        for _ in range(self.cfg.get("pad_dve", 0)):
            kb.op("DVE", lambda e: e.memset(self.epsb[:], EPS), writes=[self.Bepsb])
        P = Scope(kb)
        CKc, BCKc = P.sb("CKc", [64, 2, 256], BF16)
        VCs = [P.sb("VC", [128, 2, 129], BF16) for _ in range(2)]
        kb.op("POOL", lambda e: e.memset(CKc[:], 0.0), writes=[BCKc])
        for g in range(2):
            vc, Bvc = VCs[g]
            for c in range(2):
                kb.dma("POOL", vc[:, c, 64:129], I["c_ov"][c * 128:(c + 1) * 128, :], writes=[Bvc])
        T = Scope(kb)
        CX = [T.sb("CX", [128, S], BF16) for _ in range(2)]
        kb.dma("SP", CX[0][0][:], self.FT[1536:1664, :], writes=[CX[0][1]])
        kb.dma("SP", CX[1][0][:], self.FT[1664:1792, :], writes=[CX[1][1]])
        w1, Bw1 = T.sb("w1", [128, 2 * 32 * 256], BF16)
        w1src = I["cmp_w1"].rearrange("d a l e -> d (a l e)")
        for half in range(2):
            for a in range(2):
                kb.dma("POOL", w1[half * 64:(half + 1) * 64, a * 8192:(a + 1) * 8192], w1src[:, a * 8192:(a + 1) * 8192], writes=[Bw1])
        w1v = w1[:].rearrange("p (a l e) -> p a l e", a=2, l=32)
        posb, Bposb = T.sb("posb", [64, 2, 34], BF16)
        kb.op("POOL", lambda e: e.memset(posb[:], 0.0), writes=[Bposb])
        kb.dma("POOL", posb[:, :, 0:32], I["cmp_posT"][:, :, :], writes=[Bposb])
        w2, Bw2 = T.sb("w2", [128, 2, 2, 64], BF16)
        for kv in range(2):
            for eh in range(2):
                kb.dma("POOL", w2[:, kv, eh, :], I["cmp_w2"][kv, eh * 128:(eh + 1) * 128, :], writes=[Bw2])
        b1, Bb1 = T.sb("b1", [128, 4], F32)
        pbs = Rot([T.ps("pb", [128, 512], F32) for _ in range(2)])
        phs = Rot([T.ps("ph", [128, 512], F32) for _ in range(3)])
        for kv in range(2):
            for eh in range(2):
                pb, Bpb = pbs.next()
                for l in range(32):
                    kb.op("PE", lambda e, l=l: e.matmul(pb[:, 0:2], w1v[0:64, kv, l, eh * 128:(eh + 1) * 128],
                                                        posb[0:64, kv, l:l + 2], start=(l == 0), stop=(l == 31)),
                          reads=[Bw1, Bposb], writes=[Bpb])
                kb.op("DVE", lambda e: e.tensor_copy(b1[:, kv * 2 + eh:kv * 2 + eh + 1], pb[:, 0:1]), reads=[Bpb], writes=[Bb1])
        hid = {}
        for kv in range(2):
            cx, Bcx = CX[kv]
            cxv = cx[:].rearrange("p (n j) -> p n j", j=16)
            for g in range(2):
                for eh in range(2):
                    ph, Bph = phs.next()
                    for l in range(32):
                        a = l // 16
                        kb.op("PE", lambda e, l=l, a=a: e.matmul(ph[:, 0:255], w1v[64 * g:64 * g + 64, kv, l, eh * 128:(eh + 1) * 128],
                                                                 cxv[64 * g:64 * g + 64, a:a + 255, l % 16], start=(l == 0), stop=(l == 31)),
                              reads=[Bw1, Bcx], writes=[Bph])
                    ht, Bht = T.sb("hid", [128, 256], BF16)
                    kb.op("POOL", lambda e: e.memset(ht[:], 0.0), writes=[Bht])
                    kb.op("ACT", lambda e: e.activation(ht[:, 0:255], ph[:, 0:255], AF.Silu, bias=b1[:, kv * 2 + eh:kv * 2 + eh + 1]),
                          reads=[Bph, Bb1], writes=[Bht])
                    hid[(kv, g, eh)] = (ht, Bht)
        for g in range(2):
            ph, Bph = phs.next()
            for eh in range(2):
                ht, Bht = hid[(0, g, eh)]
                kb.op("PE", lambda e: e.matmul(ph[0:64, 0:255], w2[:, 0, eh, :], ht[:, 0:255], start=(eh == 0), stop=(eh == 1)),
                      reads=[Bw2, Bht], writes=[Bph])
            kb.op("ACT", lambda e: e.copy(CKc[:, g, 0:255], ph[0:64, 0:255]), reads=[Bph], writes=[BCKc])
            vc, Bvc = VCs[g]
            for c in range(2):
                ph, Bph = phs.next()
                for eh in range(2):
                    ht, Bht = hid[(1, g, eh)]
                    kb.op("PE", lambda e: e.matmul(ph[:, 0:64], ht[:, c * 128:(c + 1) * 128], w2[:, 1, eh, :], start=(eh == 0), stop=(eh == 1)),
                          reads=[Bw2, Bht], writes=[Bph])
                kb.op("ACT", lambda e: e.copy(vc[:, c, 0:64], ph[:, 0:64]), reads=[Bph], writes=[Bvc])
        T.close()
        if self.cfg.get("stop") == "cmpmlp":
            if self.cfg.get("debug"):
                dck = self.tap("d_ckc", [64, 512], BF16)
                kb.dma("SP", dck[:, :], CKc[:].rearrange("p g n -> p (g n)"), reads=[BCKc])
                for g in range(2):
                    dvc = self.tap("d_vc%d" % g, [128, 258], BF16)
                    kb.dma("SP", dvc[:, :], VCs[g][0][:].rearrange("p c n -> p (c n)"), reads=[VCs[g][1]])
            P.close()
            return

        T = Scope(kb)
        nfv, Bnfv = T.sb("nfv", [128, NT, 64], F32)
        cst, Bcst = T.sb("cst", [128, NT, 64], F32)
        v01, Bv01 = T.sb("v01", [128, NT, 64], F32)
        kb.dma("SP", nfv[:], I["c_nfv"][:, :, :], writes=[Bnfv])
        kb.dma("SP", cst[:], I["c_cst"][:, :, :], writes=[Bcst])
        kb.dma("SP", v01[:], I["c_v01"][:, :, :], writes=[Bv01])
        QA4, BQA4 = T.sb("QA4", [128, 4, S], BF16)
        KS, BKS = T.sb("KS", [128, S], BF16)
        KW, BKW = T.sb("KW", [64, S], BF16)
        VS, BVS = T.sb("VS", [128, NT, 65], BF16)
        VW, BVW = T.sb("VW", [128, NT, 65], BF16)
        kb.op("POOL", lambda e: e.memset(VS[:, :, 64:65], 1.0), writes=[BVS])
        kb.op("POOL", lambda e: e.memset(VW[:, :, 64:65], 1.0), writes=[BVW])
        kb.dma("POOL", KS[64:128, :], I["c_e64"][:, :], writes=[BKS])
        augs = Rot([T.sb("aug", [128, 128], BF16) for _ in range(2)])
        for (a, Ba) in augs.items:
            kb.op("POOL", lambda e, a=a: e.memset(a[:], 0.0), writes=[Ba])
        R = dict(pS=Rot([T.ps("pS", [128, 512], F32) for _ in range(2)]),
                 pT=Rot([T.sb("pT", [128, 512], BF16) for _ in range(4)]))
        Oc = [T.ps("Oc", [128, 512], F32) for _ in range(2)]
        BOc = Oc[0][1]
        Osw = Rot([T.ps("Osw", [128, 512], F32) for _ in range(3)])
        ptr = Rot([T.ps("ptr", [128, 1024], BF16) for _ in range(1)])
        occ, Bocc = T.sb("occ", [128, 4, 4, 64], F32)
        imp, Bimp = T.sb("imp", [128, 4, 64], F32)
        itmp, Bitmp = T.sb("itmp", [128, 64], F32)
        sc, Bsc = T.sb("sc", [128, 64], F32)
        sc2, Bsc2 = T.sb("sc2", [128, 64], F32)
        m8a, Bm8a = T.sb("m8a", [128, 8], F32)
        m8b, Bm8b = T.sb("m8b", [128, 8], F32)
        selm, Bselm = T.sb("selm", [128, 64], F32)
        rds = Rot([T.sb("rd", [128, 2], F32) for _ in range(6)])
        osts = Rot([T.sb("ost", [128, 4, 256], BF16) for _ in range(2)])
        gsb = self.gate_sb

        for g in range(2):
            for r in range(4):
                h = 4 * g + r
                kb.dma("SP", QA4[0:64, r, :], self.FT[1024 + h * 64:1024 + (h + 1) * 64, :], writes=[BQA4])
            kb.dma("SP", KS[0:64, :], self.FT[1792 + 64 * g:1792 + 64 * g + 64, :], writes=[BKS])
            kb.dma("SP", KW[:], self.FT[1920 + 64 * g:1920 + 64 * g + 64, :], writes=[BKW])
            tms = self.TM[:, 512 + 64 * g:512 + 64 * g + 64].rearrange("(c p) d -> p c d", p=128)
            tmw = self.TM[:, 640 + 64 * g:640 + 64 * g + 64].rearrange("(c p) d -> p c d", p=128)
            for c4 in range(4):
                kb.dma("SP", VS[:, c4 * 8:(c4 + 1) * 8, 0:64], tms[:, c4 * 8:(c4 + 1) * 8, :], writes=[BVS])
                kb.dma("SP", VW[:, c4 * 8:(c4 + 1) * 8, 0:64], tmw[:, c4 * 8:(c4 + 1) * 8, :], writes=[BVW])
            vc, Bvc = VCs[g]
            parts = self.cfg.get("nsa_parts", ("cmp", "select", "sw"))
            for qt in self.cfg.get("nsa_qts", range(8)):
                ost, Bost = osts.next()
                cchunks = [0] + ([1] if qt >= 4 else [])
                cpairs = [(c, [(s, "full") for s in range(4)]) for c in cchunks]

                def cmp_mask(c, pT, BpT, krows):
                    kb.op("POOL", lambda e: e.affine_select(pT[:], pT[:], [[1, 512]], ALU.is_ge, 0.0,
                                                            base=512 * qt - 2048 * c - 31, channel_multiplier=-16),
                          reads=[BpT], writes=[BpT])

                for r in (range(4) if "cmp" in parts else ()):
                    h = 4 * g + r
                    self.attn_qtile(R, QA4[0:64, r, qt * 512:(qt + 1) * 512], BQA4,
                                    lambda c: (CKc[:, g, c * 128:(c + 1) * 128], 128), BCKc,
                                    lambda c: vc[:, c, :], Bvc, 129, cpairs, None, BOc,
                                    lambda O, s: Oc[s // 2][0][:, (s % 2) * 256:(s % 2) * 256 + 129], post_exp=cmp_mask, bank_of=lambda s: s // 2)
                    for s in range(4):
                        t = 4 * qt + s
                        O_ = Oc[s // 2][0][:, (s % 2) * 256:(s % 2) * 256 + 129]
                        rd, Brd = rds.next()
                        self.epi_norm(T, O_, BOc, 64, rd[:, 0:1], Brd)
                        if r == 0:
                            kb.op("DVE", lambda e, s=s: e.tensor_scalar(imp[:, s, :], O_[:, 65:129], rd[:, 0:1], None, ALU.mult),
                                  reads=[BOc, Brd], writes=[Bimp])
                        else:
                            kb.op("DVE", lambda e: e.tensor_scalar(itmp[:], O_[:, 65:129], rd[:, 0:1], None, ALU.mult),
                                  reads=[BOc, Brd], writes=[Bitmp])
                            kb.op("DVE", lambda e, s=s: e.tensor_tensor(imp[:, s, :], imp[:, s, :], itmp[:], ALU.add),
                                  reads=[Bitmp, Bimp], writes=[Bimp])
                        kb.op("DVE", lambda e: e.tensor_tensor(rd[:, 1:2], rd[:, 0:1], gsb[:, t, h:h + 1], ALU.mult), reads=[Brd, self.Bgate], writes=[Brd])
                        kb.op("DVE", lambda e, s=s, r=r: e.tensor_scalar(occ[:, r, s, :], O_[:, 0:64], rd[:, 1:2], None, ALU.mult),
                              reads=[BOc, Brd], writes=[Bocc])
                for s in (range(4) if "select" in parts else ()):
                    t = 4 * qt + s
                    aug, Baug = augs.next()
                    kb.op("DVE", lambda e: e.tensor_tensor(sc[:], imp[:, s, :], nfv[:, t, :], ALU.mult), reads=[Bimp, Bnfv], writes=[Bsc])
                    kb.op("DVE", lambda e: e.tensor_tensor(sc[:], sc[:], cst[:, t, :], ALU.add), reads=[Bsc, Bcst], writes=[Bsc])
                    kb.op("DVE", lambda e: e.max(m8a[:], sc[:]), reads=[Bsc], writes=[Bm8a])
                    kb.op("DVE", lambda e: e.match_replace(sc2[:], m8a[:], sc[:], -1e30), reads=[Bsc, Bm8a], writes=[Bsc2])
                    kb.op("DVE", lambda e: e.max(m8b[:], sc2[:]), reads=[Bsc2], writes=[Bm8b])
                    kb.op("DVE", lambda e: e.tensor_scalar(selm[:], sc[:], m8b[:, 7:8], None, ALU.is_ge), reads=[Bsc, Bm8b], writes=[Bselm])
                    kb.op("DVE", lambda e: e.tensor_tensor(selm[:], selm[:], v01[:, t, :], ALU.mult), reads=[Bselm, Bv01], writes=[Bselm])
                    kb.op("DVE", lambda e: e.tensor_scalar(aug[:, 64:128], selm[:], -1.0, -MASKV, ALU.add, ALU.mult), reads=[Bselm], writes=[Baug])
                    pt, Bpt = ptr.next()
                    kb.op("PE", lambda e: e.transpose(pt[:, 0:128], aug[:], self.ident[:]), reads=[Baug, self.Bident], writes=[Bpt])
                    for r in range(4):
                        kb.op("ACT", lambda e, r=r: e.copy(QA4[64:128, r, t * 128:(t + 1) * 128], pt[64:128, 0:128]), reads=[Bpt], writes=[BQA4])
                spairs = self.causal_pairs(qt)
                wpairs = []
                for kc in range(max(0, 4 * qt - 4), 4 * qt + 4):
                    subs = []
                    for s in range(4):
                        dlt = 4 * qt + s - kc
                        if dlt == 0:
                            subs.append((s, "tri"))
                        elif 1 <= dlt <= 3:
                            subs.append((s, "full"))
                        elif dlt == 4:
                            subs.append((s, "atri"))
                    if subs:
                        wpairs.append((kc, subs))
                for r in (range(4) if "sw" in parts else ()):
                    h = 4 * g + r
                    for br, (pairs, qrows, kl, Bkl, vt, Bvt) in enumerate((
                            (spairs, 128, KS, BKS, VS, BVS), (wpairs, 64, KW, BKW, VW, BVW))):
                        if br not in self.cfg.get("nsa_br", (0, 1)):
                            continue
                        Ob, BOb = Osw.next()
                        self.attn_qtile(R, QA4[0:qrows, r, qt * 512:(qt + 1) * 512], BQA4,
                                        lambda kc, kl=kl, qrows=qrows: (kl[0:qrows, kc * 128:(kc + 1) * 128], 128), Bkl,
                                        lambda kc, vt=vt: vt[:, kc, :], Bvt, 65, pairs, Ob, BOb,
                                        lambda O, s: O[:, s * 128:s * 128 + 65])
                        for s in range(4):
                            t = 4 * qt + s
                            O_ = Ob[:, s * 128:s * 128 + 65]
                            rd, Brd = rds.next()
                            self.epi_norm(T, O_, BOb, 64, rd[:, 0:1], Brd)
                            gcol = (br + 1) * 8 + h
                            kb.op("DVE", lambda e: e.tensor_tensor(rd[:, 1:2], rd[:, 0:1], gsb[:, t, gcol:gcol + 1], ALU.mult),
                                  reads=[Brd, self.Bgate], writes=[Brd])
                            if br == 0:
                                kb.op("DVE", lambda e: e.tensor_scalar(itmp[:], O_[:, 0:64], rd[:, 1:2], None, ALU.mult),
                                      reads=[BOb, Brd], writes=[Bitmp])
                                kb.op("DVE", lambda e, s=s, r=r: e.tensor_tensor(occ[:, r, s, :], occ[:, r, s, :], itmp[:], ALU.add),
                                      reads=[Bitmp, Bocc], writes=[Bocc])
                            else:
                                kb.op("DVE", lambda e, s=s, r=r: e.scalar_tensor_tensor(ost[:, s, r * 64:(r + 1) * 64], O_[:, 0:64], rd[:, 1:2], occ[:, r, s, :], ALU.mult, ALU.add),
                                      reads=[BOb, Brd, Bocc], writes=[Bost])
                osv = self.OS[qt * 512:(qt + 1) * 512, 512 + g * 256:512 + (g + 1) * 256].rearrange("(s p) d -> p s d", p=128)
                kb.dma("POOL", osv, ost[:], reads=[Bost])
                if self.cfg.get("nsa_barrier"):
                    kb.barrier()
        T.close()
        P.close()


def _consts():
    c = {}
    c["c_ident"] = np.eye(128, dtype=np.float32)
    k = np.arange(128)[:, None]
    q = np.arange(128)[None, :]
    c["c_tri"] = (q >= k).astype(np.float32)
    c["c_atri"] = (k > q).astype(np.float32)
    key = np.arange(S)[None, :]
    c["c_e16"] = (key // 256 == np.arange(16)[:, None]).astype(np.float32)
    c["c_e64"] = (key // 64 == np.arange(64)[:, None]).astype(np.float32)
    ncmp = 255
    cs = np.arange(ncmp) * 16
    ss = np.arange(64) * 64
    ov = np.minimum(cs[:, None] + 32, ss[None, :] + 64) - np.maximum(cs[:, None], ss[None, :])
    ovp = np.zeros((256, 65), np.float32)
    ovp[:255, 0] = 1.0
    ovp[:255, 1:] = np.clip(ov, 0, None) / 32.0
    c["c_ov"] = ovp
    own = np.arange(16)[:, None]
    blk = np.arange(16)[None, :]
    c["c_t1"] = np.where(blk < own, 0.0, -1e30).astype(np.float32).reshape(1, 256)
    c["c_t2"] = (blk == own).astype(np.float32).reshape(1, 256)
    p = np.arange(128)[:, None, None]
    t = np.arange(NT)[None, :, None]
    m = np.arange(64)[None, None, :]
    cur = (t * 128 + p) // 64
    ok = m <= cur
    forced = ok & ((m == 0) | (m >= cur - 1))
    c["c_nfv"] = (ok & ~forced).astype(np.float32)
    c["c_cst"] = np.where(ok, np.where(forced, 1e4 + m, 0.0), -1e30).astype(np.float32)
    c["c_v01"] = ok.astype(np.float32)
    return c


def _core_inputs(b, inp, consts):
    f = lambda a: np.ascontiguousarray(np.asarray(a), dtype=np.float32)
    m = {}
    m["x"] = f(inp["x"][b])
    m["cT"] = f(np.asarray(inp["c"][b]).reshape(8, 128).T)
    m["posT"] = np.ascontiguousarray(np.asarray(inp["positions"][b]).reshape(NT, 128).T.astype(np.int32))
    for k in ("ada_w", "ada_b", "attn_norm", "ffn_norm", "ffn_w_gate", "ffn_w_up", "ffn_w_down"):
        m[k] = f(inp[k])
    m["sp_w_in"] = f(inp["sp_w_in"][0]); m["sp_w_out"] = f(inp["sp_w_out"][0])
    m["moba_q_norm"] = f(inp["moba_q_norm"]); m["moba_k_norm"] = f(inp["moba_k_norm"])
    m["nsa_q_norm"] = f(inp["nsa_q_norm"]); m["nsa_k_norm"] = f(inp["nsa_k_norm"][0])
    m["cmp_posT"] = f(np.asarray(inp["nsa_cmp_pos"][0]).transpose(2, 0, 1))
    m["cmp_w1"] = f(np.asarray(inp["nsa_cmp_w1"][0]).transpose(2, 0, 1, 3))
    m["cmp_w2"] = f(inp["nsa_cmp_w2"][0])
    m["diff_w_in"] = f(inp["diff_w_in"][0]); m["diff_w_out"] = f(inp["diff_w_out"][0])
    m["diff_q_norm"] = f(inp["diff_q_norm"]); m["diff_k_norm"] = f(inp["diff_k_norm"])
    m["diff_lambda"] = f(np.asarray(inp["diff_lambda"][0]).reshape(1, 256))
    m["diff_out_norm"] = f(inp["diff_out_norm"])
    m.update(consts)
    return m


_NC_CACHE = {}


def kernel(**inputs):
    consts = _consts()
    if "nc" not in _NC_CACHE:
        _NC_CACHE["nc"] = Prog({}).build()
    nc = _NC_CACHE["nc"]
    in_maps = [_core_inputs(b, inputs, consts) for b in range(8)]
    res = run_bass_kernel_spmd(nc, in_maps, core_ids=list(range(8)))
    return np.stack([np.asarray(r["out"], dtype=np.float32) for r in res.results], axis=0)
```

```python
from contextlib import ExitStack
import numpy as np
import concourse.bass as bass
import concourse.mybir as mybir
from concourse.bass_utils import run_bass_kernel_spmd

F32 = mybir.dt.float32
BF16 = mybir.dt.bfloat16
I32 = mybir.dt.int32
AF = mybir.ActivationFunctionType
ALU = mybir.AluOpType
AX = mybir.AxisListType

S = 4096
D = 1024
NT = S // 128
FFN = 2816
NJ = FFN // 128
EPS = 1e-6
MASKV = -30000.0
SP_IN = 2840
N_DMA_SEMS = 24


class Buf:
    __slots__ = ("w", "r", "name")

    def __init__(self, name=""):
        self.w = None
        self.r = {}
        self.name = name


class Rot:
    def __init__(self, items):
        self.items = list(items)
        self.i = 0

    def next(self):
        it = self.items[self.i]
        self.i = (self.i + 1) % len(self.items)
        return it


class KB:
    def __init__(self, nc):
        self.nc = nc
        self.es = ExitStack()
        self.engs = {"PE": nc.tensor, "ACT": nc.scalar, "DVE": nc.vector, "POOL": nc.gpsimd, "SP": nc.sync}
        self.sem = {}
        self.cnt = {}
        for e in ("PE", "ACT", "DVE", "POOL"):
            self.sem[e] = self.es.enter_context(nc.semaphore("s_" + e))
            self.cnt[e] = 0
        self.dsem = [self.es.enter_context(nc.semaphore("d_%d" % i)) for i in range(N_DMA_SEMS)]
        self.dcnt = [0] * N_DMA_SEMS
        self.dnext = 0
        self.dnext2 = [0, 0]
        self.waited = {e: {} for e in self.engs}
        self.n_inst = 0
        self.uid = 0
        self.limit = None
        self.log = None

    def name(self, p):
        self.uid += 1
        return "%s_%d" % (p, self.uid)

    def _wait(self, E, key, c, raw=False):
        if key == E and E == "PE":
            return
        w = self.waited[E]
        if w.get(key, 0) >= c:
            return
        w[key] = c
        if isinstance(key, int):
            self.engs[E].wait_ge(self.dsem[key], c)
        else:
            self.engs[E].wait_ge(self.sem[key], c)

    def _deps(self, E, reads, writes):
        for b in reads:
            if b.w is not None:
                self._wait(E, b.w[0], b.w[1], raw=True)
        for b in writes:
            if b.w is not None:
                self._wait(E, b.w[0], b.w[1])
            for k, c in b.r.items():
                self._wait(E, k, c)

    def _mark(self, tok, reads, writes):
        for b in reads:
            if b.r.get(tok[0], 0) < tok[1]:
                b.r[tok[0]] = tok[1]
        for b in writes:
            b.w = tok
            b.r = {}

    def op(self, E, fn, reads=(), writes=()):
        if self.limit is not None and self.n_inst >= self.limit:
            return None
        self._deps(E, reads, writes)
        if self.log is not None:
            import sys as _sys
            self.log.append((self.n_inst, E, _sys._getframe(1).f_lineno))
        ins = fn(self.engs[E])
        self.cnt[E] += 1
        ins.then_inc(self.sem[E], 1)
        tok = (E, self.cnt[E])
        self._mark(tok, reads, writes)
        self.n_inst += 1
        return tok

    def dma(self, Q, out_ap, in_ap, reads=(), writes=(), **kw):
        if self.limit is not None and self.n_inst >= self.limit:
            return None
        self._deps(Q, reads, writes)
        half = N_DMA_SEMS // 2
        qi = 0 if Q == "SP" else 1
        k = qi * half + self.dnext2[qi]
        self.dnext2[qi] = (self.dnext2[qi] + 1) % half
        if self.dcnt[k] > 0:
            self._wait(Q, k, self.dcnt[k])
        if self.log is not None:
            import sys as _sys
            self.log.append((self.n_inst, "DMA-" + Q, _sys._getframe(1).f_lineno))
        ins = self.engs[Q].dma_start(out=out_ap, in_=in_ap, **kw)
        self.dcnt[k] += 16
        ins.then_inc(self.dsem[k], 16)
        tok = (k, self.dcnt[k])
        self._mark(tok, reads, writes)
        self.n_inst += 1
        return tok

    def barrier(self, engines=("PE", "ACT", "DVE", "POOL", "SP")):
        for E in engines:
            for e2 in ("PE", "ACT", "DVE", "POOL"):
                if self.cnt[e2] > 0:
                    self._wait(E, e2, self.cnt[e2])
            for k in range(N_DMA_SEMS):
                if self.dcnt[k] > 0:
                    self._wait(E, k, self.dcnt[k])


class Scope:
    def __init__(self, kb):
        self.kb = kb
        self.es = ExitStack()

    def sb(self, name, shape, dt):
        t = self.es.enter_context(self.kb.nc.sbuf_tensor(self.kb.name(name), list(shape), dt))
        return t, Buf(name)

    def ps(self, name, shape, dt):
        t = self.es.enter_context(self.kb.nc.psum_tensor(self.kb.name(name), list(shape), dt))
        return t, Buf(name)

    def close(self):
        self.kb.barrier()
        self.es.close()


def bc_row(ap_row, n):
    return ap_row.to_broadcast([128, n])


class Prog:
    def __init__(self, cfg):
        self.cfg = cfg
        self.nc = bass.Bass("TRN2", target_bir_lowering=False)
        self.kb = KB(self.nc)
        self.kb.limit = cfg.get("limit")
        self.I = {}
        self.taps = {}

    def din(self, name, shape, dt=F32):
        self.I[name] = self.nc.dram_tensor(name, list(shape), dt, kind="ExternalInput").ap()
        return self.I[name]

    def dscr(self, name, shape, dt):
        if self.cfg.get("debug"):
            return self.nc.dram_tensor(name, list(shape), dt, kind="ExternalOutput").ap()
        return self.nc.dram_tensor(name, list(shape), dt).ap()

    def tap(self, name, shape, dt=F32):
        self.taps[name] = self.nc.dram_tensor(name, list(shape), dt, kind="ExternalOutput").ap()
        return self.taps[name]

    def declare(self):
        d = self.din
        d("x", [S, D]); d("cT", [128, 8]); d("posT", [128, NT], I32)
        d("ada_w", [2, D, 6 * D]); d("ada_b", [2, 6 * D])
        d("attn_norm", [2, D]); d("ffn_norm", [2, D])
        d("ffn_w_gate", [2, D, FFN]); d("ffn_w_up", [2, D, FFN]); d("ffn_w_down", [2, FFN, D])
        d("sp_w_in", [D, SP_IN]); d("sp_w_out", [D, D])
        d("moba_q_norm", [1, 64]); d("moba_k_norm", [1, 64]); d("nsa_q_norm", [1, 64]); d("nsa_k_norm", [3, 64])
        d("cmp_posT", [64, 2, 32]); d("cmp_w1", [64, 2, 32, 256]); d("cmp_w2", [2, 256, 64])
        d("diff_w_in", [D, 3072]); d("diff_w_out", [D, D])
        d("diff_q_norm", [1, 64]); d("diff_k_norm", [1, 64]); d("diff_lambda", [1, 256]); d("diff_out_norm", [1, 128])
        d("c_ident", [128, 128]); d("c_tri", [128, 128]); d("c_atri", [128, 128])
        d("c_e16", [16, S]); d("c_e64", [64, S]); d("c_ov", [256, 65])
        d("c_t1", [1, 256]); d("c_t2", [1, 256])
        d("c_nfv", [128, NT, 64]); d("c_cst", [128, NT, 64]); d("c_v01", [128, NT, 64])
        self.out = self.nc.dram_tensor("out", [S, D], F32, kind="ExternalOutput").ap()
        self.xmid = self.dscr("xmid", [S, D], F32)
        self.x1s = self.dscr("x1s", [S, D], F32)
        self.FT = self.dscr("FT", [2048, S], BF16)
        self.TM = self.dscr("TM", [S, D], BF16)
        self.OS = self.dscr("OS", [S, D], BF16)
        self.H2T = self.dscr("H2T", [D, S], BF16)

    def setup(self):
        kb, I = self.kb, self.I
        self.G = Scope(kb)
        G = self.G
        self.ident, self.Bident = G.sb("ident", [128, 128], BF16)
        kb.dma("POOL", self.ident[:], I["c_ident"][:, :], writes=[self.Bident])
        self.tri, self.Btri = G.sb("tri", [128, 128], BF16)
        kb.dma("POOL", self.tri[:], I["c_tri"][:, :], writes=[self.Btri])
        self.atri, self.Batri = G.sb("atri", [128, 128], BF16)
        kb.dma("POOL", self.atri[:], I["c_atri"][:, :], writes=[self.Batri])
        self.cosT, self.Bcos = G.sb("cosT", [128, NT, 8], F32)
        self.sinT, self.Bsin = G.sb("sinT", [128, NT, 8], F32)
        self.mod, self.Bmod = G.sb("mod", [128, 6, D], F32)
        self.silc, self.Bsilc = G.sb("silc", [128, 8, 128], F32)
        with_scope = Scope(kb)
        T = with_scope
        pi, Bpi = T.sb("pi", [128, NT], I32)
        pf, Bpf = T.sb("pf", [128, NT], F32)
        ang, Bang = T.sb("ang", [128, NT, 8], F32)
        tmp, Btmp = T.sb("tmp", [128, NT, 8], F32)
        tmp2, Btmp2 = T.sb("tmp2", [128, NT, 8], F32)
        kb.dma("SP", pi[:], I["posT"][:, :], writes=[Bpi])
        kb.op("DVE", lambda e: e.tensor_copy(pf[:], pi[:]), reads=[Bpi], writes=[Bpf])
        inv = (1.0 / (np.float32(500000.0) ** (np.arange(0, 16, 2, dtype=np.float32) / np.float32(16)))).astype(np.float32)
        for j in range(8):
            kb.op("DVE", lambda e, j=j: e.tensor_scalar(ang[:, :, j], pf[:], float(inv[j]), None, ALU.mult),
                  reads=[Bpf], writes=[Bang])
        TWO_PI = float(2 * np.pi)
        MAG = 12582912.0

        def sin_of(dst, Bdst, shift):
            if shift != 0.0:
                kb.op("DVE", lambda e: e.tensor_scalar(tmp2[:], ang[:], shift, None, ALU.add), reads=[Bang], writes=[Btmp2])
                src, Bsrc = tmp2, Btmp2
            else:
                src, Bsrc = ang, Bang
            kb.op("DVE", lambda e: e.tensor_scalar(tmp[:], src[:], 1.0 / TWO_PI, MAG, ALU.mult, ALU.add), reads=[Bsrc], writes=[Btmp])
            kb.op("DVE", lambda e: e.tensor_scalar(tmp[:], tmp[:], -MAG, -TWO_PI, ALU.add, ALU.mult), reads=[Btmp], writes=[Btmp])
            kb.op("DVE", lambda e: e.tensor_tensor(tmp[:], src[:], tmp[:], ALU.add), reads=[Bsrc, Btmp], writes=[Btmp])
            kb.op("DVE", lambda e: e.tensor_scalar(tmp[:], tmp[:], float(np.pi), float(-np.pi), ALU.min, ALU.max), reads=[Btmp], writes=[Btmp])
            kb.op("ACT", lambda e: e.activation(dst[:], tmp[:], AF.Sin), reads=[Btmp], writes=[Bdst])

        sin_of(self.sinT, self.Bsin, 0.0)
        sin_of(self.cosT, self.Bcos, float(np.pi / 2))
        ct, Bct = T.sb("ct", [128, 8], F32)
        kb.dma("SP", ct[:], I["cT"][:, :], writes=[Bct])
        kb.op("ACT", lambda e: e.activation(ct[:], ct[:], AF.Silu), reads=[Bct], writes=[Bct])
        kb.op("DVE", lambda e: e.tensor_copy(self.silc[:], ct[:].unsqueeze(2).to_broadcast([128, 8, 128])),
              reads=[Bct], writes=[self.Bsilc])
        T.close()

    def compute_mod(self, li):
        kb, I = self.kb, self.I
        T = Scope(kb)
        wch = [T.sb("adaw", [128, 8, 512], F32) for _ in range(2)]
        bch = [T.sb("adab", [128, 512], F32) for _ in range(2)]
        pm = [T.ps("pmod", [128, 512], F32) for _ in range(2)]
        nrm, Bnrm = T.sb("nrm", [128, 2, D], F32)
        kb.dma("SP", nrm[:, 0, :], bc_row(I["attn_norm"][li:li + 1, :], D), writes=[Bnrm])
        kb.dma("SP", nrm[:, 1, :], bc_row(I["ffn_norm"][li:li + 1, :], D), writes=[Bnrm])
        wv = I["ada_w"][li].rearrange("(kc p) n -> p kc n", p=128)
        for g in range(12):
            w, Bw = wch[g % 2]
            b, Bb = bch[g % 2]
            p, Bp = pm[g % 2]
            kb.dma("SP", w[:], wv[:, :, g * 512:(g + 1) * 512], writes=[Bw])
            kb.dma("SP", b[:], bc_row(I["ada_b"][li:li + 1, g * 512:(g + 1) * 512], 512), writes=[Bb])
            for kc in range(8):
                kb.op("PE", lambda e, kc=kc: e.matmul(p[:], self.silc[:, kc, :], w[:, kc, :], start=(kc == 0), stop=(kc == 7)),
                      reads=[Bw, self.Bsilc], writes=[Bp])
            sl = self.mod[:, g // 2, (g % 2) * 512:(g % 2 + 1) * 512]
            kb.op("DVE", lambda e: e.tensor_tensor(sl, p[:], b[:], ALU.add), reads=[Bp, Bb], writes=[self.Bmod])
        kb.op("DVE", lambda e: e.scalar_tensor_tensor(self.mod[:, 1, :], self.mod[:, 1, :], 1.0, nrm[:, 0, :], ALU.add, ALU.mult),
              reads=[self.Bmod, Bnrm], writes=[self.Bmod])
        kb.op("DVE", lambda e: e.scalar_tensor_tensor(self.mod[:, 4, :], self.mod[:, 4, :], 1.0, nrm[:, 1, :], ALU.add, ALU.mult),
              reads=[self.Bmod, Bnrm], writes=[self.Bmod])
        T.close()

    def rms_mod_tile(self, T, xt, Bxt, hb, Bhb, gi, si, scr):
        kb = self.kb
        junk, Bjunk, ss, Bss, hn, Bhn = scr
        kb.op("ACT", lambda e: e.activation(junk[:], xt[:], AF.Square, accum_out=ss[:]), reads=[Bxt], writes=[Bjunk, Bss])
        kb.op("ACT", lambda e: e.activation(ss[:], ss[:], AF.Sqrt, bias=self.epsb[:], scale=1.0 / D), reads=[Bss, self.Bepsb], writes=[Bss])
        kb.op("DVE", lambda e: e.reciprocal(ss[:], ss[:]), reads=[Bss], writes=[Bss])
        kb.op("DVE", lambda e: e.scalar_tensor_tensor(hn[:], xt[:], ss[:, 0:1], self.mod[:, gi, :], ALU.mult, ALU.mult),
              reads=[Bxt, Bss, self.Bmod], writes=[Bhn])
        kb.op("POOL", lambda e: e.tensor_tensor(hb[:], hn[:], self.mod[:, si, :], ALU.add), reads=[Bhn, self.Bmod], writes=[Bhb])

    def transpose8(self, src, Bsrc, pt, Bpt, dst_ap, Bdst, eng="ACT"):
        kb = self.kb
        for j in range(8):
            kb.op("PE", lambda e, j=j: e.transpose(pt[:, j * 128:(j + 1) * 128], src[:, j * 128:(j + 1) * 128], self.ident[:]),
                  reads=[Bsrc, self.Bident], writes=[Bpt])
        if eng == "ACT":
            kb.op("ACT", lambda e: e.copy(dst_ap, pt[:].rearrange("p (j t) -> p j t", j=8)), reads=[Bpt], writes=[Bdst])
        else:
            kb.op("DVE", lambda e: e.tensor_copy(dst_ap, pt[:].rearrange("p (j t) -> p j t", j=8)), reads=[Bpt], writes=[Bdst])

    def load_w_bf16(self, dst, Bdst, w_ap, nk):
        for kc in range(nk):
            self.kb.dma("POOL", dst[:, kc, :], w_ap[kc * 128:(kc + 1) * 128, :], writes=[Bdst])

    def phase_a(self, li, x_src):
        kb, I = self.kb, self.I
        T = Scope(kb)
        if li == 1:
            w_ap, ncols = I["diff_w_in"], 3072
            groups = []
            for g in range(6):
                if g < 4:
                    gi = 0 if g < 2 else 1
                    blocks = [("T", g * 512 + b * 128) for b in range(4)]
                    groups.append(dict(c0=g * 512, w=512, qk=[(0, 8)], gain=gi, blocks=blocks, gate=None))
                else:
                    blocks = [("V", (g - 4) * 512 + b * 128) for b in range(4)]
                    groups.append(dict(c0=g * 512, w=512, qk=[], gain=None, blocks=blocks, gate=None))
            gain_srcs = [[(I["diff_q_norm"][0:1, :], 0, 8)], [(I["diff_k_norm"][0:1, :], 0, 8)]]
        else:
            w_ap, ncols = I["sp_w_in"], SP_IN
            kn = I["nsa_k_norm"]
            groups = [
                dict(c0=0, w=512, qk=[(0, 8)], gain=0, blocks=[("T", b * 128) for b in range(4)], gate=None),
                dict(c0=512, w=512, qk=[(0, 8)], gain=1, blocks=[("T", 512 + b * 128) for b in range(4)], gate=None),
                dict(c0=1024, w=512, qk=[], gain=None, blocks=[("V", b * 128) for b in range(4)], gate=None),
                dict(c0=1536, w=512, qk=[(0, 8)], gain=2, blocks=[("T", 1024 + b * 128) for b in range(4)], gate=None),
                dict(c0=2048, w=512, qk=[(0, 2), (4, 6)], gain=3,
                     blocks=[("T", 1536), ("T", 1664), ("T", 1792), ("V", 512)], gate=None),
                dict(c0=2560, w=280, qk=[(0, 2)], gain=4, blocks=[("T", 1920), ("V", 640)], gate=(256, 24)),
            ]
            gain_srcs = [
                [(I["moba_q_norm"][0:1, :], 0, 8)], [(I["moba_k_norm"][0:1, :], 0, 8)], [(I["nsa_q_norm"][0:1, :], 0, 8)],
                [(kn[0:1, :], 0, 2), (kn[1:2, :], 4, 6)], [(kn[2:3, :], 0, 2)],
            ]
        nk = 8
        W, BW = T.sb("w_in", [128, nk, ncols], BF16)
        self.load_w_bf16(W, BW, w_ap, nk)
        gains = []
        for gs in gain_srcs:
            gr, Bgr = T.sb("gainrow", [128, 8, 64], F32)
            kb.op("POOL", lambda e: e.memset(gr[:], 1.0), writes=[Bgr])
            for (src, u0, u1) in gs:
                for u in range(u0, u1):
                    kb.dma("SP", gr[:, u, :], bc_row(src, 64), writes=[Bgr])
            gains.append((gr, Bgr))
        NG = len(groups)
        xts = Rot([T.sb("xt", [128, D], F32) for _ in range(3)])
        hbs = Rot([T.sb("hb", [128, D], BF16) for _ in range(2)])
        hTs = Rot([T.sb("hT", [128, 8, 128], BF16) for _ in range(2)])
        junk, Bjunk = T.sb("junk", [128, D], BF16)
        sss = Rot([T.sb("ss", [128, 1], F32) for _ in range(2)])
        hn, Bhn = T.sb("hn", [128, D], F32)
        sqs = [T.sb("sq", [128, 512], F32) for _ in range(NG)]
        ss8l = [T.sb("ss8", [128, 8], F32) for _ in range(NG)]
        qnl = [T.sb("qn", [128, 8, 64], F32) for _ in range(NG)]
        postl = [T.sb("post", [128, 512], BF16) for _ in range(NG)]
        rtl = [T.sb("rt", [128, 4, 8, 8], F32) for _ in range(NG)]
        pts = Rot([T.ps("ptA", [128, 1024], BF16) for _ in range(1)])
        pgl = [T.ps("pgA", [128, 512], F32) for _ in range(NG)]
        pbt, _ = T.ps("ptB", [128, 1024], BF16)
        pbh = Rot([(pbt[:, 0:512], Buf("pb0")), (pbt[:, 512:1024], Buf("pb1"))])
        stages = {}
        for gi_, g in enumerate(groups):
            if any(k == "T" for k, _ in g["blocks"]):
                stages[gi_] = [T.sb("stage", [128, 4, 512], BF16) for _ in range(2)]

        def load_x(t):
            xt, Bxt = xts.next()
            kb.dma("SP", xt[:], x_src[t * 128:(t + 1) * 128, :], writes=[Bxt])
            return xt, Bxt

        def prep(t, xtb):
            xt, Bxt = xtb
            hb, Bhb = hbs.next()
            ss, Bss = sss.next()
            self.rms_mod_tile(T, xt, Bxt, hb, Bhb, 1, 0, (junk, Bjunk, ss, Bss, hn, Bhn))
            hT, BhT = hTs.next()
            pt, Bpt = pts.next()
            self.transpose8(hb, Bhb, pt, Bpt, hT[:], BhT, eng="ACT")
            return hT, BhT

        xq = [load_x(0)]
        if NT > 1:
            xq.append(load_x(1))
        hq = [prep(0, xq.pop(0))]
        for t in range(NT):
            hT, BhT = hq.pop(0)
            if t + 2 < NT:
                xq.append(load_x(t + 2))
            tt = t % 4
            half = (t // 4) % 2
            for gi_, g in enumerate(groups):
                w = g["w"]
                pg, Bpg = pgl[gi_]
                for kc in range(8):
                    kb.op("PE", lambda e, kc=kc, pg=pg, w=w, g=g: e.matmul(pg[:, 0:w], hT[:, kc, :], W[:, kc, g["c0"]:g["c0"] + w],
                                                                          start=(kc == 0), stop=(kc == 7)),
                          reads=[BhT, BW], writes=[Bpg])
            if t + 1 < NT:
                hq.append(prep(t + 1, xq.pop(0)))
            info = []
            for gi_, g in enumerate(groups):
                w = g["w"]
                wq = (w // 64) * 64
                info.append((gi_, g, w, wq, wq // 64))
            for gi_, g, w, wq, nu in info:
                pg, Bpg = pgl[gi_]
                post, Bpost = postl[gi_]
                sq, Bsq = sqs[gi_]
                if g["qk"]:
                    kb.op("ACT", lambda e, sq=sq, pg=pg, wq=wq: e.activation(sq[:, 0:wq], pg[:, 0:wq], AF.Square), reads=[Bpg], writes=[Bsq])
                else:
                    kb.op("ACT", lambda e, post=post, pg=pg, wq=wq: e.copy(post[:, 0:wq], pg[:, 0:wq]), reads=[Bpg], writes=[Bpost])
                if g["gate"] is not None:
                    gc0, gw = g["gate"]
                    kb.op("ACT", lambda e, pg=pg, gc0=gc0, gw=gw: e.activation(self.gate_sb[:, t, :], pg[:, gc0:gc0 + gw], AF.Sigmoid),
                          reads=[Bpg], writes=[self.Bgate])
            for gi_, g, w, wq, nu in info:
                if not g["qk"]:
                    continue
                sq, Bsq = sqs[gi_]
                ss8, Bss8 = ss8l[gi_]
                kb.op("DVE", lambda e, ss8=ss8, sq=sq, wq=wq, nu=nu: e.tensor_reduce(ss8[:, 0:nu], sq[:, 0:wq].rearrange("p (u d) -> p u d", d=64), AX.X, ALU.add),
                      reads=[Bsq], writes=[Bss8])
            for gi_, g, w, wq, nu in info:
                if not g["qk"]:
                    continue
                ss8, Bss8 = ss8l[gi_]
                kb.op("ACT", lambda e, ss8=ss8, nu=nu: e.activation(ss8[:, 0:nu], ss8[:, 0:nu], AF.Sqrt, bias=self.epsb[:], scale=1.0 / 64),
                      reads=[Bss8, self.Bepsb], writes=[Bss8])
            for gi_, g, w, wq, nu in info:
                if not g["qk"]:
                    continue
                pg, Bpg = pgl[gi_]
                ss8, Bss8 = ss8l[gi_]
                qn, Bqn = qnl[gi_]
                gr, Bgr = gains[g["gain"]]
                kb.op("DVE", lambda e, ss8=ss8, nu=nu: e.reciprocal(ss8[:, 0:nu], ss8[:, 0:nu]), reads=[Bss8], writes=[Bss8])
                qk_units = set()
                for (u0, u1) in g["qk"]:
                    qk_units.update(range(u0, u1))
                for u in [u for u in range(nu) if u not in qk_units]:
                    kb.op("DVE", lambda e, u=u, ss8=ss8: e.memset(ss8[:, u:u + 1], 1.0), writes=[Bss8])
                kb.op("DVE", lambda e, qn=qn, pg=pg, ss8=ss8, nu=nu, wq=wq: e.tensor_tensor(
                    qn[:, 0:nu, :], pg[:, 0:wq].rearrange("p (u d) -> p u d", d=64),
                    ss8[:, 0:nu].unsqueeze(2).to_broadcast([128, nu, 64]), ALU.mult),
                      reads=[Bpg, Bss8], writes=[Bqn])
                kb.op("DVE", lambda e, qn=qn, gr=gr, nu=nu: e.tensor_tensor(qn[:, 0:nu, :], qn[:, 0:nu, :], gr[:, 0:nu, :], ALU.mult),
                      reads=[Bqn, Bgr], writes=[Bqn])
            for gi_, g, w, wq, nu in info:
                if not g["qk"]:
                    continue
                qn, Bqn = qnl[gi_]
                post, Bpost = postl[gi_]
                kb.op("ACT", lambda e, post=post, qn=qn, wq=wq, nu=nu: e.copy(post[:, 0:wq], qn[:, 0:nu, :].rearrange("p u d -> p (u d)")),
                      reads=[Bqn], writes=[Bpost])
            for gi_, g, w, wq, nu in info:
                if not g["qk"]:
                    continue
                qn, Bqn = qnl[gi_]
                post, Bpost = postl[gi_]
                rt, Brt = rtl[gi_]
                postv = post[:, 0:wq].rearrange("p (u d) -> p u d", d=64)
                for (u0, u1) in g["qk"]:
                    n_ = u1 - u0
                    cosb = self.cosT[:, t, :].unsqueeze(1).to_broadcast([128, n_, 8])
                    sinb = self.sinT[:, t, :].unsqueeze(1).to_broadcast([128, n_, 8])
                    t1 = qn[:, u0:u1, 0:8]
                    t2 = qn[:, u0:u1, 8:16]
                    kb.op("DVE", lambda e, rt=rt, n_=n_, t1=t1, cosb=cosb: e.tensor_tensor(rt[:, 0, 0:n_, :], t1, cosb, ALU.mult), reads=[Bqn, self.Bcos], writes=[Brt])
                    kb.op("DVE", lambda e, rt=rt, n_=n_, t2=t2, sinb=sinb: e.tensor_tensor(rt[:, 1, 0:n_, :], t2, sinb, ALU.mult), reads=[Bqn, self.Bsin], writes=[Brt])
                    kb.op("DVE", lambda e, rt=rt, n_=n_, t2=t2, cosb=cosb: e.tensor_tensor(rt[:, 2, 0:n_, :], t2, cosb, ALU.mult), reads=[Bqn, self.Bcos], writes=[Brt])
                    kb.op("DVE", lambda e, rt=rt, n_=n_, t1=t1, sinb=sinb: e.tensor_tensor(rt[:, 3, 0:n_, :], t1, sinb, ALU.mult), reads=[Bqn, self.Bsin], writes=[Brt])
                    kb.op("DVE", lambda e, rt=rt, n_=n_, postv=postv, u0=u0, u1=u1: e.tensor_tensor(postv[:, u0:u1, 0:8], rt[:, 0, 0:n_, :], rt[:, 1, 0:n_, :], ALU.subtract),
                          reads=[Brt], writes=[Bpost])
                    kb.op("DVE", lambda e, rt=rt, n_=n_, postv=postv, u0=u0, u1=u1: e.tensor_tensor(postv[:, u0:u1, 8:16], rt[:, 2, 0:n_, :], rt[:, 3, 0:n_, :], ALU.add),
                          reads=[Brt], writes=[Bpost])
            for gi_, g, w, wq, nu in info:
                post, Bpost = postl[gi_]
                tblocks = [(bi, dst) for bi, (k, dst) in enumerate(g["blocks"]) if k == "T"]
                if tblocks:
                    pb, Bpb = pbh.next()
                    st, Bst = stages[gi_][half]
                    for bi, dst in tblocks:
                        kb.op("PE", lambda e, bi=bi, pb=pb, post=post: e.transpose(pb[:, bi * 128:(bi + 1) * 128], post[:, bi * 128:(bi + 1) * 128], self.ident[:]),
                              reads=[Bpost, self.Bident], writes=[Bpb])
                    b0, b1 = tblocks[0][0], tblocks[-1][0] + 1
                    kb.op("ACT", lambda e, st=st, pb=pb, b0=b0, b1=b1: e.copy(st[:, b0:b1, tt * 128:(tt + 1) * 128],
                                                                              pb[:, b0 * 128:b1 * 128].rearrange("p (b t) -> p b t", t=128)),
                          reads=[Bpb], writes=[Bst])
                    if tt == 3:
                        t0 = (t - 3) * 128
                        for bi, dst in tblocks:
                            kb.dma("POOL", self.FT[dst:dst + 128, t0:t0 + 512], st[:, bi, :], reads=[Bst])
                for bi, (k, dst) in enumerate(g["blocks"]):
                    if k == "V":
                        kb.dma("POOL", self.TM[t * 128:(t + 1) * 128, dst:dst + 128], post[:, bi * 128:(bi + 1) * 128], reads=[Bpost])
        T.close()

    def attn_qtile(self, R, q_rhs, Bq, k_lhsT, Bk, v_rhs, Bv, vw, pairs, O, BO, o_off, post_exp=None, bank_of=lambda s: 0):
        kb = self.kb
        LOOK = self.cfg.get("look", 2)
        last = {}
        started = set()
        for kc, subs in pairs:
            for s, kind in subs:
                last[s] = kc
        live = {}

        def stage1(i):
            kc, subs = pairs[i]
            pS, BpS = R["pS"].next()
            kl, krows = k_lhsT(kc)
            kb.op("PE", lambda e: e.matmul(pS[0:krows, :], kl, q_rhs, start=True, stop=True), reads=[Bk, Bq], writes=[BpS])
            pT, BpT = R["pT"].next()
            kb.op("ACT", lambda e: e.activation(pT[0:krows, :], pS[0:krows, :], AF.Exp, scale=0.125), reads=[BpS], writes=[BpT])
            if post_exp is not None:
                post_exp(kc, pT, BpT, krows)
            for s, kind in subs:
                if kind == "tri":
                    kb.op("DVE", lambda e, s=s: e.tensor_tensor(pT[:, s * 128:(s + 1) * 128], pT[:, s * 128:(s + 1) * 128], self.tri[:], ALU.mult),
                          reads=[BpT, self.Btri], writes=[BpT])
                elif kind == "atri":
                    kb.op("DVE", lambda e, s=s: e.tensor_tensor(pT[:, s * 128:(s + 1) * 128], pT[:, s * 128:(s + 1) * 128], self.atri[:], ALU.mult),
                          reads=[BpT, self.Batri], writes=[BpT])
            live[i] = (pT, BpT, krows)

        def stage2(i):
            kc, subs = pairs[i]
            pT, BpT, krows = live.pop(i)
            for s, kind in subs:
                bk = bank_of(s)
                st_flag = bk not in started
                started.add(bk)
                kb.op("PE", lambda e, s=s, st_flag=st_flag: e.matmul(o_off(O, s), pT[0:krows, s * 128:(s + 1) * 128], v_rhs(kc)[0:krows, :],
                                                                     start=st_flag, stop=(last[s] == kc), skip_group_check=True),
                      reads=[BpT, Bv], writes=[BO])

        n = len(pairs)
        for i in range(n + LOOK):
            if i < n:
                stage1(i)
            if i - LOOK >= 0:
                stage2(i - LOOK)

    @staticmethod
    def causal_pairs(qt):
        pairs = []
        for kc in range(4 * qt + 4):
            j = kc - 4 * qt
            if j < 0:
                pairs.append((kc, [(s, "full") for s in range(4)]))
            else:
                pairs.append((kc, [(s, "tri" if s == j else "full") for s in range(j, 4)]))
        return pairs

    def phase_b_diff(self, li_odd_index):
        kb, I = self.kb, self.I
        T = Scope(kb)
        lam_init = 0.8 - 0.6 * float(np.exp(-0.3 * 1))
        lp, Blp = T.sb("lp", [128, 4, 64], F32)
        kb.dma("SP", lp[:].rearrange("p a d -> p (a d)"), bc_row(I["diff_lambda"][0:1, :], 256), writes=[Blp])
        l2, Bl2 = T.sb("l2", [128, 2, 64], F32)
        kb.op("DVE", lambda e: e.tensor_tensor(l2[:, 0, :], lp[:, 0, :], lp[:, 1, :], ALU.mult), reads=[Blp], writes=[Bl2])
        kb.op("DVE", lambda e: e.tensor_tensor(l2[:, 1, :], lp[:, 2, :], lp[:, 3, :], ALU.mult), reads=[Blp], writes=[Bl2])
        ls, Bls = T.sb("ls", [128, 2], F32)
        kb.op("DVE", lambda e: e.tensor_reduce(ls[:], l2[:], AX.X, ALU.add), reads=[Bl2], writes=[Bls])
        kb.op("ACT", lambda e: e.activation(ls[:], ls[:], AF.Exp), reads=[Bls], writes=[Bls])
        nlam, Bnlam = T.sb("nlam", [128, 1], F32)
        kb.op("DVE", lambda e: e.tensor_tensor(nlam[:], ls[:, 1:2], ls[:, 0:1], ALU.subtract), reads=[Bls], writes=[Bnlam])
        kb.op("DVE", lambda e: e.tensor_scalar(nlam[:], nlam[:], -lam_init, None, ALU.add), reads=[Bnlam], writes=[Bnlam])
        go, Bgo = T.sb("go", [128, 128], F32)
        kb.dma("SP", go[:], bc_row(I["diff_out_norm"][0:1, :], 128), writes=[Bgo])
        kb.op("DVE", lambda e: e.tensor_scalar(go[:], go[:], 1.0 - lam_init, None, ALU.mult), reads=[Bgo], writes=[Bgo])

        KTs = Rot([T.sb("KT", [128, S], BF16) for _ in range(2)])
        QTs = Rot([T.sb("QT", [128, S], BF16) for _ in range(2)])
        Vs = Rot([T.sb("V", [128, NT, 129], BF16) for _ in range(2)])
        for (v, Bv) in Vs.items:
            kb.op("POOL", lambda e, v=v: e.memset(v[:, :, 128:129], 1.0), writes=[Bv])
        R = dict(pS=Rot([T.ps("pS", [128, 512], F32) for _ in range(3)]),
                 pT=Rot([T.sb("pT", [128, 512], BF16) for _ in range(4)]))
        Os = [[T.ps("O", [128, 512], F32) for _ in range(2)] for _ in range(2)]
        osts = Rot([T.sb("ost", [128, 4, 128], BF16) for _ in range(2)])
        a0s = Rot([T.sb("a0", [128, 128], F32) for _ in range(2)])
        a1s = Rot([T.sb("a1", [128, 128], F32) for _ in range(2)])
        rds = Rot([T.sb("rd", [128, 4], F32) for _ in range(4)])
        junk, Bjunk = T.sb("junkb", [128, 128], BF16)

        def load_head(h):
            KT, BKT = KTs.next()
            QT, BQT = QTs.next()
            V, BV = Vs.next()
            kb.dma("SP", QT[:], self.FT[h * 128:(h + 1) * 128, :], writes=[BQT])
            kb.dma("SP", KT[:], self.FT[1024 + h * 128:1024 + (h + 1) * 128, :], writes=[BKT])
            tmv = self.TM[:, h * 128:(h + 1) * 128].rearrange("(c p) d -> p c d", p=128)
            for c4 in range(4):
                kb.dma("SP", V[:, c4 * 8:(c4 + 1) * 8, 0:128], tmv[:, c4 * 8:(c4 + 1) * 8, :], writes=[BV])
            return (KT, BKT, QT, BQT, V, BV)

        nxt = load_head(0)
        for h in range(8):
            KT, BKT, QT, BQT, V, BV = nxt
            if h + 1 < 8:
                nxt = load_head(h + 1)
            for qt in range(8):
                pairs = self.causal_pairs(qt)
                for c in range(2):
                    def o_off(O, s, c=c):
                        return Os[c][s // 2][0][:, (s % 2) * 256:(s % 2) * 256 + 129]
                    self.attn_qtile(R, QT[64 * c:64 * c + 64, qt * 512:(qt + 1) * 512], BQT,
                                    lambda kc, c=c: (KT[64 * c:64 * c + 64, kc * 128:(kc + 1) * 128], 128), BKT,
                                    lambda kc: V[:, kc, :], BV, 129, pairs, None, Os[c][0][1], o_off, bank_of=lambda s: s // 2)
                ost, Bost = osts.next()
                for s in range(4):
                    O0 = Os[0][s // 2][0][:, (s % 2) * 256:(s % 2) * 256 + 129]
                    O1 = Os[1][s // 2][0][:, (s % 2) * 256:(s % 2) * 256 + 129]
                    BO0, BO1 = Os[0][0][1], Os[1][0][1]
                    rd, Brd = rds.next()
                    a0, Ba0 = a0s.next()
                    a1, Ba1 = a1s.next()
                    kb.op("DVE", lambda e: e.reciprocal(rd[:, 0:1], O0[:, 128:129]), reads=[BO0], writes=[Brd])
                    kb.op("DVE", lambda e: e.reciprocal(rd[:, 1:2], O1[:, 128:129]), reads=[BO1], writes=[Brd])
                    kb.op("DVE", lambda e: e.tensor_tensor(rd[:, 1:2], rd[:, 1:2], nlam[:], ALU.mult), reads=[Brd, Bnlam], writes=[Brd])
                    kb.op("DVE", lambda e: e.tensor_scalar(a0[:], O0[:, 0:128], rd[:, 0:1], None, ALU.mult), reads=[BO0, Brd], writes=[Ba0])
                    kb.op("DVE", lambda e: e.scalar_tensor_tensor(a1[:], O1[:, 0:128], rd[:, 1:2], a0[:], ALU.mult, ALU.add),
                          reads=[BO1, Brd, Ba0], writes=[Ba1])
                    kb.op("ACT", lambda e: e.activation(junk[:], a1[:], AF.Square, accum_out=rd[:, 2:3]), reads=[Ba1], writes=[Bjunk, Brd])
                    kb.op("ACT", lambda e: e.activation(rd[:, 2:3], rd[:, 2:3], AF.Sqrt, bias=self.epsb[:], scale=1.0 / 128),
                          reads=[Brd, self.Bepsb], writes=[Brd])
                    kb.op("DVE", lambda e: e.reciprocal(rd[:, 3:4], rd[:, 2:3]), reads=[Brd], writes=[Brd])
                    kb.op("DVE", lambda e, s=s: e.scalar_tensor_tensor(ost[:, s, :], a1[:], rd[:, 3:4], go[:], ALU.mult, ALU.mult),
                          reads=[Ba1, Brd, Bgo], writes=[Bost])
                osv = self.OS[qt * 512:(qt + 1) * 512, h * 128:(h + 1) * 128].rearrange("(s p) d -> p s d", p=128)
                kb.dma("POOL", osv, ost[:], reads=[Bost])
        T.close()

    def phase_c1(self, li, x_src, w_out_ap):
        kb, I = self.kb, self.I
        T = Scope(kb)
        W, BW = T.sb("w_out", [128, 8, D], BF16)
        self.load_w_bf16(W, BW, w_out_ap, 8)
        ots = Rot([T.sb("ot", [128, D], BF16) for _ in range(2)])
        xts = Rot([T.sb("xt", [128, D], F32) for _ in range(2)])
        oTs = Rot([T.sb("oT", [128, 8, 128], BF16) for _ in range(2)])
        x1s = Rot([T.sb("x1", [128, D], F32) for _ in range(2)])
        hbs = Rot([T.sb("hb", [128, D], BF16) for _ in range(2)])
        ytmp, Bytmp = T.sb("ytmp", [128, D], F32)
        junk, Bjunk = T.sb("junk", [128, D], BF16)
        sss = Rot([T.sb("ss", [128, 1], F32) for _ in range(2)])
        hn, Bhn = T.sb("hn", [128, D], F32)
        stg = [T.sb("stage", [128, 8, 512], BF16) for _ in range(2)]
        pts = Rot([T.ps("ptA", [128, 1024], BF16) for _ in range(2)])
        pys = Rot([T.ps("py", [128, 512], F32) for _ in range(4)])

        def load(t):
            ot, Bot = ots.next()
            xt, Bxt = xts.next()
            kb.dma("SP", ot[:], self.OS[t * 128:(t + 1) * 128, :], writes=[Bot])
            kb.dma("SP", xt[:], x_src[t * 128:(t + 1) * 128, :], writes=[Bxt])
            return ot, Bot, xt, Bxt

        nxt = load(0)
        for t in range(NT):
            ot, Bot, xt, Bxt = nxt
            if t + 1 < NT:
                nxt = load(t + 1)
            oT, BoT = oTs.next()
            pt, Bpt = pts.next()
            self.transpose8(ot, Bot, pt, Bpt, oT[:], BoT, eng="ACT")
            x1, Bx1 = x1s.next()
            for g in range(2):
                py, Bpy = pys.next()
                for kc in range(8):
                    kb.op("PE", lambda e, kc=kc: e.matmul(py[:], oT[:, kc, :], W[:, kc, g * 512:(g + 1) * 512], start=(kc == 0), stop=(kc == 7)),
                          reads=[BoT, BW], writes=[Bpy])
                sl = slice(g * 512, (g + 1) * 512)
                kb.op("DVE", lambda e: e.tensor_tensor(ytmp[:, sl], py[:], self.mod[:, 2, sl], ALU.mult), reads=[Bpy, self.Bmod], writes=[Bytmp])
                kb.op("POOL", lambda e: e.tensor_tensor(x1[:, sl], ytmp[:, sl], xt[:, sl], ALU.add), reads=[Bytmp, Bxt], writes=[Bx1])
            kb.dma("POOL", self.x1s[t * 128:(t + 1) * 128, :], x1[:], reads=[Bx1])
            hb, Bhb = hbs.next()
            ss, Bss = sss.next()
            self.rms_mod_tile(T, x1, Bx1, hb, Bhb, 4, 3, (junk, Bjunk, ss, Bss, hn, Bhn))
            pt, Bpt = pts.next()
            tt = t % 4
            st, Bst = stg[(t // 4) % 2]
            self.transpose8(hb, Bhb, pt, Bpt, st[:, :, tt * 128:(tt + 1) * 128], Bst, eng="ACT")
            if tt == 3:
                t0 = (t - 3) * 128
                h2v = self.H2T.rearrange("(j p) t -> p j t", p=128)
                kb.dma("POOL", h2v[:, :, t0:t0 + 512], st[:], reads=[Bst])
        T.close()

    def phase_c2(self, li, x_dst):
        kb, I = self.kb, self.I
        T = Scope(kb)
        Wg, BWg = T.sb("wg", [128, 8, FFN], BF16)
        Wu, BWu = T.sb("wu", [128, 8, FFN], BF16)
        Wd, BWd = T.sb("wd", [128, NJ, D], BF16)
        self.load_w_bf16(Wg, BWg, I["ffn_w_gate"][li], 8)
        self.load_w_bf16(Wu, BWu, I["ffn_w_up"][li], 8)
        self.load_w_bf16(Wd, BWd, I["ffn_w_down"][li], NJ)
        TT = 256
        h2s = Rot([T.sb("h2T", [128, 8, TT], BF16) for _ in range(2)])
        act, Bact = T.sb("act", [128, NJ, TT], BF16)
        sgs = Rot([T.sb("sg", [128, TT], F32) for _ in range(2)])
        xts = Rot([T.sb("x1t", [128, D], F32) for _ in range(2)])
        ytmp, Bytmp = T.sb("ytmp", [128, 512], F32)
        pgs = Rot([T.ps("pg", [128, 512], F32) for _ in range(2)])
        pus = Rot([T.ps("pu", [128, 512], F32) for _ in range(2)])
        pys = Rot([T.ps("py", [128, 512], F32) for _ in range(3)])
        h2v = self.H2T.rearrange("(j p) t -> p j t", p=128)

        def load(st):
            h2, Bh2 = h2s.next()
            kb.dma("SP", h2[:], h2v[:, :, st * TT:(st + 1) * TT], writes=[Bh2])
            return h2, Bh2

        nxt = load(0)
        for st in range(S // TT):
            h2, Bh2 = nxt
            if st + 1 < S // TT:
                nxt = load(st + 1)
            for j in range(NJ):
                pg, Bpg = pgs.next()
                pu, Bpu = pus.next()
                for kc in range(8):
                    kb.op("PE", lambda e, kc=kc: e.matmul(pg[:, 0:TT], Wg[:, kc, j * 128:(j + 1) * 128], h2[:, kc, :], start=(kc == 0), stop=(kc == 7)),
                          reads=[BWg, Bh2], writes=[Bpg])
                for kc in range(8):
                    kb.op("PE", lambda e, kc=kc: e.matmul(pu[:, 0:TT], Wu[:, kc, j * 128:(j + 1) * 128], h2[:, kc, :], start=(kc == 0), stop=(kc == 7)),
                          reads=[BWu, Bh2], writes=[Bpu])
                sg, Bsg = sgs.next()
                kb.op("ACT", lambda e: e.activation(sg[:], pg[:, 0:TT], AF.Silu), reads=[Bpg], writes=[Bsg])
                kb.op("DVE", lambda e, j=j: e.tensor_tensor(act[:, j, :], sg[:], pu[:, 0:TT], ALU.mult), reads=[Bsg, Bpu], writes=[Bact])
            for q in range(TT // 128):
                t = st * (TT // 128) + q
                xt, Bxt = xts.next()
                kb.dma("SP", xt[:], self.x1s[t * 128:(t + 1) * 128, :], writes=[Bxt])
                for g in range(2):
                    py, Bpy = pys.next()
                    for j in range(NJ):
                        kb.op("PE", lambda e, j=j: e.matmul(py[:], act[:, j, q * 128:(q + 1) * 128], Wd[:, j, g * 512:(g + 1) * 512],
                                                            start=(j == 0), stop=(j == NJ - 1)),
                              reads=[Bact, BWd], writes=[Bpy])
                    sl = slice(g * 512, (g + 1) * 512)
                    kb.op("DVE", lambda e: e.tensor_tensor(ytmp[:], py[:], self.mod[:, 5, sl], ALU.mult), reads=[Bpy, self.Bmod], writes=[Bytmp])
                    kb.op("POOL", lambda e: e.tensor_tensor(xt[:, sl], ytmp[:], xt[:, sl], ALU.add), reads=[Bytmp, Bxt], writes=[Bxt])
                kb.dma("POOL", x_dst[t * 128:(t + 1) * 128, :], xt[:], reads=[Bxt])
        T.close()

    def build(self):
        kb = self.kb
        self.declare()
        self.setup()
        G = self.G
        self.epsb, self.Bepsb = G.sb("epsb", [128, 1], F32)
        kb.op("POOL", lambda e: e.memset(self.epsb[:], EPS), writes=[self.Bepsb])
        layers = self.cfg.get("layers", [0, 1])
        x_src = self.I["x"]
        for n, li in enumerate(layers):
            x_dst = self.out if n == len(layers) - 1 else self.xmid
            self.L = Scope(kb)
            if li == 0:
                self.gate_sb, self.Bgate = self.L.sb("gates", [128, NT, 24], F32)
            self.compute_mod(li)
            stop = self.cfg.get("stop")
            if stop == "mod":
                self.L.close()
                break
            if li == 0:
                self.phase_a(0, x_src)
                if stop == "A":
                    self.L.close()
                    break
                if not self.cfg.get("skip_moba"):
                    self.moba_part()
                if stop == "moba":
                    self.L.close()
                    break
                self.nsa_part()
                if stop in ("nsa", "cmpmlp"):
                    self.L.close()
                    break
                self.phase_c1(0, x_src, self.I["sp_w_out"])
            else:
                self.phase_a(1, x_src)
                if stop == "A":
                    self.L.close()
                    break
                self.phase_b_diff(0)
                if stop == "B":
                    self.L.close()
                    break
                self.phase_c1(1, x_src, self.I["diff_w_out"])
            if stop == "C1":
                self.L.close()
                break
            self.phase_c2(li, x_dst)
            self.L.close()
            x_src = x_dst
        kb.barrier(engines=("POOL",))
        G.es.close()
        kb.es.close()
        return self.nc

    def epi_norm(self, T, O_ap, BO, vcol, rd_ap, Brd):
        kb = self.kb
        kb.op("DVE", lambda e: e.tensor_scalar(rd_ap, O_ap[:, vcol:vcol + 1], 1e-30, None, ALU.max), reads=[BO], writes=[Brd])
        kb.op("DVE", lambda e: e.reciprocal(rd_ap, rd_ap), reads=[Brd], writes=[Brd])

    def phase_b_sparse(self):
        self.moba_part()
        self.nsa_part()

    def moba_part(self):
        kb, I = self.kb, self.I
        T = Scope(kb)
        QAs = Rot([T.sb("QA", [128, 2, S], BF16) for _ in range(2)])
        KAs = Rot([T.sb("KA", [128, 2, S], BF16) for _ in range(2)])
        Vps = Rot([T.sb("Vp", [128, NT, 2, 65], BF16) for _ in range(2)])
        for (ka, Bka) in KAs.items:
            for hh in range(2):
                kb.dma("POOL", ka[64:80, hh, :], I["c_e16"][:, :], writes=[Bka])
        for (v, Bv) in Vps.items:
            kb.op("POOL", lambda e, v=v: e.memset(v[:, :, :, 64:65], 1.0), writes=[Bv])
        t1, Bt1 = T.sb("t1", [128, 16, 16], F32)
        t2, Bt2 = T.sb("t2", [128, 16, 16], F32)
        kb.dma("SP", t1[:].rearrange("p a b -> p (a b)"), bc_row(I["c_t1"][0:1, :], 256), writes=[Bt1])
        kb.dma("SP", t2[:].rearrange("p a b -> p (a b)"), bc_row(I["c_t2"][0:1, :], 256), writes=[Bt2])
        augs = Rot([T.sb("aug", [128, 128], BF16) for _ in range(2)])
        for (a, Ba) in augs.items:
            kb.op("POOL", lambda e, a=a: e.memset(a[:], 0.0), writes=[Ba])
        km, Bkm = T.sb("km", [64, 16], F32)
        kmb, Bkmb = T.sb("kmb", [64, 16], BF16)
        gms = Rot([T.sb("gm", [128, 16], F32) for _ in range(2)])
        m8s = Rot([T.sb("m8", [128, 8], F32) for _ in range(2)])
        sels = Rot([T.sb("sel", [128, 16], F32) for _ in range(2)])
        R = dict(pS=Rot([T.ps("pS", [128, 512], F32) for _ in range(3)]),
                 pT=Rot([T.sb("pT", [128, 512], BF16) for _ in range(4)]))
        Obs = Rot([T.ps("Om", [128, 512], F32) for _ in range(2)])
        pgs = Rot([T.ps("pgate", [128, 512], F32) for _ in range(2)])
        ptr = Rot([T.ps("ptr", [128, 1024], BF16) for _ in range(1)])
        osts = Rot([T.sb("ost", [128, 4, 128], BF16) for _ in range(2)])
        rds = Rot([T.sb("rd", [128, 1], F32) for _ in range(4)])

        def load_pair(hp):
            QA, BQA = QAs.next()
            KA, BKA = KAs.next()
            Vp, BVp = Vps.next()
            for hh in range(2):
                h = 2 * hp + hh
                kb.dma("SP", QA[0:64, hh, :], self.FT[h * 64:(h + 1) * 64, :], writes=[BQA])
                kb.dma("SP", KA[0:64, hh, :], self.FT[512 + h * 64:512 + (h + 1) * 64, :], writes=[BKA])
            for hh in range(2):
                h = 2 * hp + hh
                tmv = self.TM[:, h * 64:(h + 1) * 64].rearrange("(c p) d -> p c d", p=128)
                for c4 in range(4):
                    kb.dma("SP", Vp[:, c4 * 8:(c4 + 1) * 8, hh, 0:64], tmv[:, c4 * 8:(c4 + 1) * 8, :], writes=[BVp])
            return QA, BQA, KA, BKA, Vp, BVp

        nxt = load_pair(0)
        for hp in range(4):
            QA, BQA, KA, BKA, Vp, BVp = nxt
            if hp + 1 < 4:
                nxt = load_pair(hp + 1)
            for hh in range(2):
                kb.op("DVE", lambda e: e.tensor_reduce(km[:], KA[0:64, hh, :].rearrange("p (b j) -> p b j", j=256), AX.X, ALU.add),
                      reads=[BKA], writes=[Bkm])
                kb.op("DVE", lambda e: e.tensor_scalar(kmb[:], km[:], 1.0 / 256, None, ALU.mult), reads=[Bkm], writes=[Bkmb])
                for t in range(NT):
                    own = t // 2
                    pg, Bpg = pgs.next()
                    kb.op("PE", lambda e: e.matmul(pg[:, 0:16], QA[0:64, hh, t * 128:(t + 1) * 128], kmb[:], start=True, stop=True),
                          reads=[BQA, Bkmb], writes=[Bpg])
                    gm, Bgm = gms.next()
                    m8, Bm8 = m8s.next()
                    sel, Bsel = sels.next()
                    aug, Baug = augs.next()
                    kb.op("DVE", lambda e: e.tensor_tensor(gm[:], pg[:, 0:16], t1[:, own, :], ALU.add), reads=[Bpg, Bt1], writes=[Bgm])
                    kb.op("DVE", lambda e: e.max(m8[:], gm[:]), reads=[Bgm], writes=[Bm8])
                    kb.op("DVE", lambda e: e.tensor_scalar(m8[:, 2:3], m8[:, 2:3], -1e29, None, ALU.max), reads=[Bm8], writes=[Bm8])
                    kb.op("DVE", lambda e: e.tensor_scalar(sel[:], gm[:], m8[:, 2:3], None, ALU.is_ge), reads=[Bgm, Bm8], writes=[Bsel])
                    kb.op("DVE", lambda e: e.tensor_tensor(sel[:], sel[:], t2[:, own, :], ALU.max), reads=[Bsel, Bt2], writes=[Bsel])
                    kb.op("DVE", lambda e: e.tensor_scalar(aug[:, 64:80], sel[:], -1.0, -MASKV, ALU.add, ALU.mult), reads=[Bsel], writes=[Baug])
                    pt, Bpt = ptr.next()
                    kb.op("PE", lambda e: e.transpose(pt[:, 0:128], aug[:], self.ident[:]), reads=[Baug, self.Bident], writes=[Bpt])
                    kb.op("ACT", lambda e: e.copy(QA[64:80, hh, t * 128:(t + 1) * 128], pt[64:80, 0:128]), reads=[Bpt], writes=[BQA])
            for qt in range(8):
                pairs = self.causal_pairs(qt)
                ost, Bost = osts.next()
                for hh in range(2):
                    Om, BOm = Obs.next()
                    self.attn_qtile(R, QA[0:80, hh, qt * 512:(qt + 1) * 512], BQA,
                                    lambda kc: (KA[0:80, hh, kc * 128:(kc + 1) * 128], 128), BKA,
                                    lambda kc: Vp[:, kc, hh, :], BVp, 65, pairs, Om, BOm,
                                    lambda O, s: O[:, s * 128:s * 128 + 65])
                    for s in range(4):
                        rd, Brd = rds.next()
                        Os_ = Om[:, s * 128:s * 128 + 65]
                        self.epi_norm(T, Os_, BOm, 64, rd[:], Brd)
                        kb.op("DVE", lambda e, s=s: e.tensor_scalar(ost[:, s, hh * 64:(hh + 1) * 64], Os_[:, 0:64], rd[:, 0:1], None, ALU.mult),
                              reads=[BOm, Brd], writes=[Bost])
                osv = self.OS[qt * 512:(qt + 1) * 512, hp * 128:(hp + 1) * 128].rearrange("(s p) d -> p s d", p=128)
                kb.dma("POOL", osv, ost[:], reads=[Bost])
        T.close()

    def nsa_part(self):
        kb, I = self.kb, self.I
        for _ in range(self.cfg.get("pad_dve", 0)):
            kb.op("DVE", lambda e: e.memset(self.epsb[:], EPS), writes=[self.Bepsb])
        P = Scope(kb)
        CKc, BCKc = P.sb("CKc", [64, 2, 256], BF16)
        VCs = [P.sb("VC", [128, 2, 129], BF16) for _ in range(2)]
        kb.op("POOL", lambda e: e.memset(CKc[:], 0.0), writes=[BCKc])
        for g in range(2):
            vc, Bvc = VCs[g]
            for c in range(2):
                kb.dma("POOL", vc[:, c, 64:129], I["c_ov"][c * 128:(c + 1) * 128, :], writes=[Bvc])
        T = Scope(kb)
        CX = [T.sb("CX", [128, S], BF16) for _ in range(2)]
        kb.dma("SP", CX[0][0][:], self.FT[1536:1664, :], writes=[CX[0][1]])
        kb.dma("SP", CX[1][0][:], self.FT[1664:1792, :], writes=[CX[1][1]])
        w1, Bw1 = T.sb("w1", [128, 2 * 32 * 256], BF16)
        w1src = I["cmp_w1"].rearrange("d a l e -> d (a l e)")
        for half in range(2):
            for a in range(2):
                kb.dma("POOL", w1[half * 64:(half + 1) * 64, a * 8192:(a + 1) * 8192], w1src[:, a * 8192:(a + 1) * 8192], writes=[Bw1])
        w1v = w1[:].rearrange("p (a l e) -> p a l e", a=2, l=32)
        posb, Bposb = T.sb("posb", [64, 2, 34], BF16)
        kb.op("POOL", lambda e: e.memset(posb[:], 0.0), writes=[Bposb])
        kb.dma("POOL", posb[:, :, 0:32], I["cmp_posT"][:, :, :], writes=[Bposb])
        w2, Bw2 = T.sb("w2", [128, 2, 2, 64], BF16)
        for kv in range(2):
            for eh in range(2):
                kb.dma("POOL", w2[:, kv, eh, :], I["cmp_w2"][kv, eh * 128:(eh + 1) * 128, :], writes=[Bw2])
        b1, Bb1 = T.sb("b1", [128, 4], F32)
        pbs = Rot([T.ps("pb", [128, 512], F32) for _ in range(2)])
        phs = Rot([T.ps("ph", [128, 512], F32) for _ in range(3)])
        for kv in range(2):
            for eh in range(2):
                pb, Bpb = pbs.next()
                for l in range(32):
                    kb.op("PE", lambda e, l=l: e.matmul(pb[:, 0:2], w1v[0:64, kv, l, eh * 128:(eh + 1) * 128],
                                                        posb[0:64, kv, l:l + 2], start=(l == 0), stop=(l == 31)),
                          reads=[Bw1, Bposb], writes=[Bpb])
                kb.op("DVE", lambda e: e.tensor_copy(b1[:, kv * 2 + eh:kv * 2 + eh + 1], pb[:, 0:1]), reads=[Bpb], writes=[Bb1])
        hid = {}
        for kv in range(2):
            cx, Bcx = CX[kv]
            cxv = cx[:].rearrange("p (n j) -> p n j", j=16)
            for g in range(2):
                for eh in range(2):
                    ph, Bph = phs.next()
                    for l in range(32):
                        a = l // 16
                        kb.op("PE", lambda e, l=l, a=a: e.matmul(ph[:, 0:255], w1v[64 * g:64 * g + 64, kv, l, eh * 128:(eh + 1) * 128],
                                                                 cxv[64 * g:64 * g + 64, a:a + 255, l % 16], start=(l == 0), stop=(l == 31)),
                              reads=[Bw1, Bcx], writes=[Bph])
                    ht, Bht = T.sb("hid", [128, 256], BF16)
                    kb.op("POOL", lambda e: e.memset(ht[:], 0.0), writes=[Bht])
                    kb.op("ACT", lambda e: e.activation(ht[:, 0:255], ph[:, 0:255], AF.Silu, bias=b1[:, kv * 2 + eh:kv * 2 + eh + 1]),
                          reads=[Bph, Bb1], writes=[Bht])
                    hid[(kv, g, eh)] = (ht, Bht)
        for g in range(2):
            ph, Bph = phs.next()
            for eh in range(2):
                ht, Bht = hid[(0, g, eh)]
                kb.op("PE", lambda e: e.matmul(ph[0:64, 0:255], w2[:, 0, eh, :], ht[:, 0:255], start=(eh == 0), stop=(eh == 1)),
                      reads=[Bw2, Bht], writes=[Bph])
            kb.op("ACT", lambda e: e.copy(CKc[:, g, 0:255], ph[0:64, 0:255]), reads=[Bph], writes=[BCKc])
            vc, Bvc = VCs[g]
            for c in range(2):
                ph, Bph = phs.next()
                for eh in range(2):
                    ht, Bht = hid[(1, g, eh)]
                    kb.op("PE", lambda e: e.matmul(ph[:, 0:64], ht[:, c * 128:(c + 1) * 128], w2[:, 1, eh, :], start=(eh == 0), stop=(eh == 1)),
                          reads=[Bw2, Bht], writes=[Bph])
                kb.op("ACT", lambda e: e.copy(vc[:, c, 0:64], ph[:, 0:64]), reads=[Bph], writes=[Bvc])
        T.close()
        if self.cfg.get("stop") == "cmpmlp":
            if self.cfg.get("debug"):
                dck = self.tap("d_ckc", [64, 512], BF16)
                kb.dma("SP", dck[:, :], CKc[:].rearrange("p g n -> p (g n)"), reads=[BCKc])
                for g in range(2):
                    dvc = self.tap("d_vc%d" % g, [128, 258], BF16)
                    kb.dma("SP", dvc[:, :], VCs[g][0][:].rearrange("p c n -> p (c n)"), reads=[VCs[g][1]])
            P.close()
            return

        T = Scope(kb)
        nfv, Bnfv = T.sb("nfv", [128, NT, 64], F32)
        cst, Bcst = T.sb("cst", [128, NT, 64], F32)
        v01, Bv01 = T.sb("v01", [128, NT, 64], F32)
        kb.dma("SP", nfv[:], I["c_nfv"][:, :, :], writes=[Bnfv])
        kb.dma("SP", cst[:], I["c_cst"][:, :, :], writes=[Bcst])
        kb.dma("SP", v01[:], I["c_v01"][:, :, :], writes=[Bv01])
        QA4, BQA4 = T.sb("QA4", [128, 4, S], BF16)
        KS, BKS = T.sb("KS", [128, S], BF16)
        KW, BKW = T.sb("KW", [64, S], BF16)
        VS, BVS = T.sb("VS", [128, NT, 65], BF16)
        VW, BVW = T.sb("VW", [128, NT, 65], BF16)
        kb.op("POOL", lambda e: e.memset(VS[:, :, 64:65], 1.0), writes=[BVS])
        kb.op("POOL", lambda e: e.memset(VW[:, :, 64:65], 1.0), writes=[BVW])
        kb.dma("POOL", KS[64:128, :], I["c_e64"][:, :], writes=[BKS])
        augs = Rot([T.sb("aug", [128, 128], BF16) for _ in range(2)])
        for (a, Ba) in augs.items:
            kb.op("POOL", lambda e, a=a: e.memset(a[:], 0.0), writes=[Ba])
        R = dict(pS=Rot([T.ps("pS", [128, 512], F32) for _ in range(2)]),
                 pT=Rot([T.sb("pT", [128, 512], BF16) for _ in range(4)]))
        Oc = [T.ps("Oc", [128, 512], F32) for _ in range(2)]
        BOc = Oc[0][1]
        Osw = Rot([T.ps("Osw", [128, 512], F32) for _ in range(3)])
        ptr = Rot([T.ps("ptr", [128, 1024], BF16) for _ in range(1)])
        occ, Bocc = T.sb("occ", [128, 4, 4, 64], F32)
        imp, Bimp = T.sb("imp", [128, 4, 64], F32)
        itmp, Bitmp = T.sb("itmp", [128, 64], F32)
        sc, Bsc = T.sb("sc", [128, 64], F32)
        sc2, Bsc2 = T.sb("sc2", [128, 64], F32)
        m8a, Bm8a = T.sb("m8a", [128, 8], F32)
        m8b, Bm8b = T.sb("m8b", [128, 8], F32)
        selm, Bselm = T.sb("selm", [128, 64], F32)
        rds = Rot([T.sb("rd", [128, 2], F32) for _ in range(6)])
        osts = Rot([T.sb("ost", [128, 4, 256], BF16) for _ in range(2)])
        gsb = self.gate_sb

        for g in range(2):
            for r in range(4):
                h = 4 * g + r
                kb.dma("SP", QA4[0:64, r, :], self.FT[1024 + h * 64:1024 + (h + 1) * 64, :], writes=[BQA4])
            kb.dma("SP", KS[0:64, :], self.FT[1792 + 64 * g:1792 + 64 * g + 64, :], writes=[BKS])
            kb.dma("SP", KW[:], self.FT[1920 + 64 * g:1920 + 64 * g + 64, :], writes=[BKW])
            tms = self.TM[:, 512 + 64 * g:512 + 64 * g + 64].rearrange("(c p) d -> p c d", p=128)
            tmw = self.TM[:, 640 + 64 * g:640 + 64 * g + 64].rearrange("(c p) d -> p c d", p=128)
            for c4 in range(4):
                kb.dma("SP", VS[:, c4 * 8:(c4 + 1) * 8, 0:64], tms[:, c4 * 8:(c4 + 1) * 8, :], writes=[BVS])
                kb.dma("SP", VW[:, c4 * 8:(c4 + 1) * 8, 0:64], tmw[:, c4 * 8:(c4 + 1) * 8, :], writes=[BVW])
            vc, Bvc = VCs[g]
            parts = self.cfg.get("nsa_parts", ("cmp", "select", "sw"))
            for qt in self.cfg.get("nsa_qts", range(8)):
                ost, Bost = osts.next()
                cchunks = [0] + ([1] if qt >= 4 else [])
                cpairs = [(c, [(s, "full") for s in range(4)]) for c in cchunks]

                def cmp_mask(c, pT, BpT, krows):
                    kb.op("POOL", lambda e: e.affine_select(pT[:], pT[:], [[1, 512]], ALU.is_ge, 0.0,
                                                            base=512 * qt - 2048 * c - 31, channel_multiplier=-16),
                          reads=[BpT], writes=[BpT])

                for r in (range(4) if "cmp" in parts else ()):
                    h = 4 * g + r
                    self.attn_qtile(R, QA4[0:64, r, qt * 512:(qt + 1) * 512], BQA4,
                                    lambda c: (CKc[:, g, c * 128:(c + 1) * 128], 128), BCKc,
                                    lambda c: vc[:, c, :], Bvc, 129, cpairs, None, BOc,
                                    lambda O, s: Oc[s // 2][0][:, (s % 2) * 256:(s % 2) * 256 + 129], post_exp=cmp_mask, bank_of=lambda s: s // 2)
                    for s in range(4):
                        t = 4 * qt + s
                        O_ = Oc[s // 2][0][:, (s % 2) * 256:(s % 2) * 256 + 129]
                        rd, Brd = rds.next()
                        self.epi_norm(T, O_, BOc, 64, rd[:, 0:1], Brd)
                        if r == 0:
                            kb.op("DVE", lambda e, s=s: e.tensor_scalar(imp[:, s, :], O_[:, 65:129], rd[:, 0:1], None, ALU.mult),
                                  reads=[BOc, Brd], writes=[Bimp])
                        else:
                            kb.op("DVE", lambda e: e.tensor_scalar(itmp[:], O_[:, 65:129], rd[:, 0:1], None, ALU.mult),
                                  reads=[BOc, Brd], writes=[Bitmp])
                            kb.op("DVE", lambda e, s=s: e.tensor_tensor(imp[:, s, :], imp[:, s, :], itmp[:], ALU.add),
                                  reads=[Bitmp, Bimp], writes=[Bimp])
                        kb.op("DVE", lambda e: e.tensor_tensor(rd[:, 1:2], rd[:, 0:1], gsb[:, t, h:h + 1], ALU.mult), reads=[Brd, self.Bgate], writes=[Brd])
                        kb.op("DVE", lambda e, s=s, r=r: e.tensor_scalar(occ[:, r, s, :], O_[:, 0:64], rd[:, 1:2], None, ALU.mult),
                              reads=[BOc, Brd], writes=[Bocc])
                for s in (range(4) if "select" in parts else ()):
                    t = 4 * qt + s
                    aug, Baug = augs.next()
                    kb.op("DVE", lambda e: e.tensor_tensor(sc[:], imp[:, s, :], nfv[:, t, :], ALU.mult), reads=[Bimp, Bnfv], writes=[Bsc])
                    kb.op("DVE", lambda e: e.tensor_tensor(sc[:], sc[:], cst[:, t, :], ALU.add), reads=[Bsc, Bcst], writes=[Bsc])
                    kb.op("DVE", lambda e: e.max(m8a[:], sc[:]), reads=[Bsc], writes=[Bm8a])
                    kb.op("DVE", lambda e: e.match_replace(sc2[:], m8a[:], sc[:], -1e30), reads=[Bsc, Bm8a], writes=[Bsc2])
                    kb.op("DVE", lambda e: e.max(m8b[:], sc2[:]), reads=[Bsc2], writes=[Bm8b])
                    kb.op("DVE", lambda e: e.tensor_scalar(selm[:], sc[:], m8b[:, 7:8], None, ALU.is_ge), reads=[Bsc, Bm8b], writes=[Bselm])
                    kb.op("DVE", lambda e: e.tensor_tensor(selm[:], selm[:], v01[:, t, :], ALU.mult), reads=[Bselm, Bv01], writes=[Bselm])
                    kb.op("DVE", lambda e: e.tensor_scalar(aug[:, 64:128], selm[:], -1.0, -MASKV, ALU.add, ALU.mult), reads=[Bselm], writes=[Baug])
                    pt, Bpt = ptr.next()
                    kb.op("PE", lambda e: e.transpose(pt[:, 0:128], aug[:], self.ident[:]), reads=[Baug, self.Bident], writes=[Bpt])
                    for r in range(4):
                        kb.op("ACT", lambda e, r=r: e.copy(QA4[64:128, r, t * 128:(t + 1) * 128], pt[64:128, 0:128]), reads=[Bpt], writes=[BQA4])
                spairs = self.causal_pairs(qt)
                wpairs = []
                for kc in range(max(0, 4 * qt - 4), 4 * qt + 4):
                    subs = []
                    for s in range(4):
                        dlt = 4 * qt + s - kc
                        if dlt == 0:
                            subs.append((s, "tri"))
                        elif 1 <= dlt <= 3:
                            subs.append((s, "full"))
                        elif dlt == 4:
                            subs.append((s, "atri"))
                    if subs:
                        wpairs.append((kc, subs))
                for r in (range(4) if "sw" in parts else ()):
                    h = 4 * g + r
                    for br, (pairs, qrows, kl, Bkl, vt, Bvt) in enumerate((
                            (spairs, 128, KS, BKS, VS, BVS), (wpairs, 64, KW, BKW, VW, BVW))):
                        if br not in self.cfg.get("nsa_br", (0, 1)):
                            continue
                        Ob, BOb = Osw.next()
                        self.attn_qtile(R, QA4[0:qrows, r, qt * 512:(qt + 1) * 512], BQA4,
                                        lambda kc, kl=kl, qrows=qrows: (kl[0:qrows, kc * 128:(kc + 1) * 128], 128), Bkl,
                                        lambda kc, vt=vt: vt[:, kc, :], Bvt, 65, pairs, Ob, BOb,
                                        lambda O, s: O[:, s * 128:s * 128 + 65])
                        for s in range(4):
                            t = 4 * qt + s
                            O_ = Ob[:, s * 128:s * 128 + 65]
                            rd, Brd = rds.next()
                            self.epi_norm(T, O_, BOb, 64, rd[:, 0:1], Brd)
                            gcol = (br + 1) * 8 + h
                            kb.op("DVE", lambda e: e.tensor_tensor(rd[:, 1:2], rd[:, 0:1], gsb[:, t, gcol:gcol + 1], ALU.mult),
                                  reads=[Brd, self.Bgate], writes=[Brd])
                            if br == 0:
                                kb.op("DVE", lambda e: e.tensor_scalar(itmp[:], O_[:, 0:64], rd[:, 1:2], None, ALU.mult),
                                      reads=[BOb, Brd], writes=[Bitmp])
                                kb.op("DVE", lambda e, s=s, r=r: e.tensor_tensor(occ[:, r, s, :], occ[:, r, s, :], itmp[:], ALU.add),
                                      reads=[Bitmp, Bocc], writes=[Bocc])
                            else:
                                kb.op("DVE", lambda e, s=s, r=r: e.scalar_tensor_tensor(ost[:, s, r * 64:(r + 1) * 64], O_[:, 0:64], rd[:, 1:2], occ[:, r, s, :], ALU.mult, ALU.add),
                                      reads=[BOb, Brd, Bocc], writes=[Bost])
                osv = self.OS[qt * 512:(qt + 1) * 512, 512 + g * 256:512 + (g + 1) * 256].rearrange("(s p) d -> p s d", p=128)
                kb.dma("POOL", osv, ost[:], reads=[Bost])
                if self.cfg.get("nsa_barrier"):
                    kb.barrier()
        T.close()
        P.close()


def _consts():
    c = {}
    c["c_ident"] = np.eye(128, dtype=np.float32)
    k = np.arange(128)[:, None]
    q = np.arange(128)[None, :]
    c["c_tri"] = (q >= k).astype(np.float32)
    c["c_atri"] = (k > q).astype(np.float32)
    key = np.arange(S)[None, :]
    c["c_e16"] = (key // 256 == np.arange(16)[:, None]).astype(np.float32)
    c["c_e64"] = (key // 64 == np.arange(64)[:, None]).astype(np.float32)
    ncmp = 255
    cs = np.arange(ncmp) * 16
    ss = np.arange(64) * 64
    ov = np.minimum(cs[:, None] + 32, ss[None, :] + 64) - np.maximum(cs[:, None], ss[None, :])
    ovp = np.zeros((256, 65), np.float32)
    ovp[:255, 0] = 1.0
    ovp[:255, 1:] = np.clip(ov, 0, None) / 32.0
    c["c_ov"] = ovp
    own = np.arange(16)[:, None]
    blk = np.arange(16)[None, :]
    c["c_t1"] = np.where(blk < own, 0.0, -1e30).astype(np.float32).reshape(1, 256)
    c["c_t2"] = (blk == own).astype(np.float32).reshape(1, 256)
    p = np.arange(128)[:, None, None]
    t = np.arange(NT)[None, :, None]
    m = np.arange(64)[None, None, :]
    cur = (t * 128 + p) // 64
    ok = m <= cur
    forced = ok & ((m == 0) | (m >= cur - 1))
    c["c_nfv"] = (ok & ~forced).astype(np.float32)
    c["c_cst"] = np.where(ok, np.where(forced, 1e4 + m, 0.0), -1e30).astype(np.float32)
    c["c_v01"] = ok.astype(np.float32)
    return c


def _core_inputs(b, inp, consts):
    f = lambda a: np.ascontiguousarray(np.asarray(a), dtype=np.float32)
    m = {}
    m["x"] = f(inp["x"][b])
    m["cT"] = f(np.asarray(inp["c"][b]).reshape(8, 128).T)
    m["posT"] = np.ascontiguousarray(np.asarray(inp["positions"][b]).reshape(NT, 128).T.astype(np.int32))
    for k in ("ada_w", "ada_b", "attn_norm", "ffn_norm", "ffn_w_gate", "ffn_w_up", "ffn_w_down"):
        m[k] = f(inp[k])
    m["sp_w_in"] = f(inp["sp_w_in"][0]); m["sp_w_out"] = f(inp["sp_w_out"][0])
    m["moba_q_norm"] = f(inp["moba_q_norm"]); m["moba_k_norm"] = f(inp["moba_k_norm"])
    m["nsa_q_norm"] = f(inp["nsa_q_norm"]); m["nsa_k_norm"] = f(inp["nsa_k_norm"][0])
    m["cmp_posT"] = f(np.asarray(inp["nsa_cmp_pos"][0]).transpose(2, 0, 1))
    m["cmp_w1"] = f(np.asarray(inp["nsa_cmp_w1"][0]).transpose(2, 0, 1, 3))
    m["cmp_w2"] = f(inp["nsa_cmp_w2"][0])
    m["diff_w_in"] = f(inp["diff_w_in"][0]); m["diff_w_out"] = f(inp["diff_w_out"][0])
    m["diff_q_norm"] = f(inp["diff_q_norm"]); m["diff_k_norm"] = f(inp["diff_k_norm"])
    m["diff_lambda"] = f(np.asarray(inp["diff_lambda"][0]).reshape(1, 256))
    m["diff_out_norm"] = f(inp["diff_out_norm"])
    m.update(consts)
    return m


_NC_CACHE = {}


def kernel(**inputs):
    consts = _consts()
    if "nc" not in _NC_CACHE:
        _NC_CACHE["nc"] = Prog({}).build()
    nc = _NC_CACHE["nc"]
    in_maps = [_core_inputs(b, inputs, consts) for b in range(8)]
    res = run_bass_kernel_spmd(nc, in_maps, core_ids=list(range(8)))
    return np.stack([np.asarray(r["out"], dtype=np.float32) for r in res.results], axis=0)
```

```python
from contextlib import ExitStack
import numpy as np
import concourse.bass as bass
import concourse.mybir as mybir
from concourse.bass_utils import run_bass_kernel_spmd

F32 = mybir.dt.float32
BF16 = mybir.dt.bfloat16
I32 = mybir.dt.int32
AF = mybir.ActivationFunctionType
ALU = mybir.AluOpType
AX = mybir.AxisListType

S = 4096
D = 1024
NT = S // 128
FFN = 2816
NJ = FFN // 128
EPS = 1e-6
MASKV = -30000.0
SP_IN = 2840
N_DMA_SEMS = 24


class Buf:
    __slots__ = ("w", "r", "name")

    def __init__(self, name=""):
        self.w = None
        self.r = {}
        self.name = name


class Rot:
    def __init__(self, items):
        self.items = list(items)
        self.i = 0

    def next(self):
        it = self.items[self.i]
        self.i = (self.i + 1) % len(self.items)
        return it


class KB:
    def __init__(self, nc):
        self.nc = nc
        self.es = ExitStack()
        self.engs = {"PE": nc.tensor, "ACT": nc.scalar, "DVE": nc.vector, "POOL": nc.gpsimd, "SP": nc.sync}
        self.sem = {}
        self.cnt = {}
        for e in ("PE", "ACT", "DVE", "POOL"):
            self.sem[e] = self.es.enter_context(nc.semaphore("s_" + e))
            self.cnt[e] = 0
        self.dsem = [self.es.enter_context(nc.semaphore("d_%d" % i)) for i in range(N_DMA_SEMS)]
        self.dcnt = [0] * N_DMA_SEMS
        self.dnext = 0
        self.dnext2 = [0, 0]
        self.waited = {e: {} for e in self.engs}
        self.n_inst = 0
        self.uid = 0
        self.limit = None
        self.log = None

    def name(self, p):
        self.uid += 1
        return "%s_%d" % (p, self.uid)

    def _wait(self, E, key, c, raw=False):
        if key == E and E == "PE":
            return
        w = self.waited[E]
        if w.get(key, 0) >= c:
            return
        w[key] = c
        if isinstance(key, int):
            self.engs[E].wait_ge(self.dsem[key], c)
        else:
            self.engs[E].wait_ge(self.sem[key], c)

    def _deps(self, E, reads, writes):
        for b in reads:
            if b.w is not None:
                self._wait(E, b.w[0], b.w[1], raw=True)
        for b in writes:
            if b.w is not None:
                self._wait(E, b.w[0], b.w[1])
            for k, c in b.r.items():
                self._wait(E, k, c)

    def _mark(self, tok, reads, writes):
        for b in reads:
            if b.r.get(tok[0], 0) < tok[1]:
                b.r[tok[0]] = tok[1]
        for b in writes:
            b.w = tok
            b.r = {}

    def op(self, E, fn, reads=(), writes=()):
        if self.limit is not None and self.n_inst >= self.limit:
            return None
        self._deps(E, reads, writes)
        if self.log is not None:
            import sys as _sys
            self.log.append((self.n_inst, E, _sys._getframe(1).f_lineno))
        ins = fn(self.engs[E])
        self.cnt[E] += 1
        ins.then_inc(self.sem[E], 1)
        tok = (E, self.cnt[E])
        self._mark(tok, reads, writes)
        self.n_inst += 1
        return tok

    def dma(self, Q, out_ap, in_ap, reads=(), writes=(), **kw):
        if self.limit is not None and self.n_inst >= self.limit:
            return None
        self._deps(Q, reads, writes)
        half = N_DMA_SEMS // 2
        qi = 0 if Q == "SP" else 1
        k = qi * half + self.dnext2[qi]
        self.dnext2[qi] = (self.dnext2[qi] + 1) % half
        if self.dcnt[k] > 0:
            self._wait(Q, k, self.dcnt[k])
        if self.log is not None:
            import sys as _sys
            self.log.append((self.n_inst, "DMA-" + Q, _sys._getframe(1).f_lineno))
        ins = self.engs[Q].dma_start(out=out_ap, in_=in_ap, **kw)
        self.dcnt[k] += 16
        ins.then_inc(self.dsem[k], 16)
        tok = (k, self.dcnt[k])
        self._mark(tok, reads, writes)
        self.n_inst += 1
        return tok

    def barrier(self, engines=("PE", "ACT", "DVE", "POOL", "SP")):
        for E in engines:
            for e2 in ("PE", "ACT", "DVE", "POOL"):
                if self.cnt[e2] > 0:
                    self._wait(E, e2, self.cnt[e2])
            for k in range(N_DMA_SEMS):
                if self.dcnt[k] > 0:
                    self._wait(E, k, self.dcnt[k])


class Scope:
    def __init__(self, kb):
        self.kb = kb
        self.es = ExitStack()

    def sb(self, name, shape, dt):
        t = self.es.enter_context(self.kb.nc.sbuf_tensor(self.kb.name(name), list(shape), dt))
        return t, Buf(name)

    def ps(self, name, shape, dt):
        t = self.es.enter_context(self.kb.nc.psum_tensor(self.kb.name(name), list(shape), dt))
        return t, Buf(name)

    def close(self):
        self.kb.barrier()
        self.es.close()


def bc_row(ap_row, n):
    return ap_row.to_broadcast([128, n])


class Prog:
    def __init__(self, cfg):
        self.cfg = cfg
        self.nc = bass.Bass("TRN2", target_bir_lowering=False)
        self.kb = KB(self.nc)
        self.kb.limit = cfg.get("limit")
        self.I = {}
        self.taps = {}

    def din(self, name, shape, dt=F32):
        self.I[name] = self.nc.dram_tensor(name, list(shape), dt, kind="ExternalInput").ap()
        return self.I[name]

    def dscr(self, name, shape, dt):
        if self.cfg.get("debug"):
            return self.nc.dram_tensor(name, list(shape), dt, kind="ExternalOutput").ap()
        return self.nc.dram_tensor(name, list(shape), dt).ap()

    def tap(self, name, shape, dt=F32):
        self.taps[name] = self.nc.dram_tensor(name, list(shape), dt, kind="ExternalOutput").ap()
        return self.taps[name]

    def declare(self):
        d = self.din
        d("x", [S, D]); d("cT", [128, 8]); d("posT", [128, NT], I32)
        d("ada_w", [2, D, 6 * D]); d("ada_b", [2, 6 * D])
        d("attn_norm", [2, D]); d("ffn_norm", [2, D])
        d("ffn_w_gate", [2, D, FFN]); d("ffn_w_up", [2, D, FFN]); d("ffn_w_down", [2, FFN, D])
        d("sp_w_in", [D, SP_IN]); d("sp_w_out", [D, D])
        d("moba_q_norm", [1, 64]); d("moba_k_norm", [1, 64]); d("nsa_q_norm", [1, 64]); d("nsa_k_norm", [3, 64])
        d("cmp_posT", [64, 2, 32]); d("cmp_w1", [64, 2, 32, 256]); d("cmp_w2", [2, 256, 64])
        d("diff_w_in", [D, 3072]); d("diff_w_out", [D, D])
        d("diff_q_norm", [1, 64]); d("diff_k_norm", [1, 64]); d("diff_lambda", [1, 256]); d("diff_out_norm", [1, 128])
        d("c_ident", [128, 128]); d("c_tri", [128, 128]); d("c_atri", [128, 128])
        d("c_e16", [16, S]); d("c_e64", [64, S]); d("c_ov", [256, 65])
        d("c_t1", [1, 256]); d("c_t2", [1, 256])
        d("c_nfv", [128, NT, 64]); d("c_cst", [128, NT, 64]); d("c_v01", [128, NT, 64])
        self.out = self.nc.dram_tensor("out", [S, D], F32, kind="ExternalOutput").ap()
        self.xmid = self.dscr("xmid", [S, D], F32)
        self.x1s = self.dscr("x1s", [S, D], F32)
        self.FT = self.dscr("FT", [2048, S], BF16)
        self.TM = self.dscr("TM", [S, D], BF16)
        self.OS = self.dscr("OS", [S, D], BF16)
        self.H2T = self.dscr("H2T", [D, S], BF16)

    def setup(self):
        kb, I = self.kb, self.I
        self.G = Scope(kb)
        G = self.G
        self.ident, self.Bident = G.sb("ident", [128, 128], BF16)
        kb.dma("POOL", self.ident[:], I["c_ident"][:, :], writes=[self.Bident])
        self.identF, self.BidentF = G.sb("identF", [128, 128], F32)
        kb.dma("SP", self.identF[:], I["c_ident"][:, :], writes=[self.BidentF])
        self.tri, self.Btri = G.sb("tri", [128, 128], BF16)
        kb.dma("POOL", self.tri[:], I["c_tri"][:, :], writes=[self.Btri])
        self.atri, self.Batri = G.sb("atri", [128, 128], BF16)
        kb.dma("POOL", self.atri[:], I["c_atri"][:, :], writes=[self.Batri])
        self.cosT, self.Bcos = G.sb("cosT", [128, NT, 8], F32)
        self.sinT, self.Bsin = G.sb("sinT", [128, NT, 8], F32)
        self.mod, self.Bmod = G.sb("mod", [128, 6, D], F32)
        self.silc, self.Bsilc = G.sb("silc", [128, 8, 128], F32)
        with_scope = Scope(kb)
        T = with_scope
        pi, Bpi = T.sb("pi", [128, NT], I32)
        pf, Bpf = T.sb("pf", [128, NT], F32)
        ang, Bang = T.sb("ang", [128, NT, 8], F32)
        tmp, Btmp = T.sb("tmp", [128, NT, 8], F32)
        tmp2, Btmp2 = T.sb("tmp2", [128, NT, 8], F32)
        kb.dma("SP", pi[:], I["posT"][:, :], writes=[Bpi])
        kb.op("DVE", lambda e: e.tensor_copy(pf[:], pi[:]), reads=[Bpi], writes=[Bpf])
        inv = (1.0 / (np.float32(500000.0) ** (np.arange(0, 16, 2, dtype=np.float32) / np.float32(16)))).astype(np.float32)
        for j in range(8):
            kb.op("DVE", lambda e, j=j: e.tensor_scalar(ang[:, :, j], pf[:], float(inv[j]), None, ALU.mult),
                  reads=[Bpf], writes=[Bang])
        TWO_PI = float(2 * np.pi)
        MAG = 12582912.0

        def sin_of(dst, Bdst, shift):
            if shift != 0.0:
                kb.op("DVE", lambda e: e.tensor_scalar(tmp2[:], ang[:], shift, None, ALU.add), reads=[Bang], writes=[Btmp2])
                src, Bsrc = tmp2, Btmp2
            else:
                src, Bsrc = ang, Bang
            kb.op("DVE", lambda e: e.tensor_scalar(tmp[:], src[:], 1.0 / TWO_PI, MAG, ALU.mult, ALU.add), reads=[Bsrc], writes=[Btmp])
            kb.op("DVE", lambda e: e.tensor_scalar(tmp[:], tmp[:], -MAG, -TWO_PI, ALU.add, ALU.mult), reads=[Btmp], writes=[Btmp])
            kb.op("DVE", lambda e: e.tensor_tensor(tmp[:], src[:], tmp[:], ALU.add), reads=[Bsrc, Btmp], writes=[Btmp])
            kb.op("DVE", lambda e: e.tensor_scalar(tmp[:], tmp[:], float(np.pi), float(-np.pi), ALU.min, ALU.max), reads=[Btmp], writes=[Btmp])
            kb.op("ACT", lambda e: e.activation(dst[:], tmp[:], AF.Sin), reads=[Btmp], writes=[Bdst])

        sin_of(self.sinT, self.Bsin, 0.0)
        sin_of(self.cosT, self.Bcos, float(np.pi / 2))
        ct, Bct = T.sb("ct", [128, 8], F32)
        kb.dma("SP", ct[:], I["cT"][:, :], writes=[Bct])
        kb.op("ACT", lambda e: e.activation(ct[:], ct[:], AF.Silu), reads=[Bct], writes=[Bct])
        kb.op("DVE", lambda e: e.tensor_copy(self.silc[:], ct[:].unsqueeze(2).to_broadcast([128, 8, 128])),
              reads=[Bct], writes=[self.Bsilc])
        T.close()

    def compute_mod(self, li):
        kb, I = self.kb, self.I
        T = Scope(kb)
        wch = [T.sb("adaw", [128, 8, 512], F32) for _ in range(2)]
        bch = [T.sb("adab", [128, 512], F32) for _ in range(2)]
        pm = [T.ps("pmod", [128, 512], F32) for _ in range(2)]
        nrm, Bnrm = T.sb("nrm", [128, 2, D], F32)
        kb.dma("SP", nrm[:, 0, :], bc_row(I["attn_norm"][li:li + 1, :], D), writes=[Bnrm])
        kb.dma("SP", nrm[:, 1, :], bc_row(I["ffn_norm"][li:li + 1, :], D), writes=[Bnrm])
        wv = I["ada_w"][li].rearrange("(kc p) n -> p kc n", p=128)
        for g in range(12):
            w, Bw = wch[g % 2]
            b, Bb = bch[g % 2]
            p, Bp = pm[g % 2]
            kb.dma("SP", w[:], wv[:, :, g * 512:(g + 1) * 512], writes=[Bw])
            kb.dma("SP", b[:], bc_row(I["ada_b"][li:li + 1, g * 512:(g + 1) * 512], 512), writes=[Bb])
            for kc in range(8):
                kb.op("PE", lambda e, kc=kc: e.matmul(p[:], self.silc[:, kc, :], w[:, kc, :], start=(kc == 0), stop=(kc == 7)),
                      reads=[Bw, self.Bsilc], writes=[Bp])
            sl = self.mod[:, g // 2, (g % 2) * 512:(g % 2 + 1) * 512]
            kb.op("DVE", lambda e: e.tensor_tensor(sl, p[:], b[:], ALU.add), reads=[Bp, Bb], writes=[self.Bmod])
        kb.op("DVE", lambda e: e.scalar_tensor_tensor(self.mod[:, 1, :], self.mod[:, 1, :], 1.0, nrm[:, 0, :], ALU.add, ALU.mult),
              reads=[self.Bmod, Bnrm], writes=[self.Bmod])
        kb.op("DVE", lambda e: e.scalar_tensor_tensor(self.mod[:, 4, :], self.mod[:, 4, :], 1.0, nrm[:, 1, :], ALU.add, ALU.mult),
              reads=[self.Bmod, Bnrm], writes=[self.Bmod])
        T.close()

    def rms_mod_tile(self, T, xt, Bxt, hb, Bhb, gi, si, scr):
        kb = self.kb
        junk, Bjunk, ss, Bss, hn, Bhn = scr
        kb.op("ACT", lambda e: e.activation(junk[:], xt[:], AF.Square, accum_out=ss[:]), reads=[Bxt], writes=[Bjunk, Bss])
        kb.op("ACT", lambda e: e.activation(ss[:], ss[:], AF.Sqrt, bias=self.epsb[:], scale=1.0 / D), reads=[Bss, self.Bepsb], writes=[Bss])
        kb.op("DVE", lambda e: e.reciprocal(ss[:], ss[:]), reads=[Bss], writes=[Bss])
        kb.op("DVE", lambda e: e.scalar_tensor_tensor(hn[:], xt[:], ss[:, 0:1], self.mod[:, gi, :], ALU.mult, ALU.mult),
              reads=[Bxt, Bss, self.Bmod], writes=[Bhn])
        kb.op("POOL", lambda e: e.tensor_tensor(hb[:], hn[:], self.mod[:, si, :], ALU.add), reads=[Bhn, self.Bmod], writes=[Bhb])

    def transpose8(self, src, Bsrc, pt, Bpt, dst_ap, Bdst, eng="ACT"):
        kb = self.kb
        for j in range(8):
            kb.op("PE", lambda e, j=j: e.transpose(pt[:, j * 128:(j + 1) * 128], src[:, j * 128:(j + 1) * 128], self.ident[:]),
                  reads=[Bsrc, self.Bident], writes=[Bpt])
        if eng == "ACT":
            kb.op("ACT", lambda e: e.copy(dst_ap, pt[:].rearrange("p (j t) -> p j t", j=8)), reads=[Bpt], writes=[Bdst])
        else:
            kb.op("DVE", lambda e: e.tensor_copy(dst_ap, pt[:].rearrange("p (j t) -> p j t", j=8)), reads=[Bpt], writes=[Bdst])

    def load_w_bf16(self, dst, Bdst, w_ap, nk):
        for kc in range(nk):
            self.kb.dma("POOL", dst[:, kc, :], w_ap[kc * 128:(kc + 1) * 128, :], writes=[Bdst])

    def phase_a(self, li, x_src):
        kb, I = self.kb, self.I
        T = Scope(kb)
        if li == 1:
            w_ap, ncols = I["diff_w_in"], 3072
            groups = []
            for g in range(6):
                if g < 4:
                    gi = 0 if g < 2 else 1
                    blocks = [("T", g * 512 + b * 128) for b in range(4)]
                    groups.append(dict(c0=g * 512, w=512, qk=[(0, 8)], gain=gi, blocks=blocks, gate=None))
                else:
                    blocks = [("V", (g - 4) * 512 + b * 128) for b in range(4)]
                    groups.append(dict(c0=g * 512, w=512, qk=[], gain=None, blocks=blocks, gate=None))
            gain_srcs = [[(I["diff_q_norm"][0:1, :], 0, 8)], [(I["diff_k_norm"][0:1, :], 0, 8)]]
        else:
            w_ap, ncols = I["sp_w_in"], SP_IN
            kn = I["nsa_k_norm"]
            groups = [
                dict(c0=0, w=512, qk=[(0, 8)], gain=0, blocks=[("T", b * 128) for b in range(4)], gate=None),
                dict(c0=512, w=512, qk=[(0, 8)], gain=1, blocks=[("T", 512 + b * 128) for b in range(4)], gate=None),
                dict(c0=1024, w=512, qk=[], gain=None, blocks=[("V", b * 128) for b in range(4)], gate=None),
                dict(c0=1536, w=512, qk=[(0, 8)], gain=2, blocks=[("T", 1024 + b * 128) for b in range(4)], gate=None),
                dict(c0=2048, w=512, qk=[(0, 2), (4, 6)], gain=3,
                     blocks=[("T", 1536), ("T", 1664), ("T", 1792), ("V", 512)], gate=None),
                dict(c0=2560, w=280, qk=[(0, 2)], gain=4, blocks=[("T", 1920), ("V", 640)], gate=(256, 24)),
            ]
            gain_srcs = [
                [(I["moba_q_norm"][0:1, :], 0, 8)], [(I["moba_k_norm"][0:1, :], 0, 8)], [(I["nsa_q_norm"][0:1, :], 0, 8)],
                [(kn[0:1, :], 0, 2), (kn[1:2, :], 4, 6)], [(kn[2:3, :], 0, 2)],
            ]
        nk = 8
        W, BW = T.sb("w_in", [128, nk, ncols], BF16)
        self.load_w_bf16(W, BW, w_ap, nk)
        gains = []
        for gs in gain_srcs:
            gr, Bgr = T.sb("gainrow", [128, 8, 64], F32)
            kb.op("POOL", lambda e: e.memset(gr[:], 1.0), writes=[Bgr])
            for (src, u0, u1) in gs:
                for u in range(u0, u1):
                    kb.dma("SP", gr[:, u, :], bc_row(src, 64), writes=[Bgr])
            gains.append((gr, Bgr))
        NG = len(groups)
        xts = Rot([T.sb("xt", [128, D], F32) for _ in range(3)])
        hbs = Rot([T.sb("hb", [128, D], BF16) for _ in range(2)])
        hTs = Rot([T.sb("hT", [128, 8, 128], BF16) for _ in range(2)])
        junk, Bjunk = T.sb("junk", [128, D], BF16)
        sss = Rot([T.sb("ss", [128, 1], F32) for _ in range(2)])
        hn, Bhn = T.sb("hn", [128, D], F32)
        sqs = [T.sb("sq", [128, 512], F32) for _ in range(NG)]
        ss8l = [T.sb("ss8", [128, 8], F32) for _ in range(NG)]
        qnl = [T.sb("qn", [128, 8, 64], F32) for _ in range(NG)]
        postl = [T.sb("post", [128, 512], BF16) for _ in range(NG)]
        rtl = [T.sb("rt", [128, 4, 8, 8], F32) for _ in range(NG)]
        pts = Rot([T.ps("ptA", [128, 1024], BF16) for _ in range(1)])
        pgl = [T.ps("pgA", [128, 512], F32) for _ in range(NG)]
        pbt, _ = T.ps("ptB", [128, 1024], BF16)
        pbh = Rot([(pbt[:, 0:512], Buf("pb0")), (pbt[:, 512:1024], Buf("pb1"))])
        stages = {}
        for gi_, g in enumerate(groups):
            if any(k == "T" for k, _ in g["blocks"]):
                stages[gi_] = [T.sb("stage", [128, 4, 512], BF16) for _ in range(2)]

        def load_x(t):
            xt, Bxt = xts.next()
            kb.dma("SP", xt[:], x_src[t * 128:(t + 1) * 128, :], writes=[Bxt])
            return xt, Bxt

        def prep(t, xtb):
            xt, Bxt = xtb
            hb, Bhb = hbs.next()
            ss, Bss = sss.next()
            self.rms_mod_tile(T, xt, Bxt, hb, Bhb, 1, 0, (junk, Bjunk, ss, Bss, hn, Bhn))
            hT, BhT = hTs.next()
            pt, Bpt = pts.next()
            self.transpose8(hb, Bhb, pt, Bpt, hT[:], BhT, eng="ACT")
            return hT, BhT

        xq = [load_x(0)]
        if NT > 1:
            xq.append(load_x(1))
        hq = [prep(0, xq.pop(0))]
        for t in range(NT):
            hT, BhT = hq.pop(0)
            if t + 2 < NT:
                xq.append(load_x(t + 2))
            tt = t % 4
            half = (t // 4) % 2
            for gi_, g in enumerate(groups):
                w = g["w"]
                pg, Bpg = pgl[gi_]
                for kc in range(8):
                    kb.op("PE", lambda e, kc=kc, pg=pg, w=w, g=g: e.matmul(pg[:, 0:w], hT[:, kc, :], W[:, kc, g["c0"]:g["c0"] + w],
                                                                          start=(kc == 0), stop=(kc == 7)),
                          reads=[BhT, BW], writes=[Bpg])
            if t + 1 < NT:
                hq.append(prep(t + 1, xq.pop(0)))
            info = []
            for gi_, g in enumerate(groups):
                w = g["w"]
                wq = (w // 64) * 64
                info.append((gi_, g, w, wq, wq // 64))
            for gi_, g, w, wq, nu in info:
                pg, Bpg = pgl[gi_]
                post, Bpost = postl[gi_]
                sq, Bsq = sqs[gi_]
                if g["qk"]:
                    kb.op("ACT", lambda e, sq=sq, pg=pg, wq=wq: e.activation(sq[:, 0:wq], pg[:, 0:wq], AF.Square), reads=[Bpg], writes=[Bsq])
                else:
                    kb.op("ACT", lambda e, post=post, pg=pg, wq=wq: e.copy(post[:, 0:wq], pg[:, 0:wq]), reads=[Bpg], writes=[Bpost])
                if g["gate"] is not None:
                    gc0, gw = g["gate"]
                    kb.op("ACT", lambda e, pg=pg, gc0=gc0, gw=gw: e.activation(self.gate_sb[:, t, :], pg[:, gc0:gc0 + gw], AF.Sigmoid),
                          reads=[Bpg], writes=[self.Bgate])
            for gi_, g, w, wq, nu in info:
                if not g["qk"]:
                    continue
                sq, Bsq = sqs[gi_]
                ss8, Bss8 = ss8l[gi_]
                kb.op("DVE", lambda e, ss8=ss8, sq=sq, wq=wq, nu=nu: e.tensor_reduce(ss8[:, 0:nu], sq[:, 0:wq].rearrange("p (u d) -> p u d", d=64), AX.X, ALU.add),
                      reads=[Bsq], writes=[Bss8])
            for gi_, g, w, wq, nu in info:
                if not g["qk"]:
                    continue
                ss8, Bss8 = ss8l[gi_]
                kb.op("ACT", lambda e, ss8=ss8, nu=nu: e.activation(ss8[:, 0:nu], ss8[:, 0:nu], AF.Sqrt, bias=self.epsb[:], scale=1.0 / 64),
                      reads=[Bss8, self.Bepsb], writes=[Bss8])
            for gi_, g, w, wq, nu in info:
                if not g["qk"]:
                    continue
                pg, Bpg = pgl[gi_]
                ss8, Bss8 = ss8l[gi_]
                qn, Bqn = qnl[gi_]
                gr, Bgr = gains[g["gain"]]
                kb.op("DVE", lambda e, ss8=ss8, nu=nu: e.reciprocal(ss8[:, 0:nu], ss8[:, 0:nu]), reads=[Bss8], writes=[Bss8])
                qk_units = set()
                for (u0, u1) in g["qk"]:
                    qk_units.update(range(u0, u1))
                for u in [u for u in range(nu) if u not in qk_units]:
                    kb.op("DVE", lambda e, u=u, ss8=ss8: e.memset(ss8[:, u:u + 1], 1.0), writes=[Bss8])
                kb.op("DVE", lambda e, qn=qn, pg=pg, ss8=ss8, nu=nu, wq=wq: e.tensor_tensor(
                    qn[:, 0:nu, :], pg[:, 0:wq].rearrange("p (u d) -> p u d", d=64),
                    ss8[:, 0:nu].unsqueeze(2).to_broadcast([128, nu, 64]), ALU.mult),
                      reads=[Bpg, Bss8], writes=[Bqn])
                kb.op("DVE", lambda e, qn=qn, gr=gr, nu=nu: e.tensor_tensor(qn[:, 0:nu, :], qn[:, 0:nu, :], gr[:, 0:nu, :], ALU.mult),
                      reads=[Bqn, Bgr], writes=[Bqn])
            for gi_, g, w, wq, nu in info:
                if not g["qk"]:
                    continue
                qn, Bqn = qnl[gi_]
                post, Bpost = postl[gi_]
                kb.op("ACT", lambda e, post=post, qn=qn, wq=wq, nu=nu: e.copy(post[:, 0:wq], qn[:, 0:nu, :].rearrange("p u d -> p (u d)")),
                      reads=[Bqn], writes=[Bpost])
            for gi_, g, w, wq, nu in info:
                if not g["qk"]:
                    continue
                qn, Bqn = qnl[gi_]
                post, Bpost = postl[gi_]
                rt, Brt = rtl[gi_]
                postv = post[:, 0:wq].rearrange("p (u d) -> p u d", d=64)
                for (u0, u1) in g["qk"]:
                    n_ = u1 - u0
                    cosb = self.cosT[:, t, :].unsqueeze(1).to_broadcast([128, n_, 8])
                    sinb = self.sinT[:, t, :].unsqueeze(1).to_broadcast([128, n_, 8])
                    t1 = qn[:, u0:u1, 0:8]
                    t2 = qn[:, u0:u1, 8:16]
                    kb.op("DVE", lambda e, rt=rt, n_=n_, t1=t1, cosb=cosb: e.tensor_tensor(rt[:, 0, 0:n_, :], t1, cosb, ALU.mult), reads=[Bqn, self.Bcos], writes=[Brt])
                    kb.op("DVE", lambda e, rt=rt, n_=n_, t2=t2, sinb=sinb: e.tensor_tensor(rt[:, 1, 0:n_, :], t2, sinb, ALU.mult), reads=[Bqn, self.Bsin], writes=[Brt])
                    kb.op("DVE", lambda e, rt=rt, n_=n_, t2=t2, cosb=cosb: e.tensor_tensor(rt[:, 2, 0:n_, :], t2, cosb, ALU.mult), reads=[Bqn, self.Bcos], writes=[Brt])
                    kb.op("DVE", lambda e, rt=rt, n_=n_, t1=t1, sinb=sinb: e.tensor_tensor(rt[:, 3, 0:n_, :], t1, sinb, ALU.mult), reads=[Bqn, self.Bsin], writes=[Brt])
                    kb.op("DVE", lambda e, rt=rt, n_=n_, postv=postv, u0=u0, u1=u1: e.tensor_tensor(postv[:, u0:u1, 0:8], rt[:, 0, 0:n_, :], rt[:, 1, 0:n_, :], ALU.subtract),
                          reads=[Brt], writes=[Bpost])
                    kb.op("DVE", lambda e, rt=rt, n_=n_, postv=postv, u0=u0, u1=u1: e.tensor_tensor(postv[:, u0:u1, 8:16], rt[:, 2, 0:n_, :], rt[:, 3, 0:n_, :], ALU.add),
                          reads=[Brt], writes=[Bpost])
            for gi_, g, w, wq, nu in info:
                post, Bpost = postl[gi_]
                tblocks = [(bi, dst) for bi, (k, dst) in enumerate(g["blocks"]) if k == "T"]
                if tblocks:
                    pb, Bpb = pbh.next()
                    st, Bst = stages[gi_][half]
                    for bi, dst in tblocks:
                        kb.op("PE", lambda e, bi=bi, pb=pb, post=post: e.transpose(pb[:, bi * 128:(bi + 1) * 128], post[:, bi * 128:(bi + 1) * 128], self.ident[:]),
                              reads=[Bpost, self.Bident], writes=[Bpb])
                    b0, b1 = tblocks[0][0], tblocks[-1][0] + 1
                    kb.op("ACT", lambda e, st=st, pb=pb, b0=b0, b1=b1: e.copy(st[:, b0:b1, tt * 128:(tt + 1) * 128],
                                                                              pb[:, b0 * 128:b1 * 128].rearrange("p (b t) -> p b t", t=128)),
                          reads=[Bpb], writes=[Bst])
                    if tt == 3:
                        t0 = (t - 3) * 128
                        for bi, dst in tblocks:
                            kb.dma("POOL", self.FT[dst:dst + 128, t0:t0 + 512], st[:, bi, :], reads=[Bst])
                for bi, (k, dst) in enumerate(g["blocks"]):
                    if k == "V":
                        kb.dma("POOL", self.TM[t * 128:(t + 1) * 128, dst:dst + 128], post[:, bi * 128:(bi + 1) * 128], reads=[Bpost])
        T.close()

    def attn_qtile(self, R, q_rhs, Bq, k_lhsT, Bk, v_rhs, Bv, vw, pairs, O, BO, o_off, post_exp=None, bank_of=lambda s: 0, vstat=False):
        kb = self.kb
        LOOK = self.cfg.get("look", 5)
        last = {}
        started = set()
        for kc, subs in pairs:
            for s, kind in subs:
                last[s] = kc
        live = {}

        def stage1(i):
            kc, subs = pairs[i]
            pS, BpS = R["pS"].next()
            kl, krows = k_lhsT(kc)
            kb.op("PE", lambda e: e.matmul(pS[0:krows, :], kl, q_rhs, start=True, stop=True), reads=[Bk, Bq], writes=[BpS])
            pT, BpT = R["pT"].next()
            kb.op("ACT", lambda e: e.activation(pT[0:krows, :], pS[0:krows, :], AF.Exp, scale=0.125), reads=[BpS], writes=[BpT])
            if post_exp is not None:
                post_exp(kc, pT, BpT, krows)
            for s, kind in subs:
                if kind == "tri":
                    kb.op("DVE", lambda e, s=s: e.tensor_tensor(pT[:, s * 128:(s + 1) * 128], pT[:, s * 128:(s + 1) * 128], self.tri[:], ALU.mult),
                          reads=[BpT, self.Btri], writes=[BpT])
                elif kind == "atri":
                    kb.op("DVE", lambda e, s=s: e.tensor_tensor(pT[:, s * 128:(s + 1) * 128], pT[:, s * 128:(s + 1) * 128], self.atri[:], ALU.mult),
                          reads=[BpT, self.Batri], writes=[BpT])
            live[i] = (pT, BpT, krows)

        def stage2(i):
            kc, subs = pairs[i]
            pT, BpT, krows = live.pop(i)
            if vstat:
                c0 = min(s_ for s_, _ in subs) * 128
                c1 = (max(s_ for s_, _ in subs) + 1) * 128
                st_flag = 0 not in started
                started.add(0)
                kb.op("PE", lambda e: e.matmul(O[0:vw, c0:c1], v_rhs(kc)[0:krows, 0:vw], pT[0:krows, c0:c1],
                                               start=st_flag, stop=(i == len(pairs) - 1), skip_group_check=True),
                      reads=[BpT, Bv], writes=[BO])
                return
            for s, kind in subs:
                bk = bank_of(s)
                st_flag = bk not in started
                started.add(bk)
                kb.op("PE", lambda e, s=s, st_flag=st_flag: e.matmul(o_off(O, s), pT[0:krows, s * 128:(s + 1) * 128], v_rhs(kc)[0:krows, :],
                                                                     start=st_flag, stop=(last[s] == kc), skip_group_check=True),
                      reads=[BpT, Bv], writes=[BO])

        n = len(pairs)
        for i in range(n + LOOK):
            if i < n:
                stage1(i)
            if i - LOOK >= 0:
                stage2(i - LOOK)

    def ot_to_tok(self, OT, BOT, vw, osb, Bosb, Otok, BOtok):
        kb = self.kb
        kb.op("ACT", lambda e: e.copy(osb[0:vw, :], OT[0:vw, :]), reads=[BOT], writes=[Bosb])
        for s in range(4):
            kb.op("PE", lambda e, s=s: e.transpose(Otok[:, s * 128:s * 128 + vw], osb[0:vw, s * 128:(s + 1) * 128], self.identF[0:vw, 0:vw]),
                  reads=[Bosb, self.BidentF], writes=[BOtok])

    @staticmethod
    def causal_pairs(qt):
        pairs = []
        for kc in range(4 * qt + 4):
            j = kc - 4 * qt
            if j < 0:
                pairs.append((kc, [(s, "full") for s in range(4)]))
            else:
                pairs.append((kc, [(s, "tri" if s == j else "full") for s in range(j, 4)]))
        return pairs

    def phase_b_diff(self, li_odd_index):
        kb, I = self.kb, self.I
        T = Scope(kb)
        lam_init = 0.8 - 0.6 * float(np.exp(-0.3 * 1))
        lp, Blp = T.sb("lp", [128, 4, 64], F32)
        kb.dma("SP", lp[:].rearrange("p a d -> p (a d)"), bc_row(I["diff_lambda"][0:1, :], 256), writes=[Blp])
        l2, Bl2 = T.sb("l2", [128, 2, 64], F32)
        kb.op("DVE", lambda e: e.tensor_tensor(l2[:, 0, :], lp[:, 0, :], lp[:, 1, :], ALU.mult), reads=[Blp], writes=[Bl2])
        kb.op("DVE", lambda e: e.tensor_tensor(l2[:, 1, :], lp[:, 2, :], lp[:, 3, :], ALU.mult), reads=[Blp], writes=[Bl2])
        ls, Bls = T.sb("ls", [128, 2], F32)
        kb.op("DVE", lambda e: e.tensor_reduce(ls[:], l2[:], AX.X, ALU.add), reads=[Bl2], writes=[Bls])
        kb.op("ACT", lambda e: e.activation(ls[:], ls[:], AF.Exp), reads=[Bls], writes=[Bls])
        nlam, Bnlam = T.sb("nlam", [128, 1], F32)
        kb.op("DVE", lambda e: e.tensor_tensor(nlam[:], ls[:, 1:2], ls[:, 0:1], ALU.subtract), reads=[Bls], writes=[Bnlam])
        kb.op("DVE", lambda e: e.tensor_scalar(nlam[:], nlam[:], -lam_init, None, ALU.add), reads=[Bnlam], writes=[Bnlam])
        go, Bgo = T.sb("go", [128, 128], F32)
        kb.dma("SP", go[:], bc_row(I["diff_out_norm"][0:1, :], 128), writes=[Bgo])
        kb.op("DVE", lambda e: e.tensor_scalar(go[:], go[:], 1.0 - lam_init, None, ALU.mult), reads=[Bgo], writes=[Bgo])

        KTs = Rot([T.sb("KT", [128, S], BF16) for _ in range(2)])
        QTs = Rot([T.sb("QT", [128, S], BF16) for _ in range(2)])
        Vs = Rot([T.sb("V", [128, NT, 129], BF16) for _ in range(2)])
        for (v, Bv) in Vs.items:
            kb.op("POOL", lambda e, v=v: e.memset(v[:, :, 128:129], 1.0), writes=[Bv])
        R = dict(pS=Rot([T.ps("pS", [128, 512], F32) for _ in range(3)]),
                 pT=Rot([T.sb("pT", [128, 512], BF16) for _ in range(8)]))
        Os = [[T.ps("O", [128, 512], F32) for _ in range(2)] for _ in range(2)]
        osts = Rot([T.sb("ost", [128, 4, 128], BF16) for _ in range(2)])
        a0s = Rot([T.sb("a0", [128, 128], F32) for _ in range(2)])
        a1s = Rot([T.sb("a1", [128, 128], F32) for _ in range(2)])
        rds = Rot([T.sb("rd", [128, 4], F32) for _ in range(4)])
        junk, Bjunk = T.sb("junkb", [128, 128], BF16)

        def load_head(h):
            KT, BKT = KTs.next()
            QT, BQT = QTs.next()
            V, BV = Vs.next()
            kb.dma("SP", QT[:], self.FT[h * 128:(h + 1) * 128, :], writes=[BQT])
            kb.dma("SP", KT[:], self.FT[1024 + h * 128:1024 + (h + 1) * 128, :], writes=[BKT])
            tmv = self.TM[:, h * 128:(h + 1) * 128].rearrange("(c p) d -> p c d", p=128)
            for c4 in range(4):
                kb.dma("SP", V[:, c4 * 8:(c4 + 1) * 8, 0:128], tmv[:, c4 * 8:(c4 + 1) * 8, :], writes=[BV])
            return (KT, BKT, QT, BQT, V, BV)

        nxt = load_head(0)
        for h in range(8):
            KT, BKT, QT, BQT, V, BV = nxt
            if h + 1 < 8:
                nxt = load_head(h + 1)
            for qt in range(8):
                pairs = self.causal_pairs(qt)
                for c in range(2):
                    def o_off(O, s, c=c):
                        return Os[c][s // 2][0][:, (s % 2) * 256:(s % 2) * 256 + 129]
                    self.attn_qtile(R, QT[64 * c:64 * c + 64, qt * 512:(qt + 1) * 512], BQT,
                                    lambda kc, c=c: (KT[64 * c:64 * c + 64, kc * 128:(kc + 1) * 128], 128), BKT,
                                    lambda kc: V[:, kc, :], BV, 129, pairs, None, Os[c][0][1], o_off, bank_of=lambda s: s // 2)
                ost, Bost = osts.next()
                for s in range(4):
                    O0 = Os[0][s // 2][0][:, (s % 2) * 256:(s % 2) * 256 + 129]
                    O1 = Os[1][s // 2][0][:, (s % 2) * 256:(s % 2) * 256 + 129]
                    BO0, BO1 = Os[0][0][1], Os[1][0][1]
                    rd, Brd = rds.next()
                    a0, Ba0 = a0s.next()
                    a1, Ba1 = a1s.next()
                    kb.op("DVE", lambda e: e.reciprocal(rd[:, 0:1], O0[:, 128:129]), reads=[BO0], writes=[Brd])
                    kb.op("DVE", lambda e: e.reciprocal(rd[:, 1:2], O1[:, 128:129]), reads=[BO1], writes=[Brd])
                    kb.op("DVE", lambda e: e.tensor_tensor(rd[:, 1:2], rd[:, 1:2], nlam[:], ALU.mult), reads=[Brd, Bnlam], writes=[Brd])
                    kb.op("DVE", lambda e: e.tensor_scalar(a0[:], O0[:, 0:128], rd[:, 0:1], None, ALU.mult), reads=[BO0, Brd], writes=[Ba0])
                    kb.op("DVE", lambda e: e.scalar_tensor_tensor(a1[:], O1[:, 0:128], rd[:, 1:2], a0[:], ALU.mult, ALU.add),
                          reads=[BO1, Brd, Ba0], writes=[Ba1])
                    kb.op("ACT", lambda e: e.activation(junk[:], a1[:], AF.Square, accum_out=rd[:, 2:3]), reads=[Ba1], writes=[Bjunk, Brd])
                    kb.op("ACT", lambda e: e.activation(rd[:, 2:3], rd[:, 2:3], AF.Sqrt, bias=self.epsb[:], scale=1.0 / 128),
                          reads=[Brd, self.Bepsb], writes=[Brd])
                    kb.op("DVE", lambda e: e.reciprocal(rd[:, 3:4], rd[:, 2:3]), reads=[Brd], writes=[Brd])
                    kb.op("DVE", lambda e, s=s: e.scalar_tensor_tensor(ost[:, s, :], a1[:], rd[:, 3:4], go[:], ALU.mult, ALU.mult),
                          reads=[Ba1, Brd, Bgo], writes=[Bost])
                osv = self.OS[qt * 512:(qt + 1) * 512, h * 128:(h + 1) * 128].rearrange("(s p) d -> p s d", p=128)
                kb.dma("POOL", osv, ost[:], reads=[Bost])
        T.close()

    def phase_c1(self, li, x_src, w_out_ap):
        kb, I = self.kb, self.I
        T = Scope(kb)
        W, BW = T.sb("w_out", [128, 8, D], BF16)
        self.load_w_bf16(W, BW, w_out_ap, 8)
        ots = Rot([T.sb("ot", [128, D], BF16) for _ in range(2)])
        xts = Rot([T.sb("xt", [128, D], F32) for _ in range(2)])
        oTs = Rot([T.sb("oT", [128, 8, 128], BF16) for _ in range(2)])
        x1s = Rot([T.sb("x1", [128, D], F32) for _ in range(2)])
        hbs = Rot([T.sb("hb", [128, D], BF16) for _ in range(2)])
        ytmp, Bytmp = T.sb("ytmp", [128, D], F32)
        junk, Bjunk = T.sb("junk", [128, D], BF16)
        sss = Rot([T.sb("ss", [128, 1], F32) for _ in range(2)])
        hn, Bhn = T.sb("hn", [128, D], F32)
        stg = [T.sb("stage", [128, 8, 512], BF16) for _ in range(2)]
        pts = Rot([T.ps("ptA", [128, 1024], BF16) for _ in range(2)])
        pys = Rot([T.ps("py", [128, 512], F32) for _ in range(4)])

        def load(t):
            ot, Bot = ots.next()
            xt, Bxt = xts.next()
            kb.dma("SP", ot[:], self.OS[t * 128:(t + 1) * 128, :], writes=[Bot])
            kb.dma("SP", xt[:], x_src[t * 128:(t + 1) * 128, :], writes=[Bxt])
            return ot, Bot, xt, Bxt

        nxt = load(0)
        for t in range(NT):
            ot, Bot, xt, Bxt = nxt
            if t + 1 < NT:
                nxt = load(t + 1)
            oT, BoT = oTs.next()
            pt, Bpt = pts.next()
            self.transpose8(ot, Bot, pt, Bpt, oT[:], BoT, eng="ACT")
            x1, Bx1 = x1s.next()
            for g in range(2):
                py, Bpy = pys.next()
                for kc in range(8):
                    kb.op("PE", lambda e, kc=kc: e.matmul(py[:], oT[:, kc, :], W[:, kc, g * 512:(g + 1) * 512], start=(kc == 0), stop=(kc == 7)),
                          reads=[BoT, BW], writes=[Bpy])
                sl = slice(g * 512, (g + 1) * 512)
                kb.op("DVE", lambda e: e.tensor_tensor(ytmp[:, sl], py[:], self.mod[:, 2, sl], ALU.mult), reads=[Bpy, self.Bmod], writes=[Bytmp])
                kb.op("POOL", lambda e: e.tensor_tensor(x1[:, sl], ytmp[:, sl], xt[:, sl], ALU.add), reads=[Bytmp, Bxt], writes=[Bx1])
            kb.dma("POOL", self.x1s[t * 128:(t + 1) * 128, :], x1[:], reads=[Bx1])
            hb, Bhb = hbs.next()
            ss, Bss = sss.next()
            self.rms_mod_tile(T, x1, Bx1, hb, Bhb, 4, 3, (junk, Bjunk, ss, Bss, hn, Bhn))
            pt, Bpt = pts.next()
            tt = t % 4
            st, Bst = stg[(t // 4) % 2]
            self.transpose8(hb, Bhb, pt, Bpt, st[:, :, tt * 128:(tt + 1) * 128], Bst, eng="ACT")
            if tt == 3:
                t0 = (t - 3) * 128
                h2v = self.H2T.rearrange("(j p) t -> p j t", p=128)
                kb.dma("POOL", h2v[:, :, t0:t0 + 512], st[:], reads=[Bst])
        T.close()

    def phase_c2(self, li, x_dst):
        kb, I = self.kb, self.I
        T = Scope(kb)
        Wg, BWg = T.sb("wg", [128, 8, FFN], BF16)
        Wu, BWu = T.sb("wu", [128, 8, FFN], BF16)
        Wd, BWd = T.sb("wd", [128, NJ, D], BF16)
        self.load_w_bf16(Wg, BWg, I["ffn_w_gate"][li], 8)
        self.load_w_bf16(Wu, BWu, I["ffn_w_up"][li], 8)
        self.load_w_bf16(Wd, BWd, I["ffn_w_down"][li], NJ)
        TT = 256
        h2s = Rot([T.sb("h2T", [128, 8, TT], BF16) for _ in range(2)])
        act, Bact = T.sb("act", [128, NJ, TT], BF16)
        sgs = Rot([T.sb("sg", [128, TT], F32) for _ in range(2)])
        xts = Rot([T.sb("x1t", [128, D], F32) for _ in range(2)])
        ytmp, Bytmp = T.sb("ytmp", [128, 512], F32)
        pgs = Rot([T.ps("pg", [128, 512], F32) for _ in range(2)])
        pus = Rot([T.ps("pu", [128, 512], F32) for _ in range(2)])
        pys = Rot([T.ps("py", [128, 512], F32) for _ in range(3)])
        h2v = self.H2T.rearrange("(j p) t -> p j t", p=128)

        def load(st):
            h2, Bh2 = h2s.next()
            kb.dma("SP", h2[:], h2v[:, :, st * TT:(st + 1) * TT], writes=[Bh2])
            return h2, Bh2

        nxt = load(0)
        for st in range(S // TT):
            h2, Bh2 = nxt
            if st + 1 < S // TT:
                nxt = load(st + 1)
            for j in range(NJ):
                pg, Bpg = pgs.next()
                pu, Bpu = pus.next()
                for kc in range(8):
                    kb.op("PE", lambda e, kc=kc: e.matmul(pg[:, 0:TT], Wg[:, kc, j * 128:(j + 1) * 128], h2[:, kc, :], start=(kc == 0), stop=(kc == 7)),
                          reads=[BWg, Bh2], writes=[Bpg])
                for kc in range(8):
                    kb.op("PE", lambda e, kc=kc: e.matmul(pu[:, 0:TT], Wu[:, kc, j * 128:(j + 1) * 128], h2[:, kc, :], start=(kc == 0), stop=(kc == 7)),
                          reads=[BWu, Bh2], writes=[Bpu])
                sg, Bsg = sgs.next()
                kb.op("ACT", lambda e: e.activation(sg[:], pg[:, 0:TT], AF.Silu), reads=[Bpg], writes=[Bsg])
                kb.op("DVE", lambda e, j=j: e.tensor_tensor(act[:, j, :], sg[:], pu[:, 0:TT], ALU.mult), reads=[Bsg, Bpu], writes=[Bact])
            for q in range(TT // 128):
                t = st * (TT // 128) + q
                xt, Bxt = xts.next()
                kb.dma("SP", xt[:], self.x1s[t * 128:(t + 1) * 128, :], writes=[Bxt])
                for g in range(2):
                    py, Bpy = pys.next()
                    for j in range(NJ):
                        kb.op("PE", lambda e, j=j: e.matmul(py[:], act[:, j, q * 128:(q + 1) * 128], Wd[:, j, g * 512:(g + 1) * 512],
                                                            start=(j == 0), stop=(j == NJ - 1)),
                              reads=[Bact, BWd], writes=[Bpy])
                    sl = slice(g * 512, (g + 1) * 512)
                    kb.op("DVE", lambda e: e.tensor_tensor(ytmp[:], py[:], self.mod[:, 5, sl], ALU.mult), reads=[Bpy, self.Bmod], writes=[Bytmp])
                    kb.op("POOL", lambda e: e.tensor_tensor(xt[:, sl], ytmp[:], xt[:, sl], ALU.add), reads=[Bytmp, Bxt], writes=[Bxt])
                kb.dma("POOL", x_dst[t * 128:(t + 1) * 128, :], xt[:], reads=[Bxt])
        T.close()

    def build(self):
        kb = self.kb
        self.declare()
        self.setup()
        G = self.G
        self.epsb, self.Bepsb = G.sb("epsb", [128, 1], F32)
        kb.op("POOL", lambda e: e.memset(self.epsb[:], EPS), writes=[self.Bepsb])
        layers = self.cfg.get("layers", [0, 1])
        x_src = self.I["x"]
        for n, li in enumerate(layers):
            x_dst = self.out if n == len(layers) - 1 else self.xmid
            self.L = Scope(kb)
            if li == 0:
                self.gate_sb, self.Bgate = self.L.sb("gates", [128, NT, 24], F32)
            self.compute_mod(li)
            stop = self.cfg.get("stop")
            if stop == "mod":
                self.L.close()
                break
            if li == 0:
                self.phase_a(0, x_src)
                if stop == "A":
                    self.L.close()
                    break
                if not self.cfg.get("skip_moba"):
                    self.moba_part()
                if stop == "moba":
                    self.L.close()
                    break
                self.nsa_part()
                if stop in ("nsa", "cmpmlp"):
                    self.L.close()
                    break
                self.phase_c1(0, x_src, self.I["sp_w_out"])
            else:
                self.phase_a(1, x_src)
                if stop == "A":
                    self.L.close()
                    break
                self.phase_b_diff(0)
                if stop == "B":
                    self.L.close()
                    break
                self.phase_c1(1, x_src, self.I["diff_w_out"])
            if stop == "C1":
                self.L.close()
                break
            self.phase_c2(li, x_dst)
            self.L.close()
            x_src = x_dst
        kb.barrier(engines=("POOL",))
        G.es.close()
        kb.es.close()
        return self.nc

    def epi_norm(self, T, O_ap, BO, vcol, rd_ap, Brd):
        kb = self.kb
        kb.op("DVE", lambda e: e.tensor_scalar(rd_ap, O_ap[:, vcol:vcol + 1], 1e-30, None, ALU.max), reads=[BO], writes=[Brd])
        kb.op("DVE", lambda e: e.reciprocal(rd_ap, rd_ap), reads=[Brd], writes=[Brd])

    def phase_b_sparse(self):
        self.moba_part()
        self.nsa_part()

    def moba_part(self):
        kb, I = self.kb, self.I
        T = Scope(kb)
        QAs = Rot([T.sb("QA", [128, 2, S], BF16) for _ in range(2)])
        KAs = Rot([T.sb("KA", [128, 2, S], BF16) for _ in range(2)])
        Vps = Rot([T.sb("Vp", [128, NT, 2, 65], BF16) for _ in range(2)])
        for (ka, Bka) in KAs.items:
            for hh in range(2):
                kb.dma("POOL", ka[64:80, hh, :], I["c_e16"][:, :], writes=[Bka])
        for (v, Bv) in Vps.items:
            kb.op("POOL", lambda e, v=v: e.memset(v[:, :, :, 64:65], 1.0), writes=[Bv])
        t1, Bt1 = T.sb("t1", [128, 16, 16], F32)
        t2, Bt2 = T.sb("t2", [128, 16, 16], F32)
        kb.dma("SP", t1[:].rearrange("p a b -> p (a b)"), bc_row(I["c_t1"][0:1, :], 256), writes=[Bt1])
        kb.dma("SP", t2[:].rearrange("p a b -> p (a b)"), bc_row(I["c_t2"][0:1, :], 256), writes=[Bt2])
        augs = Rot([T.sb("aug", [128, 128], BF16) for _ in range(2)])
        for (a, Ba) in augs.items:
            kb.op("POOL", lambda e, a=a: e.memset(a[:], 0.0), writes=[Ba])
        km, Bkm = T.sb("km", [64, 16], F32)
        kmb, Bkmb = T.sb("kmb", [64, 16], BF16)
        gms = Rot([T.sb("gm", [128, 16], F32) for _ in range(2)])
        m8s = Rot([T.sb("m8", [128, 8], F32) for _ in range(2)])
        sels = Rot([T.sb("sel", [128, 16], F32) for _ in range(2)])
        R = dict(pS=Rot([T.ps("pS", [128, 512], F32) for _ in range(3)]),
                 pT=Rot([T.sb("pT", [128, 512], BF16) for _ in range(8)]))
        Obs = Rot([T.ps("OTm", [128, 512], F32) for _ in range(2)])
        Otk, BOtk = T.ps("Otok", [128, 512], F32)
        osbs = Rot([T.sb("osb", [128, 512], F32) for _ in range(2)])
        pgs = Rot([T.ps("pgate", [128, 512], F32) for _ in range(1)])
        ptr = Rot([T.ps("ptr", [128, 1024], BF16) for _ in range(1)])
        osts = Rot([T.sb("ost", [128, 4, 128], BF16) for _ in range(2)])
        rds = Rot([T.sb("rd", [128, 1], F32) for _ in range(4)])

        def load_pair(hp):
            QA, BQA = QAs.next()
            KA, BKA = KAs.next()
            Vp, BVp = Vps.next()
            for hh in range(2):
                h = 2 * hp + hh
                kb.dma("SP", QA[0:64, hh, :], self.FT[h * 64:(h + 1) * 64, :], writes=[BQA])
                kb.dma("SP", KA[0:64, hh, :], self.FT[512 + h * 64:512 + (h + 1) * 64, :], writes=[BKA])
            for hh in range(2):
                h = 2 * hp + hh
                tmv = self.TM[:, h * 64:(h + 1) * 64].rearrange("(c p) d -> p c d", p=128)
                for c4 in range(4):
                    kb.dma("SP", Vp[:, c4 * 8:(c4 + 1) * 8, hh, 0:64], tmv[:, c4 * 8:(c4 + 1) * 8, :], writes=[BVp])
            return QA, BQA, KA, BKA, Vp, BVp

        nxt = load_pair(0)
        pending = []
        for hp in range(4):
            QA, BQA, KA, BKA, Vp, BVp = nxt
            if hp + 1 < 4:
                nxt = load_pair(hp + 1)
            for hh in range(2):
                kb.op("DVE", lambda e: e.tensor_reduce(km[:], KA[0:64, hh, :].rearrange("p (b j) -> p b j", j=256), AX.X, ALU.add),
                      reads=[BKA], writes=[Bkm])
                kb.op("DVE", lambda e: e.tensor_scalar(kmb[:], km[:], 1.0 / 256, None, ALU.mult), reads=[Bkm], writes=[Bkmb])
                for t in range(NT):
                    own = t // 2
                    pg, Bpg = pgs.next()
                    kb.op("PE", lambda e: e.matmul(pg[:, 0:16], QA[0:64, hh, t * 128:(t + 1) * 128], kmb[:], start=True, stop=True),
                          reads=[BQA, Bkmb], writes=[Bpg])
                    gm, Bgm = gms.next()
                    m8, Bm8 = m8s.next()
                    sel, Bsel = sels.next()
                    aug, Baug = augs.next()
                    kb.op("DVE", lambda e: e.tensor_tensor(gm[:], pg[:, 0:16], t1[:, own, :], ALU.add), reads=[Bpg, Bt1], writes=[Bgm])
                    kb.op("DVE", lambda e: e.max(m8[:], gm[:]), reads=[Bgm], writes=[Bm8])
                    kb.op("DVE", lambda e: e.tensor_scalar(m8[:, 2:3], m8[:, 2:3], -1e29, None, ALU.max), reads=[Bm8], writes=[Bm8])
                    kb.op("DVE", lambda e: e.tensor_scalar(sel[:], gm[:], m8[:, 2:3], None, ALU.is_ge), reads=[Bgm, Bm8], writes=[Bsel])
                    kb.op("DVE", lambda e: e.tensor_tensor(sel[:], sel[:], t2[:, own, :], ALU.max), reads=[Bsel, Bt2], writes=[Bsel])
                    kb.op("DVE", lambda e: e.tensor_scalar(aug[:, 64:80], sel[:], -1.0, -MASKV, ALU.add, ALU.mult), reads=[Bsel], writes=[Baug])
                    pt, Bpt = ptr.next()
                    kb.op("PE", lambda e: e.transpose(pt[:, 0:128], aug[:], self.ident[:]), reads=[Baug, self.Bident], writes=[Bpt])
                    kb.op("ACT", lambda e: e.copy(QA[64:80, hh, t * 128:(t + 1) * 128], pt[64:80, 0:128]), reads=[Bpt], writes=[BQA])
            for qt in range(8):
                pairs = self.causal_pairs(qt)
                ost, Bost = osts.next()
                for hh in range(2):
                    OTm, BOTm = Obs.next()
                    self.attn_qtile(R, QA[0:80, hh, qt * 512:(qt + 1) * 512], BQA,
                                    lambda kc: (KA[0:80, hh, kc * 128:(kc + 1) * 128], 128), BKA,
                                    lambda kc: Vp[:, kc, hh, :], BVp, 65, pairs, OTm, BOTm, None, vstat=True)
                    def epilogue(OTm=OTm, BOTm=BOTm, ost=ost, Bost=Bost, hh=hh, qt=qt, hp=hp):
                        osb, Bosb = osbs.next()
                        self.ot_to_tok(OTm, BOTm, 65, osb, Bosb, Otk, BOtk)
                        Om, BOm = Otk, BOtk
                        for s in range(4):
                            rd, Brd = rds.next()
                            Os_ = Om[:, s * 128:s * 128 + 65]
                            self.epi_norm(T, Os_, BOm, 64, rd[:], Brd)
                            kb.op("DVE", lambda e, s=s, Os_=Os_, rd=rd: e.tensor_scalar(ost[:, s, hh * 64:(hh + 1) * 64], Os_[:, 0:64], rd[:, 0:1], None, ALU.mult),
                                  reads=[BOm, Brd], writes=[Bost])
                        if hh == 1:
                            osv = self.OS[qt * 512:(qt + 1) * 512, hp * 128:(hp + 1) * 128].rearrange("(s p) d -> p s d", p=128)
                            kb.dma("POOL", osv, ost[:], reads=[Bost])
                    if pending:
                        pending.pop()()
                    pending.append(epilogue)
        if pending:
            pending.pop()()
        T.close()

    def nsa_part(self):
        kb, I = self.kb, self.I
        for _ in range(self.cfg.get("pad_dve", 0)):
            kb.op("DVE", lambda e: e.memset(self.epsb[:], EPS), writes=[self.Bepsb])
        P = Scope(kb)
        CKc, BCKc = P.sb("CKc", [64, 2, 256], BF16)
        VCs = [P.sb("VC", [128, 2, 129], BF16) for _ in range(2)]
        kb.op("POOL", lambda e: e.memset(CKc[:], 0.0), writes=[BCKc])
        for g in range(2):
            vc, Bvc = VCs[g]
            for c in range(2):
                kb.dma("POOL", vc[:, c, 64:129], I["c_ov"][c * 128:(c + 1) * 128, :], writes=[Bvc])
        T = Scope(kb)
        CX = [T.sb("CX", [128, S], BF16) for _ in range(2)]
        kb.dma("SP", CX[0][0][:], self.FT[1536:1664, :], writes=[CX[0][1]])
        kb.dma("SP", CX[1][0][:], self.FT[1664:1792, :], writes=[CX[1][1]])
        w1, Bw1 = T.sb("w1", [128, 2 * 32 * 256], BF16)
        w1src = I["cmp_w1"].rearrange("d a l e -> d (a l e)")
        for half in range(2):
            for a in range(2):
                kb.dma("POOL", w1[half * 64:(half + 1) * 64, a * 8192:(a + 1) * 8192], w1src[:, a * 8192:(a + 1) * 8192], writes=[Bw1])
        w1v = w1[:].rearrange("p (a l e) -> p a l e", a=2, l=32)
        posb, Bposb = T.sb("posb", [64, 2, 34], BF16)
        kb.op("POOL", lambda e: e.memset(posb[:], 0.0), writes=[Bposb])
        kb.dma("POOL", posb[:, :, 0:32], I["cmp_posT"][:, :, :], writes=[Bposb])
        w2, Bw2 = T.sb("w2", [128, 2, 2, 64], BF16)
        for kv in range(2):
            for eh in range(2):
                kb.dma("POOL", w2[:, kv, eh, :], I["cmp_w2"][kv, eh * 128:(eh + 1) * 128, :], writes=[Bw2])
        b1, Bb1 = T.sb("b1", [128, 4], F32)
        pbs = Rot([T.ps("pb", [128, 512], F32) for _ in range(2)])
        phs = Rot([T.ps("ph", [128, 512], F32) for _ in range(3)])
        for kv in range(2):
            for eh in range(2):
                pb, Bpb = pbs.next()
                for l in range(32):
                    kb.op("PE", lambda e, l=l: e.matmul(pb[:, 0:2], w1v[0:64, kv, l, eh * 128:(eh + 1) * 128],
                                                        posb[0:64, kv, l:l + 2], start=(l == 0), stop=(l == 31)),
                          reads=[Bw1, Bposb], writes=[Bpb])
                kb.op("DVE", lambda e: e.tensor_copy(b1[:, kv * 2 + eh:kv * 2 + eh + 1], pb[:, 0:1]), reads=[Bpb], writes=[Bb1])
        hid = {}
        for kv in range(2):
            cx, Bcx = CX[kv]
            cxv = cx[:].rearrange("p (n j) -> p n j", j=16)
            for g in range(2):
                for eh in range(2):
                    ph, Bph = phs.next()
                    for l in range(32):
                        a = l // 16
                        kb.op("PE", lambda e, l=l, a=a: e.matmul(ph[:, 0:255], w1v[64 * g:64 * g + 64, kv, l, eh * 128:(eh + 1) * 128],
                                                                 cxv[64 * g:64 * g + 64, a:a + 255, l % 16], start=(l == 0), stop=(l == 31)),
                              reads=[Bw1, Bcx], writes=[Bph])
                    ht, Bht = T.sb("hid", [128, 256], BF16)
                    kb.op("POOL", lambda e: e.memset(ht[:], 0.0), writes=[Bht])
                    kb.op("ACT", lambda e: e.activation(ht[:, 0:255], ph[:, 0:255], AF.Silu, bias=b1[:, kv * 2 + eh:kv * 2 + eh + 1]),
                          reads=[Bph, Bb1], writes=[Bht])
                    hid[(kv, g, eh)] = (ht, Bht)
        for g in range(2):
            ph, Bph = phs.next()
            for eh in range(2):
                ht, Bht = hid[(0, g, eh)]
                kb.op("PE", lambda e: e.matmul(ph[0:64, 0:255], w2[:, 0, eh, :], ht[:, 0:255], start=(eh == 0), stop=(eh == 1)),
                      reads=[Bw2, Bht], writes=[Bph])
            kb.op("ACT", lambda e: e.copy(CKc[:, g, 0:255], ph[0:64, 0:255]), reads=[Bph], writes=[BCKc])
            vc, Bvc = VCs[g]
            for c in range(2):
                ph, Bph = phs.next()
                for eh in range(2):
                    ht, Bht = hid[(1, g, eh)]
                    kb.op("PE", lambda e: e.matmul(ph[:, 0:64], ht[:, c * 128:(c + 1) * 128], w2[:, 1, eh, :], start=(eh == 0), stop=(eh == 1)),
                          reads=[Bw2, Bht], writes=[Bph])
                kb.op("ACT", lambda e: e.copy(vc[:, c, 0:64], ph[:, 0:64]), reads=[Bph], writes=[Bvc])
        T.close()
        if self.cfg.get("stop") == "cmpmlp":
            if self.cfg.get("debug"):
                dck = self.tap("d_ckc", [64, 512], BF16)
                kb.dma("SP", dck[:, :], CKc[:].rearrange("p g n -> p (g n)"), reads=[BCKc])
                for g in range(2):
                    dvc = self.tap("d_vc%d" % g, [128, 258], BF16)
                    kb.dma("SP", dvc[:, :], VCs[g][0][:].rearrange("p c n -> p (c n)"), reads=[VCs[g][1]])
            P.close()
            return

        T = Scope(kb)
        nfv, Bnfv = T.sb("nfv", [128, NT, 64], F32)
        cst, Bcst = T.sb("cst", [128, NT, 64], F32)
        v01, Bv01 = T.sb("v01", [128, NT, 64], F32)
        kb.dma("SP", nfv[:], I["c_nfv"][:, :, :], writes=[Bnfv])
        kb.dma("SP", cst[:], I["c_cst"][:, :, :], writes=[Bcst])
        kb.dma("SP", v01[:], I["c_v01"][:, :, :], writes=[Bv01])
        QA4, BQA4 = T.sb("QA4", [128, 4, S], BF16)
        KS, BKS = T.sb("KS", [128, S], BF16)
        KW, BKW = T.sb("KW", [64, S], BF16)
        VS, BVS = T.sb("VS", [128, NT, 65], BF16)
        VW, BVW = T.sb("VW", [128, NT, 65], BF16)
        kb.op("POOL", lambda e: e.memset(VS[:, :, 64:65], 1.0), writes=[BVS])
        kb.op("POOL", lambda e: e.memset(VW[:, :, 64:65], 1.0), writes=[BVW])
        kb.dma("POOL", KS[64:128, :], I["c_e64"][:, :], writes=[BKS])
        augs = Rot([T.sb("aug", [128, 128], BF16) for _ in range(2)])
        for (a, Ba) in augs.items:
            kb.op("POOL", lambda e, a=a: e.memset(a[:], 0.0), writes=[Ba])
        R = dict(pS=Rot([T.ps("pS", [128, 512], F32) for _ in range(2)]),
                 pT=Rot([T.sb("pT", [128, 512], BF16) for _ in range(8)]))
        Oc = [T.ps("Oc", [128, 512], F32) for _ in range(2)]
        BOc = Oc[0][1]
        Osw = Rot([T.ps("OTsw", [128, 512], F32) for _ in range(2)])
        Otk, BOtk = T.ps("Otok", [128, 512], F32)
        osbs = Rot([T.sb("osb", [128, 512], F32) for _ in range(2)])
        ptr = Rot([T.ps("ptr", [128, 1024], BF16) for _ in range(1)])
        occ, Bocc = T.sb("occ", [128, 4, 4, 64], F32)
        imp, Bimp = T.sb("imp", [128, 4, 64], F32)
        itmp, Bitmp = T.sb("itmp", [128, 64], F32)
        sc, Bsc = T.sb("sc", [128, 64], F32)
        sc2, Bsc2 = T.sb("sc2", [128, 64], F32)
        m8a, Bm8a = T.sb("m8a", [128, 8], F32)
        m8b, Bm8b = T.sb("m8b", [128, 8], F32)
        selm, Bselm = T.sb("selm", [128, 64], F32)
        rds = Rot([T.sb("rd", [128, 2], F32) for _ in range(6)])
        osts = Rot([T.sb("ost", [128, 4, 256], BF16) for _ in range(2)])
        gsb = self.gate_sb

        for g in range(2):
            for r in range(4):
                h = 4 * g + r
                kb.dma("SP", QA4[0:64, r, :], self.FT[1024 + h * 64:1024 + (h + 1) * 64, :], writes=[BQA4])
            kb.dma("SP", KS[0:64, :], self.FT[1792 + 64 * g:1792 + 64 * g + 64, :], writes=[BKS])
            kb.dma("SP", KW[:], self.FT[1920 + 64 * g:1920 + 64 * g + 64, :], writes=[BKW])
            tms = self.TM[:, 512 + 64 * g:512 + 64 * g + 64].rearrange("(c p) d -> p c d", p=128)
            tmw = self.TM[:, 640 + 64 * g:640 + 64 * g + 64].rearrange("(c p) d -> p c d", p=128)
            for c4 in range(4):
                kb.dma("SP", VS[:, c4 * 8:(c4 + 1) * 8, 0:64], tms[:, c4 * 8:(c4 + 1) * 8, :], writes=[BVS])
                kb.dma("SP", VW[:, c4 * 8:(c4 + 1) * 8, 0:64], tmw[:, c4 * 8:(c4 + 1) * 8, :], writes=[BVW])
            vc, Bvc = VCs[g]
            parts = self.cfg.get("nsa_parts", ("cmp", "select", "sw"))
            for qt in self.cfg.get("nsa_qts", range(8)):
                ost, Bost = osts.next()
                cchunks = [0] + ([1] if qt >= 4 else [])
                cpairs = [(c, [(s, "full") for s in range(4)]) for c in cchunks]

                def cmp_mask(c, pT, BpT, krows):
                    kb.op("POOL", lambda e: e.affine_select(pT[:], pT[:], [[1, 512]], ALU.is_ge, 0.0,
                                                            base=512 * qt - 2048 * c - 31, channel_multiplier=-16),
                          reads=[BpT], writes=[BpT])

                for r in (range(4) if "cmp" in parts else ()):
                    h = 4 * g + r
                    self.attn_qtile(R, QA4[0:64, r, qt * 512:(qt + 1) * 512], BQA4,
                                    lambda c: (CKc[:, g, c * 128:(c + 1) * 128], 128), BCKc,
                                    lambda c: vc[:, c, :], Bvc, 129, cpairs, None, BOc,
                                    lambda O, s: Oc[s // 2][0][:, (s % 2) * 256:(s % 2) * 256 + 129], post_exp=cmp_mask, bank_of=lambda s: s // 2)
                    for s in range(4):
                        t = 4 * qt + s
                        O_ = Oc[s // 2][0][:, (s % 2) * 256:(s % 2) * 256 + 129]
                        rd, Brd = rds.next()
                        self.epi_norm(T, O_, BOc, 64, rd[:, 0:1], Brd)
                        if r == 0:
                            kb.op("DVE", lambda e, s=s: e.tensor_scalar(imp[:, s, :], O_[:, 65:129], rd[:, 0:1], None, ALU.mult),
                                  reads=[BOc, Brd], writes=[Bimp])
                        else:
                            kb.op("DVE", lambda e: e.tensor_scalar(itmp[:], O_[:, 65:129], rd[:, 0:1], None, ALU.mult),
                                  reads=[BOc, Brd], writes=[Bitmp])
                            kb.op("DVE", lambda e, s=s: e.tensor_tensor(imp[:, s, :], imp[:, s, :], itmp[:], ALU.add),
                                  reads=[Bitmp, Bimp], writes=[Bimp])
                        kb.op("DVE", lambda e: e.tensor_tensor(rd[:, 1:2], rd[:, 0:1], gsb[:, t, h:h + 1], ALU.mult), reads=[Brd, self.Bgate], writes=[Brd])
                        kb.op("DVE", lambda e, s=s, r=r: e.tensor_scalar(occ[:, r, s, :], O_[:, 0:64], rd[:, 1:2], None, ALU.mult),
                              reads=[BOc, Brd], writes=[Bocc])
                for s in (range(4) if "select" in parts else ()):
                    t = 4 * qt + s
                    aug, Baug = augs.next()
                    kb.op("DVE", lambda e: e.tensor_tensor(sc[:], imp[:, s, :], nfv[:, t, :], ALU.mult), reads=[Bimp, Bnfv], writes=[Bsc])
                    kb.op("DVE", lambda e: e.tensor_tensor(sc[:], sc[:], cst[:, t, :], ALU.add), reads=[Bsc, Bcst], writes=[Bsc])
                    kb.op("DVE", lambda e: e.max(m8a[:], sc[:]), reads=[Bsc], writes=[Bm8a])
                    kb.op("DVE", lambda e: e.match_replace(sc2[:], m8a[:], sc[:], -1e30), reads=[Bsc, Bm8a], writes=[Bsc2])
                    kb.op("DVE", lambda e: e.max(m8b[:], sc2[:]), reads=[Bsc2], writes=[Bm8b])
                    kb.op("DVE", lambda e: e.tensor_scalar(selm[:], sc[:], m8b[:, 7:8], None, ALU.is_ge), reads=[Bsc, Bm8b], writes=[Bselm])
                    kb.op("DVE", lambda e: e.tensor_tensor(selm[:], selm[:], v01[:, t, :], ALU.mult), reads=[Bselm, Bv01], writes=[Bselm])
                    kb.op("DVE", lambda e: e.tensor_scalar(aug[:, 64:128], selm[:], -1.0, -MASKV, ALU.add, ALU.mult), reads=[Bselm], writes=[Baug])
                    pt, Bpt = ptr.next()
                    kb.op("PE", lambda e: e.transpose(pt[:, 0:128], aug[:], self.ident[:]), reads=[Baug, self.Bident], writes=[Bpt])
                    for r in range(4):
                        kb.op("ACT", lambda e, r=r: e.copy(QA4[64:128, r, t * 128:(t + 1) * 128], pt[64:128, 0:128]), reads=[Bpt], writes=[BQA4])
                pending = []
                spairs = self.causal_pairs(qt)
                wpairs = []
                for kc in range(max(0, 4 * qt - 4), 4 * qt + 4):
                    subs = []
                    for s in range(4):
                        dlt = 4 * qt + s - kc
                        if dlt == 0:
                            subs.append((s, "tri"))
                        elif 1 <= dlt <= 3:
                            subs.append((s, "full"))
                        elif dlt == 4:
                            subs.append((s, "atri"))
                    if subs:
                        wpairs.append((kc, subs))
                for r in (range(4) if "sw" in parts else ()):
                    h = 4 * g + r
                    for br, (pairs, qrows, kl, Bkl, vt, Bvt) in enumerate((
                            (spairs, 128, KS, BKS, VS, BVS), (wpairs, 64, KW, BKW, VW, BVW))):
                        if br not in self.cfg.get("nsa_br", (0, 1)):
                            continue
                        OTb, BOTb = Osw.next()
                        self.attn_qtile(R, QA4[0:qrows, r, qt * 512:(qt + 1) * 512], BQA4,
                                        lambda kc, kl=kl, qrows=qrows: (kl[0:qrows, kc * 128:(kc + 1) * 128], 128), Bkl,
                                        lambda kc, vt=vt: vt[:, kc, :], Bvt, 65, pairs, OTb, BOTb, None, vstat=True)
                        def epilogue(OTb=OTb, BOTb=BOTb, br=br, r=r, h=h, qt=qt, ost=ost, Bost=Bost):
                            osb, Bosb = osbs.next()
                            self.ot_to_tok(OTb, BOTb, 65, osb, Bosb, Otk, BOtk)
                            Ob, BOb = Otk, BOtk
                            for s in range(4):
                                t = 4 * qt + s
                                O_ = Ob[:, s * 128:s * 128 + 65]
                                rd, Brd = rds.next()
                                self.epi_norm(T, O_, BOb, 64, rd[:, 0:1], Brd)
                                gcol = (br + 1) * 8 + h
                                kb.op("DVE", lambda e, rd=rd, t=t, gcol=gcol: e.tensor_tensor(rd[:, 1:2], rd[:, 0:1], gsb[:, t, gcol:gcol + 1], ALU.mult),
                                      reads=[Brd, self.Bgate], writes=[Brd])
                                if br == 0:
                                    kb.op("DVE", lambda e, O_=O_, rd=rd: e.tensor_scalar(itmp[:], O_[:, 0:64], rd[:, 1:2], None, ALU.mult),
                                          reads=[BOb, Brd], writes=[Bitmp])
                                    kb.op("DVE", lambda e, s=s: e.tensor_tensor(occ[:, r, s, :], occ[:, r, s, :], itmp[:], ALU.add),
                                          reads=[Bitmp, Bocc], writes=[Bocc])
                                else:
                                    kb.op("DVE", lambda e, s=s, O_=O_, rd=rd: e.scalar_tensor_tensor(ost[:, s, r * 64:(r + 1) * 64], O_[:, 0:64], rd[:, 1:2], occ[:, r, s, :], ALU.mult, ALU.add),
                                          reads=[BOb, Brd, Bocc], writes=[Bost])
                        if pending:
                            pending.pop()()
                        pending.append(epilogue)
                if pending:
                    pending.pop()()
                osv = self.OS[qt * 512:(qt + 1) * 512, 512 + g * 256:512 + (g + 1) * 256].rearrange("(s p) d -> p s d", p=128)
                kb.dma("POOL", osv, ost[:], reads=[Bost])
                if self.cfg.get("nsa_barrier"):
                    kb.barrier()
        T.close()
        P.close()


def _consts():
    c = {}
    c["c_ident"] = np.eye(128, dtype=np.float32)
    k = np.arange(128)[:, None]
    q = np.arange(128)[None, :]
    c["c_tri"] = (q >= k).astype(np.float32)
    c["c_atri"] = (k > q).astype(np.float32)
    key = np.arange(S)[None, :]
    c["c_e16"] = (key // 256 == np.arange(16)[:, None]).astype(np.float32)
    c["c_e64"] = (key // 64 == np.arange(64)[:, None]).astype(np.float32)
    ncmp = 255
    cs = np.arange(ncmp) * 16
    ss = np.arange(64) * 64
    ov = np.minimum(cs[:, None] + 32, ss[None, :] + 64) - np.maximum(cs[:, None], ss[None, :])
    ovp = np.zeros((256, 65), np.float32)
    ovp[:255, 0] = 1.0
    ovp[:255, 1:] = np.clip(ov, 0, None) / 32.0
    c["c_ov"] = ovp
    own = np.arange(16)[:, None]
    blk = np.arange(16)[None, :]
    c["c_t1"] = np.where(blk < own, 0.0, -1e30).astype(np.float32).reshape(1, 256)
    c["c_t2"] = (blk == own).astype(np.float32).reshape(1, 256)
    p = np.arange(128)[:, None, None]
    t = np.arange(NT)[None, :, None]
    m = np.arange(64)[None, None, :]
    cur = (t * 128 + p) // 64
    ok = m <= cur
    forced = ok & ((m == 0) | (m >= cur - 1))
    c["c_nfv"] = (ok & ~forced).astype(np.float32)
    c["c_cst"] = np.where(ok, np.where(forced, 1e4 + m, 0.0), -1e30).astype(np.float32)
    c["c_v01"] = ok.astype(np.float32)
    return c


def _core_inputs(b, inp, consts):
    f = lambda a: np.ascontiguousarray(np.asarray(a), dtype=np.float32)
    m = {}
    m["x"] = f(inp["x"][b])
    m["cT"] = f(np.asarray(inp["c"][b]).reshape(8, 128).T)
    m["posT"] = np.ascontiguousarray(np.asarray(inp["positions"][b]).reshape(NT, 128).T.astype(np.int32))
    for k in ("ada_w", "ada_b", "attn_norm", "ffn_norm", "ffn_w_gate", "ffn_w_up", "ffn_w_down"):
        m[k] = f(inp[k])
    m["sp_w_in"] = f(inp["sp_w_in"][0]); m["sp_w_out"] = f(inp["sp_w_out"][0])
    m["moba_q_norm"] = f(inp["moba_q_norm"]); m["moba_k_norm"] = f(inp["moba_k_norm"])
    m["nsa_q_norm"] = f(inp["nsa_q_norm"]); m["nsa_k_norm"] = f(inp["nsa_k_norm"][0])
    m["cmp_posT"] = f(np.asarray(inp["nsa_cmp_pos"][0]).transpose(2, 0, 1))
    m["cmp_w1"] = f(np.asarray(inp["nsa_cmp_w1"][0]).transpose(2, 0, 1, 3))
    m["cmp_w2"] = f(inp["nsa_cmp_w2"][0])
    m["diff_w_in"] = f(inp["diff_w_in"][0]); m["diff_w_out"] = f(inp["diff_w_out"][0])
    m["diff_q_norm"] = f(inp["diff_q_norm"]); m["diff_k_norm"] = f(inp["diff_k_norm"])
    m["diff_lambda"] = f(np.asarray(inp["diff_lambda"][0]).reshape(1, 256))
    m["diff_out_norm"] = f(inp["diff_out_norm"])
    m.update(consts)
    return m


_NC_CACHE = {}


def kernel(**inputs):
    consts = _consts()
    if "nc" not in _NC_CACHE:
        _NC_CACHE["nc"] = Prog({}).build()
    nc = _NC_CACHE["nc"]
    in_maps = [_core_inputs(b, inputs, consts) for b in range(8)]
    res = run_bass_kernel_spmd(nc, in_maps, core_ids=list(range(8)))
    return np.stack([np.asarray(r["out"], dtype=np.float32) for r in res.results], axis=0)
```

```python
from contextlib import ExitStack
import numpy as np
import concourse.bass as bass
import concourse.mybir as mybir
from concourse.bass_utils import run_bass_kernel_spmd

F32 = mybir.dt.float32
BF16 = mybir.dt.bfloat16
I32 = mybir.dt.int32
AF = mybir.ActivationFunctionType
ALU = mybir.AluOpType
AX = mybir.AxisListType

S = 4096
D = 1024
NT = S // 128
FFN = 2816
NJ = FFN // 128
EPS = 1e-6
MASKV = -30000.0
SP_IN = 2840
N_DMA_SEMS = 24


class Buf:
    __slots__ = ("w", "r", "name")

    def __init__(self, name=""):
        self.w = None
        self.r = {}
        self.name = name


class Rot:
    def __init__(self, items):
        self.items = list(items)
        self.i = 0

    def next(self):
        it = self.items[self.i]
        self.i = (self.i + 1) % len(self.items)
        return it


class KB:
    def __init__(self, nc):
        self.nc = nc
        self.es = ExitStack()
        self.engs = {"PE": nc.tensor, "ACT": nc.scalar, "DVE": nc.vector, "POOL": nc.gpsimd, "SP": nc.sync}
        self.sem = {}
        self.cnt = {}
        for e in ("PE", "ACT", "DVE", "POOL"):
            self.sem[e] = self.es.enter_context(nc.semaphore("s_" + e))
            self.cnt[e] = 0
        self.dsem = [self.es.enter_context(nc.semaphore("d_%d" % i)) for i in range(N_DMA_SEMS)]
        self.dcnt = [0] * N_DMA_SEMS
        self.dnext = 0
        self.dnext2 = [0, 0]
        self.waited = {e: {} for e in self.engs}
        self.n_inst = 0
        self.uid = 0
        self.limit = None
        self.log = None

    def name(self, p):
        self.uid += 1
        return "%s_%d" % (p, self.uid)

    def _wait(self, E, key, c, raw=False):
        if key == E and E == "PE":
            return
        w = self.waited[E]
        if w.get(key, 0) >= c:
            return
        w[key] = c
        if isinstance(key, int):
            self.engs[E].wait_ge(self.dsem[key], c)
        else:
            self.engs[E].wait_ge(self.sem[key], c)

    def _deps(self, E, reads, writes):
        for b in reads:
            if b.w is not None:
                self._wait(E, b.w[0], b.w[1], raw=True)
        for b in writes:
            if b.w is not None:
                self._wait(E, b.w[0], b.w[1])
            for k, c in b.r.items():
                self._wait(E, k, c)

    def _mark(self, tok, reads, writes):
        for b in reads:
            if b.r.get(tok[0], 0) < tok[1]:
                b.r[tok[0]] = tok[1]
        for b in writes:
            b.w = tok
            b.r = {}

    def op(self, E, fn, reads=(), writes=()):
        if self.limit is not None and self.n_inst >= self.limit:
            return None
        self._deps(E, reads, writes)
        if self.log is not None:
            import sys as _sys
            self.log.append((self.n_inst, E, _sys._getframe(1).f_lineno))
        ins = fn(self.engs[E])
        self.cnt[E] += 1
        ins.then_inc(self.sem[E], 1)
        tok = (E, self.cnt[E])
        self._mark(tok, reads, writes)
        self.n_inst += 1
        return tok

    def dma(self, Q, out_ap, in_ap, reads=(), writes=(), **kw):
        if self.limit is not None and self.n_inst >= self.limit:
            return None
        self._deps(Q, reads, writes)
        half = N_DMA_SEMS // 2
        qi = 0 if Q == "SP" else 1
        k = qi * half + self.dnext2[qi]
        self.dnext2[qi] = (self.dnext2[qi] + 1) % half
        if self.dcnt[k] > 0:
            self._wait(Q, k, self.dcnt[k])
        if self.log is not None:
            import sys as _sys
            self.log.append((self.n_inst, "DMA-" + Q, _sys._getframe(1).f_lineno))
        ins = self.engs[Q].dma_start(out=out_ap, in_=in_ap, **kw)
        self.dcnt[k] += 16
        ins.then_inc(self.dsem[k], 16)
        tok = (k, self.dcnt[k])
        self._mark(tok, reads, writes)
        self.n_inst += 1
        return tok

    def barrier(self, engines=("PE", "ACT", "DVE", "POOL", "SP")):
        for E in engines:
            for e2 in ("PE", "ACT", "DVE", "POOL"):
                if self.cnt[e2] > 0:
                    self._wait(E, e2, self.cnt[e2])
            for k in range(N_DMA_SEMS):
                if self.dcnt[k] > 0:
                    self._wait(E, k, self.dcnt[k])


class Scope:
    def __init__(self, kb):
        self.kb = kb
        self.es = ExitStack()

    def sb(self, name, shape, dt):
        t = self.es.enter_context(self.kb.nc.sbuf_tensor(self.kb.name(name), list(shape), dt))
        return t, Buf(name)

    def ps(self, name, shape, dt):
        t = self.es.enter_context(self.kb.nc.psum_tensor(self.kb.name(name), list(shape), dt))
        return t, Buf(name)

    def close(self):
        self.kb.barrier()
        self.es.close()


def bc_row(ap_row, n):
    return ap_row.to_broadcast([128, n])


class Prog:
    def __init__(self, cfg):
        self.cfg = cfg
        self.nc = bass.Bass("TRN2", target_bir_lowering=False)
        self.kb = KB(self.nc)
        self.kb.limit = cfg.get("limit")
        self.I = {}
        self.taps = {}

    def din(self, name, shape, dt=F32):
        self.I[name] = self.nc.dram_tensor(name, list(shape), dt, kind="ExternalInput").ap()
        return self.I[name]

    def dscr(self, name, shape, dt):
        if self.cfg.get("debug"):
            return self.nc.dram_tensor(name, list(shape), dt, kind="ExternalOutput").ap()
        return self.nc.dram_tensor(name, list(shape), dt).ap()

    def tap(self, name, shape, dt=F32):
        self.taps[name] = self.nc.dram_tensor(name, list(shape), dt, kind="ExternalOutput").ap()
        return self.taps[name]

    def declare(self):
        d = self.din
        d("x", [S, D]); d("cT", [128, 8]); d("posT", [128, NT], I32)
        d("ada_w", [2, D, 6 * D]); d("ada_b", [2, 6 * D])
        d("attn_norm", [2, D]); d("ffn_norm", [2, D])
        d("ffn_w_gate", [2, D, FFN]); d("ffn_w_up", [2, D, FFN]); d("ffn_w_down", [2, FFN, D])
        d("sp_w_in", [D, SP_IN]); d("sp_w_out", [D, D])
        d("moba_q_norm", [1, 64]); d("moba_k_norm", [1, 64]); d("nsa_q_norm", [1, 64]); d("nsa_k_norm", [3, 64])
        d("cmp_posT", [64, 2, 32]); d("cmp_w1", [64, 2, 32, 256]); d("cmp_w2", [2, 256, 64])
        d("diff_w_in", [D, 3072]); d("diff_w_out", [D, D])
        d("diff_q_norm", [1, 64]); d("diff_k_norm", [1, 64]); d("diff_lambda", [1, 256]); d("diff_out_norm", [1, 128])
        d("c_ident", [128, 128]); d("c_tri", [128, 128]); d("c_atri", [128, 128])
        d("c_e16", [16, S]); d("c_e64", [64, S]); d("c_ov", [256, 65])
        d("c_t1", [1, 256]); d("c_t2", [1, 256])
        d("c_nfv", [128, NT, 64]); d("c_cst", [128, NT, 64]); d("c_v01", [128, NT, 64])
        self.out = self.nc.dram_tensor("out", [S, D], F32, kind="ExternalOutput").ap()
        self.xmid = self.dscr("xmid", [S, D], F32)
        self.x1s = self.dscr("x1s", [S, D], F32)
        self.FT = self.dscr("FT", [2048, S], BF16)
        self.TM = self.dscr("TM", [S, D], BF16)
        self.OS = self.dscr("OS", [S, D], BF16)
        self.H2T = self.dscr("H2T", [D, S], BF16)

    def setup(self):
        kb, I = self.kb, self.I
        self.G = Scope(kb)
        G = self.G
        self.ident, self.Bident = G.sb("ident", [128, 128], BF16)
        kb.dma("POOL", self.ident[:], I["c_ident"][:, :], writes=[self.Bident])
        self.identF, self.BidentF = G.sb("identF", [128, 128], F32)
        kb.dma("SP", self.identF[:], I["c_ident"][:, :], writes=[self.BidentF])
        self.tri, self.Btri = G.sb("tri", [128, 128], BF16)
        kb.dma("POOL", self.tri[:], I["c_tri"][:, :], writes=[self.Btri])
        self.atri, self.Batri = G.sb("atri", [128, 128], BF16)
        kb.dma("POOL", self.atri[:], I["c_atri"][:, :], writes=[self.Batri])
        self.cosT, self.Bcos = G.sb("cosT", [128, NT, 8], F32)
        self.sinT, self.Bsin = G.sb("sinT", [128, NT, 8], F32)
        self.mod, self.Bmod = G.sb("mod", [128, 6, D], F32)
        self.silc, self.Bsilc = G.sb("silc", [128, 8, 128], F32)
        with_scope = Scope(kb)
        T = with_scope
        pi, Bpi = T.sb("pi", [128, NT], I32)
        pf, Bpf = T.sb("pf", [128, NT], F32)
        ang, Bang = T.sb("ang", [128, NT, 8], F32)
        tmp, Btmp = T.sb("tmp", [128, NT, 8], F32)
        tmp2, Btmp2 = T.sb("tmp2", [128, NT, 8], F32)
        kb.dma("SP", pi[:], I["posT"][:, :], writes=[Bpi])
        kb.op("DVE", lambda e: e.tensor_copy(pf[:], pi[:]), reads=[Bpi], writes=[Bpf])
        inv = (1.0 / (np.float32(500000.0) ** (np.arange(0, 16, 2, dtype=np.float32) / np.float32(16)))).astype(np.float32)
        for j in range(8):
            kb.op("DVE", lambda e, j=j: e.tensor_scalar(ang[:, :, j], pf[:], float(inv[j]), None, ALU.mult),
                  reads=[Bpf], writes=[Bang])
        TWO_PI = float(2 * np.pi)
        MAG = 12582912.0

        def sin_of(dst, Bdst, shift):
            if shift != 0.0:
                kb.op("DVE", lambda e: e.tensor_scalar(tmp2[:], ang[:], shift, None, ALU.add), reads=[Bang], writes=[Btmp2])
                src, Bsrc = tmp2, Btmp2
            else:
                src, Bsrc = ang, Bang
            kb.op("DVE", lambda e: e.tensor_scalar(tmp[:], src[:], 1.0 / TWO_PI, MAG, ALU.mult, ALU.add), reads=[Bsrc], writes=[Btmp])
            kb.op("DVE", lambda e: e.tensor_scalar(tmp[:], tmp[:], -MAG, -TWO_PI, ALU.add, ALU.mult), reads=[Btmp], writes=[Btmp])
            kb.op("DVE", lambda e: e.tensor_tensor(tmp[:], src[:], tmp[:], ALU.add), reads=[Bsrc, Btmp], writes=[Btmp])
            kb.op("DVE", lambda e: e.tensor_scalar(tmp[:], tmp[:], float(np.pi), float(-np.pi), ALU.min, ALU.max), reads=[Btmp], writes=[Btmp])
            kb.op("ACT", lambda e: e.activation(dst[:], tmp[:], AF.Sin), reads=[Btmp], writes=[Bdst])

        sin_of(self.sinT, self.Bsin, 0.0)
        sin_of(self.cosT, self.Bcos, float(np.pi / 2))
        ct, Bct = T.sb("ct", [128, 8], F32)
        kb.dma("SP", ct[:], I["cT"][:, :], writes=[Bct])
        kb.op("ACT", lambda e: e.activation(ct[:], ct[:], AF.Silu), reads=[Bct], writes=[Bct])
        kb.op("DVE", lambda e: e.tensor_copy(self.silc[:], ct[:].unsqueeze(2).to_broadcast([128, 8, 128])),
              reads=[Bct], writes=[self.Bsilc])
        T.close()

    def compute_mod(self, li):
        kb, I = self.kb, self.I
        T = Scope(kb)
        wch = [T.sb("adaw", [128, 8, 512], F32) for _ in range(2)]
        bch = [T.sb("adab", [128, 512], F32) for _ in range(2)]
        pm = [T.ps("pmod", [128, 512], F32) for _ in range(2)]
        nrm, Bnrm = T.sb("nrm", [128, 2, D], F32)
        kb.dma("SP", nrm[:, 0, :], bc_row(I["attn_norm"][li:li + 1, :], D), writes=[Bnrm])
        kb.dma("SP", nrm[:, 1, :], bc_row(I["ffn_norm"][li:li + 1, :], D), writes=[Bnrm])
        wv = I["ada_w"][li].rearrange("(kc p) n -> p kc n", p=128)
        for g in range(12):
            w, Bw = wch[g % 2]
            b, Bb = bch[g % 2]
            p, Bp = pm[g % 2]
            kb.dma("SP", w[:], wv[:, :, g * 512:(g + 1) * 512], writes=[Bw])
            kb.dma("SP", b[:], bc_row(I["ada_b"][li:li + 1, g * 512:(g + 1) * 512], 512), writes=[Bb])
            for kc in range(8):
                kb.op("PE", lambda e, kc=kc: e.matmul(p[:], self.silc[:, kc, :], w[:, kc, :], start=(kc == 0), stop=(kc == 7)),
                      reads=[Bw, self.Bsilc], writes=[Bp])
            sl = self.mod[:, g // 2, (g % 2) * 512:(g % 2 + 1) * 512]
            kb.op("DVE", lambda e: e.tensor_tensor(sl, p[:], b[:], ALU.add), reads=[Bp, Bb], writes=[self.Bmod])
        kb.op("DVE", lambda e: e.scalar_tensor_tensor(self.mod[:, 1, :], self.mod[:, 1, :], 1.0, nrm[:, 0, :], ALU.add, ALU.mult),
              reads=[self.Bmod, Bnrm], writes=[self.Bmod])
        kb.op("DVE", lambda e: e.scalar_tensor_tensor(self.mod[:, 4, :], self.mod[:, 4, :], 1.0, nrm[:, 1, :], ALU.add, ALU.mult),
              reads=[self.Bmod, Bnrm], writes=[self.Bmod])
        T.close()

    def rms_mod_tile(self, T, xt, Bxt, hb, Bhb, gi, si, scr):
        kb = self.kb
        junk, Bjunk, ss, Bss, hn, Bhn = scr
        kb.op("ACT", lambda e: e.activation(junk[:], xt[:], AF.Square, accum_out=ss[:]), reads=[Bxt], writes=[Bjunk, Bss])
        kb.op("ACT", lambda e: e.activation(ss[:], ss[:], AF.Sqrt, bias=self.epsb[:], scale=1.0 / D), reads=[Bss, self.Bepsb], writes=[Bss])
        kb.op("DVE", lambda e: e.reciprocal(ss[:], ss[:]), reads=[Bss], writes=[Bss])
        kb.op("DVE", lambda e: e.scalar_tensor_tensor(hn[:], xt[:], ss[:, 0:1], self.mod[:, gi, :], ALU.mult, ALU.mult),
              reads=[Bxt, Bss, self.Bmod], writes=[Bhn])
        kb.op("POOL", lambda e: e.tensor_tensor(hb[:], hn[:], self.mod[:, si, :], ALU.add), reads=[Bhn, self.Bmod], writes=[Bhb])

    def transpose8(self, src, Bsrc, pt, Bpt, dst_ap, Bdst, eng="ACT"):
        kb = self.kb
        for j in range(8):
            kb.op("PE", lambda e, j=j: e.transpose(pt[:, j * 128:(j + 1) * 128], src[:, j * 128:(j + 1) * 128], self.ident[:]),
                  reads=[Bsrc, self.Bident], writes=[Bpt])
        if eng == "ACT":
            kb.op("ACT", lambda e: e.copy(dst_ap, pt[:].rearrange("p (j t) -> p j t", j=8)), reads=[Bpt], writes=[Bdst])
        else:
            kb.op("DVE", lambda e: e.tensor_copy(dst_ap, pt[:].rearrange("p (j t) -> p j t", j=8)), reads=[Bpt], writes=[Bdst])

    def load_w_bf16(self, dst, Bdst, w_ap, nk):
        for kc in range(nk):
            self.kb.dma("POOL", dst[:, kc, :], w_ap[kc * 128:(kc + 1) * 128, :], writes=[Bdst])

    def phase_a(self, li, x_src):
        kb, I = self.kb, self.I
        T = Scope(kb)
        if li == 1:
            w_ap, ncols = I["diff_w_in"], 3072
            groups = []
            for g in range(6):
                if g < 4:
                    gi = 0 if g < 2 else 1
                    blocks = [("T", g * 512 + b * 128) for b in range(4)]
                    groups.append(dict(c0=g * 512, w=512, qk=[(0, 8)], gain=gi, blocks=blocks, gate=None))
                else:
                    blocks = [("V", (g - 4) * 512 + b * 128) for b in range(4)]
                    groups.append(dict(c0=g * 512, w=512, qk=[], gain=None, blocks=blocks, gate=None))
            gain_srcs = [[(I["diff_q_norm"][0:1, :], 0, 8)], [(I["diff_k_norm"][0:1, :], 0, 8)]]
        else:
            w_ap, ncols = I["sp_w_in"], SP_IN
            kn = I["nsa_k_norm"]
            groups = [
                dict(c0=0, w=512, qk=[(0, 8)], gain=0, blocks=[("T", b * 128) for b in range(4)], gate=None),
                dict(c0=512, w=512, qk=[(0, 8)], gain=1, blocks=[("T", 512 + b * 128) for b in range(4)], gate=None),
                dict(c0=1024, w=512, qk=[], gain=None, blocks=[("V", b * 128) for b in range(4)], gate=None),
                dict(c0=1536, w=512, qk=[(0, 8)], gain=2, blocks=[("T", 1024 + b * 128) for b in range(4)], gate=None),
                dict(c0=2048, w=512, qk=[(0, 2), (4, 6)], gain=3,
                     blocks=[("T", 1536), ("T", 1664), ("T", 1792), ("V", 512)], gate=None),
                dict(c0=2560, w=280, qk=[(0, 2)], gain=4, blocks=[("T", 1920), ("V", 640)], gate=(256, 24)),
            ]
            gain_srcs = [
                [(I["moba_q_norm"][0:1, :], 0, 8)], [(I["moba_k_norm"][0:1, :], 0, 8)], [(I["nsa_q_norm"][0:1, :], 0, 8)],
                [(kn[0:1, :], 0, 2), (kn[1:2, :], 4, 6)], [(kn[2:3, :], 0, 2)],
            ]
        nk = 8
        W, BW = T.sb("w_in", [128, nk, ncols], BF16)
        self.load_w_bf16(W, BW, w_ap, nk)
        gains = []
        for gs in gain_srcs:
            gr, Bgr = T.sb("gainrow", [128, 8, 64], F32)
            kb.op("POOL", lambda e: e.memset(gr[:], 1.0), writes=[Bgr])
            for (src, u0, u1) in gs:
                for u in range(u0, u1):
                    kb.dma("SP", gr[:, u, :], bc_row(src, 64), writes=[Bgr])
            gains.append((gr, Bgr))
        NG = len(groups)
        xts = Rot([T.sb("xt", [128, D], F32) for _ in range(3)])
        hbs = Rot([T.sb("hb", [128, D], BF16) for _ in range(2)])
        hTs = Rot([T.sb("hT", [128, 8, 128], BF16) for _ in range(2)])
        junk, Bjunk = T.sb("junk", [128, D], BF16)
        sss = Rot([T.sb("ss", [128, 1], F32) for _ in range(2)])
        hn, Bhn = T.sb("hn", [128, D], F32)
        sqs = [T.sb("sq", [128, 512], F32) for _ in range(NG)]
        ss8l = [T.sb("ss8", [128, 8], F32) for _ in range(NG)]
        qnl = [T.sb("qn", [128, 8, 64], F32) for _ in range(NG)]
        postl = [T.sb("post", [128, 512], BF16) for _ in range(NG)]
        rtl = [T.sb("rt", [128, 4, 8, 8], F32) for _ in range(NG)]
        pts = Rot([T.ps("ptA", [128, 1024], BF16) for _ in range(1)])
        pgl = [T.ps("pgA", [128, 512], F32) for _ in range(NG)]
        pbt, _ = T.ps("ptB", [128, 1024], BF16)
        pbh = Rot([(pbt[:, 0:512], Buf("pb0")), (pbt[:, 512:1024], Buf("pb1"))])
        stages = {}
        for gi_, g in enumerate(groups):
            if any(k == "T" for k, _ in g["blocks"]):
                stages[gi_] = [T.sb("stage", [128, 4, 512], BF16) for _ in range(2)]

        def load_x(t):
            xt, Bxt = xts.next()
            kb.dma("SP", xt[:], x_src[t * 128:(t + 1) * 128, :], writes=[Bxt])
            return xt, Bxt

        def prep(t, xtb):
            xt, Bxt = xtb
            hb, Bhb = hbs.next()
            ss, Bss = sss.next()
            self.rms_mod_tile(T, xt, Bxt, hb, Bhb, 1, 0, (junk, Bjunk, ss, Bss, hn, Bhn))
            hT, BhT = hTs.next()
            pt, Bpt = pts.next()
            self.transpose8(hb, Bhb, pt, Bpt, hT[:], BhT, eng="ACT")
            return hT, BhT

        xq = [load_x(0)]
        if NT > 1:
            xq.append(load_x(1))
        hq = [prep(0, xq.pop(0))]
        for t in range(NT):
            hT, BhT = hq.pop(0)
            if t + 2 < NT:
                xq.append(load_x(t + 2))
            tt = t % 4
            half = (t // 4) % 2
            for gi_, g in enumerate(groups):
                w = g["w"]
                pg, Bpg = pgl[gi_]
                for kc in range(8):
                    kb.op("PE", lambda e, kc=kc, pg=pg, w=w, g=g: e.matmul(pg[:, 0:w], hT[:, kc, :], W[:, kc, g["c0"]:g["c0"] + w],
                                                                          start=(kc == 0), stop=(kc == 7)),
                          reads=[BhT, BW], writes=[Bpg])
            if t + 1 < NT:
                hq.append(prep(t + 1, xq.pop(0)))
            info = []
            for gi_, g in enumerate(groups):
                w = g["w"]
                wq = (w // 64) * 64
                info.append((gi_, g, w, wq, wq // 64))
            for gi_, g, w, wq, nu in info:
                pg, Bpg = pgl[gi_]
                post, Bpost = postl[gi_]
                sq, Bsq = sqs[gi_]
                if g["qk"]:
                    kb.op("ACT", lambda e, sq=sq, pg=pg, wq=wq: e.activation(sq[:, 0:wq], pg[:, 0:wq], AF.Square), reads=[Bpg], writes=[Bsq])
                else:
                    kb.op("ACT", lambda e, post=post, pg=pg, wq=wq: e.copy(post[:, 0:wq], pg[:, 0:wq]), reads=[Bpg], writes=[Bpost])
                if g["gate"] is not None:
                    gc0, gw = g["gate"]
                    kb.op("ACT", lambda e, pg=pg, gc0=gc0, gw=gw: e.activation(self.gate_sb[:, t, :], pg[:, gc0:gc0 + gw], AF.Sigmoid),
                          reads=[Bpg], writes=[self.Bgate])
            for gi_, g, w, wq, nu in info:
                if not g["qk"]:
                    continue
                sq, Bsq = sqs[gi_]
                ss8, Bss8 = ss8l[gi_]
                kb.op("DVE", lambda e, ss8=ss8, sq=sq, wq=wq, nu=nu: e.tensor_reduce(ss8[:, 0:nu], sq[:, 0:wq].rearrange("p (u d) -> p u d", d=64), AX.X, ALU.add),
                      reads=[Bsq], writes=[Bss8])
            for gi_, g, w, wq, nu in info:
                if not g["qk"]:
                    continue
                ss8, Bss8 = ss8l[gi_]
                kb.op("ACT", lambda e, ss8=ss8, nu=nu: e.activation(ss8[:, 0:nu], ss8[:, 0:nu], AF.Sqrt, bias=self.epsb[:], scale=1.0 / 64),
                      reads=[Bss8, self.Bepsb], writes=[Bss8])
            for gi_, g, w, wq, nu in info:
                if not g["qk"]:
                    continue
                pg, Bpg = pgl[gi_]
                ss8, Bss8 = ss8l[gi_]
                qn, Bqn = qnl[gi_]
                gr, Bgr = gains[g["gain"]]
                kb.op("DVE", lambda e, ss8=ss8, nu=nu: e.reciprocal(ss8[:, 0:nu], ss8[:, 0:nu]), reads=[Bss8], writes=[Bss8])
                qk_units = set()
                for (u0, u1) in g["qk"]:
                    qk_units.update(range(u0, u1))
                for u in [u for u in range(nu) if u not in qk_units]:
                    kb.op("DVE", lambda e, u=u, ss8=ss8: e.memset(ss8[:, u:u + 1], 1.0), writes=[Bss8])
                kb.op("DVE", lambda e, qn=qn, pg=pg, ss8=ss8, nu=nu, wq=wq: e.tensor_tensor(
                    qn[:, 0:nu, :], pg[:, 0:wq].rearrange("p (u d) -> p u d", d=64),
                    ss8[:, 0:nu].unsqueeze(2).to_broadcast([128, nu, 64]), ALU.mult),
                      reads=[Bpg, Bss8], writes=[Bqn])
                kb.op("DVE", lambda e, qn=qn, gr=gr, nu=nu: e.tensor_tensor(qn[:, 0:nu, :], qn[:, 0:nu, :], gr[:, 0:nu, :], ALU.mult),
                      reads=[Bqn, Bgr], writes=[Bqn])
            for gi_, g, w, wq, nu in info:
                if not g["qk"]:
                    continue
                qn, Bqn = qnl[gi_]
                post, Bpost = postl[gi_]
                kb.op("ACT", lambda e, post=post, qn=qn, wq=wq, nu=nu: e.copy(post[:, 0:wq], qn[:, 0:nu, :].rearrange("p u d -> p (u d)")),
                      reads=[Bqn], writes=[Bpost])
            for gi_, g, w, wq, nu in info:
                if not g["qk"]:
                    continue
                qn, Bqn = qnl[gi_]
                post, Bpost = postl[gi_]
                rt, Brt = rtl[gi_]
                postv = post[:, 0:wq].rearrange("p (u d) -> p u d", d=64)
                for (u0, u1) in g["qk"]:
                    n_ = u1 - u0
                    cosb = self.cosT[:, t, :].unsqueeze(1).to_broadcast([128, n_, 8])
                    sinb = self.sinT[:, t, :].unsqueeze(1).to_broadcast([128, n_, 8])
                    t1 = qn[:, u0:u1, 0:8]
                    t2 = qn[:, u0:u1, 8:16]
                    kb.op("DVE", lambda e, rt=rt, n_=n_, t1=t1, cosb=cosb: e.tensor_tensor(rt[:, 0, 0:n_, :], t1, cosb, ALU.mult), reads=[Bqn, self.Bcos], writes=[Brt])
                    kb.op("DVE", lambda e, rt=rt, n_=n_, t2=t2, sinb=sinb: e.tensor_tensor(rt[:, 1, 0:n_, :], t2, sinb, ALU.mult), reads=[Bqn, self.Bsin], writes=[Brt])
                    kb.op("DVE", lambda e, rt=rt, n_=n_, t2=t2, cosb=cosb: e.tensor_tensor(rt[:, 2, 0:n_, :], t2, cosb, ALU.mult), reads=[Bqn, self.Bcos], writes=[Brt])
                    kb.op("DVE", lambda e, rt=rt, n_=n_, t1=t1, sinb=sinb: e.tensor_tensor(rt[:, 3, 0:n_, :], t1, sinb, ALU.mult), reads=[Bqn, self.Bsin], writes=[Brt])
                    kb.op("DVE", lambda e, rt=rt, n_=n_, postv=postv, u0=u0, u1=u1: e.tensor_tensor(postv[:, u0:u1, 0:8], rt[:, 0, 0:n_, :], rt[:, 1, 0:n_, :], ALU.subtract),
                          reads=[Brt], writes=[Bpost])
                    kb.op("DVE", lambda e, rt=rt, n_=n_, postv=postv, u0=u0, u1=u1: e.tensor_tensor(postv[:, u0:u1, 8:16], rt[:, 2, 0:n_, :], rt[:, 3, 0:n_, :], ALU.add),
                          reads=[Brt], writes=[Bpost])
            for gi_, g, w, wq, nu in info:
                post, Bpost = postl[gi_]
                tblocks = [(bi, dst) for bi, (k, dst) in enumerate(g["blocks"]) if k == "T"]
                if tblocks:
                    pb, Bpb = pbh.next()
                    st, Bst = stages[gi_][half]
                    for bi, dst in tblocks:
                        kb.op("PE", lambda e, bi=bi, pb=pb, post=post: e.transpose(pb[:, bi * 128:(bi + 1) * 128], post[:, bi * 128:(bi + 1) * 128], self.ident[:]),
                              reads=[Bpost, self.Bident], writes=[Bpb])
                    b0, b1 = tblocks[0][0], tblocks[-1][0] + 1
                    kb.op("ACT", lambda e, st=st, pb=pb, b0=b0, b1=b1: e.copy(st[:, b0:b1, tt * 128:(tt + 1) * 128],
                                                                              pb[:, b0 * 128:b1 * 128].rearrange("p (b t) -> p b t", t=128)),
                          reads=[Bpb], writes=[Bst])
                    if tt == 3:
                        t0 = (t - 3) * 128
                        for bi, dst in tblocks:
                            kb.dma("POOL", self.FT[dst:dst + 128, t0:t0 + 512], st[:, bi, :], reads=[Bst])
                for bi, (k, dst) in enumerate(g["blocks"]):
                    if k == "V":
                        kb.dma("POOL", self.TM[t * 128:(t + 1) * 128, dst:dst + 128], post[:, bi * 128:(bi + 1) * 128], reads=[Bpost])
        T.close()

    def attn_qtile(self, R, q_rhs, Bq, k_lhsT, Bk, v_rhs, Bv, vw, pairs, O, BO, o_off, post_exp=None, bank_of=lambda s: 0, vstat=False, hooks=None):
        kb = self.kb
        LOOK = self.cfg.get("look", 5)
        last = {}
        started = set()
        for kc, subs in pairs:
            for s, kind in subs:
                last[s] = kc
        live = {}

        def stage1(i):
            kc, subs = pairs[i]
            pS, BpS = R["pS"].next()
            kl, krows = k_lhsT(kc)
            kb.op("PE", lambda e: e.matmul(pS[0:krows, :], kl, q_rhs, start=True, stop=True), reads=[Bk, Bq], writes=[BpS])
            pT, BpT = R["pT"].next()
            kb.op("ACT", lambda e: e.activation(pT[0:krows, :], pS[0:krows, :], AF.Exp, scale=0.125), reads=[BpS], writes=[BpT])
            if post_exp is not None:
                post_exp(kc, pT, BpT, krows)
            for s, kind in subs:
                if kind == "tri":
                    kb.op("DVE", lambda e, s=s: e.tensor_tensor(pT[:, s * 128:(s + 1) * 128], pT[:, s * 128:(s + 1) * 128], self.tri[:], ALU.mult),
                          reads=[BpT, self.Btri], writes=[BpT])
                elif kind == "atri":
                    kb.op("DVE", lambda e, s=s: e.tensor_tensor(pT[:, s * 128:(s + 1) * 128], pT[:, s * 128:(s + 1) * 128], self.atri[:], ALU.mult),
                          reads=[BpT, self.Batri], writes=[BpT])
            live[i] = (pT, BpT, krows)

        def stage2(i):
            kc, subs = pairs[i]
            pT, BpT, krows = live.pop(i)
            if vstat:
                c0 = min(s_ for s_, _ in subs) * 128
                c1 = (max(s_ for s_, _ in subs) + 1) * 128
                st_flag = 0 not in started
                started.add(0)
                kb.op("PE", lambda e: e.matmul(O[0:vw, c0:c1], v_rhs(kc)[0:krows, 0:vw], pT[0:krows, c0:c1],
                                               start=st_flag, stop=(i == len(pairs) - 1), skip_group_check=True),
                      reads=[BpT, Bv], writes=[BO])
                return
            for s, kind in subs:
                bk = bank_of(s)
                st_flag = bk not in started
                started.add(bk)
                kb.op("PE", lambda e, s=s, st_flag=st_flag: e.matmul(o_off(O, s), pT[0:krows, s * 128:(s + 1) * 128], v_rhs(kc)[0:krows, :],
                                                                     start=st_flag, stop=(last[s] == kc), skip_group_check=True),
                      reads=[BpT, Bv], writes=[BO])

        n = len(pairs)
        for i in range(n + LOOK):
            if i < n:
                stage1(i)
            if hooks and i in hooks:
                hooks.pop(i)()
            if i - LOOK >= 0:
                stage2(i - LOOK)
        if hooks:
            for k in sorted(hooks):
                hooks.pop(k)()

    def ot_to_tok(self, OT, BOT, vw, osb, Bosb, Otok, BOtok):
        kb = self.kb
        kb.op("ACT", lambda e: e.copy(osb[0:vw, :], OT[0:vw, :]), reads=[BOT], writes=[Bosb])
        for s in range(4):
            kb.op("PE", lambda e, s=s: e.transpose(Otok[:, s * 128:s * 128 + vw], osb[0:vw, s * 128:(s + 1) * 128], self.identF[0:vw, 0:vw]),
                  reads=[Bosb, self.BidentF], writes=[BOtok])

    @staticmethod
    def causal_pairs(qt):
        pairs = []
        for kc in range(4 * qt + 4):
            j = kc - 4 * qt
            if j < 0:
                pairs.append((kc, [(s, "full") for s in range(4)]))
            else:
                pairs.append((kc, [(s, "tri" if s == j else "full") for s in range(j, 4)]))
        return pairs

    def phase_b_diff(self, li_odd_index):
        kb, I = self.kb, self.I
        T = Scope(kb)
        lam_init = 0.8 - 0.6 * float(np.exp(-0.3 * 1))
        lp, Blp = T.sb("lp", [128, 4, 64], F32)
        kb.dma("SP", lp[:].rearrange("p a d -> p (a d)"), bc_row(I["diff_lambda"][0:1, :], 256), writes=[Blp])
        l2, Bl2 = T.sb("l2", [128, 2, 64], F32)
        kb.op("DVE", lambda e: e.tensor_tensor(l2[:, 0, :], lp[:, 0, :], lp[:, 1, :], ALU.mult), reads=[Blp], writes=[Bl2])
        kb.op("DVE", lambda e: e.tensor_tensor(l2[:, 1, :], lp[:, 2, :], lp[:, 3, :], ALU.mult), reads=[Blp], writes=[Bl2])
        ls, Bls = T.sb("ls", [128, 2], F32)
        kb.op("DVE", lambda e: e.tensor_reduce(ls[:], l2[:], AX.X, ALU.add), reads=[Bl2], writes=[Bls])
        kb.op("ACT", lambda e: e.activation(ls[:], ls[:], AF.Exp), reads=[Bls], writes=[Bls])
        nlam, Bnlam = T.sb("nlam", [128, 1], F32)
        kb.op("DVE", lambda e: e.tensor_tensor(nlam[:], ls[:, 1:2], ls[:, 0:1], ALU.subtract), reads=[Bls], writes=[Bnlam])
        kb.op("DVE", lambda e: e.tensor_scalar(nlam[:], nlam[:], -lam_init, None, ALU.add), reads=[Bnlam], writes=[Bnlam])
        go, Bgo = T.sb("go", [128, 128], F32)
        kb.dma("SP", go[:], bc_row(I["diff_out_norm"][0:1, :], 128), writes=[Bgo])
        kb.op("DVE", lambda e: e.tensor_scalar(go[:], go[:], 1.0 - lam_init, None, ALU.mult), reads=[Bgo], writes=[Bgo])

        KTs = Rot([T.sb("KT", [128, S], BF16) for _ in range(2)])
        QTs = Rot([T.sb("QT", [128, S], BF16) for _ in range(2)])
        Vs = Rot([T.sb("V", [128, NT, 129], BF16) for _ in range(2)])
        for (v, Bv) in Vs.items:
            kb.op("POOL", lambda e, v=v: e.memset(v[:, :, 128:129], 1.0), writes=[Bv])
        R = dict(pS=Rot([T.ps("pS", [128, 512], F32) for _ in range(3)]),
                 pT=Rot([T.sb("pT", [128, 512], BF16) for _ in range(8)]))
        Os = [[T.ps("O", [128, 512], F32) for _ in range(2)] for _ in range(2)]
        osts = Rot([T.sb("ost", [128, 4, 128], BF16) for _ in range(2)])
        osbs = Rot([[[T.sb("osbd", [128, 512], F32) for _ in range(2)] for _ in range(2)] for _ in range(2)])
        pending = []
        a1ts = Rot([T.sb("a1t", [128, 4, 128], F32) for _ in range(2)])
        rdts = Rot([T.sb("rdt", [128, 4, 4], F32) for _ in range(2)])
        a0s = Rot([T.sb("a0", [128, 128], F32) for _ in range(2)])
        a1s = Rot([T.sb("a1", [128, 128], F32) for _ in range(2)])
        rds = Rot([T.sb("rd", [128, 4], F32) for _ in range(4)])
        junk, Bjunk = T.sb("junkb", [128, 128], BF16)

        def load_head(h):
            KT, BKT = KTs.next()
            QT, BQT = QTs.next()
            V, BV = Vs.next()
            kb.dma("SP", QT[:], self.FT[h * 128:(h + 1) * 128, :], writes=[BQT])
            kb.dma("SP", KT[:], self.FT[1024 + h * 128:1024 + (h + 1) * 128, :], writes=[BKT])
            tmv = self.TM[:, h * 128:(h + 1) * 128].rearrange("(c p) d -> p c d", p=128)
            for c4 in range(4):
                kb.dma("SP", V[:, c4 * 8:(c4 + 1) * 8, 0:128], tmv[:, c4 * 8:(c4 + 1) * 8, :], writes=[BV])
            return (KT, BKT, QT, BQT, V, BV)

        nxt = load_head(0)
        for h in range(8):
            KT, BKT, QT, BQT, V, BV = nxt
            if h + 1 < 8:
                nxt = load_head(h + 1)
            for qt in range(8):
                pairs = self.causal_pairs(qt)
                for c in range(2):
                    def o_off(O, s, c=c):
                        return Os[c][s // 2][0][:, (s % 2) * 256:(s % 2) * 256 + 129]
                    hooks = None
                    if c == 0 and pending:
                        e2_, e3_ = pending.pop(0)
                        hooks = {min(6, len(pairs) - 1): e2_, 10 ** 6: e3_}
                    self.attn_qtile(R, QT[64 * c:64 * c + 64, qt * 512:(qt + 1) * 512], BQT,
                                    lambda kc, c=c: (KT[64 * c:64 * c + 64, kc * 128:(kc + 1) * 128], 128), BKT,
                                    lambda kc: V[:, kc, :], BV, 129, pairs, None, Os[c][0][1], o_off, bank_of=lambda s: s // 2, hooks=hooks)
                oset = osbs.next()
                for c in range(2):
                    for b_ in range(2):
                        kb.op("DVE", lambda e, c=c, b_=b_: e.tensor_copy(oset[c][b_][0][:, 0:385], Os[c][b_][0][:, 0:385]),
                              reads=[Os[c][0][1]], writes=[oset[c][b_][1]])

                ep = dict(oset=oset, h=h, qt=qt)
                a1t, Ba1t = a1ts.next()
                rdt, Brdt = rdts.next()
                ep.update(a1t=a1t, Ba1t=Ba1t, rdt=rdt, Brdt=Brdt)

                def e1(ep=ep):
                    oset, a1t, Ba1t, rdt, Brdt = ep["oset"], ep["a1t"], ep["Ba1t"], ep["rdt"], ep["Brdt"]
                    for s in range(4):
                        O0 = oset[0][s // 2][0][:, (s % 2) * 256:(s % 2) * 256 + 129]
                        O1 = oset[1][s // 2][0][:, (s % 2) * 256:(s % 2) * 256 + 129]
                        BO0, BO1 = oset[0][s // 2][1], oset[1][s // 2][1]
                        a0, Ba0 = a0s.next()
                        kb.op("DVE", lambda e, s=s, O0=O0: e.reciprocal(rdt[:, s, 0:1], O0[:, 128:129]), reads=[BO0], writes=[Brdt])
                        kb.op("DVE", lambda e, s=s, O1=O1: e.reciprocal(rdt[:, s, 1:2], O1[:, 128:129]), reads=[BO1], writes=[Brdt])
                        kb.op("DVE", lambda e, s=s: e.tensor_tensor(rdt[:, s, 1:2], rdt[:, s, 1:2], nlam[:], ALU.mult), reads=[Brdt, Bnlam], writes=[Brdt])
                        kb.op("DVE", lambda e, s=s, a0=a0, O0=O0: e.tensor_scalar(a0[:], O0[:, 0:128], rdt[:, s, 0:1], None, ALU.mult), reads=[BO0, Brdt], writes=[Ba0])
                        kb.op("DVE", lambda e, s=s, a0=a0, O1=O1: e.scalar_tensor_tensor(a1t[:, s, :], O1[:, 0:128], rdt[:, s, 1:2], a0[:], ALU.mult, ALU.add),
                              reads=[BO1, Brdt, Ba0], writes=[Ba1t])

                def e2(ep=ep):
                    a1t, Ba1t, rdt, Brdt = ep["a1t"], ep["Ba1t"], ep["rdt"], ep["Brdt"]
                    for s in range(4):
                        kb.op("ACT", lambda e, s=s: e.activation(junk[:], a1t[:, s, :], AF.Square, accum_out=rdt[:, s, 2:3]), reads=[Ba1t], writes=[Bjunk, Brdt])
                    kb.op("ACT", lambda e: e.activation(rdt[:, :, 2:3], rdt[:, :, 2:3], AF.Sqrt, bias=self.epsb[:], scale=1.0 / 128),
                          reads=[Brdt, self.Bepsb], writes=[Brdt])

                def e3(ep=ep):
                    a1t, Ba1t, rdt, Brdt, h, qt = ep["a1t"], ep["Ba1t"], ep["rdt"], ep["Brdt"], ep["h"], ep["qt"]
                    ost, Bost = osts.next()
                    kb.op("DVE", lambda e: e.reciprocal(rdt[:, :, 3:4], rdt[:, :, 2:3]), reads=[Brdt], writes=[Brdt])
                    for s in range(4):
                        kb.op("DVE", lambda e, s=s, ost=ost: e.scalar_tensor_tensor(ost[:, s, :], a1t[:, s, :], rdt[:, s, 3:4], go[:], ALU.mult, ALU.mult),
                              reads=[Ba1t, Brdt, Bgo], writes=[Bost])
                    osv = self.OS[qt * 512:(qt + 1) * 512, h * 128:(h + 1) * 128].rearrange("(s p) d -> p s d", p=128)
                    kb.dma("POOL", osv, ost[:], reads=[Bost])
                e1()
                pending.append((e2, e3))
        while pending:
            e2_, e3_ = pending.pop(0)
            e2_()
            e3_()
        T.close()

    def phase_c1(self, li, x_src, w_out_ap):
        kb, I = self.kb, self.I
        T = Scope(kb)
        W, BW = T.sb("w_out", [128, 8, D], BF16)
        self.load_w_bf16(W, BW, w_out_ap, 8)
        ots = Rot([T.sb("ot", [128, D], BF16) for _ in range(2)])
        xts = Rot([T.sb("xt", [128, D], F32) for _ in range(2)])
        oTs = Rot([T.sb("oT", [128, 8, 128], BF16) for _ in range(2)])
        x1s = Rot([T.sb("x1", [128, D], F32) for _ in range(2)])
        hbs = Rot([T.sb("hb", [128, D], BF16) for _ in range(2)])
        ytmp, Bytmp = T.sb("ytmp", [128, D], F32)
        junk, Bjunk = T.sb("junk", [128, D], BF16)
        sss = Rot([T.sb("ss", [128, 1], F32) for _ in range(2)])
        hn, Bhn = T.sb("hn", [128, D], F32)
        stg = [T.sb("stage", [128, 8, 512], BF16) for _ in range(2)]
        pts = Rot([T.ps("ptA", [128, 1024], BF16) for _ in range(2)])
        pys = Rot([T.ps("py", [128, 512], F32) for _ in range(4)])

        def load(t):
            ot, Bot = ots.next()
            xt, Bxt = xts.next()
            kb.dma("SP", ot[:], self.OS[t * 128:(t + 1) * 128, :], writes=[Bot])
            kb.dma("SP", xt[:], x_src[t * 128:(t + 1) * 128, :], writes=[Bxt])
            return ot, Bot, xt, Bxt

        nxt = load(0)
        for t in range(NT):
            ot, Bot, xt, Bxt = nxt
            if t + 1 < NT:
                nxt = load(t + 1)
            oT, BoT = oTs.next()
            pt, Bpt = pts.next()
            self.transpose8(ot, Bot, pt, Bpt, oT[:], BoT, eng="ACT")
            x1, Bx1 = x1s.next()
            for g in range(2):
                py, Bpy = pys.next()
                for kc in range(8):
                    kb.op("PE", lambda e, kc=kc: e.matmul(py[:], oT[:, kc, :], W[:, kc, g * 512:(g + 1) * 512], start=(kc == 0), stop=(kc == 7)),
                          reads=[BoT, BW], writes=[Bpy])
                sl = slice(g * 512, (g + 1) * 512)
                kb.op("DVE", lambda e: e.tensor_tensor(ytmp[:, sl], py[:], self.mod[:, 2, sl], ALU.mult), reads=[Bpy, self.Bmod], writes=[Bytmp])
                kb.op("POOL", lambda e: e.tensor_tensor(x1[:, sl], ytmp[:, sl], xt[:, sl], ALU.add), reads=[Bytmp, Bxt], writes=[Bx1])
            kb.dma("POOL", self.x1s[t * 128:(t + 1) * 128, :], x1[:], reads=[Bx1])
            hb, Bhb = hbs.next()
            ss, Bss = sss.next()
            self.rms_mod_tile(T, x1, Bx1, hb, Bhb, 4, 3, (junk, Bjunk, ss, Bss, hn, Bhn))
            pt, Bpt = pts.next()
            tt = t % 4
            st, Bst = stg[(t // 4) % 2]
            self.transpose8(hb, Bhb, pt, Bpt, st[:, :, tt * 128:(tt + 1) * 128], Bst, eng="ACT")
            if tt == 3:
                t0 = (t - 3) * 128
                h2v = self.H2T.rearrange("(j p) t -> p j t", p=128)
                kb.dma("POOL", h2v[:, :, t0:t0 + 512], st[:], reads=[Bst])
        T.close()

    def phase_c2(self, li, x_dst):
        kb, I = self.kb, self.I
        T = Scope(kb)
        Wg, BWg = T.sb("wg", [128, 8, FFN], BF16)
        Wu, BWu = T.sb("wu", [128, 8, FFN], BF16)
        Wd, BWd = T.sb("wd", [128, NJ, D], BF16)
        self.load_w_bf16(Wg, BWg, I["ffn_w_gate"][li], 8)
        self.load_w_bf16(Wu, BWu, I["ffn_w_up"][li], 8)
        self.load_w_bf16(Wd, BWd, I["ffn_w_down"][li], NJ)
        TT = 256
        h2s = Rot([T.sb("h2T", [128, 8, TT], BF16) for _ in range(2)])
        act, Bact = T.sb("act", [128, NJ, TT], BF16)
        sgs = Rot([T.sb("sg", [128, TT], F32) for _ in range(2)])
        xts = Rot([T.sb("x1t", [128, D], F32) for _ in range(2)])
        ytmp, Bytmp = T.sb("ytmp", [128, 512], F32)
        pgs = Rot([T.ps("pg", [128, 512], F32) for _ in range(2)])
        pus = Rot([T.ps("pu", [128, 512], F32) for _ in range(2)])
        pys = Rot([T.ps("py", [128, 512], F32) for _ in range(3)])
        h2v = self.H2T.rearrange("(j p) t -> p j t", p=128)

        def load(st):
            h2, Bh2 = h2s.next()
            kb.dma("SP", h2[:], h2v[:, :, st * TT:(st + 1) * TT], writes=[Bh2])
            return h2, Bh2

        nxt = load(0)
        for st in range(S // TT):
            h2, Bh2 = nxt
            if st + 1 < S // TT:
                nxt = load(st + 1)
            for j in range(NJ):
                pg, Bpg = pgs.next()
                pu, Bpu = pus.next()
                for kc in range(8):
                    kb.op("PE", lambda e, kc=kc: e.matmul(pg[:, 0:TT], Wg[:, kc, j * 128:(j + 1) * 128], h2[:, kc, :], start=(kc == 0), stop=(kc == 7)),
                          reads=[BWg, Bh2], writes=[Bpg])
                for kc in range(8):
                    kb.op("PE", lambda e, kc=kc: e.matmul(pu[:, 0:TT], Wu[:, kc, j * 128:(j + 1) * 128], h2[:, kc, :], start=(kc == 0), stop=(kc == 7)),
                          reads=[BWu, Bh2], writes=[Bpu])
                sg, Bsg = sgs.next()
                kb.op("ACT", lambda e: e.activation(sg[:], pg[:, 0:TT], AF.Silu), reads=[Bpg], writes=[Bsg])
                kb.op("DVE", lambda e, j=j: e.tensor_tensor(act[:, j, :], sg[:], pu[:, 0:TT], ALU.mult), reads=[Bsg, Bpu], writes=[Bact])
            for q in range(TT // 128):
                t = st * (TT // 128) + q
                xt, Bxt = xts.next()
                kb.dma("SP", xt[:], self.x1s[t * 128:(t + 1) * 128, :], writes=[Bxt])
                for g in range(2):
                    py, Bpy = pys.next()
                    for j in range(NJ):
                        kb.op("PE", lambda e, j=j: e.matmul(py[:], act[:, j, q * 128:(q + 1) * 128], Wd[:, j, g * 512:(g + 1) * 512],
                                                            start=(j == 0), stop=(j == NJ - 1)),
                              reads=[Bact, BWd], writes=[Bpy])
                    sl = slice(g * 512, (g + 1) * 512)
                    kb.op("DVE", lambda e: e.tensor_tensor(ytmp[:], py[:], self.mod[:, 5, sl], ALU.mult), reads=[Bpy, self.Bmod], writes=[Bytmp])
                    kb.op("POOL", lambda e: e.tensor_tensor(xt[:, sl], ytmp[:], xt[:, sl], ALU.add), reads=[Bytmp, Bxt], writes=[Bxt])
                kb.dma("POOL", x_dst[t * 128:(t + 1) * 128, :], xt[:], reads=[Bxt])
        T.close()

    def build(self):
        kb = self.kb
        self.declare()
        self.setup()
        G = self.G
        self.epsb, self.Bepsb = G.sb("epsb", [128, 1], F32)
        kb.op("POOL", lambda e: e.memset(self.epsb[:], EPS), writes=[self.Bepsb])
        layers = self.cfg.get("layers", [0, 1])
        x_src = self.I["x"]
        for n, li in enumerate(layers):
            x_dst = self.out if n == len(layers) - 1 else self.xmid
            self.L = Scope(kb)
            if li == 0:
                self.gate_sb, self.Bgate = self.L.sb("gates", [128, NT, 24], F32)
            self.compute_mod(li)
            stop = self.cfg.get("stop")
            if stop == "mod":
                self.L.close()
                break
            if li == 0:
                self.phase_a(0, x_src)
                if stop == "A":
                    self.L.close()
                    break
                if not self.cfg.get("skip_moba"):
                    self.moba_part()
                if stop == "moba":
                    self.L.close()
                    break
                self.nsa_part()
                if stop in ("nsa", "cmpmlp"):
                    self.L.close()
                    break
                self.phase_c1(0, x_src, self.I["sp_w_out"])
            else:
                self.phase_a(1, x_src)
                if stop == "A":
                    self.L.close()
                    break
                self.phase_b_diff(0)
                if stop == "B":
                    self.L.close()
                    break
                self.phase_c1(1, x_src, self.I["diff_w_out"])
            if stop == "C1":
                self.L.close()
                break
            self.phase_c2(li, x_dst)
            self.L.close()
            x_src = x_dst
        kb.barrier(engines=("POOL",))
        G.es.close()
        kb.es.close()
        return self.nc

    def epi_norm(self, T, O_ap, BO, vcol, rd_ap, Brd):
        kb = self.kb
        kb.op("DVE", lambda e: e.tensor_scalar(rd_ap, O_ap[:, vcol:vcol + 1], 1e-30, None, ALU.max), reads=[BO], writes=[Brd])
        kb.op("DVE", lambda e: e.reciprocal(rd_ap, rd_ap), reads=[Brd], writes=[Brd])

    def phase_b_sparse(self):
        self.moba_part()
        self.nsa_part()

    def moba_part(self):
        kb, I = self.kb, self.I
        T = Scope(kb)
        QAs = Rot([T.sb("QA", [128, 2, S], BF16) for _ in range(2)])
        KAs = Rot([T.sb("KA", [128, 2, S], BF16) for _ in range(2)])
        Vps = Rot([T.sb("Vp", [128, NT, 2, 65], BF16) for _ in range(2)])
        for (ka, Bka) in KAs.items:
            for hh in range(2):
                kb.dma("POOL", ka[64:80, hh, :], I["c_e16"][:, :], writes=[Bka])
        for (v, Bv) in Vps.items:
            kb.op("POOL", lambda e, v=v: e.memset(v[:, :, :, 64:65], 1.0), writes=[Bv])
        t1, Bt1 = T.sb("t1", [128, 16, 16], F32)
        t2, Bt2 = T.sb("t2", [128, 16, 16], F32)
        kb.dma("SP", t1[:].rearrange("p a b -> p (a b)"), bc_row(I["c_t1"][0:1, :], 256), writes=[Bt1])
        kb.dma("SP", t2[:].rearrange("p a b -> p (a b)"), bc_row(I["c_t2"][0:1, :], 256), writes=[Bt2])
        augs = Rot([T.sb("aug", [128, 128], BF16) for _ in range(2)])
        for (a, Ba) in augs.items:
            kb.op("POOL", lambda e, a=a: e.memset(a[:], 0.0), writes=[Ba])
        km, Bkm = T.sb("km", [64, 16], F32)
        kmb, Bkmb = T.sb("kmb", [64, 16], BF16)
        gms = Rot([T.sb("gm", [128, 16], F32) for _ in range(2)])
        m8s = Rot([T.sb("m8", [128, 8], F32) for _ in range(2)])
        sels = Rot([T.sb("sel", [128, 16], F32) for _ in range(2)])
        R = dict(pS=Rot([T.ps("pS", [128, 512], F32) for _ in range(3)]),
                 pT=Rot([T.sb("pT", [128, 512], BF16) for _ in range(8)]))
        Obs = Rot([T.ps("OTm", [128, 512], F32) for _ in range(2)])
        Otk, BOtk = T.ps("Otok", [128, 512], F32)
        osbs = Rot([T.sb("osb", [128, 512], F32) for _ in range(2)])
        pgs = Rot([T.ps("pgate", [128, 512], F32) for _ in range(1)])
        ptr = Rot([T.ps("ptr", [128, 1024], BF16) for _ in range(1)])
        osts = Rot([T.sb("ost", [128, 4, 128], BF16) for _ in range(2)])
        rds = Rot([T.sb("rd", [128, 1], F32) for _ in range(4)])

        def load_pair(hp):
            QA, BQA = QAs.next()
            KA, BKA = KAs.next()
            Vp, BVp = Vps.next()
            for hh in range(2):
                h = 2 * hp + hh
                kb.dma("SP", QA[0:64, hh, :], self.FT[h * 64:(h + 1) * 64, :], writes=[BQA])
                kb.dma("SP", KA[0:64, hh, :], self.FT[512 + h * 64:512 + (h + 1) * 64, :], writes=[BKA])
            for hh in range(2):
                h = 2 * hp + hh
                tmv = self.TM[:, h * 64:(h + 1) * 64].rearrange("(c p) d -> p c d", p=128)
                for c4 in range(4):
                    kb.dma("SP", Vp[:, c4 * 8:(c4 + 1) * 8, hh, 0:64], tmv[:, c4 * 8:(c4 + 1) * 8, :], writes=[BVp])
            return QA, BQA, KA, BKA, Vp, BVp

        nxt = load_pair(0)
        pending = []
        for hp in range(4):
            QA, BQA, KA, BKA, Vp, BVp = nxt
            if hp + 1 < 4:
                nxt = load_pair(hp + 1)
            for hh in range(2):
                kb.op("DVE", lambda e: e.tensor_reduce(km[:], KA[0:64, hh, :].rearrange("p (b j) -> p b j", j=256), AX.X, ALU.add),
                      reads=[BKA], writes=[Bkm])
                kb.op("DVE", lambda e: e.tensor_scalar(kmb[:], km[:], 1.0 / 256, None, ALU.mult), reads=[Bkm], writes=[Bkmb])
                for t in range(NT):
                    own = t // 2
                    pg, Bpg = pgs.next()
                    kb.op("PE", lambda e: e.matmul(pg[:, 0:16], QA[0:64, hh, t * 128:(t + 1) * 128], kmb[:], start=True, stop=True),
                          reads=[BQA, Bkmb], writes=[Bpg])
                    gm, Bgm = gms.next()
                    m8, Bm8 = m8s.next()
                    sel, Bsel = sels.next()
                    aug, Baug = augs.next()
                    kb.op("DVE", lambda e: e.tensor_tensor(gm[:], pg[:, 0:16], t1[:, own, :], ALU.add), reads=[Bpg, Bt1], writes=[Bgm])
                    kb.op("DVE", lambda e: e.max(m8[:], gm[:]), reads=[Bgm], writes=[Bm8])
                    kb.op("DVE", lambda e: e.tensor_scalar(m8[:, 2:3], m8[:, 2:3], -1e29, None, ALU.max), reads=[Bm8], writes=[Bm8])
                    kb.op("DVE", lambda e: e.tensor_scalar(sel[:], gm[:], m8[:, 2:3], None, ALU.is_ge), reads=[Bgm, Bm8], writes=[Bsel])
                    kb.op("DVE", lambda e: e.tensor_tensor(sel[:], sel[:], t2[:, own, :], ALU.max), reads=[Bsel, Bt2], writes=[Bsel])
                    kb.op("DVE", lambda e: e.tensor_scalar(aug[:, 64:80], sel[:], -1.0, -MASKV, ALU.add, ALU.mult), reads=[Bsel], writes=[Baug])
                    pt, Bpt = ptr.next()
                    kb.op("PE", lambda e: e.transpose(pt[:, 0:128], aug[:], self.ident[:]), reads=[Baug, self.Bident], writes=[Bpt])
                    kb.op("ACT", lambda e: e.copy(QA[64:80, hh, t * 128:(t + 1) * 128], pt[64:80, 0:128]), reads=[Bpt], writes=[BQA])
            for qt in range(8):
                pairs = self.causal_pairs(qt)
                ost, Bost = osts.next()
                for hh in range(2):
                    OTm, BOTm = Obs.next()
                    self.attn_qtile(R, QA[0:80, hh, qt * 512:(qt + 1) * 512], BQA,
                                    lambda kc: (KA[0:80, hh, kc * 128:(kc + 1) * 128], 128), BKA,
                                    lambda kc: Vp[:, kc, hh, :], BVp, 65, pairs, OTm, BOTm, None, vstat=True)
                    def epilogue(OTm=OTm, BOTm=BOTm, ost=ost, Bost=Bost, hh=hh, qt=qt, hp=hp):
                        osb, Bosb = osbs.next()
                        self.ot_to_tok(OTm, BOTm, 65, osb, Bosb, Otk, BOtk)
                        Om, BOm = Otk, BOtk
                        for s in range(4):
                            rd, Brd = rds.next()
                            Os_ = Om[:, s * 128:s * 128 + 65]
                            self.epi_norm(T, Os_, BOm, 64, rd[:], Brd)
                            kb.op("DVE", lambda e, s=s, Os_=Os_, rd=rd: e.tensor_scalar(ost[:, s, hh * 64:(hh + 1) * 64], Os_[:, 0:64], rd[:, 0:1], None, ALU.mult),
                                  reads=[BOm, Brd], writes=[Bost])
                        if hh == 1:
                            osv = self.OS[qt * 512:(qt + 1) * 512, hp * 128:(hp + 1) * 128].rearrange("(s p) d -> p s d", p=128)
                            kb.dma("POOL", osv, ost[:], reads=[Bost])
                    if pending:
                        pending.pop()()
                    pending.append(epilogue)
        if pending:
            pending.pop()()
        T.close()

    def nsa_part(self):
        kb, I = self.kb, self.I
        for _ in range(self.cfg.get("pad_dve", 0)):
            kb.op("DVE", lambda e: e.memset(self.epsb[:], EPS), writes=[self.Bepsb])
        P = Scope(kb)
        CKc, BCKc = P.sb("CKc", [64, 2, 256], BF16)
        VCs = [P.sb("VC", [128, 2, 129], BF16) for _ in range(2)]
        kb.op("POOL", lambda e: e.memset(CKc[:], 0.0), writes=[BCKc])
        for g in range(2):
            vc, Bvc = VCs[g]
            for c in range(2):
                kb.dma("POOL", vc[:, c, 64:129], I["c_ov"][c * 128:(c + 1) * 128, :], writes=[Bvc])
        T = Scope(kb)
        CX = [T.sb("CX", [128, S], BF16) for _ in range(2)]
        kb.dma("SP", CX[0][0][:], self.FT[1536:1664, :], writes=[CX[0][1]])
        kb.dma("SP", CX[1][0][:], self.FT[1664:1792, :], writes=[CX[1][1]])
        w1, Bw1 = T.sb("w1", [128, 2 * 32 * 256], BF16)
        w1src = I["cmp_w1"].rearrange("d a l e -> d (a l e)")
        for half in range(2):
            for a in range(2):
                kb.dma("POOL", w1[half * 64:(half + 1) * 64, a * 8192:(a + 1) * 8192], w1src[:, a * 8192:(a + 1) * 8192], writes=[Bw1])
        w1v = w1[:].rearrange("p (a l e) -> p a l e", a=2, l=32)
        posb, Bposb = T.sb("posb", [64, 2, 34], BF16)
        kb.op("POOL", lambda e: e.memset(posb[:], 0.0), writes=[Bposb])
        kb.dma("POOL", posb[:, :, 0:32], I["cmp_posT"][:, :, :], writes=[Bposb])
        w2, Bw2 = T.sb("w2", [128, 2, 2, 64], BF16)
        for kv in range(2):
            for eh in range(2):
                kb.dma("POOL", w2[:, kv, eh, :], I["cmp_w2"][kv, eh * 128:(eh + 1) * 128, :], writes=[Bw2])
        b1, Bb1 = T.sb("b1", [128, 4], F32)
        pbs = Rot([T.ps("pb", [128, 512], F32) for _ in range(2)])
        phs = Rot([T.ps("ph", [128, 512], F32) for _ in range(3)])
        for kv in range(2):
            for eh in range(2):
                pb, Bpb = pbs.next()
                for l in range(32):
                    kb.op("PE", lambda e, l=l: e.matmul(pb[:, 0:2], w1v[0:64, kv, l, eh * 128:(eh + 1) * 128],
                                                        posb[0:64, kv, l:l + 2], start=(l == 0), stop=(l == 31)),
                          reads=[Bw1, Bposb], writes=[Bpb])
                kb.op("DVE", lambda e: e.tensor_copy(b1[:, kv * 2 + eh:kv * 2 + eh + 1], pb[:, 0:1]), reads=[Bpb], writes=[Bb1])
        hid = {}
        for kv in range(2):
            cx, Bcx = CX[kv]
            cxv = cx[:].rearrange("p (n j) -> p n j", j=16)
            for g in range(2):
                for eh in range(2):
                    ph, Bph = phs.next()
                    for l in range(32):
                        a = l // 16
                        kb.op("PE", lambda e, l=l, a=a: e.matmul(ph[:, 0:255], w1v[64 * g:64 * g + 64, kv, l, eh * 128:(eh + 1) * 128],
                                                                 cxv[64 * g:64 * g + 64, a:a + 255, l % 16], start=(l == 0), stop=(l == 31)),
                              reads=[Bw1, Bcx], writes=[Bph])
                    ht, Bht = T.sb("hid", [128, 256], BF16)
                    kb.op("POOL", lambda e: e.memset(ht[:], 0.0), writes=[Bht])
                    kb.op("ACT", lambda e: e.activation(ht[:, 0:255], ph[:, 0:255], AF.Silu, bias=b1[:, kv * 2 + eh:kv * 2 + eh + 1]),
                          reads=[Bph, Bb1], writes=[Bht])
                    hid[(kv, g, eh)] = (ht, Bht)
        for g in range(2):
            ph, Bph = phs.next()
            for eh in range(2):
                ht, Bht = hid[(0, g, eh)]
                kb.op("PE", lambda e: e.matmul(ph[0:64, 0:255], w2[:, 0, eh, :], ht[:, 0:255], start=(eh == 0), stop=(eh == 1)),
                      reads=[Bw2, Bht], writes=[Bph])
            kb.op("ACT", lambda e: e.copy(CKc[:, g, 0:255], ph[0:64, 0:255]), reads=[Bph], writes=[BCKc])
            vc, Bvc = VCs[g]
            for c in range(2):
                ph, Bph = phs.next()
                for eh in range(2):
                    ht, Bht = hid[(1, g, eh)]
                    kb.op("PE", lambda e: e.matmul(ph[:, 0:64], ht[:, c * 128:(c + 1) * 128], w2[:, 1, eh, :], start=(eh == 0), stop=(eh == 1)),
                          reads=[Bw2, Bht], writes=[Bph])
                kb.op("ACT", lambda e: e.copy(vc[:, c, 0:64], ph[:, 0:64]), reads=[Bph], writes=[Bvc])
        T.close()
        if self.cfg.get("stop") == "cmpmlp":
            if self.cfg.get("debug"):
                dck = self.tap("d_ckc", [64, 512], BF16)
                kb.dma("SP", dck[:, :], CKc[:].rearrange("p g n -> p (g n)"), reads=[BCKc])
                for g in range(2):
                    dvc = self.tap("d_vc%d" % g, [128, 258], BF16)
                    kb.dma("SP", dvc[:, :], VCs[g][0][:].rearrange("p c n -> p (c n)"), reads=[VCs[g][1]])
            P.close()
            return

        T = Scope(kb)
        nfv, Bnfv = T.sb("nfv", [128, NT, 64], F32)
        cst, Bcst = T.sb("cst", [128, NT, 64], F32)
        v01, Bv01 = T.sb("v01", [128, NT, 64], F32)
        kb.dma("SP", nfv[:], I["c_nfv"][:, :, :], writes=[Bnfv])
        kb.dma("SP", cst[:], I["c_cst"][:, :, :], writes=[Bcst])
        kb.dma("SP", v01[:], I["c_v01"][:, :, :], writes=[Bv01])
        QA4, BQA4 = T.sb("QA4", [128, 4, S], BF16)
        KS, BKS = T.sb("KS", [128, S], BF16)
        KW, BKW = T.sb("KW", [64, S], BF16)
        VS, BVS = T.sb("VS", [128, NT, 65], BF16)
        VW, BVW = T.sb("VW", [128, NT, 65], BF16)
        kb.op("POOL", lambda e: e.memset(VS[:, :, 64:65], 1.0), writes=[BVS])
        kb.op("POOL", lambda e: e.memset(VW[:, :, 64:65], 1.0), writes=[BVW])
        kb.dma("POOL", KS[64:128, :], I["c_e64"][:, :], writes=[BKS])
        augs = Rot([T.sb("aug", [128, 128], BF16) for _ in range(2)])
        for (a, Ba) in augs.items:
            kb.op("POOL", lambda e, a=a: e.memset(a[:], 0.0), writes=[Ba])
        R = dict(pS=Rot([T.ps("pS", [128, 512], F32) for _ in range(2)]),
                 pT=Rot([T.sb("pT", [128, 512], BF16) for _ in range(8)]))
        Oc = [T.ps("Oc", [128, 512], F32) for _ in range(2)]
        BOc = Oc[0][1]
        Osw = Rot([T.ps("OTsw", [128, 512], F32) for _ in range(2)])
        Otk, BOtk = T.ps("Otok", [128, 512], F32)
        osbs = Rot([T.sb("osb", [128, 512], F32) for _ in range(2)])
        ptr = Rot([T.ps("ptr", [128, 1024], BF16) for _ in range(1)])
        occ, Bocc = T.sb("occ", [128, 4, 4, 64], F32)
        imp, Bimp = T.sb("imp", [128, 4, 64], F32)
        itmp, Bitmp = T.sb("itmp", [128, 64], F32)
        sc, Bsc = T.sb("sc", [128, 64], F32)
        sc2, Bsc2 = T.sb("sc2", [128, 64], F32)
        m8a, Bm8a = T.sb("m8a", [128, 8], F32)
        m8b, Bm8b = T.sb("m8b", [128, 8], F32)
        selm, Bselm = T.sb("selm", [128, 64], F32)
        rds = Rot([T.sb("rd", [128, 2], F32) for _ in range(6)])
        osts = Rot([T.sb("ost", [128, 4, 256], BF16) for _ in range(2)])
        gsb = self.gate_sb

        for g in range(2):
            for r in range(4):
                h = 4 * g + r
                kb.dma("SP", QA4[0:64, r, :], self.FT[1024 + h * 64:1024 + (h + 1) * 64, :], writes=[BQA4])
            kb.dma("SP", KS[0:64, :], self.FT[1792 + 64 * g:1792 + 64 * g + 64, :], writes=[BKS])
            kb.dma("SP", KW[:], self.FT[1920 + 64 * g:1920 + 64 * g + 64, :], writes=[BKW])
            tms = self.TM[:, 512 + 64 * g:512 + 64 * g + 64].rearrange("(c p) d -> p c d", p=128)
            tmw = self.TM[:, 640 + 64 * g:640 + 64 * g + 64].rearrange("(c p) d -> p c d", p=128)
            for c4 in range(4):
                kb.dma("SP", VS[:, c4 * 8:(c4 + 1) * 8, 0:64], tms[:, c4 * 8:(c4 + 1) * 8, :], writes=[BVS])
                kb.dma("SP", VW[:, c4 * 8:(c4 + 1) * 8, 0:64], tmw[:, c4 * 8:(c4 + 1) * 8, :], writes=[BVW])
            vc, Bvc = VCs[g]
            parts = self.cfg.get("nsa_parts", ("cmp", "select", "sw"))
            for qt in self.cfg.get("nsa_qts", range(8)):
                ost, Bost = osts.next()
                cchunks = [0] + ([1] if qt >= 4 else [])
                cpairs = [(c, [(s, "full") for s in range(4)]) for c in cchunks]

                def cmp_mask(c, pT, BpT, krows):
                    kb.op("POOL", lambda e: e.affine_select(pT[:], pT[:], [[1, 512]], ALU.is_ge, 0.0,
                                                            base=512 * qt - 2048 * c - 31, channel_multiplier=-16),
                          reads=[BpT], writes=[BpT])

                for r in (range(4) if "cmp" in parts else ()):
                    h = 4 * g + r
                    self.attn_qtile(R, QA4[0:64, r, qt * 512:(qt + 1) * 512], BQA4,
                                    lambda c: (CKc[:, g, c * 128:(c + 1) * 128], 128), BCKc,
                                    lambda c: vc[:, c, :], Bvc, 129, cpairs, None, BOc,
                                    lambda O, s: Oc[s // 2][0][:, (s % 2) * 256:(s % 2) * 256 + 129], post_exp=cmp_mask, bank_of=lambda s: s // 2)
                    for s in range(4):
                        t = 4 * qt + s
                        O_ = Oc[s // 2][0][:, (s % 2) * 256:(s % 2) * 256 + 129]
                        rd, Brd = rds.next()
                        self.epi_norm(T, O_, BOc, 64, rd[:, 0:1], Brd)
                        if r == 0:
                            kb.op("DVE", lambda e, s=s: e.tensor_scalar(imp[:, s, :], O_[:, 65:129], rd[:, 0:1], None, ALU.mult),
                                  reads=[BOc, Brd], writes=[Bimp])
                        else:
                            kb.op("DVE", lambda e: e.tensor_scalar(itmp[:], O_[:, 65:129], rd[:, 0:1], None, ALU.mult),
                                  reads=[BOc, Brd], writes=[Bitmp])
                            kb.op("DVE", lambda e, s=s: e.tensor_tensor(imp[:, s, :], imp[:, s, :], itmp[:], ALU.add),
                                  reads=[Bitmp, Bimp], writes=[Bimp])
                        kb.op("DVE", lambda e: e.tensor_tensor(rd[:, 1:2], rd[:, 0:1], gsb[:, t, h:h + 1], ALU.mult), reads=[Brd, self.Bgate], writes=[Brd])
                        kb.op("DVE", lambda e, s=s, r=r: e.tensor_scalar(occ[:, r, s, :], O_[:, 0:64], rd[:, 1:2], None, ALU.mult),
                              reads=[BOc, Brd], writes=[Bocc])
                for s in (range(4) if "select" in parts else ()):
                    t = 4 * qt + s
                    aug, Baug = augs.next()
                    kb.op("DVE", lambda e: e.tensor_tensor(sc[:], imp[:, s, :], nfv[:, t, :], ALU.mult), reads=[Bimp, Bnfv], writes=[Bsc])
                    kb.op("DVE", lambda e: e.tensor_tensor(sc[:], sc[:], cst[:, t, :], ALU.add), reads=[Bsc, Bcst], writes=[Bsc])
                    kb.op("DVE", lambda e: e.max(m8a[:], sc[:]), reads=[Bsc], writes=[Bm8a])
                    kb.op("DVE", lambda e: e.match_replace(sc2[:], m8a[:], sc[:], -1e30), reads=[Bsc, Bm8a], writes=[Bsc2])
                    kb.op("DVE", lambda e: e.max(m8b[:], sc2[:]), reads=[Bsc2], writes=[Bm8b])
                    kb.op("DVE", lambda e: e.tensor_scalar(selm[:], sc[:], m8b[:, 7:8], None, ALU.is_ge), reads=[Bsc, Bm8b], writes=[Bselm])
                    kb.op("DVE", lambda e: e.tensor_tensor(selm[:], selm[:], v01[:, t, :], ALU.mult), reads=[Bselm, Bv01], writes=[Bselm])
                    kb.op("DVE", lambda e: e.tensor_scalar(aug[:, 64:128], selm[:], -1.0, -MASKV, ALU.add, ALU.mult), reads=[Bselm], writes=[Baug])
                    pt, Bpt = ptr.next()
                    kb.op("PE", lambda e: e.transpose(pt[:, 0:128], aug[:], self.ident[:]), reads=[Baug, self.Bident], writes=[Bpt])
                    for r in range(4):
                        kb.op("ACT", lambda e, r=r: e.copy(QA4[64:128, r, t * 128:(t + 1) * 128], pt[64:128, 0:128]), reads=[Bpt], writes=[BQA4])
                pending = []
                spairs = self.causal_pairs(qt)
                wpairs = []
                for kc in range(max(0, 4 * qt - 4), 4 * qt + 4):
                    subs = []
                    for s in range(4):
                        dlt = 4 * qt + s - kc
                        if dlt == 0:
                            subs.append((s, "tri"))
                        elif 1 <= dlt <= 3:
                            subs.append((s, "full"))
                        elif dlt == 4:
                            subs.append((s, "atri"))
                    if subs:
                        wpairs.append((kc, subs))
                for r in (range(4) if "sw" in parts else ()):
                    h = 4 * g + r
                    for br, (pairs, qrows, kl, Bkl, vt, Bvt) in enumerate((
                            (spairs, 128, KS, BKS, VS, BVS), (wpairs, 64, KW, BKW, VW, BVW))):
                        if br not in self.cfg.get("nsa_br", (0, 1)):
                            continue
                        OTb, BOTb = Osw.next()
                        self.attn_qtile(R, QA4[0:qrows, r, qt * 512:(qt + 1) * 512], BQA4,
                                        lambda kc, kl=kl, qrows=qrows: (kl[0:qrows, kc * 128:(kc + 1) * 128], 128), Bkl,
                                        lambda kc, vt=vt: vt[:, kc, :], Bvt, 65, pairs, OTb, BOTb, None, vstat=True)
                        def epilogue(OTb=OTb, BOTb=BOTb, br=br, r=r, h=h, qt=qt, ost=ost, Bost=Bost):
                            osb, Bosb = osbs.next()
                            self.ot_to_tok(OTb, BOTb, 65, osb, Bosb, Otk, BOtk)
                            Ob, BOb = Otk, BOtk
                            for s in range(4):
                                t = 4 * qt + s
                                O_ = Ob[:, s * 128:s * 128 + 65]
                                rd, Brd = rds.next()
                                self.epi_norm(T, O_, BOb, 64, rd[:, 0:1], Brd)
                                gcol = (br + 1) * 8 + h
                                kb.op("DVE", lambda e, rd=rd, t=t, gcol=gcol: e.tensor_tensor(rd[:, 1:2], rd[:, 0:1], gsb[:, t, gcol:gcol + 1], ALU.mult),
                                      reads=[Brd, self.Bgate], writes=[Brd])
                                if br == 0:
                                    kb.op("DVE", lambda e, O_=O_, rd=rd: e.tensor_scalar(itmp[:], O_[:, 0:64], rd[:, 1:2], None, ALU.mult),
                                          reads=[BOb, Brd], writes=[Bitmp])
                                    kb.op("DVE", lambda e, s=s: e.tensor_tensor(occ[:, r, s, :], occ[:, r, s, :], itmp[:], ALU.add),
                                          reads=[Bitmp, Bocc], writes=[Bocc])
                                else:
                                    kb.op("DVE", lambda e, s=s, O_=O_, rd=rd: e.scalar_tensor_tensor(ost[:, s, r * 64:(r + 1) * 64], O_[:, 0:64], rd[:, 1:2], occ[:, r, s, :], ALU.mult, ALU.add),
                                          reads=[BOb, Brd, Bocc], writes=[Bost])
                        if pending:
                            pending.pop()()
                        pending.append(epilogue)
                if pending:
                    pending.pop()()
                osv = self.OS[qt * 512:(qt + 1) * 512, 512 + g * 256:512 + (g + 1) * 256].rearrange("(s p) d -> p s d", p=128)
                kb.dma("POOL", osv, ost[:], reads=[Bost])
                if self.cfg.get("nsa_barrier"):
                    kb.barrier()
        T.close()
        P.close()


def _consts():
    c = {}
    c["c_ident"] = np.eye(128, dtype=np.float32)
    k = np.arange(128)[:, None]
    q = np.arange(128)[None, :]
    c["c_tri"] = (q >= k).astype(np.float32)
    c["c_atri"] = (k > q).astype(np.float32)
    key = np.arange(S)[None, :]
    c["c_e16"] = (key // 256 == np.arange(16)[:, None]).astype(np.float32)
    c["c_e64"] = (key // 64 == np.arange(64)[:, None]).astype(np.float32)
    ncmp = 255
    cs = np.arange(ncmp) * 16
    ss = np.arange(64) * 64
    ov = np.minimum(cs[:, None] + 32, ss[None, :] + 64) - np.maximum(cs[:, None], ss[None, :])
    ovp = np.zeros((256, 65), np.float32)
    ovp[:255, 0] = 1.0
    ovp[:255, 1:] = np.clip(ov, 0, None) / 32.0
    c["c_ov"] = ovp
    own = np.arange(16)[:, None]
    blk = np.arange(16)[None, :]
    c["c_t1"] = np.where(blk < own, 0.0, -1e30).astype(np.float32).reshape(1, 256)
    c["c_t2"] = (blk == own).astype(np.float32).reshape(1, 256)
    p = np.arange(128)[:, None, None]
    t = np.arange(NT)[None, :, None]
    m = np.arange(64)[None, None, :]
    cur = (t * 128 + p) // 64
    ok = m <= cur
    forced = ok & ((m == 0) | (m >= cur - 1))
    c["c_nfv"] = (ok & ~forced).astype(np.float32)
    c["c_cst"] = np.where(ok, np.where(forced, 1e4 + m, 0.0), -1e30).astype(np.float32)
    c["c_v01"] = ok.astype(np.float32)
    return c


def _core_inputs(b, inp, consts):
    f = lambda a: np.ascontiguousarray(np.asarray(a), dtype=np.float32)
    m = {}
    m["x"] = f(inp["x"][b])
    m["cT"] = f(np.asarray(inp["c"][b]).reshape(8, 128).T)
    m["posT"] = np.ascontiguousarray(np.asarray(inp["positions"][b]).reshape(NT, 128).T.astype(np.int32))
    for k in ("ada_w", "ada_b", "attn_norm", "ffn_norm", "ffn_w_gate", "ffn_w_up", "ffn_w_down"):
        m[k] = f(inp[k])
    m["sp_w_in"] = f(inp["sp_w_in"][0]); m["sp_w_out"] = f(inp["sp_w_out"][0])
    m["moba_q_norm"] = f(inp["moba_q_norm"]); m["moba_k_norm"] = f(inp["moba_k_norm"])
    m["nsa_q_norm"] = f(inp["nsa_q_norm"]); m["nsa_k_norm"] = f(inp["nsa_k_norm"][0])
    m["cmp_posT"] = f(np.asarray(inp["nsa_cmp_pos"][0]).transpose(2, 0, 1))
    m["cmp_w1"] = f(np.asarray(inp["nsa_cmp_w1"][0]).transpose(2, 0, 1, 3))
    m["cmp_w2"] = f(inp["nsa_cmp_w2"][0])
    m["diff_w_in"] = f(inp["diff_w_in"][0]); m["diff_w_out"] = f(inp["diff_w_out"][0])
    m["diff_q_norm"] = f(inp["diff_q_norm"]); m["diff_k_norm"] = f(inp["diff_k_norm"])
    m["diff_lambda"] = f(np.asarray(inp["diff_lambda"][0]).reshape(1, 256))
    m["diff_out_norm"] = f(inp["diff_out_norm"])
    m.update(consts)
    return m


_NC_CACHE = {}


def kernel(**inputs):
    consts = _consts()
    if "nc" not in _NC_CACHE:
        _NC_CACHE["nc"] = Prog({}).build()
    nc = _NC_CACHE["nc"]
    in_maps = [_core_inputs(b, inputs, consts) for b in range(8)]
    res = run_bass_kernel_spmd(nc, in_maps, core_ids=list(range(8)))
    return np.stack([np.asarray(r["out"], dtype=np.float32) for r in res.results], axis=0)
```

```python
from contextlib import ExitStack
import numpy as np
import concourse.bass as bass
import concourse.mybir as mybir
from concourse.bass_utils import run_bass_kernel_spmd

F32 = mybir.dt.float32
BF16 = mybir.dt.bfloat16
I32 = mybir.dt.int32
AF = mybir.ActivationFunctionType
ALU = mybir.AluOpType
AX = mybir.AxisListType

S = 4096
D = 1024
NT = S // 128
FFN = 2816
NJ = FFN // 128
EPS = 1e-6
MASKV = -30000.0
SP_IN = 2840
N_DMA_SEMS = 24


class Buf:
    __slots__ = ("w", "r", "name")

    def __init__(self, name=""):
        self.w = None
        self.r = {}
        self.name = name


class Rot:
    def __init__(self, items):
        self.items = list(items)
        self.i = 0

    def next(self):
        it = self.items[self.i]
        self.i = (self.i + 1) % len(self.items)
        return it


class KB:
    def __init__(self, nc):
        self.nc = nc
        self.es = ExitStack()
        self.engs = {"PE": nc.tensor, "ACT": nc.scalar, "DVE": nc.vector, "POOL": nc.gpsimd, "SP": nc.sync}
        self.sem = {}
        self.cnt = {}
        for e in ("PE", "ACT", "DVE", "POOL"):
            self.sem[e] = self.es.enter_context(nc.semaphore("s_" + e))
            self.cnt[e] = 0
        self.dsem = [self.es.enter_context(nc.semaphore("d_%d" % i)) for i in range(N_DMA_SEMS)]
        self.dcnt = [0] * N_DMA_SEMS
        self.dnext = 0
        self.dnext2 = [0, 0]
        self.waited = {e: {} for e in self.engs}
        self.n_inst = 0
        self.uid = 0
        self.limit = None
        self.log = None

    def name(self, p):
        self.uid += 1
        return "%s_%d" % (p, self.uid)

    def _wait(self, E, key, c, raw=False):
        if key == E and E == "PE":
            return
        w = self.waited[E]
        if w.get(key, 0) >= c:
            return
        w[key] = c
        if isinstance(key, int):
            self.engs[E].wait_ge(self.dsem[key], c)
        else:
            self.engs[E].wait_ge(self.sem[key], c)

    def _deps(self, E, reads, writes):
        for b in reads:
            if b.w is not None:
                self._wait(E, b.w[0], b.w[1], raw=True)
        for b in writes:
            if b.w is not None:
                self._wait(E, b.w[0], b.w[1])
            for k, c in b.r.items():
                self._wait(E, k, c)

    def _mark(self, tok, reads, writes):
        for b in reads:
            if b.r.get(tok[0], 0) < tok[1]:
                b.r[tok[0]] = tok[1]
        for b in writes:
            b.w = tok
            b.r = {}

    def op(self, E, fn, reads=(), writes=()):
        if self.limit is not None and self.n_inst >= self.limit:
            return None
        self._deps(E, reads, writes)
        if self.log is not None:
            import sys as _sys
            self.log.append((self.n_inst, E, _sys._getframe(1).f_lineno))
        ins = fn(self.engs[E])
        self.cnt[E] += 1
        ins.then_inc(self.sem[E], 1)
        tok = (E, self.cnt[E])
        self._mark(tok, reads, writes)
        self.n_inst += 1
        return tok

    def dma(self, Q, out_ap, in_ap, reads=(), writes=(), **kw):
        if self.limit is not None and self.n_inst >= self.limit:
            return None
        self._deps(Q, reads, writes)
        half = N_DMA_SEMS // 2
        qi = 0 if Q == "SP" else 1
        k = qi * half + self.dnext2[qi]
        self.dnext2[qi] = (self.dnext2[qi] + 1) % half
        if self.dcnt[k] > 0:
            self._wait(Q, k, self.dcnt[k])
        if self.log is not None:
            import sys as _sys
            self.log.append((self.n_inst, "DMA-" + Q, _sys._getframe(1).f_lineno))
        ins = self.engs[Q].dma_start(out=out_ap, in_=in_ap, **kw)
        self.dcnt[k] += 16
        ins.then_inc(self.dsem[k], 16)
        tok = (k, self.dcnt[k])
        self._mark(tok, reads, writes)
        self.n_inst += 1
        return tok

    def barrier(self, engines=("PE", "ACT", "DVE", "POOL", "SP")):
        for E in engines:
            for e2 in ("PE", "ACT", "DVE", "POOL"):
                if self.cnt[e2] > 0:
                    self._wait(E, e2, self.cnt[e2])
            for k in range(N_DMA_SEMS):
                if self.dcnt[k] > 0:
                    self._wait(E, k, self.dcnt[k])


class Scope:
    def __init__(self, kb):
        self.kb = kb
        self.es = ExitStack()

    def sb(self, name, shape, dt):
        t = self.es.enter_context(self.kb.nc.sbuf_tensor(self.kb.name(name), list(shape), dt))
        return t, Buf(name)

    def ps(self, name, shape, dt):
        t = self.es.enter_context(self.kb.nc.psum_tensor(self.kb.name(name), list(shape), dt))
        return t, Buf(name)

    def close(self):
        self.kb.barrier()
        self.es.close()


def bc_row(ap_row, n):
    return ap_row.to_broadcast([128, n])


class Prog:
    def __init__(self, cfg):
        self.cfg = cfg
        self.nc = bass.Bass("TRN2", target_bir_lowering=False)
        self.kb = KB(self.nc)
        self.kb.limit = cfg.get("limit")
        self.I = {}
        self.taps = {}

    def din(self, name, shape, dt=F32):
        self.I[name] = self.nc.dram_tensor(name, list(shape), dt, kind="ExternalInput").ap()
        return self.I[name]

    def dscr(self, name, shape, dt):
        if self.cfg.get("debug"):
            return self.nc.dram_tensor(name, list(shape), dt, kind="ExternalOutput").ap()
        return self.nc.dram_tensor(name, list(shape), dt).ap()

    def tap(self, name, shape, dt=F32):
        self.taps[name] = self.nc.dram_tensor(name, list(shape), dt, kind="ExternalOutput").ap()
        return self.taps[name]

    def declare(self):
        d = self.din
        d("x", [S, D]); d("cT", [128, 8]); d("posT", [128, NT], I32)
        d("ada_w", [2, D, 6 * D]); d("ada_b", [2, 6 * D])
        d("attn_norm", [2, D]); d("ffn_norm", [2, D])
        d("ffn_w_gate", [2, D, FFN]); d("ffn_w_up", [2, D, FFN]); d("ffn_w_down", [2, FFN, D])
        d("sp_w_in", [D, SP_IN]); d("sp_w_out", [D, D])
        d("moba_q_norm", [1, 64]); d("moba_k_norm", [1, 64]); d("nsa_q_norm", [1, 64]); d("nsa_k_norm", [3, 64])
        d("cmp_posT", [64, 2, 32]); d("cmp_w1", [64, 2, 32, 256]); d("cmp_w2", [2, 256, 64])
        d("diff_w_in", [D, 3072]); d("diff_w_out", [D, D])
        d("diff_q_norm", [1, 64]); d("diff_k_norm", [1, 64]); d("diff_lambda", [1, 256]); d("diff_out_norm", [1, 128])
        d("c_ident", [128, 128]); d("c_tri", [128, 128]); d("c_atri", [128, 128])
        d("c_e16", [16, S]); d("c_e64", [64, S]); d("c_ov", [256, 65])
        d("c_t1", [1, 256]); d("c_t2", [1, 256])
        d("c_nfv", [128, NT, 64]); d("c_cst", [128, NT, 64]); d("c_v01", [128, NT, 64])
        self.out = self.nc.dram_tensor("out", [S, D], F32, kind="ExternalOutput").ap()
        self.xmid = self.dscr("xmid", [S, D], F32)
        self.x1s = self.dscr("x1s", [S, D], F32)
        self.FT = self.dscr("FT", [2048, S], BF16)
        self.TM = self.dscr("TM", [S, D], BF16)
        self.OS = self.dscr("OS", [S, D], BF16)
        self.H2T = self.dscr("H2T", [D, S], BF16)

    def setup(self):
        kb, I = self.kb, self.I
        self.G = Scope(kb)
        G = self.G
        self.ident, self.Bident = G.sb("ident", [128, 128], BF16)
        kb.dma("POOL", self.ident[:], I["c_ident"][:, :], writes=[self.Bident])
        self.identF, self.BidentF = G.sb("identF", [128, 128], F32)
        kb.dma("SP", self.identF[:], I["c_ident"][:, :], writes=[self.BidentF])
        self.tri, self.Btri = G.sb("tri", [128, 128], BF16)
        kb.dma("POOL", self.tri[:], I["c_tri"][:, :], writes=[self.Btri])
        self.atri, self.Batri = G.sb("atri", [128, 128], BF16)
        kb.dma("POOL", self.atri[:], I["c_atri"][:, :], writes=[self.Batri])
        self.cosT, self.Bcos = G.sb("cosT", [128, NT, 8], F32)
        self.sinT, self.Bsin = G.sb("sinT", [128, NT, 8], F32)
        self.mod, self.Bmod = G.sb("mod", [128, 6, D], F32)
        self.silc, self.Bsilc = G.sb("silc", [128, 8, 128], F32)
        with_scope = Scope(kb)
        T = with_scope
        pi, Bpi = T.sb("pi", [128, NT], I32)
        pf, Bpf = T.sb("pf", [128, NT], F32)
        ang, Bang = T.sb("ang", [128, NT, 8], F32)
        tmp, Btmp = T.sb("tmp", [128, NT, 8], F32)
        tmp2, Btmp2 = T.sb("tmp2", [128, NT, 8], F32)
        kb.dma("SP", pi[:], I["posT"][:, :], writes=[Bpi])
        kb.op("DVE", lambda e: e.tensor_copy(pf[:], pi[:]), reads=[Bpi], writes=[Bpf])
        inv = (1.0 / (np.float32(500000.0) ** (np.arange(0, 16, 2, dtype=np.float32) / np.float32(16)))).astype(np.float32)
        for j in range(8):
            kb.op("DVE", lambda e, j=j: e.tensor_scalar(ang[:, :, j], pf[:], float(inv[j]), None, ALU.mult),
                  reads=[Bpf], writes=[Bang])
        TWO_PI = float(2 * np.pi)
        MAG = 12582912.0

        def sin_of(dst, Bdst, shift):
            if shift != 0.0:
                kb.op("DVE", lambda e: e.tensor_scalar(tmp2[:], ang[:], shift, None, ALU.add), reads=[Bang], writes=[Btmp2])
                src, Bsrc = tmp2, Btmp2
            else:
                src, Bsrc = ang, Bang
            kb.op("DVE", lambda e: e.tensor_scalar(tmp[:], src[:], 1.0 / TWO_PI, MAG, ALU.mult, ALU.add), reads=[Bsrc], writes=[Btmp])
            kb.op("DVE", lambda e: e.tensor_scalar(tmp[:], tmp[:], -MAG, -TWO_PI, ALU.add, ALU.mult), reads=[Btmp], writes=[Btmp])
            kb.op("DVE", lambda e: e.tensor_tensor(tmp[:], src[:], tmp[:], ALU.add), reads=[Bsrc, Btmp], writes=[Btmp])
            kb.op("DVE", lambda e: e.tensor_scalar(tmp[:], tmp[:], float(np.pi), float(-np.pi), ALU.min, ALU.max), reads=[Btmp], writes=[Btmp])
            kb.op("ACT", lambda e: e.activation(dst[:], tmp[:], AF.Sin), reads=[Btmp], writes=[Bdst])

        sin_of(self.sinT, self.Bsin, 0.0)
        sin_of(self.cosT, self.Bcos, float(np.pi / 2))
        ct, Bct = T.sb("ct", [128, 8], F32)
        kb.dma("SP", ct[:], I["cT"][:, :], writes=[Bct])
        kb.op("ACT", lambda e: e.activation(ct[:], ct[:], AF.Silu), reads=[Bct], writes=[Bct])
        kb.op("DVE", lambda e: e.tensor_copy(self.silc[:], ct[:].unsqueeze(2).to_broadcast([128, 8, 128])),
              reads=[Bct], writes=[self.Bsilc])
        T.close()

    def compute_mod(self, li):
        kb, I = self.kb, self.I
        T = Scope(kb)
        wch = [T.sb("adaw", [128, 8, 512], F32) for _ in range(2)]
        bch = [T.sb("adab", [128, 512], F32) for _ in range(2)]
        pm = [T.ps("pmod", [128, 512], F32) for _ in range(2)]
        nrm, Bnrm = T.sb("nrm", [128, 2, D], F32)
        kb.dma("SP", nrm[:, 0, :], bc_row(I["attn_norm"][li:li + 1, :], D), writes=[Bnrm])
        kb.dma("SP", nrm[:, 1, :], bc_row(I["ffn_norm"][li:li + 1, :], D), writes=[Bnrm])
        wv = I["ada_w"][li].rearrange("(kc p) n -> p kc n", p=128)
        for g in range(12):
            w, Bw = wch[g % 2]
            b, Bb = bch[g % 2]
            p, Bp = pm[g % 2]
            kb.dma("SP", w[:], wv[:, :, g * 512:(g + 1) * 512], writes=[Bw])
            kb.dma("SP", b[:], bc_row(I["ada_b"][li:li + 1, g * 512:(g + 1) * 512], 512), writes=[Bb])
            for kc in range(8):
                kb.op("PE", lambda e, kc=kc: e.matmul(p[:], self.silc[:, kc, :], w[:, kc, :], start=(kc == 0), stop=(kc == 7)),
                      reads=[Bw, self.Bsilc], writes=[Bp])
            sl = self.mod[:, g // 2, (g % 2) * 512:(g % 2 + 1) * 512]
            kb.op("DVE", lambda e: e.tensor_tensor(sl, p[:], b[:], ALU.add), reads=[Bp, Bb], writes=[self.Bmod])
        kb.op("DVE", lambda e: e.scalar_tensor_tensor(self.mod[:, 1, :], self.mod[:, 1, :], 1.0, nrm[:, 0, :], ALU.add, ALU.mult),
              reads=[self.Bmod, Bnrm], writes=[self.Bmod])
        kb.op("DVE", lambda e: e.scalar_tensor_tensor(self.mod[:, 4, :], self.mod[:, 4, :], 1.0, nrm[:, 1, :], ALU.add, ALU.mult),
              reads=[self.Bmod, Bnrm], writes=[self.Bmod])
        T.close()

    def rms_mod_tile(self, T, xt, Bxt, hb, Bhb, gi, si, scr):
        kb = self.kb
        junk, Bjunk, ss, Bss, hn, Bhn = scr
        kb.op("ACT", lambda e: e.activation(junk[:], xt[:], AF.Square, accum_out=ss[:]), reads=[Bxt], writes=[Bjunk, Bss])
        kb.op("ACT", lambda e: e.activation(ss[:], ss[:], AF.Sqrt, bias=self.epsb[:], scale=1.0 / D), reads=[Bss, self.Bepsb], writes=[Bss])
        kb.op("DVE", lambda e: e.reciprocal(ss[:], ss[:]), reads=[Bss], writes=[Bss])
        kb.op("DVE", lambda e: e.scalar_tensor_tensor(hn[:], xt[:], ss[:, 0:1], self.mod[:, gi, :], ALU.mult, ALU.mult),
              reads=[Bxt, Bss, self.Bmod], writes=[Bhn])
        kb.op("POOL", lambda e: e.tensor_tensor(hb[:], hn[:], self.mod[:, si, :], ALU.add), reads=[Bhn, self.Bmod], writes=[Bhb])

    def transpose8(self, src, Bsrc, pt, Bpt, dst_ap, Bdst, eng="ACT"):
        kb = self.kb
        for j in range(8):
            kb.op("PE", lambda e, j=j: e.transpose(pt[:, j * 128:(j + 1) * 128], src[:, j * 128:(j + 1) * 128], self.ident[:]),
                  reads=[Bsrc, self.Bident], writes=[Bpt])
        if eng == "ACT":
            kb.op("ACT", lambda e: e.copy(dst_ap, pt[:].rearrange("p (j t) -> p j t", j=8)), reads=[Bpt], writes=[Bdst])
        else:
            kb.op("DVE", lambda e: e.tensor_copy(dst_ap, pt[:].rearrange("p (j t) -> p j t", j=8)), reads=[Bpt], writes=[Bdst])

    def load_w_bf16(self, dst, Bdst, w_ap, nk):
        for kc in range(nk):
            self.kb.dma("POOL", dst[:, kc, :], w_ap[kc * 128:(kc + 1) * 128, :], writes=[Bdst])

    def phase_a(self, li, x_src):
        kb, I = self.kb, self.I
        T = Scope(kb)
        if li == 1:
            w_ap, ncols = I["diff_w_in"], 3072
            groups = []
            for g in range(6):
                if g < 4:
                    gi = 0 if g < 2 else 1
                    blocks = [("T", g * 512 + b * 128) for b in range(4)]
                    groups.append(dict(c0=g * 512, w=512, qk=[(0, 8)], gain=gi, blocks=blocks, gate=None))
                else:
                    blocks = [("V", (g - 4) * 512 + b * 128) for b in range(4)]
                    groups.append(dict(c0=g * 512, w=512, qk=[], gain=None, blocks=blocks, gate=None))
            gain_srcs = [[(I["diff_q_norm"][0:1, :], 0, 8)], [(I["diff_k_norm"][0:1, :], 0, 8)]]
        else:
            w_ap, ncols = I["sp_w_in"], SP_IN
            kn = I["nsa_k_norm"]
            groups = [
                dict(c0=0, w=512, qk=[(0, 8)], gain=0, blocks=[("T", b * 128) for b in range(4)], gate=None),
                dict(c0=512, w=512, qk=[(0, 8)], gain=1, blocks=[("T", 512 + b * 128) for b in range(4)], gate=None),
                dict(c0=1024, w=512, qk=[], gain=None, blocks=[("V", b * 128) for b in range(4)], gate=None),
                dict(c0=1536, w=512, qk=[(0, 8)], gain=2, blocks=[("T", 1024 + b * 128) for b in range(4)], gate=None),
                dict(c0=2048, w=512, qk=[(0, 2), (4, 6)], gain=3,
                     blocks=[("T", 1536), ("T", 1664), ("T", 1792), ("V", 512)], gate=None),
                dict(c0=2560, w=280, qk=[(0, 2)], gain=4, blocks=[("T", 1920), ("V", 640)], gate=(256, 24)),
            ]
            gain_srcs = [
                [(I["moba_q_norm"][0:1, :], 0, 8)], [(I["moba_k_norm"][0:1, :], 0, 8)], [(I["nsa_q_norm"][0:1, :], 0, 8)],
                [(kn[0:1, :], 0, 2), (kn[1:2, :], 4, 6)], [(kn[2:3, :], 0, 2)],
            ]
        nk = 8
        W, BW = T.sb("w_in", [128, nk, ncols], BF16)
        self.load_w_bf16(W, BW, w_ap, nk)
        gains = []
        for gs in gain_srcs:
            gr, Bgr = T.sb("gainrow", [128, 8, 64], F32)
            kb.op("POOL", lambda e: e.memset(gr[:], 1.0), writes=[Bgr])
            for (src, u0, u1) in gs:
                for u in range(u0, u1):
                    kb.dma("SP", gr[:, u, :], bc_row(src, 64), writes=[Bgr])
            gains.append((gr, Bgr))
        NG = len(groups)
        xts = Rot([T.sb("xt", [128, D], F32) for _ in range(3)])
        hbs = Rot([T.sb("hb", [128, D], BF16) for _ in range(2)])
        hTs = Rot([T.sb("hT", [128, 8, 128], BF16) for _ in range(2)])
        junk, Bjunk = T.sb("junk", [128, D], BF16)
        sss = Rot([T.sb("ss", [128, 1], F32) for _ in range(2)])
        hn, Bhn = T.sb("hn", [128, D], F32)
        sqs = [T.sb("sq", [128, 512], F32) for _ in range(NG)]
        ss8l = [T.sb("ss8", [128, 8], F32) for _ in range(NG)]
        qnl = [T.sb("qn", [128, 8, 64], F32) for _ in range(NG)]
        postl = [T.sb("post", [128, 512], BF16) for _ in range(NG)]
        rtl = [T.sb("rt", [128, 4, 8, 8], F32) for _ in range(NG)]
        pts = Rot([T.ps("ptA", [128, 1024], BF16) for _ in range(1)])
        pgl = [T.ps("pgA", [128, 512], F32) for _ in range(NG)]
        pbt, _ = T.ps("ptB", [128, 1024], BF16)
        pbh = Rot([(pbt[:, 0:512], Buf("pb0")), (pbt[:, 512:1024], Buf("pb1"))])
        stages = {}
        for gi_, g in enumerate(groups):
            if any(k == "T" for k, _ in g["blocks"]):
                stages[gi_] = [T.sb("stage", [128, 4, 512], BF16) for _ in range(2)]

        def load_x(t):
            xt, Bxt = xts.next()
            kb.dma("SP", xt[:], x_src[t * 128:(t + 1) * 128, :], writes=[Bxt])
            return xt, Bxt

        def prep(t, xtb):
            xt, Bxt = xtb
            hb, Bhb = hbs.next()
            ss, Bss = sss.next()
            self.rms_mod_tile(T, xt, Bxt, hb, Bhb, 1, 0, (junk, Bjunk, ss, Bss, hn, Bhn))
            hT, BhT = hTs.next()
            pt, Bpt = pts.next()
            self.transpose8(hb, Bhb, pt, Bpt, hT[:], BhT, eng="ACT")
            return hT, BhT

        xq = [load_x(0)]
        if NT > 1:
            xq.append(load_x(1))
        hq = [prep(0, xq.pop(0))]
        for t in range(NT):
            hT, BhT = hq.pop(0)
            if t + 2 < NT:
                xq.append(load_x(t + 2))
            tt = t % 4
            half = (t // 4) % 2
            for gi_, g in enumerate(groups):
                w = g["w"]
                pg, Bpg = pgl[gi_]
                for kc in range(8):
                    kb.op("PE", lambda e, kc=kc, pg=pg, w=w, g=g: e.matmul(pg[:, 0:w], hT[:, kc, :], W[:, kc, g["c0"]:g["c0"] + w],
                                                                          start=(kc == 0), stop=(kc == 7)),
                          reads=[BhT, BW], writes=[Bpg])
            if t + 1 < NT:
                hq.append(prep(t + 1, xq.pop(0)))
            info = []
            for gi_, g in enumerate(groups):
                w = g["w"]
                wq = (w // 64) * 64
                info.append((gi_, g, w, wq, wq // 64))
            for gi_, g, w, wq, nu in info:
                pg, Bpg = pgl[gi_]
                post, Bpost = postl[gi_]
                sq, Bsq = sqs[gi_]
                if g["qk"]:
                    kb.op("ACT", lambda e, sq=sq, pg=pg, wq=wq: e.activation(sq[:, 0:wq], pg[:, 0:wq], AF.Square), reads=[Bpg], writes=[Bsq])
                else:
                    kb.op("ACT", lambda e, post=post, pg=pg, wq=wq: e.copy(post[:, 0:wq], pg[:, 0:wq]), reads=[Bpg], writes=[Bpost])
                if g["gate"] is not None:
                    gc0, gw = g["gate"]
                    kb.op("ACT", lambda e, pg=pg, gc0=gc0, gw=gw: e.activation(self.gate_sb[:, t, :], pg[:, gc0:gc0 + gw], AF.Sigmoid),
                          reads=[Bpg], writes=[self.Bgate])
            for gi_, g, w, wq, nu in info:
                if not g["qk"]:
                    continue
                sq, Bsq = sqs[gi_]
                ss8, Bss8 = ss8l[gi_]
                kb.op("DVE", lambda e, ss8=ss8, sq=sq, wq=wq, nu=nu: e.tensor_reduce(ss8[:, 0:nu], sq[:, 0:wq].rearrange("p (u d) -> p u d", d=64), AX.X, ALU.add),
                      reads=[Bsq], writes=[Bss8])
            for gi_, g, w, wq, nu in info:
                if not g["qk"]:
                    continue
                ss8, Bss8 = ss8l[gi_]
                kb.op("ACT", lambda e, ss8=ss8, nu=nu: e.activation(ss8[:, 0:nu], ss8[:, 0:nu], AF.Sqrt, bias=self.epsb[:], scale=1.0 / 64),
                      reads=[Bss8, self.Bepsb], writes=[Bss8])
            for gi_, g, w, wq, nu in info:
                if not g["qk"]:
                    continue
                pg, Bpg = pgl[gi_]
                ss8, Bss8 = ss8l[gi_]
                qn, Bqn = qnl[gi_]
                gr, Bgr = gains[g["gain"]]
                kb.op("DVE", lambda e, ss8=ss8, nu=nu: e.reciprocal(ss8[:, 0:nu], ss8[:, 0:nu]), reads=[Bss8], writes=[Bss8])
                qk_units = set()
                for (u0, u1) in g["qk"]:
                    qk_units.update(range(u0, u1))
                for u in [u for u in range(nu) if u not in qk_units]:
                    kb.op("DVE", lambda e, u=u, ss8=ss8: e.memset(ss8[:, u:u + 1], 1.0), writes=[Bss8])
                kb.op("DVE", lambda e, qn=qn, pg=pg, ss8=ss8, nu=nu, wq=wq: e.tensor_tensor(
                    qn[:, 0:nu, :], pg[:, 0:wq].rearrange("p (u d) -> p u d", d=64),
                    ss8[:, 0:nu].unsqueeze(2).to_broadcast([128, nu, 64]), ALU.mult),
                      reads=[Bpg, Bss8], writes=[Bqn])
                kb.op("DVE", lambda e, qn=qn, gr=gr, nu=nu: e.tensor_tensor(qn[:, 0:nu, :], qn[:, 0:nu, :], gr[:, 0:nu, :], ALU.mult),
                      reads=[Bqn, Bgr], writes=[Bqn])
            for gi_, g, w, wq, nu in info:
                if not g["qk"]:
                    continue
                qn, Bqn = qnl[gi_]
                post, Bpost = postl[gi_]
                kb.op("ACT", lambda e, post=post, qn=qn, wq=wq, nu=nu: e.copy(post[:, 0:wq], qn[:, 0:nu, :].rearrange("p u d -> p (u d)")),
                      reads=[Bqn], writes=[Bpost])
            for gi_, g, w, wq, nu in info:
                if not g["qk"]:
                    continue
                qn, Bqn = qnl[gi_]
                post, Bpost = postl[gi_]
                rt, Brt = rtl[gi_]
                postv = post[:, 0:wq].rearrange("p (u d) -> p u d", d=64)
                for (u0, u1) in g["qk"]:
                    n_ = u1 - u0
                    cosb = self.cosT[:, t, :].unsqueeze(1).to_broadcast([128, n_, 8])
                    sinb = self.sinT[:, t, :].unsqueeze(1).to_broadcast([128, n_, 8])
                    t1 = qn[:, u0:u1, 0:8]
                    t2 = qn[:, u0:u1, 8:16]
                    kb.op("DVE", lambda e, rt=rt, n_=n_, t1=t1, cosb=cosb: e.tensor_tensor(rt[:, 0, 0:n_, :], t1, cosb, ALU.mult), reads=[Bqn, self.Bcos], writes=[Brt])
                    kb.op("DVE", lambda e, rt=rt, n_=n_, t2=t2, sinb=sinb: e.tensor_tensor(rt[:, 1, 0:n_, :], t2, sinb, ALU.mult), reads=[Bqn, self.Bsin], writes=[Brt])
                    kb.op("DVE", lambda e, rt=rt, n_=n_, t2=t2, cosb=cosb: e.tensor_tensor(rt[:, 2, 0:n_, :], t2, cosb, ALU.mult), reads=[Bqn, self.Bcos], writes=[Brt])
                    kb.op("DVE", lambda e, rt=rt, n_=n_, t1=t1, sinb=sinb: e.tensor_tensor(rt[:, 3, 0:n_, :], t1, sinb, ALU.mult), reads=[Bqn, self.Bsin], writes=[Brt])
                    kb.op("DVE", lambda e, rt=rt, n_=n_, postv=postv, u0=u0, u1=u1: e.tensor_tensor(postv[:, u0:u1, 0:8], rt[:, 0, 0:n_, :], rt[:, 1, 0:n_, :], ALU.subtract),
                          reads=[Brt], writes=[Bpost])
                    kb.op("DVE", lambda e, rt=rt, n_=n_, postv=postv, u0=u0, u1=u1: e.tensor_tensor(postv[:, u0:u1, 8:16], rt[:, 2, 0:n_, :], rt[:, 3, 0:n_, :], ALU.add),
                          reads=[Brt], writes=[Bpost])
            for gi_, g, w, wq, nu in info:
                post, Bpost = postl[gi_]
                tblocks = [(bi, dst) for bi, (k, dst) in enumerate(g["blocks"]) if k == "T"]
                if tblocks:
                    pb, Bpb = pbh.next()
                    st, Bst = stages[gi_][half]
                    for bi, dst in tblocks:
                        kb.op("PE", lambda e, bi=bi, pb=pb, post=post: e.transpose(pb[:, bi * 128:(bi + 1) * 128], post[:, bi * 128:(bi + 1) * 128], self.ident[:]),
                              reads=[Bpost, self.Bident], writes=[Bpb])
                    b0, b1 = tblocks[0][0], tblocks[-1][0] + 1
                    kb.op("ACT", lambda e, st=st, pb=pb, b0=b0, b1=b1: e.copy(st[:, b0:b1, tt * 128:(tt + 1) * 128],
                                                                              pb[:, b0 * 128:b1 * 128].rearrange("p (b t) -> p b t", t=128)),
                          reads=[Bpb], writes=[Bst])
                    if tt == 3:
                        t0 = (t - 3) * 128
                        for bi, dst in tblocks:
                            kb.dma("POOL", self.FT[dst:dst + 128, t0:t0 + 512], st[:, bi, :], reads=[Bst])
                for bi, (k, dst) in enumerate(g["blocks"]):
                    if k == "V":
                        kb.dma("POOL", self.TM[t * 128:(t + 1) * 128, dst:dst + 128], post[:, bi * 128:(bi + 1) * 128], reads=[Bpost])
        T.close()

    def attn_qtile(self, R, q_rhs, Bq, k_lhsT, Bk, v_rhs, Bv, vw, pairs, O, BO, o_off, post_exp=None, bank_of=lambda s: 0, vstat=False, hooks=None):
        kb = self.kb
        LOOK = self.cfg.get("look", 5)
        last = {}
        started = set()
        for kc, subs in pairs:
            for s, kind in subs:
                last[s] = kc
        live = {}

        def stage1(i):
            kc, subs = pairs[i]
            pS, BpS = R["pS"].next()
            kl, krows = k_lhsT(kc)
            kb.op("PE", lambda e: e.matmul(pS[0:krows, :], kl, q_rhs, start=True, stop=True), reads=[Bk, Bq], writes=[BpS])
            pT, BpT = R["pT"].next()
            kb.op("ACT", lambda e: e.activation(pT[0:krows, :], pS[0:krows, :], AF.Exp, scale=0.125), reads=[BpS], writes=[BpT])
            if post_exp is not None:
                post_exp(kc, pT, BpT, krows)
            for s, kind in subs:
                if kind == "tri":
                    kb.op("DVE", lambda e, s=s: e.tensor_tensor(pT[:, s * 128:(s + 1) * 128], pT[:, s * 128:(s + 1) * 128], self.tri[:], ALU.mult),
                          reads=[BpT, self.Btri], writes=[BpT])
                elif kind == "atri":
                    kb.op("DVE", lambda e, s=s: e.tensor_tensor(pT[:, s * 128:(s + 1) * 128], pT[:, s * 128:(s + 1) * 128], self.atri[:], ALU.mult),
                          reads=[BpT, self.Batri], writes=[BpT])
            live[i] = (pT, BpT, krows)

        def stage2(i):
            kc, subs = pairs[i]
            pT, BpT, krows = live.pop(i)
            if vstat:
                c0 = min(s_ for s_, _ in subs) * 128
                c1 = (max(s_ for s_, _ in subs) + 1) * 128
                st_flag = 0 not in started
                started.add(0)
                kb.op("PE", lambda e: e.matmul(O[0:vw, c0:c1], v_rhs(kc)[0:krows, 0:vw], pT[0:krows, c0:c1],
                                               start=st_flag, stop=(i == len(pairs) - 1), skip_group_check=True),
                      reads=[BpT, Bv], writes=[BO])
                return
            for s, kind in subs:
                bk = bank_of(s)
                st_flag = bk not in started
                started.add(bk)
                kb.op("PE", lambda e, s=s, st_flag=st_flag: e.matmul(o_off(O, s), pT[0:krows, s * 128:(s + 1) * 128], v_rhs(kc)[0:krows, :],
                                                                     start=st_flag, stop=(last[s] == kc), skip_group_check=True),
                      reads=[BpT, Bv], writes=[BO])

        n = len(pairs)
        for i in range(n + LOOK):
            if i < n:
                stage1(i)
            if hooks and i in hooks:
                hooks.pop(i)()
            if i - LOOK >= 0:
                stage2(i - LOOK)
        if hooks:
            for k in sorted(hooks):
                hooks.pop(k)()

    def ot_to_tok(self, OT, BOT, vw, osb, Bosb, Otok, BOtok):
        kb = self.kb
        kb.op("ACT", lambda e: e.copy(osb[0:vw, :], OT[0:vw, :]), reads=[BOT], writes=[Bosb])
        for s in range(4):
            kb.op("PE", lambda e, s=s: e.transpose(Otok[:, s * 128:s * 128 + vw], osb[0:vw, s * 128:(s + 1) * 128], self.identF[0:vw, 0:vw]),
                  reads=[Bosb, self.BidentF], writes=[BOtok])

    @staticmethod
    def causal_pairs(qt):
        pairs = []
        for kc in range(4 * qt + 4):
            j = kc - 4 * qt
            if j < 0:
                pairs.append((kc, [(s, "full") for s in range(4)]))
            else:
                pairs.append((kc, [(s, "tri" if s == j else "full") for s in range(j, 4)]))
        return pairs

    def phase_b_diff(self, li_odd_index):
        kb, I = self.kb, self.I
        T = Scope(kb)
        lam_init = 0.8 - 0.6 * float(np.exp(-0.3 * 1))
        lp, Blp = T.sb("lp", [128, 4, 64], F32)
        kb.dma("SP", lp[:].rearrange("p a d -> p (a d)"), bc_row(I["diff_lambda"][0:1, :], 256), writes=[Blp])
        l2, Bl2 = T.sb("l2", [128, 2, 64], F32)
        kb.op("DVE", lambda e: e.tensor_tensor(l2[:, 0, :], lp[:, 0, :], lp[:, 1, :], ALU.mult), reads=[Blp], writes=[Bl2])
        kb.op("DVE", lambda e: e.tensor_tensor(l2[:, 1, :], lp[:, 2, :], lp[:, 3, :], ALU.mult), reads=[Blp], writes=[Bl2])
        ls, Bls = T.sb("ls", [128, 2], F32)
        kb.op("DVE", lambda e: e.tensor_reduce(ls[:], l2[:], AX.X, ALU.add), reads=[Bl2], writes=[Bls])
        kb.op("ACT", lambda e: e.activation(ls[:], ls[:], AF.Exp), reads=[Bls], writes=[Bls])
        nlam, Bnlam = T.sb("nlam", [128, 1], F32)
        kb.op("DVE", lambda e: e.tensor_tensor(nlam[:], ls[:, 1:2], ls[:, 0:1], ALU.subtract), reads=[Bls], writes=[Bnlam])
        kb.op("DVE", lambda e: e.tensor_scalar(nlam[:], nlam[:], -lam_init, None, ALU.add), reads=[Bnlam], writes=[Bnlam])
        go, Bgo = T.sb("go", [128, 128], F32)
        kb.dma("SP", go[:], bc_row(I["diff_out_norm"][0:1, :], 128), writes=[Bgo])
        kb.op("DVE", lambda e: e.tensor_scalar(go[:], go[:], 1.0 - lam_init, None, ALU.mult), reads=[Bgo], writes=[Bgo])

        KTs = Rot([T.sb("KT", [128, S], BF16) for _ in range(2)])
        QTs = Rot([T.sb("QT", [128, S], BF16) for _ in range(2)])
        Vs = Rot([T.sb("V", [128, NT, 129], BF16) for _ in range(2)])
        for (v, Bv) in Vs.items:
            kb.op("POOL", lambda e, v=v: e.memset(v[:, :, 128:129], 1.0), writes=[Bv])
        R = dict(pS=Rot([T.ps("pS", [128, 512], F32) for _ in range(3)]),
                 pT=Rot([T.sb("pT", [128, 512], BF16) for _ in range(8)]))
        Os = [[T.ps("O", [128, 512], F32) for _ in range(2)] for _ in range(2)]
        osts = Rot([T.sb("ost", [128, 4, 128], BF16) for _ in range(2)])
        osbs = Rot([[[T.sb("osbd", [128, 512], F32) for _ in range(2)] for _ in range(2)] for _ in range(2)])
        pending = []
        a1ts = Rot([T.sb("a1t", [128, 4, 128], F32) for _ in range(2)])
        rdts = Rot([T.sb("rdt", [128, 4, 4], F32) for _ in range(2)])
        a0s = Rot([T.sb("a0", [128, 128], F32) for _ in range(2)])
        a1s = Rot([T.sb("a1", [128, 128], F32) for _ in range(2)])
        rds = Rot([T.sb("rd", [128, 4], F32) for _ in range(4)])
        junk, Bjunk = T.sb("junkb", [128, 128], BF16)

        def load_head(h):
            KT, BKT = KTs.next()
            QT, BQT = QTs.next()
            V, BV = Vs.next()
            kb.dma("SP", QT[:], self.FT[h * 128:(h + 1) * 128, :], writes=[BQT])
            kb.dma("SP", KT[:], self.FT[1024 + h * 128:1024 + (h + 1) * 128, :], writes=[BKT])
            tmv = self.TM[:, h * 128:(h + 1) * 128].rearrange("(c p) d -> p c d", p=128)
            for c4 in range(4):
                kb.dma("SP", V[:, c4 * 8:(c4 + 1) * 8, 0:128], tmv[:, c4 * 8:(c4 + 1) * 8, :], writes=[BV])
            return (KT, BKT, QT, BQT, V, BV)

        nxt = load_head(0)
        for h in range(8):
            KT, BKT, QT, BQT, V, BV = nxt
            if h + 1 < 8:
                nxt = load_head(h + 1)
            for qt in range(8):
                pairs = self.causal_pairs(qt)
                for c in range(2):
                    def o_off(O, s, c=c):
                        return Os[c][s // 2][0][:, (s % 2) * 256:(s % 2) * 256 + 129]
                    hooks = None
                    if c == 0 and pending:
                        e2_, e3_ = pending.pop(0)
                        hooks = {min(6, len(pairs) - 1): e2_, 10 ** 6: e3_}
                    self.attn_qtile(R, QT[64 * c:64 * c + 64, qt * 512:(qt + 1) * 512], BQT,
                                    lambda kc, c=c: (KT[64 * c:64 * c + 64, kc * 128:(kc + 1) * 128], 128), BKT,
                                    lambda kc: V[:, kc, :], BV, 129, pairs, None, Os[c][0][1], o_off, bank_of=lambda s: s // 2, hooks=hooks)
                oset = osbs.next()
                for c in range(2):
                    for b_ in range(2):
                        kb.op("DVE", lambda e, c=c, b_=b_: e.tensor_copy(oset[c][b_][0][:, 0:385], Os[c][b_][0][:, 0:385]),
                              reads=[Os[c][0][1]], writes=[oset[c][b_][1]])

                ep = dict(oset=oset, h=h, qt=qt)
                a1t, Ba1t = a1ts.next()
                rdt, Brdt = rdts.next()
                ep.update(a1t=a1t, Ba1t=Ba1t, rdt=rdt, Brdt=Brdt)

                def e1(ep=ep):
                    oset, a1t, Ba1t, rdt, Brdt = ep["oset"], ep["a1t"], ep["Ba1t"], ep["rdt"], ep["Brdt"]
                    for s in range(4):
                        O0 = oset[0][s // 2][0][:, (s % 2) * 256:(s % 2) * 256 + 129]
                        O1 = oset[1][s // 2][0][:, (s % 2) * 256:(s % 2) * 256 + 129]
                        BO0, BO1 = oset[0][s // 2][1], oset[1][s // 2][1]
                        a0, Ba0 = a0s.next()
                        kb.op("DVE", lambda e, s=s, O0=O0: e.reciprocal(rdt[:, s, 0:1], O0[:, 128:129]), reads=[BO0], writes=[Brdt])
                        kb.op("DVE", lambda e, s=s, O1=O1: e.reciprocal(rdt[:, s, 1:2], O1[:, 128:129]), reads=[BO1], writes=[Brdt])
                        kb.op("DVE", lambda e, s=s: e.tensor_tensor(rdt[:, s, 1:2], rdt[:, s, 1:2], nlam[:], ALU.mult), reads=[Brdt, Bnlam], writes=[Brdt])
                        kb.op("DVE", lambda e, s=s, a0=a0, O0=O0: e.tensor_scalar(a0[:], O0[:, 0:128], rdt[:, s, 0:1], None, ALU.mult), reads=[BO0, Brdt], writes=[Ba0])
                        kb.op("DVE", lambda e, s=s, a0=a0, O1=O1: e.scalar_tensor_tensor(a1t[:, s, :], O1[:, 0:128], rdt[:, s, 1:2], a0[:], ALU.mult, ALU.add),
                              reads=[BO1, Brdt, Ba0], writes=[Ba1t])

                def e2(ep=ep):
                    a1t, Ba1t, rdt, Brdt = ep["a1t"], ep["Ba1t"], ep["rdt"], ep["Brdt"]
                    for s in range(4):
                        kb.op("ACT", lambda e, s=s: e.activation(junk[:], a1t[:, s, :], AF.Square, accum_out=rdt[:, s, 2:3]), reads=[Ba1t], writes=[Bjunk, Brdt])
                    kb.op("ACT", lambda e: e.activation(rdt[:, :, 2:3], rdt[:, :, 2:3], AF.Sqrt, bias=self.epsb[:], scale=1.0 / 128),
                          reads=[Brdt, self.Bepsb], writes=[Brdt])

                def e3(ep=ep):
                    a1t, Ba1t, rdt, Brdt, h, qt = ep["a1t"], ep["Ba1t"], ep["rdt"], ep["Brdt"], ep["h"], ep["qt"]
                    ost, Bost = osts.next()
                    kb.op("DVE", lambda e: e.reciprocal(rdt[:, :, 3:4], rdt[:, :, 2:3]), reads=[Brdt], writes=[Brdt])
                    for s in range(4):
                        kb.op("DVE", lambda e, s=s, ost=ost: e.scalar_tensor_tensor(ost[:, s, :], a1t[:, s, :], rdt[:, s, 3:4], go[:], ALU.mult, ALU.mult),
                              reads=[Ba1t, Brdt, Bgo], writes=[Bost])
                    osv = self.OS[qt * 512:(qt + 1) * 512, h * 128:(h + 1) * 128].rearrange("(s p) d -> p s d", p=128)
                    kb.dma("POOL", osv, ost[:], reads=[Bost])
                e1()
                pending.append((e2, e3))
        while pending:
            e2_, e3_ = pending.pop(0)
            e2_()
            e3_()
        T.close()

    def phase_c1(self, li, x_src, w_out_ap):
        kb, I = self.kb, self.I
        T = Scope(kb)
        W, BW = T.sb("w_out", [128, 8, D], BF16)
        self.load_w_bf16(W, BW, w_out_ap, 8)
        ots = Rot([T.sb("ot", [128, D], BF16) for _ in range(2)])
        xts = Rot([T.sb("xt", [128, D], F32) for _ in range(2)])
        oTs = Rot([T.sb("oT", [128, 8, 128], BF16) for _ in range(2)])
        x1s = Rot([T.sb("x1", [128, D], F32) for _ in range(2)])
        hbs = Rot([T.sb("hb", [128, D], BF16) for _ in range(2)])
        ytmp, Bytmp = T.sb("ytmp", [128, D], F32)
        junk, Bjunk = T.sb("junk", [128, D], BF16)
        sss = Rot([T.sb("ss", [128, 1], F32) for _ in range(2)])
        hn, Bhn = T.sb("hn", [128, D], F32)
        stg = [T.sb("stage", [128, 8, 512], BF16) for _ in range(2)]
        pts = Rot([T.ps("ptA", [128, 1024], BF16) for _ in range(2)])
        pys = Rot([T.ps("py", [128, 512], F32) for _ in range(4)])

        def load(t):
            ot, Bot = ots.next()
            xt, Bxt = xts.next()
            kb.dma("SP", ot[:], self.OS[t * 128:(t + 1) * 128, :], writes=[Bot])
            kb.dma("SP", xt[:], x_src[t * 128:(t + 1) * 128, :], writes=[Bxt])
            return ot, Bot, xt, Bxt

        nxt = load(0)
        for t in range(NT):
            ot, Bot, xt, Bxt = nxt
            if t + 1 < NT:
                nxt = load(t + 1)
            oT, BoT = oTs.next()
            pt, Bpt = pts.next()
            self.transpose8(ot, Bot, pt, Bpt, oT[:], BoT, eng="ACT")
            x1, Bx1 = x1s.next()
            for g in range(2):
                py, Bpy = pys.next()
                for kc in range(8):
                    kb.op("PE", lambda e, kc=kc: e.matmul(py[:], oT[:, kc, :], W[:, kc, g * 512:(g + 1) * 512], start=(kc == 0), stop=(kc == 7)),
                          reads=[BoT, BW], writes=[Bpy])
                sl = slice(g * 512, (g + 1) * 512)
                kb.op("DVE", lambda e: e.tensor_tensor(ytmp[:, sl], py[:], self.mod[:, 2, sl], ALU.mult), reads=[Bpy, self.Bmod], writes=[Bytmp])
                kb.op("POOL", lambda e: e.tensor_tensor(x1[:, sl], ytmp[:, sl], xt[:, sl], ALU.add), reads=[Bytmp, Bxt], writes=[Bx1])
            kb.dma("POOL", self.x1s[t * 128:(t + 1) * 128, :], x1[:], reads=[Bx1])
            hb, Bhb = hbs.next()
            ss, Bss = sss.next()
            self.rms_mod_tile(T, x1, Bx1, hb, Bhb, 4, 3, (junk, Bjunk, ss, Bss, hn, Bhn))
            pt, Bpt = pts.next()
            tt = t % 4
            st, Bst = stg[(t // 4) % 2]
            self.transpose8(hb, Bhb, pt, Bpt, st[:, :, tt * 128:(tt + 1) * 128], Bst, eng="ACT")
            if tt == 3:
                t0 = (t - 3) * 128
                h2v = self.H2T.rearrange("(j p) t -> p j t", p=128)
                kb.dma("POOL", h2v[:, :, t0:t0 + 512], st[:], reads=[Bst])
        T.close()

    def phase_c2(self, li, x_dst):
        kb, I = self.kb, self.I
        T = Scope(kb)
        Wg, BWg = T.sb("wg", [128, 8, FFN], BF16)
        Wu, BWu = T.sb("wu", [128, 8, FFN], BF16)
        Wd, BWd = T.sb("wd", [128, NJ, D], BF16)
        self.load_w_bf16(Wg, BWg, I["ffn_w_gate"][li], 8)
        self.load_w_bf16(Wu, BWu, I["ffn_w_up"][li], 8)
        self.load_w_bf16(Wd, BWd, I["ffn_w_down"][li], NJ)
        TT = 256
        h2s = Rot([T.sb("h2T", [128, 8, TT], BF16) for _ in range(2)])
        act, Bact = T.sb("act", [128, NJ, TT], BF16)
        sgs = Rot([T.sb("sg", [128, TT], F32) for _ in range(2)])
        xts = Rot([T.sb("x1t", [128, D], F32) for _ in range(2)])
        ytmp, Bytmp = T.sb("ytmp", [128, 512], F32)
        pgs = Rot([T.ps("pg", [128, 512], F32) for _ in range(2)])
        pus = Rot([T.ps("pu", [128, 512], F32) for _ in range(2)])
        pys = Rot([T.ps("py", [128, 512], F32) for _ in range(3)])
        h2v = self.H2T.rearrange("(j p) t -> p j t", p=128)

        def load(st):
            h2, Bh2 = h2s.next()
            kb.dma("SP", h2[:], h2v[:, :, st * TT:(st + 1) * TT], writes=[Bh2])
            return h2, Bh2

        nxt = load(0)
        for st in range(S // TT):
            h2, Bh2 = nxt
            if st + 1 < S // TT:
                nxt = load(st + 1)
            for j in range(NJ):
                pg, Bpg = pgs.next()
                pu, Bpu = pus.next()
                for kc in range(8):
                    kb.op("PE", lambda e, kc=kc: e.matmul(pg[:, 0:TT], Wg[:, kc, j * 128:(j + 1) * 128], h2[:, kc, :], start=(kc == 0), stop=(kc == 7)),
                          reads=[BWg, Bh2], writes=[Bpg])
                for kc in range(8):
                    kb.op("PE", lambda e, kc=kc: e.matmul(pu[:, 0:TT], Wu[:, kc, j * 128:(j + 1) * 128], h2[:, kc, :], start=(kc == 0), stop=(kc == 7)),
                          reads=[BWu, Bh2], writes=[Bpu])
                sg, Bsg = sgs.next()
                kb.op("ACT", lambda e: e.activation(sg[:], pg[:, 0:TT], AF.Silu), reads=[Bpg], writes=[Bsg])
                kb.op("DVE", lambda e, j=j: e.tensor_tensor(act[:, j, :], sg[:], pu[:, 0:TT], ALU.mult), reads=[Bsg, Bpu], writes=[Bact])
            for q in range(TT // 128):
                t = st * (TT // 128) + q
                xt, Bxt = xts.next()
                kb.dma("SP", xt[:], self.x1s[t * 128:(t + 1) * 128, :], writes=[Bxt])
                for g in range(2):
                    py, Bpy = pys.next()
                    for j in range(NJ):
                        kb.op("PE", lambda e, j=j: e.matmul(py[:], act[:, j, q * 128:(q + 1) * 128], Wd[:, j, g * 512:(g + 1) * 512],
                                                            start=(j == 0), stop=(j == NJ - 1)),
                              reads=[Bact, BWd], writes=[Bpy])
                    sl = slice(g * 512, (g + 1) * 512)
                    kb.op("DVE", lambda e: e.tensor_tensor(ytmp[:], py[:], self.mod[:, 5, sl], ALU.mult), reads=[Bpy, self.Bmod], writes=[Bytmp])
                    kb.op("POOL", lambda e: e.tensor_tensor(xt[:, sl], ytmp[:], xt[:, sl], ALU.add), reads=[Bytmp, Bxt], writes=[Bxt])
                kb.dma("POOL", x_dst[t * 128:(t + 1) * 128, :], xt[:], reads=[Bxt])
        T.close()

    def build(self):
        kb = self.kb
        self.declare()
        self.setup()
        G = self.G
        self.epsb, self.Bepsb = G.sb("epsb", [128, 1], F32)
        kb.op("POOL", lambda e: e.memset(self.epsb[:], EPS), writes=[self.Bepsb])
        layers = self.cfg.get("layers", [0, 1])
        x_src = self.I["x"]
        for n, li in enumerate(layers):
            x_dst = self.out if n == len(layers) - 1 else self.xmid
            self.L = Scope(kb)
            if li == 0:
                self.gate_sb, self.Bgate = self.L.sb("gates", [128, NT, 24], F32)
            self.compute_mod(li)
            stop = self.cfg.get("stop")
            if stop == "mod":
                self.L.close()
                break
            if li == 0:
                self.phase_a(0, x_src)
                if stop == "A":
                    self.L.close()
                    break
                if not self.cfg.get("skip_moba"):
                    self.moba_part()
                if stop == "moba":
                    self.L.close()
                    break
                self.nsa_part()
                if stop in ("nsa", "cmpmlp"):
                    self.L.close()
                    break
                self.phase_c1(0, x_src, self.I["sp_w_out"])
            else:
                self.phase_a(1, x_src)
                if stop == "A":
                    self.L.close()
                    break
                self.phase_b_diff(0)
                if stop == "B":
                    self.L.close()
                    break
                self.phase_c1(1, x_src, self.I["diff_w_out"])
            if stop == "C1":
                self.L.close()
                break
            self.phase_c2(li, x_dst)
            self.L.close()
            x_src = x_dst
        kb.barrier(engines=("POOL",))
        G.es.close()
        kb.es.close()
        return self.nc

    def epi_norm(self, T, O_ap, BO, vcol, rd_ap, Brd):
        kb = self.kb
        kb.op("DVE", lambda e: e.tensor_scalar(rd_ap, O_ap[:, vcol:vcol + 1], 1e-30, None, ALU.max), reads=[BO], writes=[Brd])
        kb.op("DVE", lambda e: e.reciprocal(rd_ap, rd_ap), reads=[Brd], writes=[Brd])

    def phase_b_sparse(self):
        self.moba_part()
        self.nsa_part()

    def moba_part(self):
        kb, I = self.kb, self.I
        T = Scope(kb)
        QAs = Rot([T.sb("QA", [128, 2, S], BF16) for _ in range(2)])
        KAs = Rot([T.sb("KA", [128, 2, S], BF16) for _ in range(2)])
        Vps = Rot([T.sb("Vp", [128, NT, 2, 65], BF16) for _ in range(2)])
        for (ka, Bka) in KAs.items:
            for hh in range(2):
                kb.dma("POOL", ka[64:80, hh, :], I["c_e16"][:, :], writes=[Bka])
        for (v, Bv) in Vps.items:
            kb.op("POOL", lambda e, v=v: e.memset(v[:, :, :, 64:65], 1.0), writes=[Bv])
        t1, Bt1 = T.sb("t1", [128, 16, 16], F32)
        t2, Bt2 = T.sb("t2", [128, 16, 16], F32)
        kb.dma("SP", t1[:].rearrange("p a b -> p (a b)"), bc_row(I["c_t1"][0:1, :], 256), writes=[Bt1])
        kb.dma("SP", t2[:].rearrange("p a b -> p (a b)"), bc_row(I["c_t2"][0:1, :], 256), writes=[Bt2])
        augs = Rot([T.sb("aug", [128, 128], BF16) for _ in range(2)])
        for (a, Ba) in augs.items:
            kb.op("POOL", lambda e, a=a: e.memset(a[:], 0.0), writes=[Ba])
        km, Bkm = T.sb("km", [64, 16], F32)
        kmb, Bkmb = T.sb("kmb", [64, 16], BF16)
        gms = Rot([T.sb("gm", [128, 16], F32) for _ in range(2)])
        m8s = Rot([T.sb("m8", [128, 8], F32) for _ in range(2)])
        sels = Rot([T.sb("sel", [128, 16], F32) for _ in range(2)])
        R = dict(pS=Rot([T.ps("pS", [128, 512], F32) for _ in range(3)]),
                 pT=Rot([T.sb("pT", [128, 512], BF16) for _ in range(8)]))
        Obs = Rot([T.ps("OTm", [128, 512], F32) for _ in range(2)])
        Otk, BOtk = T.ps("Otok", [128, 512], F32)
        osbs = Rot([T.sb("osb", [128, 512], F32) for _ in range(2)])
        pgs = Rot([T.ps("pgate", [128, 512], F32) for _ in range(1)])
        ptr = Rot([T.ps("ptr", [128, 1024], BF16) for _ in range(1)])
        osts = Rot([T.sb("ost", [128, 4, 128], BF16) for _ in range(2)])
        rds = Rot([T.sb("rd", [128, 1], F32) for _ in range(4)])

        def load_pair(hp):
            QA, BQA = QAs.next()
            KA, BKA = KAs.next()
            Vp, BVp = Vps.next()
            for hh in range(2):
                h = 2 * hp + hh
                kb.dma("SP", QA[0:64, hh, :], self.FT[h * 64:(h + 1) * 64, :], writes=[BQA])
                kb.dma("SP", KA[0:64, hh, :], self.FT[512 + h * 64:512 + (h + 1) * 64, :], writes=[BKA])
            for hh in range(2):
                h = 2 * hp + hh
                tmv = self.TM[:, h * 64:(h + 1) * 64].rearrange("(c p) d -> p c d", p=128)
                for c4 in range(4):
                    kb.dma("SP", Vp[:, c4 * 8:(c4 + 1) * 8, hh, 0:64], tmv[:, c4 * 8:(c4 + 1) * 8, :], writes=[BVp])
            return QA, BQA, KA, BKA, Vp, BVp

        kms = [T.sb("km2", [64, 16], F32) for _ in range(2)]
        kmbs = [T.sb("kmb2", [64, 16], BF16) for _ in range(2)]
        pgt, _ = pgs.items[0]
        pg_slots = Rot([(pgt[:, j * 16:(j + 1) * 16], Buf("pgs%d" % j)) for j in range(8)])
        ptt, _ = ptr.items[0]
        pt_slots = Rot([(ptt[:, j * 128:(j + 1) * 128], Buf("pts%d" % j)) for j in range(8)])
        augs8 = Rot([T.sb("aug8", [128, 128], BF16) for _ in range(8)])
        for (a_, Ba_) in augs8.items:
            kb.op("POOL", lambda e, a_=a_: e.memset(a_[:], 0.0), writes=[Ba_])
        gms8 = Rot([T.sb("gm8", [128, 16], F32) for _ in range(8)])
        m8s8 = Rot([T.sb("m88", [128, 8], F32) for _ in range(8)])
        sels8 = Rot([T.sb("sel8", [128, 16], F32) for _ in range(8)])

        def gating_jobs(QA, BQA, KA, BKA):
            def prep():
                for hh in range(2):
                    km_, Bkm_ = kms[hh]
                    kmb_, Bkmb_ = kmbs[hh]
                    kb.op("DVE", lambda e, hh=hh, km_=km_: e.tensor_reduce(km_[:], KA[0:64, hh, :].rearrange("p (b j) -> p b j", j=256), AX.X, ALU.add),
                          reads=[BKA], writes=[Bkm_])
                    kb.op("DVE", lambda e, km_=km_, kmb_=kmb_: e.tensor_scalar(kmb_[:], km_[:], 1.0 / 256, None, ALU.mult), reads=[Bkm_], writes=[Bkmb_])
            jobs = []
            for hh in range(2):
                for t in range(NT):
                    st = {}

                    def part1(hh=hh, t=t, st=st):
                        own = t // 2
                        kmb_, Bkmb_ = kmbs[hh]
                        pg, Bpg = pg_slots.next()
                        kb.op("PE", lambda e: e.matmul(pg, QA[0:64, hh, t * 128:(t + 1) * 128], kmb_[:], start=True, stop=True),
                              reads=[BQA, Bkmb_], writes=[Bpg])
                        gm, Bgm = gms8.next()
                        m8, Bm8 = m8s8.next()
                        sel, Bsel = sels8.next()
                        aug, Baug = augs8.next()
                        kb.op("DVE", lambda e: e.tensor_tensor(gm[:], pg, t1[:, own, :], ALU.add), reads=[Bpg, Bt1], writes=[Bgm])
                        kb.op("DVE", lambda e: e.max(m8[:], gm[:]), reads=[Bgm], writes=[Bm8])
                        kb.op("DVE", lambda e: e.tensor_scalar(m8[:, 2:3], m8[:, 2:3], -1e29, None, ALU.max), reads=[Bm8], writes=[Bm8])
                        kb.op("DVE", lambda e: e.tensor_scalar(sel[:], gm[:], m8[:, 2:3], None, ALU.is_ge), reads=[Bgm, Bm8], writes=[Bsel])
                        kb.op("DVE", lambda e: e.tensor_tensor(sel[:], sel[:], t2[:, own, :], ALU.max), reads=[Bsel, Bt2], writes=[Bsel])
                        kb.op("DVE", lambda e: e.tensor_scalar(aug[:, 64:80], sel[:], -1.0, -MASKV, ALU.add, ALU.mult), reads=[Bsel], writes=[Baug])
                        st["aug"] = (aug, Baug)

                    def part2(hh=hh, t=t, st=st):
                        aug, Baug = st["aug"]
                        pt, Bpt = pt_slots.next()
                        kb.op("PE", lambda e: e.transpose(pt, aug[:], self.ident[:]), reads=[Baug, self.Bident], writes=[Bpt])
                        kb.op("ACT", lambda e: e.copy(QA[64:80, hh, t * 128:(t + 1) * 128], pt[64:80, :]), reads=[Bpt], writes=[BQA])
                    jobs.append((part1, part2))
            return prep, jobs

        nxt = load_pair(0)
        pending = []
        prep0, jobs0 = gating_jobs(nxt[0], nxt[1], nxt[2], nxt[3])
        prep0()
        for j0 in range(0, len(jobs0), 4):
            for p1, _ in jobs0[j0:j0 + 4]:
                p1()
            for _, p2 in jobs0[j0:j0 + 4]:
                p2()
        for hp in range(4):
            QA, BQA, KA, BKA, Vp, BVp = nxt
            njobs = None
            if hp + 1 < 4:
                nxt = load_pair(hp + 1)
                nprep, njobs = gating_jobs(nxt[0], nxt[1], nxt[2], nxt[3])
            for qt in range(8):
                pairs = self.causal_pairs(qt)
                ost, Bost = osts.next()
                for hh in range(2):
                    OTm, BOTm = Obs.next()
                    hooks = None
                    if njobs is not None:
                        ci = qt * 2 + hh
                        mine = njobs[ci * 4:(ci + 1) * 4]

                        def h1(mine=mine, first=(ci == 0)):
                            if first:
                                nprep()
                            for p1, _ in mine:
                                p1()

                        def h2(mine=mine):
                            for _, p2 in mine:
                                p2()
                        hooks = {0: h1, 6: h2}
                    self.attn_qtile(R, QA[0:80, hh, qt * 512:(qt + 1) * 512], BQA,
                                    lambda kc: (KA[0:80, hh, kc * 128:(kc + 1) * 128], 128), BKA,
                                    lambda kc: Vp[:, kc, hh, :], BVp, 65, pairs, OTm, BOTm, None, vstat=True, hooks=hooks)
                    def epilogue(OTm=OTm, BOTm=BOTm, ost=ost, Bost=Bost, hh=hh, qt=qt, hp=hp):
                        osb, Bosb = osbs.next()
                        self.ot_to_tok(OTm, BOTm, 65, osb, Bosb, Otk, BOtk)
                        Om, BOm = Otk, BOtk
                        for s in range(4):
                            rd, Brd = rds.next()
                            Os_ = Om[:, s * 128:s * 128 + 65]
                            self.epi_norm(T, Os_, BOm, 64, rd[:], Brd)
                            kb.op("DVE", lambda e, s=s, Os_=Os_, rd=rd: e.tensor_scalar(ost[:, s, hh * 64:(hh + 1) * 64], Os_[:, 0:64], rd[:, 0:1], None, ALU.mult),
                                  reads=[BOm, Brd], writes=[Bost])
                        if hh == 1:
                            osv = self.OS[qt * 512:(qt + 1) * 512, hp * 128:(hp + 1) * 128].rearrange("(s p) d -> p s d", p=128)
                            kb.dma("POOL", osv, ost[:], reads=[Bost])
                    if pending:
                        pending.pop()()
                    pending.append(epilogue)
        if pending:
            pending.pop()()
        T.close()

    def nsa_part(self):
        kb, I = self.kb, self.I
        for _ in range(self.cfg.get("pad_dve", 0)):
            kb.op("DVE", lambda e: e.memset(self.epsb[:], EPS), writes=[self.Bepsb])
        P = Scope(kb)
        CKc, BCKc = P.sb("CKc", [64, 2, 256], BF16)
        VCs = [P.sb("VC", [128, 2, 129], BF16) for _ in range(2)]
        kb.op("POOL", lambda e: e.memset(CKc[:], 0.0), writes=[BCKc])
        for g in range(2):
            vc, Bvc = VCs[g]
            for c in range(2):
                kb.dma("POOL", vc[:, c, 64:129], I["c_ov"][c * 128:(c + 1) * 128, :], writes=[Bvc])
        T = Scope(kb)
        CX = [T.sb("CX", [128, S], BF16) for _ in range(2)]
        kb.dma("SP", CX[0][0][:], self.FT[1536:1664, :], writes=[CX[0][1]])
        kb.dma("SP", CX[1][0][:], self.FT[1664:1792, :], writes=[CX[1][1]])
        w1, Bw1 = T.sb("w1", [128, 2 * 32 * 256], BF16)
        w1src = I["cmp_w1"].rearrange("d a l e -> d (a l e)")
        for half in range(2):
            for a in range(2):
                kb.dma("POOL", w1[half * 64:(half + 1) * 64, a * 8192:(a + 1) * 8192], w1src[:, a * 8192:(a + 1) * 8192], writes=[Bw1])
        w1v = w1[:].rearrange("p (a l e) -> p a l e", a=2, l=32)
        posb, Bposb = T.sb("posb", [64, 2, 34], BF16)
        kb.op("POOL", lambda e: e.memset(posb[:], 0.0), writes=[Bposb])
        kb.dma("POOL", posb[:, :, 0:32], I["cmp_posT"][:, :, :], writes=[Bposb])
        w2, Bw2 = T.sb("w2", [128, 2, 2, 64], BF16)
        for kv in range(2):
            for eh in range(2):
                kb.dma("POOL", w2[:, kv, eh, :], I["cmp_w2"][kv, eh * 128:(eh + 1) * 128, :], writes=[Bw2])
        b1, Bb1 = T.sb("b1", [128, 4], F32)
        pbs = Rot([T.ps("pb", [128, 512], F32) for _ in range(2)])
        phs = Rot([T.ps("ph", [128, 512], F32) for _ in range(3)])
        for kv in range(2):
            for eh in range(2):
                pb, Bpb = pbs.next()
                for l in range(32):
                    kb.op("PE", lambda e, l=l: e.matmul(pb[:, 0:2], w1v[0:64, kv, l, eh * 128:(eh + 1) * 128],
                                                        posb[0:64, kv, l:l + 2], start=(l == 0), stop=(l == 31)),
                          reads=[Bw1, Bposb], writes=[Bpb])
                kb.op("DVE", lambda e: e.tensor_copy(b1[:, kv * 2 + eh:kv * 2 + eh + 1], pb[:, 0:1]), reads=[Bpb], writes=[Bb1])
        hid = {}
        for kv in range(2):
            cx, Bcx = CX[kv]
            cxv = cx[:].rearrange("p (n j) -> p n j", j=16)
            for g in range(2):
                for eh in range(2):
                    ph, Bph = phs.next()
                    for l in range(32):
                        a = l // 16
                        kb.op("PE", lambda e, l=l, a=a: e.matmul(ph[:, 0:255], w1v[64 * g:64 * g + 64, kv, l, eh * 128:(eh + 1) * 128],
                                                                 cxv[64 * g:64 * g + 64, a:a + 255, l % 16], start=(l == 0), stop=(l == 31)),
                              reads=[Bw1, Bcx], writes=[Bph])
                    ht, Bht = T.sb("hid", [128, 256], BF16)
                    kb.op("POOL", lambda e: e.memset(ht[:], 0.0), writes=[Bht])
                    kb.op("ACT", lambda e: e.activation(ht[:, 0:255], ph[:, 0:255], AF.Silu, bias=b1[:, kv * 2 + eh:kv * 2 + eh + 1]),
                          reads=[Bph, Bb1], writes=[Bht])
                    hid[(kv, g, eh)] = (ht, Bht)
        for g in range(2):
            ph, Bph = phs.next()
            for eh in range(2):
                ht, Bht = hid[(0, g, eh)]
                kb.op("PE", lambda e: e.matmul(ph[0:64, 0:255], w2[:, 0, eh, :], ht[:, 0:255], start=(eh == 0), stop=(eh == 1)),
                      reads=[Bw2, Bht], writes=[Bph])
            kb.op("ACT", lambda e: e.copy(CKc[:, g, 0:255], ph[0:64, 0:255]), reads=[Bph], writes=[BCKc])
            vc, Bvc = VCs[g]
            for c in range(2):
                ph, Bph = phs.next()
                for eh in range(2):
                    ht, Bht = hid[(1, g, eh)]
                    kb.op("PE", lambda e: e.matmul(ph[:, 0:64], ht[:, c * 128:(c + 1) * 128], w2[:, 1, eh, :], start=(eh == 0), stop=(eh == 1)),
                          reads=[Bw2, Bht], writes=[Bph])
                kb.op("ACT", lambda e: e.copy(vc[:, c, 0:64], ph[:, 0:64]), reads=[Bph], writes=[Bvc])
        T.close()
        if self.cfg.get("stop") == "cmpmlp":
            if self.cfg.get("debug"):
                dck = self.tap("d_ckc", [64, 512], BF16)
                kb.dma("SP", dck[:, :], CKc[:].rearrange("p g n -> p (g n)"), reads=[BCKc])
                for g in range(2):
                    dvc = self.tap("d_vc%d" % g, [128, 258], BF16)
                    kb.dma("SP", dvc[:, :], VCs[g][0][:].rearrange("p c n -> p (c n)"), reads=[VCs[g][1]])
            P.close()
            return

        T = Scope(kb)
        nfv, Bnfv = T.sb("nfv", [128, NT, 64], F32)
        cst, Bcst = T.sb("cst", [128, NT, 64], F32)
        v01, Bv01 = T.sb("v01", [128, NT, 64], F32)
        kb.dma("SP", nfv[:], I["c_nfv"][:, :, :], writes=[Bnfv])
        kb.dma("SP", cst[:], I["c_cst"][:, :, :], writes=[Bcst])
        kb.dma("SP", v01[:], I["c_v01"][:, :, :], writes=[Bv01])
        QA4, BQA4 = T.sb("QA4", [128, 4, S], BF16)
        KS, BKS = T.sb("KS", [128, S], BF16)
        KW, BKW = T.sb("KW", [64, S], BF16)
        VS, BVS = T.sb("VS", [128, NT, 65], BF16)
        VW, BVW = T.sb("VW", [128, NT, 65], BF16)
        kb.op("POOL", lambda e: e.memset(VS[:, :, 64:65], 1.0), writes=[BVS])
        kb.op("POOL", lambda e: e.memset(VW[:, :, 64:65], 1.0), writes=[BVW])
        kb.dma("POOL", KS[64:128, :], I["c_e64"][:, :], writes=[BKS])
        augs = Rot([T.sb("aug", [128, 128], BF16) for _ in range(2)])
        for (a, Ba) in augs.items:
            kb.op("POOL", lambda e, a=a: e.memset(a[:], 0.0), writes=[Ba])
        R = dict(pS=Rot([T.ps("pS", [128, 512], F32) for _ in range(2)]),
                 pT=Rot([T.sb("pT", [128, 512], BF16) for _ in range(8)]))
        Oc = [T.ps("Oc", [128, 512], F32) for _ in range(2)]
        BOc = Oc[0][1]
        Osw = Rot([T.ps("OTsw", [128, 512], F32) for _ in range(2)])
        Otk, BOtk = T.ps("Otok", [128, 512], F32)
        osbs = Rot([T.sb("osb", [128, 512], F32) for _ in range(2)])
        ptr = Rot([T.ps("ptr", [128, 1024], BF16) for _ in range(1)])
        occ, Bocc = T.sb("occ", [128, 4, 4, 64], F32)
        imp, Bimp = T.sb("imp", [128, 4, 64], F32)
        itmp, Bitmp = T.sb("itmp", [128, 64], F32)
        sc, Bsc = T.sb("sc", [128, 64], F32)
        sc2, Bsc2 = T.sb("sc2", [128, 64], F32)
        m8a, Bm8a = T.sb("m8a", [128, 8], F32)
        m8b, Bm8b = T.sb("m8b", [128, 8], F32)
        selm, Bselm = T.sb("selm", [128, 64], F32)
        rds = Rot([T.sb("rd", [128, 2], F32) for _ in range(6)])
        osts = Rot([T.sb("ost", [128, 4, 256], BF16) for _ in range(2)])
        gsb = self.gate_sb

        for g in range(2):
            for r in range(4):
                h = 4 * g + r
                kb.dma("SP", QA4[0:64, r, :], self.FT[1024 + h * 64:1024 + (h + 1) * 64, :], writes=[BQA4])
            kb.dma("SP", KS[0:64, :], self.FT[1792 + 64 * g:1792 + 64 * g + 64, :], writes=[BKS])
            kb.dma("SP", KW[:], self.FT[1920 + 64 * g:1920 + 64 * g + 64, :], writes=[BKW])
            tms = self.TM[:, 512 + 64 * g:512 + 64 * g + 64].rearrange("(c p) d -> p c d", p=128)
            tmw = self.TM[:, 640 + 64 * g:640 + 64 * g + 64].rearrange("(c p) d -> p c d", p=128)
            for c4 in range(4):
                kb.dma("SP", VS[:, c4 * 8:(c4 + 1) * 8, 0:64], tms[:, c4 * 8:(c4 + 1) * 8, :], writes=[BVS])
                kb.dma("SP", VW[:, c4 * 8:(c4 + 1) * 8, 0:64], tmw[:, c4 * 8:(c4 + 1) * 8, :], writes=[BVW])
            vc, Bvc = VCs[g]
            parts = self.cfg.get("nsa_parts", ("cmp", "select", "sw"))
            for qt in self.cfg.get("nsa_qts", range(8)):
                ost, Bost = osts.next()
                cchunks = [0] + ([1] if qt >= 4 else [])
                cpairs = [(c, [(s, "full") for s in range(4)]) for c in cchunks]

                def cmp_mask(c, pT, BpT, krows):
                    kb.op("POOL", lambda e: e.affine_select(pT[:], pT[:], [[1, 512]], ALU.is_ge, 0.0,
                                                            base=512 * qt - 2048 * c - 31, channel_multiplier=-16),
                          reads=[BpT], writes=[BpT])

                for r in (range(4) if "cmp" in parts else ()):
                    h = 4 * g + r
                    self.attn_qtile(R, QA4[0:64, r, qt * 512:(qt + 1) * 512], BQA4,
                                    lambda c: (CKc[:, g, c * 128:(c + 1) * 128], 128), BCKc,
                                    lambda c: vc[:, c, :], Bvc, 129, cpairs, None, BOc,
                                    lambda O, s: Oc[s // 2][0][:, (s % 2) * 256:(s % 2) * 256 + 129], post_exp=cmp_mask, bank_of=lambda s: s // 2)
                    for s in range(4):
                        t = 4 * qt + s
                        O_ = Oc[s // 2][0][:, (s % 2) * 256:(s % 2) * 256 + 129]
                        rd, Brd = rds.next()
                        self.epi_norm(T, O_, BOc, 64, rd[:, 0:1], Brd)
                        if r == 0:
                            kb.op("DVE", lambda e, s=s: e.tensor_scalar(imp[:, s, :], O_[:, 65:129], rd[:, 0:1], None, ALU.mult),
                                  reads=[BOc, Brd], writes=[Bimp])
                        else:
                            kb.op("DVE", lambda e: e.tensor_scalar(itmp[:], O_[:, 65:129], rd[:, 0:1], None, ALU.mult),
                                  reads=[BOc, Brd], writes=[Bitmp])
                            kb.op("DVE", lambda e, s=s: e.tensor_tensor(imp[:, s, :], imp[:, s, :], itmp[:], ALU.add),
                                  reads=[Bitmp, Bimp], writes=[Bimp])
                        kb.op("DVE", lambda e: e.tensor_tensor(rd[:, 1:2], rd[:, 0:1], gsb[:, t, h:h + 1], ALU.mult), reads=[Brd, self.Bgate], writes=[Brd])
                        kb.op("DVE", lambda e, s=s, r=r: e.tensor_scalar(occ[:, r, s, :], O_[:, 0:64], rd[:, 1:2], None, ALU.mult),
                              reads=[BOc, Brd], writes=[Bocc])
                for s in (range(4) if "select" in parts else ()):
                    t = 4 * qt + s
                    aug, Baug = augs.next()
                    kb.op("DVE", lambda e: e.tensor_tensor(sc[:], imp[:, s, :], nfv[:, t, :], ALU.mult), reads=[Bimp, Bnfv], writes=[Bsc])
                    kb.op("DVE", lambda e: e.tensor_tensor(sc[:], sc[:], cst[:, t, :], ALU.add), reads=[Bsc, Bcst], writes=[Bsc])
                    kb.op("DVE", lambda e: e.max(m8a[:], sc[:]), reads=[Bsc], writes=[Bm8a])
                    kb.op("DVE", lambda e: e.match_replace(sc2[:], m8a[:], sc[:], -1e30), reads=[Bsc, Bm8a], writes=[Bsc2])
                    kb.op("DVE", lambda e: e.max(m8b[:], sc2[:]), reads=[Bsc2], writes=[Bm8b])
                    kb.op("DVE", lambda e: e.tensor_scalar(selm[:], sc[:], m8b[:, 7:8], None, ALU.is_ge), reads=[Bsc, Bm8b], writes=[Bselm])
                    kb.op("DVE", lambda e: e.tensor_tensor(selm[:], selm[:], v01[:, t, :], ALU.mult), reads=[Bselm, Bv01], writes=[Bselm])
                    kb.op("DVE", lambda e: e.tensor_scalar(aug[:, 64:128], selm[:], -1.0, -MASKV, ALU.add, ALU.mult), reads=[Bselm], writes=[Baug])
                    pt, Bpt = ptr.next()
                    kb.op("PE", lambda e: e.transpose(pt[:, 0:128], aug[:], self.ident[:]), reads=[Baug, self.Bident], writes=[Bpt])
                    for r in range(4):
                        kb.op("ACT", lambda e, r=r: e.copy(QA4[64:128, r, t * 128:(t + 1) * 128], pt[64:128, 0:128]), reads=[Bpt], writes=[BQA4])
                pending = []
                spairs = self.causal_pairs(qt)
                wpairs = []
                for kc in range(max(0, 4 * qt - 4), 4 * qt + 4):
                    subs = []
                    for s in range(4):
                        dlt = 4 * qt + s - kc
                        if dlt == 0:
                            subs.append((s, "tri"))
                        elif 1 <= dlt <= 3:
                            subs.append((s, "full"))
                        elif dlt == 4:
                            subs.append((s, "atri"))
                    if subs:
                        wpairs.append((kc, subs))
                for r in (range(4) if "sw" in parts else ()):
                    h = 4 * g + r
                    for br, (pairs, qrows, kl, Bkl, vt, Bvt) in enumerate((
                            (spairs, 128, KS, BKS, VS, BVS), (wpairs, 64, KW, BKW, VW, BVW))):
                        if br not in self.cfg.get("nsa_br", (0, 1)):
                            continue
                        OTb, BOTb = Osw.next()
                        self.attn_qtile(R, QA4[0:qrows, r, qt * 512:(qt + 1) * 512], BQA4,
                                        lambda kc, kl=kl, qrows=qrows: (kl[0:qrows, kc * 128:(kc + 1) * 128], 128), Bkl,
                                        lambda kc, vt=vt: vt[:, kc, :], Bvt, 65, pairs, OTb, BOTb, None, vstat=True)
                        def epilogue(OTb=OTb, BOTb=BOTb, br=br, r=r, h=h, qt=qt, ost=ost, Bost=Bost):
                            osb, Bosb = osbs.next()
                            self.ot_to_tok(OTb, BOTb, 65, osb, Bosb, Otk, BOtk)
                            Ob, BOb = Otk, BOtk
                            for s in range(4):
                                t = 4 * qt + s
                                O_ = Ob[:, s * 128:s * 128 + 65]
                                rd, Brd = rds.next()
                                self.epi_norm(T, O_, BOb, 64, rd[:, 0:1], Brd)
                                gcol = (br + 1) * 8 + h
                                kb.op("DVE", lambda e, rd=rd, t=t, gcol=gcol: e.tensor_tensor(rd[:, 1:2], rd[:, 0:1], gsb[:, t, gcol:gcol + 1], ALU.mult),
                                      reads=[Brd, self.Bgate], writes=[Brd])
                                if br == 0:
                                    kb.op("DVE", lambda e, O_=O_, rd=rd: e.tensor_scalar(itmp[:], O_[:, 0:64], rd[:, 1:2], None, ALU.mult),
                                          reads=[BOb, Brd], writes=[Bitmp])
                                    kb.op("DVE", lambda e, s=s: e.tensor_tensor(occ[:, r, s, :], occ[:, r, s, :], itmp[:], ALU.add),
                                          reads=[Bitmp, Bocc], writes=[Bocc])
                                else:
                                    kb.op("DVE", lambda e, s=s, O_=O_, rd=rd: e.scalar_tensor_tensor(ost[:, s, r * 64:(r + 1) * 64], O_[:, 0:64], rd[:, 1:2], occ[:, r, s, :], ALU.mult, ALU.add),
                                          reads=[BOb, Brd, Bocc], writes=[Bost])
                        if pending:
                            pending.pop()()
                        pending.append(epilogue)
                if pending:
                    pending.pop()()
                osv = self.OS[qt * 512:(qt + 1) * 512, 512 + g * 256:512 + (g + 1) * 256].rearrange("(s p) d -> p s d", p=128)
                kb.dma("POOL", osv, ost[:], reads=[Bost])
                if self.cfg.get("nsa_barrier"):
                    kb.barrier()
        T.close()
        P.close()


def _consts():
    c = {}
    c["c_ident"] = np.eye(128, dtype=np.float32)
    k = np.arange(128)[:, None]
    q = np.arange(128)[None, :]
    c["c_tri"] = (q >= k).astype(np.float32)
    c["c_atri"] = (k > q).astype(np.float32)
    key = np.arange(S)[None, :]
    c["c_e16"] = (key // 256 == np.arange(16)[:, None]).astype(np.float32)
    c["c_e64"] = (key // 64 == np.arange(64)[:, None]).astype(np.float32)
    ncmp = 255
    cs = np.arange(ncmp) * 16
    ss = np.arange(64) * 64
    ov = np.minimum(cs[:, None] + 32, ss[None, :] + 64) - np.maximum(cs[:, None], ss[None, :])
    ovp = np.zeros((256, 65), np.float32)
    ovp[:255, 0] = 1.0
    ovp[:255, 1:] = np.clip(ov, 0, None) / 32.0
    c["c_ov"] = ovp
    own = np.arange(16)[:, None]
    blk = np.arange(16)[None, :]
    c["c_t1"] = np.where(blk < own, 0.0, -1e30).astype(np.float32).reshape(1, 256)
    c["c_t2"] = (blk == own).astype(np.float32).reshape(1, 256)
    p = np.arange(128)[:, None, None]
    t = np.arange(NT)[None, :, None]
    m = np.arange(64)[None, None, :]
    cur = (t * 128 + p) // 64
    ok = m <= cur
    forced = ok & ((m == 0) | (m >= cur - 1))
    c["c_nfv"] = (ok & ~forced).astype(np.float32)
    c["c_cst"] = np.where(ok, np.where(forced, 1e4 + m, 0.0), -1e30).astype(np.float32)
    c["c_v01"] = ok.astype(np.float32)
    return c


def _core_inputs(b, inp, consts):
    f = lambda a: np.ascontiguousarray(np.asarray(a), dtype=np.float32)
    m = {}
    m["x"] = f(inp["x"][b])
    m["cT"] = f(np.asarray(inp["c"][b]).reshape(8, 128).T)
    m["posT"] = np.ascontiguousarray(np.asarray(inp["positions"][b]).reshape(NT, 128).T.astype(np.int32))
    for k in ("ada_w", "ada_b", "attn_norm", "ffn_norm", "ffn_w_gate", "ffn_w_up", "ffn_w_down"):
        m[k] = f(inp[k])
    m["sp_w_in"] = f(inp["sp_w_in"][0]); m["sp_w_out"] = f(inp["sp_w_out"][0])
    m["moba_q_norm"] = f(inp["moba_q_norm"]); m["moba_k_norm"] = f(inp["moba_k_norm"])
    m["nsa_q_norm"] = f(inp["nsa_q_norm"]); m["nsa_k_norm"] = f(inp["nsa_k_norm"][0])
    m["cmp_posT"] = f(np.asarray(inp["nsa_cmp_pos"][0]).transpose(2, 0, 1))
    m["cmp_w1"] = f(np.asarray(inp["nsa_cmp_w1"][0]).transpose(2, 0, 1, 3))
    m["cmp_w2"] = f(inp["nsa_cmp_w2"][0])
    m["diff_w_in"] = f(inp["diff_w_in"][0]); m["diff_w_out"] = f(inp["diff_w_out"][0])
    m["diff_q_norm"] = f(inp["diff_q_norm"]); m["diff_k_norm"] = f(inp["diff_k_norm"])
    m["diff_lambda"] = f(np.asarray(inp["diff_lambda"][0]).reshape(1, 256))
    m["diff_out_norm"] = f(inp["diff_out_norm"])
    m.update(consts)
    return m


_NC_CACHE = {}


def kernel(**inputs):
    consts = _consts()
    if "nc" not in _NC_CACHE:
        _NC_CACHE["nc"] = Prog({}).build()
    nc = _NC_CACHE["nc"]
    in_maps = [_core_inputs(b, inputs, consts) for b in range(8)]
    res = run_bass_kernel_spmd(nc, in_maps, core_ids=list(range(8)))
    return np.stack([np.asarray(r["out"], dtype=np.float32) for r in res.results], axis=0)
```

```python
from contextlib import ExitStack
import numpy as np
import concourse.bass as bass
import concourse.mybir as mybir
from concourse.bass_utils import run_bass_kernel_spmd

F32 = mybir.dt.float32
BF16 = mybir.dt.bfloat16
I32 = mybir.dt.int32
AF = mybir.ActivationFunctionType
ALU = mybir.AluOpType
AX = mybir.AxisListType

S = 4096
D = 1024
NT = S // 128
FFN = 2816
NJ = FFN // 128
EPS = 1e-6
MASKV = -30000.0
SP_IN = 2840
N_DMA_SEMS = 24


class Buf:
    __slots__ = ("w", "r", "name")

    def __init__(self, name=""):
        self.w = None
        self.r = {}
        self.name = name


class Rot:
    def __init__(self, items):
        self.items = list(items)
        self.i = 0

    def next(self):
        it = self.items[self.i]
        self.i = (self.i + 1) % len(self.items)
        return it


class KB:
    def __init__(self, nc):
        self.nc = nc
        self.es = ExitStack()
        self.engs = {"PE": nc.tensor, "ACT": nc.scalar, "DVE": nc.vector, "POOL": nc.gpsimd, "SP": nc.sync}
        self.sem = {}
        self.cnt = {}
        for e in ("PE", "ACT", "DVE", "POOL"):
            self.sem[e] = self.es.enter_context(nc.semaphore("s_" + e))
            self.cnt[e] = 0
        self.dsem = [self.es.enter_context(nc.semaphore("d_%d" % i)) for i in range(N_DMA_SEMS)]
        self.dcnt = [0] * N_DMA_SEMS
        self.dnext = 0
        self.dnext2 = [0, 0]
        self.waited = {e: {} for e in self.engs}
        self.n_inst = 0
        self.uid = 0
        self.limit = None
        self.log = None

    def name(self, p):
        self.uid += 1
        return "%s_%d" % (p, self.uid)

    def _wait(self, E, key, c, raw=False):
        if key == E and E == "PE":
            return
        w = self.waited[E]
        if w.get(key, 0) >= c:
            return
        w[key] = c
        if isinstance(key, int):
            self.engs[E].wait_ge(self.dsem[key], c)
        else:
            self.engs[E].wait_ge(self.sem[key], c)

    def _deps(self, E, reads, writes):
        for b in reads:
            if b.w is not None:
                self._wait(E, b.w[0], b.w[1], raw=True)
        for b in writes:
            if b.w is not None:
                self._wait(E, b.w[0], b.w[1])
            for k, c in b.r.items():
                self._wait(E, k, c)

    def _mark(self, tok, reads, writes):
        for b in reads:
            if b.r.get(tok[0], 0) < tok[1]:
                b.r[tok[0]] = tok[1]
        for b in writes:
            b.w = tok
            b.r = {}

    def op(self, E, fn, reads=(), writes=()):
        if self.limit is not None and self.n_inst >= self.limit:
            return None
        self._deps(E, reads, writes)
        if self.log is not None:
            import sys as _sys
            self.log.append((self.n_inst, E, _sys._getframe(1).f_lineno))
        ins = fn(self.engs[E])
        self.cnt[E] += 1
        ins.then_inc(self.sem[E], 1)
        tok = (E, self.cnt[E])
        self._mark(tok, reads, writes)
        self.n_inst += 1
        return tok

    def dma(self, Q, out_ap, in_ap, reads=(), writes=(), **kw):
        if self.limit is not None and self.n_inst >= self.limit:
            return None
        self._deps(Q, reads, writes)
        half = N_DMA_SEMS // 2
        qi = 0 if Q == "SP" else 1
        k = qi * half + self.dnext2[qi]
        self.dnext2[qi] = (self.dnext2[qi] + 1) % half
        if self.dcnt[k] > 0:
            self._wait(Q, k, self.dcnt[k])
        if self.log is not None:
            import sys as _sys
            self.log.append((self.n_inst, "DMA-" + Q, _sys._getframe(1).f_lineno))
        ins = self.engs[Q].dma_start(out=out_ap, in_=in_ap, **kw)
        self.dcnt[k] += 16
        ins.then_inc(self.dsem[k], 16)
        tok = (k, self.dcnt[k])
        self._mark(tok, reads, writes)
        self.n_inst += 1
        return tok

    def barrier(self, engines=("PE", "ACT", "DVE", "POOL", "SP")):
        for E in engines:
            for e2 in ("PE", "ACT", "DVE", "POOL"):
                if self.cnt[e2] > 0:
                    self._wait(E, e2, self.cnt[e2])
            for k in range(N_DMA_SEMS):
                if self.dcnt[k] > 0:
                    self._wait(E, k, self.dcnt[k])


class Scope:
    def __init__(self, kb):
        self.kb = kb
        self.es = ExitStack()

    def sb(self, name, shape, dt):
        t = self.es.enter_context(self.kb.nc.sbuf_tensor(self.kb.name(name), list(shape), dt))
        return t, Buf(name)

    def ps(self, name, shape, dt):
        t = self.es.enter_context(self.kb.nc.psum_tensor(self.kb.name(name), list(shape), dt))
        return t, Buf(name)

    def close(self):
        self.kb.barrier()
        self.es.close()


def bc_row(ap_row, n):
    return ap_row.to_broadcast([128, n])


class Prog:
    def __init__(self, cfg):
        self.cfg = cfg
        self.nc = bass.Bass("TRN2", target_bir_lowering=False)
        self.kb = KB(self.nc)
        self.kb.limit = cfg.get("limit")
        self.I = {}
        self.taps = {}

    def din(self, name, shape, dt=F32):
        self.I[name] = self.nc.dram_tensor(name, list(shape), dt, kind="ExternalInput").ap()
        return self.I[name]

    def dscr(self, name, shape, dt):
        if self.cfg.get("debug"):
            return self.nc.dram_tensor(name, list(shape), dt, kind="ExternalOutput").ap()
        return self.nc.dram_tensor(name, list(shape), dt).ap()

    def tap(self, name, shape, dt=F32):
        self.taps[name] = self.nc.dram_tensor(name, list(shape), dt, kind="ExternalOutput").ap()
        return self.taps[name]

    def declare(self):
        d = self.din
        d("x", [S, D]); d("cT", [128, 8]); d("posT", [128, NT], I32)
        d("ada_w", [2, D, 6 * D]); d("ada_b", [2, 6 * D])
        d("attn_norm", [2, D]); d("ffn_norm", [2, D])
        d("ffn_w_gate", [2, D, FFN]); d("ffn_w_up", [2, D, FFN]); d("ffn_w_down", [2, FFN, D])
        d("sp_w_in", [D, SP_IN]); d("sp_w_out", [D, D])
        d("moba_q_norm", [1, 64]); d("moba_k_norm", [1, 64]); d("nsa_q_norm", [1, 64]); d("nsa_k_norm", [3, 64])
        d("cmp_posT", [64, 2, 32]); d("cmp_w1", [64, 2, 32, 256]); d("cmp_w2", [2, 256, 64])
        d("diff_w_in", [D, 3072]); d("diff_w_out", [D, D])
        d("diff_q_norm", [1, 64]); d("diff_k_norm", [1, 64]); d("diff_lambda", [1, 256]); d("diff_out_norm", [1, 128])
        d("c_ident", [128, 128]); d("c_tri", [128, 128]); d("c_atri", [128, 128])
        d("c_e16", [16, S]); d("c_e64", [64, S]); d("c_ov", [256, 65])
        d("c_t1", [1, 256]); d("c_t2", [1, 256])
        d("c_nfv", [128, NT, 64]); d("c_cst", [128, NT, 64]); d("c_v01", [128, NT, 64])
        self.out = self.nc.dram_tensor("out", [S, D], F32, kind="ExternalOutput").ap()
        self.xmid = self.dscr("xmid", [S, D], F32)
        self.x1s = self.dscr("x1s", [S, D], F32)
        self.FT = self.dscr("FT", [2048, S], BF16)
        self.TM = self.dscr("TM", [S, D], BF16)
        self.OS = self.dscr("OS", [S, D], BF16)
        self.H2T = self.dscr("H2T", [D, S], BF16)

    def setup(self):
        kb, I = self.kb, self.I
        self.G = Scope(kb)
        G = self.G
        self.ident, self.Bident = G.sb("ident", [128, 128], BF16)
        kb.dma("POOL", self.ident[:], I["c_ident"][:, :], writes=[self.Bident])
        self.identF, self.BidentF = G.sb("identF", [128, 128], F32)
        kb.dma("SP", self.identF[:], I["c_ident"][:, :], writes=[self.BidentF])
        self.tri, self.Btri = G.sb("tri", [128, 128], BF16)
        kb.dma("POOL", self.tri[:], I["c_tri"][:, :], writes=[self.Btri])
        self.atri, self.Batri = G.sb("atri", [128, 128], BF16)
        kb.dma("POOL", self.atri[:], I["c_atri"][:, :], writes=[self.Batri])
        self.cosT, self.Bcos = G.sb("cosT", [128, NT, 8], F32)
        self.sinT, self.Bsin = G.sb("sinT", [128, NT, 8], F32)
        self.mod, self.Bmod = G.sb("mod", [128, 6, D], F32)
        self.silc, self.Bsilc = G.sb("silc", [128, 8, 128], F32)
        with_scope = Scope(kb)
        T = with_scope
        pi, Bpi = T.sb("pi", [128, NT], I32)
        pf, Bpf = T.sb("pf", [128, NT], F32)
        ang, Bang = T.sb("ang", [128, NT, 8], F32)
        tmp, Btmp = T.sb("tmp", [128, NT, 8], F32)
        tmp2, Btmp2 = T.sb("tmp2", [128, NT, 8], F32)
        kb.dma("SP", pi[:], I["posT"][:, :], writes=[Bpi])
        kb.op("DVE", lambda e: e.tensor_copy(pf[:], pi[:]), reads=[Bpi], writes=[Bpf])
        inv = (1.0 / (np.float32(500000.0) ** (np.arange(0, 16, 2, dtype=np.float32) / np.float32(16)))).astype(np.float32)
        for j in range(8):
            kb.op("DVE", lambda e, j=j: e.tensor_scalar(ang[:, :, j], pf[:], float(inv[j]), None, ALU.mult),
                  reads=[Bpf], writes=[Bang])
        TWO_PI = float(2 * np.pi)
        MAG = 12582912.0

        def sin_of(dst, Bdst, shift):
            if shift != 0.0:
                kb.op("DVE", lambda e: e.tensor_scalar(tmp2[:], ang[:], shift, None, ALU.add), reads=[Bang], writes=[Btmp2])
                src, Bsrc = tmp2, Btmp2
            else:
                src, Bsrc = ang, Bang
            kb.op("DVE", lambda e: e.tensor_scalar(tmp[:], src[:], 1.0 / TWO_PI, MAG, ALU.mult, ALU.add), reads=[Bsrc], writes=[Btmp])
            kb.op("DVE", lambda e: e.tensor_scalar(tmp[:], tmp[:], -MAG, -TWO_PI, ALU.add, ALU.mult), reads=[Btmp], writes=[Btmp])
            kb.op("DVE", lambda e: e.tensor_tensor(tmp[:], src[:], tmp[:], ALU.add), reads=[Bsrc, Btmp], writes=[Btmp])
            kb.op("DVE", lambda e: e.tensor_scalar(tmp[:], tmp[:], float(np.pi), float(-np.pi), ALU.min, ALU.max), reads=[Btmp], writes=[Btmp])
            kb.op("ACT", lambda e: e.activation(dst[:], tmp[:], AF.Sin), reads=[Btmp], writes=[Bdst])

        sin_of(self.sinT, self.Bsin, 0.0)
        sin_of(self.cosT, self.Bcos, float(np.pi / 2))
        ct, Bct = T.sb("ct", [128, 8], F32)
        kb.dma("SP", ct[:], I["cT"][:, :], writes=[Bct])
        kb.op("ACT", lambda e: e.activation(ct[:], ct[:], AF.Silu), reads=[Bct], writes=[Bct])
        kb.op("DVE", lambda e: e.tensor_copy(self.silc[:], ct[:].unsqueeze(2).to_broadcast([128, 8, 128])),
              reads=[Bct], writes=[self.Bsilc])
        T.close()

    def compute_mod(self, li):
        kb, I = self.kb, self.I
        T = Scope(kb)
        wch = [T.sb("adaw", [128, 8, 512], F32) for _ in range(2)]
        bch = [T.sb("adab", [128, 512], F32) for _ in range(2)]
        pm = [T.ps("pmod", [128, 512], F32) for _ in range(2)]
        nrm, Bnrm = T.sb("nrm", [128, 2, D], F32)
        kb.dma("SP", nrm[:, 0, :], bc_row(I["attn_norm"][li:li + 1, :], D), writes=[Bnrm])
        kb.dma("SP", nrm[:, 1, :], bc_row(I["ffn_norm"][li:li + 1, :], D), writes=[Bnrm])
        wv = I["ada_w"][li].rearrange("(kc p) n -> p kc n", p=128)
        for g in range(12):
            w, Bw = wch[g % 2]
            b, Bb = bch[g % 2]
            p, Bp = pm[g % 2]
            kb.dma("SP", w[:], wv[:, :, g * 512:(g + 1) * 512], writes=[Bw])
            kb.dma("SP", b[:], bc_row(I["ada_b"][li:li + 1, g * 512:(g + 1) * 512], 512), writes=[Bb])
            for kc in range(8):
                kb.op("PE", lambda e, kc=kc: e.matmul(p[:], self.silc[:, kc, :], w[:, kc, :], start=(kc == 0), stop=(kc == 7)),
                      reads=[Bw, self.Bsilc], writes=[Bp])
            sl = self.mod[:, g // 2, (g % 2) * 512:(g % 2 + 1) * 512]
            kb.op("DVE", lambda e: e.tensor_tensor(sl, p[:], b[:], ALU.add), reads=[Bp, Bb], writes=[self.Bmod])
        kb.op("DVE", lambda e: e.scalar_tensor_tensor(self.mod[:, 1, :], self.mod[:, 1, :], 1.0, nrm[:, 0, :], ALU.add, ALU.mult),
              reads=[self.Bmod, Bnrm], writes=[self.Bmod])
        kb.op("DVE", lambda e: e.scalar_tensor_tensor(self.mod[:, 4, :], self.mod[:, 4, :], 1.0, nrm[:, 1, :], ALU.add, ALU.mult),
              reads=[self.Bmod, Bnrm], writes=[self.Bmod])
        T.close()

    def rms_mod_tile(self, T, xt, Bxt, hb, Bhb, gi, si, scr):
        kb = self.kb
        junk, Bjunk, ss, Bss, hn, Bhn = scr
        kb.op("ACT", lambda e: e.activation(junk[:], xt[:], AF.Square, accum_out=ss[:]), reads=[Bxt], writes=[Bjunk, Bss])
        kb.op("ACT", lambda e: e.activation(ss[:], ss[:], AF.Sqrt, bias=self.epsb[:], scale=1.0 / D), reads=[Bss, self.Bepsb], writes=[Bss])
        kb.op("DVE", lambda e: e.reciprocal(ss[:], ss[:]), reads=[Bss], writes=[Bss])
        kb.op("DVE", lambda e: e.scalar_tensor_tensor(hn[:], xt[:], ss[:, 0:1], self.mod[:, gi, :], ALU.mult, ALU.mult),
              reads=[Bxt, Bss, self.Bmod], writes=[Bhn])
        kb.op("POOL", lambda e: e.tensor_tensor(hb[:], hn[:], self.mod[:, si, :], ALU.add), reads=[Bhn, self.Bmod], writes=[Bhb])

    def transpose8(self, src, Bsrc, pt, Bpt, dst_ap, Bdst, eng="ACT"):
        kb = self.kb
        for j in range(8):
            kb.op("PE", lambda e, j=j: e.transpose(pt[:, j * 128:(j + 1) * 128], src[:, j * 128:(j + 1) * 128], self.ident[:]),
                  reads=[Bsrc, self.Bident], writes=[Bpt])
        if eng == "ACT":
            kb.op("ACT", lambda e: e.copy(dst_ap, pt[:].rearrange("p (j t) -> p j t", j=8)), reads=[Bpt], writes=[Bdst])
        else:
            kb.op("DVE", lambda e: e.tensor_copy(dst_ap, pt[:].rearrange("p (j t) -> p j t", j=8)), reads=[Bpt], writes=[Bdst])

    def load_w_bf16(self, dst, Bdst, w_ap, nk):
        for kc in range(nk):
            self.kb.dma("POOL", dst[:, kc, :], w_ap[kc * 128:(kc + 1) * 128, :], writes=[Bdst])

    def phase_a(self, li, x_src):
        kb, I = self.kb, self.I
        T = Scope(kb)
        if li == 1:
            w_ap, ncols = I["diff_w_in"], 3072
            groups = []
            for g in range(6):
                if g < 4:
                    gi = 0 if g < 2 else 1
                    blocks = [("T", g * 512 + b * 128) for b in range(4)]
                    groups.append(dict(c0=g * 512, w=512, qk=[(0, 8)], gain=gi, blocks=blocks, gate=None))
                else:
                    blocks = [("V", (g - 4) * 512 + b * 128) for b in range(4)]
                    groups.append(dict(c0=g * 512, w=512, qk=[], gain=None, blocks=blocks, gate=None))
            gain_srcs = [[(I["diff_q_norm"][0:1, :], 0, 8)], [(I["diff_k_norm"][0:1, :], 0, 8)]]
        else:
            w_ap, ncols = I["sp_w_in"], SP_IN
            kn = I["nsa_k_norm"]
            groups = [
                dict(c0=0, w=512, qk=[(0, 8)], gain=0, blocks=[("T", b * 128) for b in range(4)], gate=None),
                dict(c0=512, w=512, qk=[(0, 8)], gain=1, blocks=[("T", 512 + b * 128) for b in range(4)], gate=None),
                dict(c0=1024, w=512, qk=[], gain=None, blocks=[("V", b * 128) for b in range(4)], gate=None),
                dict(c0=1536, w=512, qk=[(0, 8)], gain=2, blocks=[("T", 1024 + b * 128) for b in range(4)], gate=None),
                dict(c0=2048, w=512, qk=[(0, 2), (4, 6)], gain=3,
                     blocks=[("T", 1536), ("T", 1664), ("T", 1792), ("V", 512)], gate=None),
                dict(c0=2560, w=280, qk=[(0, 2)], gain=4, blocks=[("T", 1920), ("V", 640)], gate=(256, 24)),
            ]
            gain_srcs = [
                [(I["moba_q_norm"][0:1, :], 0, 8)], [(I["moba_k_norm"][0:1, :], 0, 8)], [(I["nsa_q_norm"][0:1, :], 0, 8)],
                [(kn[0:1, :], 0, 2), (kn[1:2, :], 4, 6)], [(kn[2:3, :], 0, 2)],
            ]
        nk = 8
        W, BW = T.sb("w_in", [128, nk, ncols], BF16)
        self.load_w_bf16(W, BW, w_ap, nk)
        gains = []
        for gs in gain_srcs:
            gr, Bgr = T.sb("gainrow", [128, 8, 64], F32)
            kb.op("POOL", lambda e: e.memset(gr[:], 1.0), writes=[Bgr])
            for (src, u0, u1) in gs:
                for u in range(u0, u1):
                    kb.dma("SP", gr[:, u, :], bc_row(src, 64), writes=[Bgr])
            gains.append((gr, Bgr))
        NG = len(groups)
        xts = Rot([T.sb("xt", [128, D], F32) for _ in range(3)])
        hbs = Rot([T.sb("hb", [128, D], BF16) for _ in range(2)])
        hTs = Rot([T.sb("hT", [128, 8, 128], BF16) for _ in range(2)])
        junk, Bjunk = T.sb("junk", [128, D], BF16)
        sss = Rot([T.sb("ss", [128, 1], F32) for _ in range(2)])
        hn, Bhn = T.sb("hn", [128, D], F32)
        sqs = [T.sb("sq", [128, 512], F32) for _ in range(NG)]
        ss8l = [T.sb("ss8", [128, 8], F32) for _ in range(NG)]
        qnl = [T.sb("qn", [128, 8, 64], F32) for _ in range(NG)]
        postl = [T.sb("post", [128, 512], BF16) for _ in range(NG)]
        rtl = [T.sb("rt", [128, 4, 8, 8], F32) for _ in range(NG)]
        pts = Rot([T.ps("ptA", [128, 1024], BF16) for _ in range(1)])
        pgl = [T.ps("pgA", [128, 512], F32) for _ in range(NG)]
        pbt, _ = T.ps("ptB", [128, 1024], BF16)
        pbh = Rot([(pbt[:, 0:512], Buf("pb0")), (pbt[:, 512:1024], Buf("pb1"))])
        stages = {}
        for gi_, g in enumerate(groups):
            if any(k == "T" for k, _ in g["blocks"]):
                stages[gi_] = [T.sb("stage", [128, 4, 512], BF16) for _ in range(2)]

        def load_x(t):
            xt, Bxt = xts.next()
            kb.dma("SP", xt[:], x_src[t * 128:(t + 1) * 128, :], writes=[Bxt])
            return xt, Bxt

        def prep(t, xtb):
            xt, Bxt = xtb
            hb, Bhb = hbs.next()
            ss, Bss = sss.next()
            self.rms_mod_tile(T, xt, Bxt, hb, Bhb, 1, 0, (junk, Bjunk, ss, Bss, hn, Bhn))
            hT, BhT = hTs.next()
            pt, Bpt = pts.next()
            self.transpose8(hb, Bhb, pt, Bpt, hT[:], BhT, eng="ACT")
            return hT, BhT

        xq = [load_x(0)]
        if NT > 1:
            xq.append(load_x(1))
        hq = [prep(0, xq.pop(0))]
        for t in range(NT):
            hT, BhT = hq.pop(0)
            if t + 2 < NT:
                xq.append(load_x(t + 2))
            tt = t % 4
            half = (t // 4) % 2
            for gi_, g in enumerate(groups):
                w = g["w"]
                pg, Bpg = pgl[gi_]
                for kc in range(8):
                    kb.op("PE", lambda e, kc=kc, pg=pg, w=w, g=g: e.matmul(pg[:, 0:w], hT[:, kc, :], W[:, kc, g["c0"]:g["c0"] + w],
                                                                          start=(kc == 0), stop=(kc == 7)),
                          reads=[BhT, BW], writes=[Bpg])
            if t + 1 < NT:
                hq.append(prep(t + 1, xq.pop(0)))
            info = []
            for gi_, g in enumerate(groups):
                w = g["w"]
                wq = (w // 64) * 64
                info.append((gi_, g, w, wq, wq // 64))
            for gi_, g, w, wq, nu in info:
                pg, Bpg = pgl[gi_]
                post, Bpost = postl[gi_]
                sq, Bsq = sqs[gi_]
                if g["qk"]:
                    kb.op("ACT", lambda e, sq=sq, pg=pg, wq=wq: e.activation(sq[:, 0:wq], pg[:, 0:wq], AF.Square), reads=[Bpg], writes=[Bsq])
                else:
                    kb.op("ACT", lambda e, post=post, pg=pg, wq=wq: e.copy(post[:, 0:wq], pg[:, 0:wq]), reads=[Bpg], writes=[Bpost])
                if g["gate"] is not None:
                    gc0, gw = g["gate"]
                    kb.op("ACT", lambda e, pg=pg, gc0=gc0, gw=gw: e.activation(self.gate_sb[:, t, :], pg[:, gc0:gc0 + gw], AF.Sigmoid),
                          reads=[Bpg], writes=[self.Bgate])
            for gi_, g, w, wq, nu in info:
                if not g["qk"]:
                    continue
                sq, Bsq = sqs[gi_]
                ss8, Bss8 = ss8l[gi_]
                kb.op("DVE", lambda e, ss8=ss8, sq=sq, wq=wq, nu=nu: e.tensor_reduce(ss8[:, 0:nu], sq[:, 0:wq].rearrange("p (u d) -> p u d", d=64), AX.X, ALU.add),
                      reads=[Bsq], writes=[Bss8])
            for gi_, g, w, wq, nu in info:
                if not g["qk"]:
                    continue
                ss8, Bss8 = ss8l[gi_]
                kb.op("ACT", lambda e, ss8=ss8, nu=nu: e.activation(ss8[:, 0:nu], ss8[:, 0:nu], AF.Sqrt, bias=self.epsb[:], scale=1.0 / 64),
                      reads=[Bss8, self.Bepsb], writes=[Bss8])
            for gi_, g, w, wq, nu in info:
                if not g["qk"]:
                    continue
                pg, Bpg = pgl[gi_]
                ss8, Bss8 = ss8l[gi_]
                qn, Bqn = qnl[gi_]
                gr, Bgr = gains[g["gain"]]
                kb.op("DVE", lambda e, ss8=ss8, nu=nu: e.reciprocal(ss8[:, 0:nu], ss8[:, 0:nu]), reads=[Bss8], writes=[Bss8])
                qk_units = set()
                for (u0, u1) in g["qk"]:
                    qk_units.update(range(u0, u1))
                for u in [u for u in range(nu) if u not in qk_units]:
                    kb.op("DVE", lambda e, u=u, ss8=ss8: e.memset(ss8[:, u:u + 1], 1.0), writes=[Bss8])
                kb.op("DVE", lambda e, qn=qn, pg=pg, ss8=ss8, nu=nu, wq=wq: e.tensor_tensor(
                    qn[:, 0:nu, :], pg[:, 0:wq].rearrange("p (u d) -> p u d", d=64),
                    ss8[:, 0:nu].unsqueeze(2).to_broadcast([128, nu, 64]), ALU.mult),
                      reads=[Bpg, Bss8], writes=[Bqn])
                kb.op("DVE", lambda e, qn=qn, gr=gr, nu=nu: e.tensor_tensor(qn[:, 0:nu, :], qn[:, 0:nu, :], gr[:, 0:nu, :], ALU.mult),
                      reads=[Bqn, Bgr], writes=[Bqn])
            for gi_, g, w, wq, nu in info:
                if not g["qk"]:
                    continue
                qn, Bqn = qnl[gi_]
                post, Bpost = postl[gi_]
                kb.op("ACT", lambda e, post=post, qn=qn, wq=wq, nu=nu: e.copy(post[:, 0:wq], qn[:, 0:nu, :].rearrange("p u d -> p (u d)")),
                      reads=[Bqn], writes=[Bpost])
            for gi_, g, w, wq, nu in info:
                if not g["qk"]:
                    continue
                qn, Bqn = qnl[gi_]
                post, Bpost = postl[gi_]
                rt, Brt = rtl[gi_]
                postv = post[:, 0:wq].rearrange("p (u d) -> p u d", d=64)
                for (u0, u1) in g["qk"]:
                    n_ = u1 - u0
                    cosb = self.cosT[:, t, :].unsqueeze(1).to_broadcast([128, n_, 8])
                    sinb = self.sinT[:, t, :].unsqueeze(1).to_broadcast([128, n_, 8])
                    t1 = qn[:, u0:u1, 0:8]
                    t2 = qn[:, u0:u1, 8:16]
                    kb.op("DVE", lambda e, rt=rt, n_=n_, t1=t1, cosb=cosb: e.tensor_tensor(rt[:, 0, 0:n_, :], t1, cosb, ALU.mult), reads=[Bqn, self.Bcos], writes=[Brt])
                    kb.op("DVE", lambda e, rt=rt, n_=n_, t2=t2, sinb=sinb: e.tensor_tensor(rt[:, 1, 0:n_, :], t2, sinb, ALU.mult), reads=[Bqn, self.Bsin], writes=[Brt])
                    kb.op("DVE", lambda e, rt=rt, n_=n_, t2=t2, cosb=cosb: e.tensor_tensor(rt[:, 2, 0:n_, :], t2, cosb, ALU.mult), reads=[Bqn, self.Bcos], writes=[Brt])
                    kb.op("DVE", lambda e, rt=rt, n_=n_, t1=t1, sinb=sinb: e.tensor_tensor(rt[:, 3, 0:n_, :], t1, sinb, ALU.mult), reads=[Bqn, self.Bsin], writes=[Brt])
                    kb.op("DVE", lambda e, rt=rt, n_=n_, postv=postv, u0=u0, u1=u1: e.tensor_tensor(postv[:, u0:u1, 0:8], rt[:, 0, 0:n_, :], rt[:, 1, 0:n_, :], ALU.subtract),
                          reads=[Brt], writes=[Bpost])
                    kb.op("DVE", lambda e, rt=rt, n_=n_, postv=postv, u0=u0, u1=u1: e.tensor_tensor(postv[:, u0:u1, 8:16], rt[:, 2, 0:n_, :], rt[:, 3, 0:n_, :], ALU.add),
                          reads=[Brt], writes=[Bpost])
            for gi_, g, w, wq, nu in info:
                post, Bpost = postl[gi_]
                tblocks = [(bi, dst) for bi, (k, dst) in enumerate(g["blocks"]) if k == "T"]
                if tblocks:
                    pb, Bpb = pbh.next()
                    st, Bst = stages[gi_][half]
                    for bi, dst in tblocks:
                        kb.op("PE", lambda e, bi=bi, pb=pb, post=post: e.transpose(pb[:, bi * 128:(bi + 1) * 128], post[:, bi * 128:(bi + 1) * 128], self.ident[:]),
                              reads=[Bpost, self.Bident], writes=[Bpb])
                    b0, b1 = tblocks[0][0], tblocks[-1][0] + 1
                    kb.op("ACT", lambda e, st=st, pb=pb, b0=b0, b1=b1: e.copy(st[:, b0:b1, tt * 128:(tt + 1) * 128],
                                                                              pb[:, b0 * 128:b1 * 128].rearrange("p (b t) -> p b t", t=128)),
                          reads=[Bpb], writes=[Bst])
                    if tt == 3:
                        t0 = (t - 3) * 128
                        for bi, dst in tblocks:
                            kb.dma("POOL", self.FT[dst:dst + 128, t0:t0 + 512], st[:, bi, :], reads=[Bst])
                for bi, (k, dst) in enumerate(g["blocks"]):
                    if k == "V":
                        kb.dma("POOL", self.TM[t * 128:(t + 1) * 128, dst:dst + 128], post[:, bi * 128:(bi + 1) * 128], reads=[Bpost])
        T.close()

    def attn_qtile(self, R, q_rhs, Bq, k_lhsT, Bk, v_rhs, Bv, vw, pairs, O, BO, o_off, post_exp=None, bank_of=lambda s: 0, vstat=False, hooks=None):
        kb = self.kb
        LOOK = self.cfg.get("look", 5)
        last = {}
        started = set()
        for kc, subs in pairs:
            for s, kind in subs:
                last[s] = kc
        live = {}

        def stage1(i):
            kc, subs = pairs[i]
            pS, BpS = R["pS"].next()
            kl, krows = k_lhsT(kc)
            kb.op("PE", lambda e: e.matmul(pS[0:krows, :], kl, q_rhs, start=True, stop=True), reads=[Bk, Bq], writes=[BpS])
            pT, BpT = R["pT"].next()
            kb.op("ACT", lambda e: e.activation(pT[0:krows, :], pS[0:krows, :], AF.Exp, scale=0.125), reads=[BpS], writes=[BpT])
            if post_exp is not None:
                post_exp(kc, pT, BpT, krows)
            for s, kind in subs:
                if kind == "tri":
                    kb.op("DVE", lambda e, s=s: e.tensor_tensor(pT[:, s * 128:(s + 1) * 128], pT[:, s * 128:(s + 1) * 128], self.tri[:], ALU.mult),
                          reads=[BpT, self.Btri], writes=[BpT])
                elif kind == "atri":
                    kb.op("DVE", lambda e, s=s: e.tensor_tensor(pT[:, s * 128:(s + 1) * 128], pT[:, s * 128:(s + 1) * 128], self.atri[:], ALU.mult),
                          reads=[BpT, self.Batri], writes=[BpT])
            live[i] = (pT, BpT, krows)

        def stage2(i):
            kc, subs = pairs[i]
            pT, BpT, krows = live.pop(i)
            if vstat:
                c0 = min(s_ for s_, _ in subs) * 128
                c1 = (max(s_ for s_, _ in subs) + 1) * 128
                st_flag = 0 not in started
                started.add(0)
                kb.op("PE", lambda e: e.matmul(O[0:vw, c0:c1], v_rhs(kc)[0:krows, 0:vw], pT[0:krows, c0:c1],
                                               start=st_flag, stop=(i == len(pairs) - 1), skip_group_check=True),
                      reads=[BpT, Bv], writes=[BO])
                return
            for s, kind in subs:
                bk = bank_of(s)
                st_flag = bk not in started
                started.add(bk)
                kb.op("PE", lambda e, s=s, st_flag=st_flag: e.matmul(o_off(O, s), pT[0:krows, s * 128:(s + 1) * 128], v_rhs(kc)[0:krows, :],
                                                                     start=st_flag, stop=(last[s] == kc), skip_group_check=True),
                      reads=[BpT, Bv], writes=[BO])

        n = len(pairs)
        for i in range(n + LOOK):
            if i < n:
                stage1(i)
            if hooks and i in hooks:
                hooks.pop(i)()
            if i - LOOK >= 0:
                stage2(i - LOOK)
        if hooks:
            for k in sorted(hooks):
                hooks.pop(k)()

    def ot_to_tok(self, OT, BOT, vw, osb, Bosb, Otok, BOtok):
        kb = self.kb
        kb.op("ACT", lambda e: e.copy(osb[0:vw, :], OT[0:vw, :]), reads=[BOT], writes=[Bosb])
        for s in range(4):
            kb.op("PE", lambda e, s=s: e.transpose(Otok[:, s * 128:s * 128 + vw], osb[0:vw, s * 128:(s + 1) * 128], self.identF[0:vw, 0:vw]),
                  reads=[Bosb, self.BidentF], writes=[BOtok])

    @staticmethod
    def causal_pairs(qt):
        pairs = []
        for kc in range(4 * qt + 4):
            j = kc - 4 * qt
            if j < 0:
                pairs.append((kc, [(s, "full") for s in range(4)]))
            else:
                pairs.append((kc, [(s, "tri" if s == j else "full") for s in range(j, 4)]))
        return pairs

    def phase_b_diff(self, li_odd_index):
        kb, I = self.kb, self.I
        T = Scope(kb)
        lam_init = 0.8 - 0.6 * float(np.exp(-0.3 * 1))
        lp, Blp = T.sb("lp", [128, 4, 64], F32)
        kb.dma("SP", lp[:].rearrange("p a d -> p (a d)"), bc_row(I["diff_lambda"][0:1, :], 256), writes=[Blp])
        l2, Bl2 = T.sb("l2", [128, 2, 64], F32)
        kb.op("DVE", lambda e: e.tensor_tensor(l2[:, 0, :], lp[:, 0, :], lp[:, 1, :], ALU.mult), reads=[Blp], writes=[Bl2])
        kb.op("DVE", lambda e: e.tensor_tensor(l2[:, 1, :], lp[:, 2, :], lp[:, 3, :], ALU.mult), reads=[Blp], writes=[Bl2])
        ls, Bls = T.sb("ls", [128, 2], F32)
        kb.op("DVE", lambda e: e.tensor_reduce(ls[:], l2[:], AX.X, ALU.add), reads=[Bl2], writes=[Bls])
        kb.op("ACT", lambda e: e.activation(ls[:], ls[:], AF.Exp), reads=[Bls], writes=[Bls])
        nlam, Bnlam = T.sb("nlam", [128, 1], F32)
        kb.op("DVE", lambda e: e.tensor_tensor(nlam[:], ls[:, 1:2], ls[:, 0:1], ALU.subtract), reads=[Bls], writes=[Bnlam])
        kb.op("DVE", lambda e: e.tensor_scalar(nlam[:], nlam[:], -lam_init, None, ALU.add), reads=[Bnlam], writes=[Bnlam])
        go, Bgo = T.sb("go", [128, 128], F32)
        kb.dma("SP", go[:], bc_row(I["diff_out_norm"][0:1, :], 128), writes=[Bgo])
        kb.op("DVE", lambda e: e.tensor_scalar(go[:], go[:], 1.0 - lam_init, None, ALU.mult), reads=[Bgo], writes=[Bgo])

        KTs = Rot([T.sb("KT", [128, S], BF16) for _ in range(2)])
        QTs = Rot([T.sb("QT", [128, S], BF16) for _ in range(2)])
        Vs = Rot([T.sb("V", [128, NT, 129], BF16) for _ in range(2)])
        for (v, Bv) in Vs.items:
            kb.op("POOL", lambda e, v=v: e.memset(v[:, :, 128:129], 1.0), writes=[Bv])
        R = dict(pS=Rot([T.ps("pS", [128, 512], F32) for _ in range(3)]),
                 pT=Rot([T.sb("pT", [128, 512], BF16) for _ in range(8)]))
        Os = [[T.ps("O", [128, 512], F32) for _ in range(2)] for _ in range(2)]
        osts = Rot([T.sb("ost", [128, 4, 128], BF16) for _ in range(2)])
        osbs = Rot([[[T.sb("osbd", [128, 512], F32) for _ in range(2)] for _ in range(2)] for _ in range(2)])
        pending = []
        a1ts = Rot([T.sb("a1t", [128, 4, 128], F32) for _ in range(2)])
        rdts = Rot([T.sb("rdt", [128, 4, 4], F32) for _ in range(2)])
        a0s = Rot([T.sb("a0", [128, 128], F32) for _ in range(2)])
        a1s = Rot([T.sb("a1", [128, 128], F32) for _ in range(2)])
        rds = Rot([T.sb("rd", [128, 4], F32) for _ in range(4)])
        junk, Bjunk = T.sb("junkb", [128, 128], BF16)

        def load_head(h):
            KT, BKT = KTs.next()
            QT, BQT = QTs.next()
            V, BV = Vs.next()
            kb.dma("SP", QT[:], self.FT[h * 128:(h + 1) * 128, :], writes=[BQT])
            kb.dma("SP", KT[:], self.FT[1024 + h * 128:1024 + (h + 1) * 128, :], writes=[BKT])
            tmv = self.TM[:, h * 128:(h + 1) * 128].rearrange("(c p) d -> p c d", p=128)
            for c4 in range(4):
                kb.dma("SP", V[:, c4 * 8:(c4 + 1) * 8, 0:128], tmv[:, c4 * 8:(c4 + 1) * 8, :], writes=[BV])
            return (KT, BKT, QT, BQT, V, BV)

        nxt = load_head(0)
        for h in range(8):
            KT, BKT, QT, BQT, V, BV = nxt
            if h + 1 < 8:
                nxt = load_head(h + 1)
            for qt in range(8):
                pairs = self.causal_pairs(qt)
                for c in range(2):
                    def o_off(O, s, c=c):
                        return Os[c][s // 2][0][:, (s % 2) * 256:(s % 2) * 256 + 129]
                    hooks = None
                    if c == 0 and pending:
                        e2_, e3_ = pending.pop(0)
                        hooks = {min(6, len(pairs) - 1): e2_, 10 ** 6: e3_}
                    self.attn_qtile(R, QT[64 * c:64 * c + 64, qt * 512:(qt + 1) * 512], BQT,
                                    lambda kc, c=c: (KT[64 * c:64 * c + 64, kc * 128:(kc + 1) * 128], 128), BKT,
                                    lambda kc: V[:, kc, :], BV, 129, pairs, None, Os[c][0][1], o_off, bank_of=lambda s: s // 2, hooks=hooks)
                oset = osbs.next()
                for c in range(2):
                    for b_ in range(2):
                        kb.op("DVE", lambda e, c=c, b_=b_: e.tensor_copy(oset[c][b_][0][:, 0:385], Os[c][b_][0][:, 0:385]),
                              reads=[Os[c][0][1]], writes=[oset[c][b_][1]])

                ep = dict(oset=oset, h=h, qt=qt)
                a1t, Ba1t = a1ts.next()
                rdt, Brdt = rdts.next()
                ep.update(a1t=a1t, Ba1t=Ba1t, rdt=rdt, Brdt=Brdt)

                def e1(ep=ep):
                    oset, a1t, Ba1t, rdt, Brdt = ep["oset"], ep["a1t"], ep["Ba1t"], ep["rdt"], ep["Brdt"]
                    for s in range(4):
                        O0 = oset[0][s // 2][0][:, (s % 2) * 256:(s % 2) * 256 + 129]
                        O1 = oset[1][s // 2][0][:, (s % 2) * 256:(s % 2) * 256 + 129]
                        BO0, BO1 = oset[0][s // 2][1], oset[1][s // 2][1]
                        a0, Ba0 = a0s.next()
                        kb.op("DVE", lambda e, s=s, O0=O0: e.reciprocal(rdt[:, s, 0:1], O0[:, 128:129]), reads=[BO0], writes=[Brdt])
                        kb.op("DVE", lambda e, s=s, O1=O1: e.reciprocal(rdt[:, s, 1:2], O1[:, 128:129]), reads=[BO1], writes=[Brdt])
                        kb.op("DVE", lambda e, s=s: e.tensor_tensor(rdt[:, s, 1:2], rdt[:, s, 1:2], nlam[:], ALU.mult), reads=[Brdt, Bnlam], writes=[Brdt])
                        kb.op("DVE", lambda e, s=s, a0=a0, O0=O0: e.tensor_scalar(a0[:], O0[:, 0:128], rdt[:, s, 0:1], None, ALU.mult), reads=[BO0, Brdt], writes=[Ba0])
                        kb.op("DVE", lambda e, s=s, a0=a0, O1=O1: e.scalar_tensor_tensor(a1t[:, s, :], O1[:, 0:128], rdt[:, s, 1:2], a0[:], ALU.mult, ALU.add),
                              reads=[BO1, Brdt, Ba0], writes=[Ba1t])

                def e2(ep=ep):
                    a1t, Ba1t, rdt, Brdt = ep["a1t"], ep["Ba1t"], ep["rdt"], ep["Brdt"]
                    for s in range(4):
                        kb.op("ACT", lambda e, s=s: e.activation(junk[:], a1t[:, s, :], AF.Square, accum_out=rdt[:, s, 2:3]), reads=[Ba1t], writes=[Bjunk, Brdt])
                    kb.op("ACT", lambda e: e.activation(rdt[:, :, 2:3], rdt[:, :, 2:3], AF.Sqrt, bias=self.epsb[:], scale=1.0 / 128),
                          reads=[Brdt, self.Bepsb], writes=[Brdt])

                def e3(ep=ep):
                    a1t, Ba1t, rdt, Brdt, h, qt = ep["a1t"], ep["Ba1t"], ep["rdt"], ep["Brdt"], ep["h"], ep["qt"]
                    ost, Bost = osts.next()
                    kb.op("DVE", lambda e: e.reciprocal(rdt[:, :, 3:4], rdt[:, :, 2:3]), reads=[Brdt], writes=[Brdt])
                    for s in range(4):
                        kb.op("DVE", lambda e, s=s, ost=ost: e.scalar_tensor_tensor(ost[:, s, :], a1t[:, s, :], rdt[:, s, 3:4], go[:], ALU.mult, ALU.mult),
                              reads=[Ba1t, Brdt, Bgo], writes=[Bost])
                    osv = self.OS[qt * 512:(qt + 1) * 512, h * 128:(h + 1) * 128].rearrange("(s p) d -> p s d", p=128)
                    kb.dma("POOL", osv, ost[:], reads=[Bost])
                e1()
                pending.append((e2, e3))
        while pending:
            e2_, e3_ = pending.pop(0)
            e2_()
            e3_()
        T.close()

    def phase_c1(self, li, x_src, w_out_ap):
        kb, I = self.kb, self.I
        T = Scope(kb)
        W, BW = T.sb("w_out", [128, 8, D], BF16)
        self.load_w_bf16(W, BW, w_out_ap, 8)
        ots = Rot([T.sb("ot", [128, D], BF16) for _ in range(3)])
        xts = Rot([T.sb("xt", [128, D], F32) for _ in range(3)])
        oTs = Rot([T.sb("oT", [128, 8, 128], BF16) for _ in range(2)])
        x1s = Rot([T.sb("x1", [128, D], F32) for _ in range(2)])
        hbs = Rot([T.sb("hb", [128, D], BF16) for _ in range(2)])
        ytmp, Bytmp = T.sb("ytmp", [128, D], F32)
        junk, Bjunk = T.sb("junk", [128, D], BF16)
        sss = Rot([T.sb("ss", [128, 1], F32) for _ in range(2)])
        hn, Bhn = T.sb("hn", [128, D], F32)
        stg = [T.sb("stage", [128, 8, 512], BF16) for _ in range(2)]
        pts = Rot([T.ps("ptA", [128, 1024], BF16) for _ in range(2)])
        pys = Rot([T.ps("py", [128, 512], F32) for _ in range(4)])

        def load(t):
            ot, Bot = ots.next()
            xt, Bxt = xts.next()
            kb.dma("SP", ot[:], self.OS[t * 128:(t + 1) * 128, :], writes=[Bot])
            kb.dma("SP", xt[:], x_src[t * 128:(t + 1) * 128, :], writes=[Bxt])
            return ot, Bot, xt, Bxt

        def front(t, ld):
            ot, Bot, xt, Bxt = ld
            oT, BoT = oTs.next()
            pt, Bpt = pts.next()
            self.transpose8(ot, Bot, pt, Bpt, oT[:], BoT, eng="ACT")
            pyl = []
            for g in range(2):
                py, Bpy = pys.next()
                for kc in range(8):
                    kb.op("PE", lambda e, kc=kc, g=g, py=py: e.matmul(py[:], oT[:, kc, :], W[:, kc, g * 512:(g + 1) * 512], start=(kc == 0), stop=(kc == 7)),
                          reads=[BoT, BW], writes=[Bpy])
                pyl.append((py, Bpy))
            return pyl, xt, Bxt

        def back(t, fr):
            pyl, xt, Bxt = fr
            x1, Bx1 = x1s.next()
            for g in range(2):
                py, Bpy = pyl[g]
                sl = slice(g * 512, (g + 1) * 512)
                kb.op("DVE", lambda e, py=py, sl=sl: e.tensor_tensor(ytmp[:, sl], py[:], self.mod[:, 2, sl], ALU.mult), reads=[Bpy, self.Bmod], writes=[Bytmp])
                kb.op("POOL", lambda e, sl=sl: e.tensor_tensor(x1[:, sl], ytmp[:, sl], xt[:, sl], ALU.add), reads=[Bytmp, Bxt], writes=[Bx1])
            kb.dma("POOL", self.x1s[t * 128:(t + 1) * 128, :], x1[:], reads=[Bx1])
            hb, Bhb = hbs.next()
            ss, Bss = sss.next()
            self.rms_mod_tile(T, x1, Bx1, hb, Bhb, 4, 3, (junk, Bjunk, ss, Bss, hn, Bhn))
            pt, Bpt = pts.next()
            tt = t % 4
            st, Bst = stg[(t // 4) % 2]
            self.transpose8(hb, Bhb, pt, Bpt, st[:, :, tt * 128:(tt + 1) * 128], Bst, eng="ACT")
            if tt == 3:
                t0 = (t - 3) * 128
                h2v = self.H2T.rearrange("(j p) t -> p j t", p=128)
                kb.dma("POOL", h2v[:, :, t0:t0 + 512], st[:], reads=[Bst])

        lds = [load(0)]
        if NT > 1:
            lds.append(load(1))
        fr = front(0, lds.pop(0))
        for t in range(NT):
            if t + 2 < NT:
                lds.append(load(t + 2))
            nfr = front(t + 1, lds.pop(0)) if t + 1 < NT else None
            back(t, fr)
            fr = nfr
        T.close()

    def phase_c2(self, li, x_dst):
        kb, I = self.kb, self.I
        T = Scope(kb)
        Wg, BWg = T.sb("wg", [128, 8, FFN], BF16)
        Wu, BWu = T.sb("wu", [128, 8, FFN], BF16)
        Wd, BWd = T.sb("wd", [128, NJ, D], BF16)
        self.load_w_bf16(Wg, BWg, I["ffn_w_gate"][li], 8)
        self.load_w_bf16(Wu, BWu, I["ffn_w_up"][li], 8)
        self.load_w_bf16(Wd, BWd, I["ffn_w_down"][li], NJ)
        TT = 256
        h2s = Rot([T.sb("h2T", [128, 8, TT], BF16) for _ in range(2)])
        act, Bact = T.sb("act", [128, NJ, TT], BF16)
        sgs = Rot([T.sb("sg", [128, TT], F32) for _ in range(2)])
        xts = Rot([T.sb("x1t", [128, D], F32) for _ in range(2)])
        ytmp, Bytmp = T.sb("ytmp", [128, 512], F32)
        pgs = Rot([T.ps("pg", [128, 512], F32) for _ in range(2)])
        pus = Rot([T.ps("pu", [128, 512], F32) for _ in range(2)])
        pys = Rot([T.ps("py", [128, 512], F32) for _ in range(3)])
        h2v = self.H2T.rearrange("(j p) t -> p j t", p=128)

        def load(st):
            h2, Bh2 = h2s.next()
            kb.dma("SP", h2[:], h2v[:, :, st * TT:(st + 1) * TT], writes=[Bh2])
            return h2, Bh2

        nxt = load(0)
        for st in range(S // TT):
            h2, Bh2 = nxt
            if st + 1 < S // TT:
                nxt = load(st + 1)
            for j in range(NJ):
                pg, Bpg = pgs.next()
                pu, Bpu = pus.next()
                for kc in range(8):
                    kb.op("PE", lambda e, kc=kc: e.matmul(pg[:, 0:TT], Wg[:, kc, j * 128:(j + 1) * 128], h2[:, kc, :], start=(kc == 0), stop=(kc == 7)),
                          reads=[BWg, Bh2], writes=[Bpg])
                for kc in range(8):
                    kb.op("PE", lambda e, kc=kc: e.matmul(pu[:, 0:TT], Wu[:, kc, j * 128:(j + 1) * 128], h2[:, kc, :], start=(kc == 0), stop=(kc == 7)),
                          reads=[BWu, Bh2], writes=[Bpu])
                sg, Bsg = sgs.next()
                kb.op("ACT", lambda e: e.activation(sg[:], pg[:, 0:TT], AF.Silu), reads=[Bpg], writes=[Bsg])
                kb.op("DVE", lambda e, j=j: e.tensor_tensor(act[:, j, :], sg[:], pu[:, 0:TT], ALU.mult), reads=[Bsg, Bpu], writes=[Bact])
            for q in range(TT // 128):
                t = st * (TT // 128) + q
                xt, Bxt = xts.next()
                kb.dma("SP", xt[:], self.x1s[t * 128:(t + 1) * 128, :], writes=[Bxt])
                for g in range(2):
                    py, Bpy = pys.next()
                    for j in range(NJ):
                        kb.op("PE", lambda e, j=j: e.matmul(py[:], act[:, j, q * 128:(q + 1) * 128], Wd[:, j, g * 512:(g + 1) * 512],
                                                            start=(j == 0), stop=(j == NJ - 1)),
                              reads=[Bact, BWd], writes=[Bpy])
                    sl = slice(g * 512, (g + 1) * 512)
                    kb.op("DVE", lambda e: e.tensor_tensor(ytmp[:], py[:], self.mod[:, 5, sl], ALU.mult), reads=[Bpy, self.Bmod], writes=[Bytmp])
                    kb.op("POOL", lambda e: e.tensor_tensor(xt[:, sl], ytmp[:], xt[:, sl], ALU.add), reads=[Bytmp, Bxt], writes=[Bxt])
                kb.dma("POOL", x_dst[t * 128:(t + 1) * 128, :], xt[:], reads=[Bxt])
        T.close()

    def build(self):
        kb = self.kb
        self.declare()
        self.setup()
        G = self.G
        self.epsb, self.Bepsb = G.sb("epsb", [128, 1], F32)
        kb.op("POOL", lambda e: e.memset(self.epsb[:], EPS), writes=[self.Bepsb])
        layers = self.cfg.get("layers", [0, 1])
        x_src = self.I["x"]
        for n, li in enumerate(layers):
            x_dst = self.out if n == len(layers) - 1 else self.xmid
            self.L = Scope(kb)
            if li == 0:
                self.gate_sb, self.Bgate = self.L.sb("gates", [128, NT, 24], F32)
            self.compute_mod(li)
            stop = self.cfg.get("stop")
            if stop == "mod":
                self.L.close()
                break
            if li == 0:
                self.phase_a(0, x_src)
                if stop == "A":
                    self.L.close()
                    break
                if not self.cfg.get("skip_moba"):
                    self.moba_part()
                if stop == "moba":
                    self.L.close()
                    break
                self.nsa_part()
                if stop in ("nsa", "cmpmlp"):
                    self.L.close()
                    break
                self.phase_c1(0, x_src, self.I["sp_w_out"])
            else:
                self.phase_a(1, x_src)
                if stop == "A":
                    self.L.close()
                    break
                self.phase_b_diff(0)
                if stop == "B":
                    self.L.close()
                    break
                self.phase_c1(1, x_src, self.I["diff_w_out"])
            if stop == "C1":
                self.L.close()
                break
            self.phase_c2(li, x_dst)
            self.L.close()
            x_src = x_dst
        kb.barrier(engines=("POOL",))
        G.es.close()
        kb.es.close()
        return self.nc

    def epi_norm(self, T, O_ap, BO, vcol, rd_ap, Brd):
        kb = self.kb
        kb.op("DVE", lambda e: e.tensor_scalar(rd_ap, O_ap[:, vcol:vcol + 1], 1e-30, None, ALU.max), reads=[BO], writes=[Brd])
        kb.op("DVE", lambda e: e.reciprocal(rd_ap, rd_ap), reads=[Brd], writes=[Brd])

    def phase_b_sparse(self):
        self.moba_part()
        self.nsa_part()

    def moba_part(self):
        kb, I = self.kb, self.I
        T = Scope(kb)
        QAs = Rot([T.sb("QA", [128, 2, S], BF16) for _ in range(2)])
        KAs = Rot([T.sb("KA", [128, 2, S], BF16) for _ in range(2)])
        Vps = Rot([T.sb("Vp", [128, NT, 2, 65], BF16) for _ in range(2)])
        for (ka, Bka) in KAs.items:
            for hh in range(2):
                kb.dma("POOL", ka[64:80, hh, :], I["c_e16"][:, :], writes=[Bka])
        for (v, Bv) in Vps.items:
            kb.op("POOL", lambda e, v=v: e.memset(v[:, :, :, 64:65], 1.0), writes=[Bv])
        t1, Bt1 = T.sb("t1", [128, 16, 16], F32)
        t2, Bt2 = T.sb("t2", [128, 16, 16], F32)
        kb.dma("SP", t1[:].rearrange("p a b -> p (a b)"), bc_row(I["c_t1"][0:1, :], 256), writes=[Bt1])
        kb.dma("SP", t2[:].rearrange("p a b -> p (a b)"), bc_row(I["c_t2"][0:1, :], 256), writes=[Bt2])
        augs = Rot([T.sb("aug", [128, 128], BF16) for _ in range(2)])
        for (a, Ba) in augs.items:
            kb.op("POOL", lambda e, a=a: e.memset(a[:], 0.0), writes=[Ba])
        km, Bkm = T.sb("km", [64, 16], F32)
        kmb, Bkmb = T.sb("kmb", [64, 16], BF16)
        gms = Rot([T.sb("gm", [128, 16], F32) for _ in range(2)])
        m8s = Rot([T.sb("m8", [128, 8], F32) for _ in range(2)])
        sels = Rot([T.sb("sel", [128, 16], F32) for _ in range(2)])
        R = dict(pS=Rot([T.ps("pS", [128, 512], F32) for _ in range(3)]),
                 pT=Rot([T.sb("pT", [128, 512], BF16) for _ in range(8)]))
        Obs = Rot([T.ps("OTm", [128, 512], F32) for _ in range(2)])
        Otk, BOtk = T.ps("Otok", [128, 512], F32)
        osbs = Rot([T.sb("osb", [128, 512], F32) for _ in range(2)])
        pgs = Rot([T.ps("pgate", [128, 512], F32) for _ in range(1)])
        ptr = Rot([T.ps("ptr", [128, 1024], BF16) for _ in range(1)])
        osts = Rot([T.sb("ost", [128, 4, 128], BF16) for _ in range(2)])
        rds = Rot([T.sb("rd", [128, 1], F32) for _ in range(4)])

        def load_pair(hp):
            QA, BQA = QAs.next()
            KA, BKA = KAs.next()
            Vp, BVp = Vps.next()
            for hh in range(2):
                h = 2 * hp + hh
                kb.dma("SP", QA[0:64, hh, :], self.FT[h * 64:(h + 1) * 64, :], writes=[BQA])
                kb.dma("SP", KA[0:64, hh, :], self.FT[512 + h * 64:512 + (h + 1) * 64, :], writes=[BKA])
            for hh in range(2):
                h = 2 * hp + hh
                tmv = self.TM[:, h * 64:(h + 1) * 64].rearrange("(c p) d -> p c d", p=128)
                for c4 in range(4):
                    kb.dma("SP", Vp[:, c4 * 8:(c4 + 1) * 8, hh, 0:64], tmv[:, c4 * 8:(c4 + 1) * 8, :], writes=[BVp])
            return QA, BQA, KA, BKA, Vp, BVp

        kms = [T.sb("km2", [64, 16], F32) for _ in range(2)]
        kmbs = [T.sb("kmb2", [64, 16], BF16) for _ in range(2)]
        pgt, _ = pgs.items[0]
        pg_slots = Rot([(pgt[:, j * 16:(j + 1) * 16], Buf("pgs%d" % j)) for j in range(8)])
        ptt, _ = ptr.items[0]
        pt_slots = Rot([(ptt[:, j * 128:(j + 1) * 128], Buf("pts%d" % j)) for j in range(8)])
        augs8 = Rot([T.sb("aug8", [128, 128], BF16) for _ in range(8)])
        for (a_, Ba_) in augs8.items:
            kb.op("POOL", lambda e, a_=a_: e.memset(a_[:], 0.0), writes=[Ba_])
        gms8 = Rot([T.sb("gm8", [128, 16], F32) for _ in range(8)])
        m8s8 = Rot([T.sb("m88", [128, 8], F32) for _ in range(8)])
        sels8 = Rot([T.sb("sel8", [128, 16], F32) for _ in range(8)])

        def gating_jobs(QA, BQA, KA, BKA):
            def prep():
                for hh in range(2):
                    km_, Bkm_ = kms[hh]
                    kmb_, Bkmb_ = kmbs[hh]
                    kb.op("DVE", lambda e, hh=hh, km_=km_: e.tensor_reduce(km_[:], KA[0:64, hh, :].rearrange("p (b j) -> p b j", j=256), AX.X, ALU.add),
                          reads=[BKA], writes=[Bkm_])
                    kb.op("DVE", lambda e, km_=km_, kmb_=kmb_: e.tensor_scalar(kmb_[:], km_[:], 1.0 / 256, None, ALU.mult), reads=[Bkm_], writes=[Bkmb_])
            jobs = []
            for hh in range(2):
                for t in range(NT):
                    st = {}

                    def part1(hh=hh, t=t, st=st):
                        own = t // 2
                        kmb_, Bkmb_ = kmbs[hh]
                        pg, Bpg = pg_slots.next()
                        kb.op("PE", lambda e: e.matmul(pg, QA[0:64, hh, t * 128:(t + 1) * 128], kmb_[:], start=True, stop=True),
                              reads=[BQA, Bkmb_], writes=[Bpg])
                        gm, Bgm = gms8.next()
                        m8, Bm8 = m8s8.next()
                        sel, Bsel = sels8.next()
                        aug, Baug = augs8.next()
                        kb.op("DVE", lambda e: e.tensor_tensor(gm[:], pg, t1[:, own, :], ALU.add), reads=[Bpg, Bt1], writes=[Bgm])
                        kb.op("DVE", lambda e: e.max(m8[:], gm[:]), reads=[Bgm], writes=[Bm8])
                        kb.op("DVE", lambda e: e.tensor_scalar(m8[:, 2:3], m8[:, 2:3], -1e29, None, ALU.max), reads=[Bm8], writes=[Bm8])
                        kb.op("DVE", lambda e: e.tensor_scalar(sel[:], gm[:], m8[:, 2:3], None, ALU.is_ge), reads=[Bgm, Bm8], writes=[Bsel])
                        kb.op("DVE", lambda e: e.tensor_tensor(sel[:], sel[:], t2[:, own, :], ALU.max), reads=[Bsel, Bt2], writes=[Bsel])
                        kb.op("DVE", lambda e: e.tensor_scalar(aug[:, 64:80], sel[:], -1.0, -MASKV, ALU.add, ALU.mult), reads=[Bsel], writes=[Baug])
                        st["aug"] = (aug, Baug)

                    def part2(hh=hh, t=t, st=st):
                        aug, Baug = st["aug"]
                        pt, Bpt = pt_slots.next()
                        kb.op("PE", lambda e: e.transpose(pt, aug[:], self.ident[:]), reads=[Baug, self.Bident], writes=[Bpt])
                        kb.op("ACT", lambda e: e.copy(QA[64:80, hh, t * 128:(t + 1) * 128], pt[64:80, :]), reads=[Bpt], writes=[BQA])
                    jobs.append((part1, part2))
            return prep, jobs

        nxt = load_pair(0)
        pending = []
        prep0, jobs0 = gating_jobs(nxt[0], nxt[1], nxt[2], nxt[3])
        prep0()
        for j0 in range(0, len(jobs0), 4):
            for p1, _ in jobs0[j0:j0 + 4]:
                p1()
            for _, p2 in jobs0[j0:j0 + 4]:
                p2()
        for hp in range(4):
            QA, BQA, KA, BKA, Vp, BVp = nxt
            njobs = None
            if hp + 1 < 4:
                nxt = load_pair(hp + 1)
                nprep, njobs = gating_jobs(nxt[0], nxt[1], nxt[2], nxt[3])
            for qt in range(8):
                pairs = self.causal_pairs(qt)
                ost, Bost = osts.next()
                for hh in range(2):
                    OTm, BOTm = Obs.next()
                    hooks = None
                    if njobs is not None:
                        ci = qt * 2 + hh
                        mine = njobs[ci * 4:(ci + 1) * 4]

                        def h1(mine=mine, first=(ci == 0)):
                            if first:
                                nprep()
                            for p1, _ in mine:
                                p1()

                        def h2(mine=mine):
                            for _, p2 in mine:
                                p2()
                        hooks = {0: h1, 6: h2}
                    self.attn_qtile(R, QA[0:80, hh, qt * 512:(qt + 1) * 512], BQA,
                                    lambda kc: (KA[0:80, hh, kc * 128:(kc + 1) * 128], 128), BKA,
                                    lambda kc: Vp[:, kc, hh, :], BVp, 65, pairs, OTm, BOTm, None, vstat=True, hooks=hooks)
                    def epilogue(OTm=OTm, BOTm=BOTm, ost=ost, Bost=Bost, hh=hh, qt=qt, hp=hp):
                        osb, Bosb = osbs.next()
                        self.ot_to_tok(OTm, BOTm, 65, osb, Bosb, Otk, BOtk)
                        Om, BOm = Otk, BOtk
                        for s in range(4):
                            rd, Brd = rds.next()
                            Os_ = Om[:, s * 128:s * 128 + 65]
                            self.epi_norm(T, Os_, BOm, 64, rd[:], Brd)
                            kb.op("DVE", lambda e, s=s, Os_=Os_, rd=rd: e.tensor_scalar(ost[:, s, hh * 64:(hh + 1) * 64], Os_[:, 0:64], rd[:, 0:1], None, ALU.mult),
                                  reads=[BOm, Brd], writes=[Bost])
                        if hh == 1:
                            osv = self.OS[qt * 512:(qt + 1) * 512, hp * 128:(hp + 1) * 128].rearrange("(s p) d -> p s d", p=128)
                            kb.dma("POOL", osv, ost[:], reads=[Bost])
                    if pending:
                        pending.pop()()
                    pending.append(epilogue)
        if pending:
            pending.pop()()
        T.close()

    def nsa_part(self):
        kb, I = self.kb, self.I
        for _ in range(self.cfg.get("pad_dve", 0)):
            kb.op("DVE", lambda e: e.memset(self.epsb[:], EPS), writes=[self.Bepsb])
        P = Scope(kb)
        CKc, BCKc = P.sb("CKc", [64, 2, 256], BF16)
        VCs = [P.sb("VC", [128, 2, 129], BF16) for _ in range(2)]
        kb.op("POOL", lambda e: e.memset(CKc[:], 0.0), writes=[BCKc])
        for g in range(2):
            vc, Bvc = VCs[g]
            for c in range(2):
                kb.dma("POOL", vc[:, c, 64:129], I["c_ov"][c * 128:(c + 1) * 128, :], writes=[Bvc])
        T = Scope(kb)
        CX = [T.sb("CX", [128, S], BF16) for _ in range(2)]
        kb.dma("SP", CX[0][0][:], self.FT[1536:1664, :], writes=[CX[0][1]])
        kb.dma("SP", CX[1][0][:], self.FT[1664:1792, :], writes=[CX[1][1]])
        w1, Bw1 = T.sb("w1", [128, 2 * 32 * 256], BF16)
        w1src = I["cmp_w1"].rearrange("d a l e -> d (a l e)")
        for half in range(2):
            for a in range(2):
                kb.dma("POOL", w1[half * 64:(half + 1) * 64, a * 8192:(a + 1) * 8192], w1src[:, a * 8192:(a + 1) * 8192], writes=[Bw1])
        w1v = w1[:].rearrange("p (a l e) -> p a l e", a=2, l=32)
        posb, Bposb = T.sb("posb", [64, 2, 34], BF16)
        kb.op("POOL", lambda e: e.memset(posb[:], 0.0), writes=[Bposb])
        kb.dma("POOL", posb[:, :, 0:32], I["cmp_posT"][:, :, :], writes=[Bposb])
        w2, Bw2 = T.sb("w2", [128, 2, 2, 64], BF16)
        for kv in range(2):
            for eh in range(2):
                kb.dma("POOL", w2[:, kv, eh, :], I["cmp_w2"][kv, eh * 128:(eh + 1) * 128, :], writes=[Bw2])
        b1, Bb1 = T.sb("b1", [128, 4], F32)
        pbs = Rot([T.ps("pb", [128, 512], F32) for _ in range(2)])
        phs = Rot([T.ps("ph", [128, 512], F32) for _ in range(3)])
        for kv in range(2):
            for eh in range(2):
                pb, Bpb = pbs.next()
                for l in range(32):
                    kb.op("PE", lambda e, l=l: e.matmul(pb[:, 0:2], w1v[0:64, kv, l, eh * 128:(eh + 1) * 128],
                                                        posb[0:64, kv, l:l + 2], start=(l == 0), stop=(l == 31)),
                          reads=[Bw1, Bposb], writes=[Bpb])
                kb.op("DVE", lambda e: e.tensor_copy(b1[:, kv * 2 + eh:kv * 2 + eh + 1], pb[:, 0:1]), reads=[Bpb], writes=[Bb1])
        hid = {}
        for kv in range(2):
            cx, Bcx = CX[kv]
            cxv = cx[:].rearrange("p (n j) -> p n j", j=16)
            for g in range(2):
                for eh in range(2):
                    ph, Bph = phs.next()
                    for l in range(32):
                        a = l // 16
                        kb.op("PE", lambda e, l=l, a=a: e.matmul(ph[:, 0:255], w1v[64 * g:64 * g + 64, kv, l, eh * 128:(eh + 1) * 128],
                                                                 cxv[64 * g:64 * g + 64, a:a + 255, l % 16], start=(l == 0), stop=(l == 31)),
                              reads=[Bw1, Bcx], writes=[Bph])
                    ht, Bht = T.sb("hid", [128, 256], BF16)
                    kb.op("POOL", lambda e: e.memset(ht[:], 0.0), writes=[Bht])
                    kb.op("ACT", lambda e: e.activation(ht[:, 0:255], ph[:, 0:255], AF.Silu, bias=b1[:, kv * 2 + eh:kv * 2 + eh + 1]),
                          reads=[Bph, Bb1], writes=[Bht])
                    hid[(kv, g, eh)] = (ht, Bht)
        for g in range(2):
            ph, Bph = phs.next()
            for eh in range(2):
                ht, Bht = hid[(0, g, eh)]
                kb.op("PE", lambda e: e.matmul(ph[0:64, 0:255], w2[:, 0, eh, :], ht[:, 0:255], start=(eh == 0), stop=(eh == 1)),
                      reads=[Bw2, Bht], writes=[Bph])
            kb.op("ACT", lambda e: e.copy(CKc[:, g, 0:255], ph[0:64, 0:255]), reads=[Bph], writes=[BCKc])
            vc, Bvc = VCs[g]
            for c in range(2):
                ph, Bph = phs.next()
                for eh in range(2):
                    ht, Bht = hid[(1, g, eh)]
                    kb.op("PE", lambda e: e.matmul(ph[:, 0:64], ht[:, c * 128:(c + 1) * 128], w2[:, 1, eh, :], start=(eh == 0), stop=(eh == 1)),
                          reads=[Bw2, Bht], writes=[Bph])
                kb.op("ACT", lambda e: e.copy(vc[:, c, 0:64], ph[:, 0:64]), reads=[Bph], writes=[Bvc])
        T.close()
        if self.cfg.get("stop") == "cmpmlp":
            if self.cfg.get("debug"):
                dck = self.tap("d_ckc", [64, 512], BF16)
                kb.dma("SP", dck[:, :], CKc[:].rearrange("p g n -> p (g n)"), reads=[BCKc])
                for g in range(2):
                    dvc = self.tap("d_vc%d" % g, [128, 258], BF16)
                    kb.dma("SP", dvc[:, :], VCs[g][0][:].rearrange("p c n -> p (c n)"), reads=[VCs[g][1]])
            P.close()
            return

        T = Scope(kb)
        nfv, Bnfv = T.sb("nfv", [128, NT, 64], F32)
        cst, Bcst = T.sb("cst", [128, NT, 64], F32)
        v01, Bv01 = T.sb("v01", [128, NT, 64], F32)
        kb.dma("SP", nfv[:], I["c_nfv"][:, :, :], writes=[Bnfv])
        kb.dma("SP", cst[:], I["c_cst"][:, :, :], writes=[Bcst])
        kb.dma("SP", v01[:], I["c_v01"][:, :, :], writes=[Bv01])
        QA4, BQA4 = T.sb("QA4", [128, 4, S], BF16)
        KS, BKS = T.sb("KS", [128, S], BF16)
        KW, BKW = T.sb("KW", [64, S], BF16)
        VS, BVS = T.sb("VS", [128, NT, 65], BF16)
        VW, BVW = T.sb("VW", [128, NT, 65], BF16)
        kb.op("POOL", lambda e: e.memset(VS[:, :, 64:65], 1.0), writes=[BVS])
        kb.op("POOL", lambda e: e.memset(VW[:, :, 64:65], 1.0), writes=[BVW])
        kb.dma("POOL", KS[64:128, :], I["c_e64"][:, :], writes=[BKS])
        augs = Rot([T.sb("aug", [128, 128], BF16) for _ in range(2)])
        for (a, Ba) in augs.items:
            kb.op("POOL", lambda e, a=a: e.memset(a[:], 0.0), writes=[Ba])
        R = dict(pS=Rot([T.ps("pS", [128, 512], F32) for _ in range(2)]),
                 pT=Rot([T.sb("pT", [128, 512], BF16) for _ in range(8)]))
        Oc = [T.ps("Oc", [128, 512], F32) for _ in range(2)]
        BOc = Oc[0][1]
        Osw = Rot([T.ps("OTsw", [128, 512], F32) for _ in range(2)])
        Otk, BOtk = T.ps("Otok", [128, 512], F32)
        osbs = Rot([T.sb("osb", [128, 512], F32) for _ in range(2)])
        ptr = Rot([T.ps("ptr", [128, 1024], BF16) for _ in range(1)])
        occ, Bocc = T.sb("occ", [128, 4, 4, 64], F32)
        imp, Bimp = T.sb("imp", [128, 4, 64], F32)
        itmp, Bitmp = T.sb("itmp", [128, 64], F32)
        sc, Bsc = T.sb("sc", [128, 64], F32)
        sc2, Bsc2 = T.sb("sc2", [128, 64], F32)
        m8a, Bm8a = T.sb("m8a", [128, 8], F32)
        m8b, Bm8b = T.sb("m8b", [128, 8], F32)
        selm, Bselm = T.sb("selm", [128, 64], F32)
        rds = Rot([T.sb("rd", [128, 2], F32) for _ in range(6)])
        osts = Rot([T.sb("ost", [128, 4, 256], BF16) for _ in range(2)])
        gsb = self.gate_sb

        for g in range(2):
            for r in range(4):
                h = 4 * g + r
                kb.dma("SP", QA4[0:64, r, :], self.FT[1024 + h * 64:1024 + (h + 1) * 64, :], writes=[BQA4])
            kb.dma("SP", KS[0:64, :], self.FT[1792 + 64 * g:1792 + 64 * g + 64, :], writes=[BKS])
            kb.dma("SP", KW[:], self.FT[1920 + 64 * g:1920 + 64 * g + 64, :], writes=[BKW])
            tms = self.TM[:, 512 + 64 * g:512 + 64 * g + 64].rearrange("(c p) d -> p c d", p=128)
            tmw = self.TM[:, 640 + 64 * g:640 + 64 * g + 64].rearrange("(c p) d -> p c d", p=128)
            for c4 in range(4):
                kb.dma("SP", VS[:, c4 * 8:(c4 + 1) * 8, 0:64], tms[:, c4 * 8:(c4 + 1) * 8, :], writes=[BVS])
                kb.dma("SP", VW[:, c4 * 8:(c4 + 1) * 8, 0:64], tmw[:, c4 * 8:(c4 + 1) * 8, :], writes=[BVW])
            vc, Bvc = VCs[g]
            parts = self.cfg.get("nsa_parts", ("cmp", "select", "sw"))
            for qt in self.cfg.get("nsa_qts", range(8)):
                ost, Bost = osts.next()
                cchunks = [0] + ([1] if qt >= 4 else [])
                cpairs = [(c, [(s, "full") for s in range(4)]) for c in cchunks]

                def cmp_mask(c, pT, BpT, krows):
                    kb.op("POOL", lambda e: e.affine_select(pT[:], pT[:], [[1, 512]], ALU.is_ge, 0.0,
                                                            base=512 * qt - 2048 * c - 31, channel_multiplier=-16),
                          reads=[BpT], writes=[BpT])

                for r in (range(4) if "cmp" in parts else ()):
                    h = 4 * g + r
                    self.attn_qtile(R, QA4[0:64, r, qt * 512:(qt + 1) * 512], BQA4,
                                    lambda c: (CKc[:, g, c * 128:(c + 1) * 128], 128), BCKc,
                                    lambda c: vc[:, c, :], Bvc, 129, cpairs, None, BOc,
                                    lambda O, s: Oc[s // 2][0][:, (s % 2) * 256:(s % 2) * 256 + 129], post_exp=cmp_mask, bank_of=lambda s: s // 2)
                    for s in range(4):
                        t = 4 * qt + s
                        O_ = Oc[s // 2][0][:, (s % 2) * 256:(s % 2) * 256 + 129]
                        rd, Brd = rds.next()
                        self.epi_norm(T, O_, BOc, 64, rd[:, 0:1], Brd)
                        if r == 0:
                            kb.op("DVE", lambda e, s=s: e.tensor_scalar(imp[:, s, :], O_[:, 65:129], rd[:, 0:1], None, ALU.mult),
                                  reads=[BOc, Brd], writes=[Bimp])
                        else:
                            kb.op("DVE", lambda e: e.tensor_scalar(itmp[:], O_[:, 65:129], rd[:, 0:1], None, ALU.mult),
                                  reads=[BOc, Brd], writes=[Bitmp])
                            kb.op("DVE", lambda e, s=s: e.tensor_tensor(imp[:, s, :], imp[:, s, :], itmp[:], ALU.add),
                                  reads=[Bitmp, Bimp], writes=[Bimp])
                        kb.op("DVE", lambda e: e.tensor_tensor(rd[:, 1:2], rd[:, 0:1], gsb[:, t, h:h + 1], ALU.mult), reads=[Brd, self.Bgate], writes=[Brd])
                        kb.op("DVE", lambda e, s=s, r=r: e.tensor_scalar(occ[:, r, s, :], O_[:, 0:64], rd[:, 1:2], None, ALU.mult),
                              reads=[BOc, Brd], writes=[Bocc])
                for s in (range(4) if "select" in parts else ()):
                    t = 4 * qt + s
                    aug, Baug = augs.next()
                    kb.op("DVE", lambda e: e.tensor_tensor(sc[:], imp[:, s, :], nfv[:, t, :], ALU.mult), reads=[Bimp, Bnfv], writes=[Bsc])
                    kb.op("DVE", lambda e: e.tensor_tensor(sc[:], sc[:], cst[:, t, :], ALU.add), reads=[Bsc, Bcst], writes=[Bsc])
                    kb.op("DVE", lambda e: e.max(m8a[:], sc[:]), reads=[Bsc], writes=[Bm8a])
                    kb.op("DVE", lambda e: e.match_replace(sc2[:], m8a[:], sc[:], -1e30), reads=[Bsc, Bm8a], writes=[Bsc2])
                    kb.op("DVE", lambda e: e.max(m8b[:], sc2[:]), reads=[Bsc2], writes=[Bm8b])
                    kb.op("DVE", lambda e: e.tensor_scalar(selm[:], sc[:], m8b[:, 7:8], None, ALU.is_ge), reads=[Bsc, Bm8b], writes=[Bselm])
                    kb.op("DVE", lambda e: e.tensor_tensor(selm[:], selm[:], v01[:, t, :], ALU.mult), reads=[Bselm, Bv01], writes=[Bselm])
                    kb.op("DVE", lambda e: e.tensor_scalar(aug[:, 64:128], selm[:], -1.0, -MASKV, ALU.add, ALU.mult), reads=[Bselm], writes=[Baug])
                    pt, Bpt = ptr.next()
                    kb.op("PE", lambda e: e.transpose(pt[:, 0:128], aug[:], self.ident[:]), reads=[Baug, self.Bident], writes=[Bpt])
                    for r in range(4):
                        kb.op("ACT", lambda e, r=r: e.copy(QA4[64:128, r, t * 128:(t + 1) * 128], pt[64:128, 0:128]), reads=[Bpt], writes=[BQA4])
                pending = []
                spairs = self.causal_pairs(qt)
                wpairs = []
                for kc in range(max(0, 4 * qt - 4), 4 * qt + 4):
                    subs = []
                    for s in range(4):
                        dlt = 4 * qt + s - kc
                        if dlt == 0:
                            subs.append((s, "tri"))
                        elif 1 <= dlt <= 3:
                            subs.append((s, "full"))
                        elif dlt == 4:
                            subs.append((s, "atri"))
                    if subs:
                        wpairs.append((kc, subs))
                for r in (range(4) if "sw" in parts else ()):
                    h = 4 * g + r
                    for br, (pairs, qrows, kl, Bkl, vt, Bvt) in enumerate((
                            (spairs, 128, KS, BKS, VS, BVS), (wpairs, 64, KW, BKW, VW, BVW))):
                        if br not in self.cfg.get("nsa_br", (0, 1)):
                            continue
                        OTb, BOTb = Osw.next()
                        self.attn_qtile(R, QA4[0:qrows, r, qt * 512:(qt + 1) * 512], BQA4,
                                        lambda kc, kl=kl, qrows=qrows: (kl[0:qrows, kc * 128:(kc + 1) * 128], 128), Bkl,
                                        lambda kc, vt=vt: vt[:, kc, :], Bvt, 65, pairs, OTb, BOTb, None, vstat=True)
                        def epilogue(OTb=OTb, BOTb=BOTb, br=br, r=r, h=h, qt=qt, ost=ost, Bost=Bost):
                            osb, Bosb = osbs.next()
                            self.ot_to_tok(OTb, BOTb, 65, osb, Bosb, Otk, BOtk)
                            Ob, BOb = Otk, BOtk
                            for s in range(4):
                                t = 4 * qt + s
                                O_ = Ob[:, s * 128:s * 128 + 65]
                                rd, Brd = rds.next()
                                self.epi_norm(T, O_, BOb, 64, rd[:, 0:1], Brd)
                                gcol = (br + 1) * 8 + h
                                kb.op("DVE", lambda e, rd=rd, t=t, gcol=gcol: e.tensor_tensor(rd[:, 1:2], rd[:, 0:1], gsb[:, t, gcol:gcol + 1], ALU.mult),
                                      reads=[Brd, self.Bgate], writes=[Brd])
                                if br == 0:
                                    kb.op("DVE", lambda e, O_=O_, rd=rd: e.tensor_scalar(itmp[:], O_[:, 0:64], rd[:, 1:2], None, ALU.mult),
                                          reads=[BOb, Brd], writes=[Bitmp])
                                    kb.op("DVE", lambda e, s=s: e.tensor_tensor(occ[:, r, s, :], occ[:, r, s, :], itmp[:], ALU.add),
                                          reads=[Bitmp, Bocc], writes=[Bocc])
                                else:
                                    kb.op("DVE", lambda e, s=s, O_=O_, rd=rd: e.scalar_tensor_tensor(ost[:, s, r * 64:(r + 1) * 64], O_[:, 0:64], rd[:, 1:2], occ[:, r, s, :], ALU.mult, ALU.add),
                                          reads=[BOb, Brd, Bocc], writes=[Bost])
                        if pending:
                            pending.pop()()
                        pending.append(epilogue)
                if pending:
                    pending.pop()()
                osv = self.OS[qt * 512:(qt + 1) * 512, 512 + g * 256:512 + (g + 1) * 256].rearrange("(s p) d -> p s d", p=128)
                kb.dma("POOL", osv, ost[:], reads=[Bost])
                if self.cfg.get("nsa_barrier"):
                    kb.barrier()
        T.close()
        P.close()


def _consts():
    c = {}
    c["c_ident"] = np.eye(128, dtype=np.float32)
    k = np.arange(128)[:, None]
    q = np.arange(128)[None, :]
    c["c_tri"] = (q >= k).astype(np.float32)
    c["c_atri"] = (k > q).astype(np.float32)
    key = np.arange(S)[None, :]
    c["c_e16"] = (key // 256 == np.arange(16)[:, None]).astype(np.float32)
    c["c_e64"] = (key // 64 == np.arange(64)[:, None]).astype(np.float32)
    ncmp = 255
    cs = np.arange(ncmp) * 16
    ss = np.arange(64) * 64
    ov = np.minimum(cs[:, None] + 32, ss[None, :] + 64) - np.maximum(cs[:, None], ss[None, :])
    ovp = np.zeros((256, 65), np.float32)
    ovp[:255, 0] = 1.0
    ovp[:255, 1:] = np.clip(ov, 0, None) / 32.0
    c["c_ov"] = ovp
    own = np.arange(16)[:, None]
    blk = np.arange(16)[None, :]
    c["c_t1"] = np.where(blk < own, 0.0, -1e30).astype(np.float32).reshape(1, 256)
    c["c_t2"] = (blk == own).astype(np.float32).reshape(1, 256)
    p = np.arange(128)[:, None, None]
    t = np.arange(NT)[None, :, None]
    m = np.arange(64)[None, None, :]
    cur = (t * 128 + p) // 64
    ok = m <= cur
    forced = ok & ((m == 0) | (m >= cur - 1))
    c["c_nfv"] = (ok & ~forced).astype(np.float32)
    c["c_cst"] = np.where(ok, np.where(forced, 1e4 + m, 0.0), -1e30).astype(np.float32)
    c["c_v01"] = ok.astype(np.float32)
    return c


def _core_inputs(b, inp, consts):
    f = lambda a: np.ascontiguousarray(np.asarray(a), dtype=np.float32)
    m = {}
    m["x"] = f(inp["x"][b])
    m["cT"] = f(np.asarray(inp["c"][b]).reshape(8, 128).T)
    m["posT"] = np.ascontiguousarray(np.asarray(inp["positions"][b]).reshape(NT, 128).T.astype(np.int32))
    for k in ("ada_w", "ada_b", "attn_norm", "ffn_norm", "ffn_w_gate", "ffn_w_up", "ffn_w_down"):
        m[k] = f(inp[k])
    m["sp_w_in"] = f(inp["sp_w_in"][0]); m["sp_w_out"] = f(inp["sp_w_out"][0])
    m["moba_q_norm"] = f(inp["moba_q_norm"]); m["moba_k_norm"] = f(inp["moba_k_norm"])
    m["nsa_q_norm"] = f(inp["nsa_q_norm"]); m["nsa_k_norm"] = f(inp["nsa_k_norm"][0])
    m["cmp_posT"] = f(np.asarray(inp["nsa_cmp_pos"][0]).transpose(2, 0, 1))
    m["cmp_w1"] = f(np.asarray(inp["nsa_cmp_w1"][0]).transpose(2, 0, 1, 3))
    m["cmp_w2"] = f(inp["nsa_cmp_w2"][0])
    m["diff_w_in"] = f(inp["diff_w_in"][0]); m["diff_w_out"] = f(inp["diff_w_out"][0])
    m["diff_q_norm"] = f(inp["diff_q_norm"]); m["diff_k_norm"] = f(inp["diff_k_norm"])
    m["diff_lambda"] = f(np.asarray(inp["diff_lambda"][0]).reshape(1, 256))
    m["diff_out_norm"] = f(inp["diff_out_norm"])
    m.update(consts)
    return m


_NC_CACHE = {}


def kernel(**inputs):
    consts = _consts()
    if "nc" not in _NC_CACHE:
        _NC_CACHE["nc"] = Prog({}).build()
    nc = _NC_CACHE["nc"]
    in_maps = [_core_inputs(b, inputs, consts) for b in range(8)]
    res = run_bass_kernel_spmd(nc, in_maps, core_ids=list(range(8)))
    return np.stack([np.asarray(r["out"], dtype=np.float32) for r in res.results], axis=0)
```

```python
from contextlib import ExitStack
import numpy as np
import concourse.bass as bass
import concourse.mybir as mybir
from concourse.bass_utils import run_bass_kernel_spmd

F32 = mybir.dt.float32
BF16 = mybir.dt.bfloat16
I32 = mybir.dt.int32
AF = mybir.ActivationFunctionType
ALU = mybir.AluOpType
AX = mybir.AxisListType

S = 4096
D = 1024
NT = S // 128
FFN = 2816
NJ = FFN // 128
EPS = 1e-6
MASKV = -30000.0
SP_IN = 2840
N_DMA_SEMS = 24


class Buf:
    __slots__ = ("w", "r", "name")

    def __init__(self, name=""):
        self.w = None
        self.r = {}
        self.name = name


class Rot:
    def __init__(self, items):
        self.items = list(items)
        self.i = 0

    def next(self):
        it = self.items[self.i]
        self.i = (self.i + 1) % len(self.items)
        return it


class KB:
    def __init__(self, nc):
        self.nc = nc
        self.es = ExitStack()
        self.engs = {"PE": nc.tensor, "ACT": nc.scalar, "DVE": nc.vector, "POOL": nc.gpsimd, "SP": nc.sync}
        self.sem = {}
        self.cnt = {}
        for e in ("PE", "ACT", "DVE", "POOL"):
            self.sem[e] = self.es.enter_context(nc.semaphore("s_" + e))
            self.cnt[e] = 0
        self.dsem = [self.es.enter_context(nc.semaphore("d_%d" % i)) for i in range(N_DMA_SEMS)]
        self.dcnt = [0] * N_DMA_SEMS
        self.dnext = 0
        self.dnext2 = [0, 0]
        self.waited = {e: {} for e in self.engs}
        self.n_inst = 0
        self.uid = 0
        self.limit = None
        self.log = None

    def name(self, p):
        self.uid += 1
        return "%s_%d" % (p, self.uid)

    def _wait(self, E, key, c, raw=False):
        if key == E and E == "PE":
            return
        w = self.waited[E]
        if w.get(key, 0) >= c:
            return
        w[key] = c
        if isinstance(key, int):
            self.engs[E].wait_ge(self.dsem[key], c)
        else:
            self.engs[E].wait_ge(self.sem[key], c)

    def _deps(self, E, reads, writes):
        for b in reads:
            if b.w is not None:
                self._wait(E, b.w[0], b.w[1], raw=True)
        for b in writes:
            if b.w is not None:
                self._wait(E, b.w[0], b.w[1])
            for k, c in b.r.items():
                self._wait(E, k, c)

    def _mark(self, tok, reads, writes):
        for b in reads:
            if b.r.get(tok[0], 0) < tok[1]:
                b.r[tok[0]] = tok[1]
        for b in writes:
            b.w = tok
            b.r = {}

    def op(self, E, fn, reads=(), writes=()):
        if self.limit is not None and self.n_inst >= self.limit:
            return None
        self._deps(E, reads, writes)
        if self.log is not None:
            import sys as _sys
            self.log.append((self.n_inst, E, _sys._getframe(1).f_lineno))
        ins = fn(self.engs[E])
        self.cnt[E] += 1
        ins.then_inc(self.sem[E], 1)
        tok = (E, self.cnt[E])
        self._mark(tok, reads, writes)
        self.n_inst += 1
        return tok

    def dma(self, Q, out_ap, in_ap, reads=(), writes=(), **kw):
        if self.limit is not None and self.n_inst >= self.limit:
            return None
        self._deps(Q, reads, writes)
        half = N_DMA_SEMS // 2
        qi = 0 if Q == "SP" else 1
        k = qi * half + self.dnext2[qi]
        self.dnext2[qi] = (self.dnext2[qi] + 1) % half
        if self.dcnt[k] > 0:
            self._wait(Q, k, self.dcnt[k])
        if self.log is not None:
            import sys as _sys
            self.log.append((self.n_inst, "DMA-" + Q, _sys._getframe(1).f_lineno))
        ins = self.engs[Q].dma_start(out=out_ap, in_=in_ap, **kw)
        self.dcnt[k] += 16
        ins.then_inc(self.dsem[k], 16)
        tok = (k, self.dcnt[k])
        self._mark(tok, reads, writes)
        self.n_inst += 1
        return tok

    def barrier(self, engines=("PE", "ACT", "DVE", "POOL", "SP")):
        for E in engines:
            for e2 in ("PE", "ACT", "DVE", "POOL"):
                if self.cnt[e2] > 0:
                    self._wait(E, e2, self.cnt[e2])
            for k in range(N_DMA_SEMS):
                if self.dcnt[k] > 0:
                    self._wait(E, k, self.dcnt[k])


class Scope:
    def __init__(self, kb):
        self.kb = kb
        self.es = ExitStack()

    def sb(self, name, shape, dt):
        t = self.es.enter_context(self.kb.nc.sbuf_tensor(self.kb.name(name), list(shape), dt))
        return t, Buf(name)

    def ps(self, name, shape, dt):
        t = self.es.enter_context(self.kb.nc.psum_tensor(self.kb.name(name), list(shape), dt))
        return t, Buf(name)

    def close(self):
        self.kb.barrier()
        self.es.close()


def bc_row(ap_row, n):
    return ap_row.to_broadcast([128, n])


class Prog:
    def __init__(self, cfg):
        self.cfg = cfg
        self.nc = bass.Bass("TRN2", target_bir_lowering=False)
        self.kb = KB(self.nc)
        self.kb.limit = cfg.get("limit")
        self.I = {}
        self.taps = {}

    def din(self, name, shape, dt=F32):
        self.I[name] = self.nc.dram_tensor(name, list(shape), dt, kind="ExternalInput").ap()
        return self.I[name]

    def dscr(self, name, shape, dt):
        if self.cfg.get("debug"):
            return self.nc.dram_tensor(name, list(shape), dt, kind="ExternalOutput").ap()
        return self.nc.dram_tensor(name, list(shape), dt).ap()

    def tap(self, name, shape, dt=F32):
        self.taps[name] = self.nc.dram_tensor(name, list(shape), dt, kind="ExternalOutput").ap()
        return self.taps[name]

    def declare(self):
        d = self.din
        d("x", [S, D]); d("cT", [128, 8]); d("posT", [128, NT], I32)
        d("ada_w", [2, D, 6 * D]); d("ada_b", [2, 6 * D])
        d("attn_norm", [2, D]); d("ffn_norm", [2, D])
        d("ffn_w_gate", [2, D, FFN]); d("ffn_w_up", [2, D, FFN]); d("ffn_w_down", [2, FFN, D])
        d("sp_w_in", [D, SP_IN]); d("sp_w_out", [D, D])
        d("moba_q_norm", [1, 64]); d("moba_k_norm", [1, 64]); d("nsa_q_norm", [1, 64]); d("nsa_k_norm", [3, 64])
        d("cmp_posT", [64, 2, 32]); d("cmp_w1", [64, 2, 32, 256]); d("cmp_w2", [2, 256, 64])
        d("diff_w_in", [D, 3072]); d("diff_w_out", [D, D])
        d("diff_q_norm", [1, 64]); d("diff_k_norm", [1, 64]); d("diff_lambda", [1, 256]); d("diff_out_norm", [1, 128])
        d("c_ident", [128, 128]); d("c_tri", [128, 128]); d("c_atri", [128, 128])
        d("c_e16", [16, S]); d("c_e64", [64, S]); d("c_ov", [256, 65])
        d("c_t1", [1, 256]); d("c_t2", [1, 256])
        d("c_nfv", [128, NT, 64]); d("c_cst", [128, NT, 64]); d("c_v01", [128, NT, 64])
        self.out = self.nc.dram_tensor("out", [S, D], F32, kind="ExternalOutput").ap()
        self.xmid = self.dscr("xmid", [S, D], F32)
        self.x1s = self.dscr("x1s", [S, D], F32)
        self.FT = self.dscr("FT", [2048, S], BF16)
        self.TM = self.dscr("TM", [S, D], BF16)
        self.OS = self.dscr("OS", [S, D], BF16)
        self.H2T = self.dscr("H2T", [D, S], BF16)

    def setup(self):
        kb, I = self.kb, self.I
        self.G = Scope(kb)
        G = self.G
        self.ident, self.Bident = G.sb("ident", [128, 128], BF16)
        kb.dma("POOL", self.ident[:], I["c_ident"][:, :], writes=[self.Bident])
        self.identF, self.BidentF = G.sb("identF", [128, 128], F32)
        kb.dma("SP", self.identF[:], I["c_ident"][:, :], writes=[self.BidentF])
        self.tri, self.Btri = G.sb("tri", [128, 128], BF16)
        kb.dma("POOL", self.tri[:], I["c_tri"][:, :], writes=[self.Btri])
        self.atri, self.Batri = G.sb("atri", [128, 128], BF16)
        kb.dma("POOL", self.atri[:], I["c_atri"][:, :], writes=[self.Batri])
        self.cosT, self.Bcos = G.sb("cosT", [128, NT, 8], F32)
        self.sinT, self.Bsin = G.sb("sinT", [128, NT, 8], F32)
        self.mod, self.Bmod = G.sb("mod", [128, 6, D], F32)
        self.silc, self.Bsilc = G.sb("silc", [128, 8, 128], F32)
        with_scope = Scope(kb)
        T = with_scope
        pi, Bpi = T.sb("pi", [128, NT], I32)
        pf, Bpf = T.sb("pf", [128, NT], F32)
        ang, Bang = T.sb("ang", [128, NT, 8], F32)
        tmp, Btmp = T.sb("tmp", [128, NT, 8], F32)
        tmp2, Btmp2 = T.sb("tmp2", [128, NT, 8], F32)
        kb.dma("SP", pi[:], I["posT"][:, :], writes=[Bpi])
        kb.op("DVE", lambda e: e.tensor_copy(pf[:], pi[:]), reads=[Bpi], writes=[Bpf])
        inv = (1.0 / (np.float32(500000.0) ** (np.arange(0, 16, 2, dtype=np.float32) / np.float32(16)))).astype(np.float32)
        for j in range(8):
            kb.op("DVE", lambda e, j=j: e.tensor_scalar(ang[:, :, j], pf[:], float(inv[j]), None, ALU.mult),
                  reads=[Bpf], writes=[Bang])
        TWO_PI = float(2 * np.pi)
        MAG = 12582912.0

        def sin_of(dst, Bdst, shift):
            if shift != 0.0:
                kb.op("DVE", lambda e: e.tensor_scalar(tmp2[:], ang[:], shift, None, ALU.add), reads=[Bang], writes=[Btmp2])
                src, Bsrc = tmp2, Btmp2
            else:
                src, Bsrc = ang, Bang
            kb.op("DVE", lambda e: e.tensor_scalar(tmp[:], src[:], 1.0 / TWO_PI, MAG, ALU.mult, ALU.add), reads=[Bsrc], writes=[Btmp])
            kb.op("DVE", lambda e: e.tensor_scalar(tmp[:], tmp[:], -MAG, -TWO_PI, ALU.add, ALU.mult), reads=[Btmp], writes=[Btmp])
            kb.op("DVE", lambda e: e.tensor_tensor(tmp[:], src[:], tmp[:], ALU.add), reads=[Bsrc, Btmp], writes=[Btmp])
            kb.op("DVE", lambda e: e.tensor_scalar(tmp[:], tmp[:], float(np.pi), float(-np.pi), ALU.min, ALU.max), reads=[Btmp], writes=[Btmp])
            kb.op("ACT", lambda e: e.activation(dst[:], tmp[:], AF.Sin), reads=[Btmp], writes=[Bdst])

        sin_of(self.sinT, self.Bsin, 0.0)
        sin_of(self.cosT, self.Bcos, float(np.pi / 2))
        ct, Bct = T.sb("ct", [128, 8], F32)
        kb.dma("SP", ct[:], I["cT"][:, :], writes=[Bct])
        kb.op("ACT", lambda e: e.activation(ct[:], ct[:], AF.Silu), reads=[Bct], writes=[Bct])
        kb.op("DVE", lambda e: e.tensor_copy(self.silc[:], ct[:].unsqueeze(2).to_broadcast([128, 8, 128])),
              reads=[Bct], writes=[self.Bsilc])
        T.close()

    def compute_mod(self, li):
        kb, I = self.kb, self.I
        T = Scope(kb)
        wch = [T.sb("adaw", [128, 8, 512], F32) for _ in range(2)]
        bch = [T.sb("adab", [128, 512], F32) for _ in range(2)]
        pm = [T.ps("pmod", [128, 512], F32) for _ in range(2)]
        nrm, Bnrm = T.sb("nrm", [128, 2, D], F32)
        kb.dma("SP", nrm[:, 0, :], bc_row(I["attn_norm"][li:li + 1, :], D), writes=[Bnrm])
        kb.dma("SP", nrm[:, 1, :], bc_row(I["ffn_norm"][li:li + 1, :], D), writes=[Bnrm])
        wv = I["ada_w"][li].rearrange("(kc p) n -> p kc n", p=128)
        for g in range(12):
            w, Bw = wch[g % 2]
            b, Bb = bch[g % 2]
            p, Bp = pm[g % 2]
            kb.dma("SP", w[:], wv[:, :, g * 512:(g + 1) * 512], writes=[Bw])
            kb.dma("SP", b[:], bc_row(I["ada_b"][li:li + 1, g * 512:(g + 1) * 512], 512), writes=[Bb])
            for kc in range(8):
                kb.op("PE", lambda e, kc=kc: e.matmul(p[:], self.silc[:, kc, :], w[:, kc, :], start=(kc == 0), stop=(kc == 7)),
                      reads=[Bw, self.Bsilc], writes=[Bp])
            sl = self.mod[:, g // 2, (g % 2) * 512:(g % 2 + 1) * 512]
            kb.op("DVE", lambda e: e.tensor_tensor(sl, p[:], b[:], ALU.add), reads=[Bp, Bb], writes=[self.Bmod])
        kb.op("DVE", lambda e: e.scalar_tensor_tensor(self.mod[:, 1, :], self.mod[:, 1, :], 1.0, nrm[:, 0, :], ALU.add, ALU.mult),
              reads=[self.Bmod, Bnrm], writes=[self.Bmod])
        kb.op("DVE", lambda e: e.scalar_tensor_tensor(self.mod[:, 4, :], self.mod[:, 4, :], 1.0, nrm[:, 1, :], ALU.add, ALU.mult),
              reads=[self.Bmod, Bnrm], writes=[self.Bmod])
        T.close()

    def rms_mod_tile(self, T, xt, Bxt, hb, Bhb, gi, si, scr):
        kb = self.kb
        junk, Bjunk, ss, Bss, hn, Bhn = scr
        kb.op("ACT", lambda e: e.activation(junk[:], xt[:], AF.Square, accum_out=ss[:]), reads=[Bxt], writes=[Bjunk, Bss])
        kb.op("ACT", lambda e: e.activation(ss[:], ss[:], AF.Sqrt, bias=self.epsb[:], scale=1.0 / D), reads=[Bss, self.Bepsb], writes=[Bss])
        kb.op("DVE", lambda e: e.reciprocal(ss[:], ss[:]), reads=[Bss], writes=[Bss])
        kb.op("DVE", lambda e: e.scalar_tensor_tensor(hn[:], xt[:], ss[:, 0:1], self.mod[:, gi, :], ALU.mult, ALU.mult),
              reads=[Bxt, Bss, self.Bmod], writes=[Bhn])
        kb.op("POOL", lambda e: e.tensor_tensor(hb[:], hn[:], self.mod[:, si, :], ALU.add), reads=[Bhn, self.Bmod], writes=[Bhb])

    def transpose8(self, src, Bsrc, pt, Bpt, dst_ap, Bdst, eng="ACT"):
        kb = self.kb
        for j in range(8):
            kb.op("PE", lambda e, j=j: e.transpose(pt[:, j * 128:(j + 1) * 128], src[:, j * 128:(j + 1) * 128], self.ident[:]),
                  reads=[Bsrc, self.Bident], writes=[Bpt])
        if eng == "ACT":
            kb.op("ACT", lambda e: e.copy(dst_ap, pt[:].rearrange("p (j t) -> p j t", j=8)), reads=[Bpt], writes=[Bdst])
        else:
            kb.op("DVE", lambda e: e.tensor_copy(dst_ap, pt[:].rearrange("p (j t) -> p j t", j=8)), reads=[Bpt], writes=[Bdst])

    def load_w_bf16(self, dst, Bdst, w_ap, nk):
        for kc in range(nk):
            self.kb.dma("POOL", dst[:, kc, :], w_ap[kc * 128:(kc + 1) * 128, :], writes=[Bdst])

    def phase_a(self, li, x_src):
        kb, I = self.kb, self.I
        T = Scope(kb)
        if li == 1:
            w_ap, ncols = I["diff_w_in"], 3072
            groups = []
            for g in range(6):
                if g < 4:
                    gi = 0 if g < 2 else 1
                    blocks = [("T", g * 512 + b * 128) for b in range(4)]
                    groups.append(dict(c0=g * 512, w=512, qk=[(0, 8)], gain=gi, blocks=blocks, gate=None))
                else:
                    blocks = [("V", (g - 4) * 512 + b * 128) for b in range(4)]
                    groups.append(dict(c0=g * 512, w=512, qk=[], gain=None, blocks=blocks, gate=None))
            gain_srcs = [[(I["diff_q_norm"][0:1, :], 0, 8)], [(I["diff_k_norm"][0:1, :], 0, 8)]]
        else:
            w_ap, ncols = I["sp_w_in"], SP_IN
            kn = I["nsa_k_norm"]
            groups = [
                dict(c0=0, w=512, qk=[(0, 8)], gain=0, blocks=[("T", b * 128) for b in range(4)], gate=None),
                dict(c0=512, w=512, qk=[(0, 8)], gain=1, blocks=[("T", 512 + b * 128) for b in range(4)], gate=None),
                dict(c0=1024, w=512, qk=[], gain=None, blocks=[("V", b * 128) for b in range(4)], gate=None),
                dict(c0=1536, w=512, qk=[(0, 8)], gain=2, blocks=[("T", 1024 + b * 128) for b in range(4)], gate=None),
                dict(c0=2048, w=512, qk=[(0, 2), (4, 6)], gain=3,
                     blocks=[("T", 1536), ("T", 1664), ("T", 1792), ("V", 512)], gate=None),
                dict(c0=2560, w=280, qk=[(0, 2)], gain=4, blocks=[("T", 1920), ("V", 640)], gate=(256, 24)),
            ]
            gain_srcs = [
                [(I["moba_q_norm"][0:1, :], 0, 8)], [(I["moba_k_norm"][0:1, :], 0, 8)], [(I["nsa_q_norm"][0:1, :], 0, 8)],
                [(kn[0:1, :], 0, 2), (kn[1:2, :], 4, 6)], [(kn[2:3, :], 0, 2)],
            ]
        nk = 8
        W, BW = T.sb("w_in", [128, nk, ncols], BF16)
        self.load_w_bf16(W, BW, w_ap, nk)
        gains = []
        for gs in gain_srcs:
            gr, Bgr = T.sb("gainrow", [128, 8, 64], F32)
            kb.op("POOL", lambda e: e.memset(gr[:], 1.0), writes=[Bgr])
            for (src, u0, u1) in gs:
                for u in range(u0, u1):
                    kb.dma("SP", gr[:, u, :], bc_row(src, 64), writes=[Bgr])
            gains.append((gr, Bgr))
        NG = len(groups)
        xts = Rot([T.sb("xt", [128, D], F32) for _ in range(3)])
        hbs = Rot([T.sb("hb", [128, D], BF16) for _ in range(2)])
        hTs = Rot([T.sb("hT", [128, 8, 128], BF16) for _ in range(2)])
        junk, Bjunk = T.sb("junk", [128, D], BF16)
        sss = Rot([T.sb("ss", [128, 1], F32) for _ in range(2)])
        hn, Bhn = T.sb("hn", [128, D], F32)
        sqs = [T.sb("sq", [128, 512], F32) for _ in range(NG)]
        ss8l = [T.sb("ss8", [128, 8], F32) for _ in range(NG)]
        qnl = [T.sb("qn", [128, 8, 64], F32) for _ in range(NG)]
        postl = [T.sb("post", [128, 512], BF16) for _ in range(NG)]
        rtl = [T.sb("rt", [128, 4, 8, 8], F32) for _ in range(NG)]
        pts = Rot([T.ps("ptA", [128, 1024], BF16) for _ in range(1)])
        pgl = [T.ps("pgA", [128, 512], F32) for _ in range(NG)]
        pbt, _ = T.ps("ptB", [128, 1024], BF16)
        pbh = Rot([(pbt[:, 0:512], Buf("pb0")), (pbt[:, 512:1024], Buf("pb1"))])
        stages = {}
        for gi_, g in enumerate(groups):
            if any(k == "T" for k, _ in g["blocks"]):
                stages[gi_] = [T.sb("stage", [128, 4, 512], BF16) for _ in range(2)]

        def load_x(t):
            xt, Bxt = xts.next()
            kb.dma("SP", xt[:], x_src[t * 128:(t + 1) * 128, :], writes=[Bxt])
            return xt, Bxt

        def prep(t, xtb):
            xt, Bxt = xtb
            hb, Bhb = hbs.next()
            ss, Bss = sss.next()
            self.rms_mod_tile(T, xt, Bxt, hb, Bhb, 1, 0, (junk, Bjunk, ss, Bss, hn, Bhn))
            hT, BhT = hTs.next()
            pt, Bpt = pts.next()
            self.transpose8(hb, Bhb, pt, Bpt, hT[:], BhT, eng="ACT")
            return hT, BhT

        xq = [load_x(0)]
        if NT > 1:
            xq.append(load_x(1))
        hq = [prep(0, xq.pop(0))]
        for t in range(NT):
            hT, BhT = hq.pop(0)
            if t + 2 < NT:
                xq.append(load_x(t + 2))
            tt = t % 4
            half = (t // 4) % 2
            for gi_, g in enumerate(groups):
                w = g["w"]
                pg, Bpg = pgl[gi_]
                for kc in range(8):
                    kb.op("PE", lambda e, kc=kc, pg=pg, w=w, g=g: e.matmul(pg[:, 0:w], hT[:, kc, :], W[:, kc, g["c0"]:g["c0"] + w],
                                                                          start=(kc == 0), stop=(kc == 7)),
                          reads=[BhT, BW], writes=[Bpg])
            if t + 1 < NT:
                hq.append(prep(t + 1, xq.pop(0)))
            info = []
            for gi_, g in enumerate(groups):
                w = g["w"]
                wq = (w // 64) * 64
                info.append((gi_, g, w, wq, wq // 64))
            for gi_, g, w, wq, nu in info:
                pg, Bpg = pgl[gi_]
                post, Bpost = postl[gi_]
                sq, Bsq = sqs[gi_]
                if g["qk"]:
                    kb.op("ACT", lambda e, sq=sq, pg=pg, wq=wq: e.activation(sq[:, 0:wq], pg[:, 0:wq], AF.Square), reads=[Bpg], writes=[Bsq])
                else:
                    kb.op("ACT", lambda e, post=post, pg=pg, wq=wq: e.copy(post[:, 0:wq], pg[:, 0:wq]), reads=[Bpg], writes=[Bpost])
                if g["gate"] is not None:
                    gc0, gw = g["gate"]
                    kb.op("ACT", lambda e, pg=pg, gc0=gc0, gw=gw: e.activation(self.gate_sb[:, t, :], pg[:, gc0:gc0 + gw], AF.Sigmoid),
                          reads=[Bpg], writes=[self.Bgate])
            for gi_, g, w, wq, nu in info:
                if not g["qk"]:
                    continue
                sq, Bsq = sqs[gi_]
                ss8, Bss8 = ss8l[gi_]
                kb.op("DVE", lambda e, ss8=ss8, sq=sq, wq=wq, nu=nu: e.tensor_reduce(ss8[:, 0:nu], sq[:, 0:wq].rearrange("p (u d) -> p u d", d=64), AX.X, ALU.add),
                      reads=[Bsq], writes=[Bss8])
            for gi_, g, w, wq, nu in info:
                if not g["qk"]:
                    continue
                ss8, Bss8 = ss8l[gi_]
                kb.op("ACT", lambda e, ss8=ss8, nu=nu: e.activation(ss8[:, 0:nu], ss8[:, 0:nu], AF.Sqrt, bias=self.epsb[:], scale=1.0 / 64),
                      reads=[Bss8, self.Bepsb], writes=[Bss8])
            for gi_, g, w, wq, nu in info:
                if not g["qk"]:
                    continue
                pg, Bpg = pgl[gi_]
                ss8, Bss8 = ss8l[gi_]
                qn, Bqn = qnl[gi_]
                gr, Bgr = gains[g["gain"]]
                kb.op("DVE", lambda e, ss8=ss8, nu=nu: e.reciprocal(ss8[:, 0:nu], ss8[:, 0:nu]), reads=[Bss8], writes=[Bss8])
                qk_units = set()
                for (u0, u1) in g["qk"]:
                    qk_units.update(range(u0, u1))
                for u in [u for u in range(nu) if u not in qk_units]:
                    kb.op("DVE", lambda e, u=u, ss8=ss8: e.memset(ss8[:, u:u + 1], 1.0), writes=[Bss8])
                kb.op("DVE", lambda e, qn=qn, pg=pg, ss8=ss8, nu=nu, wq=wq: e.tensor_tensor(
                    qn[:, 0:nu, :], pg[:, 0:wq].rearrange("p (u d) -> p u d", d=64),
                    ss8[:, 0:nu].unsqueeze(2).to_broadcast([128, nu, 64]), ALU.mult),
                      reads=[Bpg, Bss8], writes=[Bqn])
                kb.op("DVE", lambda e, qn=qn, gr=gr, nu=nu: e.tensor_tensor(qn[:, 0:nu, :], qn[:, 0:nu, :], gr[:, 0:nu, :], ALU.mult),
                      reads=[Bqn, Bgr], writes=[Bqn])
            for gi_, g, w, wq, nu in info:
                if not g["qk"]:
                    continue
                qn, Bqn = qnl[gi_]
                post, Bpost = postl[gi_]
                kb.op("ACT", lambda e, post=post, qn=qn, wq=wq, nu=nu: e.copy(post[:, 0:wq], qn[:, 0:nu, :].rearrange("p u d -> p (u d)")),
                      reads=[Bqn], writes=[Bpost])
            for gi_, g, w, wq, nu in info:
                if not g["qk"]:
                    continue
                qn, Bqn = qnl[gi_]
                post, Bpost = postl[gi_]
                rt, Brt = rtl[gi_]
                postv = post[:, 0:wq].rearrange("p (u d) -> p u d", d=64)
                for (u0, u1) in g["qk"]:
                    n_ = u1 - u0
                    cosb = self.cosT[:, t, :].unsqueeze(1).to_broadcast([128, n_, 8])
                    sinb = self.sinT[:, t, :].unsqueeze(1).to_broadcast([128, n_, 8])
                    t1 = qn[:, u0:u1, 0:8]
                    t2 = qn[:, u0:u1, 8:16]
                    kb.op("DVE", lambda e, rt=rt, n_=n_, t1=t1, cosb=cosb: e.tensor_tensor(rt[:, 0, 0:n_, :], t1, cosb, ALU.mult), reads=[Bqn, self.Bcos], writes=[Brt])
                    kb.op("DVE", lambda e, rt=rt, n_=n_, t2=t2, sinb=sinb: e.tensor_tensor(rt[:, 1, 0:n_, :], t2, sinb, ALU.mult), reads=[Bqn, self.Bsin], writes=[Brt])
                    kb.op("DVE", lambda e, rt=rt, n_=n_, t2=t2, cosb=cosb: e.tensor_tensor(rt[:, 2, 0:n_, :], t2, cosb, ALU.mult), reads=[Bqn, self.Bcos], writes=[Brt])
                    kb.op("DVE", lambda e, rt=rt, n_=n_, t1=t1, sinb=sinb: e.tensor_tensor(rt[:, 3, 0:n_, :], t1, sinb, ALU.mult), reads=[Bqn, self.Bsin], writes=[Brt])
                    kb.op("DVE", lambda e, rt=rt, n_=n_, postv=postv, u0=u0, u1=u1: e.tensor_tensor(postv[:, u0:u1, 0:8], rt[:, 0, 0:n_, :], rt[:, 1, 0:n_, :], ALU.subtract),
                          reads=[Brt], writes=[Bpost])
                    kb.op("DVE", lambda e, rt=rt, n_=n_, postv=postv, u0=u0, u1=u1: e.tensor_tensor(postv[:, u0:u1, 8:16], rt[:, 2, 0:n_, :], rt[:, 3, 0:n_, :], ALU.add),
                          reads=[Brt], writes=[Bpost])
            for gi_, g, w, wq, nu in info:
                post, Bpost = postl[gi_]
                tblocks = [(bi, dst) for bi, (k, dst) in enumerate(g["blocks"]) if k == "T"]
                if tblocks:
                    pb, Bpb = pbh.next()
                    st, Bst = stages[gi_][half]
                    for bi, dst in tblocks:
                        kb.op("PE", lambda e, bi=bi, pb=pb, post=post: e.transpose(pb[:, bi * 128:(bi + 1) * 128], post[:, bi * 128:(bi + 1) * 128], self.ident[:]),
                              reads=[Bpost, self.Bident], writes=[Bpb])
                    b0, b1 = tblocks[0][0], tblocks[-1][0] + 1
                    kb.op("ACT", lambda e, st=st, pb=pb, b0=b0, b1=b1: e.copy(st[:, b0:b1, tt * 128:(tt + 1) * 128],
                                                                              pb[:, b0 * 128:b1 * 128].rearrange("p (b t) -> p b t", t=128)),
                          reads=[Bpb], writes=[Bst])
                    if tt == 3:
                        t0 = (t - 3) * 128
                        for bi, dst in tblocks:
                            kb.dma("POOL", self.FT[dst:dst + 128, t0:t0 + 512], st[:, bi, :], reads=[Bst])
                for bi, (k, dst) in enumerate(g["blocks"]):
                    if k == "V":
                        kb.dma("POOL", self.TM[t * 128:(t + 1) * 128, dst:dst + 128], post[:, bi * 128:(bi + 1) * 128], reads=[Bpost])
        T.close()

    def attn_qtile(self, R, q_rhs, Bq, k_lhsT, Bk, v_rhs, Bv, vw, pairs, O, BO, o_off, post_exp=None, bank_of=lambda s: 0, vstat=False, hooks=None):
        kb = self.kb
        LOOK = self.cfg.get("look", 5)
        last = {}
        started = set()
        for kc, subs in pairs:
            for s, kind in subs:
                last[s] = kc
        live = {}

        def stage1(i):
            kc, subs = pairs[i]
            pS, BpS = R["pS"].next()
            kl, krows = k_lhsT(kc)
            c0 = min(s_ for s_, _ in subs) * 128
            c1 = (max(s_ for s_, _ in subs) + 1) * 128
            if post_exp is not None:
                c0, c1 = 0, 512
            kb.op("PE", lambda e: e.matmul(pS[0:krows, c0:c1], kl, q_rhs[:, c0:c1], start=True, stop=True), reads=[Bk, Bq], writes=[BpS])
            pT, BpT = R["pT"].next()
            kb.op("ACT", lambda e: e.activation(pT[0:krows, c0:c1], pS[0:krows, c0:c1], AF.Exp, scale=0.125), reads=[BpS], writes=[BpT])
            if post_exp is not None:
                post_exp(kc, pT, BpT, krows)
            for s, kind in subs:
                if kind == "tri":
                    kb.op("DVE", lambda e, s=s: e.tensor_tensor(pT[:, s * 128:(s + 1) * 128], pT[:, s * 128:(s + 1) * 128], self.tri[:], ALU.mult),
                          reads=[BpT, self.Btri], writes=[BpT])
                elif kind == "atri":
                    kb.op("DVE", lambda e, s=s: e.tensor_tensor(pT[:, s * 128:(s + 1) * 128], pT[:, s * 128:(s + 1) * 128], self.atri[:], ALU.mult),
                          reads=[BpT, self.Batri], writes=[BpT])
            live[i] = (pT, BpT, krows)

        def stage2(i):
            kc, subs = pairs[i]
            pT, BpT, krows = live.pop(i)
            if vstat:
                c0 = min(s_ for s_, _ in subs) * 128
                c1 = (max(s_ for s_, _ in subs) + 1) * 128
                st_flag = 0 not in started
                started.add(0)
                kb.op("PE", lambda e: e.matmul(O[0:vw, c0:c1], v_rhs(kc)[0:krows, 0:vw], pT[0:krows, c0:c1],
                                               start=st_flag, stop=(i == len(pairs) - 1), skip_group_check=True),
                      reads=[BpT, Bv], writes=[BO])
                return
            for s, kind in subs:
                bk = bank_of(s)
                st_flag = bk not in started
                started.add(bk)
                kb.op("PE", lambda e, s=s, st_flag=st_flag: e.matmul(o_off(O, s), pT[0:krows, s * 128:(s + 1) * 128], v_rhs(kc)[0:krows, :],
                                                                     start=st_flag, stop=(last[s] == kc), skip_group_check=True),
                      reads=[BpT, Bv], writes=[BO])

        n = len(pairs)
        for i in range(n + LOOK):
            if i < n:
                stage1(i)
            if hooks and i in hooks:
                hooks.pop(i)()
            if i - LOOK >= 0:
                stage2(i - LOOK)
        if hooks:
            for k in sorted(hooks):
                hooks.pop(k)()

    def ot_to_tok(self, OT, BOT, vw, osb, Bosb, Otok, BOtok):
        kb = self.kb
        kb.op("ACT", lambda e: e.copy(osb[0:vw, :], OT[0:vw, :]), reads=[BOT], writes=[Bosb])
        for s in range(4):
            kb.op("PE", lambda e, s=s: e.transpose(Otok[:, s * 128:s * 128 + vw], osb[0:vw, s * 128:(s + 1) * 128], self.identF[0:vw, 0:vw]),
                  reads=[Bosb, self.BidentF], writes=[BOtok])

    @staticmethod
    def causal_pairs(qt):
        pairs = []
        for kc in range(4 * qt + 4):
            j = kc - 4 * qt
            if j < 0:
                pairs.append((kc, [(s, "full") for s in range(4)]))
            else:
                pairs.append((kc, [(s, "tri" if s == j else "full") for s in range(j, 4)]))
        return pairs

    def phase_b_diff(self, li_odd_index):
        kb, I = self.kb, self.I
        T = Scope(kb)
        lam_init = 0.8 - 0.6 * float(np.exp(-0.3 * 1))
        lp, Blp = T.sb("lp", [128, 4, 64], F32)
        kb.dma("SP", lp[:].rearrange("p a d -> p (a d)"), bc_row(I["diff_lambda"][0:1, :], 256), writes=[Blp])
        l2, Bl2 = T.sb("l2", [128, 2, 64], F32)
        kb.op("DVE", lambda e: e.tensor_tensor(l2[:, 0, :], lp[:, 0, :], lp[:, 1, :], ALU.mult), reads=[Blp], writes=[Bl2])
        kb.op("DVE", lambda e: e.tensor_tensor(l2[:, 1, :], lp[:, 2, :], lp[:, 3, :], ALU.mult), reads=[Blp], writes=[Bl2])
        ls, Bls = T.sb("ls", [128, 2], F32)
        kb.op("DVE", lambda e: e.tensor_reduce(ls[:], l2[:], AX.X, ALU.add), reads=[Bl2], writes=[Bls])
        kb.op("ACT", lambda e: e.activation(ls[:], ls[:], AF.Exp), reads=[Bls], writes=[Bls])
        nlam, Bnlam = T.sb("nlam", [128, 1], F32)
        kb.op("DVE", lambda e: e.tensor_tensor(nlam[:], ls[:, 1:2], ls[:, 0:1], ALU.subtract), reads=[Bls], writes=[Bnlam])
        kb.op("DVE", lambda e: e.tensor_scalar(nlam[:], nlam[:], -lam_init, None, ALU.add), reads=[Bnlam], writes=[Bnlam])
        go, Bgo = T.sb("go", [128, 128], F32)
        kb.dma("SP", go[:], bc_row(I["diff_out_norm"][0:1, :], 128), writes=[Bgo])
        kb.op("DVE", lambda e: e.tensor_scalar(go[:], go[:], 1.0 - lam_init, None, ALU.mult), reads=[Bgo], writes=[Bgo])

        KTs = Rot([T.sb("KT", [128, S], BF16) for _ in range(2)])
        QTs = Rot([T.sb("QT", [128, S], BF16) for _ in range(2)])
        Vs = Rot([T.sb("V", [128, NT, 129], BF16) for _ in range(2)])
        for (v, Bv) in Vs.items:
            kb.op("POOL", lambda e, v=v: e.memset(v[:, :, 128:129], 1.0), writes=[Bv])
        R = dict(pS=Rot([T.ps("pS", [128, 512], F32) for _ in range(3)]),
                 pT=Rot([T.sb("pT", [128, 512], BF16) for _ in range(8)]))
        Os = [[T.ps("O", [128, 512], F32) for _ in range(2)] for _ in range(2)]
        osts = Rot([T.sb("ost", [128, 4, 128], BF16) for _ in range(2)])
        osbs = Rot([[[T.sb("osbd", [128, 512], F32) for _ in range(2)] for _ in range(2)] for _ in range(2)])
        pending = []
        a1ts = Rot([T.sb("a1t", [128, 4, 128], F32) for _ in range(2)])
        rdts = Rot([T.sb("rdt", [128, 4, 4], F32) for _ in range(2)])
        a0s = Rot([T.sb("a0", [128, 128], F32) for _ in range(2)])
        a1s = Rot([T.sb("a1", [128, 128], F32) for _ in range(2)])
        rds = Rot([T.sb("rd", [128, 4], F32) for _ in range(4)])
        junk, Bjunk = T.sb("junkb", [128, 128], BF16)

        def load_head(h):
            KT, BKT = KTs.next()
            QT, BQT = QTs.next()
            V, BV = Vs.next()
            kb.dma("SP", QT[:], self.FT[h * 128:(h + 1) * 128, :], writes=[BQT])
            kb.dma("SP", KT[:], self.FT[1024 + h * 128:1024 + (h + 1) * 128, :], writes=[BKT])
            tmv = self.TM[:, h * 128:(h + 1) * 128].rearrange("(c p) d -> p c d", p=128)
            for c4 in range(4):
                kb.dma("SP", V[:, c4 * 8:(c4 + 1) * 8, 0:128], tmv[:, c4 * 8:(c4 + 1) * 8, :], writes=[BV])
            return (KT, BKT, QT, BQT, V, BV)

        nxt = load_head(0)
        for h in range(8):
            KT, BKT, QT, BQT, V, BV = nxt
            if h + 1 < 8:
                nxt = load_head(h + 1)
            for qt in range(8):
                pairs = self.causal_pairs(qt)
                for c in range(2):
                    def o_off(O, s, c=c):
                        return Os[c][s // 2][0][:, (s % 2) * 256:(s % 2) * 256 + 129]
                    hooks = None
                    if c == 0 and pending:
                        e2_, e3_ = pending.pop(0)
                        hooks = {min(6, len(pairs) - 1): e2_, 10 ** 6: e3_}
                    self.attn_qtile(R, QT[64 * c:64 * c + 64, qt * 512:(qt + 1) * 512], BQT,
                                    lambda kc, c=c: (KT[64 * c:64 * c + 64, kc * 128:(kc + 1) * 128], 128), BKT,
                                    lambda kc: V[:, kc, :], BV, 129, pairs, None, Os[c][0][1], o_off, bank_of=lambda s: s // 2, hooks=hooks)
                oset = osbs.next()
                for c in range(2):
                    for b_ in range(2):
                        kb.op("DVE", lambda e, c=c, b_=b_: e.tensor_copy(oset[c][b_][0][:, 0:385], Os[c][b_][0][:, 0:385]),
                              reads=[Os[c][0][1]], writes=[oset[c][b_][1]])

                ep = dict(oset=oset, h=h, qt=qt)
                a1t, Ba1t = a1ts.next()
                rdt, Brdt = rdts.next()
                ep.update(a1t=a1t, Ba1t=Ba1t, rdt=rdt, Brdt=Brdt)

                def e1(ep=ep):
                    oset, a1t, Ba1t, rdt, Brdt = ep["oset"], ep["a1t"], ep["Ba1t"], ep["rdt"], ep["Brdt"]
                    for s in range(4):
                        O0 = oset[0][s // 2][0][:, (s % 2) * 256:(s % 2) * 256 + 129]
                        O1 = oset[1][s // 2][0][:, (s % 2) * 256:(s % 2) * 256 + 129]
                        BO0, BO1 = oset[0][s // 2][1], oset[1][s // 2][1]
                        a0, Ba0 = a0s.next()
                        kb.op("DVE", lambda e, s=s, O0=O0: e.reciprocal(rdt[:, s, 0:1], O0[:, 128:129]), reads=[BO0], writes=[Brdt])
                        kb.op("DVE", lambda e, s=s, O1=O1: e.reciprocal(rdt[:, s, 1:2], O1[:, 128:129]), reads=[BO1], writes=[Brdt])
                        kb.op("DVE", lambda e, s=s: e.tensor_tensor(rdt[:, s, 1:2], rdt[:, s, 1:2], nlam[:], ALU.mult), reads=[Brdt, Bnlam], writes=[Brdt])
                        kb.op("DVE", lambda e, s=s, a0=a0, O0=O0: e.tensor_scalar(a0[:], O0[:, 0:128], rdt[:, s, 0:1], None, ALU.mult), reads=[BO0, Brdt], writes=[Ba0])
                        kb.op("DVE", lambda e, s=s, a0=a0, O1=O1: e.scalar_tensor_tensor(a1t[:, s, :], O1[:, 0:128], rdt[:, s, 1:2], a0[:], ALU.mult, ALU.add),
                              reads=[BO1, Brdt, Ba0], writes=[Ba1t])

                def e2(ep=ep):
                    a1t, Ba1t, rdt, Brdt = ep["a1t"], ep["Ba1t"], ep["rdt"], ep["Brdt"]
                    for s in range(4):
                        kb.op("ACT", lambda e, s=s: e.activation(junk[:], a1t[:, s, :], AF.Square, accum_out=rdt[:, s, 2:3]), reads=[Ba1t], writes=[Bjunk, Brdt])
                    kb.op("ACT", lambda e: e.activation(rdt[:, :, 2:3], rdt[:, :, 2:3], AF.Sqrt, bias=self.epsb[:], scale=1.0 / 128),
                          reads=[Brdt, self.Bepsb], writes=[Brdt])

                def e3(ep=ep):
                    a1t, Ba1t, rdt, Brdt, h, qt = ep["a1t"], ep["Ba1t"], ep["rdt"], ep["Brdt"], ep["h"], ep["qt"]
                    ost, Bost = osts.next()
                    kb.op("DVE", lambda e: e.reciprocal(rdt[:, :, 3:4], rdt[:, :, 2:3]), reads=[Brdt], writes=[Brdt])
                    for s in range(4):
                        kb.op("DVE", lambda e, s=s, ost=ost: e.scalar_tensor_tensor(ost[:, s, :], a1t[:, s, :], rdt[:, s, 3:4], go[:], ALU.mult, ALU.mult),
                              reads=[Ba1t, Brdt, Bgo], writes=[Bost])
                    osv = self.OS[qt * 512:(qt + 1) * 512, h * 128:(h + 1) * 128].rearrange("(s p) d -> p s d", p=128)
                    kb.dma("POOL", osv, ost[:], reads=[Bost])
                e1()
                pending.append((e2, e3))
        while pending:
            e2_, e3_ = pending.pop(0)
            e2_()
            e3_()
        T.close()

    def phase_c1(self, li, x_src, w_out_ap):
        kb, I = self.kb, self.I
        T = Scope(kb)
        W, BW = T.sb("w_out", [128, 8, D], BF16)
        self.load_w_bf16(W, BW, w_out_ap, 8)
        ots = Rot([T.sb("ot", [128, D], BF16) for _ in range(3)])
        xts = Rot([T.sb("xt", [128, D], F32) for _ in range(3)])
        oTs = Rot([T.sb("oT", [128, 8, 128], BF16) for _ in range(2)])
        x1s = Rot([T.sb("x1", [128, D], F32) for _ in range(2)])
        hbs = Rot([T.sb("hb", [128, D], BF16) for _ in range(2)])
        ytmp, Bytmp = T.sb("ytmp", [128, D], F32)
        junk, Bjunk = T.sb("junk", [128, D], BF16)
        sss = Rot([T.sb("ss", [128, 1], F32) for _ in range(2)])
        hn, Bhn = T.sb("hn", [128, D], F32)
        stg = [T.sb("stage", [128, 8, 512], BF16) for _ in range(2)]
        pts = Rot([T.ps("ptA", [128, 1024], BF16) for _ in range(2)])
        pys = Rot([T.ps("py", [128, 512], F32) for _ in range(4)])

        def load(t):
            ot, Bot = ots.next()
            xt, Bxt = xts.next()
            kb.dma("SP", ot[:], self.OS[t * 128:(t + 1) * 128, :], writes=[Bot])
            kb.dma("SP", xt[:], x_src[t * 128:(t + 1) * 128, :], writes=[Bxt])
            return ot, Bot, xt, Bxt

        def front(t, ld):
            ot, Bot, xt, Bxt = ld
            oT, BoT = oTs.next()
            pt, Bpt = pts.next()
            self.transpose8(ot, Bot, pt, Bpt, oT[:], BoT, eng="ACT")
            pyl = []
            for g in range(2):
                py, Bpy = pys.next()
                for kc in range(8):
                    kb.op("PE", lambda e, kc=kc, g=g, py=py: e.matmul(py[:], oT[:, kc, :], W[:, kc, g * 512:(g + 1) * 512], start=(kc == 0), stop=(kc == 7)),
                          reads=[BoT, BW], writes=[Bpy])
                pyl.append((py, Bpy))
            return pyl, xt, Bxt

        def back(t, fr):
            pyl, xt, Bxt = fr
            x1, Bx1 = x1s.next()
            for g in range(2):
                py, Bpy = pyl[g]
                sl = slice(g * 512, (g + 1) * 512)
                kb.op("DVE", lambda e, py=py, sl=sl: e.tensor_tensor(ytmp[:, sl], py[:], self.mod[:, 2, sl], ALU.mult), reads=[Bpy, self.Bmod], writes=[Bytmp])
                kb.op("POOL", lambda e, sl=sl: e.tensor_tensor(x1[:, sl], ytmp[:, sl], xt[:, sl], ALU.add), reads=[Bytmp, Bxt], writes=[Bx1])
            kb.dma("POOL", self.x1s[t * 128:(t + 1) * 128, :], x1[:], reads=[Bx1])
            hb, Bhb = hbs.next()
            ss, Bss = sss.next()
            self.rms_mod_tile(T, x1, Bx1, hb, Bhb, 4, 3, (junk, Bjunk, ss, Bss, hn, Bhn))
            pt, Bpt = pts.next()
            tt = t % 4
            st, Bst = stg[(t // 4) % 2]
            self.transpose8(hb, Bhb, pt, Bpt, st[:, :, tt * 128:(tt + 1) * 128], Bst, eng="ACT")
            if tt == 3:
                t0 = (t - 3) * 128
                h2v = self.H2T.rearrange("(j p) t -> p j t", p=128)
                kb.dma("POOL", h2v[:, :, t0:t0 + 512], st[:], reads=[Bst])

        lds = [load(0)]
        if NT > 1:
            lds.append(load(1))
        fr = front(0, lds.pop(0))
        for t in range(NT):
            if t + 2 < NT:
                lds.append(load(t + 2))
            nfr = front(t + 1, lds.pop(0)) if t + 1 < NT else None
            back(t, fr)
            fr = nfr
        T.close()

    def phase_c2(self, li, x_dst):
        kb, I = self.kb, self.I
        T = Scope(kb)
        Wg, BWg = T.sb("wg", [128, 8, FFN], BF16)
        Wu, BWu = T.sb("wu", [128, 8, FFN], BF16)
        Wd, BWd = T.sb("wd", [128, NJ, D], BF16)
        self.load_w_bf16(Wg, BWg, I["ffn_w_gate"][li], 8)
        self.load_w_bf16(Wu, BWu, I["ffn_w_up"][li], 8)
        self.load_w_bf16(Wd, BWd, I["ffn_w_down"][li], NJ)
        TT = 256
        h2s = Rot([T.sb("h2T", [128, 8, TT], BF16) for _ in range(2)])
        act, Bact = T.sb("act", [128, NJ, TT], BF16)
        sgs = Rot([T.sb("sg", [128, TT], F32) for _ in range(2)])
        xts = Rot([T.sb("x1t", [128, D], F32) for _ in range(2)])
        ytmp, Bytmp = T.sb("ytmp", [128, 512], F32)
        pgs = Rot([T.ps("pg", [128, 512], F32) for _ in range(2)])
        pus = Rot([T.ps("pu", [128, 512], F32) for _ in range(2)])
        pys = Rot([T.ps("py", [128, 512], F32) for _ in range(3)])
        h2v = self.H2T.rearrange("(j p) t -> p j t", p=128)

        def load(st):
            h2, Bh2 = h2s.next()
            kb.dma("SP", h2[:], h2v[:, :, st * TT:(st + 1) * TT], writes=[Bh2])
            return h2, Bh2

        nxt = load(0)
        for st in range(S // TT):
            h2, Bh2 = nxt
            if st + 1 < S // TT:
                nxt = load(st + 1)
            for j in range(NJ):
                pg, Bpg = pgs.next()
                pu, Bpu = pus.next()
                for kc in range(8):
                    kb.op("PE", lambda e, kc=kc: e.matmul(pg[:, 0:TT], Wg[:, kc, j * 128:(j + 1) * 128], h2[:, kc, :], start=(kc == 0), stop=(kc == 7)),
                          reads=[BWg, Bh2], writes=[Bpg])
                for kc in range(8):
                    kb.op("PE", lambda e, kc=kc: e.matmul(pu[:, 0:TT], Wu[:, kc, j * 128:(j + 1) * 128], h2[:, kc, :], start=(kc == 0), stop=(kc == 7)),
                          reads=[BWu, Bh2], writes=[Bpu])
                sg, Bsg = sgs.next()
                kb.op("ACT", lambda e: e.activation(sg[:], pg[:, 0:TT], AF.Silu), reads=[Bpg], writes=[Bsg])
                kb.op("DVE", lambda e, j=j: e.tensor_tensor(act[:, j, :], sg[:], pu[:, 0:TT], ALU.mult), reads=[Bsg, Bpu], writes=[Bact])
            for q in range(TT // 128):
                t = st * (TT // 128) + q
                xt, Bxt = xts.next()
                kb.dma("SP", xt[:], self.x1s[t * 128:(t + 1) * 128, :], writes=[Bxt])
                for g in range(2):
                    py, Bpy = pys.next()
                    for j in range(NJ):
                        kb.op("PE", lambda e, j=j: e.matmul(py[:], act[:, j, q * 128:(q + 1) * 128], Wd[:, j, g * 512:(g + 1) * 512],
                                                            start=(j == 0), stop=(j == NJ - 1)),
                              reads=[Bact, BWd], writes=[Bpy])
                    sl = slice(g * 512, (g + 1) * 512)
                    kb.op("DVE", lambda e: e.tensor_tensor(ytmp[:], py[:], self.mod[:, 5, sl], ALU.mult), reads=[Bpy, self.Bmod], writes=[Bytmp])
                    kb.op("POOL", lambda e: e.tensor_tensor(xt[:, sl], ytmp[:], xt[:, sl], ALU.add), reads=[Bytmp, Bxt], writes=[Bxt])
                kb.dma("POOL", x_dst[t * 128:(t + 1) * 128, :], xt[:], reads=[Bxt])
        T.close()

    def build(self):
        kb = self.kb
        self.declare()
        self.setup()
        G = self.G
        self.epsb, self.Bepsb = G.sb("epsb", [128, 1], F32)
        kb.op("POOL", lambda e: e.memset(self.epsb[:], EPS), writes=[self.Bepsb])
        layers = self.cfg.get("layers", [0, 1])
        x_src = self.I["x"]
        for n, li in enumerate(layers):
            x_dst = self.out if n == len(layers) - 1 else self.xmid
            self.L = Scope(kb)
            if li == 0:
                self.gate_sb, self.Bgate = self.L.sb("gates", [128, NT, 24], F32)
            self.compute_mod(li)
            stop = self.cfg.get("stop")
            if stop == "mod":
                self.L.close()
                break
            if li == 0:
                self.phase_a(0, x_src)
                if stop == "A":
                    self.L.close()
                    break
                if not self.cfg.get("skip_moba"):
                    self.moba_part()
                if stop == "moba":
                    self.L.close()
                    break
                self.nsa_part()
                if stop in ("nsa", "cmpmlp"):
                    self.L.close()
                    break
                self.phase_c1(0, x_src, self.I["sp_w_out"])
            else:
                self.phase_a(1, x_src)
                if stop == "A":
                    self.L.close()
                    break
                self.phase_b_diff(0)
                if stop == "B":
                    self.L.close()
                    break
                self.phase_c1(1, x_src, self.I["diff_w_out"])
            if stop == "C1":
                self.L.close()
                break
            self.phase_c2(li, x_dst)
            self.L.close()
            x_src = x_dst
        kb.barrier(engines=("POOL",))
        G.es.close()
        kb.es.close()
        return self.nc

    def epi_norm(self, T, O_ap, BO, vcol, rd_ap, Brd):
        kb = self.kb
        kb.op("DVE", lambda e: e.tensor_scalar(rd_ap, O_ap[:, vcol:vcol + 1], 1e-30, None, ALU.max), reads=[BO], writes=[Brd])
        kb.op("DVE", lambda e: e.reciprocal(rd_ap, rd_ap), reads=[Brd], writes=[Brd])

    def phase_b_sparse(self):
        self.moba_part()
        self.nsa_part()

    def moba_part(self):
        kb, I = self.kb, self.I
        T = Scope(kb)
        QAs = Rot([T.sb("QA", [128, 2, S], BF16) for _ in range(2)])
        KAs = Rot([T.sb("KA", [128, 2, S], BF16) for _ in range(2)])
        Vps = Rot([T.sb("Vp", [128, NT, 2, 65], BF16) for _ in range(2)])
        for (ka, Bka) in KAs.items:
            for hh in range(2):
                kb.dma("POOL", ka[64:80, hh, :], I["c_e16"][:, :], writes=[Bka])
        for (v, Bv) in Vps.items:
            kb.op("POOL", lambda e, v=v: e.memset(v[:, :, :, 64:65], 1.0), writes=[Bv])
        t1, Bt1 = T.sb("t1", [128, 16, 16], F32)
        t2, Bt2 = T.sb("t2", [128, 16, 16], F32)
        kb.dma("SP", t1[:].rearrange("p a b -> p (a b)"), bc_row(I["c_t1"][0:1, :], 256), writes=[Bt1])
        kb.dma("SP", t2[:].rearrange("p a b -> p (a b)"), bc_row(I["c_t2"][0:1, :], 256), writes=[Bt2])
        augs = Rot([T.sb("aug", [128, 128], BF16) for _ in range(2)])
        for (a, Ba) in augs.items:
            kb.op("POOL", lambda e, a=a: e.memset(a[:], 0.0), writes=[Ba])
        km, Bkm = T.sb("km", [64, 16], F32)
        kmb, Bkmb = T.sb("kmb", [64, 16], BF16)
        gms = Rot([T.sb("gm", [128, 16], F32) for _ in range(2)])
        m8s = Rot([T.sb("m8", [128, 8], F32) for _ in range(2)])
        sels = Rot([T.sb("sel", [128, 16], F32) for _ in range(2)])
        R = dict(pS=Rot([T.ps("pS", [128, 512], F32) for _ in range(3)]),
                 pT=Rot([T.sb("pT", [128, 512], BF16) for _ in range(8)]))
        Obs = Rot([T.ps("OTm", [128, 512], F32) for _ in range(2)])
        Otk, BOtk = T.ps("Otok", [128, 512], F32)
        osbs = Rot([T.sb("osb", [128, 512], F32) for _ in range(2)])
        pgs = Rot([T.ps("pgate", [128, 512], F32) for _ in range(1)])
        ptr = Rot([T.ps("ptr", [128, 1024], BF16) for _ in range(1)])
        osts = Rot([T.sb("ost", [128, 4, 128], BF16) for _ in range(2)])
        rds = Rot([T.sb("rd", [128, 1], F32) for _ in range(4)])

        def load_pair(hp):
            QA, BQA = QAs.next()
            KA, BKA = KAs.next()
            Vp, BVp = Vps.next()
            for hh in range(2):
                h = 2 * hp + hh
                kb.dma("SP", QA[0:64, hh, :], self.FT[h * 64:(h + 1) * 64, :], writes=[BQA])
                kb.dma("SP", KA[0:64, hh, :], self.FT[512 + h * 64:512 + (h + 1) * 64, :], writes=[BKA])
            for hh in range(2):
                h = 2 * hp + hh
                tmv = self.TM[:, h * 64:(h + 1) * 64].rearrange("(c p) d -> p c d", p=128)
                for c4 in range(4):
                    kb.dma("SP", Vp[:, c4 * 8:(c4 + 1) * 8, hh, 0:64], tmv[:, c4 * 8:(c4 + 1) * 8, :], writes=[BVp])
            return QA, BQA, KA, BKA, Vp, BVp

        kms = [T.sb("km2", [64, 16], F32) for _ in range(2)]
        kmbs = [T.sb("kmb2", [64, 16], BF16) for _ in range(2)]
        pgt, _ = pgs.items[0]
        pg_slots = Rot([(pgt[:, j * 16:(j + 1) * 16], Buf("pgs%d" % j)) for j in range(8)])
        ptt, _ = ptr.items[0]
        pt_slots = Rot([(ptt[:, j * 128:(j + 1) * 128], Buf("pts%d" % j)) for j in range(8)])
        augs8 = Rot([T.sb("aug8", [128, 128], BF16) for _ in range(8)])
        for (a_, Ba_) in augs8.items:
            kb.op("POOL", lambda e, a_=a_: e.memset(a_[:], 0.0), writes=[Ba_])
        gms8 = Rot([T.sb("gm8", [128, 16], F32) for _ in range(8)])
        m8s8 = Rot([T.sb("m88", [128, 8], F32) for _ in range(8)])
        sels8 = Rot([T.sb("sel8", [128, 16], F32) for _ in range(8)])

        def gating_jobs(QA, BQA, KA, BKA):
            def prep():
                for hh in range(2):
                    km_, Bkm_ = kms[hh]
                    kmb_, Bkmb_ = kmbs[hh]
                    kb.op("DVE", lambda e, hh=hh, km_=km_: e.tensor_reduce(km_[:], KA[0:64, hh, :].rearrange("p (b j) -> p b j", j=256), AX.X, ALU.add),
                          reads=[BKA], writes=[Bkm_])
                    kb.op("DVE", lambda e, km_=km_, kmb_=kmb_: e.tensor_scalar(kmb_[:], km_[:], 1.0 / 256, None, ALU.mult), reads=[Bkm_], writes=[Bkmb_])
            jobs = []
            for hh in range(2):
                for t in range(NT):
                    st = {}

                    def part1(hh=hh, t=t, st=st):
                        own = t // 2
                        kmb_, Bkmb_ = kmbs[hh]
                        pg, Bpg = pg_slots.next()
                        kb.op("PE", lambda e: e.matmul(pg, QA[0:64, hh, t * 128:(t + 1) * 128], kmb_[:], start=True, stop=True),
                              reads=[BQA, Bkmb_], writes=[Bpg])
                        gm, Bgm = gms8.next()
                        m8, Bm8 = m8s8.next()
                        sel, Bsel = sels8.next()
                        aug, Baug = augs8.next()
                        kb.op("DVE", lambda e: e.tensor_tensor(gm[:], pg, t1[:, own, :], ALU.add), reads=[Bpg, Bt1], writes=[Bgm])
                        kb.op("DVE", lambda e: e.max(m8[:], gm[:]), reads=[Bgm], writes=[Bm8])
                        kb.op("DVE", lambda e: e.tensor_scalar(m8[:, 2:3], m8[:, 2:3], -1e29, None, ALU.max), reads=[Bm8], writes=[Bm8])
                        kb.op("DVE", lambda e: e.tensor_scalar(sel[:], gm[:], m8[:, 2:3], None, ALU.is_ge), reads=[Bgm, Bm8], writes=[Bsel])
                        kb.op("DVE", lambda e: e.tensor_tensor(sel[:], sel[:], t2[:, own, :], ALU.max), reads=[Bsel, Bt2], writes=[Bsel])
                        kb.op("DVE", lambda e: e.tensor_scalar(aug[:, 64:80], sel[:], -1.0, -MASKV, ALU.add, ALU.mult), reads=[Bsel], writes=[Baug])
                        st["aug"] = (aug, Baug)

                    def part2(hh=hh, t=t, st=st):
                        aug, Baug = st["aug"]
                        pt, Bpt = pt_slots.next()
                        kb.op("PE", lambda e: e.transpose(pt, aug[:], self.ident[:]), reads=[Baug, self.Bident], writes=[Bpt])
                        kb.op("ACT", lambda e: e.copy(QA[64:80, hh, t * 128:(t + 1) * 128], pt[64:80, :]), reads=[Bpt], writes=[BQA])
                    jobs.append((part1, part2))
            return prep, jobs

        nxt = load_pair(0)
        pending = []
        prep0, jobs0 = gating_jobs(nxt[0], nxt[1], nxt[2], nxt[3])
        prep0()
        for j0 in range(0, len(jobs0), 4):
            for p1, _ in jobs0[j0:j0 + 4]:
                p1()
            for _, p2 in jobs0[j0:j0 + 4]:
                p2()
        for hp in range(4):
            QA, BQA, KA, BKA, Vp, BVp = nxt
            njobs = None
            if hp + 1 < 4:
                nxt = load_pair(hp + 1)
                nprep, njobs = gating_jobs(nxt[0], nxt[1], nxt[2], nxt[3])
            for qt in range(8):
                pairs = self.causal_pairs(qt)
                ost, Bost = osts.next()
                for hh in range(2):
                    OTm, BOTm = Obs.next()
                    hooks = None
                    if njobs is not None:
                        ci = qt * 2 + hh
                        mine = njobs[ci * 4:(ci + 1) * 4]

                        def h1(mine=mine, first=(ci == 0)):
                            if first:
                                nprep()
                            for p1, _ in mine:
                                p1()

                        def h2(mine=mine):
                            for _, p2 in mine:
                                p2()
                        hooks = {0: h1, 6: h2}
                    self.attn_qtile(R, QA[0:80, hh, qt * 512:(qt + 1) * 512], BQA,
                                    lambda kc: (KA[0:80, hh, kc * 128:(kc + 1) * 128], 128), BKA,
                                    lambda kc: Vp[:, kc, hh, :], BVp, 65, pairs, OTm, BOTm, None, vstat=True, hooks=hooks)
                    def epilogue(OTm=OTm, BOTm=BOTm, ost=ost, Bost=Bost, hh=hh, qt=qt, hp=hp):
                        osb, Bosb = osbs.next()
                        self.ot_to_tok(OTm, BOTm, 65, osb, Bosb, Otk, BOtk)
                        Om, BOm = Otk, BOtk
                        for s in range(4):
                            rd, Brd = rds.next()
                            Os_ = Om[:, s * 128:s * 128 + 65]
                            self.epi_norm(T, Os_, BOm, 64, rd[:], Brd)
                            kb.op("DVE", lambda e, s=s, Os_=Os_, rd=rd: e.tensor_scalar(ost[:, s, hh * 64:(hh + 1) * 64], Os_[:, 0:64], rd[:, 0:1], None, ALU.mult),
                                  reads=[BOm, Brd], writes=[Bost])
                        if hh == 1:
                            osv = self.OS[qt * 512:(qt + 1) * 512, hp * 128:(hp + 1) * 128].rearrange("(s p) d -> p s d", p=128)
                            kb.dma("POOL", osv, ost[:], reads=[Bost])
                    if pending:
                        pending.pop()()
                    pending.append(epilogue)
        if pending:
            pending.pop()()
        T.close()

    def nsa_part(self):
        kb, I = self.kb, self.I
        for _ in range(self.cfg.get("pad_dve", 0)):
            kb.op("DVE", lambda e: e.memset(self.epsb[:], EPS), writes=[self.Bepsb])
        P = Scope(kb)
        CKc, BCKc = P.sb("CKc", [64, 2, 256], BF16)
        VCs = [P.sb("VC", [128, 2, 129], BF16) for _ in range(2)]
        kb.op("POOL", lambda e: e.memset(CKc[:], 0.0), writes=[BCKc])
        for g in range(2):
            vc, Bvc = VCs[g]
            for c in range(2):
                kb.dma("POOL", vc[:, c, 64:129], I["c_ov"][c * 128:(c + 1) * 128, :], writes=[Bvc])
        T = Scope(kb)
        CX = [T.sb("CX", [128, S], BF16) for _ in range(2)]
        kb.dma("SP", CX[0][0][:], self.FT[1536:1664, :], writes=[CX[0][1]])
        kb.dma("SP", CX[1][0][:], self.FT[1664:1792, :], writes=[CX[1][1]])
        w1, Bw1 = T.sb("w1", [128, 2 * 32 * 256], BF16)
        w1src = I["cmp_w1"].rearrange("d a l e -> d (a l e)")
        for half in range(2):
            for a in range(2):
                kb.dma("POOL", w1[half * 64:(half + 1) * 64, a * 8192:(a + 1) * 8192], w1src[:, a * 8192:(a + 1) * 8192], writes=[Bw1])
        w1v = w1[:].rearrange("p (a l e) -> p a l e", a=2, l=32)
        posb, Bposb = T.sb("posb", [64, 2, 34], BF16)
        kb.op("POOL", lambda e: e.memset(posb[:], 0.0), writes=[Bposb])
        kb.dma("POOL", posb[:, :, 0:32], I["cmp_posT"][:, :, :], writes=[Bposb])
        w2, Bw2 = T.sb("w2", [128, 2, 2, 64], BF16)
        for kv in range(2):
            for eh in range(2):
                kb.dma("POOL", w2[:, kv, eh, :], I["cmp_w2"][kv, eh * 128:(eh + 1) * 128, :], writes=[Bw2])
        b1, Bb1 = T.sb("b1", [128, 4], F32)
        pbs = Rot([T.ps("pb", [128, 512], F32) for _ in range(2)])
        phs = Rot([T.ps("ph", [128, 512], F32) for _ in range(3)])
        for kv in range(2):
            for eh in range(2):
                pb, Bpb = pbs.next()
                for l in range(32):
                    kb.op("PE", lambda e, l=l: e.matmul(pb[:, 0:2], w1v[0:64, kv, l, eh * 128:(eh + 1) * 128],
                                                        posb[0:64, kv, l:l + 2], start=(l == 0), stop=(l == 31)),
                          reads=[Bw1, Bposb], writes=[Bpb])
                kb.op("DVE", lambda e: e.tensor_copy(b1[:, kv * 2 + eh:kv * 2 + eh + 1], pb[:, 0:1]), reads=[Bpb], writes=[Bb1])
        hid = {}
        for kv in range(2):
            cx, Bcx = CX[kv]
            cxv = cx[:].rearrange("p (n j) -> p n j", j=16)
            for g in range(2):
                for eh in range(2):
                    ph, Bph = phs.next()
                    for l in range(32):
                        a = l // 16
                        kb.op("PE", lambda e, l=l, a=a: e.matmul(ph[:, 0:255], w1v[64 * g:64 * g + 64, kv, l, eh * 128:(eh + 1) * 128],
                                                                 cxv[64 * g:64 * g + 64, a:a + 255, l % 16], start=(l == 0), stop=(l == 31)),
                              reads=[Bw1, Bcx], writes=[Bph])
                    ht, Bht = T.sb("hid", [128, 256], BF16)
                    kb.op("POOL", lambda e: e.memset(ht[:], 0.0), writes=[Bht])
                    kb.op("ACT", lambda e: e.activation(ht[:, 0:255], ph[:, 0:255], AF.Silu, bias=b1[:, kv * 2 + eh:kv * 2 + eh + 1]),
                          reads=[Bph, Bb1], writes=[Bht])
                    hid[(kv, g, eh)] = (ht, Bht)
        for g in range(2):
            ph, Bph = phs.next()
            for eh in range(2):
                ht, Bht = hid[(0, g, eh)]
                kb.op("PE", lambda e: e.matmul(ph[0:64, 0:255], w2[:, 0, eh, :], ht[:, 0:255], start=(eh == 0), stop=(eh == 1)),
                      reads=[Bw2, Bht], writes=[Bph])
            kb.op("ACT", lambda e: e.copy(CKc[:, g, 0:255], ph[0:64, 0:255]), reads=[Bph], writes=[BCKc])
            vc, Bvc = VCs[g]
            for c in range(2):
                ph, Bph = phs.next()
                for eh in range(2):
                    ht, Bht = hid[(1, g, eh)]
                    kb.op("PE", lambda e: e.matmul(ph[:, 0:64], ht[:, c * 128:(c + 1) * 128], w2[:, 1, eh, :], start=(eh == 0), stop=(eh == 1)),
                          reads=[Bw2, Bht], writes=[Bph])
                kb.op("ACT", lambda e: e.copy(vc[:, c, 0:64], ph[:, 0:64]), reads=[Bph], writes=[Bvc])
        T.close()
        if self.cfg.get("stop") == "cmpmlp":
            if self.cfg.get("debug"):
                dck = self.tap("d_ckc", [64, 512], BF16)
                kb.dma("SP", dck[:, :], CKc[:].rearrange("p g n -> p (g n)"), reads=[BCKc])
                for g in range(2):
                    dvc = self.tap("d_vc%d" % g, [128, 258], BF16)
                    kb.dma("SP", dvc[:, :], VCs[g][0][:].rearrange("p c n -> p (c n)"), reads=[VCs[g][1]])
            P.close()
            return

        T = Scope(kb)
        nfv, Bnfv = T.sb("nfv", [128, NT, 64], F32)
        cst, Bcst = T.sb("cst", [128, NT, 64], F32)
        v01, Bv01 = T.sb("v01", [128, NT, 64], F32)
        kb.dma("SP", nfv[:], I["c_nfv"][:, :, :], writes=[Bnfv])
        kb.dma("SP", cst[:], I["c_cst"][:, :, :], writes=[Bcst])
        kb.dma("SP", v01[:], I["c_v01"][:, :, :], writes=[Bv01])
        QA4, BQA4 = T.sb("QA4", [128, 4, S], BF16)
        KS, BKS = T.sb("KS", [128, S], BF16)
        KW, BKW = T.sb("KW", [64, S], BF16)
        VS, BVS = T.sb("VS", [128, NT, 65], BF16)
        VW, BVW = T.sb("VW", [128, NT, 65], BF16)
        kb.op("POOL", lambda e: e.memset(VS[:, :, 64:65], 1.0), writes=[BVS])
        kb.op("POOL", lambda e: e.memset(VW[:, :, 64:65], 1.0), writes=[BVW])
        kb.dma("POOL", KS[64:128, :], I["c_e64"][:, :], writes=[BKS])
        augs = Rot([T.sb("aug", [128, 128], BF16) for _ in range(2)])
        for (a, Ba) in augs.items:
            kb.op("POOL", lambda e, a=a: e.memset(a[:], 0.0), writes=[Ba])
        R = dict(pS=Rot([T.ps("pS", [128, 512], F32) for _ in range(2)]),
                 pT=Rot([T.sb("pT", [128, 512], BF16) for _ in range(8)]))
        Oc = [T.ps("Oc", [128, 512], F32) for _ in range(2)]
        BOc = Oc[0][1]
        Osw = Rot([T.ps("OTsw", [128, 512], F32) for _ in range(2)])
        Otk, BOtk = T.ps("Otok", [128, 512], F32)
        osbs = Rot([T.sb("osb", [128, 512], F32) for _ in range(2)])
        ptr = Rot([T.ps("ptr", [128, 1024], BF16) for _ in range(1)])
        occ, Bocc = T.sb("occ", [128, 4, 4, 64], F32)
        imp, Bimp = T.sb("imp", [128, 4, 64], F32)
        itmp, Bitmp = T.sb("itmp", [128, 64], F32)
        sc, Bsc = T.sb("sc", [128, 64], F32)
        sc2, Bsc2 = T.sb("sc2", [128, 64], F32)
        m8a, Bm8a = T.sb("m8a", [128, 8], F32)
        m8b, Bm8b = T.sb("m8b", [128, 8], F32)
        selm, Bselm = T.sb("selm", [128, 64], F32)
        rds = Rot([T.sb("rd", [128, 2], F32) for _ in range(6)])
        osts = Rot([T.sb("ost", [128, 4, 256], BF16) for _ in range(2)])
        gsb = self.gate_sb

        for g in range(2):
            for r in range(4):
                h = 4 * g + r
                kb.dma("SP", QA4[0:64, r, :], self.FT[1024 + h * 64:1024 + (h + 1) * 64, :], writes=[BQA4])
            kb.dma("SP", KS[0:64, :], self.FT[1792 + 64 * g:1792 + 64 * g + 64, :], writes=[BKS])
            kb.dma("SP", KW[:], self.FT[1920 + 64 * g:1920 + 64 * g + 64, :], writes=[BKW])
            tms = self.TM[:, 512 + 64 * g:512 + 64 * g + 64].rearrange("(c p) d -> p c d", p=128)
            tmw = self.TM[:, 640 + 64 * g:640 + 64 * g + 64].rearrange("(c p) d -> p c d", p=128)
            for c4 in range(4):
                kb.dma("SP", VS[:, c4 * 8:(c4 + 1) * 8, 0:64], tms[:, c4 * 8:(c4 + 1) * 8, :], writes=[BVS])
                kb.dma("SP", VW[:, c4 * 8:(c4 + 1) * 8, 0:64], tmw[:, c4 * 8:(c4 + 1) * 8, :], writes=[BVW])
            vc, Bvc = VCs[g]
            parts = self.cfg.get("nsa_parts", ("cmp", "select", "sw"))
            for qt in self.cfg.get("nsa_qts", range(8)):
                ost, Bost = osts.next()
                cchunks = [0] + ([1] if qt >= 4 else [])
                cpairs = [(c, [(s, "full") for s in range(4)]) for c in cchunks]

                def cmp_mask(c, pT, BpT, krows):
                    kb.op("POOL", lambda e: e.affine_select(pT[:], pT[:], [[1, 512]], ALU.is_ge, 0.0,
                                                            base=512 * qt - 2048 * c - 31, channel_multiplier=-16),
                          reads=[BpT], writes=[BpT])

                for r in (range(4) if "cmp" in parts else ()):
                    h = 4 * g + r
                    self.attn_qtile(R, QA4[0:64, r, qt * 512:(qt + 1) * 512], BQA4,
                                    lambda c: (CKc[:, g, c * 128:(c + 1) * 128], 128), BCKc,
                                    lambda c: vc[:, c, :], Bvc, 129, cpairs, None, BOc,
                                    lambda O, s: Oc[s // 2][0][:, (s % 2) * 256:(s % 2) * 256 + 129], post_exp=cmp_mask, bank_of=lambda s: s // 2)
                    for s in range(4):
                        t = 4 * qt + s
                        O_ = Oc[s // 2][0][:, (s % 2) * 256:(s % 2) * 256 + 129]
                        rd, Brd = rds.next()
                        self.epi_norm(T, O_, BOc, 64, rd[:, 0:1], Brd)
                        if r == 0:
                            kb.op("DVE", lambda e, s=s: e.tensor_scalar(imp[:, s, :], O_[:, 65:129], rd[:, 0:1], None, ALU.mult),
                                  reads=[BOc, Brd], writes=[Bimp])
                        else:
                            kb.op("DVE", lambda e: e.tensor_scalar(itmp[:], O_[:, 65:129], rd[:, 0:1], None, ALU.mult),
                                  reads=[BOc, Brd], writes=[Bitmp])
                            kb.op("DVE", lambda e, s=s: e.tensor_tensor(imp[:, s, :], imp[:, s, :], itmp[:], ALU.add),
                                  reads=[Bitmp, Bimp], writes=[Bimp])
                        kb.op("DVE", lambda e: e.tensor_tensor(rd[:, 1:2], rd[:, 0:1], gsb[:, t, h:h + 1], ALU.mult), reads=[Brd, self.Bgate], writes=[Brd])
                        kb.op("DVE", lambda e, s=s, r=r: e.tensor_scalar(occ[:, r, s, :], O_[:, 0:64], rd[:, 1:2], None, ALU.mult),
                              reads=[BOc, Brd], writes=[Bocc])
                for s in (range(4) if "select" in parts else ()):
                    t = 4 * qt + s
                    aug, Baug = augs.next()
                    kb.op("DVE", lambda e: e.tensor_tensor(sc[:], imp[:, s, :], nfv[:, t, :], ALU.mult), reads=[Bimp, Bnfv], writes=[Bsc])
                    kb.op("DVE", lambda e: e.tensor_tensor(sc[:], sc[:], cst[:, t, :], ALU.add), reads=[Bsc, Bcst], writes=[Bsc])
                    kb.op("DVE", lambda e: e.max(m8a[:], sc[:]), reads=[Bsc], writes=[Bm8a])
                    kb.op("DVE", lambda e: e.match_replace(sc2[:], m8a[:], sc[:], -1e30), reads=[Bsc, Bm8a], writes=[Bsc2])
                    kb.op("DVE", lambda e: e.max(m8b[:], sc2[:]), reads=[Bsc2], writes=[Bm8b])
                    kb.op("DVE", lambda e: e.tensor_scalar(selm[:], sc[:], m8b[:, 7:8], None, ALU.is_ge), reads=[Bsc, Bm8b], writes=[Bselm])
                    kb.op("DVE", lambda e: e.tensor_tensor(selm[:], selm[:], v01[:, t, :], ALU.mult), reads=[Bselm, Bv01], writes=[Bselm])
                    kb.op("DVE", lambda e: e.tensor_scalar(aug[:, 64:128], selm[:], -1.0, -MASKV, ALU.add, ALU.mult), reads=[Bselm], writes=[Baug])
                    pt, Bpt = ptr.next()
                    kb.op("PE", lambda e: e.transpose(pt[:, 0:128], aug[:], self.ident[:]), reads=[Baug, self.Bident], writes=[Bpt])
                    for r in range(4):
                        kb.op("ACT", lambda e, r=r: e.copy(QA4[64:128, r, t * 128:(t + 1) * 128], pt[64:128, 0:128]), reads=[Bpt], writes=[BQA4])
                pending = []
                spairs = self.causal_pairs(qt)
                wpairs = []
                for kc in range(max(0, 4 * qt - 4), 4 * qt + 4):
                    subs = []
                    for s in range(4):
                        dlt = 4 * qt + s - kc
                        if dlt == 0:
                            subs.append((s, "tri"))
                        elif 1 <= dlt <= 3:
                            subs.append((s, "full"))
                        elif dlt == 4:
                            subs.append((s, "atri"))
                    if subs:
                        wpairs.append((kc, subs))
                for r in (range(4) if "sw" in parts else ()):
                    h = 4 * g + r
                    for br, (pairs, qrows, kl, Bkl, vt, Bvt) in enumerate((
                            (spairs, 128, KS, BKS, VS, BVS), (wpairs, 64, KW, BKW, VW, BVW))):
                        if br not in self.cfg.get("nsa_br", (0, 1)):
                            continue
                        OTb, BOTb = Osw.next()
                        self.attn_qtile(R, QA4[0:qrows, r, qt * 512:(qt + 1) * 512], BQA4,
                                        lambda kc, kl=kl, qrows=qrows: (kl[0:qrows, kc * 128:(kc + 1) * 128], 128), Bkl,
                                        lambda kc, vt=vt: vt[:, kc, :], Bvt, 65, pairs, OTb, BOTb, None, vstat=True)
                        def epilogue(OTb=OTb, BOTb=BOTb, br=br, r=r, h=h, qt=qt, ost=ost, Bost=Bost):
                            osb, Bosb = osbs.next()
                            self.ot_to_tok(OTb, BOTb, 65, osb, Bosb, Otk, BOtk)
                            Ob, BOb = Otk, BOtk
                            for s in range(4):
                                t = 4 * qt + s
                                O_ = Ob[:, s * 128:s * 128 + 65]
                                rd, Brd = rds.next()
                                self.epi_norm(T, O_, BOb, 64, rd[:, 0:1], Brd)
                                gcol = (br + 1) * 8 + h
                                kb.op("DVE", lambda e, rd=rd, t=t, gcol=gcol: e.tensor_tensor(rd[:, 1:2], rd[:, 0:1], gsb[:, t, gcol:gcol + 1], ALU.mult),
                                      reads=[Brd, self.Bgate], writes=[Brd])
                                if br == 0:
                                    kb.op("DVE", lambda e, O_=O_, rd=rd: e.tensor_scalar(itmp[:], O_[:, 0:64], rd[:, 1:2], None, ALU.mult),
                                          reads=[BOb, Brd], writes=[Bitmp])
                                    kb.op("DVE", lambda e, s=s: e.tensor_tensor(occ[:, r, s, :], occ[:, r, s, :], itmp[:], ALU.add),
                                          reads=[Bitmp, Bocc], writes=[Bocc])
                                else:
                                    kb.op("DVE", lambda e, s=s, O_=O_, rd=rd: e.scalar_tensor_tensor(ost[:, s, r * 64:(r + 1) * 64], O_[:, 0:64], rd[:, 1:2], occ[:, r, s, :], ALU.mult, ALU.add),
                                          reads=[BOb, Brd, Bocc], writes=[Bost])
                        if pending:
                            pending.pop()()
                        pending.append(epilogue)
                if pending:
                    pending.pop()()
                osv = self.OS[qt * 512:(qt + 1) * 512, 512 + g * 256:512 + (g + 1) * 256].rearrange("(s p) d -> p s d", p=128)
                kb.dma("POOL", osv, ost[:], reads=[Bost])
                if self.cfg.get("nsa_barrier"):
                    kb.barrier()
        T.close()
        P.close()


def _consts():
    c = {}
    c["c_ident"] = np.eye(128, dtype=np.float32)
    k = np.arange(128)[:, None]
    q = np.arange(128)[None, :]
    c["c_tri"] = (q >= k).astype(np.float32)
    c["c_atri"] = (k > q).astype(np.float32)
    key = np.arange(S)[None, :]
    c["c_e16"] = (key // 256 == np.arange(16)[:, None]).astype(np.float32)
    c["c_e64"] = (key // 64 == np.arange(64)[:, None]).astype(np.float32)
    ncmp = 255
    cs = np.arange(ncmp) * 16
    ss = np.arange(64) * 64
    ov = np.minimum(cs[:, None] + 32, ss[None, :] + 64) - np.maximum(cs[:, None], ss[None, :])
    ovp = np.zeros((256, 65), np.float32)
    ovp[:255, 0] = 1.0
    ovp[:255, 1:] = np.clip(ov, 0, None) / 32.0
    c["c_ov"] = ovp
    own = np.arange(16)[:, None]
    blk = np.arange(16)[None, :]
    c["c_t1"] = np.where(blk < own, 0.0, -1e30).astype(np.float32).reshape(1, 256)
    c["c_t2"] = (blk == own).astype(np.float32).reshape(1, 256)
    p = np.arange(128)[:, None, None]
    t = np.arange(NT)[None, :, None]
    m = np.arange(64)[None, None, :]
    cur = (t * 128 + p) // 64
    ok = m <= cur
    forced = ok & ((m == 0) | (m >= cur - 1))
    c["c_nfv"] = (ok & ~forced).astype(np.float32)
    c["c_cst"] = np.where(ok, np.where(forced, 1e4 + m, 0.0), -1e30).astype(np.float32)
    c["c_v01"] = ok.astype(np.float32)
    return c


def _core_inputs(b, inp, consts):
    f = lambda a: np.ascontiguousarray(np.asarray(a), dtype=np.float32)
    m = {}
    m["x"] = f(inp["x"][b])
    m["cT"] = f(np.asarray(inp["c"][b]).reshape(8, 128).T)
    m["posT"] = np.ascontiguousarray(np.asarray(inp["positions"][b]).reshape(NT, 128).T.astype(np.int32))
    for k in ("ada_w", "ada_b", "attn_norm", "ffn_norm", "ffn_w_gate", "ffn_w_up", "ffn_w_down"):
        m[k] = f(inp[k])
    m["sp_w_in"] = f(inp["sp_w_in"][0]); m["sp_w_out"] = f(inp["sp_w_out"][0])
    m["moba_q_norm"] = f(inp["moba_q_norm"]); m["moba_k_norm"] = f(inp["moba_k_norm"])
    m["nsa_q_norm"] = f(inp["nsa_q_norm"]); m["nsa_k_norm"] = f(inp["nsa_k_norm"][0])
    m["cmp_posT"] = f(np.asarray(inp["nsa_cmp_pos"][0]).transpose(2, 0, 1))
    m["cmp_w1"] = f(np.asarray(inp["nsa_cmp_w1"][0]).transpose(2, 0, 1, 3))
    m["cmp_w2"] = f(inp["nsa_cmp_w2"][0])
    m["diff_w_in"] = f(inp["diff_w_in"][0]); m["diff_w_out"] = f(inp["diff_w_out"][0])
    m["diff_q_norm"] = f(inp["diff_q_norm"]); m["diff_k_norm"] = f(inp["diff_k_norm"])
    m["diff_lambda"] = f(np.asarray(inp["diff_lambda"][0]).reshape(1, 256))
    m["diff_out_norm"] = f(inp["diff_out_norm"])
    m.update(consts)
    return m


_NC_CACHE = {}


def kernel(**inputs):
    consts = _consts()
    if "nc" not in _NC_CACHE:
        _NC_CACHE["nc"] = Prog({}).build()
    nc = _NC_CACHE["nc"]
    in_maps = [_core_inputs(b, inputs, consts) for b in range(8)]
    res = run_bass_kernel_spmd(nc, in_maps, core_ids=list(range(8)))
    return np.stack([np.asarray(r["out"], dtype=np.float32) for r in res.results], axis=0)
```

```python
from contextlib import ExitStack
import numpy as np
import concourse.bass as bass
import concourse.mybir as mybir
from concourse.bass_utils import run_bass_kernel_spmd

F32 = mybir.dt.float32
BF16 = mybir.dt.bfloat16
I32 = mybir.dt.int32
AF = mybir.ActivationFunctionType
ALU = mybir.AluOpType
AX = mybir.AxisListType

S = 4096
D = 1024
NT = S // 128
FFN = 2816
NJ = FFN // 128
EPS = 1e-6
MASKV = -30000.0
SP_IN = 2840
N_DMA_SEMS = 24


class Buf:
    __slots__ = ("w", "r", "name")

    def __init__(self, name=""):
        self.w = None
        self.r = {}
        self.name = name


class Rot:
    def __init__(self, items):
        self.items = list(items)
        self.i = 0

    def next(self):
        it = self.items[self.i]
        self.i = (self.i + 1) % len(self.items)
        return it


class KB:
    def __init__(self, nc):
        self.nc = nc
        self.es = ExitStack()
        self.engs = {"PE": nc.tensor, "ACT": nc.scalar, "DVE": nc.vector, "POOL": nc.gpsimd, "SP": nc.sync}
        self.sem = {}
        self.cnt = {}
        for e in ("PE", "ACT", "DVE", "POOL"):
            self.sem[e] = self.es.enter_context(nc.semaphore("s_" + e))
            self.cnt[e] = 0
        self.dsem = [self.es.enter_context(nc.semaphore("d_%d" % i)) for i in range(N_DMA_SEMS)]
        self.dcnt = [0] * N_DMA_SEMS
        self.dnext = 0
        self.dnext2 = [0, 0]
        self.waited = {e: {} for e in self.engs}
        self.n_inst = 0
        self.uid = 0
        self.limit = None
        self.log = None

    def name(self, p):
        self.uid += 1
        return "%s_%d" % (p, self.uid)

    def _wait(self, E, key, c, raw=False):
        if key == E and E == "PE":
            return
        w = self.waited[E]
        if w.get(key, 0) >= c:
            return
        w[key] = c
        if isinstance(key, int):
            self.engs[E].wait_ge(self.dsem[key], c)
        else:
            self.engs[E].wait_ge(self.sem[key], c)

    def _deps(self, E, reads, writes):
        for b in reads:
            if b.w is not None:
                self._wait(E, b.w[0], b.w[1], raw=True)
        for b in writes:
            if b.w is not None:
                self._wait(E, b.w[0], b.w[1])
            for k, c in b.r.items():
                self._wait(E, k, c)

    def _mark(self, tok, reads, writes):
        for b in reads:
            if b.r.get(tok[0], 0) < tok[1]:
                b.r[tok[0]] = tok[1]
        for b in writes:
            b.w = tok
            b.r = {}

    def op(self, E, fn, reads=(), writes=()):
        if self.limit is not None and self.n_inst >= self.limit:
            return None
        self._deps(E, reads, writes)
        if self.log is not None:
            import sys as _sys
            self.log.append((self.n_inst, E, _sys._getframe(1).f_lineno))
        ins = fn(self.engs[E])
        self.cnt[E] += 1
        ins.then_inc(self.sem[E], 1)
        tok = (E, self.cnt[E])
        self._mark(tok, reads, writes)
        self.n_inst += 1
        return tok

    def dma(self, Q, out_ap, in_ap, reads=(), writes=(), **kw):
        if self.limit is not None and self.n_inst >= self.limit:
            return None
        self._deps(Q, reads, writes)
        half = N_DMA_SEMS // 2
        qi = 0 if Q == "SP" else 1
        k = qi * half + self.dnext2[qi]
        self.dnext2[qi] = (self.dnext2[qi] + 1) % half
        if self.dcnt[k] > 0:
            self._wait(Q, k, self.dcnt[k])
        if self.log is not None:
            import sys as _sys
            self.log.append((self.n_inst, "DMA-" + Q, _sys._getframe(1).f_lineno))
        ins = self.engs[Q].dma_start(out=out_ap, in_=in_ap, **kw)
        self.dcnt[k] += 16
        ins.then_inc(self.dsem[k], 16)
        tok = (k, self.dcnt[k])
        self._mark(tok, reads, writes)
        self.n_inst += 1
        return tok

    def barrier(self, engines=("PE", "ACT", "DVE", "POOL", "SP")):
        for E in engines:
            for e2 in ("PE", "ACT", "DVE", "POOL"):
                if self.cnt[e2] > 0:
                    self._wait(E, e2, self.cnt[e2])
            for k in range(N_DMA_SEMS):
                if self.dcnt[k] > 0:
                    self._wait(E, k, self.dcnt[k])


class Scope:
    def __init__(self, kb):
        self.kb = kb
        self.es = ExitStack()

    def sb(self, name, shape, dt):
        t = self.es.enter_context(self.kb.nc.sbuf_tensor(self.kb.name(name), list(shape), dt))
        return t, Buf(name)

    def ps(self, name, shape, dt):
        t = self.es.enter_context(self.kb.nc.psum_tensor(self.kb.name(name), list(shape), dt))
        return t, Buf(name)

    def close(self):
        self.kb.barrier()
        self.es.close()


def bc_row(ap_row, n):
    return ap_row.to_broadcast([128, n])


class Prog:
    def __init__(self, cfg):
        self.cfg = cfg
        self.nc = bass.Bass("TRN2", target_bir_lowering=False)
        self.kb = KB(self.nc)
        self.kb.limit = cfg.get("limit")
        self.I = {}
        self.taps = {}

    def din(self, name, shape, dt=F32):
        self.I[name] = self.nc.dram_tensor(name, list(shape), dt, kind="ExternalInput").ap()
        return self.I[name]

    def dscr(self, name, shape, dt):
        if self.cfg.get("debug"):
            return self.nc.dram_tensor(name, list(shape), dt, kind="ExternalOutput").ap()
        return self.nc.dram_tensor(name, list(shape), dt).ap()

    def tap(self, name, shape, dt=F32):
        self.taps[name] = self.nc.dram_tensor(name, list(shape), dt, kind="ExternalOutput").ap()
        return self.taps[name]

    def declare(self):
        d = self.din
        d("x", [S, D]); d("cT", [128, 8]); d("posT", [128, NT], I32)
        d("ada_w", [2, D, 6 * D]); d("ada_b", [2, 6 * D])
        d("attn_norm", [2, D]); d("ffn_norm", [2, D])
        d("ffn_w_gate", [2, D, FFN]); d("ffn_w_up", [2, D, FFN]); d("ffn_w_down", [2, FFN, D])
        d("sp_w_in", [D, SP_IN]); d("sp_w_out", [D, D])
        d("moba_q_norm", [1, 64]); d("moba_k_norm", [1, 64]); d("nsa_q_norm", [1, 64]); d("nsa_k_norm", [3, 64])
        d("cmp_posT", [64, 2, 32]); d("cmp_w1", [64, 2, 32, 256]); d("cmp_w2", [2, 256, 64])
        d("diff_w_in", [D, 3072]); d("diff_w_out", [D, D])
        d("diff_q_norm", [1, 64]); d("diff_k_norm", [1, 64]); d("diff_lambda", [1, 256]); d("diff_out_norm", [1, 128])
        d("c_ident", [128, 128]); d("c_tri", [128, 128]); d("c_atri", [128, 128])
        d("c_e16", [16, S]); d("c_e64", [64, S]); d("c_ov", [256, 65])
        d("c_t1", [1, 256]); d("c_t2", [1, 256])
        d("c_nfv", [128, NT, 64]); d("c_cst", [128, NT, 64]); d("c_v01", [128, NT, 64])
        self.out = self.nc.dram_tensor("out", [S, D], F32, kind="ExternalOutput").ap()
        self.xmid = self.dscr("xmid", [S, D], F32)
        self.x1s = self.dscr("x1s", [S, D], F32)
        self.FT = self.dscr("FT", [2048, S], BF16)
        self.TM = self.dscr("TM", [S, D], BF16)
        self.OS = self.dscr("OS", [S, D], BF16)
        self.H2T = self.dscr("H2T", [D, S], BF16)

    def setup(self):
        kb, I = self.kb, self.I
        self.G = Scope(kb)
        G = self.G
        self.ident, self.Bident = G.sb("ident", [128, 128], BF16)
        kb.dma("POOL", self.ident[:], I["c_ident"][:, :], writes=[self.Bident])
        self.identF, self.BidentF = G.sb("identF", [128, 128], F32)
        kb.dma("SP", self.identF[:], I["c_ident"][:, :], writes=[self.BidentF])
        self.tri, self.Btri = G.sb("tri", [128, 128], BF16)
        kb.dma("POOL", self.tri[:], I["c_tri"][:, :], writes=[self.Btri])
        self.atri, self.Batri = G.sb("atri", [128, 128], BF16)
        kb.dma("POOL", self.atri[:], I["c_atri"][:, :], writes=[self.Batri])
        self.cosT, self.Bcos = G.sb("cosT", [128, NT, 8], F32)
        self.sinT, self.Bsin = G.sb("sinT", [128, NT, 8], F32)
        self.mod, self.Bmod = G.sb("mod", [128, 6, D], F32)
        self.silc, self.Bsilc = G.sb("silc", [128, 8, 128], F32)
        with_scope = Scope(kb)
        T = with_scope
        pi, Bpi = T.sb("pi", [128, NT], I32)
        pf, Bpf = T.sb("pf", [128, NT], F32)
        ang, Bang = T.sb("ang", [128, NT, 8], F32)
        tmp, Btmp = T.sb("tmp", [128, NT, 8], F32)
        tmp2, Btmp2 = T.sb("tmp2", [128, NT, 8], F32)
        kb.dma("SP", pi[:], I["posT"][:, :], writes=[Bpi])
        kb.op("DVE", lambda e: e.tensor_copy(pf[:], pi[:]), reads=[Bpi], writes=[Bpf])
        inv = (1.0 / (np.float32(500000.0) ** (np.arange(0, 16, 2, dtype=np.float32) / np.float32(16)))).astype(np.float32)
        for j in range(8):
            kb.op("DVE", lambda e, j=j: e.tensor_scalar(ang[:, :, j], pf[:], float(inv[j]), None, ALU.mult),
                  reads=[Bpf], writes=[Bang])
        TWO_PI = float(2 * np.pi)
        MAG = 12582912.0

        def sin_of(dst, Bdst, shift):
            if shift != 0.0:
                kb.op("DVE", lambda e: e.tensor_scalar(tmp2[:], ang[:], shift, None, ALU.add), reads=[Bang], writes=[Btmp2])
                src, Bsrc = tmp2, Btmp2
            else:
                src, Bsrc = ang, Bang
            kb.op("DVE", lambda e: e.tensor_scalar(tmp[:], src[:], 1.0 / TWO_PI, MAG, ALU.mult, ALU.add), reads=[Bsrc], writes=[Btmp])
            kb.op("DVE", lambda e: e.tensor_scalar(tmp[:], tmp[:], -MAG, -TWO_PI, ALU.add, ALU.mult), reads=[Btmp], writes=[Btmp])
            kb.op("DVE", lambda e: e.tensor_tensor(tmp[:], src[:], tmp[:], ALU.add), reads=[Bsrc, Btmp], writes=[Btmp])
            kb.op("DVE", lambda e: e.tensor_scalar(tmp[:], tmp[:], float(np.pi), float(-np.pi), ALU.min, ALU.max), reads=[Btmp], writes=[Btmp])
            kb.op("ACT", lambda e: e.activation(dst[:], tmp[:], AF.Sin), reads=[Btmp], writes=[Bdst])

        sin_of(self.sinT, self.Bsin, 0.0)
        sin_of(self.cosT, self.Bcos, float(np.pi / 2))
        ct, Bct = T.sb("ct", [128, 8], F32)
        kb.dma("SP", ct[:], I["cT"][:, :], writes=[Bct])
        kb.op("ACT", lambda e: e.activation(ct[:], ct[:], AF.Silu), reads=[Bct], writes=[Bct])
        kb.op("DVE", lambda e: e.tensor_copy(self.silc[:], ct[:].unsqueeze(2).to_broadcast([128, 8, 128])),
              reads=[Bct], writes=[self.Bsilc])
        T.close()

    def compute_mod(self, li):
        kb, I = self.kb, self.I
        T = Scope(kb)
        wch = [T.sb("adaw", [128, 8, 512], F32) for _ in range(2)]
        bch = [T.sb("adab", [128, 512], F32) for _ in range(2)]
        pm = [T.ps("pmod", [128, 512], F32) for _ in range(2)]
        nrm, Bnrm = T.sb("nrm", [128, 2, D], F32)
        kb.dma("SP", nrm[:, 0, :], bc_row(I["attn_norm"][li:li + 1, :], D), writes=[Bnrm])
        kb.dma("SP", nrm[:, 1, :], bc_row(I["ffn_norm"][li:li + 1, :], D), writes=[Bnrm])
        wv = I["ada_w"][li].rearrange("(kc p) n -> p kc n", p=128)
        for g in range(12):
            w, Bw = wch[g % 2]
            b, Bb = bch[g % 2]
            p, Bp = pm[g % 2]
            kb.dma("SP", w[:], wv[:, :, g * 512:(g + 1) * 512], writes=[Bw])
            kb.dma("SP", b[:], bc_row(I["ada_b"][li:li + 1, g * 512:(g + 1) * 512], 512), writes=[Bb])
            for kc in range(8):
                kb.op("PE", lambda e, kc=kc: e.matmul(p[:], self.silc[:, kc, :], w[:, kc, :], start=(kc == 0), stop=(kc == 7)),
                      reads=[Bw, self.Bsilc], writes=[Bp])
            sl = self.mod[:, g // 2, (g % 2) * 512:(g % 2 + 1) * 512]
            kb.op("DVE", lambda e: e.tensor_tensor(sl, p[:], b[:], ALU.add), reads=[Bp, Bb], writes=[self.Bmod])
        kb.op("DVE", lambda e: e.scalar_tensor_tensor(self.mod[:, 1, :], self.mod[:, 1, :], 1.0, nrm[:, 0, :], ALU.add, ALU.mult),
              reads=[self.Bmod, Bnrm], writes=[self.Bmod])
        kb.op("DVE", lambda e: e.scalar_tensor_tensor(self.mod[:, 4, :], self.mod[:, 4, :], 1.0, nrm[:, 1, :], ALU.add, ALU.mult),
              reads=[self.Bmod, Bnrm], writes=[self.Bmod])
        T.close()

    def rms_mod_tile(self, T, xt, Bxt, hb, Bhb, gi, si, scr):
        kb = self.kb
        junk, Bjunk, ss, Bss, hn, Bhn = scr
        kb.op("ACT", lambda e: e.activation(junk[:], xt[:], AF.Square, accum_out=ss[:]), reads=[Bxt], writes=[Bjunk, Bss])
        kb.op("ACT", lambda e: e.activation(ss[:], ss[:], AF.Sqrt, bias=self.epsb[:], scale=1.0 / D), reads=[Bss, self.Bepsb], writes=[Bss])
        kb.op("DVE", lambda e: e.reciprocal(ss[:], ss[:]), reads=[Bss], writes=[Bss])
        kb.op("DVE", lambda e: e.scalar_tensor_tensor(hn[:], xt[:], ss[:, 0:1], self.mod[:, gi, :], ALU.mult, ALU.mult),
              reads=[Bxt, Bss, self.Bmod], writes=[Bhn])
        kb.op("POOL", lambda e: e.tensor_tensor(hb[:], hn[:], self.mod[:, si, :], ALU.add), reads=[Bhn, self.Bmod], writes=[Bhb])

    def transpose8(self, src, Bsrc, pt, Bpt, dst_ap, Bdst, eng="ACT"):
        kb = self.kb
        for j in range(8):
            kb.op("PE", lambda e, j=j: e.transpose(pt[:, j * 128:(j + 1) * 128], src[:, j * 128:(j + 1) * 128], self.ident[:]),
                  reads=[Bsrc, self.Bident], writes=[Bpt])
        if eng == "ACT":
            kb.op("ACT", lambda e: e.copy(dst_ap, pt[:].rearrange("p (j t) -> p j t", j=8)), reads=[Bpt], writes=[Bdst])
        else:
            kb.op("DVE", lambda e: e.tensor_copy(dst_ap, pt[:].rearrange("p (j t) -> p j t", j=8)), reads=[Bpt], writes=[Bdst])

    def load_w_bf16(self, dst, Bdst, w_ap, nk):
        for kc in range(nk):
            self.kb.dma("POOL", dst[:, kc, :], w_ap[kc * 128:(kc + 1) * 128, :], writes=[Bdst])

    def phase_a(self, li, x_src):
        kb, I = self.kb, self.I
        T = Scope(kb)
        if li == 1:
            w_ap, ncols = I["diff_w_in"], 3072
            groups = []
            for g in range(6):
                if g < 4:
                    gi = 0 if g < 2 else 1
                    blocks = [("T", g * 512 + b * 128) for b in range(4)]
                    groups.append(dict(c0=g * 512, w=512, qk=[(0, 8)], gain=gi, blocks=blocks, gate=None))
                else:
                    blocks = [("V", (g - 4) * 512 + b * 128) for b in range(4)]
                    groups.append(dict(c0=g * 512, w=512, qk=[], gain=None, blocks=blocks, gate=None))
            gain_srcs = [[(I["diff_q_norm"][0:1, :], 0, 8)], [(I["diff_k_norm"][0:1, :], 0, 8)]]
        else:
            w_ap, ncols = I["sp_w_in"], SP_IN
            kn = I["nsa_k_norm"]
            groups = [
                dict(c0=0, w=512, qk=[(0, 8)], gain=0, blocks=[("T", b * 128) for b in range(4)], gate=None),
                dict(c0=512, w=512, qk=[(0, 8)], gain=1, blocks=[("T", 512 + b * 128) for b in range(4)], gate=None),
                dict(c0=1024, w=512, qk=[], gain=None, blocks=[("V", b * 128) for b in range(4)], gate=None),
                dict(c0=1536, w=512, qk=[(0, 8)], gain=2, blocks=[("T", 1024 + b * 128) for b in range(4)], gate=None),
                dict(c0=2048, w=512, qk=[(0, 2), (4, 6)], gain=3,
                     blocks=[("T", 1536), ("T", 1664), ("T", 1792), ("V", 512)], gate=None),
                dict(c0=2560, w=280, qk=[(0, 2)], gain=4, blocks=[("T", 1920), ("V", 640)], gate=(256, 24)),
            ]
            gain_srcs = [
                [(I["moba_q_norm"][0:1, :], 0, 8)], [(I["moba_k_norm"][0:1, :], 0, 8)], [(I["nsa_q_norm"][0:1, :], 0, 8)],
                [(kn[0:1, :], 0, 2), (kn[1:2, :], 4, 6)], [(kn[2:3, :], 0, 2)],
            ]
        nk = 8
        W, BW = self.w_in_pre
        gains = []
        for gs in gain_srcs:
            gr, Bgr = T.sb("gainrow", [128, 8, 64], F32)
            kb.op("POOL", lambda e: e.memset(gr[:], 1.0), writes=[Bgr])
            for (src, u0, u1) in gs:
                for u in range(u0, u1):
                    kb.dma("SP", gr[:, u, :], bc_row(src, 64), writes=[Bgr])
            gains.append((gr, Bgr))
        NG = len(groups)
        xts = Rot([T.sb("xt", [128, D], F32) for _ in range(3)])
        hbs = Rot([T.sb("hb", [128, D], BF16) for _ in range(2)])
        hTs = Rot([T.sb("hT", [128, 8, 128], BF16) for _ in range(2)])
        junk, Bjunk = T.sb("junk", [128, D], BF16)
        sss = Rot([T.sb("ss", [128, 1], F32) for _ in range(2)])
        hn, Bhn = T.sb("hn", [128, D], F32)
        sqs = [T.sb("sq", [128, 512], F32) for _ in range(NG)]
        ss8l = [T.sb("ss8", [128, 8], F32) for _ in range(NG)]
        qnl = [T.sb("qn", [128, 8, 64], F32) for _ in range(NG)]
        postl = [T.sb("post", [128, 512], BF16) for _ in range(NG)]
        rtl = [T.sb("rt", [128, 4, 8, 8], F32) for _ in range(NG)]
        pts = Rot([T.ps("ptA", [128, 1024], BF16) for _ in range(1)])
        pgl = [T.ps("pgA", [128, 512], F32) for _ in range(NG)]
        pbt, _ = T.ps("ptB", [128, 1024], BF16)
        pbh = Rot([(pbt[:, 0:512], Buf("pb0")), (pbt[:, 512:1024], Buf("pb1"))])
        stages = {}
        for gi_, g in enumerate(groups):
            if any(k == "T" for k, _ in g["blocks"]):
                stages[gi_] = [T.sb("stage", [128, 4, 512], BF16) for _ in range(2)]

        def load_x(t):
            xt, Bxt = xts.next()
            kb.dma("SP", xt[:], x_src[t * 128:(t + 1) * 128, :], writes=[Bxt])
            return xt, Bxt

        def prep(t, xtb):
            xt, Bxt = xtb
            hb, Bhb = hbs.next()
            ss, Bss = sss.next()
            self.rms_mod_tile(T, xt, Bxt, hb, Bhb, 1, 0, (junk, Bjunk, ss, Bss, hn, Bhn))
            hT, BhT = hTs.next()
            pt, Bpt = pts.next()
            self.transpose8(hb, Bhb, pt, Bpt, hT[:], BhT, eng="ACT")
            return hT, BhT

        xq = [load_x(0)]
        if NT > 1:
            xq.append(load_x(1))
        hq = [prep(0, xq.pop(0))]
        for t in range(NT):
            hT, BhT = hq.pop(0)
            if t + 2 < NT:
                xq.append(load_x(t + 2))
            tt = t % 4
            half = (t // 4) % 2
            for gi_, g in enumerate(groups):
                w = g["w"]
                pg, Bpg = pgl[gi_]
                for kc in range(8):
                    kb.op("PE", lambda e, kc=kc, pg=pg, w=w, g=g: e.matmul(pg[:, 0:w], hT[:, kc, :], W[:, kc, g["c0"]:g["c0"] + w],
                                                                          start=(kc == 0), stop=(kc == 7)),
                          reads=[BhT, BW], writes=[Bpg])
            if t + 1 < NT:
                hq.append(prep(t + 1, xq.pop(0)))
            info = []
            for gi_, g in enumerate(groups):
                w = g["w"]
                wq = (w // 64) * 64
                info.append((gi_, g, w, wq, wq // 64))
            for gi_, g, w, wq, nu in info:
                pg, Bpg = pgl[gi_]
                post, Bpost = postl[gi_]
                sq, Bsq = sqs[gi_]
                if g["qk"]:
                    kb.op("ACT", lambda e, sq=sq, pg=pg, wq=wq: e.activation(sq[:, 0:wq], pg[:, 0:wq], AF.Square), reads=[Bpg], writes=[Bsq])
                else:
                    kb.op("ACT", lambda e, post=post, pg=pg, wq=wq: e.copy(post[:, 0:wq], pg[:, 0:wq]), reads=[Bpg], writes=[Bpost])
                if g["gate"] is not None:
                    gc0, gw = g["gate"]
                    kb.op("ACT", lambda e, pg=pg, gc0=gc0, gw=gw: e.activation(self.gate_sb[:, t, :], pg[:, gc0:gc0 + gw], AF.Sigmoid),
                          reads=[Bpg], writes=[self.Bgate])
            for gi_, g, w, wq, nu in info:
                if not g["qk"]:
                    continue
                sq, Bsq = sqs[gi_]
                ss8, Bss8 = ss8l[gi_]
                kb.op("DVE", lambda e, ss8=ss8, sq=sq, wq=wq, nu=nu: e.tensor_reduce(ss8[:, 0:nu], sq[:, 0:wq].rearrange("p (u d) -> p u d", d=64), AX.X, ALU.add),
                      reads=[Bsq], writes=[Bss8])
            for gi_, g, w, wq, nu in info:
                if not g["qk"]:
                    continue
                ss8, Bss8 = ss8l[gi_]
                kb.op("ACT", lambda e, ss8=ss8, nu=nu: e.activation(ss8[:, 0:nu], ss8[:, 0:nu], AF.Sqrt, bias=self.epsb[:], scale=1.0 / 64),
                      reads=[Bss8, self.Bepsb], writes=[Bss8])
            for gi_, g, w, wq, nu in info:
                if not g["qk"]:
                    continue
                pg, Bpg = pgl[gi_]
                ss8, Bss8 = ss8l[gi_]
                qn, Bqn = qnl[gi_]
                gr, Bgr = gains[g["gain"]]
                kb.op("DVE", lambda e, ss8=ss8, nu=nu: e.reciprocal(ss8[:, 0:nu], ss8[:, 0:nu]), reads=[Bss8], writes=[Bss8])
                qk_units = set()
                for (u0, u1) in g["qk"]:
                    qk_units.update(range(u0, u1))
                for u in [u for u in range(nu) if u not in qk_units]:
                    kb.op("DVE", lambda e, u=u, ss8=ss8: e.memset(ss8[:, u:u + 1], 1.0), writes=[Bss8])
                kb.op("DVE", lambda e, qn=qn, pg=pg, ss8=ss8, nu=nu, wq=wq: e.tensor_tensor(
                    qn[:, 0:nu, :], pg[:, 0:wq].rearrange("p (u d) -> p u d", d=64),
                    ss8[:, 0:nu].unsqueeze(2).to_broadcast([128, nu, 64]), ALU.mult),
                      reads=[Bpg, Bss8], writes=[Bqn])
                kb.op("POOL", lambda e, qn=qn, gr=gr, nu=nu: e.tensor_tensor(qn[:, 0:nu, :], qn[:, 0:nu, :], gr[:, 0:nu, :], ALU.mult),
                      reads=[Bqn, Bgr], writes=[Bqn])
            for gi_, g, w, wq, nu in info:
                if not g["qk"]:
                    continue
                qn, Bqn = qnl[gi_]
                post, Bpost = postl[gi_]
                kb.op("ACT", lambda e, post=post, qn=qn, wq=wq, nu=nu: e.copy(post[:, 0:wq], qn[:, 0:nu, :].rearrange("p u d -> p (u d)")),
                      reads=[Bqn], writes=[Bpost])
            for gi_, g, w, wq, nu in info:
                if not g["qk"]:
                    continue
                qn, Bqn = qnl[gi_]
                post, Bpost = postl[gi_]
                rt, Brt = rtl[gi_]
                postv = post[:, 0:wq].rearrange("p (u d) -> p u d", d=64)
                for (u0, u1) in g["qk"]:
                    n_ = u1 - u0
                    cosb = self.cosT[:, t, :].unsqueeze(1).to_broadcast([128, n_, 8])
                    sinb = self.sinT[:, t, :].unsqueeze(1).to_broadcast([128, n_, 8])
                    t1 = qn[:, u0:u1, 0:8]
                    t2 = qn[:, u0:u1, 8:16]
                    kb.op("DVE", lambda e, rt=rt, n_=n_, t1=t1, cosb=cosb: e.tensor_tensor(rt[:, 0, 0:n_, :], t1, cosb, ALU.mult), reads=[Bqn, self.Bcos], writes=[Brt])
                    kb.op("DVE", lambda e, rt=rt, n_=n_, t2=t2, sinb=sinb: e.tensor_tensor(rt[:, 1, 0:n_, :], t2, sinb, ALU.mult), reads=[Bqn, self.Bsin], writes=[Brt])
                    kb.op("DVE", lambda e, rt=rt, n_=n_, t2=t2, cosb=cosb: e.tensor_tensor(rt[:, 2, 0:n_, :], t2, cosb, ALU.mult), reads=[Bqn, self.Bcos], writes=[Brt])
                    kb.op("DVE", lambda e, rt=rt, n_=n_, t1=t1, sinb=sinb: e.tensor_tensor(rt[:, 3, 0:n_, :], t1, sinb, ALU.mult), reads=[Bqn, self.Bsin], writes=[Brt])
                    kb.op("DVE", lambda e, rt=rt, n_=n_, postv=postv, u0=u0, u1=u1: e.tensor_tensor(postv[:, u0:u1, 0:8], rt[:, 0, 0:n_, :], rt[:, 1, 0:n_, :], ALU.subtract),
                          reads=[Brt], writes=[Bpost])
                    kb.op("DVE", lambda e, rt=rt, n_=n_, postv=postv, u0=u0, u1=u1: e.tensor_tensor(postv[:, u0:u1, 8:16], rt[:, 2, 0:n_, :], rt[:, 3, 0:n_, :], ALU.add),
                          reads=[Brt], writes=[Bpost])
            for gi_, g, w, wq, nu in info:
                post, Bpost = postl[gi_]
                tblocks = [(bi, dst) for bi, (k, dst) in enumerate(g["blocks"]) if k == "T"]
                if tblocks:
                    pb, Bpb = pbh.next()
                    st, Bst = stages[gi_][half]
                    for bi, dst in tblocks:
                        kb.op("PE", lambda e, bi=bi, pb=pb, post=post: e.transpose(pb[:, bi * 128:(bi + 1) * 128], post[:, bi * 128:(bi + 1) * 128], self.ident[:]),
                              reads=[Bpost, self.Bident], writes=[Bpb])
                    b0, b1 = tblocks[0][0], tblocks[-1][0] + 1
                    kb.op("ACT", lambda e, st=st, pb=pb, b0=b0, b1=b1: e.copy(st[:, b0:b1, tt * 128:(tt + 1) * 128],
                                                                              pb[:, b0 * 128:b1 * 128].rearrange("p (b t) -> p b t", t=128)),
                          reads=[Bpb], writes=[Bst])
                    if tt == 3:
                        t0 = (t - 3) * 128
                        for bi, dst in tblocks:
                            kb.dma("POOL", self.FT[dst:dst + 128, t0:t0 + 512], st[:, bi, :], reads=[Bst])
                for bi, (k, dst) in enumerate(g["blocks"]):
                    if k == "V":
                        kb.dma("POOL", self.TM[t * 128:(t + 1) * 128, dst:dst + 128], post[:, bi * 128:(bi + 1) * 128], reads=[Bpost])
        T.close()

    def attn_qtile(self, R, q_rhs, Bq, k_lhsT, Bk, v_rhs, Bv, vw, pairs, O, BO, o_off, post_exp=None, bank_of=lambda s: 0, vstat=False, hooks=None):
        kb = self.kb
        LOOK = self.cfg.get("look", 5)
        last = {}
        started = set()
        for kc, subs in pairs:
            for s, kind in subs:
                last[s] = kc
        live = {}

        def stage1(i):
            kc, subs = pairs[i]
            pS, BpS = R["pS"].next()
            kl, krows = k_lhsT(kc)
            c0 = min(s_ for s_, _ in subs) * 128
            c1 = (max(s_ for s_, _ in subs) + 1) * 128
            if post_exp is not None:
                c0, c1 = 0, 512
            kb.op("PE", lambda e: e.matmul(pS[0:krows, c0:c1], kl, q_rhs[:, c0:c1], start=True, stop=True), reads=[Bk, Bq], writes=[BpS])
            pT, BpT = R["pT"].next()
            kb.op("ACT", lambda e: e.activation(pT[0:krows, c0:c1], pS[0:krows, c0:c1], AF.Exp, scale=0.125), reads=[BpS], writes=[BpT])
            if post_exp is not None:
                post_exp(kc, pT, BpT, krows)
            for s, kind in subs:
                if kind == "tri":
                    kb.op("DVE", lambda e, s=s: e.tensor_tensor(pT[:, s * 128:(s + 1) * 128], pT[:, s * 128:(s + 1) * 128], self.tri[:], ALU.mult),
                          reads=[BpT, self.Btri], writes=[BpT])
                elif kind == "atri":
                    kb.op("DVE", lambda e, s=s: e.tensor_tensor(pT[:, s * 128:(s + 1) * 128], pT[:, s * 128:(s + 1) * 128], self.atri[:], ALU.mult),
                          reads=[BpT, self.Batri], writes=[BpT])
            live[i] = (pT, BpT, krows)

        def stage2(i):
            kc, subs = pairs[i]
            pT, BpT, krows = live.pop(i)
            if vstat:
                c0 = min(s_ for s_, _ in subs) * 128
                c1 = (max(s_ for s_, _ in subs) + 1) * 128
                st_flag = 0 not in started
                started.add(0)
                kb.op("PE", lambda e: e.matmul(O[0:vw, c0:c1], v_rhs(kc)[0:krows, 0:vw], pT[0:krows, c0:c1],
                                               start=st_flag, stop=(i == len(pairs) - 1), skip_group_check=True),
                      reads=[BpT, Bv], writes=[BO])
                return
            for s, kind in subs:
                bk = bank_of(s)
                st_flag = bk not in started
                started.add(bk)
                kb.op("PE", lambda e, s=s, st_flag=st_flag: e.matmul(o_off(O, s), pT[0:krows, s * 128:(s + 1) * 128], v_rhs(kc)[0:krows, :],
                                                                     start=st_flag, stop=(last[s] == kc), skip_group_check=True),
                      reads=[BpT, Bv], writes=[BO])

        n = len(pairs)
        for i in range(n + LOOK):
            if i < n:
                stage1(i)
            if hooks and i in hooks:
                hooks.pop(i)()
            if i - LOOK >= 0:
                stage2(i - LOOK)
        if hooks:
            for k in sorted(hooks):
                hooks.pop(k)()

    def ot_to_tok(self, OT, BOT, vw, osb, Bosb, Otok, BOtok):
        kb = self.kb
        kb.op("ACT", lambda e: e.copy(osb[0:vw, :], OT[0:vw, :]), reads=[BOT], writes=[Bosb])
        for s in range(4):
            kb.op("PE", lambda e, s=s: e.transpose(Otok[:, s * 128:s * 128 + vw], osb[0:vw, s * 128:(s + 1) * 128], self.identF[0:vw, 0:vw]),
                  reads=[Bosb, self.BidentF], writes=[BOtok])

    @staticmethod
    def causal_pairs(qt):
        pairs = []
        for kc in range(4 * qt + 4):
            j = kc - 4 * qt
            if j < 0:
                pairs.append((kc, [(s, "full") for s in range(4)]))
            else:
                pairs.append((kc, [(s, "tri" if s == j else "full") for s in range(j, 4)]))
        return pairs

    def phase_b_diff(self, li_odd_index):
        kb, I = self.kb, self.I
        T = Scope(kb)
        lam_init = 0.8 - 0.6 * float(np.exp(-0.3 * 1))
        lp, Blp = T.sb("lp", [128, 4, 64], F32)
        kb.dma("SP", lp[:].rearrange("p a d -> p (a d)"), bc_row(I["diff_lambda"][0:1, :], 256), writes=[Blp])
        l2, Bl2 = T.sb("l2", [128, 2, 64], F32)
        kb.op("DVE", lambda e: e.tensor_tensor(l2[:, 0, :], lp[:, 0, :], lp[:, 1, :], ALU.mult), reads=[Blp], writes=[Bl2])
        kb.op("DVE", lambda e: e.tensor_tensor(l2[:, 1, :], lp[:, 2, :], lp[:, 3, :], ALU.mult), reads=[Blp], writes=[Bl2])
        ls, Bls = T.sb("ls", [128, 2], F32)
        kb.op("DVE", lambda e: e.tensor_reduce(ls[:], l2[:], AX.X, ALU.add), reads=[Bl2], writes=[Bls])
        kb.op("ACT", lambda e: e.activation(ls[:], ls[:], AF.Exp), reads=[Bls], writes=[Bls])
        nlam, Bnlam = T.sb("nlam", [128, 1], F32)
        kb.op("DVE", lambda e: e.tensor_tensor(nlam[:], ls[:, 1:2], ls[:, 0:1], ALU.subtract), reads=[Bls], writes=[Bnlam])
        kb.op("DVE", lambda e: e.tensor_scalar(nlam[:], nlam[:], -lam_init, None, ALU.add), reads=[Bnlam], writes=[Bnlam])
        go, Bgo = T.sb("go", [128, 128], F32)
        kb.dma("SP", go[:], bc_row(I["diff_out_norm"][0:1, :], 128), writes=[Bgo])
        kb.op("DVE", lambda e: e.tensor_scalar(go[:], go[:], 1.0 - lam_init, None, ALU.mult), reads=[Bgo], writes=[Bgo])

        KTs = Rot([T.sb("KT", [128, S], BF16) for _ in range(2)])
        QTs = Rot([T.sb("QT", [128, S], BF16) for _ in range(2)])
        Vs = Rot([T.sb("V", [128, NT, 129], BF16) for _ in range(2)])
        for (v, Bv) in Vs.items:
            kb.op("POOL", lambda e, v=v: e.memset(v[:, :, 128:129], 1.0), writes=[Bv])
        R = dict(pS=Rot([T.ps("pS", [128, 512], F32) for _ in range(3)]),
                 pT=Rot([T.sb("pT", [128, 512], BF16) for _ in range(8)]))
        Os = [[T.ps("O", [128, 512], F32) for _ in range(2)] for _ in range(2)]
        osts = Rot([T.sb("ost", [128, 4, 128], BF16) for _ in range(2)])
        osbs = Rot([[[T.sb("osbd", [128, 512], F32) for _ in range(2)] for _ in range(2)] for _ in range(2)])
        pending = []
        a1ts = Rot([T.sb("a1t", [128, 4, 128], F32) for _ in range(2)])
        rdts = Rot([T.sb("rdt", [128, 4, 4], F32) for _ in range(2)])
        a0s = Rot([T.sb("a0", [128, 128], F32) for _ in range(2)])
        a1s = Rot([T.sb("a1", [128, 128], F32) for _ in range(2)])
        rds = Rot([T.sb("rd", [128, 4], F32) for _ in range(4)])
        junk, Bjunk = T.sb("junkb", [128, 128], BF16)

        def load_head(h):
            KT, BKT = KTs.next()
            QT, BQT = QTs.next()
            V, BV = Vs.next()
            kb.dma("SP", QT[:], self.FT[h * 128:(h + 1) * 128, :], writes=[BQT])
            kb.dma("SP", KT[:], self.FT[1024 + h * 128:1024 + (h + 1) * 128, :], writes=[BKT])
            tmv = self.TM[:, h * 128:(h + 1) * 128].rearrange("(c p) d -> p c d", p=128)
            for c4 in range(4):
                kb.dma("SP", V[:, c4 * 8:(c4 + 1) * 8, 0:128], tmv[:, c4 * 8:(c4 + 1) * 8, :], writes=[BV])
            return (KT, BKT, QT, BQT, V, BV)

        nxt = load_head(0)
        for h in range(8):
            KT, BKT, QT, BQT, V, BV = nxt
            if h + 1 < 8:
                nxt = load_head(h + 1)
            for qt in range(8):
                pairs = self.causal_pairs(qt)
                for c in range(2):
                    def o_off(O, s, c=c):
                        return Os[c][s // 2][0][:, (s % 2) * 256:(s % 2) * 256 + 129]
                    hooks = None
                    if c == 0 and pending:
                        e2_, e3_ = pending.pop(0)
                        hooks = {min(6, len(pairs) - 1): e2_, 10 ** 6: e3_}
                    self.attn_qtile(R, QT[64 * c:64 * c + 64, qt * 512:(qt + 1) * 512], BQT,
                                    lambda kc, c=c: (KT[64 * c:64 * c + 64, kc * 128:(kc + 1) * 128], 128), BKT,
                                    lambda kc: V[:, kc, :], BV, 129, pairs, None, Os[c][0][1], o_off, bank_of=lambda s: s // 2, hooks=hooks)
                oset = osbs.next()
                for c in range(2):
                    for b_ in range(2):
                        kb.op("DVE", lambda e, c=c, b_=b_: e.tensor_copy(oset[c][b_][0][:, 0:385], Os[c][b_][0][:, 0:385]),
                              reads=[Os[c][0][1]], writes=[oset[c][b_][1]])

                ep = dict(oset=oset, h=h, qt=qt)
                a1t, Ba1t = a1ts.next()
                rdt, Brdt = rdts.next()
                ep.update(a1t=a1t, Ba1t=Ba1t, rdt=rdt, Brdt=Brdt)

                def e1(ep=ep):
                    oset, a1t, Ba1t, rdt, Brdt = ep["oset"], ep["a1t"], ep["Ba1t"], ep["rdt"], ep["Brdt"]
                    for s in range(4):
                        O0 = oset[0][s // 2][0][:, (s % 2) * 256:(s % 2) * 256 + 129]
                        O1 = oset[1][s // 2][0][:, (s % 2) * 256:(s % 2) * 256 + 129]
                        BO0, BO1 = oset[0][s // 2][1], oset[1][s // 2][1]
                        a0, Ba0 = a0s.next()
                        kb.op("DVE", lambda e, s=s, O0=O0: e.reciprocal(rdt[:, s, 0:1], O0[:, 128:129]), reads=[BO0], writes=[Brdt])
                        kb.op("DVE", lambda e, s=s, O1=O1: e.reciprocal(rdt[:, s, 1:2], O1[:, 128:129]), reads=[BO1], writes=[Brdt])
                        kb.op("DVE", lambda e, s=s: e.tensor_tensor(rdt[:, s, 1:2], rdt[:, s, 1:2], nlam[:], ALU.mult), reads=[Brdt, Bnlam], writes=[Brdt])
                        kb.op("DVE", lambda e, s=s, a0=a0, O0=O0: e.tensor_scalar(a0[:], O0[:, 0:128], rdt[:, s, 0:1], None, ALU.mult), reads=[BO0, Brdt], writes=[Ba0])
                        kb.op("DVE", lambda e, s=s, a0=a0, O1=O1: e.scalar_tensor_tensor(a1t[:, s, :], O1[:, 0:128], rdt[:, s, 1:2], a0[:], ALU.mult, ALU.add),
                              reads=[BO1, Brdt, Ba0], writes=[Ba1t])

                def e2(ep=ep):
                    a1t, Ba1t, rdt, Brdt = ep["a1t"], ep["Ba1t"], ep["rdt"], ep["Brdt"]
                    for s in range(4):
                        kb.op("ACT", lambda e, s=s: e.activation(junk[:], a1t[:, s, :], AF.Square, accum_out=rdt[:, s, 2:3]), reads=[Ba1t], writes=[Bjunk, Brdt])
                    kb.op("ACT", lambda e: e.activation(rdt[:, :, 2:3], rdt[:, :, 2:3], AF.Sqrt, bias=self.epsb[:], scale=1.0 / 128),
                          reads=[Brdt, self.Bepsb], writes=[Brdt])

                def e3(ep=ep):
                    a1t, Ba1t, rdt, Brdt, h, qt = ep["a1t"], ep["Ba1t"], ep["rdt"], ep["Brdt"], ep["h"], ep["qt"]
                    ost, Bost = osts.next()
                    kb.op("DVE", lambda e: e.reciprocal(rdt[:, :, 3:4], rdt[:, :, 2:3]), reads=[Brdt], writes=[Brdt])
                    for s in range(4):
                        kb.op("DVE", lambda e, s=s, ost=ost: e.scalar_tensor_tensor(ost[:, s, :], a1t[:, s, :], rdt[:, s, 3:4], go[:], ALU.mult, ALU.mult),
                              reads=[Ba1t, Brdt, Bgo], writes=[Bost])
                    osv = self.OS[qt * 512:(qt + 1) * 512, h * 128:(h + 1) * 128].rearrange("(s p) d -> p s d", p=128)
                    kb.dma("POOL", osv, ost[:], reads=[Bost])
                e1()
                pending.append((e2, e3))
        while pending:
            e2_, e3_ = pending.pop(0)
            e2_()
            e3_()
        T.close()

    def phase_c1(self, li, x_src, w_out_ap):
        kb, I = self.kb, self.I
        T = Scope(kb)
        W, BW = T.sb("w_out", [128, 8, D], BF16)
        self.load_w_bf16(W, BW, w_out_ap, 8)
        ots = Rot([T.sb("ot", [128, D], BF16) for _ in range(3)])
        xts = Rot([T.sb("xt", [128, D], F32) for _ in range(3)])
        oTs = Rot([T.sb("oT", [128, 8, 128], BF16) for _ in range(2)])
        x1s = Rot([T.sb("x1", [128, D], F32) for _ in range(2)])
        hbs = Rot([T.sb("hb", [128, D], BF16) for _ in range(2)])
        ytmp, Bytmp = T.sb("ytmp", [128, D], F32)
        junk, Bjunk = T.sb("junk", [128, D], BF16)
        sss = Rot([T.sb("ss", [128, 1], F32) for _ in range(2)])
        hn, Bhn = T.sb("hn", [128, D], F32)
        stg = [T.sb("stage", [128, 8, 512], BF16) for _ in range(2)]
        pts = Rot([T.ps("ptA", [128, 1024], BF16) for _ in range(2)])
        pys = Rot([T.ps("py", [128, 512], F32) for _ in range(4)])

        def load(t):
            ot, Bot = ots.next()
            xt, Bxt = xts.next()
            kb.dma("SP", ot[:], self.OS[t * 128:(t + 1) * 128, :], writes=[Bot])
            kb.dma("SP", xt[:], x_src[t * 128:(t + 1) * 128, :], writes=[Bxt])
            return ot, Bot, xt, Bxt

        def front(t, ld):
            ot, Bot, xt, Bxt = ld
            oT, BoT = oTs.next()
            pt, Bpt = pts.next()
            self.transpose8(ot, Bot, pt, Bpt, oT[:], BoT, eng="ACT")
            pyl = []
            for g in range(2):
                py, Bpy = pys.next()
                for kc in range(8):
                    kb.op("PE", lambda e, kc=kc, g=g, py=py: e.matmul(py[:], oT[:, kc, :], W[:, kc, g * 512:(g + 1) * 512], start=(kc == 0), stop=(kc == 7)),
                          reads=[BoT, BW], writes=[Bpy])
                pyl.append((py, Bpy))
            return pyl, xt, Bxt

        def back(t, fr):
            pyl, xt, Bxt = fr
            x1, Bx1 = x1s.next()
            for g in range(2):
                py, Bpy = pyl[g]
                sl = slice(g * 512, (g + 1) * 512)
                kb.op("DVE", lambda e, py=py, sl=sl: e.tensor_tensor(ytmp[:, sl], py[:], self.mod[:, 2, sl], ALU.mult), reads=[Bpy, self.Bmod], writes=[Bytmp])
                kb.op("POOL", lambda e, sl=sl: e.tensor_tensor(x1[:, sl], ytmp[:, sl], xt[:, sl], ALU.add), reads=[Bytmp, Bxt], writes=[Bx1])
            kb.dma("POOL", self.x1s[t * 128:(t + 1) * 128, :], x1[:], reads=[Bx1])
            hb, Bhb = hbs.next()
            ss, Bss = sss.next()
            self.rms_mod_tile(T, x1, Bx1, hb, Bhb, 4, 3, (junk, Bjunk, ss, Bss, hn, Bhn))
            pt, Bpt = pts.next()
            tt = t % 4
            st, Bst = stg[(t // 4) % 2]
            self.transpose8(hb, Bhb, pt, Bpt, st[:, :, tt * 128:(tt + 1) * 128], Bst, eng="ACT")
            if tt == 3:
                t0 = (t - 3) * 128
                h2v = self.H2T.rearrange("(j p) t -> p j t", p=128)
                kb.dma("POOL", h2v[:, :, t0:t0 + 512], st[:], reads=[Bst])

        lds = [load(0)]
        if NT > 1:
            lds.append(load(1))
        fr = front(0, lds.pop(0))
        for t in range(NT):
            if t + 2 < NT:
                lds.append(load(t + 2))
            nfr = front(t + 1, lds.pop(0)) if t + 1 < NT else None
            back(t, fr)
            fr = nfr
        T.close()

    def phase_c2(self, li, x_dst):
        kb, I = self.kb, self.I
        T = Scope(kb)
        Wg, BWg = T.sb("wg", [128, 8, FFN], BF16)
        Wu, BWu = T.sb("wu", [128, 8, FFN], BF16)
        Wd, BWd = T.sb("wd", [128, NJ, D], BF16)
        self.load_w_bf16(Wg, BWg, I["ffn_w_gate"][li], 8)
        self.load_w_bf16(Wu, BWu, I["ffn_w_up"][li], 8)
        self.load_w_bf16(Wd, BWd, I["ffn_w_down"][li], NJ)
        TT = 256
        h2s = Rot([T.sb("h2T", [128, 8, TT], BF16) for _ in range(2)])
        act, Bact = T.sb("act", [128, NJ, TT], BF16)
        sgs = Rot([T.sb("sg", [128, TT], F32) for _ in range(2)])
        xts = Rot([T.sb("x1t", [128, D], F32) for _ in range(2)])
        ytmp, Bytmp = T.sb("ytmp", [128, 512], F32)
        pgs = Rot([T.ps("pg", [128, 512], F32) for _ in range(2)])
        pus = Rot([T.ps("pu", [128, 512], F32) for _ in range(2)])
        pys = Rot([T.ps("py", [128, 512], F32) for _ in range(3)])
        h2v = self.H2T.rearrange("(j p) t -> p j t", p=128)

        def load(st):
            h2, Bh2 = h2s.next()
            kb.dma("SP", h2[:], h2v[:, :, st * TT:(st + 1) * TT], writes=[Bh2])
            return h2, Bh2

        nxt = load(0)
        for st in range(S // TT):
            h2, Bh2 = nxt
            if st + 1 < S // TT:
                nxt = load(st + 1)
            for j in range(NJ):
                pg, Bpg = pgs.next()
                pu, Bpu = pus.next()
                for kc in range(8):
                    kb.op("PE", lambda e, kc=kc: e.matmul(pg[:, 0:TT], Wg[:, kc, j * 128:(j + 1) * 128], h2[:, kc, :], start=(kc == 0), stop=(kc == 7)),
                          reads=[BWg, Bh2], writes=[Bpg])
                for kc in range(8):
                    kb.op("PE", lambda e, kc=kc: e.matmul(pu[:, 0:TT], Wu[:, kc, j * 128:(j + 1) * 128], h2[:, kc, :], start=(kc == 0), stop=(kc == 7)),
                          reads=[BWu, Bh2], writes=[Bpu])
                sg, Bsg = sgs.next()
                kb.op("ACT", lambda e: e.activation(sg[:], pg[:, 0:TT], AF.Silu), reads=[Bpg], writes=[Bsg])
                kb.op("DVE", lambda e, j=j: e.tensor_tensor(act[:, j, :], sg[:], pu[:, 0:TT], ALU.mult), reads=[Bsg, Bpu], writes=[Bact])
            for q in range(TT // 128):
                t = st * (TT // 128) + q
                xt, Bxt = xts.next()
                kb.dma("SP", xt[:], self.x1s[t * 128:(t + 1) * 128, :], writes=[Bxt])
                for g in range(2):
                    py, Bpy = pys.next()
                    for j in range(NJ):
                        kb.op("PE", lambda e, j=j: e.matmul(py[:], act[:, j, q * 128:(q + 1) * 128], Wd[:, j, g * 512:(g + 1) * 512],
                                                            start=(j == 0), stop=(j == NJ - 1)),
                              reads=[Bact, BWd], writes=[Bpy])
                    sl = slice(g * 512, (g + 1) * 512)
                    kb.op("DVE", lambda e: e.tensor_tensor(ytmp[:], py[:], self.mod[:, 5, sl], ALU.mult), reads=[Bpy, self.Bmod], writes=[Bytmp])
                    kb.op("POOL", lambda e: e.tensor_tensor(xt[:, sl], ytmp[:], xt[:, sl], ALU.add), reads=[Bytmp, Bxt], writes=[Bxt])
                kb.dma("POOL", x_dst[t * 128:(t + 1) * 128, :], xt[:], reads=[Bxt])
        T.close()

    def build(self):
        kb = self.kb
        self.declare()
        self.setup()
        G = self.G
        self.epsb, self.Bepsb = G.sb("epsb", [128, 1], F32)
        kb.op("POOL", lambda e: e.memset(self.epsb[:], EPS), writes=[self.Bepsb])
        layers = self.cfg.get("layers", [0, 1])
        x_src = self.I["x"]
        for n, li in enumerate(layers):
            x_dst = self.out if n == len(layers) - 1 else self.xmid
            self.L = Scope(kb)
            if li == 0:
                self.gate_sb, self.Bgate = self.L.sb("gates", [128, NT, 24], F32)
            Wa = Scope(kb)
            w_ap_, ncols_ = (self.I["sp_w_in"], SP_IN) if li == 0 else (self.I["diff_w_in"], 3072)
            Wt, BWt = Wa.sb("w_in", [128, 8, ncols_], BF16)
            self.load_w_bf16(Wt, BWt, w_ap_, 8)
            self.w_in_pre = (Wt, BWt)
            self.compute_mod(li)
            stop = self.cfg.get("stop")
            if stop == "mod":
                Wa.close()
                self.L.close()
                break
            if li == 0:
                self.phase_a(0, x_src)
                Wa.close()
                if stop == "A":
                    self.L.close()
                    break
                if not self.cfg.get("skip_moba"):
                    self.moba_part()
                if stop == "moba":
                    self.L.close()
                    break
                self.nsa_part()
                if stop in ("nsa", "cmpmlp"):
                    self.L.close()
                    break
                self.phase_c1(0, x_src, self.I["sp_w_out"])
            else:
                self.phase_a(1, x_src)
                Wa.close()
                if stop == "A":
                    self.L.close()
                    break
                self.phase_b_diff(0)
                if stop == "B":
                    self.L.close()
                    break
                self.phase_c1(1, x_src, self.I["diff_w_out"])
            if stop == "C1":
                self.L.close()
                break
            self.phase_c2(li, x_dst)
            self.L.close()
            x_src = x_dst
        kb.barrier(engines=("POOL",))
        G.es.close()
        kb.es.close()
        return self.nc

    def epi_norm(self, T, O_ap, BO, vcol, rd_ap, Brd):
        kb = self.kb
        kb.op("DVE", lambda e: e.tensor_scalar(rd_ap, O_ap[:, vcol:vcol + 1], 1e-30, None, ALU.max), reads=[BO], writes=[Brd])
        kb.op("DVE", lambda e: e.reciprocal(rd_ap, rd_ap), reads=[Brd], writes=[Brd])

    def phase_b_sparse(self):
        self.moba_part()
        self.nsa_part()

    def moba_part(self):
        kb, I = self.kb, self.I
        T = Scope(kb)
        QAs = Rot([T.sb("QA", [128, 2, S], BF16) for _ in range(2)])
        KAs = Rot([T.sb("KA", [128, 2, S], BF16) for _ in range(2)])
        Vps = Rot([T.sb("Vp", [128, NT, 2, 65], BF16) for _ in range(2)])
        for (ka, Bka) in KAs.items:
            for hh in range(2):
                kb.dma("POOL", ka[64:80, hh, :], I["c_e16"][:, :], writes=[Bka])
        for (v, Bv) in Vps.items:
            kb.op("POOL", lambda e, v=v: e.memset(v[:, :, :, 64:65], 1.0), writes=[Bv])
        t1, Bt1 = T.sb("t1", [128, 16, 16], F32)
        t2, Bt2 = T.sb("t2", [128, 16, 16], F32)
        kb.dma("SP", t1[:].rearrange("p a b -> p (a b)"), bc_row(I["c_t1"][0:1, :], 256), writes=[Bt1])
        kb.dma("SP", t2[:].rearrange("p a b -> p (a b)"), bc_row(I["c_t2"][0:1, :], 256), writes=[Bt2])
        augs = Rot([T.sb("aug", [128, 128], BF16) for _ in range(2)])
        for (a, Ba) in augs.items:
            kb.op("POOL", lambda e, a=a: e.memset(a[:], 0.0), writes=[Ba])
        km, Bkm = T.sb("km", [64, 16], F32)
        kmb, Bkmb = T.sb("kmb", [64, 16], BF16)
        gms = Rot([T.sb("gm", [128, 16], F32) for _ in range(2)])
        m8s = Rot([T.sb("m8", [128, 8], F32) for _ in range(2)])
        sels = Rot([T.sb("sel", [128, 16], F32) for _ in range(2)])
        R = dict(pS=Rot([T.ps("pS", [128, 512], F32) for _ in range(3)]),
                 pT=Rot([T.sb("pT", [128, 512], BF16) for _ in range(8)]))
        Obs = Rot([T.ps("OTm", [128, 512], F32) for _ in range(2)])
        Otk, BOtk = T.ps("Otok", [128, 512], F32)
        osbs = Rot([T.sb("osb", [128, 512], F32) for _ in range(2)])
        pgs = Rot([T.ps("pgate", [128, 512], F32) for _ in range(1)])
        ptr = Rot([T.ps("ptr", [128, 1024], BF16) for _ in range(1)])
        osts = Rot([T.sb("ost", [128, 4, 128], BF16) for _ in range(2)])
        rds = Rot([T.sb("rd", [128, 1], F32) for _ in range(4)])

        def load_pair(hp):
            QA, BQA = QAs.next()
            KA, BKA = KAs.next()
            Vp, BVp = Vps.next()
            for hh in range(2):
                h = 2 * hp + hh
                kb.dma("SP", QA[0:64, hh, :], self.FT[h * 64:(h + 1) * 64, :], writes=[BQA])
                kb.dma("SP", KA[0:64, hh, :], self.FT[512 + h * 64:512 + (h + 1) * 64, :], writes=[BKA])
            for hh in range(2):
                h = 2 * hp + hh
                tmv = self.TM[:, h * 64:(h + 1) * 64].rearrange("(c p) d -> p c d", p=128)
                for c4 in range(4):
                    kb.dma("SP", Vp[:, c4 * 8:(c4 + 1) * 8, hh, 0:64], tmv[:, c4 * 8:(c4 + 1) * 8, :], writes=[BVp])
            return QA, BQA, KA, BKA, Vp, BVp

        kms = [T.sb("km2", [64, 16], F32) for _ in range(2)]
        kmbs = [T.sb("kmb2", [64, 16], BF16) for _ in range(2)]
        pgt, _ = pgs.items[0]
        pg_slots = Rot([(pgt[:, j * 16:(j + 1) * 16], Buf("pgs%d" % j)) for j in range(8)])
        ptt, _ = ptr.items[0]
        pt_slots = Rot([(ptt[:, j * 128:(j + 1) * 128], Buf("pts%d" % j)) for j in range(8)])
        augs8 = Rot([T.sb("aug8", [128, 128], BF16) for _ in range(8)])
        for (a_, Ba_) in augs8.items:
            kb.op("POOL", lambda e, a_=a_: e.memset(a_[:], 0.0), writes=[Ba_])
        gms8 = Rot([T.sb("gm8", [128, 16], F32) for _ in range(8)])
        m8s8 = Rot([T.sb("m88", [128, 8], F32) for _ in range(8)])
        sels8 = Rot([T.sb("sel8", [128, 16], F32) for _ in range(8)])

        def gating_jobs(QA, BQA, KA, BKA):
            def prep():
                for hh in range(2):
                    km_, Bkm_ = kms[hh]
                    kmb_, Bkmb_ = kmbs[hh]
                    kb.op("DVE", lambda e, hh=hh, km_=km_: e.tensor_reduce(km_[:], KA[0:64, hh, :].rearrange("p (b j) -> p b j", j=256), AX.X, ALU.add),
                          reads=[BKA], writes=[Bkm_])
                    kb.op("DVE", lambda e, km_=km_, kmb_=kmb_: e.tensor_scalar(kmb_[:], km_[:], 1.0 / 256, None, ALU.mult), reads=[Bkm_], writes=[Bkmb_])
            jobs = []
            for hh in range(2):
                for t in range(NT):
                    st = {}

                    def part1(hh=hh, t=t, st=st):
                        own = t // 2
                        kmb_, Bkmb_ = kmbs[hh]
                        pg, Bpg = pg_slots.next()
                        kb.op("PE", lambda e: e.matmul(pg, QA[0:64, hh, t * 128:(t + 1) * 128], kmb_[:], start=True, stop=True),
                              reads=[BQA, Bkmb_], writes=[Bpg])
                        gm, Bgm = gms8.next()
                        m8, Bm8 = m8s8.next()
                        sel, Bsel = sels8.next()
                        aug, Baug = augs8.next()
                        kb.op("DVE", lambda e: e.tensor_tensor(gm[:], pg, t1[:, own, :], ALU.add), reads=[Bpg, Bt1], writes=[Bgm])
                        kb.op("DVE", lambda e: e.max(m8[:], gm[:]), reads=[Bgm], writes=[Bm8])
                        kb.op("DVE", lambda e: e.tensor_scalar(m8[:, 2:3], m8[:, 2:3], -1e29, None, ALU.max), reads=[Bm8], writes=[Bm8])
                        kb.op("DVE", lambda e: e.tensor_scalar(sel[:], gm[:], m8[:, 2:3], None, ALU.is_ge), reads=[Bgm, Bm8], writes=[Bsel])
                        kb.op("DVE", lambda e: e.tensor_tensor(sel[:], sel[:], t2[:, own, :], ALU.max), reads=[Bsel, Bt2], writes=[Bsel])
                        kb.op("DVE", lambda e: e.tensor_scalar(aug[:, 64:80], sel[:], -1.0, -MASKV, ALU.add, ALU.mult), reads=[Bsel], writes=[Baug])
                        st["aug"] = (aug, Baug)

                    def part2(hh=hh, t=t, st=st):
                        aug, Baug = st["aug"]
                        pt, Bpt = pt_slots.next()
                        kb.op("PE", lambda e: e.transpose(pt, aug[:], self.ident[:]), reads=[Baug, self.Bident], writes=[Bpt])
                        kb.op("ACT", lambda e: e.copy(QA[64:80, hh, t * 128:(t + 1) * 128], pt[64:80, :]), reads=[Bpt], writes=[BQA])
                    jobs.append((part1, part2))
            return prep, jobs

        nxt = load_pair(0)
        pending = []
        prep0, jobs0 = gating_jobs(nxt[0], nxt[1], nxt[2], nxt[3])
        prep0()
        for j0 in range(0, len(jobs0), 4):
            for p1, _ in jobs0[j0:j0 + 4]:
                p1()
            for _, p2 in jobs0[j0:j0 + 4]:
                p2()
        for hp in range(4):
            QA, BQA, KA, BKA, Vp, BVp = nxt
            njobs = None
            if hp + 1 < 4:
                nxt = load_pair(hp + 1)
                nprep, njobs = gating_jobs(nxt[0], nxt[1], nxt[2], nxt[3])
            for qt in range(8):
                pairs = self.causal_pairs(qt)
                ost, Bost = osts.next()
                for hh in range(2):
                    OTm, BOTm = Obs.next()
                    hooks = None
                    if njobs is not None:
                        ci = qt * 2 + hh
                        mine = njobs[ci * 4:(ci + 1) * 4]

                        def h1(mine=mine, first=(ci == 0)):
                            if first:
                                nprep()
                            for p1, _ in mine:
                                p1()

                        def h2(mine=mine):
                            for _, p2 in mine:
                                p2()
                        hooks = {0: h1, 6: h2}
                    self.attn_qtile(R, QA[0:80, hh, qt * 512:(qt + 1) * 512], BQA,
                                    lambda kc: (KA[0:80, hh, kc * 128:(kc + 1) * 128], 128), BKA,
                                    lambda kc: Vp[:, kc, hh, :], BVp, 65, pairs, OTm, BOTm, None, vstat=True, hooks=hooks)
                    def epilogue(OTm=OTm, BOTm=BOTm, ost=ost, Bost=Bost, hh=hh, qt=qt, hp=hp):
                        osb, Bosb = osbs.next()
                        self.ot_to_tok(OTm, BOTm, 65, osb, Bosb, Otk, BOtk)
                        Om, BOm = Otk, BOtk
                        for s in range(4):
                            rd, Brd = rds.next()
                            Os_ = Om[:, s * 128:s * 128 + 65]
                            self.epi_norm(T, Os_, BOm, 64, rd[:], Brd)
                            kb.op("DVE", lambda e, s=s, Os_=Os_, rd=rd: e.tensor_scalar(ost[:, s, hh * 64:(hh + 1) * 64], Os_[:, 0:64], rd[:, 0:1], None, ALU.mult),
                                  reads=[BOm, Brd], writes=[Bost])
                        if hh == 1:
                            osv = self.OS[qt * 512:(qt + 1) * 512, hp * 128:(hp + 1) * 128].rearrange("(s p) d -> p s d", p=128)
                            kb.dma("POOL", osv, ost[:], reads=[Bost])
                    if pending:
                        pending.pop()()
                    pending.append(epilogue)
        if pending:
            pending.pop()()
        T.close()

    def nsa_part(self):
        kb, I = self.kb, self.I
        for _ in range(self.cfg.get("pad_dve", 0)):
            kb.op("DVE", lambda e: e.memset(self.epsb[:], EPS), writes=[self.Bepsb])
        P = Scope(kb)
        CKc, BCKc = P.sb("CKc", [64, 2, 256], BF16)
        VCs = [P.sb("VC", [128, 2, 129], BF16) for _ in range(2)]
        kb.op("POOL", lambda e: e.memset(CKc[:], 0.0), writes=[BCKc])
        for g in range(2):
            vc, Bvc = VCs[g]
            for c in range(2):
                kb.dma("POOL", vc[:, c, 64:129], I["c_ov"][c * 128:(c + 1) * 128, :], writes=[Bvc])
        T = Scope(kb)
        CX = [T.sb("CX", [128, S], BF16) for _ in range(2)]
        kb.dma("SP", CX[0][0][:], self.FT[1536:1664, :], writes=[CX[0][1]])
        kb.dma("SP", CX[1][0][:], self.FT[1664:1792, :], writes=[CX[1][1]])
        w1, Bw1 = T.sb("w1", [128, 2 * 32 * 256], BF16)
        w1src = I["cmp_w1"].rearrange("d a l e -> d (a l e)")
        for half in range(2):
            for a in range(2):
                kb.dma("POOL", w1[half * 64:(half + 1) * 64, a * 8192:(a + 1) * 8192], w1src[:, a * 8192:(a + 1) * 8192], writes=[Bw1])
        w1v = w1[:].rearrange("p (a l e) -> p a l e", a=2, l=32)
        posb, Bposb = T.sb("posb", [64, 2, 34], BF16)
        kb.op("POOL", lambda e: e.memset(posb[:], 0.0), writes=[Bposb])
        kb.dma("POOL", posb[:, :, 0:32], I["cmp_posT"][:, :, :], writes=[Bposb])
        w2, Bw2 = T.sb("w2", [128, 2, 2, 64], BF16)
        for kv in range(2):
            for eh in range(2):
                kb.dma("POOL", w2[:, kv, eh, :], I["cmp_w2"][kv, eh * 128:(eh + 1) * 128, :], writes=[Bw2])
        b1, Bb1 = T.sb("b1", [128, 4], F32)
        pbs = Rot([T.ps("pb", [128, 512], F32) for _ in range(2)])
        phs = Rot([T.ps("ph", [128, 512], F32) for _ in range(3)])
        for kv in range(2):
            for eh in range(2):
                pb, Bpb = pbs.next()
                for l in range(32):
                    kb.op("PE", lambda e, l=l: e.matmul(pb[:, 0:2], w1v[0:64, kv, l, eh * 128:(eh + 1) * 128],
                                                        posb[0:64, kv, l:l + 2], start=(l == 0), stop=(l == 31)),
                          reads=[Bw1, Bposb], writes=[Bpb])
                kb.op("DVE", lambda e: e.tensor_copy(b1[:, kv * 2 + eh:kv * 2 + eh + 1], pb[:, 0:1]), reads=[Bpb], writes=[Bb1])
        hid = {}
        for kv in range(2):
            cx, Bcx = CX[kv]
            cxv = cx[:].rearrange("p (n j) -> p n j", j=16)
            for g in range(2):
                for eh in range(2):
                    ph, Bph = phs.next()
                    for l in range(32):
                        a = l // 16
                        kb.op("PE", lambda e, l=l, a=a: e.matmul(ph[:, 0:255], w1v[64 * g:64 * g + 64, kv, l, eh * 128:(eh + 1) * 128],
                                                                 cxv[64 * g:64 * g + 64, a:a + 255, l % 16], start=(l == 0), stop=(l == 31)),
                              reads=[Bw1, Bcx], writes=[Bph])
                    ht, Bht = T.sb("hid", [128, 256], BF16)
                    kb.op("POOL", lambda e: e.memset(ht[:], 0.0), writes=[Bht])
                    kb.op("ACT", lambda e: e.activation(ht[:, 0:255], ph[:, 0:255], AF.Silu, bias=b1[:, kv * 2 + eh:kv * 2 + eh + 1]),
                          reads=[Bph, Bb1], writes=[Bht])
                    hid[(kv, g, eh)] = (ht, Bht)
        for g in range(2):
            ph, Bph = phs.next()
            for eh in range(2):
                ht, Bht = hid[(0, g, eh)]
                kb.op("PE", lambda e: e.matmul(ph[0:64, 0:255], w2[:, 0, eh, :], ht[:, 0:255], start=(eh == 0), stop=(eh == 1)),
                      reads=[Bw2, Bht], writes=[Bph])
            kb.op("ACT", lambda e: e.copy(CKc[:, g, 0:255], ph[0:64, 0:255]), reads=[Bph], writes=[BCKc])
            vc, Bvc = VCs[g]
            for c in range(2):
                ph, Bph = phs.next()
                for eh in range(2):
                    ht, Bht = hid[(1, g, eh)]
                    kb.op("PE", lambda e: e.matmul(ph[:, 0:64], ht[:, c * 128:(c + 1) * 128], w2[:, 1, eh, :], start=(eh == 0), stop=(eh == 1)),
                          reads=[Bw2, Bht], writes=[Bph])
                kb.op("ACT", lambda e: e.copy(vc[:, c, 0:64], ph[:, 0:64]), reads=[Bph], writes=[Bvc])
        T.close()
        if self.cfg.get("stop") == "cmpmlp":
            if self.cfg.get("debug"):
                dck = self.tap("d_ckc", [64, 512], BF16)
                kb.dma("SP", dck[:, :], CKc[:].rearrange("p g n -> p (g n)"), reads=[BCKc])
                for g in range(2):
                    dvc = self.tap("d_vc%d" % g, [128, 258], BF16)
                    kb.dma("SP", dvc[:, :], VCs[g][0][:].rearrange("p c n -> p (c n)"), reads=[VCs[g][1]])
            P.close()
            return

        T = Scope(kb)
        nfv, Bnfv = T.sb("nfv", [128, NT, 64], F32)
        cst, Bcst = T.sb("cst", [128, NT, 64], F32)
        v01, Bv01 = T.sb("v01", [128, NT, 64], F32)
        kb.dma("SP", nfv[:], I["c_nfv"][:, :, :], writes=[Bnfv])
        kb.dma("SP", cst[:], I["c_cst"][:, :, :], writes=[Bcst])
        kb.dma("SP", v01[:], I["c_v01"][:, :, :], writes=[Bv01])
        QA4, BQA4 = T.sb("QA4", [128, 4, S], BF16)
        KS, BKS = T.sb("KS", [128, S], BF16)
        KW, BKW = T.sb("KW", [64, S], BF16)
        VS, BVS = T.sb("VS", [128, NT, 65], BF16)
        VW, BVW = T.sb("VW", [128, NT, 65], BF16)
        kb.op("POOL", lambda e: e.memset(VS[:, :, 64:65], 1.0), writes=[BVS])
        kb.op("POOL", lambda e: e.memset(VW[:, :, 64:65], 1.0), writes=[BVW])
        kb.dma("POOL", KS[64:128, :], I["c_e64"][:, :], writes=[BKS])
        augs = Rot([T.sb("aug", [128, 128], BF16) for _ in range(2)])
        for (a, Ba) in augs.items:
            kb.op("POOL", lambda e, a=a: e.memset(a[:], 0.0), writes=[Ba])
        R = dict(pS=Rot([T.ps("pS", [128, 512], F32) for _ in range(2)]),
                 pT=Rot([T.sb("pT", [128, 512], BF16) for _ in range(8)]))
        Oc = [T.ps("Oc", [128, 512], F32) for _ in range(2)]
        BOc = Oc[0][1]
        Osw = Rot([T.ps("OTsw", [128, 512], F32) for _ in range(2)])
        Otk, BOtk = T.ps("Otok", [128, 512], F32)
        osbs = Rot([T.sb("osb", [128, 512], F32) for _ in range(2)])
        ptr = Rot([T.ps("ptr", [128, 1024], BF16) for _ in range(1)])
        occ, Bocc = T.sb("occ", [128, 4, 4, 64], F32)
        imp, Bimp = T.sb("imp", [128, 4, 64], F32)
        itmp, Bitmp = T.sb("itmp", [128, 64], F32)
        sc, Bsc = T.sb("sc", [128, 64], F32)
        sc2, Bsc2 = T.sb("sc2", [128, 64], F32)
        m8a, Bm8a = T.sb("m8a", [128, 8], F32)
        m8b, Bm8b = T.sb("m8b", [128, 8], F32)
        selm, Bselm = T.sb("selm", [128, 64], F32)
        rds = Rot([T.sb("rd", [128, 2], F32) for _ in range(6)])
        osts = Rot([T.sb("ost", [128, 4, 256], BF16) for _ in range(2)])
        gsb = self.gate_sb

        for g in range(2):
            for r in range(4):
                h = 4 * g + r
                kb.dma("SP", QA4[0:64, r, :], self.FT[1024 + h * 64:1024 + (h + 1) * 64, :], writes=[BQA4])
            kb.dma("SP", KS[0:64, :], self.FT[1792 + 64 * g:1792 + 64 * g + 64, :], writes=[BKS])
            kb.dma("SP", KW[:], self.FT[1920 + 64 * g:1920 + 64 * g + 64, :], writes=[BKW])
            tms = self.TM[:, 512 + 64 * g:512 + 64 * g + 64].rearrange("(c p) d -> p c d", p=128)
            tmw = self.TM[:, 640 + 64 * g:640 + 64 * g + 64].rearrange("(c p) d -> p c d", p=128)
            for c4 in range(4):
                kb.dma("SP", VS[:, c4 * 8:(c4 + 1) * 8, 0:64], tms[:, c4 * 8:(c4 + 1) * 8, :], writes=[BVS])
                kb.dma("SP", VW[:, c4 * 8:(c4 + 1) * 8, 0:64], tmw[:, c4 * 8:(c4 + 1) * 8, :], writes=[BVW])
            vc, Bvc = VCs[g]
            parts = self.cfg.get("nsa_parts", ("cmp", "select", "sw"))
            for qt in self.cfg.get("nsa_qts", range(8)):
                ost, Bost = osts.next()
                cchunks = [0] + ([1] if qt >= 4 else [])
                cpairs = [(c, [(s, "full") for s in range(4)]) for c in cchunks]

                def cmp_mask(c, pT, BpT, krows):
                    kb.op("POOL", lambda e: e.affine_select(pT[:], pT[:], [[1, 512]], ALU.is_ge, 0.0,
                                                            base=512 * qt - 2048 * c - 31, channel_multiplier=-16),
                          reads=[BpT], writes=[BpT])

                for r in (range(4) if "cmp" in parts else ()):
                    h = 4 * g + r
                    self.attn_qtile(R, QA4[0:64, r, qt * 512:(qt + 1) * 512], BQA4,
                                    lambda c: (CKc[:, g, c * 128:(c + 1) * 128], 128), BCKc,
                                    lambda c: vc[:, c, :], Bvc, 129, cpairs, None, BOc,
                                    lambda O, s: Oc[s // 2][0][:, (s % 2) * 256:(s % 2) * 256 + 129], post_exp=cmp_mask, bank_of=lambda s: s // 2)
                    for s in range(4):
                        t = 4 * qt + s
                        O_ = Oc[s // 2][0][:, (s % 2) * 256:(s % 2) * 256 + 129]
                        rd, Brd = rds.next()
                        self.epi_norm(T, O_, BOc, 64, rd[:, 0:1], Brd)
                        if r == 0:
                            kb.op("DVE", lambda e, s=s: e.tensor_scalar(imp[:, s, :], O_[:, 65:129], rd[:, 0:1], None, ALU.mult),
                                  reads=[BOc, Brd], writes=[Bimp])
                        else:
                            kb.op("DVE", lambda e: e.tensor_scalar(itmp[:], O_[:, 65:129], rd[:, 0:1], None, ALU.mult),
                                  reads=[BOc, Brd], writes=[Bitmp])
                            kb.op("DVE", lambda e, s=s: e.tensor_tensor(imp[:, s, :], imp[:, s, :], itmp[:], ALU.add),
                                  reads=[Bitmp, Bimp], writes=[Bimp])
                        kb.op("DVE", lambda e: e.tensor_tensor(rd[:, 1:2], rd[:, 0:1], gsb[:, t, h:h + 1], ALU.mult), reads=[Brd, self.Bgate], writes=[Brd])
                        kb.op("DVE", lambda e, s=s, r=r: e.tensor_scalar(occ[:, r, s, :], O_[:, 0:64], rd[:, 1:2], None, ALU.mult),
                              reads=[BOc, Brd], writes=[Bocc])
                for s in (range(4) if "select" in parts else ()):
                    t = 4 * qt + s
                    aug, Baug = augs.next()
                    kb.op("DVE", lambda e: e.tensor_tensor(sc[:], imp[:, s, :], nfv[:, t, :], ALU.mult), reads=[Bimp, Bnfv], writes=[Bsc])
                    kb.op("DVE", lambda e: e.tensor_tensor(sc[:], sc[:], cst[:, t, :], ALU.add), reads=[Bsc, Bcst], writes=[Bsc])
                    kb.op("DVE", lambda e: e.max(m8a[:], sc[:]), reads=[Bsc], writes=[Bm8a])
                    kb.op("DVE", lambda e: e.match_replace(sc2[:], m8a[:], sc[:], -1e30), reads=[Bsc, Bm8a], writes=[Bsc2])
                    kb.op("DVE", lambda e: e.max(m8b[:], sc2[:]), reads=[Bsc2], writes=[Bm8b])
                    kb.op("DVE", lambda e: e.tensor_scalar(selm[:], sc[:], m8b[:, 7:8], None, ALU.is_ge), reads=[Bsc, Bm8b], writes=[Bselm])
                    kb.op("DVE", lambda e: e.tensor_tensor(selm[:], selm[:], v01[:, t, :], ALU.mult), reads=[Bselm, Bv01], writes=[Bselm])
                    kb.op("DVE", lambda e: e.tensor_scalar(aug[:, 64:128], selm[:], -1.0, -MASKV, ALU.add, ALU.mult), reads=[Bselm], writes=[Baug])
                    pt, Bpt = ptr.next()
                    kb.op("PE", lambda e: e.transpose(pt[:, 0:128], aug[:], self.ident[:]), reads=[Baug, self.Bident], writes=[Bpt])
                    for r in range(4):
                        kb.op("ACT", lambda e, r=r: e.copy(QA4[64:128, r, t * 128:(t + 1) * 128], pt[64:128, 0:128]), reads=[Bpt], writes=[BQA4])
                pending = []
                spairs = self.causal_pairs(qt)
                wpairs = []
                for kc in range(max(0, 4 * qt - 4), 4 * qt + 4):
                    subs = []
                    for s in range(4):
                        dlt = 4 * qt + s - kc
                        if dlt == 0:
                            subs.append((s, "tri"))
                        elif 1 <= dlt <= 3:
                            subs.append((s, "full"))
                        elif dlt == 4:
                            subs.append((s, "atri"))
                    if subs:
                        wpairs.append((kc, subs))
                for r in (range(4) if "sw" in parts else ()):
                    h = 4 * g + r
                    for br, (pairs, qrows, kl, Bkl, vt, Bvt) in enumerate((
                            (spairs, 128, KS, BKS, VS, BVS), (wpairs, 64, KW, BKW, VW, BVW))):
                        if br not in self.cfg.get("nsa_br", (0, 1)):
                            continue
                        OTb, BOTb = Osw.next()
                        self.attn_qtile(R, QA4[0:qrows, r, qt * 512:(qt + 1) * 512], BQA4,
                                        lambda kc, kl=kl, qrows=qrows: (kl[0:qrows, kc * 128:(kc + 1) * 128], 128), Bkl,
                                        lambda kc, vt=vt: vt[:, kc, :], Bvt, 65, pairs, OTb, BOTb, None, vstat=True)
                        def epilogue(OTb=OTb, BOTb=BOTb, br=br, r=r, h=h, qt=qt, ost=ost, Bost=Bost):
                            osb, Bosb = osbs.next()
                            self.ot_to_tok(OTb, BOTb, 65, osb, Bosb, Otk, BOtk)
                            Ob, BOb = Otk, BOtk
                            for s in range(4):
                                t = 4 * qt + s
                                O_ = Ob[:, s * 128:s * 128 + 65]
                                rd, Brd = rds.next()
                                self.epi_norm(T, O_, BOb, 64, rd[:, 0:1], Brd)
                                gcol = (br + 1) * 8 + h
                                kb.op("DVE", lambda e, rd=rd, t=t, gcol=gcol: e.tensor_tensor(rd[:, 1:2], rd[:, 0:1], gsb[:, t, gcol:gcol + 1], ALU.mult),
                                      reads=[Brd, self.Bgate], writes=[Brd])
                                if br == 0:
                                    kb.op("DVE", lambda e, O_=O_, rd=rd: e.tensor_scalar(itmp[:], O_[:, 0:64], rd[:, 1:2], None, ALU.mult),
                                          reads=[BOb, Brd], writes=[Bitmp])
                                    kb.op("DVE", lambda e, s=s: e.tensor_tensor(occ[:, r, s, :], occ[:, r, s, :], itmp[:], ALU.add),
                                          reads=[Bitmp, Bocc], writes=[Bocc])
                                else:
                                    kb.op("DVE", lambda e, s=s, O_=O_, rd=rd: e.scalar_tensor_tensor(ost[:, s, r * 64:(r + 1) * 64], O_[:, 0:64], rd[:, 1:2], occ[:, r, s, :], ALU.mult, ALU.add),
                                          reads=[BOb, Brd, Bocc], writes=[Bost])
                        if pending:
                            pending.pop()()
                        pending.append(epilogue)
                if pending:
                    pending.pop()()
                osv = self.OS[qt * 512:(qt + 1) * 512, 512 + g * 256:512 + (g + 1) * 256].rearrange("(s p) d -> p s d", p=128)
                kb.dma("POOL", osv, ost[:], reads=[Bost])
                if self.cfg.get("nsa_barrier"):
                    kb.barrier()
        T.close()
        P.close()


def _consts():
    c = {}
    c["c_ident"] = np.eye(128, dtype=np.float32)
    k = np.arange(128)[:, None]
    q = np.arange(128)[None, :]
    c["c_tri"] = (q >= k).astype(np.float32)
    c["c_atri"] = (k > q).astype(np.float32)
    key = np.arange(S)[None, :]
    c["c_e16"] = (key // 256 == np.arange(16)[:, None]).astype(np.float32)
    c["c_e64"] = (key // 64 == np.arange(64)[:, None]).astype(np.float32)
    ncmp = 255
    cs = np.arange(ncmp) * 16
    ss = np.arange(64) * 64
    ov = np.minimum(cs[:, None] + 32, ss[None, :] + 64) - np.maximum(cs[:, None], ss[None, :])
    ovp = np.zeros((256, 65), np.float32)
    ovp[:255, 0] = 1.0
    ovp[:255, 1:] = np.clip(ov, 0, None) / 32.0
    c["c_ov"] = ovp
    own = np.arange(16)[:, None]
    blk = np.arange(16)[None, :]
    c["c_t1"] = np.where(blk < own, 0.0, -1e30).astype(np.float32).reshape(1, 256)
    c["c_t2"] = (blk == own).astype(np.float32).reshape(1, 256)
    p = np.arange(128)[:, None, None]
    t = np.arange(NT)[None, :, None]
    m = np.arange(64)[None, None, :]
    cur = (t * 128 + p) // 64
    ok = m <= cur
    forced = ok & ((m == 0) | (m >= cur - 1))
    c["c_nfv"] = (ok & ~forced).astype(np.float32)
    c["c_cst"] = np.where(ok, np.where(forced, 1e4 + m, 0.0), -1e30).astype(np.float32)
    c["c_v01"] = ok.astype(np.float32)
    return c


def _core_inputs(b, inp, consts):
    f = lambda a: np.ascontiguousarray(np.asarray(a), dtype=np.float32)
    m = {}
    m["x"] = f(inp["x"][b])
    m["cT"] = f(np.asarray(inp["c"][b]).reshape(8, 128).T)
    m["posT"] = np.ascontiguousarray(np.asarray(inp["positions"][b]).reshape(NT, 128).T.astype(np.int32))
    for k in ("ada_w", "ada_b", "attn_norm", "ffn_norm", "ffn_w_gate", "ffn_w_up", "ffn_w_down"):
        m[k] = f(inp[k])
    m["sp_w_in"] = f(inp["sp_w_in"][0]); m["sp_w_out"] = f(inp["sp_w_out"][0])
    m["moba_q_norm"] = f(inp["moba_q_norm"]); m["moba_k_norm"] = f(inp["moba_k_norm"])
    m["nsa_q_norm"] = f(inp["nsa_q_norm"]); m["nsa_k_norm"] = f(inp["nsa_k_norm"][0])
    m["cmp_posT"] = f(np.asarray(inp["nsa_cmp_pos"][0]).transpose(2, 0, 1))
    m["cmp_w1"] = f(np.asarray(inp["nsa_cmp_w1"][0]).transpose(2, 0, 1, 3))
    m["cmp_w2"] = f(inp["nsa_cmp_w2"][0])
    m["diff_w_in"] = f(inp["diff_w_in"][0]); m["diff_w_out"] = f(inp["diff_w_out"][0])
    m["diff_q_norm"] = f(inp["diff_q_norm"]); m["diff_k_norm"] = f(inp["diff_k_norm"])
    m["diff_lambda"] = f(np.asarray(inp["diff_lambda"][0]).reshape(1, 256))
    m["diff_out_norm"] = f(inp["diff_out_norm"])
    m.update(consts)
    return m


_NC_CACHE = {}


def kernel(**inputs):
    consts = _consts()
    if "nc" not in _NC_CACHE:
        _NC_CACHE["nc"] = Prog({}).build()
    nc = _NC_CACHE["nc"]
    in_maps = [_core_inputs(b, inputs, consts) for b in range(8)]
    res = run_bass_kernel_spmd(nc, in_maps, core_ids=list(range(8)))
    return np.stack([np.asarray(r["out"], dtype=np.float32) for r in res.results], axis=0)
```

```python
from contextlib import ExitStack
import numpy as np
import concourse.bass as bass
import concourse.mybir as mybir
from concourse.bass_utils import run_bass_kernel_spmd

F32 = mybir.dt.float32
BF16 = mybir.dt.bfloat16
I32 = mybir.dt.int32
AF = mybir.ActivationFunctionType
ALU = mybir.AluOpType
AX = mybir.AxisListType

S = 4096
D = 1024
NT = S // 128
FFN = 2816
NJ = FFN // 128
EPS = 1e-6
MASKV = -30000.0
SP_IN = 2840
N_DMA_SEMS = 24


class Buf:
    __slots__ = ("w", "r", "name")

    def __init__(self, name=""):
        self.w = None
        self.r = {}
        self.name = name


class Rot:
    def __init__(self, items):
        self.items = list(items)
        self.i = 0

    def next(self):
        it = self.items[self.i]
        self.i = (self.i + 1) % len(self.items)
        return it


class KB:
    def __init__(self, nc):
        self.nc = nc
        self.es = ExitStack()
        self.engs = {"PE": nc.tensor, "ACT": nc.scalar, "DVE": nc.vector, "POOL": nc.gpsimd, "SP": nc.sync}
        self.sem = {}
        self.cnt = {}
        for e in ("PE", "ACT", "DVE", "POOL"):
            self.sem[e] = self.es.enter_context(nc.semaphore("s_" + e))
            self.cnt[e] = 0
        self.dsem = [self.es.enter_context(nc.semaphore("d_%d" % i)) for i in range(N_DMA_SEMS)]
        self.dcnt = [0] * N_DMA_SEMS
        self.dnext = 0
        self.dnext2 = [0, 0]
        self.waited = {e: {} for e in self.engs}
        self.n_inst = 0
        self.uid = 0
        self.limit = None
        self.log = None

    def name(self, p):
        self.uid += 1
        return "%s_%d" % (p, self.uid)

    def _wait(self, E, key, c, raw=False):
        if key == E and E == "PE":
            return
        w = self.waited[E]
        if w.get(key, 0) >= c:
            return
        w[key] = c
        if isinstance(key, int):
            self.engs[E].wait_ge(self.dsem[key], c)
        else:
            self.engs[E].wait_ge(self.sem[key], c)

    def _deps(self, E, reads, writes):
        for b in reads:
            if b.w is not None:
                self._wait(E, b.w[0], b.w[1], raw=True)
        for b in writes:
            if b.w is not None:
                self._wait(E, b.w[0], b.w[1])
            for k, c in b.r.items():
                self._wait(E, k, c)

    def _mark(self, tok, reads, writes):
        for b in reads:
            if b.r.get(tok[0], 0) < tok[1]:
                b.r[tok[0]] = tok[1]
        for b in writes:
            b.w = tok
            b.r = {}

    def op(self, E, fn, reads=(), writes=()):
        if self.limit is not None and self.n_inst >= self.limit:
            return None
        self._deps(E, reads, writes)
        if self.log is not None:
            import sys as _sys
            self.log.append((self.n_inst, E, _sys._getframe(1).f_lineno))
        ins = fn(self.engs[E])
        self.cnt[E] += 1
        ins.then_inc(self.sem[E], 1)
        tok = (E, self.cnt[E])
        self._mark(tok, reads, writes)
        self.n_inst += 1
        return tok

    def dma(self, Q, out_ap, in_ap, reads=(), writes=(), **kw):
        if self.limit is not None and self.n_inst >= self.limit:
            return None
        self._deps(Q, reads, writes)
        half = N_DMA_SEMS // 2
        qi = 0 if Q == "SP" else 1
        k = qi * half + self.dnext2[qi]
        self.dnext2[qi] = (self.dnext2[qi] + 1) % half
        if self.dcnt[k] > 0:
            self._wait(Q, k, self.dcnt[k])
        if self.log is not None:
            import sys as _sys
            self.log.append((self.n_inst, "DMA-" + Q, _sys._getframe(1).f_lineno))
        ins = self.engs[Q].dma_start(out=out_ap, in_=in_ap, **kw)
        self.dcnt[k] += 16
        ins.then_inc(self.dsem[k], 16)
        tok = (k, self.dcnt[k])
        self._mark(tok, reads, writes)
        self.n_inst += 1
        return tok

    def barrier(self, engines=("PE", "ACT", "DVE", "POOL", "SP")):
        for E in engines:
            for e2 in ("PE", "ACT", "DVE", "POOL"):
                if self.cnt[e2] > 0:
                    self._wait(E, e2, self.cnt[e2])
            for k in range(N_DMA_SEMS):
                if self.dcnt[k] > 0:
                    self._wait(E, k, self.dcnt[k])


class Scope:
    def __init__(self, kb):
        self.kb = kb
        self.es = ExitStack()

    def sb(self, name, shape, dt):
        t = self.es.enter_context(self.kb.nc.sbuf_tensor(self.kb.name(name), list(shape), dt))
        return t, Buf(name)

    def ps(self, name, shape, dt):
        t = self.es.enter_context(self.kb.nc.psum_tensor(self.kb.name(name), list(shape), dt))
        return t, Buf(name)

    def close(self):
        self.kb.barrier()
        self.es.close()


def bc_row(ap_row, n):
    return ap_row.to_broadcast([128, n])


class Prog:
    def __init__(self, cfg):
        self.cfg = cfg
        self.nc = bass.Bass("TRN2", target_bir_lowering=False)
        self.kb = KB(self.nc)
        self.kb.limit = cfg.get("limit")
        self.I = {}
        self.taps = {}

    def din(self, name, shape, dt=F32):
        self.I[name] = self.nc.dram_tensor(name, list(shape), dt, kind="ExternalInput").ap()
        return self.I[name]

    def dscr(self, name, shape, dt):
        if self.cfg.get("debug"):
            return self.nc.dram_tensor(name, list(shape), dt, kind="ExternalOutput").ap()
        return self.nc.dram_tensor(name, list(shape), dt).ap()

    def tap(self, name, shape, dt=F32):
        self.taps[name] = self.nc.dram_tensor(name, list(shape), dt, kind="ExternalOutput").ap()
        return self.taps[name]

    def declare(self):
        d = self.din
        d("x", [S, D]); d("cT", [128, 8]); d("posT", [128, NT], I32)
        d("ada_w", [2, D, 6 * D]); d("ada_b", [2, 6 * D])
        d("attn_norm", [2, D]); d("ffn_norm", [2, D])
        d("ffn_w_gate", [2, D, FFN]); d("ffn_w_up", [2, D, FFN]); d("ffn_w_down", [2, FFN, D])
        d("sp_w_in", [D, SP_IN]); d("sp_w_out", [D, D])
        d("moba_q_norm", [1, 64]); d("moba_k_norm", [1, 64]); d("nsa_q_norm", [1, 64]); d("nsa_k_norm", [3, 64])
        d("cmp_posT", [64, 2, 32]); d("cmp_w1", [64, 2, 32, 256]); d("cmp_w2", [2, 256, 64])
        d("diff_w_in", [D, 3072]); d("diff_w_out", [D, D])
        d("diff_q_norm", [1, 64]); d("diff_k_norm", [1, 64]); d("diff_lambda", [1, 256]); d("diff_out_norm", [1, 128])
        d("c_ident", [128, 128]); d("c_tri", [128, 128]); d("c_atri", [128, 128])
        d("c_e16", [16, S]); d("c_e64", [64, S]); d("c_ov", [256, 65])
        d("c_t1", [1, 256]); d("c_t2", [1, 256])
        d("c_nfv", [128, NT, 64]); d("c_cst", [128, NT, 64]); d("c_v01", [128, NT, 64])
        self.out = self.nc.dram_tensor("out", [S, D], F32, kind="ExternalOutput").ap()
        self.xmid = self.dscr("xmid", [S, D], F32)
        self.x1s = self.dscr("x1s", [S, D], F32)
        self.FT = self.dscr("FT", [2048, S], BF16)
        self.TM = self.dscr("TM", [S, D], BF16)
        self.OS = self.dscr("OS", [S, D], BF16)
        self.H2T = self.dscr("H2T", [D, S], BF16)

    def setup(self):
        kb, I = self.kb, self.I
        self.G = Scope(kb)
        G = self.G
        self.ident, self.Bident = G.sb("ident", [128, 128], BF16)
        kb.dma("POOL", self.ident[:], I["c_ident"][:, :], writes=[self.Bident])
        self.identF, self.BidentF = G.sb("identF", [128, 128], F32)
        kb.dma("SP", self.identF[:], I["c_ident"][:, :], writes=[self.BidentF])
        self.tri, self.Btri = G.sb("tri", [128, 128], BF16)
        kb.dma("POOL", self.tri[:], I["c_tri"][:, :], writes=[self.Btri])
        self.atri, self.Batri = G.sb("atri", [128, 128], BF16)
        kb.dma("POOL", self.atri[:], I["c_atri"][:, :], writes=[self.Batri])
        self.cosT, self.Bcos = G.sb("cosT", [128, NT, 8], F32)
        self.sinT, self.Bsin = G.sb("sinT", [128, NT, 8], F32)
        self.mod, self.Bmod = G.sb("mod", [128, 6, D], F32)
        self.silc, self.Bsilc = G.sb("silc", [128, 8, 128], F32)
        with_scope = Scope(kb)
        T = with_scope
        pi, Bpi = T.sb("pi", [128, NT], I32)
        pf, Bpf = T.sb("pf", [128, NT], F32)
        ang, Bang = T.sb("ang", [128, NT, 8], F32)
        tmp, Btmp = T.sb("tmp", [128, NT, 8], F32)
        tmp2, Btmp2 = T.sb("tmp2", [128, NT, 8], F32)
        kb.dma("SP", pi[:], I["posT"][:, :], writes=[Bpi])
        kb.op("DVE", lambda e: e.tensor_copy(pf[:], pi[:]), reads=[Bpi], writes=[Bpf])
        inv = (1.0 / (np.float32(500000.0) ** (np.arange(0, 16, 2, dtype=np.float32) / np.float32(16)))).astype(np.float32)
        for j in range(8):
            kb.op("DVE", lambda e, j=j: e.tensor_scalar(ang[:, :, j], pf[:], float(inv[j]), None, ALU.mult),
                  reads=[Bpf], writes=[Bang])
        TWO_PI = float(2 * np.pi)
        MAG = 12582912.0

        def sin_of(dst, Bdst, shift):
            if shift != 0.0:
                kb.op("DVE", lambda e: e.tensor_scalar(tmp2[:], ang[:], shift, None, ALU.add), reads=[Bang], writes=[Btmp2])
                src, Bsrc = tmp2, Btmp2
            else:
                src, Bsrc = ang, Bang
            kb.op("DVE", lambda e: e.tensor_scalar(tmp[:], src[:], 1.0 / TWO_PI, MAG, ALU.mult, ALU.add), reads=[Bsrc], writes=[Btmp])
            kb.op("DVE", lambda e: e.tensor_scalar(tmp[:], tmp[:], -MAG, -TWO_PI, ALU.add, ALU.mult), reads=[Btmp], writes=[Btmp])
            kb.op("DVE", lambda e: e.tensor_tensor(tmp[:], src[:], tmp[:], ALU.add), reads=[Bsrc, Btmp], writes=[Btmp])
            kb.op("DVE", lambda e: e.tensor_scalar(tmp[:], tmp[:], float(np.pi), float(-np.pi), ALU.min, ALU.max), reads=[Btmp], writes=[Btmp])
            kb.op("ACT", lambda e: e.activation(dst[:], tmp[:], AF.Sin), reads=[Btmp], writes=[Bdst])

        sin_of(self.sinT, self.Bsin, 0.0)
        sin_of(self.cosT, self.Bcos, float(np.pi / 2))
        ct, Bct = T.sb("ct", [128, 8], F32)
        kb.dma("SP", ct[:], I["cT"][:, :], writes=[Bct])
        kb.op("ACT", lambda e: e.activation(ct[:], ct[:], AF.Silu), reads=[Bct], writes=[Bct])
        kb.op("DVE", lambda e: e.tensor_copy(self.silc[:], ct[:].unsqueeze(2).to_broadcast([128, 8, 128])),
              reads=[Bct], writes=[self.Bsilc])
        T.close()

    def compute_mod(self, li):
        kb, I = self.kb, self.I
        T = Scope(kb)
        wch = [T.sb("adaw", [128, 8, 512], F32) for _ in range(2)]
        bch = [T.sb("adab", [128, 512], F32) for _ in range(2)]
        pm = [T.ps("pmod", [128, 512], F32) for _ in range(2)]
        nrm, Bnrm = T.sb("nrm", [128, 2, D], F32)
        kb.dma("SP", nrm[:, 0, :], bc_row(I["attn_norm"][li:li + 1, :], D), writes=[Bnrm])
        kb.dma("SP", nrm[:, 1, :], bc_row(I["ffn_norm"][li:li + 1, :], D), writes=[Bnrm])
        wv = I["ada_w"][li].rearrange("(kc p) n -> p kc n", p=128)
        for g in range(12):
            w, Bw = wch[g % 2]
            b, Bb = bch[g % 2]
            p, Bp = pm[g % 2]
            kb.dma("SP", w[:], wv[:, :, g * 512:(g + 1) * 512], writes=[Bw])
            kb.dma("SP", b[:], bc_row(I["ada_b"][li:li + 1, g * 512:(g + 1) * 512], 512), writes=[Bb])
            for kc in range(8):
                kb.op("PE", lambda e, kc=kc: e.matmul(p[:], self.silc[:, kc, :], w[:, kc, :], start=(kc == 0), stop=(kc == 7)),
                      reads=[Bw, self.Bsilc], writes=[Bp])
            sl = self.mod[:, g // 2, (g % 2) * 512:(g % 2 + 1) * 512]
            kb.op("DVE", lambda e: e.tensor_tensor(sl, p[:], b[:], ALU.add), reads=[Bp, Bb], writes=[self.Bmod])
        kb.op("DVE", lambda e: e.scalar_tensor_tensor(self.mod[:, 1, :], self.mod[:, 1, :], 1.0, nrm[:, 0, :], ALU.add, ALU.mult),
              reads=[self.Bmod, Bnrm], writes=[self.Bmod])
        kb.op("DVE", lambda e: e.scalar_tensor_tensor(self.mod[:, 4, :], self.mod[:, 4, :], 1.0, nrm[:, 1, :], ALU.add, ALU.mult),
              reads=[self.Bmod, Bnrm], writes=[self.Bmod])
        T.close()

    def rms_mod_tile(self, T, xt, Bxt, hb, Bhb, gi, si, scr):
        kb = self.kb
        junk, Bjunk, ss, Bss, hn, Bhn = scr
        kb.op("ACT", lambda e: e.activation(junk[:], xt[:], AF.Square, accum_out=ss[:]), reads=[Bxt], writes=[Bjunk, Bss])
        kb.op("ACT", lambda e: e.activation(ss[:], ss[:], AF.Sqrt, bias=self.epsb[:], scale=1.0 / D), reads=[Bss, self.Bepsb], writes=[Bss])
        kb.op("DVE", lambda e: e.reciprocal(ss[:], ss[:]), reads=[Bss], writes=[Bss])
        kb.op("DVE", lambda e: e.scalar_tensor_tensor(hn[:], xt[:], ss[:, 0:1], self.mod[:, gi, :], ALU.mult, ALU.mult),
              reads=[Bxt, Bss, self.Bmod], writes=[Bhn])
        kb.op("POOL", lambda e: e.tensor_tensor(hb[:], hn[:], self.mod[:, si, :], ALU.add), reads=[Bhn, self.Bmod], writes=[Bhb])

    def transpose8(self, src, Bsrc, pt, Bpt, dst_ap, Bdst, eng="ACT"):
        kb = self.kb
        for j in range(8):
            kb.op("PE", lambda e, j=j: e.transpose(pt[:, j * 128:(j + 1) * 128], src[:, j * 128:(j + 1) * 128], self.ident[:]),
                  reads=[Bsrc, self.Bident], writes=[Bpt])
        if eng == "ACT":
            kb.op("ACT", lambda e: e.copy(dst_ap, pt[:].rearrange("p (j t) -> p j t", j=8)), reads=[Bpt], writes=[Bdst])
        else:
            kb.op("DVE", lambda e: e.tensor_copy(dst_ap, pt[:].rearrange("p (j t) -> p j t", j=8)), reads=[Bpt], writes=[Bdst])

    def load_w_bf16(self, dst, Bdst, w_ap, nk):
        for kc in range(nk):
            self.kb.dma("POOL", dst[:, kc, :], w_ap[kc * 128:(kc + 1) * 128, :], writes=[Bdst])

    def phase_a(self, li, x_src):
        kb, I = self.kb, self.I
        T = Scope(kb)
        if li == 1:
            w_ap, ncols = I["diff_w_in"], 3072
            groups = []
            for g in range(6):
                if g < 4:
                    gi = 0 if g < 2 else 1
                    blocks = [("T", g * 512 + b * 128) for b in range(4)]
                    groups.append(dict(c0=g * 512, w=512, qk=[(0, 8)], gain=gi, blocks=blocks, gate=None))
                else:
                    blocks = [("V", (g - 4) * 512 + b * 128) for b in range(4)]
                    groups.append(dict(c0=g * 512, w=512, qk=[], gain=None, blocks=blocks, gate=None))
            gain_srcs = [[(I["diff_q_norm"][0:1, :], 0, 8)], [(I["diff_k_norm"][0:1, :], 0, 8)]]
        else:
            w_ap, ncols = I["sp_w_in"], SP_IN
            kn = I["nsa_k_norm"]
            groups = [
                dict(c0=0, w=512, qk=[(0, 8)], gain=0, blocks=[("T", b * 128) for b in range(4)], gate=None),
                dict(c0=512, w=512, qk=[(0, 8)], gain=1, blocks=[("T", 512 + b * 128) for b in range(4)], gate=None),
                dict(c0=1024, w=512, qk=[], gain=None, blocks=[("V", b * 128) for b in range(4)], gate=None),
                dict(c0=1536, w=512, qk=[(0, 8)], gain=2, blocks=[("T", 1024 + b * 128) for b in range(4)], gate=None),
                dict(c0=2048, w=512, qk=[(0, 2), (4, 6)], gain=3,
                     blocks=[("T", 1536), ("T", 1664), ("T", 1792), ("V", 512)], gate=None),
                dict(c0=2560, w=280, qk=[(0, 2)], gain=4, blocks=[("T", 1920), ("V", 640)], gate=(256, 24)),
            ]
            gain_srcs = [
                [(I["moba_q_norm"][0:1, :], 0, 8)], [(I["moba_k_norm"][0:1, :], 0, 8)], [(I["nsa_q_norm"][0:1, :], 0, 8)],
                [(kn[0:1, :], 0, 2), (kn[1:2, :], 4, 6)], [(kn[2:3, :], 0, 2)],
            ]
        nk = 8
        W, BW = self.w_in_pre
        gains = []
        for gs in gain_srcs:
            gr, Bgr = T.sb("gainrow", [128, 8, 64], F32)
            kb.op("POOL", lambda e: e.memset(gr[:], 1.0), writes=[Bgr])
            for (src, u0, u1) in gs:
                for u in range(u0, u1):
                    kb.dma("SP", gr[:, u, :], bc_row(src, 64), writes=[Bgr])
            gains.append((gr, Bgr))
        NG = len(groups)
        xts = Rot([T.sb("xt", [128, D], F32) for _ in range(3)])
        hbs = Rot([T.sb("hb", [128, D], BF16) for _ in range(2)])
        hTs = Rot([T.sb("hT", [128, 8, 128], BF16) for _ in range(2)])
        junk, Bjunk = T.sb("junk", [128, D], BF16)
        sss = Rot([T.sb("ss", [128, 1], F32) for _ in range(2)])
        hn, Bhn = T.sb("hn", [128, D], F32)
        sqs = [T.sb("sq", [128, 512], F32) for _ in range(NG)]
        ss8l = [T.sb("ss8", [128, 8], F32) for _ in range(NG)]
        qnl = [T.sb("qn", [128, 8, 64], F32) for _ in range(NG)]
        postl = [T.sb("post", [128, 512], BF16) for _ in range(NG)]
        rtl = [T.sb("rt", [128, 4, 8, 8], F32) for _ in range(NG)]
        pts = Rot([T.ps("ptA", [128, 1024], BF16) for _ in range(1)])
        pgl = [T.ps("pgA", [128, 512], F32) for _ in range(NG)]
        pbt, _ = T.ps("ptB", [128, 1024], BF16)
        pbh = Rot([(pbt[:, 0:512], Buf("pb0")), (pbt[:, 512:1024], Buf("pb1"))])
        stages = {}
        for gi_, g in enumerate(groups):
            if any(k == "T" for k, _ in g["blocks"]):
                stages[gi_] = [T.sb("stage", [128, 4, 512], BF16) for _ in range(2)]

        def load_x(t):
            xt, Bxt = xts.next()
            kb.dma("SP", xt[:], x_src[t * 128:(t + 1) * 128, :], writes=[Bxt])
            return xt, Bxt

        def prep(t, xtb):
            xt, Bxt = xtb
            hb, Bhb = hbs.next()
            ss, Bss = sss.next()
            self.rms_mod_tile(T, xt, Bxt, hb, Bhb, 1, 0, (junk, Bjunk, ss, Bss, hn, Bhn))
            hT, BhT = hTs.next()
            pt, Bpt = pts.next()
            self.transpose8(hb, Bhb, pt, Bpt, hT[:], BhT, eng="ACT")
            return hT, BhT

        xq = [load_x(0)]
        if NT > 1:
            xq.append(load_x(1))
        hq = [prep(0, xq.pop(0))]
        for t in range(NT):
            hT, BhT = hq.pop(0)
            if t + 2 < NT:
                xq.append(load_x(t + 2))
            tt = t % 4
            half = (t // 4) % 2
            for gi_, g in enumerate(groups):
                w = g["w"]
                pg, Bpg = pgl[gi_]
                for kc in range(8):
                    kb.op("PE", lambda e, kc=kc, pg=pg, w=w, g=g: e.matmul(pg[:, 0:w], hT[:, kc, :], W[:, kc, g["c0"]:g["c0"] + w],
                                                                          start=(kc == 0), stop=(kc == 7)),
                          reads=[BhT, BW], writes=[Bpg])
            if t + 1 < NT:
                hq.append(prep(t + 1, xq.pop(0)))
            info = []
            for gi_, g in enumerate(groups):
                w = g["w"]
                wq = (w // 64) * 64
                info.append((gi_, g, w, wq, wq // 64))
            for gi_, g, w, wq, nu in info:
                pg, Bpg = pgl[gi_]
                post, Bpost = postl[gi_]
                sq, Bsq = sqs[gi_]
                if g["qk"]:
                    kb.op("ACT", lambda e, sq=sq, pg=pg, wq=wq: e.activation(sq[:, 0:wq], pg[:, 0:wq], AF.Square), reads=[Bpg], writes=[Bsq])
                else:
                    kb.op("ACT", lambda e, post=post, pg=pg, wq=wq: e.copy(post[:, 0:wq], pg[:, 0:wq]), reads=[Bpg], writes=[Bpost])
                if g["gate"] is not None:
                    gc0, gw = g["gate"]
                    kb.op("ACT", lambda e, pg=pg, gc0=gc0, gw=gw: e.activation(self.gate_sb[:, t, :], pg[:, gc0:gc0 + gw], AF.Sigmoid),
                          reads=[Bpg], writes=[self.Bgate])
            for gi_, g, w, wq, nu in info:
                if not g["qk"]:
                    continue
                sq, Bsq = sqs[gi_]
                ss8, Bss8 = ss8l[gi_]
                kb.op("DVE", lambda e, ss8=ss8, sq=sq, wq=wq, nu=nu: e.tensor_reduce(ss8[:, 0:nu], sq[:, 0:wq].rearrange("p (u d) -> p u d", d=64), AX.X, ALU.add),
                      reads=[Bsq], writes=[Bss8])
            for gi_, g, w, wq, nu in info:
                if not g["qk"]:
                    continue
                ss8, Bss8 = ss8l[gi_]
                kb.op("ACT", lambda e, ss8=ss8, nu=nu: e.activation(ss8[:, 0:nu], ss8[:, 0:nu], AF.Sqrt, bias=self.epsb[:], scale=1.0 / 64),
                      reads=[Bss8, self.Bepsb], writes=[Bss8])
            for gi_, g, w, wq, nu in info:
                if not g["qk"]:
                    continue
                pg, Bpg = pgl[gi_]
                ss8, Bss8 = ss8l[gi_]
                qn, Bqn = qnl[gi_]
                gr, Bgr = gains[g["gain"]]
                kb.op("DVE", lambda e, ss8=ss8, nu=nu: e.reciprocal(ss8[:, 0:nu], ss8[:, 0:nu]), reads=[Bss8], writes=[Bss8])
                qk_units = set()
                for (u0, u1) in g["qk"]:
                    qk_units.update(range(u0, u1))
                for u in [u for u in range(nu) if u not in qk_units]:
                    kb.op("DVE", lambda e, u=u, ss8=ss8: e.memset(ss8[:, u:u + 1], 1.0), writes=[Bss8])
                kb.op("DVE", lambda e, qn=qn, pg=pg, ss8=ss8, nu=nu, wq=wq: e.tensor_tensor(
                    qn[:, 0:nu, :], pg[:, 0:wq].rearrange("p (u d) -> p u d", d=64),
                    ss8[:, 0:nu].unsqueeze(2).to_broadcast([128, nu, 64]), ALU.mult),
                      reads=[Bpg, Bss8], writes=[Bqn])
                kb.op("POOL", lambda e, qn=qn, gr=gr, nu=nu: e.tensor_tensor(qn[:, 0:nu, :], qn[:, 0:nu, :], gr[:, 0:nu, :], ALU.mult),
                      reads=[Bqn, Bgr], writes=[Bqn])
            for gi_, g, w, wq, nu in info:
                if not g["qk"]:
                    continue
                qn, Bqn = qnl[gi_]
                post, Bpost = postl[gi_]
                kb.op("ACT", lambda e, post=post, qn=qn, wq=wq, nu=nu: e.copy(post[:, 0:wq], qn[:, 0:nu, :].rearrange("p u d -> p (u d)")),
                      reads=[Bqn], writes=[Bpost])
            for gi_, g, w, wq, nu in info:
                if not g["qk"]:
                    continue
                qn, Bqn = qnl[gi_]
                post, Bpost = postl[gi_]
                rt, Brt = rtl[gi_]
                postv = post[:, 0:wq].rearrange("p (u d) -> p u d", d=64)
                for (u0, u1) in g["qk"]:
                    n_ = u1 - u0
                    cosb = self.cosT[:, t, :].unsqueeze(1).to_broadcast([128, n_, 8])
                    sinb = self.sinT[:, t, :].unsqueeze(1).to_broadcast([128, n_, 8])
                    t1 = qn[:, u0:u1, 0:8]
                    t2 = qn[:, u0:u1, 8:16]
                    kb.op("DVE", lambda e, rt=rt, n_=n_, t1=t1, cosb=cosb: e.tensor_tensor(rt[:, 0, 0:n_, :], t1, cosb, ALU.mult), reads=[Bqn, self.Bcos], writes=[Brt])
                    kb.op("DVE", lambda e, rt=rt, n_=n_, t2=t2, sinb=sinb: e.tensor_tensor(rt[:, 1, 0:n_, :], t2, sinb, ALU.mult), reads=[Bqn, self.Bsin], writes=[Brt])
                    kb.op("DVE", lambda e, rt=rt, n_=n_, t2=t2, cosb=cosb: e.tensor_tensor(rt[:, 2, 0:n_, :], t2, cosb, ALU.mult), reads=[Bqn, self.Bcos], writes=[Brt])
                    kb.op("DVE", lambda e, rt=rt, n_=n_, t1=t1, sinb=sinb: e.tensor_tensor(rt[:, 3, 0:n_, :], t1, sinb, ALU.mult), reads=[Bqn, self.Bsin], writes=[Brt])
                    kb.op("DVE", lambda e, rt=rt, n_=n_, postv=postv, u0=u0, u1=u1: e.tensor_tensor(postv[:, u0:u1, 0:8], rt[:, 0, 0:n_, :], rt[:, 1, 0:n_, :], ALU.subtract),
                          reads=[Brt], writes=[Bpost])
                    kb.op("DVE", lambda e, rt=rt, n_=n_, postv=postv, u0=u0, u1=u1: e.tensor_tensor(postv[:, u0:u1, 8:16], rt[:, 2, 0:n_, :], rt[:, 3, 0:n_, :], ALU.add),
                          reads=[Brt], writes=[Bpost])
            for gi_, g, w, wq, nu in info:
                post, Bpost = postl[gi_]
                tblocks = [(bi, dst) for bi, (k, dst) in enumerate(g["blocks"]) if k == "T"]
                if tblocks:
                    pb, Bpb = pbh.next()
                    st, Bst = stages[gi_][half]
                    for bi, dst in tblocks:
                        kb.op("PE", lambda e, bi=bi, pb=pb, post=post: e.transpose(pb[:, bi * 128:(bi + 1) * 128], post[:, bi * 128:(bi + 1) * 128], self.ident[:]),
                              reads=[Bpost, self.Bident], writes=[Bpb])
                    b0, b1 = tblocks[0][0], tblocks[-1][0] + 1
                    kb.op("ACT", lambda e, st=st, pb=pb, b0=b0, b1=b1: e.copy(st[:, b0:b1, tt * 128:(tt + 1) * 128],
                                                                              pb[:, b0 * 128:b1 * 128].rearrange("p (b t) -> p b t", t=128)),
                          reads=[Bpb], writes=[Bst])
                    if tt == 3:
                        t0 = (t - 3) * 128
                        for bi, dst in tblocks:
                            kb.dma("POOL", self.FT[dst:dst + 128, t0:t0 + 512], st[:, bi, :], reads=[Bst])
                for bi, (k, dst) in enumerate(g["blocks"]):
                    if k == "V":
                        kb.dma("POOL", self.TM[t * 128:(t + 1) * 128, dst:dst + 128], post[:, bi * 128:(bi + 1) * 128], reads=[Bpost])
        T.close()

    def attn_qtile(self, R, q_rhs, Bq, k_lhsT, Bk, v_rhs, Bv, vw, pairs, O, BO, o_off, post_exp=None, bank_of=lambda s: 0, vstat=False, hooks=None, extra_reads=()):
        kb = self.kb
        LOOK = self.cfg.get("look", 5)
        last = {}
        started = set()
        for kc, subs in pairs:
            for s, kind in subs:
                last[s] = kc
        live = {}

        def stage1(i):
            kc, subs = pairs[i]
            pS, BpS = R["pS"].next()
            kl, krows = k_lhsT(kc)
            c0 = min(s_ for s_, _ in subs) * 128
            c1 = (max(s_ for s_, _ in subs) + 1) * 128
            if post_exp is not None:
                c0, c1 = 0, 512
            kb.op("PE", lambda e: e.matmul(pS[0:krows, c0:c1], kl, q_rhs[:, c0:c1], start=True, stop=True), reads=[Bk, Bq] + list(extra_reads), writes=[BpS])
            pT, BpT = R["pT"].next()
            kb.op("ACT", lambda e: e.activation(pT[0:krows, c0:c1], pS[0:krows, c0:c1], AF.Exp, scale=0.125), reads=[BpS], writes=[BpT])
            if post_exp is not None:
                post_exp(kc, pT, BpT, krows)
            for s, kind in subs:
                if kind == "tri":
                    kb.op("DVE", lambda e, s=s: e.tensor_tensor(pT[:, s * 128:(s + 1) * 128], pT[:, s * 128:(s + 1) * 128], self.tri[:], ALU.mult),
                          reads=[BpT, self.Btri], writes=[BpT])
                elif kind == "atri":
                    kb.op("DVE", lambda e, s=s: e.tensor_tensor(pT[:, s * 128:(s + 1) * 128], pT[:, s * 128:(s + 1) * 128], self.atri[:], ALU.mult),
                          reads=[BpT, self.Batri], writes=[BpT])
            live[i] = (pT, BpT, krows)

        def stage2(i):
            kc, subs = pairs[i]
            pT, BpT, krows = live.pop(i)
            if vstat:
                c0 = min(s_ for s_, _ in subs) * 128
                c1 = (max(s_ for s_, _ in subs) + 1) * 128
                st_flag = 0 not in started
                started.add(0)
                kb.op("PE", lambda e: e.matmul(O[0:vw, c0:c1], v_rhs(kc)[0:krows, 0:vw], pT[0:krows, c0:c1],
                                               start=st_flag, stop=(i == len(pairs) - 1), skip_group_check=True),
                      reads=[BpT, Bv], writes=[BO])
                return
            for s, kind in subs:
                bk = bank_of(s)
                st_flag = bk not in started
                started.add(bk)
                kb.op("PE", lambda e, s=s, st_flag=st_flag: e.matmul(o_off(O, s), pT[0:krows, s * 128:(s + 1) * 128], v_rhs(kc)[0:krows, :],
                                                                     start=st_flag, stop=(last[s] == kc), skip_group_check=True),
                      reads=[BpT, Bv], writes=[BO])

        n = len(pairs)
        for i in range(n + LOOK):
            if i < n:
                stage1(i)
            if hooks and i in hooks:
                hooks.pop(i)()
            if i - LOOK >= 0:
                stage2(i - LOOK)
        if hooks:
            for k in sorted(hooks):
                hooks.pop(k)()

    def ot_to_tok(self, OT, BOT, vw, osb, Bosb, Otok, BOtok):
        kb = self.kb
        kb.op("ACT", lambda e: e.copy(osb[0:vw, :], OT[0:vw, :]), reads=[BOT], writes=[Bosb])
        for s in range(4):
            kb.op("PE", lambda e, s=s: e.transpose(Otok[:, s * 128:s * 128 + vw], osb[0:vw, s * 128:(s + 1) * 128], self.identF[0:vw, 0:vw]),
                  reads=[Bosb, self.BidentF], writes=[BOtok])

    @staticmethod
    def causal_pairs(qt):
        pairs = []
        for kc in range(4 * qt + 4):
            j = kc - 4 * qt
            if j < 0:
                pairs.append((kc, [(s, "full") for s in range(4)]))
            else:
                pairs.append((kc, [(s, "tri" if s == j else "full") for s in range(j, 4)]))
        return pairs

    def phase_b_diff(self, li_odd_index):
        kb, I = self.kb, self.I
        T = Scope(kb)
        lam_init = 0.8 - 0.6 * float(np.exp(-0.3 * 1))
        lp, Blp = T.sb("lp", [128, 4, 64], F32)
        kb.dma("SP", lp[:].rearrange("p a d -> p (a d)"), bc_row(I["diff_lambda"][0:1, :], 256), writes=[Blp])
        l2, Bl2 = T.sb("l2", [128, 2, 64], F32)
        kb.op("DVE", lambda e: e.tensor_tensor(l2[:, 0, :], lp[:, 0, :], lp[:, 1, :], ALU.mult), reads=[Blp], writes=[Bl2])
        kb.op("DVE", lambda e: e.tensor_tensor(l2[:, 1, :], lp[:, 2, :], lp[:, 3, :], ALU.mult), reads=[Blp], writes=[Bl2])
        ls, Bls = T.sb("ls", [128, 2], F32)
        kb.op("DVE", lambda e: e.tensor_reduce(ls[:], l2[:], AX.X, ALU.add), reads=[Bl2], writes=[Bls])
        kb.op("ACT", lambda e: e.activation(ls[:], ls[:], AF.Exp), reads=[Bls], writes=[Bls])
        nlam, Bnlam = T.sb("nlam", [128, 1], F32)
        kb.op("DVE", lambda e: e.tensor_tensor(nlam[:], ls[:, 1:2], ls[:, 0:1], ALU.subtract), reads=[Bls], writes=[Bnlam])
        kb.op("DVE", lambda e: e.tensor_scalar(nlam[:], nlam[:], -lam_init, None, ALU.add), reads=[Bnlam], writes=[Bnlam])
        go, Bgo = T.sb("go", [128, 128], F32)
        kb.dma("SP", go[:], bc_row(I["diff_out_norm"][0:1, :], 128), writes=[Bgo])
        kb.op("DVE", lambda e: e.tensor_scalar(go[:], go[:], 1.0 - lam_init, None, ALU.mult), reads=[Bgo], writes=[Bgo])

        KTs = Rot([T.sb("KT", [128, S], BF16) for _ in range(2)])
        QTs = Rot([T.sb("QT", [128, S], BF16) for _ in range(2)])
        Vs = Rot([T.sb("V", [128, NT, 129], BF16) for _ in range(2)])
        for (v, Bv) in Vs.items:
            kb.op("POOL", lambda e, v=v: e.memset(v[:, :, 128:129], 1.0), writes=[Bv])
        R = dict(pS=Rot([T.ps("pS", [128, 512], F32) for _ in range(3)]),
                 pT=Rot([T.sb("pT", [128, 512], BF16) for _ in range(8)]))
        Os = [[T.ps("O", [128, 512], F32) for _ in range(2)] for _ in range(2)]
        osts = Rot([T.sb("ost", [128, 4, 128], BF16) for _ in range(2)])
        osbs = Rot([[[T.sb("osbd", [128, 512], F32) for _ in range(2)] for _ in range(2)] for _ in range(2)])
        pending = []
        a1ts = Rot([T.sb("a1t", [128, 4, 128], F32) for _ in range(2)])
        rdts = Rot([T.sb("rdt", [128, 4, 4], F32) for _ in range(2)])
        a0s = Rot([T.sb("a0", [128, 128], F32) for _ in range(2)])
        a1s = Rot([T.sb("a1", [128, 128], F32) for _ in range(2)])
        rds = Rot([T.sb("rd", [128, 4], F32) for _ in range(4)])
        junk, Bjunk = T.sb("junkb", [128, 128], BF16)

        def load_head(h):
            KT, BKT = KTs.next()
            QT, BQT = QTs.next()
            V, BV = Vs.next()
            kb.dma("SP", QT[:], self.FT[h * 128:(h + 1) * 128, :], writes=[BQT])
            kb.dma("SP", KT[:], self.FT[1024 + h * 128:1024 + (h + 1) * 128, :], writes=[BKT])
            tmv = self.TM[:, h * 128:(h + 1) * 128].rearrange("(c p) d -> p c d", p=128)
            for c4 in range(4):
                kb.dma("SP", V[:, c4 * 8:(c4 + 1) * 8, 0:128], tmv[:, c4 * 8:(c4 + 1) * 8, :], writes=[BV])
            return (KT, BKT, QT, BQT, V, BV)

        nxt = load_head(0)
        for h in range(8):
            KT, BKT, QT, BQT, V, BV = nxt
            if h + 1 < 8:
                nxt = load_head(h + 1)
            for qt in range(8):
                pairs = self.causal_pairs(qt)
                for c in range(2):
                    def o_off(O, s, c=c):
                        return Os[c][s // 2][0][:, (s % 2) * 256:(s % 2) * 256 + 129]
                    hooks = None
                    if c == 0 and pending:
                        e2_, e3_ = pending.pop(0)
                        hooks = {min(6, len(pairs) - 1): e2_, 10 ** 6: e3_}
                    self.attn_qtile(R, QT[64 * c:64 * c + 64, qt * 512:(qt + 1) * 512], BQT,
                                    lambda kc, c=c: (KT[64 * c:64 * c + 64, kc * 128:(kc + 1) * 128], 128), BKT,
                                    lambda kc: V[:, kc, :], BV, 129, pairs, None, Os[c][0][1], o_off, bank_of=lambda s: s // 2, hooks=hooks)
                oset = osbs.next()
                for c in range(2):
                    for b_ in range(2):
                        kb.op("DVE", lambda e, c=c, b_=b_: e.tensor_copy(oset[c][b_][0][:, 0:385], Os[c][b_][0][:, 0:385]),
                              reads=[Os[c][0][1]], writes=[oset[c][b_][1]])

                ep = dict(oset=oset, h=h, qt=qt)
                a1t, Ba1t = a1ts.next()
                rdt, Brdt = rdts.next()
                ep.update(a1t=a1t, Ba1t=Ba1t, rdt=rdt, Brdt=Brdt)

                def e1(ep=ep):
                    oset, a1t, Ba1t, rdt, Brdt = ep["oset"], ep["a1t"], ep["Ba1t"], ep["rdt"], ep["Brdt"]
                    for s in range(4):
                        O0 = oset[0][s // 2][0][:, (s % 2) * 256:(s % 2) * 256 + 129]
                        O1 = oset[1][s // 2][0][:, (s % 2) * 256:(s % 2) * 256 + 129]
                        BO0, BO1 = oset[0][s // 2][1], oset[1][s // 2][1]
                        a0, Ba0 = a0s.next()
                        kb.op("DVE", lambda e, s=s, O0=O0: e.reciprocal(rdt[:, s, 0:1], O0[:, 128:129]), reads=[BO0], writes=[Brdt])
                        kb.op("DVE", lambda e, s=s, O1=O1: e.reciprocal(rdt[:, s, 1:2], O1[:, 128:129]), reads=[BO1], writes=[Brdt])
                        kb.op("DVE", lambda e, s=s: e.tensor_tensor(rdt[:, s, 1:2], rdt[:, s, 1:2], nlam[:], ALU.mult), reads=[Brdt, Bnlam], writes=[Brdt])
                        kb.op("DVE", lambda e, s=s, a0=a0, O0=O0: e.tensor_scalar(a0[:], O0[:, 0:128], rdt[:, s, 0:1], None, ALU.mult), reads=[BO0, Brdt], writes=[Ba0])
                        kb.op("DVE", lambda e, s=s, a0=a0, O1=O1: e.scalar_tensor_tensor(a1t[:, s, :], O1[:, 0:128], rdt[:, s, 1:2], a0[:], ALU.mult, ALU.add),
                              reads=[BO1, Brdt, Ba0], writes=[Ba1t])

                def e2(ep=ep):
                    a1t, Ba1t, rdt, Brdt = ep["a1t"], ep["Ba1t"], ep["rdt"], ep["Brdt"]
                    for s in range(4):
                        kb.op("ACT", lambda e, s=s: e.activation(junk[:], a1t[:, s, :], AF.Square, accum_out=rdt[:, s, 2:3]), reads=[Ba1t], writes=[Bjunk, Brdt])
                    kb.op("ACT", lambda e: e.activation(rdt[:, :, 2:3], rdt[:, :, 2:3], AF.Sqrt, bias=self.epsb[:], scale=1.0 / 128),
                          reads=[Brdt, self.Bepsb], writes=[Brdt])

                def e3(ep=ep):
                    a1t, Ba1t, rdt, Brdt, h, qt = ep["a1t"], ep["Ba1t"], ep["rdt"], ep["Brdt"], ep["h"], ep["qt"]
                    ost, Bost = osts.next()
                    kb.op("DVE", lambda e: e.reciprocal(rdt[:, :, 3:4], rdt[:, :, 2:3]), reads=[Brdt], writes=[Brdt])
                    for s in range(4):
                        kb.op("DVE", lambda e, s=s, ost=ost: e.scalar_tensor_tensor(ost[:, s, :], a1t[:, s, :], rdt[:, s, 3:4], go[:], ALU.mult, ALU.mult),
                              reads=[Ba1t, Brdt, Bgo], writes=[Bost])
                    osv = self.OS[qt * 512:(qt + 1) * 512, h * 128:(h + 1) * 128].rearrange("(s p) d -> p s d", p=128)
                    kb.dma("POOL", osv, ost[:], reads=[Bost])
                e1()
                pending.append((e2, e3))
        while pending:
            e2_, e3_ = pending.pop(0)
            e2_()
            e3_()
        T.close()

    def phase_c1(self, li, x_src, w_out_ap):
        kb, I = self.kb, self.I
        T = Scope(kb)
        W, BW = T.sb("w_out", [128, 8, D], BF16)
        self.load_w_bf16(W, BW, w_out_ap, 8)
        ots = Rot([T.sb("ot", [128, D], BF16) for _ in range(3)])
        xts = Rot([T.sb("xt", [128, D], F32) for _ in range(3)])
        oTs = Rot([T.sb("oT", [128, 8, 128], BF16) for _ in range(2)])
        x1s = Rot([T.sb("x1", [128, D], F32) for _ in range(2)])
        hbs = Rot([T.sb("hb", [128, D], BF16) for _ in range(2)])
        ytmp, Bytmp = T.sb("ytmp", [128, D], F32)
        junk, Bjunk = T.sb("junk", [128, D], BF16)
        sss = Rot([T.sb("ss", [128, 1], F32) for _ in range(2)])
        hn, Bhn = T.sb("hn", [128, D], F32)
        stg = [T.sb("stage", [128, 8, 512], BF16) for _ in range(2)]
        pts = Rot([T.ps("ptA", [128, 1024], BF16) for _ in range(2)])
        pys = Rot([T.ps("py", [128, 512], F32) for _ in range(4)])

        def load(t):
            ot, Bot = ots.next()
            xt, Bxt = xts.next()
            kb.dma("SP", ot[:], self.OS[t * 128:(t + 1) * 128, :], writes=[Bot])
            kb.dma("SP", xt[:], x_src[t * 128:(t + 1) * 128, :], writes=[Bxt])
            return ot, Bot, xt, Bxt

        def front(t, ld):
            ot, Bot, xt, Bxt = ld
            oT, BoT = oTs.next()
            pt, Bpt = pts.next()
            self.transpose8(ot, Bot, pt, Bpt, oT[:], BoT, eng="ACT")
            pyl = []
            for g in range(2):
                py, Bpy = pys.next()
                for kc in range(8):
                    kb.op("PE", lambda e, kc=kc, g=g, py=py: e.matmul(py[:], oT[:, kc, :], W[:, kc, g * 512:(g + 1) * 512], start=(kc == 0), stop=(kc == 7)),
                          reads=[BoT, BW], writes=[Bpy])
                pyl.append((py, Bpy))
            return pyl, xt, Bxt

        def back(t, fr):
            pyl, xt, Bxt = fr
            x1, Bx1 = x1s.next()
            for g in range(2):
                py, Bpy = pyl[g]
                sl = slice(g * 512, (g + 1) * 512)
                kb.op("DVE", lambda e, py=py, sl=sl: e.tensor_tensor(ytmp[:, sl], py[:], self.mod[:, 2, sl], ALU.mult), reads=[Bpy, self.Bmod], writes=[Bytmp])
                kb.op("POOL", lambda e, sl=sl: e.tensor_tensor(x1[:, sl], ytmp[:, sl], xt[:, sl], ALU.add), reads=[Bytmp, Bxt], writes=[Bx1])
            kb.dma("POOL", self.x1s[t * 128:(t + 1) * 128, :], x1[:], reads=[Bx1])
            hb, Bhb = hbs.next()
            ss, Bss = sss.next()
            self.rms_mod_tile(T, x1, Bx1, hb, Bhb, 4, 3, (junk, Bjunk, ss, Bss, hn, Bhn))
            pt, Bpt = pts.next()
            tt = t % 4
            st, Bst = stg[(t // 4) % 2]
            self.transpose8(hb, Bhb, pt, Bpt, st[:, :, tt * 128:(tt + 1) * 128], Bst, eng="ACT")
            if tt == 3:
                t0 = (t - 3) * 128
                h2v = self.H2T.rearrange("(j p) t -> p j t", p=128)
                kb.dma("POOL", h2v[:, :, t0:t0 + 512], st[:], reads=[Bst])

        lds = [load(0)]
        if NT > 1:
            lds.append(load(1))
        fr = front(0, lds.pop(0))
        for t in range(NT):
            if t + 2 < NT:
                lds.append(load(t + 2))
            nfr = front(t + 1, lds.pop(0)) if t + 1 < NT else None
            back(t, fr)
            fr = nfr
        T.close()

    def phase_c2(self, li, x_dst):
        kb, I = self.kb, self.I
        T = Scope(kb)
        Wg, BWg = T.sb("wg", [128, 8, FFN], BF16)
        Wu, BWu = T.sb("wu", [128, 8, FFN], BF16)
        Wd, BWd = T.sb("wd", [128, NJ, D], BF16)
        self.load_w_bf16(Wg, BWg, I["ffn_w_gate"][li], 8)
        self.load_w_bf16(Wu, BWu, I["ffn_w_up"][li], 8)
        self.load_w_bf16(Wd, BWd, I["ffn_w_down"][li], NJ)
        TT = 256
        h2s = Rot([T.sb("h2T", [128, 8, TT], BF16) for _ in range(2)])
        act, Bact = T.sb("act", [128, NJ, TT], BF16)
        sgs = Rot([T.sb("sg", [128, TT], F32) for _ in range(2)])
        xts = Rot([T.sb("x1t", [128, D], F32) for _ in range(2)])
        ytmp, Bytmp = T.sb("ytmp", [128, 512], F32)
        pgs = Rot([T.ps("pg", [128, 512], F32) for _ in range(2)])
        pus = Rot([T.ps("pu", [128, 512], F32) for _ in range(2)])
        pys = Rot([T.ps("py", [128, 512], F32) for _ in range(3)])
        h2v = self.H2T.rearrange("(j p) t -> p j t", p=128)

        def load(st):
            h2, Bh2 = h2s.next()
            kb.dma("SP", h2[:], h2v[:, :, st * TT:(st + 1) * TT], writes=[Bh2])
            return h2, Bh2

        nxt = load(0)
        for st in range(S // TT):
            h2, Bh2 = nxt
            if st + 1 < S // TT:
                nxt = load(st + 1)
            for j in range(NJ):
                pg, Bpg = pgs.next()
                pu, Bpu = pus.next()
                for kc in range(8):
                    kb.op("PE", lambda e, kc=kc: e.matmul(pg[:, 0:TT], Wg[:, kc, j * 128:(j + 1) * 128], h2[:, kc, :], start=(kc == 0), stop=(kc == 7)),
                          reads=[BWg, Bh2], writes=[Bpg])
                for kc in range(8):
                    kb.op("PE", lambda e, kc=kc: e.matmul(pu[:, 0:TT], Wu[:, kc, j * 128:(j + 1) * 128], h2[:, kc, :], start=(kc == 0), stop=(kc == 7)),
                          reads=[BWu, Bh2], writes=[Bpu])
                sg, Bsg = sgs.next()
                kb.op("ACT", lambda e: e.activation(sg[:], pg[:, 0:TT], AF.Silu), reads=[Bpg], writes=[Bsg])
                kb.op("DVE", lambda e, j=j: e.tensor_tensor(act[:, j, :], sg[:], pu[:, 0:TT], ALU.mult), reads=[Bsg, Bpu], writes=[Bact])
            for q in range(TT // 128):
                t = st * (TT // 128) + q
                xt, Bxt = xts.next()
                kb.dma("SP", xt[:], self.x1s[t * 128:(t + 1) * 128, :], writes=[Bxt])
                for g in range(2):
                    py, Bpy = pys.next()
                    for j in range(NJ):
                        kb.op("PE", lambda e, j=j: e.matmul(py[:], act[:, j, q * 128:(q + 1) * 128], Wd[:, j, g * 512:(g + 1) * 512],
                                                            start=(j == 0), stop=(j == NJ - 1)),
                              reads=[Bact, BWd], writes=[Bpy])
                    sl = slice(g * 512, (g + 1) * 512)
                    kb.op("DVE", lambda e: e.tensor_tensor(ytmp[:], py[:], self.mod[:, 5, sl], ALU.mult), reads=[Bpy, self.Bmod], writes=[Bytmp])
                    kb.op("POOL", lambda e: e.tensor_tensor(xt[:, sl], ytmp[:], xt[:, sl], ALU.add), reads=[Bytmp, Bxt], writes=[Bxt])
                kb.dma("POOL", x_dst[t * 128:(t + 1) * 128, :], xt[:], reads=[Bxt])
        T.close()

    def build(self):
        kb = self.kb
        self.declare()
        self.setup()
        G = self.G
        self.epsb, self.Bepsb = G.sb("epsb", [128, 1], F32)
        kb.op("POOL", lambda e: e.memset(self.epsb[:], EPS), writes=[self.Bepsb])
        layers = self.cfg.get("layers", [0, 1])
        x_src = self.I["x"]
        for n, li in enumerate(layers):
            x_dst = self.out if n == len(layers) - 1 else self.xmid
            self.L = Scope(kb)
            if li == 0:
                self.gate_sb, self.Bgate = self.L.sb("gates", [128, NT, 24], F32)
            Wa = Scope(kb)
            w_ap_, ncols_ = (self.I["sp_w_in"], SP_IN) if li == 0 else (self.I["diff_w_in"], 3072)
            Wt, BWt = Wa.sb("w_in", [128, 8, ncols_], BF16)
            self.load_w_bf16(Wt, BWt, w_ap_, 8)
            self.w_in_pre = (Wt, BWt)
            self.compute_mod(li)
            stop = self.cfg.get("stop")
            if stop == "mod":
                Wa.close()
                self.L.close()
                break
            if li == 0:
                self.phase_a(0, x_src)
                Wa.close()
                if stop == "A":
                    self.L.close()
                    break
                if not self.cfg.get("skip_moba"):
                    self.moba_part()
                if stop == "moba":
                    self.L.close()
                    break
                self.nsa_part()
                if stop in ("nsa", "cmpmlp"):
                    self.L.close()
                    break
                self.phase_c1(0, x_src, self.I["sp_w_out"])
            else:
                self.phase_a(1, x_src)
                Wa.close()
                if stop == "A":
                    self.L.close()
                    break
                self.phase_b_diff(0)
                if stop == "B":
                    self.L.close()
                    break
                self.phase_c1(1, x_src, self.I["diff_w_out"])
            if stop == "C1":
                self.L.close()
                break
            self.phase_c2(li, x_dst)
            self.L.close()
            x_src = x_dst
        kb.barrier(engines=("POOL",))
        G.es.close()
        kb.es.close()
        return self.nc

    def epi_norm(self, T, O_ap, BO, vcol, rd_ap, Brd):
        kb = self.kb
        kb.op("DVE", lambda e: e.tensor_scalar(rd_ap, O_ap[:, vcol:vcol + 1], 1e-30, None, ALU.max), reads=[BO], writes=[Brd])
        kb.op("DVE", lambda e: e.reciprocal(rd_ap, rd_ap), reads=[Brd], writes=[Brd])

    def phase_b_sparse(self):
        self.moba_part()
        self.nsa_part()

    def moba_part(self):
        kb, I = self.kb, self.I
        T = Scope(kb)
        QAs = Rot([T.sb("QA", [128, 2, S], BF16) for _ in range(2)])
        KAs = Rot([T.sb("KA", [128, 2, S], BF16) for _ in range(2)])
        Vps = Rot([T.sb("Vp", [128, NT, 2, 65], BF16) for _ in range(2)])
        for (ka, Bka) in KAs.items:
            for hh in range(2):
                kb.dma("POOL", ka[64:80, hh, :], I["c_e16"][:, :], writes=[Bka])
        for (v, Bv) in Vps.items:
            kb.op("POOL", lambda e, v=v: e.memset(v[:, :, :, 64:65], 1.0), writes=[Bv])
        t1, Bt1 = T.sb("t1", [128, 16, 16], F32)
        t2, Bt2 = T.sb("t2", [128, 16, 16], F32)
        kb.dma("SP", t1[:].rearrange("p a b -> p (a b)"), bc_row(I["c_t1"][0:1, :], 256), writes=[Bt1])
        kb.dma("SP", t2[:].rearrange("p a b -> p (a b)"), bc_row(I["c_t2"][0:1, :], 256), writes=[Bt2])
        augs = Rot([T.sb("aug", [128, 128], BF16) for _ in range(2)])
        for (a, Ba) in augs.items:
            kb.op("POOL", lambda e, a=a: e.memset(a[:], 0.0), writes=[Ba])
        km, Bkm = T.sb("km", [64, 16], F32)
        kmb, Bkmb = T.sb("kmb", [64, 16], BF16)
        gms = Rot([T.sb("gm", [128, 16], F32) for _ in range(2)])
        m8s = Rot([T.sb("m8", [128, 8], F32) for _ in range(2)])
        sels = Rot([T.sb("sel", [128, 16], F32) for _ in range(2)])
        R = dict(pS=Rot([T.ps("pS", [128, 512], F32) for _ in range(3)]),
                 pT=Rot([T.sb("pT", [128, 512], BF16) for _ in range(8)]))
        Obs = Rot([T.ps("OTm", [128, 512], F32) for _ in range(2)])
        Otk, BOtk = T.ps("Otok", [128, 512], F32)
        osbs = Rot([T.sb("osb", [128, 512], F32) for _ in range(2)])
        pgs = Rot([T.ps("pgate", [128, 512], F32) for _ in range(1)])
        ptr = Rot([T.ps("ptr", [128, 1024], BF16) for _ in range(1)])
        osts = Rot([T.sb("ost", [128, 4, 128], BF16) for _ in range(2)])
        rds = Rot([T.sb("rd", [128, 1], F32) for _ in range(4)])

        def load_pair(hp):
            QA, BQA = QAs.next()
            KA, BKA = KAs.next()
            Vp, BVp = Vps.next()
            for hh in range(2):
                h = 2 * hp + hh
                kb.dma("SP", QA[0:64, hh, :], self.FT[h * 64:(h + 1) * 64, :], writes=[BQA])
                kb.dma("SP", KA[0:64, hh, :], self.FT[512 + h * 64:512 + (h + 1) * 64, :], writes=[BKA])
            for hh in range(2):
                h = 2 * hp + hh
                tmv = self.TM[:, h * 64:(h + 1) * 64].rearrange("(c p) d -> p c d", p=128)
                for c4 in range(4):
                    kb.dma("SP", Vp[:, c4 * 8:(c4 + 1) * 8, hh, 0:64], tmv[:, c4 * 8:(c4 + 1) * 8, :], writes=[BVp])
            return QA, BQA, KA, BKA, Vp, BVp

        kms = [T.sb("km2", [64, 16], F32) for _ in range(2)]
        kmbs = [T.sb("kmb2", [64, 16], BF16) for _ in range(2)]
        pgt, _ = pgs.items[0]
        pg_slots = Rot([(pgt[:, j * 16:(j + 1) * 16], Buf("pgs%d" % j)) for j in range(8)])
        ptt, _ = ptr.items[0]
        pt_slots = Rot([(ptt[:, j * 128:(j + 1) * 128], Buf("pts%d" % j)) for j in range(8)])
        augs8 = Rot([T.sb("aug8", [128, 128], BF16) for _ in range(8)])
        for (a_, Ba_) in augs8.items:
            kb.op("POOL", lambda e, a_=a_: e.memset(a_[:], 0.0), writes=[Ba_])
        gms8 = Rot([T.sb("gm8", [128, 16], F32) for _ in range(8)])
        m8s8 = Rot([T.sb("m88", [128, 8], F32) for _ in range(8)])
        sels8 = Rot([T.sb("sel8", [128, 16], F32) for _ in range(8)])

        def gating_jobs(QA, BQA, KA, BKA):
            def prep():
                for hh in range(2):
                    km_, Bkm_ = kms[hh]
                    kmb_, Bkmb_ = kmbs[hh]
                    kb.op("DVE", lambda e, hh=hh, km_=km_: e.tensor_reduce(km_[:], KA[0:64, hh, :].rearrange("p (b j) -> p b j", j=256), AX.X, ALU.add),
                          reads=[BKA], writes=[Bkm_])
                    kb.op("DVE", lambda e, km_=km_, kmb_=kmb_: e.tensor_scalar(kmb_[:], km_[:], 1.0 / 256, None, ALU.mult), reads=[Bkm_], writes=[Bkmb_])
            jobs = []
            for hh in range(2):
                for t in range(NT):
                    st = {}

                    def part1(hh=hh, t=t, st=st):
                        own = t // 2
                        kmb_, Bkmb_ = kmbs[hh]
                        pg, Bpg = pg_slots.next()
                        kb.op("PE", lambda e: e.matmul(pg, QA[0:64, hh, t * 128:(t + 1) * 128], kmb_[:], start=True, stop=True),
                              reads=[BQA, Bkmb_], writes=[Bpg])
                        gm, Bgm = gms8.next()
                        m8, Bm8 = m8s8.next()
                        sel, Bsel = sels8.next()
                        aug, Baug = augs8.next()
                        kb.op("DVE", lambda e: e.tensor_tensor(gm[:], pg, t1[:, own, :], ALU.add), reads=[Bpg, Bt1], writes=[Bgm])
                        kb.op("DVE", lambda e: e.max(m8[:], gm[:]), reads=[Bgm], writes=[Bm8])
                        kb.op("DVE", lambda e: e.tensor_scalar(m8[:, 2:3], m8[:, 2:3], -1e29, None, ALU.max), reads=[Bm8], writes=[Bm8])
                        kb.op("DVE", lambda e: e.tensor_scalar(sel[:], gm[:], m8[:, 2:3], None, ALU.is_ge), reads=[Bgm, Bm8], writes=[Bsel])
                        kb.op("DVE", lambda e: e.tensor_tensor(sel[:], sel[:], t2[:, own, :], ALU.max), reads=[Bsel, Bt2], writes=[Bsel])
                        kb.op("DVE", lambda e: e.tensor_scalar(aug[:, 64:80], sel[:], -1.0, -MASKV, ALU.add, ALU.mult), reads=[Bsel], writes=[Baug])
                        st["aug"] = (aug, Baug)

                    def part2(hh=hh, t=t, st=st):
                        aug, Baug = st["aug"]
                        pt, Bpt = pt_slots.next()
                        kb.op("PE", lambda e: e.transpose(pt, aug[:], self.ident[:]), reads=[Baug, self.Bident], writes=[Bpt])
                        kb.op("ACT", lambda e: e.copy(QA[64:80, hh, t * 128:(t + 1) * 128], pt[64:80, :]), reads=[Bpt], writes=[BQA])
                    jobs.append((part1, part2))
            return prep, jobs

        nxt = load_pair(0)
        pending = []
        prep0, jobs0 = gating_jobs(nxt[0], nxt[1], nxt[2], nxt[3])
        prep0()
        for j0 in range(0, len(jobs0), 4):
            for p1, _ in jobs0[j0:j0 + 4]:
                p1()
            for _, p2 in jobs0[j0:j0 + 4]:
                p2()
        for hp in range(4):
            QA, BQA, KA, BKA, Vp, BVp = nxt
            njobs = None
            if hp + 1 < 4:
                nxt = load_pair(hp + 1)
                nprep, njobs = gating_jobs(nxt[0], nxt[1], nxt[2], nxt[3])
            for qt in range(8):
                pairs = self.causal_pairs(qt)
                ost, Bost = osts.next()
                for hh in range(2):
                    OTm, BOTm = Obs.next()
                    hooks = None
                    if njobs is not None:
                        ci = qt * 2 + hh
                        mine = njobs[ci * 4:(ci + 1) * 4]

                        def h1(mine=mine, first=(ci == 0)):
                            if first:
                                nprep()
                            for p1, _ in mine:
                                p1()

                        def h2(mine=mine):
                            for _, p2 in mine:
                                p2()
                        hooks = {0: h1, 6: h2}
                    self.attn_qtile(R, QA[0:80, hh, qt * 512:(qt + 1) * 512], BQA,
                                    lambda kc: (KA[0:80, hh, kc * 128:(kc + 1) * 128], 128), BKA,
                                    lambda kc: Vp[:, kc, hh, :], BVp, 65, pairs, OTm, BOTm, None, vstat=True, hooks=hooks)
                    def epilogue(OTm=OTm, BOTm=BOTm, ost=ost, Bost=Bost, hh=hh, qt=qt, hp=hp):
                        osb, Bosb = osbs.next()
                        self.ot_to_tok(OTm, BOTm, 65, osb, Bosb, Otk, BOtk)
                        Om, BOm = Otk, BOtk
                        for s in range(4):
                            rd, Brd = rds.next()
                            Os_ = Om[:, s * 128:s * 128 + 65]
                            self.epi_norm(T, Os_, BOm, 64, rd[:], Brd)
                            kb.op("DVE", lambda e, s=s, Os_=Os_, rd=rd: e.tensor_scalar(ost[:, s, hh * 64:(hh + 1) * 64], Os_[:, 0:64], rd[:, 0:1], None, ALU.mult),
                                  reads=[BOm, Brd], writes=[Bost])
                        if hh == 1:
                            osv = self.OS[qt * 512:(qt + 1) * 512, hp * 128:(hp + 1) * 128].rearrange("(s p) d -> p s d", p=128)
                            kb.dma("POOL", osv, ost[:], reads=[Bost])
                    if pending:
                        pending.pop()()
                    pending.append(epilogue)
        if pending:
            pending.pop()()
        T.close()

    def nsa_part(self):
        kb, I = self.kb, self.I
        for _ in range(self.cfg.get("pad_dve", 0)):
            kb.op("DVE", lambda e: e.memset(self.epsb[:], EPS), writes=[self.Bepsb])
        P = Scope(kb)
        CKc, BCKc = P.sb("CKc", [64, 2, 256], BF16)
        VCs = [P.sb("VC", [128, 2, 129], BF16) for _ in range(2)]
        kb.op("POOL", lambda e: e.memset(CKc[:], 0.0), writes=[BCKc])
        for g in range(2):
            vc, Bvc = VCs[g]
            for c in range(2):
                kb.dma("POOL", vc[:, c, 64:129], I["c_ov"][c * 128:(c + 1) * 128, :], writes=[Bvc])
        T = Scope(kb)
        CX = [T.sb("CX", [128, S], BF16) for _ in range(2)]
        kb.dma("SP", CX[0][0][:], self.FT[1536:1664, :], writes=[CX[0][1]])
        kb.dma("SP", CX[1][0][:], self.FT[1664:1792, :], writes=[CX[1][1]])
        w1, Bw1 = T.sb("w1", [128, 2 * 32 * 256], BF16)
        w1src = I["cmp_w1"].rearrange("d a l e -> d (a l e)")
        for half in range(2):
            for a in range(2):
                kb.dma("POOL", w1[half * 64:(half + 1) * 64, a * 8192:(a + 1) * 8192], w1src[:, a * 8192:(a + 1) * 8192], writes=[Bw1])
        w1v = w1[:].rearrange("p (a l e) -> p a l e", a=2, l=32)
        posb, Bposb = T.sb("posb", [64, 2, 34], BF16)
        kb.op("POOL", lambda e: e.memset(posb[:], 0.0), writes=[Bposb])
        kb.dma("POOL", posb[:, :, 0:32], I["cmp_posT"][:, :, :], writes=[Bposb])
        w2, Bw2 = T.sb("w2", [128, 2, 2, 64], BF16)
        for kv in range(2):
            for eh in range(2):
                kb.dma("POOL", w2[:, kv, eh, :], I["cmp_w2"][kv, eh * 128:(eh + 1) * 128, :], writes=[Bw2])
        b1, Bb1 = T.sb("b1", [128, 4], F32)
        pbs = Rot([T.ps("pb", [128, 512], F32) for _ in range(2)])
        phs = Rot([T.ps("ph", [128, 512], F32) for _ in range(3)])
        for kv in range(2):
            for eh in range(2):
                pb, Bpb = pbs.next()
                for l in range(32):
                    kb.op("PE", lambda e, l=l: e.matmul(pb[:, 0:2], w1v[0:64, kv, l, eh * 128:(eh + 1) * 128],
                                                        posb[0:64, kv, l:l + 2], start=(l == 0), stop=(l == 31)),
                          reads=[Bw1, Bposb], writes=[Bpb])
                kb.op("DVE", lambda e: e.tensor_copy(b1[:, kv * 2 + eh:kv * 2 + eh + 1], pb[:, 0:1]), reads=[Bpb], writes=[Bb1])
        hid = {}
        for kv in range(2):
            cx, Bcx = CX[kv]
            cxv = cx[:].rearrange("p (n j) -> p n j", j=16)
            for g in range(2):
                for eh in range(2):
                    ph, Bph = phs.next()
                    for l in range(32):
                        a = l // 16
                        kb.op("PE", lambda e, l=l, a=a: e.matmul(ph[:, 0:255], w1v[64 * g:64 * g + 64, kv, l, eh * 128:(eh + 1) * 128],
                                                                 cxv[64 * g:64 * g + 64, a:a + 255, l % 16], start=(l == 0), stop=(l == 31)),
                              reads=[Bw1, Bcx], writes=[Bph])
                    ht, Bht = T.sb("hid", [128, 256], BF16)
                    kb.op("POOL", lambda e: e.memset(ht[:], 0.0), writes=[Bht])
                    kb.op("ACT", lambda e: e.activation(ht[:, 0:255], ph[:, 0:255], AF.Silu, bias=b1[:, kv * 2 + eh:kv * 2 + eh + 1]),
                          reads=[Bph, Bb1], writes=[Bht])
                    hid[(kv, g, eh)] = (ht, Bht)
        for g in range(2):
            ph, Bph = phs.next()
            for eh in range(2):
                ht, Bht = hid[(0, g, eh)]
                kb.op("PE", lambda e: e.matmul(ph[0:64, 0:255], w2[:, 0, eh, :], ht[:, 0:255], start=(eh == 0), stop=(eh == 1)),
                      reads=[Bw2, Bht], writes=[Bph])
            kb.op("ACT", lambda e: e.copy(CKc[:, g, 0:255], ph[0:64, 0:255]), reads=[Bph], writes=[BCKc])
            vc, Bvc = VCs[g]
            for c in range(2):
                ph, Bph = phs.next()
                for eh in range(2):
                    ht, Bht = hid[(1, g, eh)]
                    kb.op("PE", lambda e: e.matmul(ph[:, 0:64], ht[:, c * 128:(c + 1) * 128], w2[:, 1, eh, :], start=(eh == 0), stop=(eh == 1)),
                          reads=[Bw2, Bht], writes=[Bph])
                kb.op("ACT", lambda e: e.copy(vc[:, c, 0:64], ph[:, 0:64]), reads=[Bph], writes=[Bvc])
        T.close()
        if self.cfg.get("stop") == "cmpmlp":
            if self.cfg.get("debug"):
                dck = self.tap("d_ckc", [64, 512], BF16)
                kb.dma("SP", dck[:, :], CKc[:].rearrange("p g n -> p (g n)"), reads=[BCKc])
                for g in range(2):
                    dvc = self.tap("d_vc%d" % g, [128, 258], BF16)
                    kb.dma("SP", dvc[:, :], VCs[g][0][:].rearrange("p c n -> p (c n)"), reads=[VCs[g][1]])
            P.close()
            return

        T = Scope(kb)
        nfv, Bnfv = T.sb("nfv", [128, NT, 64], F32)
        cst, Bcst = T.sb("cst", [128, NT, 64], F32)
        v01, Bv01 = T.sb("v01", [128, NT, 64], F32)
        kb.dma("SP", nfv[:], I["c_nfv"][:, :, :], writes=[Bnfv])
        kb.dma("SP", cst[:], I["c_cst"][:, :, :], writes=[Bcst])
        kb.dma("SP", v01[:], I["c_v01"][:, :, :], writes=[Bv01])
        QA4, BQA4 = T.sb("QA4", [128, 4, S], BF16)
        BQA4m = Buf("qa4_maskrows")
        KS, BKS = T.sb("KS", [128, S], BF16)
        KW, BKW = T.sb("KW", [64, S], BF16)
        VS, BVS = T.sb("VS", [128, NT, 65], BF16)
        VW, BVW = T.sb("VW", [128, NT, 65], BF16)
        kb.op("POOL", lambda e: e.memset(VS[:, :, 64:65], 1.0), writes=[BVS])
        kb.op("POOL", lambda e: e.memset(VW[:, :, 64:65], 1.0), writes=[BVW])
        kb.dma("POOL", KS[64:128, :], I["c_e64"][:, :], writes=[BKS])
        augs = Rot([T.sb("aug", [128, 128], BF16) for _ in range(8)])
        for (a, Ba) in augs.items:
            kb.op("POOL", lambda e, a=a: e.memset(a[:], 0.0), writes=[Ba])
        R = dict(pS=Rot([T.ps("pS", [128, 512], F32) for _ in range(2)]),
                 pT=Rot([T.sb("pT", [128, 512], BF16) for _ in range(8)]))
        Oc = [T.ps("Oc", [128, 512], F32) for _ in range(2)]
        BOc = Oc[0][1]
        Osw = Rot([T.ps("OTsw", [128, 512], F32) for _ in range(2)])
        Otk, BOtk = T.ps("Otok", [128, 512], F32)
        osbs = Rot([T.sb("osb", [128, 512], F32) for _ in range(2)])
        ptr = Rot([T.ps("ptr", [128, 1024], BF16) for _ in range(1)])
        occ, Bocc = T.sb("occ", [128, 4, 4, 64], F32)
        imp, Bimp = T.sb("imp", [128, 4, 64], F32)
        itmp, Bitmp = T.sb("itmp", [128, 64], F32)
        sc, Bsc = T.sb("sc", [128, 64], F32)
        sc2, Bsc2 = T.sb("sc2", [128, 64], F32)
        m8a, Bm8a = T.sb("m8a", [128, 8], F32)
        m8b, Bm8b = T.sb("m8b", [128, 8], F32)
        selm, Bselm = T.sb("selm", [128, 64], F32)
        rds = Rot([T.sb("rd", [128, 2], F32) for _ in range(6)])
        osts = Rot([T.sb("ost", [128, 4, 256], BF16) for _ in range(2)])
        gsb = self.gate_sb

        for g in range(2):
            for r in range(4):
                h = 4 * g + r
                kb.dma("SP", QA4[0:64, r, :], self.FT[1024 + h * 64:1024 + (h + 1) * 64, :], writes=[BQA4])
            kb.dma("SP", KS[0:64, :], self.FT[1792 + 64 * g:1792 + 64 * g + 64, :], writes=[BKS])
            kb.dma("SP", KW[:], self.FT[1920 + 64 * g:1920 + 64 * g + 64, :], writes=[BKW])
            tms = self.TM[:, 512 + 64 * g:512 + 64 * g + 64].rearrange("(c p) d -> p c d", p=128)
            tmw = self.TM[:, 640 + 64 * g:640 + 64 * g + 64].rearrange("(c p) d -> p c d", p=128)
            for c4 in range(4):
                kb.dma("SP", VS[:, c4 * 8:(c4 + 1) * 8, 0:64], tms[:, c4 * 8:(c4 + 1) * 8, :], writes=[BVS])
                kb.dma("SP", VW[:, c4 * 8:(c4 + 1) * 8, 0:64], tmw[:, c4 * 8:(c4 + 1) * 8, :], writes=[BVW])
            vc, Bvc = VCs[g]
            parts = self.cfg.get("nsa_parts", ("cmp", "select", "sw"))
            for qt in self.cfg.get("nsa_qts", range(8)):
                ost, Bost = osts.next()
                cchunks = [0] + ([1] if qt >= 4 else [])
                cpairs = [(c, [(s, "full") for s in range(4)]) for c in cchunks]

                def cmp_mask(c, pT, BpT, krows):
                    kb.op("POOL", lambda e: e.affine_select(pT[:], pT[:], [[1, 512]], ALU.is_ge, 0.0,
                                                            base=512 * qt - 2048 * c - 31, channel_multiplier=-16),
                          reads=[BpT], writes=[BpT])

                for r in (range(4) if "cmp" in parts else ()):
                    h = 4 * g + r
                    self.attn_qtile(R, QA4[0:64, r, qt * 512:(qt + 1) * 512], BQA4,
                                    lambda c: (CKc[:, g, c * 128:(c + 1) * 128], 128), BCKc,
                                    lambda c: vc[:, c, :], Bvc, 129, cpairs, None, BOc,
                                    lambda O, s: Oc[s // 2][0][:, (s % 2) * 256:(s % 2) * 256 + 129], post_exp=cmp_mask, bank_of=lambda s: s // 2)
                    for s in range(4):
                        t = 4 * qt + s
                        O_ = Oc[s // 2][0][:, (s % 2) * 256:(s % 2) * 256 + 129]
                        rd, Brd = rds.next()
                        self.epi_norm(T, O_, BOc, 64, rd[:, 0:1], Brd)
                        if r == 0:
                            kb.op("DVE", lambda e, s=s: e.tensor_scalar(imp[:, s, :], O_[:, 65:129], rd[:, 0:1], None, ALU.mult),
                                  reads=[BOc, Brd], writes=[Bimp])
                        else:
                            kb.op("DVE", lambda e: e.tensor_scalar(itmp[:], O_[:, 65:129], rd[:, 0:1], None, ALU.mult),
                                  reads=[BOc, Brd], writes=[Bitmp])
                            kb.op("DVE", lambda e, s=s: e.tensor_tensor(imp[:, s, :], imp[:, s, :], itmp[:], ALU.add),
                                  reads=[Bitmp, Bimp], writes=[Bimp])
                        kb.op("DVE", lambda e: e.tensor_tensor(rd[:, 1:2], rd[:, 0:1], gsb[:, t, h:h + 1], ALU.mult), reads=[Brd, self.Bgate], writes=[Brd])
                        kb.op("DVE", lambda e, s=s, r=r: e.tensor_scalar(occ[:, r, s, :], O_[:, 0:64], rd[:, 1:2], None, ALU.mult),
                              reads=[BOc, Brd], writes=[Bocc])
                sel_fin = []
                for s in (range(4) if "select" in parts else ()):
                    t = 4 * qt + s
                    aug, Baug = augs.next()
                    kb.op("DVE", lambda e: e.tensor_tensor(sc[:], imp[:, s, :], nfv[:, t, :], ALU.mult), reads=[Bimp, Bnfv], writes=[Bsc])
                    kb.op("DVE", lambda e: e.tensor_tensor(sc[:], sc[:], cst[:, t, :], ALU.add), reads=[Bsc, Bcst], writes=[Bsc])
                    kb.op("DVE", lambda e: e.max(m8a[:], sc[:]), reads=[Bsc], writes=[Bm8a])
                    kb.op("DVE", lambda e: e.match_replace(sc2[:], m8a[:], sc[:], -1e30), reads=[Bsc, Bm8a], writes=[Bsc2])
                    kb.op("DVE", lambda e: e.max(m8b[:], sc2[:]), reads=[Bsc2], writes=[Bm8b])
                    kb.op("DVE", lambda e: e.tensor_scalar(selm[:], sc[:], m8b[:, 7:8], None, ALU.is_ge), reads=[Bsc, Bm8b], writes=[Bselm])
                    kb.op("DVE", lambda e: e.tensor_tensor(selm[:], selm[:], v01[:, t, :], ALU.mult), reads=[Bselm, Bv01], writes=[Bselm])
                    kb.op("DVE", lambda e, aug=aug: e.tensor_scalar(aug[:, 64:128], selm[:], -1.0, -MASKV, ALU.add, ALU.mult), reads=[Bselm], writes=[Baug])

                    def fin(aug=aug, Baug=Baug, t=t, s=s):
                        pt, Bpt = ptr.next()
                        kb.op("PE", lambda e: e.transpose(pt[:, s * 128:(s + 1) * 128], aug[:], self.ident[:]), reads=[Baug, self.Bident], writes=[Bpt])
                        for r in range(4):
                            kb.op("ACT", lambda e, r=r: e.copy(QA4[64:128, r, t * 128:(t + 1) * 128], pt[64:128, s * 128:(s + 1) * 128]),
                                  reads=[Bpt], writes=[BQA4m])
                    sel_fin.append(fin)
                pending = []
                spairs = self.causal_pairs(qt)
                wpairs = []
                for kc in range(max(0, 4 * qt - 4), 4 * qt + 4):
                    subs = []
                    for s in range(4):
                        dlt = 4 * qt + s - kc
                        if dlt == 0:
                            subs.append((s, "tri"))
                        elif 1 <= dlt <= 3:
                            subs.append((s, "full"))
                        elif dlt == 4:
                            subs.append((s, "atri"))
                    if subs:
                        wpairs.append((kc, subs))
                branches = ((spairs, 128, KS, BKS, VS, BVS), (wpairs, 64, KW, BKW, VW, BVW))
                for br in ((1, 0) if "sw" in parts else ()):
                    pairs, qrows, kl, Bkl, vt, Bvt = branches[br]
                    for r in range(4):
                        h = 4 * g + r
                        OTb, BOTb = Osw.next()
                        hooks = None
                        if br == 1 and sel_fin:
                            hooks = {(4 if r == 0 else 2): sel_fin[r]}
                        self.attn_qtile(R, QA4[0:qrows, r, qt * 512:(qt + 1) * 512], BQA4,
                                        lambda kc, kl=kl, qrows=qrows: (kl[0:qrows, kc * 128:(kc + 1) * 128], 128), Bkl,
                                        lambda kc, vt=vt: vt[:, kc, :], Bvt, 65, pairs, OTb, BOTb, None, vstat=True,
                                        hooks=hooks, extra_reads=([BQA4m] if br == 0 else []))
                        def epilogue(OTb=OTb, BOTb=BOTb, br=br, r=r, h=h, qt=qt, ost=ost, Bost=Bost):
                            osb, Bosb = osbs.next()
                            self.ot_to_tok(OTb, BOTb, 65, osb, Bosb, Otk, BOtk)
                            Ob, BOb = Otk, BOtk
                            for s in range(4):
                                t = 4 * qt + s
                                O_ = Ob[:, s * 128:s * 128 + 65]
                                rd, Brd = rds.next()
                                self.epi_norm(T, O_, BOb, 64, rd[:, 0:1], Brd)
                                gcol = (br + 1) * 8 + h
                                kb.op("DVE", lambda e, rd=rd, t=t, gcol=gcol: e.tensor_tensor(rd[:, 1:2], rd[:, 0:1], gsb[:, t, gcol:gcol + 1], ALU.mult),
                                      reads=[Brd, self.Bgate], writes=[Brd])
                                if br == 1:
                                    kb.op("DVE", lambda e, O_=O_, rd=rd: e.tensor_scalar(itmp[:], O_[:, 0:64], rd[:, 1:2], None, ALU.mult),
                                          reads=[BOb, Brd], writes=[Bitmp])
                                    kb.op("DVE", lambda e, s=s: e.tensor_tensor(occ[:, r, s, :], occ[:, r, s, :], itmp[:], ALU.add),
                                          reads=[Bitmp, Bocc], writes=[Bocc])
                                else:
                                    kb.op("DVE", lambda e, s=s, O_=O_, rd=rd: e.scalar_tensor_tensor(ost[:, s, r * 64:(r + 1) * 64], O_[:, 0:64], rd[:, 1:2], occ[:, r, s, :], ALU.mult, ALU.add),
                                          reads=[BOb, Brd, Bocc], writes=[Bost])
                        if pending:
                            pending.pop()()
                        pending.append(epilogue)
                if pending:
                    pending.pop()()
                osv = self.OS[qt * 512:(qt + 1) * 512, 512 + g * 256:512 + (g + 1) * 256].rearrange("(s p) d -> p s d", p=128)
                kb.dma("POOL", osv, ost[:], reads=[Bost])
                if self.cfg.get("nsa_barrier"):
                    kb.barrier()
        T.close()
        P.close()


def _consts():
    c = {}
    c["c_ident"] = np.eye(128, dtype=np.float32)
    k = np.arange(128)[:, None]
    q = np.arange(128)[None, :]
    c["c_tri"] = (q >= k).astype(np.float32)
    c["c_atri"] = (k > q).astype(np.float32)
    key = np.arange(S)[None, :]
    c["c_e16"] = (key // 256 == np.arange(16)[:, None]).astype(np.float32)
    c["c_e64"] = (key // 64 == np.arange(64)[:, None]).astype(np.float32)
    ncmp = 255
    cs = np.arange(ncmp) * 16
    ss = np.arange(64) * 64
    ov = np.minimum(cs[:, None] + 32, ss[None, :] + 64) - np.maximum(cs[:, None], ss[None, :])
    ovp = np.zeros((256, 65), np.float32)
    ovp[:255, 0] = 1.0
    ovp[:255, 1:] = np.clip(ov, 0, None) / 32.0
    c["c_ov"] = ovp
    own = np.arange(16)[:, None]
    blk = np.arange(16)[None, :]
    c["c_t1"] = np.where(blk < own, 0.0, -1e30).astype(np.float32).reshape(1, 256)
    c["c_t2"] = (blk == own).astype(np.float32).reshape(1, 256)
    p = np.arange(128)[:, None, None]
    t = np.arange(NT)[None, :, None]
    m = np.arange(64)[None, None, :]
    cur = (t * 128 + p) // 64
    ok = m <= cur
    forced = ok & ((m == 0) | (m >= cur - 1))
    c["c_nfv"] = (ok & ~forced).astype(np.float32)
    c["c_cst"] = np.where(ok, np.where(forced, 1e4 + m, 0.0), -1e30).astype(np.float32)
    c["c_v01"] = ok.astype(np.float32)
    return c


def _core_inputs(b, inp, consts):
    f = lambda a: np.ascontiguousarray(np.asarray(a), dtype=np.float32)
    m = {}
    m["x"] = f(inp["x"][b])
    m["cT"] = f(np.asarray(inp["c"][b]).reshape(8, 128).T)
    m["posT"] = np.ascontiguousarray(np.asarray(inp["positions"][b]).reshape(NT, 128).T.astype(np.int32))
    for k in ("ada_w", "ada_b", "attn_norm", "ffn_norm", "ffn_w_gate", "ffn_w_up", "ffn_w_down"):
        m[k] = f(inp[k])
    m["sp_w_in"] = f(inp["sp_w_in"][0]); m["sp_w_out"] = f(inp["sp_w_out"][0])
    m["moba_q_norm"] = f(inp["moba_q_norm"]); m["moba_k_norm"] = f(inp["moba_k_norm"])
    m["nsa_q_norm"] = f(inp["nsa_q_norm"]); m["nsa_k_norm"] = f(inp["nsa_k_norm"][0])
    m["cmp_posT"] = f(np.asarray(inp["nsa_cmp_pos"][0]).transpose(2, 0, 1))
    m["cmp_w1"] = f(np.asarray(inp["nsa_cmp_w1"][0]).transpose(2, 0, 1, 3))
    m["cmp_w2"] = f(inp["nsa_cmp_w2"][0])
    m["diff_w_in"] = f(inp["diff_w_in"][0]); m["diff_w_out"] = f(inp["diff_w_out"][0])
    m["diff_q_norm"] = f(inp["diff_q_norm"]); m["diff_k_norm"] = f(inp["diff_k_norm"])
    m["diff_lambda"] = f(np.asarray(inp["diff_lambda"][0]).reshape(1, 256))
    m["diff_out_norm"] = f(inp["diff_out_norm"])
    m.update(consts)
    return m


_NC_CACHE = {}


def kernel(**inputs):
    consts = _consts()
    if "nc" not in _NC_CACHE:
        _NC_CACHE["nc"] = Prog({}).build()
    nc = _NC_CACHE["nc"]
    in_maps = [_core_inputs(b, inputs, consts) for b in range(8)]
    res = run_bass_kernel_spmd(nc, in_maps, core_ids=list(range(8)))
    return np.stack([np.asarray(r["out"], dtype=np.float32) for r in res.results], axis=0)
```

```python
from contextlib import ExitStack
import numpy as np
import concourse.bass as bass
import concourse.mybir as mybir
from concourse.bass_utils import run_bass_kernel_spmd

F32 = mybir.dt.float32
BF16 = mybir.dt.bfloat16
I32 = mybir.dt.int32
AF = mybir.ActivationFunctionType
ALU = mybir.AluOpType
AX = mybir.AxisListType

S = 4096
D = 1024
NT = S // 128
FFN = 2816
NJ = FFN // 128
EPS = 1e-6
MASKV = -30000.0
SP_IN = 2840
N_DMA_SEMS = 24


class Buf:
    __slots__ = ("w", "r", "name")

    def __init__(self, name=""):
        self.w = None
        self.r = {}
        self.name = name


class Rot:
    def __init__(self, items):
        self.items = list(items)
        self.i = 0

    def next(self):
        it = self.items[self.i]
        self.i = (self.i + 1) % len(self.items)
        return it


class KB:
    def __init__(self, nc):
        self.nc = nc
        self.es = ExitStack()
        self.engs = {"PE": nc.tensor, "ACT": nc.scalar, "DVE": nc.vector, "POOL": nc.gpsimd, "SP": nc.sync}
        self.sem = {}
        self.cnt = {}
        for e in ("PE", "ACT", "DVE", "POOL"):
            self.sem[e] = self.es.enter_context(nc.semaphore("s_" + e))
            self.cnt[e] = 0
        self.dsem = [self.es.enter_context(nc.semaphore("d_%d" % i)) for i in range(N_DMA_SEMS)]
        self.dcnt = [0] * N_DMA_SEMS
        self.dnext = 0
        self.dnext2 = [0, 0]
        self.waited = {e: {} for e in self.engs}
        self.n_inst = 0
        self.uid = 0
        self.limit = None
        self.log = None

    def name(self, p):
        self.uid += 1
        return "%s_%d" % (p, self.uid)

    def _wait(self, E, key, c, raw=False):
        if key == E and E == "PE":
            return
        w = self.waited[E]
        if w.get(key, 0) >= c:
            return
        w[key] = c
        if isinstance(key, int):
            self.engs[E].wait_ge(self.dsem[key], c)
        else:
            self.engs[E].wait_ge(self.sem[key], c)

    def _deps(self, E, reads, writes):
        for b in reads:
            if b.w is not None:
                self._wait(E, b.w[0], b.w[1], raw=True)
        for b in writes:
            if b.w is not None:
                self._wait(E, b.w[0], b.w[1])
            for k, c in b.r.items():
                self._wait(E, k, c)

    def _mark(self, tok, reads, writes):
        for b in reads:
            if b.r.get(tok[0], 0) < tok[1]:
                b.r[tok[0]] = tok[1]
        for b in writes:
            b.w = tok
            b.r = {}

    def op(self, E, fn, reads=(), writes=()):
        if self.limit is not None and self.n_inst >= self.limit:
            return None
        self._deps(E, reads, writes)
        if self.log is not None:
            import sys as _sys
            self.log.append((self.n_inst, E, _sys._getframe(1).f_lineno))
        ins = fn(self.engs[E])
        self.cnt[E] += 1
        ins.then_inc(self.sem[E], 1)
        tok = (E, self.cnt[E])
        self._mark(tok, reads, writes)
        self.n_inst += 1
        return tok

    def dma(self, Q, out_ap, in_ap, reads=(), writes=(), **kw):
        if self.limit is not None and self.n_inst >= self.limit:
            return None
        self._deps(Q, reads, writes)
        half = N_DMA_SEMS // 2
        qi = 0 if Q == "SP" else 1
        k = qi * half + self.dnext2[qi]
        self.dnext2[qi] = (self.dnext2[qi] + 1) % half
        if self.dcnt[k] > 0:
            self._wait(Q, k, self.dcnt[k])
        if self.log is not None:
            import sys as _sys
            self.log.append((self.n_inst, "DMA-" + Q, _sys._getframe(1).f_lineno))
        ins = self.engs[Q].dma_start(out=out_ap, in_=in_ap, **kw)
        self.dcnt[k] += 16
        ins.then_inc(self.dsem[k], 16)
        tok = (k, self.dcnt[k])
        self._mark(tok, reads, writes)
        self.n_inst += 1
        return tok

    def barrier(self, engines=("PE", "ACT", "DVE", "POOL", "SP")):
        for E in engines:
            for e2 in ("PE", "ACT", "DVE", "POOL"):
                if self.cnt[e2] > 0:
                    self._wait(E, e2, self.cnt[e2])
            for k in range(N_DMA_SEMS):
                if self.dcnt[k] > 0:
                    self._wait(E, k, self.dcnt[k])


class Scope:
    def __init__(self, kb):
        self.kb = kb
        self.es = ExitStack()

    def sb(self, name, shape, dt):
        t = self.es.enter_context(self.kb.nc.sbuf_tensor(self.kb.name(name), list(shape), dt))
        return t, Buf(name)

    def ps(self, name, shape, dt):
        t = self.es.enter_context(self.kb.nc.psum_tensor(self.kb.name(name), list(shape), dt))
        return t, Buf(name)

    def close(self):
        self.kb.barrier()
        self.es.close()


def bc_row(ap_row, n):
    return ap_row.to_broadcast([128, n])


class Prog:
    def __init__(self, cfg):
        self.cfg = cfg
        self.nc = bass.Bass("TRN2", target_bir_lowering=False)
        self.kb = KB(self.nc)
        self.kb.limit = cfg.get("limit")
        self.I = {}
        self.taps = {}

    def din(self, name, shape, dt=F32):
        self.I[name] = self.nc.dram_tensor(name, list(shape), dt, kind="ExternalInput").ap()
        return self.I[name]

    def dscr(self, name, shape, dt):
        if self.cfg.get("debug"):
            return self.nc.dram_tensor(name, list(shape), dt, kind="ExternalOutput").ap()
        return self.nc.dram_tensor(name, list(shape), dt).ap()

    def tap(self, name, shape, dt=F32):
        self.taps[name] = self.nc.dram_tensor(name, list(shape), dt, kind="ExternalOutput").ap()
        return self.taps[name]

    def declare(self):
        d = self.din
        d("x", [S, D]); d("cT", [128, 8]); d("posT", [128, NT], I32)
        d("ada_w", [2, D, 6 * D]); d("ada_b", [2, 6 * D])
        d("attn_norm", [2, D]); d("ffn_norm", [2, D])
        d("ffn_w_gate", [2, D, FFN]); d("ffn_w_up", [2, D, FFN]); d("ffn_w_down", [2, FFN, D])
        d("sp_w_in", [D, SP_IN]); d("sp_w_out", [D, D])
        d("moba_q_norm", [1, 64]); d("moba_k_norm", [1, 64]); d("nsa_q_norm", [1, 64]); d("nsa_k_norm", [3, 64])
        d("cmp_posT", [64, 2, 32]); d("cmp_w1", [64, 2, 32, 256]); d("cmp_w2", [2, 256, 64])
        d("diff_w_in", [D, 3072]); d("diff_w_out", [D, D])
        d("diff_q_norm", [1, 64]); d("diff_k_norm", [1, 64]); d("diff_lambda", [1, 256]); d("diff_out_norm", [1, 128])
        d("c_ident", [128, 128]); d("c_tri", [128, 128]); d("c_atri", [128, 128])
        d("c_e16", [16, S]); d("c_e64", [64, S]); d("c_ov", [256, 65])
        d("c_t1", [1, 256]); d("c_t2", [1, 256])
        d("c_nfv", [128, NT, 64]); d("c_cst", [128, NT, 64]); d("c_v01", [128, NT, 64])
        self.out = self.nc.dram_tensor("out", [S, D], F32, kind="ExternalOutput").ap()
        self.xmid = self.dscr("xmid", [S, D], F32)
        self.x1s = self.dscr("x1s", [S, D], F32)
        self.FT = self.dscr("FT", [2048, S], BF16)
        self.TM = self.dscr("TM", [S, D], BF16)
        self.OS = self.dscr("OS", [S, D], BF16)
        self.H2T = self.dscr("H2T", [D, S], BF16)

    def setup(self):
        kb, I = self.kb, self.I
        self.G = Scope(kb)
        G = self.G
        self.ident, self.Bident = G.sb("ident", [128, 128], BF16)
        kb.dma("POOL", self.ident[:], I["c_ident"][:, :], writes=[self.Bident])
        self.identF, self.BidentF = G.sb("identF", [128, 128], F32)
        kb.dma("SP", self.identF[:], I["c_ident"][:, :], writes=[self.BidentF])
        self.tri, self.Btri = G.sb("tri", [128, 128], BF16)
        kb.dma("POOL", self.tri[:], I["c_tri"][:, :], writes=[self.Btri])
        self.atri, self.Batri = G.sb("atri", [128, 128], BF16)
        kb.dma("POOL", self.atri[:], I["c_atri"][:, :], writes=[self.Batri])
        self.cosT, self.Bcos = G.sb("cosT", [128, NT, 8], F32)
        self.sinT, self.Bsin = G.sb("sinT", [128, NT, 8], F32)
        self.mod, self.Bmod = G.sb("mod", [128, 6, D], F32)
        self.silc, self.Bsilc = G.sb("silc", [128, 8, 128], F32)
        with_scope = Scope(kb)
        T = with_scope
        pi, Bpi = T.sb("pi", [128, NT], I32)
        pf, Bpf = T.sb("pf", [128, NT], F32)
        ang, Bang = T.sb("ang", [128, NT, 8], F32)
        tmp, Btmp = T.sb("tmp", [128, NT, 8], F32)
        tmp2, Btmp2 = T.sb("tmp2", [128, NT, 8], F32)
        kb.dma("SP", pi[:], I["posT"][:, :], writes=[Bpi])
        kb.op("DVE", lambda e: e.tensor_copy(pf[:], pi[:]), reads=[Bpi], writes=[Bpf])
        inv = (1.0 / (np.float32(500000.0) ** (np.arange(0, 16, 2, dtype=np.float32) / np.float32(16)))).astype(np.float32)
        for j in range(8):
            kb.op("DVE", lambda e, j=j: e.tensor_scalar(ang[:, :, j], pf[:], float(inv[j]), None, ALU.mult),
                  reads=[Bpf], writes=[Bang])
        TWO_PI = float(2 * np.pi)
        MAG = 12582912.0

        def sin_of(dst, Bdst, shift):
            if shift != 0.0:
                kb.op("DVE", lambda e: e.tensor_scalar(tmp2[:], ang[:], shift, None, ALU.add), reads=[Bang], writes=[Btmp2])
                src, Bsrc = tmp2, Btmp2
            else:
                src, Bsrc = ang, Bang
            kb.op("DVE", lambda e: e.tensor_scalar(tmp[:], src[:], 1.0 / TWO_PI, MAG, ALU.mult, ALU.add), reads=[Bsrc], writes=[Btmp])
            kb.op("DVE", lambda e: e.tensor_scalar(tmp[:], tmp[:], -MAG, -TWO_PI, ALU.add, ALU.mult), reads=[Btmp], writes=[Btmp])
            kb.op("DVE", lambda e: e.tensor_tensor(tmp[:], src[:], tmp[:], ALU.add), reads=[Bsrc, Btmp], writes=[Btmp])
            kb.op("DVE", lambda e: e.tensor_scalar(tmp[:], tmp[:], float(np.pi), float(-np.pi), ALU.min, ALU.max), reads=[Btmp], writes=[Btmp])
            kb.op("ACT", lambda e: e.activation(dst[:], tmp[:], AF.Sin), reads=[Btmp], writes=[Bdst])

        sin_of(self.sinT, self.Bsin, 0.0)
        sin_of(self.cosT, self.Bcos, float(np.pi / 2))
        ct, Bct = T.sb("ct", [128, 8], F32)
        kb.dma("SP", ct[:], I["cT"][:, :], writes=[Bct])
        kb.op("ACT", lambda e: e.activation(ct[:], ct[:], AF.Silu), reads=[Bct], writes=[Bct])
        kb.op("DVE", lambda e: e.tensor_copy(self.silc[:], ct[:].unsqueeze(2).to_broadcast([128, 8, 128])),
              reads=[Bct], writes=[self.Bsilc])
        T.close()

    def compute_mod(self, li):
        kb, I = self.kb, self.I
        T = Scope(kb)
        wch = [T.sb("adaw", [128, 8, 512], F32) for _ in range(2)]
        bch = [T.sb("adab", [128, 512], F32) for _ in range(2)]
        pm = [T.ps("pmod", [128, 512], F32) for _ in range(2)]
        nrm, Bnrm = T.sb("nrm", [128, 2, D], F32)
        kb.dma("SP", nrm[:, 0, :], bc_row(I["attn_norm"][li:li + 1, :], D), writes=[Bnrm])
        kb.dma("SP", nrm[:, 1, :], bc_row(I["ffn_norm"][li:li + 1, :], D), writes=[Bnrm])
        wv = I["ada_w"][li].rearrange("(kc p) n -> p kc n", p=128)
        for g in range(12):
            w, Bw = wch[g % 2]
            b, Bb = bch[g % 2]
            p, Bp = pm[g % 2]
            kb.dma("SP", w[:], wv[:, :, g * 512:(g + 1) * 512], writes=[Bw])
            kb.dma("SP", b[:], bc_row(I["ada_b"][li:li + 1, g * 512:(g + 1) * 512], 512), writes=[Bb])
            for kc in range(8):
                kb.op("PE", lambda e, kc=kc: e.matmul(p[:], self.silc[:, kc, :], w[:, kc, :], start=(kc == 0), stop=(kc == 7)),
                      reads=[Bw, self.Bsilc], writes=[Bp])
            sl = self.mod[:, g // 2, (g % 2) * 512:(g % 2 + 1) * 512]
            kb.op("DVE", lambda e: e.tensor_tensor(sl, p[:], b[:], ALU.add), reads=[Bp, Bb], writes=[self.Bmod])
        kb.op("DVE", lambda e: e.scalar_tensor_tensor(self.mod[:, 1, :], self.mod[:, 1, :], 1.0, nrm[:, 0, :], ALU.add, ALU.mult),
              reads=[self.Bmod, Bnrm], writes=[self.Bmod])
        kb.op("DVE", lambda e: e.scalar_tensor_tensor(self.mod[:, 4, :], self.mod[:, 4, :], 1.0, nrm[:, 1, :], ALU.add, ALU.mult),
              reads=[self.Bmod, Bnrm], writes=[self.Bmod])
        T.close()

    def rms_mod_tile(self, T, xt, Bxt, hb, Bhb, gi, si, scr):
        kb = self.kb
        junk, Bjunk, ss, Bss, hn, Bhn = scr
        kb.op("ACT", lambda e: e.activation(junk[:], xt[:], AF.Square, accum_out=ss[:]), reads=[Bxt], writes=[Bjunk, Bss])
        kb.op("ACT", lambda e: e.activation(ss[:], ss[:], AF.Sqrt, bias=self.epsb[:], scale=1.0 / D), reads=[Bss, self.Bepsb], writes=[Bss])
        kb.op("DVE", lambda e: e.reciprocal(ss[:], ss[:]), reads=[Bss], writes=[Bss])
        kb.op("DVE", lambda e: e.scalar_tensor_tensor(hn[:], xt[:], ss[:, 0:1], self.mod[:, gi, :], ALU.mult, ALU.mult),
              reads=[Bxt, Bss, self.Bmod], writes=[Bhn])
        kb.op("POOL", lambda e: e.tensor_tensor(hb[:], hn[:], self.mod[:, si, :], ALU.add), reads=[Bhn, self.Bmod], writes=[Bhb])

    def transpose8(self, src, Bsrc, pt, Bpt, dst_ap, Bdst, eng="ACT"):
        kb = self.kb
        for j in range(8):
            kb.op("PE", lambda e, j=j: e.transpose(pt[:, j * 128:(j + 1) * 128], src[:, j * 128:(j + 1) * 128], self.ident[:]),
                  reads=[Bsrc, self.Bident], writes=[Bpt])
        if eng == "ACT":
            kb.op("ACT", lambda e: e.copy(dst_ap, pt[:].rearrange("p (j t) -> p j t", j=8)), reads=[Bpt], writes=[Bdst])
        else:
            kb.op("DVE", lambda e: e.tensor_copy(dst_ap, pt[:].rearrange("p (j t) -> p j t", j=8)), reads=[Bpt], writes=[Bdst])

    def load_w_bf16(self, dst, Bdst, w_ap, nk):
        for kc in range(nk):
            self.kb.dma("POOL", dst[:, kc, :], w_ap[kc * 128:(kc + 1) * 128, :], writes=[Bdst])

    def phase_a(self, li, x_src):
        kb, I = self.kb, self.I
        T = Scope(kb)
        if li == 1:
            w_ap, ncols = I["diff_w_in"], 3072
            groups = []
            for g in range(6):
                if g < 4:
                    gi = 0 if g < 2 else 1
                    blocks = [("T", g * 512 + b * 128) for b in range(4)]
                    groups.append(dict(c0=g * 512, w=512, qk=[(0, 8)], gain=gi, blocks=blocks, gate=None))
                else:
                    blocks = [("V", (g - 4) * 512 + b * 128) for b in range(4)]
                    groups.append(dict(c0=g * 512, w=512, qk=[], gain=None, blocks=blocks, gate=None))
            gain_srcs = [[(I["diff_q_norm"][0:1, :], 0, 8)], [(I["diff_k_norm"][0:1, :], 0, 8)]]
        else:
            w_ap, ncols = I["sp_w_in"], SP_IN
            kn = I["nsa_k_norm"]
            groups = [
                dict(c0=0, w=512, qk=[(0, 8)], gain=0, blocks=[("T", b * 128) for b in range(4)], gate=None),
                dict(c0=512, w=512, qk=[(0, 8)], gain=1, blocks=[("T", 512 + b * 128) for b in range(4)], gate=None),
                dict(c0=1024, w=512, qk=[], gain=None, blocks=[("V", b * 128) for b in range(4)], gate=None),
                dict(c0=1536, w=512, qk=[(0, 8)], gain=2, blocks=[("T", 1024 + b * 128) for b in range(4)], gate=None),
                dict(c0=2048, w=512, qk=[(0, 2), (4, 6)], gain=3,
                     blocks=[("T", 1536), ("T", 1664), ("T", 1792), ("V", 512)], gate=None),
                dict(c0=2560, w=280, qk=[(0, 2)], gain=4, blocks=[("T", 1920), ("V", 640)], gate=(256, 24)),
            ]
            gain_srcs = [
                [(I["moba_q_norm"][0:1, :], 0, 8)], [(I["moba_k_norm"][0:1, :], 0, 8)], [(I["nsa_q_norm"][0:1, :], 0, 8)],
                [(kn[0:1, :], 0, 2), (kn[1:2, :], 4, 6)], [(kn[2:3, :], 0, 2)],
            ]
        nk = 8
        W, BW = self.w_in_pre
        gains = []
        for gs in gain_srcs:
            gr, Bgr = T.sb("gainrow", [128, 8, 64], F32)
            kb.op("POOL", lambda e: e.memset(gr[:], 1.0), writes=[Bgr])
            for (src, u0, u1) in gs:
                for u in range(u0, u1):
                    kb.dma("SP", gr[:, u, :], bc_row(src, 64), writes=[Bgr])
            gains.append((gr, Bgr))
        NG = len(groups)
        xts = Rot([T.sb("xt", [128, D], F32) for _ in range(3)])
        hbs = Rot([T.sb("hb", [128, D], BF16) for _ in range(2)])
        hTs = Rot([T.sb("hT", [128, 8, 128], BF16) for _ in range(2)])
        junk, Bjunk = T.sb("junk", [128, D], BF16)
        sss = Rot([T.sb("ss", [128, 1], F32) for _ in range(2)])
        hn, Bhn = T.sb("hn", [128, D], F32)
        sqs = [T.sb("sq", [128, 512], F32) for _ in range(NG)]
        ss8l = [T.sb("ss8", [128, 8], F32) for _ in range(NG)]
        qnl = [T.sb("qn", [128, 8, 64], F32) for _ in range(NG)]
        postl = [T.sb("post", [128, 512], BF16) for _ in range(NG)]
        rtl = [T.sb("rt", [128, 4, 8, 8], F32) for _ in range(NG)]
        pts = Rot([T.ps("ptA", [128, 1024], BF16) for _ in range(1)])
        pgl = [T.ps("pgA", [128, 512], F32) for _ in range(NG)]
        pbt, _ = T.ps("ptB", [128, 1024], BF16)
        pbh = Rot([(pbt[:, 0:512], Buf("pb0")), (pbt[:, 512:1024], Buf("pb1"))])
        stages = {}
        for gi_, g in enumerate(groups):
            if any(k == "T" for k, _ in g["blocks"]):
                stages[gi_] = [T.sb("stage", [128, 4, 512], BF16) for _ in range(2)]

        def load_x(t):
            xt, Bxt = xts.next()
            kb.dma("SP", xt[:], x_src[t * 128:(t + 1) * 128, :], writes=[Bxt])
            return xt, Bxt

        def prep(t, xtb):
            xt, Bxt = xtb
            hb, Bhb = hbs.next()
            ss, Bss = sss.next()
            self.rms_mod_tile(T, xt, Bxt, hb, Bhb, 1, 0, (junk, Bjunk, ss, Bss, hn, Bhn))
            hT, BhT = hTs.next()
            pt, Bpt = pts.next()
            self.transpose8(hb, Bhb, pt, Bpt, hT[:], BhT, eng="ACT")
            return hT, BhT

        xq = [load_x(0)]
        if NT > 1:
            xq.append(load_x(1))
        hq = [prep(0, xq.pop(0))]
        for t in range(NT):
            hT, BhT = hq.pop(0)
            if t + 2 < NT:
                xq.append(load_x(t + 2))
            tt = t % 4
            half = (t // 4) % 2
            for gi_, g in enumerate(groups):
                w = g["w"]
                pg, Bpg = pgl[gi_]
                for kc in range(8):
                    kb.op("PE", lambda e, kc=kc, pg=pg, w=w, g=g: e.matmul(pg[:, 0:w], hT[:, kc, :], W[:, kc, g["c0"]:g["c0"] + w],
                                                                          start=(kc == 0), stop=(kc == 7)),
                          reads=[BhT, BW], writes=[Bpg])
            if t + 1 < NT:
                hq.append(prep(t + 1, xq.pop(0)))
            info = []
            for gi_, g in enumerate(groups):
                w = g["w"]
                wq = (w // 64) * 64
                info.append((gi_, g, w, wq, wq // 64))
            for gi_, g, w, wq, nu in info:
                pg, Bpg = pgl[gi_]
                post, Bpost = postl[gi_]
                sq, Bsq = sqs[gi_]
                if g["qk"]:
                    kb.op("ACT", lambda e, sq=sq, pg=pg, wq=wq: e.activation(sq[:, 0:wq], pg[:, 0:wq], AF.Square), reads=[Bpg], writes=[Bsq])
                else:
                    kb.op("ACT", lambda e, post=post, pg=pg, wq=wq: e.copy(post[:, 0:wq], pg[:, 0:wq]), reads=[Bpg], writes=[Bpost])
                if g["gate"] is not None:
                    gc0, gw = g["gate"]
                    kb.op("ACT", lambda e, pg=pg, gc0=gc0, gw=gw: e.activation(self.gate_sb[:, t, :], pg[:, gc0:gc0 + gw], AF.Sigmoid),
                          reads=[Bpg], writes=[self.Bgate])
            for gi_, g, w, wq, nu in info:
                if not g["qk"]:
                    continue
                sq, Bsq = sqs[gi_]
                ss8, Bss8 = ss8l[gi_]
                kb.op("DVE", lambda e, ss8=ss8, sq=sq, wq=wq, nu=nu: e.tensor_reduce(ss8[:, 0:nu], sq[:, 0:wq].rearrange("p (u d) -> p u d", d=64), AX.X, ALU.add),
                      reads=[Bsq], writes=[Bss8])
            for gi_, g, w, wq, nu in info:
                if not g["qk"]:
                    continue
                ss8, Bss8 = ss8l[gi_]
                kb.op("ACT", lambda e, ss8=ss8, nu=nu: e.activation(ss8[:, 0:nu], ss8[:, 0:nu], AF.Sqrt, bias=self.epsb[:], scale=1.0 / 64),
                      reads=[Bss8, self.Bepsb], writes=[Bss8])
            for gi_, g, w, wq, nu in info:
                if not g["qk"]:
                    continue
                pg, Bpg = pgl[gi_]
                ss8, Bss8 = ss8l[gi_]
                qn, Bqn = qnl[gi_]
                gr, Bgr = gains[g["gain"]]
                kb.op("DVE", lambda e, ss8=ss8, nu=nu: e.reciprocal(ss8[:, 0:nu], ss8[:, 0:nu]), reads=[Bss8], writes=[Bss8])
                qk_units = set()
                for (u0, u1) in g["qk"]:
                    qk_units.update(range(u0, u1))
                for u in [u for u in range(nu) if u not in qk_units]:
                    kb.op("DVE", lambda e, u=u, ss8=ss8: e.memset(ss8[:, u:u + 1], 1.0), writes=[Bss8])
                kb.op("DVE", lambda e, qn=qn, pg=pg, ss8=ss8, nu=nu, wq=wq: e.tensor_tensor(
                    qn[:, 0:nu, :], pg[:, 0:wq].rearrange("p (u d) -> p u d", d=64),
                    ss8[:, 0:nu].unsqueeze(2).to_broadcast([128, nu, 64]), ALU.mult),
                      reads=[Bpg, Bss8], writes=[Bqn])
                kb.op("POOL", lambda e, qn=qn, gr=gr, nu=nu: e.tensor_tensor(qn[:, 0:nu, :], qn[:, 0:nu, :], gr[:, 0:nu, :], ALU.mult),
                      reads=[Bqn, Bgr], writes=[Bqn])
            for gi_, g, w, wq, nu in info:
                if not g["qk"]:
                    continue
                qn, Bqn = qnl[gi_]
                post, Bpost = postl[gi_]
                kb.op("ACT", lambda e, post=post, qn=qn, wq=wq, nu=nu: e.copy(post[:, 0:wq], qn[:, 0:nu, :].rearrange("p u d -> p (u d)")),
                      reads=[Bqn], writes=[Bpost])
            for gi_, g, w, wq, nu in info:
                if not g["qk"]:
                    continue
                qn, Bqn = qnl[gi_]
                post, Bpost = postl[gi_]
                rt, Brt = rtl[gi_]
                postv = post[:, 0:wq].rearrange("p (u d) -> p u d", d=64)
                for (u0, u1) in g["qk"]:
                    n_ = u1 - u0
                    cosb = self.cosT[:, t, :].unsqueeze(1).to_broadcast([128, n_, 8])
                    sinb = self.sinT[:, t, :].unsqueeze(1).to_broadcast([128, n_, 8])
                    t1 = qn[:, u0:u1, 0:8]
                    t2 = qn[:, u0:u1, 8:16]
                    kb.op("DVE", lambda e, rt=rt, n_=n_, t1=t1, cosb=cosb: e.tensor_tensor(rt[:, 0, 0:n_, :], t1, cosb, ALU.mult), reads=[Bqn, self.Bcos], writes=[Brt])
                    kb.op("DVE", lambda e, rt=rt, n_=n_, t2=t2, sinb=sinb: e.tensor_tensor(rt[:, 1, 0:n_, :], t2, sinb, ALU.mult), reads=[Bqn, self.Bsin], writes=[Brt])
                    kb.op("DVE", lambda e, rt=rt, n_=n_, t2=t2, cosb=cosb: e.tensor_tensor(rt[:, 2, 0:n_, :], t2, cosb, ALU.mult), reads=[Bqn, self.Bcos], writes=[Brt])
                    kb.op("DVE", lambda e, rt=rt, n_=n_, t1=t1, sinb=sinb: e.tensor_tensor(rt[:, 3, 0:n_, :], t1, sinb, ALU.mult), reads=[Bqn, self.Bsin], writes=[Brt])
                    kb.op("DVE", lambda e, rt=rt, n_=n_, postv=postv, u0=u0, u1=u1: e.tensor_tensor(postv[:, u0:u1, 0:8], rt[:, 0, 0:n_, :], rt[:, 1, 0:n_, :], ALU.subtract),
                          reads=[Brt], writes=[Bpost])
                    kb.op("DVE", lambda e, rt=rt, n_=n_, postv=postv, u0=u0, u1=u1: e.tensor_tensor(postv[:, u0:u1, 8:16], rt[:, 2, 0:n_, :], rt[:, 3, 0:n_, :], ALU.add),
                          reads=[Brt], writes=[Bpost])
            for gi_, g, w, wq, nu in info:
                post, Bpost = postl[gi_]
                tblocks = [(bi, dst) for bi, (k, dst) in enumerate(g["blocks"]) if k == "T"]
                if tblocks:
                    pb, Bpb = pbh.next()
                    st, Bst = stages[gi_][half]
                    for bi, dst in tblocks:
                        kb.op("PE", lambda e, bi=bi, pb=pb, post=post: e.transpose(pb[:, bi * 128:(bi + 1) * 128], post[:, bi * 128:(bi + 1) * 128], self.ident[:]),
                              reads=[Bpost, self.Bident], writes=[Bpb])
                    b0, b1 = tblocks[0][0], tblocks[-1][0] + 1
                    kb.op("ACT", lambda e, st=st, pb=pb, b0=b0, b1=b1: e.copy(st[:, b0:b1, tt * 128:(tt + 1) * 128],
                                                                              pb[:, b0 * 128:b1 * 128].rearrange("p (b t) -> p b t", t=128)),
                          reads=[Bpb], writes=[Bst])
                    if tt == 3:
                        t0 = (t - 3) * 128
                        for bi, dst in tblocks:
                            kb.dma("POOL", self.FT[dst:dst + 128, t0:t0 + 512], st[:, bi, :], reads=[Bst])
                for bi, (k, dst) in enumerate(g["blocks"]):
                    if k == "V":
                        kb.dma("POOL", self.TM[t * 128:(t + 1) * 128, dst:dst + 128], post[:, bi * 128:(bi + 1) * 128], reads=[Bpost])
        T.close()

    def attn_qtile(self, R, q_rhs, Bq, k_lhsT, Bk, v_rhs, Bv, vw, pairs, O, BO, o_off, post_exp=None, bank_of=lambda s: 0, vstat=False, hooks=None, extra_reads=()):
        kb = self.kb
        LOOK = self.cfg.get("look", 5)
        last = {}
        started = set()
        for kc, subs in pairs:
            for s, kind in subs:
                last[s] = kc
        live = {}

        def stage1(i):
            kc, subs = pairs[i]
            pS, BpS = R["pS"].next()
            kl, krows = k_lhsT(kc)
            c0 = min(s_ for s_, _ in subs) * 128
            c1 = (max(s_ for s_, _ in subs) + 1) * 128
            if post_exp is not None:
                c0, c1 = 0, 512
            kb.op("PE", lambda e: e.matmul(pS[0:krows, c0:c1], kl, q_rhs[:, c0:c1], start=True, stop=True), reads=[Bk, Bq] + list(extra_reads), writes=[BpS])
            pT, BpT = R["pT"].next()
            kb.op("ACT", lambda e: e.activation(pT[0:krows, c0:c1], pS[0:krows, c0:c1], AF.Exp, scale=0.125), reads=[BpS], writes=[BpT])
            if post_exp is not None:
                post_exp(kc, pT, BpT, krows)
            for s, kind in subs:
                if kind == "tri":
                    kb.op("DVE", lambda e, s=s: e.tensor_tensor(pT[:, s * 128:(s + 1) * 128], pT[:, s * 128:(s + 1) * 128], self.tri[:], ALU.mult),
                          reads=[BpT, self.Btri], writes=[BpT])
                elif kind == "atri":
                    kb.op("DVE", lambda e, s=s: e.tensor_tensor(pT[:, s * 128:(s + 1) * 128], pT[:, s * 128:(s + 1) * 128], self.atri[:], ALU.mult),
                          reads=[BpT, self.Batri], writes=[BpT])
            live[i] = (pT, BpT, krows)

        def stage2(i):
            kc, subs = pairs[i]
            pT, BpT, krows = live.pop(i)
            if vstat:
                c0 = min(s_ for s_, _ in subs) * 128
                c1 = (max(s_ for s_, _ in subs) + 1) * 128
                st_flag = 0 not in started
                started.add(0)
                kb.op("PE", lambda e: e.matmul(O[0:vw, c0:c1], v_rhs(kc)[0:krows, 0:vw], pT[0:krows, c0:c1],
                                               start=st_flag, stop=(i == len(pairs) - 1), skip_group_check=True),
                      reads=[BpT, Bv], writes=[BO])
                return
            for s, kind in subs:
                bk = bank_of(s)
                st_flag = bk not in started
                started.add(bk)
                kb.op("PE", lambda e, s=s, st_flag=st_flag: e.matmul(o_off(O, s), pT[0:krows, s * 128:(s + 1) * 128], v_rhs(kc)[0:krows, :],
                                                                     start=st_flag, stop=(last[s] == kc), skip_group_check=True),
                      reads=[BpT, Bv], writes=[BO])

        n = len(pairs)
        for i in range(n + LOOK):
            if i < n:
                stage1(i)
            if hooks and i in hooks:
                hooks.pop(i)()
            if i - LOOK >= 0:
                stage2(i - LOOK)
        if hooks:
            for k in sorted(hooks):
                hooks.pop(k)()

    def ot_to_tok(self, OT, BOT, vw, osb, Bosb, Otok, BOtok):
        kb = self.kb
        kb.op("ACT", lambda e: e.copy(osb[0:vw, :], OT[0:vw, :]), reads=[BOT], writes=[Bosb])
        for s in range(4):
            kb.op("PE", lambda e, s=s: e.transpose(Otok[:, s * 128:s * 128 + vw], osb[0:vw, s * 128:(s + 1) * 128], self.identF[0:vw, 0:vw]),
                  reads=[Bosb, self.BidentF], writes=[BOtok])

    @staticmethod
    def causal_pairs(qt):
        pairs = []
        for kc in range(4 * qt + 4):
            j = kc - 4 * qt
            if j < 0:
                pairs.append((kc, [(s, "full") for s in range(4)]))
            else:
                pairs.append((kc, [(s, "tri" if s == j else "full") for s in range(j, 4)]))
        return pairs

    def phase_b_diff(self, li_odd_index):
        kb, I = self.kb, self.I
        T = Scope(kb)
        lam_init = 0.8 - 0.6 * float(np.exp(-0.3 * 1))
        lp, Blp = T.sb("lp", [128, 4, 64], F32)
        kb.dma("SP", lp[:].rearrange("p a d -> p (a d)"), bc_row(I["diff_lambda"][0:1, :], 256), writes=[Blp])
        l2, Bl2 = T.sb("l2", [128, 2, 64], F32)
        kb.op("DVE", lambda e: e.tensor_tensor(l2[:, 0, :], lp[:, 0, :], lp[:, 1, :], ALU.mult), reads=[Blp], writes=[Bl2])
        kb.op("DVE", lambda e: e.tensor_tensor(l2[:, 1, :], lp[:, 2, :], lp[:, 3, :], ALU.mult), reads=[Blp], writes=[Bl2])
        ls, Bls = T.sb("ls", [128, 2], F32)
        kb.op("DVE", lambda e: e.tensor_reduce(ls[:], l2[:], AX.X, ALU.add), reads=[Bl2], writes=[Bls])
        kb.op("ACT", lambda e: e.activation(ls[:], ls[:], AF.Exp), reads=[Bls], writes=[Bls])
        nlam, Bnlam = T.sb("nlam", [128, 1], F32)
        kb.op("DVE", lambda e: e.tensor_tensor(nlam[:], ls[:, 1:2], ls[:, 0:1], ALU.subtract), reads=[Bls], writes=[Bnlam])
        kb.op("DVE", lambda e: e.tensor_scalar(nlam[:], nlam[:], -lam_init, None, ALU.add), reads=[Bnlam], writes=[Bnlam])
        go, Bgo = T.sb("go", [128, 128], F32)
        kb.dma("SP", go[:], bc_row(I["diff_out_norm"][0:1, :], 128), writes=[Bgo])
        kb.op("DVE", lambda e: e.tensor_scalar(go[:], go[:], 1.0 - lam_init, None, ALU.mult), reads=[Bgo], writes=[Bgo])

        KTs = Rot([T.sb("KT", [128, S], BF16) for _ in range(2)])
        QTs = Rot([T.sb("QT", [128, S], BF16) for _ in range(2)])
        Vs = Rot([T.sb("V", [128, NT, 129], BF16) for _ in range(2)])
        for (v, Bv) in Vs.items:
            kb.op("POOL", lambda e, v=v: e.memset(v[:, :, 128:129], 1.0), writes=[Bv])
        R = dict(pS=Rot([T.ps("pS", [128, 512], F32) for _ in range(3)]),
                 pT=Rot([T.sb("pT", [128, 512], BF16) for _ in range(8)]))
        Os = [[T.ps("O", [128, 512], F32) for _ in range(2)] for _ in range(2)]
        osts = Rot([T.sb("ost", [128, 4, 128], BF16) for _ in range(2)])
        osbs = Rot([[[T.sb("osbd", [128, 512], F32) for _ in range(2)] for _ in range(2)] for _ in range(2)])
        pending = []
        a1ts = Rot([T.sb("a1t", [128, 4, 128], F32) for _ in range(2)])
        rdts = Rot([T.sb("rdt", [128, 4, 4], F32) for _ in range(2)])
        a0s = Rot([T.sb("a0", [128, 128], F32) for _ in range(2)])
        a1s = Rot([T.sb("a1", [128, 128], F32) for _ in range(2)])
        rds = Rot([T.sb("rd", [128, 4], F32) for _ in range(4)])
        junk, Bjunk = T.sb("junkb", [128, 128], BF16)

        def load_head(h):
            KT, BKT = KTs.next()
            QT, BQT = QTs.next()
            V, BV = Vs.next()
            kb.dma("SP", QT[:], self.FT[h * 128:(h + 1) * 128, :], writes=[BQT])
            kb.dma("SP", KT[:], self.FT[1024 + h * 128:1024 + (h + 1) * 128, :], writes=[BKT])
            tmv = self.TM[:, h * 128:(h + 1) * 128].rearrange("(c p) d -> p c d", p=128)
            for c4 in range(4):
                kb.dma("SP", V[:, c4 * 8:(c4 + 1) * 8, 0:128], tmv[:, c4 * 8:(c4 + 1) * 8, :], writes=[BV])
            return (KT, BKT, QT, BQT, V, BV)

        nxt = load_head(0)
        for h in range(8):
            KT, BKT, QT, BQT, V, BV = nxt
            if h + 1 < 8:
                nxt = load_head(h + 1)
            for qt in range(8):
                pairs = self.causal_pairs(qt)
                for c in range(2):
                    def o_off(O, s, c=c):
                        return Os[c][s // 2][0][:, (s % 2) * 256:(s % 2) * 256 + 129]
                    hooks = None
                    if c == 0 and pending:
                        e2_, e3_ = pending.pop(0)
                        hooks = {min(6, len(pairs) - 1): e2_, 10 ** 6: e3_}
                    self.attn_qtile(R, QT[64 * c:64 * c + 64, qt * 512:(qt + 1) * 512], BQT,
                                    lambda kc, c=c: (KT[64 * c:64 * c + 64, kc * 128:(kc + 1) * 128], 128), BKT,
                                    lambda kc: V[:, kc, :], BV, 129, pairs, None, Os[c][0][1], o_off, bank_of=lambda s: s // 2, hooks=hooks)
                oset = osbs.next()
                for c in range(2):
                    for b_ in range(2):
                        kb.op("DVE", lambda e, c=c, b_=b_: e.tensor_copy(oset[c][b_][0][:, 0:385], Os[c][b_][0][:, 0:385]),
                              reads=[Os[c][0][1]], writes=[oset[c][b_][1]])

                ep = dict(oset=oset, h=h, qt=qt)
                a1t, Ba1t = a1ts.next()
                rdt, Brdt = rdts.next()
                ep.update(a1t=a1t, Ba1t=Ba1t, rdt=rdt, Brdt=Brdt)

                def e1(ep=ep):
                    oset, a1t, Ba1t, rdt, Brdt = ep["oset"], ep["a1t"], ep["Ba1t"], ep["rdt"], ep["Brdt"]
                    for s in range(4):
                        O0 = oset[0][s // 2][0][:, (s % 2) * 256:(s % 2) * 256 + 129]
                        O1 = oset[1][s // 2][0][:, (s % 2) * 256:(s % 2) * 256 + 129]
                        BO0, BO1 = oset[0][s // 2][1], oset[1][s // 2][1]
                        a0, Ba0 = a0s.next()
                        kb.op("DVE", lambda e, s=s, O0=O0: e.reciprocal(rdt[:, s, 0:1], O0[:, 128:129]), reads=[BO0], writes=[Brdt])
                        kb.op("DVE", lambda e, s=s, O1=O1: e.reciprocal(rdt[:, s, 1:2], O1[:, 128:129]), reads=[BO1], writes=[Brdt])
                        kb.op("DVE", lambda e, s=s: e.tensor_tensor(rdt[:, s, 1:2], rdt[:, s, 1:2], nlam[:], ALU.mult), reads=[Brdt, Bnlam], writes=[Brdt])
                        kb.op("DVE", lambda e, s=s, a0=a0, O0=O0: e.tensor_scalar(a0[:], O0[:, 0:128], rdt[:, s, 0:1], None, ALU.mult), reads=[BO0, Brdt], writes=[Ba0])
                        kb.op("DVE", lambda e, s=s, a0=a0, O1=O1: e.scalar_tensor_tensor(a1t[:, s, :], O1[:, 0:128], rdt[:, s, 1:2], a0[:], ALU.mult, ALU.add),
                              reads=[BO1, Brdt, Ba0], writes=[Ba1t])

                def e2(ep=ep):
                    a1t, Ba1t, rdt, Brdt = ep["a1t"], ep["Ba1t"], ep["rdt"], ep["Brdt"]
                    for s in range(4):
                        kb.op("ACT", lambda e, s=s: e.activation(junk[:], a1t[:, s, :], AF.Square, accum_out=rdt[:, s, 2:3]), reads=[Ba1t], writes=[Bjunk, Brdt])
                    kb.op("ACT", lambda e: e.activation(rdt[:, :, 2:3], rdt[:, :, 2:3], AF.Sqrt, bias=self.epsb[:], scale=1.0 / 128),
                          reads=[Brdt, self.Bepsb], writes=[Brdt])

                def e3(ep=ep):
                    a1t, Ba1t, rdt, Brdt, h, qt = ep["a1t"], ep["Ba1t"], ep["rdt"], ep["Brdt"], ep["h"], ep["qt"]
                    ost, Bost = osts.next()
                    kb.op("DVE", lambda e: e.reciprocal(rdt[:, :, 3:4], rdt[:, :, 2:3]), reads=[Brdt], writes=[Brdt])
                    for s in range(4):
                        kb.op("DVE", lambda e, s=s, ost=ost: e.scalar_tensor_tensor(ost[:, s, :], a1t[:, s, :], rdt[:, s, 3:4], go[:], ALU.mult, ALU.mult),
                              reads=[Ba1t, Brdt, Bgo], writes=[Bost])
                    osv = self.OS[qt * 512:(qt + 1) * 512, h * 128:(h + 1) * 128].rearrange("(s p) d -> p s d", p=128)
                    kb.dma("POOL", osv, ost[:], reads=[Bost])
                e1()
                pending.append((e2, e3))
        while pending:
            e2_, e3_ = pending.pop(0)
            e2_()
            e3_()
        T.close()

    def phase_c1(self, li, x_src, w_out_ap):
        kb, I = self.kb, self.I
        T = Scope(kb)
        W, BW = T.sb("w_out", [128, 8, D], BF16)
        self.load_w_bf16(W, BW, w_out_ap, 8)
        ots = Rot([T.sb("ot", [128, D], BF16) for _ in range(3)])
        xts = Rot([T.sb("xt", [128, D], F32) for _ in range(3)])
        oTs = Rot([T.sb("oT", [128, 8, 128], BF16) for _ in range(2)])
        x1s = Rot([T.sb("x1", [128, D], F32) for _ in range(2)])
        hbs = Rot([T.sb("hb", [128, D], BF16) for _ in range(2)])
        ytmp, Bytmp = T.sb("ytmp", [128, D], F32)
        junk, Bjunk = T.sb("junk", [128, D], BF16)
        sss = Rot([T.sb("ss", [128, 1], F32) for _ in range(2)])
        hn, Bhn = T.sb("hn", [128, D], F32)
        stg = [T.sb("stage", [128, 8, 512], BF16) for _ in range(2)]
        pts = Rot([T.ps("ptA", [128, 1024], BF16) for _ in range(2)])
        pys = Rot([T.ps("py", [128, 512], F32) for _ in range(4)])

        def load(t):
            ot, Bot = ots.next()
            xt, Bxt = xts.next()
            kb.dma("SP", ot[:], self.OS[t * 128:(t + 1) * 128, :], writes=[Bot])
            kb.dma("SP", xt[:], x_src[t * 128:(t + 1) * 128, :], writes=[Bxt])
            return ot, Bot, xt, Bxt

        def front(t, ld):
            ot, Bot, xt, Bxt = ld
            oT, BoT = oTs.next()
            pt, Bpt = pts.next()
            self.transpose8(ot, Bot, pt, Bpt, oT[:], BoT, eng="ACT")
            pyl = []
            for g in range(2):
                py, Bpy = pys.next()
                for kc in range(8):
                    kb.op("PE", lambda e, kc=kc, g=g, py=py: e.matmul(py[:], oT[:, kc, :], W[:, kc, g * 512:(g + 1) * 512], start=(kc == 0), stop=(kc == 7)),
                          reads=[BoT, BW], writes=[Bpy])
                pyl.append((py, Bpy))
            return pyl, xt, Bxt

        def back(t, fr):
            pyl, xt, Bxt = fr
            x1, Bx1 = x1s.next()
            for g in range(2):
                py, Bpy = pyl[g]
                sl = slice(g * 512, (g + 1) * 512)
                kb.op("DVE", lambda e, py=py, sl=sl: e.tensor_tensor(ytmp[:, sl], py[:], self.mod[:, 2, sl], ALU.mult), reads=[Bpy, self.Bmod], writes=[Bytmp])
                kb.op("POOL", lambda e, sl=sl: e.tensor_tensor(x1[:, sl], ytmp[:, sl], xt[:, sl], ALU.add), reads=[Bytmp, Bxt], writes=[Bx1])
            kb.dma("POOL", self.x1s[t * 128:(t + 1) * 128, :], x1[:], reads=[Bx1])
            hb, Bhb = hbs.next()
            ss, Bss = sss.next()
            self.rms_mod_tile(T, x1, Bx1, hb, Bhb, 4, 3, (junk, Bjunk, ss, Bss, hn, Bhn))
            pt, Bpt = pts.next()
            tt = t % 4
            st, Bst = stg[(t // 4) % 2]
            self.transpose8(hb, Bhb, pt, Bpt, st[:, :, tt * 128:(tt + 1) * 128], Bst, eng="ACT")
            if tt == 3:
                t0 = (t - 3) * 128
                h2v = self.H2T.rearrange("(j p) t -> p j t", p=128)
                kb.dma("POOL", h2v[:, :, t0:t0 + 512], st[:], reads=[Bst])

        lds = [load(0)]
        if NT > 1:
            lds.append(load(1))
        fr = front(0, lds.pop(0))
        for t in range(NT):
            if t + 2 < NT:
                lds.append(load(t + 2))
            nfr = front(t + 1, lds.pop(0)) if t + 1 < NT else None
            back(t, fr)
            fr = nfr
        T.close()

    def phase_c2(self, li, x_dst):
        kb, I = self.kb, self.I
        T = Scope(kb)
        Wg, BWg = T.sb("wg", [128, 8, FFN], BF16)
        Wu, BWu = T.sb("wu", [128, 8, FFN], BF16)
        Wd, BWd = T.sb("wd", [128, NJ, D], BF16)
        self.load_w_bf16(Wg, BWg, I["ffn_w_gate"][li], 8)
        self.load_w_bf16(Wu, BWu, I["ffn_w_up"][li], 8)
        self.load_w_bf16(Wd, BWd, I["ffn_w_down"][li], NJ)
        TT = 256
        h2s = Rot([T.sb("h2T", [128, 8, TT], BF16) for _ in range(2)])
        act, Bact = T.sb("act", [128, NJ, TT], BF16)
        sgs = Rot([T.sb("sg", [128, TT], F32) for _ in range(2)])
        xts = Rot([T.sb("x1t", [128, D], F32) for _ in range(2)])
        ytmp, Bytmp = T.sb("ytmp", [128, 512], F32)
        pgs = Rot([T.ps("pg", [128, 512], F32) for _ in range(2)])
        pus = Rot([T.ps("pu", [128, 512], F32) for _ in range(2)])
        pys = Rot([T.ps("py", [128, 512], F32) for _ in range(3)])
        h2v = self.H2T.rearrange("(j p) t -> p j t", p=128)

        def load(st):
            h2, Bh2 = h2s.next()
            kb.dma("SP", h2[:], h2v[:, :, st * TT:(st + 1) * TT], writes=[Bh2])
            return h2, Bh2

        nxt = load(0)
        for st in range(S // TT):
            h2, Bh2 = nxt
            if st + 1 < S // TT:
                nxt = load(st + 1)
            for j in range(NJ):
                pg, Bpg = pgs.next()
                pu, Bpu = pus.next()
                for kc in range(8):
                    kb.op("PE", lambda e, kc=kc: e.matmul(pg[:, 0:TT], Wg[:, kc, j * 128:(j + 1) * 128], h2[:, kc, :], start=(kc == 0), stop=(kc == 7)),
                          reads=[BWg, Bh2], writes=[Bpg])
                for kc in range(8):
                    kb.op("PE", lambda e, kc=kc: e.matmul(pu[:, 0:TT], Wu[:, kc, j * 128:(j + 1) * 128], h2[:, kc, :], start=(kc == 0), stop=(kc == 7)),
                          reads=[BWu, Bh2], writes=[Bpu])
                sg, Bsg = sgs.next()
                kb.op("ACT", lambda e: e.activation(sg[:], pg[:, 0:TT], AF.Silu), reads=[Bpg], writes=[Bsg])
                kb.op("DVE", lambda e, j=j: e.tensor_tensor(act[:, j, :], sg[:], pu[:, 0:TT], ALU.mult), reads=[Bsg, Bpu], writes=[Bact])
            for q in range(TT // 128):
                t = st * (TT // 128) + q
                xt, Bxt = xts.next()
                kb.dma("SP", xt[:], self.x1s[t * 128:(t + 1) * 128, :], writes=[Bxt])
                for g in range(2):
                    py, Bpy = pys.next()
                    for j in range(NJ):
                        kb.op("PE", lambda e, j=j: e.matmul(py[:], act[:, j, q * 128:(q + 1) * 128], Wd[:, j, g * 512:(g + 1) * 512],
                                                            start=(j == 0), stop=(j == NJ - 1)),
                              reads=[Bact, BWd], writes=[Bpy])
                    sl = slice(g * 512, (g + 1) * 512)
                    kb.op("DVE", lambda e: e.tensor_tensor(ytmp[:], py[:], self.mod[:, 5, sl], ALU.mult), reads=[Bpy, self.Bmod], writes=[Bytmp])
                    kb.op("POOL", lambda e: e.tensor_tensor(xt[:, sl], ytmp[:], xt[:, sl], ALU.add), reads=[Bytmp, Bxt], writes=[Bxt])
                kb.dma("POOL", x_dst[t * 128:(t + 1) * 128, :], xt[:], reads=[Bxt])
        T.close()

    def build(self):
        kb = self.kb
        self.declare()
        self.setup()
        G = self.G
        self.epsb, self.Bepsb = G.sb("epsb", [128, 1], F32)
        kb.op("POOL", lambda e: e.memset(self.epsb[:], EPS), writes=[self.Bepsb])
        layers = self.cfg.get("layers", [0, 1])
        x_src = self.I["x"]
        for n, li in enumerate(layers):
            x_dst = self.out if n == len(layers) - 1 else self.xmid
            self.L = Scope(kb)
            if li == 0:
                self.gate_sb, self.Bgate = self.L.sb("gates", [128, NT, 24], F32)
            Wa = Scope(kb)
            w_ap_, ncols_ = (self.I["sp_w_in"], SP_IN) if li == 0 else (self.I["diff_w_in"], 3072)
            Wt, BWt = Wa.sb("w_in", [128, 8, ncols_], BF16)
            self.load_w_bf16(Wt, BWt, w_ap_, 8)
            self.w_in_pre = (Wt, BWt)
            self.compute_mod(li)
            stop = self.cfg.get("stop")
            if stop == "mod":
                Wa.close()
                self.L.close()
                break
            if li == 0:
                self.phase_a(0, x_src)
                Wa.close()
                if stop == "A":
                    self.L.close()
                    break
                if not self.cfg.get("skip_moba"):
                    self.moba_part()
                if stop == "moba":
                    self.L.close()
                    break
                self.nsa_part()
                if stop in ("nsa", "cmpmlp"):
                    self.L.close()
                    break
                self.phase_c1(0, x_src, self.I["sp_w_out"])
            else:
                self.phase_a(1, x_src)
                Wa.close()
                if stop == "A":
                    self.L.close()
                    break
                self.phase_b_diff(0)
                if stop == "B":
                    self.L.close()
                    break
                self.phase_c1(1, x_src, self.I["diff_w_out"])
            if stop == "C1":
                self.L.close()
                break
            self.phase_c2(li, x_dst)
            self.L.close()
            x_src = x_dst
        kb.barrier(engines=("POOL",))
        G.es.close()
        kb.es.close()
        return self.nc

    def epi_norm(self, T, O_ap, BO, vcol, rd_ap, Brd):
        kb = self.kb
        kb.op("DVE", lambda e: e.tensor_scalar(rd_ap, O_ap[:, vcol:vcol + 1], 1e-30, None, ALU.max), reads=[BO], writes=[Brd])
        kb.op("DVE", lambda e: e.reciprocal(rd_ap, rd_ap), reads=[Brd], writes=[Brd])

    def phase_b_sparse(self):
        self.moba_part()
        self.nsa_part()

    def moba_part(self):
        kb, I = self.kb, self.I
        T = Scope(kb)
        QAs = Rot([T.sb("QA", [128, 2, S], BF16) for _ in range(2)])
        KAs = Rot([T.sb("KA", [128, 2, S], BF16) for _ in range(2)])
        Vps = Rot([T.sb("Vp", [128, NT, 2, 65], BF16) for _ in range(2)])
        for (ka, Bka) in KAs.items:
            for hh in range(2):
                kb.dma("POOL", ka[64:80, hh, :], I["c_e16"][:, :], writes=[Bka])
        for (v, Bv) in Vps.items:
            kb.op("POOL", lambda e, v=v: e.memset(v[:, :, :, 64:65], 1.0), writes=[Bv])
        t1, Bt1 = T.sb("t1", [128, 16, 16], F32)
        t2, Bt2 = T.sb("t2", [128, 16, 16], F32)
        kb.dma("SP", t1[:].rearrange("p a b -> p (a b)"), bc_row(I["c_t1"][0:1, :], 256), writes=[Bt1])
        kb.dma("SP", t2[:].rearrange("p a b -> p (a b)"), bc_row(I["c_t2"][0:1, :], 256), writes=[Bt2])
        augs = Rot([T.sb("aug", [128, 128], BF16) for _ in range(2)])
        for (a, Ba) in augs.items:
            kb.op("POOL", lambda e, a=a: e.memset(a[:], 0.0), writes=[Ba])
        km, Bkm = T.sb("km", [64, 16], F32)
        kmb, Bkmb = T.sb("kmb", [64, 16], BF16)
        gms = Rot([T.sb("gm", [128, 16], F32) for _ in range(2)])
        m8s = Rot([T.sb("m8", [128, 8], F32) for _ in range(2)])
        sels = Rot([T.sb("sel", [128, 16], F32) for _ in range(2)])
        R = dict(pS=Rot([T.ps("pS", [128, 512], F32) for _ in range(3)]),
                 pT=Rot([T.sb("pT", [128, 512], BF16) for _ in range(8)]))
        Obs = Rot([T.ps("OTm", [128, 512], F32) for _ in range(2)])
        Otk, BOtk = T.ps("Otok", [128, 512], F32)
        osbs = Rot([T.sb("osb", [128, 512], F32) for _ in range(2)])
        pgs = Rot([T.ps("pgate", [128, 512], F32) for _ in range(1)])
        ptr = Rot([T.ps("ptr", [128, 1024], BF16) for _ in range(1)])
        osts = Rot([T.sb("ost", [128, 4, 128], BF16) for _ in range(2)])
        rds = Rot([T.sb("rd", [128, 1], F32) for _ in range(4)])

        def load_pair(hp):
            QA, BQA = QAs.next()
            KA, BKA = KAs.next()
            Vp, BVp = Vps.next()
            for hh in range(2):
                h = 2 * hp + hh
                kb.dma("SP", QA[0:64, hh, :], self.FT[h * 64:(h + 1) * 64, :], writes=[BQA])
                kb.dma("SP", KA[0:64, hh, :], self.FT[512 + h * 64:512 + (h + 1) * 64, :], writes=[BKA])
            for hh in range(2):
                h = 2 * hp + hh
                tmv = self.TM[:, h * 64:(h + 1) * 64].rearrange("(c p) d -> p c d", p=128)
                for c4 in range(4):
                    kb.dma("SP", Vp[:, c4 * 8:(c4 + 1) * 8, hh, 0:64], tmv[:, c4 * 8:(c4 + 1) * 8, :], writes=[BVp])
            return QA, BQA, KA, BKA, Vp, BVp

        kms = [T.sb("km2", [64, 16], F32) for _ in range(2)]
        kmbs = [T.sb("kmb2", [64, 16], BF16) for _ in range(2)]
        pgt, _ = pgs.items[0]
        pg_slots = Rot([(pgt[:, j * 16:(j + 1) * 16], Buf("pgs%d" % j)) for j in range(8)])
        ptt, _ = ptr.items[0]
        pt_slots = Rot([(ptt[:, j * 128:(j + 1) * 128], Buf("pts%d" % j)) for j in range(8)])
        augs8 = Rot([T.sb("aug8", [128, 128], BF16) for _ in range(8)])
        for (a_, Ba_) in augs8.items:
            kb.op("POOL", lambda e, a_=a_: e.memset(a_[:], 0.0), writes=[Ba_])
        gms8 = Rot([T.sb("gm8", [128, 16], F32) for _ in range(8)])
        m8s8 = Rot([T.sb("m88", [128, 8], F32) for _ in range(8)])
        sels8 = Rot([T.sb("sel8", [128, 16], F32) for _ in range(8)])

        def gating_jobs(QA, BQA, KA, BKA):
            def prep():
                for hh in range(2):
                    km_, Bkm_ = kms[hh]
                    kmb_, Bkmb_ = kmbs[hh]
                    kb.op("DVE", lambda e, hh=hh, km_=km_: e.tensor_reduce(km_[:], KA[0:64, hh, :].rearrange("p (b j) -> p b j", j=256), AX.X, ALU.add),
                          reads=[BKA], writes=[Bkm_])
                    kb.op("DVE", lambda e, km_=km_, kmb_=kmb_: e.tensor_scalar(kmb_[:], km_[:], 1.0 / 256, None, ALU.mult), reads=[Bkm_], writes=[Bkmb_])
            jobs = []
            for hh in range(2):
                for t in range(NT):
                    st = {}

                    def part1(hh=hh, t=t, st=st):
                        own = t // 2
                        kmb_, Bkmb_ = kmbs[hh]
                        pg, Bpg = pg_slots.next()
                        kb.op("PE", lambda e: e.matmul(pg, QA[0:64, hh, t * 128:(t + 1) * 128], kmb_[:], start=True, stop=True),
                              reads=[BQA, Bkmb_], writes=[Bpg])
                        gm, Bgm = gms8.next()
                        m8, Bm8 = m8s8.next()
                        sel, Bsel = sels8.next()
                        aug, Baug = augs8.next()
                        kb.op("DVE", lambda e: e.tensor_tensor(gm[:], pg, t1[:, own, :], ALU.add), reads=[Bpg, Bt1], writes=[Bgm])
                        kb.op("DVE", lambda e: e.max(m8[:], gm[:]), reads=[Bgm], writes=[Bm8])
                        kb.op("DVE", lambda e: e.tensor_scalar(m8[:, 2:3], m8[:, 2:3], -1e29, None, ALU.max), reads=[Bm8], writes=[Bm8])
                        kb.op("DVE", lambda e: e.tensor_scalar(sel[:], gm[:], m8[:, 2:3], None, ALU.is_ge), reads=[Bgm, Bm8], writes=[Bsel])
                        kb.op("DVE", lambda e: e.tensor_tensor(sel[:], sel[:], t2[:, own, :], ALU.max), reads=[Bsel, Bt2], writes=[Bsel])
                        kb.op("DVE", lambda e: e.tensor_scalar(aug[:, 64:80], sel[:], -1.0, -MASKV, ALU.add, ALU.mult), reads=[Bsel], writes=[Baug])
                        st["aug"] = (aug, Baug)

                    def part2(hh=hh, t=t, st=st):
                        aug, Baug = st["aug"]
                        pt, Bpt = pt_slots.next()
                        kb.op("PE", lambda e: e.transpose(pt, aug[:], self.ident[:]), reads=[Baug, self.Bident], writes=[Bpt])
                        kb.op("ACT", lambda e: e.copy(QA[64:80, hh, t * 128:(t + 1) * 128], pt[64:80, :]), reads=[Bpt], writes=[BQA])
                    jobs.append((part1, part2))
            return prep, jobs

        nxt = load_pair(0)
        pending = []
        prep0, jobs0 = gating_jobs(nxt[0], nxt[1], nxt[2], nxt[3])
        prep0()
        for j0 in range(0, len(jobs0), 4):
            for p1, _ in jobs0[j0:j0 + 4]:
                p1()
            for _, p2 in jobs0[j0:j0 + 4]:
                p2()
        for hp in range(4):
            QA, BQA, KA, BKA, Vp, BVp = nxt
            njobs = None
            if hp + 1 < 4:
                nxt = load_pair(hp + 1)
                nprep, njobs = gating_jobs(nxt[0], nxt[1], nxt[2], nxt[3])
            for qt in range(8):
                pairs = self.causal_pairs(qt)
                ost, Bost = osts.next()
                for hh in range(2):
                    OTm, BOTm = Obs.next()
                    hooks = None
                    if njobs is not None:
                        ci = qt * 2 + hh
                        mine = njobs[ci * 4:(ci + 1) * 4]

                        def h1(mine=mine, first=(ci == 0)):
                            if first:
                                nprep()
                            for p1, _ in mine:
                                p1()

                        def h2(mine=mine):
                            for _, p2 in mine:
                                p2()
                        hooks = {0: h1, 6: h2}
                    self.attn_qtile(R, QA[0:80, hh, qt * 512:(qt + 1) * 512], BQA,
                                    lambda kc: (KA[0:80, hh, kc * 128:(kc + 1) * 128], 128), BKA,
                                    lambda kc: Vp[:, kc, hh, :], BVp, 65, pairs, OTm, BOTm, None, vstat=True, hooks=hooks)
                    def epilogue(OTm=OTm, BOTm=BOTm, ost=ost, Bost=Bost, hh=hh, qt=qt, hp=hp):
                        osb, Bosb = osbs.next()
                        self.ot_to_tok(OTm, BOTm, 65, osb, Bosb, Otk, BOtk)
                        Om, BOm = Otk, BOtk
                        for s in range(4):
                            rd, Brd = rds.next()
                            Os_ = Om[:, s * 128:s * 128 + 65]
                            self.epi_norm(T, Os_, BOm, 64, rd[:], Brd)
                            kb.op("DVE", lambda e, s=s, Os_=Os_, rd=rd: e.tensor_scalar(ost[:, s, hh * 64:(hh + 1) * 64], Os_[:, 0:64], rd[:, 0:1], None, ALU.mult),
                                  reads=[BOm, Brd], writes=[Bost])
                        if hh == 1:
                            osv = self.OS[qt * 512:(qt + 1) * 512, hp * 128:(hp + 1) * 128].rearrange("(s p) d -> p s d", p=128)
                            kb.dma("POOL", osv, ost[:], reads=[Bost])
                    if pending:
                        pending.pop()()
                    pending.append(epilogue)
        if pending:
            pending.pop()()
        T.close()

    def nsa_part(self):
        kb, I = self.kb, self.I
        for _ in range(self.cfg.get("pad_dve", 0)):
            kb.op("DVE", lambda e: e.memset(self.epsb[:], EPS), writes=[self.Bepsb])
        P = Scope(kb)
        CKc, BCKc = P.sb("CKc", [64, 2, 256], BF16)
        VCs = [P.sb("VC", [128, 2, 129], BF16) for _ in range(2)]
        kb.op("POOL", lambda e: e.memset(CKc[:], 0.0), writes=[BCKc])
        for g in range(2):
            vc, Bvc = VCs[g]
            for c in range(2):
                kb.dma("POOL", vc[:, c, 64:129], I["c_ov"][c * 128:(c + 1) * 128, :], writes=[Bvc])
        T = Scope(kb)
        CX = [T.sb("CX", [128, S], BF16) for _ in range(2)]
        kb.dma("SP", CX[0][0][:], self.FT[1536:1664, :], writes=[CX[0][1]])
        kb.dma("SP", CX[1][0][:], self.FT[1664:1792, :], writes=[CX[1][1]])
        w1, Bw1 = T.sb("w1", [128, 2 * 32 * 256], BF16)
        w1src = I["cmp_w1"].rearrange("d a l e -> d (a l e)")
        for half in range(2):
            for a in range(2):
                kb.dma("POOL", w1[half * 64:(half + 1) * 64, a * 8192:(a + 1) * 8192], w1src[:, a * 8192:(a + 1) * 8192], writes=[Bw1])
        w1v = w1[:].rearrange("p (a l e) -> p a l e", a=2, l=32)
        posb, Bposb = T.sb("posb", [64, 2, 34], BF16)
        kb.op("POOL", lambda e: e.memset(posb[:], 0.0), writes=[Bposb])
        kb.dma("POOL", posb[:, :, 0:32], I["cmp_posT"][:, :, :], writes=[Bposb])
        w2, Bw2 = T.sb("w2", [128, 2, 2, 64], BF16)
        for kv in range(2):
            for eh in range(2):
                kb.dma("POOL", w2[:, kv, eh, :], I["cmp_w2"][kv, eh * 128:(eh + 1) * 128, :], writes=[Bw2])
        b1, Bb1 = T.sb("b1", [128, 4], F32)
        pbs = Rot([T.ps("pb", [128, 512], F32) for _ in range(2)])
        phs = Rot([T.ps("ph", [128, 512], F32) for _ in range(3)])
        for kv in range(2):
            for eh in range(2):
                pb, Bpb = pbs.next()
                for l in range(32):
                    kb.op("PE", lambda e, l=l: e.matmul(pb[:, 0:2], w1v[0:64, kv, l, eh * 128:(eh + 1) * 128],
                                                        posb[0:64, kv, l:l + 2], start=(l == 0), stop=(l == 31)),
                          reads=[Bw1, Bposb], writes=[Bpb])
                kb.op("DVE", lambda e: e.tensor_copy(b1[:, kv * 2 + eh:kv * 2 + eh + 1], pb[:, 0:1]), reads=[Bpb], writes=[Bb1])
        hid = {}
        for kv in range(2):
            cx, Bcx = CX[kv]
            cxv = cx[:].rearrange("p (n j) -> p n j", j=16)
            for g in range(2):
                for eh in range(2):
                    ph, Bph = phs.next()
                    for l in range(32):
                        a = l // 16
                        kb.op("PE", lambda e, l=l, a=a: e.matmul(ph[:, 0:255], w1v[64 * g:64 * g + 64, kv, l, eh * 128:(eh + 1) * 128],
                                                                 cxv[64 * g:64 * g + 64, a:a + 255, l % 16], start=(l == 0), stop=(l == 31)),
                              reads=[Bw1, Bcx], writes=[Bph])
                    ht, Bht = T.sb("hid", [128, 256], BF16)
                    kb.op("POOL", lambda e: e.memset(ht[:], 0.0), writes=[Bht])
                    kb.op("ACT", lambda e: e.activation(ht[:, 0:255], ph[:, 0:255], AF.Silu, bias=b1[:, kv * 2 + eh:kv * 2 + eh + 1]),
                          reads=[Bph, Bb1], writes=[Bht])
                    hid[(kv, g, eh)] = (ht, Bht)
        for g in range(2):
            ph, Bph = phs.next()
            for eh in range(2):
                ht, Bht = hid[(0, g, eh)]
                kb.op("PE", lambda e: e.matmul(ph[0:64, 0:255], w2[:, 0, eh, :], ht[:, 0:255], start=(eh == 0), stop=(eh == 1)),
                      reads=[Bw2, Bht], writes=[Bph])
            kb.op("ACT", lambda e: e.copy(CKc[:, g, 0:255], ph[0:64, 0:255]), reads=[Bph], writes=[BCKc])
            vc, Bvc = VCs[g]
            for c in range(2):
                ph, Bph = phs.next()
                for eh in range(2):
                    ht, Bht = hid[(1, g, eh)]
                    kb.op("PE", lambda e: e.matmul(ph[:, 0:64], ht[:, c * 128:(c + 1) * 128], w2[:, 1, eh, :], start=(eh == 0), stop=(eh == 1)),
                          reads=[Bw2, Bht], writes=[Bph])
                kb.op("ACT", lambda e: e.copy(vc[:, c, 0:64], ph[:, 0:64]), reads=[Bph], writes=[Bvc])
        T.close()
        if self.cfg.get("stop") == "cmpmlp":
            if self.cfg.get("debug"):
                dck = self.tap("d_ckc", [64, 512], BF16)
                kb.dma("SP", dck[:, :], CKc[:].rearrange("p g n -> p (g n)"), reads=[BCKc])
                for g in range(2):
                    dvc = self.tap("d_vc%d" % g, [128, 258], BF16)
                    kb.dma("SP", dvc[:, :], VCs[g][0][:].rearrange("p c n -> p (c n)"), reads=[VCs[g][1]])
            P.close()
            return

        T = Scope(kb)
        nfv, Bnfv = T.sb("nfv", [128, NT, 64], F32)
        cst, Bcst = T.sb("cst", [128, NT, 64], F32)
        v01, Bv01 = T.sb("v01", [128, NT, 64], F32)
        kb.dma("SP", nfv[:], I["c_nfv"][:, :, :], writes=[Bnfv])
        kb.dma("SP", cst[:], I["c_cst"][:, :, :], writes=[Bcst])
        kb.dma("SP", v01[:], I["c_v01"][:, :, :], writes=[Bv01])
        QA4, BQA4 = T.sb("QA4", [128, 4, S], BF16)
        BQA4m = Buf("qa4_maskrows")
        KS, BKS = T.sb("KS", [128, S], BF16)
        KW, BKW = T.sb("KW", [64, S], BF16)
        VS, BVS = T.sb("VS", [128, NT, 65], BF16)
        VW, BVW = T.sb("VW", [128, NT, 65], BF16)
        kb.op("POOL", lambda e: e.memset(VS[:, :, 64:65], 1.0), writes=[BVS])
        kb.op("POOL", lambda e: e.memset(VW[:, :, 64:65], 1.0), writes=[BVW])
        kb.dma("POOL", KS[64:128, :], I["c_e64"][:, :], writes=[BKS])
        augs = Rot([T.sb("aug", [128, 128], BF16) for _ in range(8)])
        for (a, Ba) in augs.items:
            kb.op("POOL", lambda e, a=a: e.memset(a[:], 0.0), writes=[Ba])
        R = dict(pS=Rot([T.ps("pS", [128, 512], F32) for _ in range(2)]),
                 pT=Rot([T.sb("pT", [128, 512], BF16) for _ in range(8)]))
        Oc = [T.ps("Oc", [128, 512], F32) for _ in range(2)]
        BOc = Oc[0][1]
        Osw = Rot([T.ps("OTsw", [128, 512], F32) for _ in range(2)])
        Otk, BOtk = T.ps("Otok", [128, 512], F32)
        osbs = Rot([T.sb("osb", [128, 512], F32) for _ in range(2)])
        ptr = Rot([T.ps("ptr", [128, 1024], BF16) for _ in range(1)])
        occ, Bocc = T.sb("occ", [128, 4, 4, 64], F32)
        imp, Bimp = T.sb("imp", [128, 4, 64], F32)
        itmp, Bitmp = T.sb("itmp", [128, 64], F32)
        sc, Bsc = T.sb("sc", [128, 64], F32)
        sc2, Bsc2 = T.sb("sc2", [128, 64], F32)
        m8a, Bm8a = T.sb("m8a", [128, 8], F32)
        m8b, Bm8b = T.sb("m8b", [128, 8], F32)
        selm, Bselm = T.sb("selm", [128, 64], F32)
        rds = Rot([T.sb("rd", [128, 2], F32) for _ in range(6)])
        osts = Rot([T.sb("ost", [128, 4, 256], BF16) for _ in range(2)])
        gsb = self.gate_sb
        cmasks = {}
        for c in range(2):
            for qt in range(8):
                if c == 1 and qt < 4:
                    continue
                base = 512 * qt - 2048 * c - 31
                if base - 16 * 127 >= 0:
                    continue
                mk, Bmk = T.sb("cmask", [128, 512], BF16)
                kb.op("POOL", lambda e, mk=mk: e.memset(mk[:], 1.0), writes=[Bmk])
                kb.op("POOL", lambda e, mk=mk, base=base: e.affine_select(mk[:], mk[:], [[1, 512]], ALU.is_ge, 0.0,
                                                                          base=base, channel_multiplier=-16),
                      reads=[Bmk], writes=[Bmk])
                cmasks[(c, qt)] = (mk, Bmk)

        for g in range(2):
            for r in range(4):
                h = 4 * g + r
                kb.dma("SP", QA4[0:64, r, :], self.FT[1024 + h * 64:1024 + (h + 1) * 64, :], writes=[BQA4])
            kb.dma("SP", KS[0:64, :], self.FT[1792 + 64 * g:1792 + 64 * g + 64, :], writes=[BKS])
            kb.dma("SP", KW[:], self.FT[1920 + 64 * g:1920 + 64 * g + 64, :], writes=[BKW])
            tms = self.TM[:, 512 + 64 * g:512 + 64 * g + 64].rearrange("(c p) d -> p c d", p=128)
            tmw = self.TM[:, 640 + 64 * g:640 + 64 * g + 64].rearrange("(c p) d -> p c d", p=128)
            for c4 in range(4):
                kb.dma("SP", VS[:, c4 * 8:(c4 + 1) * 8, 0:64], tms[:, c4 * 8:(c4 + 1) * 8, :], writes=[BVS])
                kb.dma("SP", VW[:, c4 * 8:(c4 + 1) * 8, 0:64], tmw[:, c4 * 8:(c4 + 1) * 8, :], writes=[BVW])
            vc, Bvc = VCs[g]
            parts = self.cfg.get("nsa_parts", ("cmp", "select", "sw"))
            for qt in self.cfg.get("nsa_qts", range(8)):
                ost, Bost = osts.next()
                cchunks = [0] + ([1] if qt >= 4 else [])
                cpairs = [(c, [(s, "full") for s in range(4)]) for c in cchunks]

                def cmp_mask(c, pT, BpT, krows):
                    if (c, qt) in cmasks:
                        mk, Bmk = cmasks[(c, qt)]
                        kb.op("DVE", lambda e: e.tensor_tensor(pT[:], pT[:], mk[:], ALU.mult), reads=[BpT, Bmk], writes=[BpT])

                for r in (range(4) if "cmp" in parts else ()):
                    h = 4 * g + r
                    self.attn_qtile(R, QA4[0:64, r, qt * 512:(qt + 1) * 512], BQA4,
                                    lambda c: (CKc[:, g, c * 128:(c + 1) * 128], 128), BCKc,
                                    lambda c: vc[:, c, :], Bvc, 129, cpairs, None, BOc,
                                    lambda O, s: Oc[s // 2][0][:, (s % 2) * 256:(s % 2) * 256 + 129], post_exp=cmp_mask, bank_of=lambda s: s // 2)
                    for s in range(4):
                        t = 4 * qt + s
                        O_ = Oc[s // 2][0][:, (s % 2) * 256:(s % 2) * 256 + 129]
                        rd, Brd = rds.next()
                        self.epi_norm(T, O_, BOc, 64, rd[:, 0:1], Brd)
                        if r == 0:
                            kb.op("DVE", lambda e, s=s: e.tensor_scalar(imp[:, s, :], O_[:, 65:129], rd[:, 0:1], None, ALU.mult),
                                  reads=[BOc, Brd], writes=[Bimp])
                        else:
                            kb.op("DVE", lambda e: e.tensor_scalar(itmp[:], O_[:, 65:129], rd[:, 0:1], None, ALU.mult),
                                  reads=[BOc, Brd], writes=[Bitmp])
                            kb.op("DVE", lambda e, s=s: e.tensor_tensor(imp[:, s, :], imp[:, s, :], itmp[:], ALU.add),
                                  reads=[Bitmp, Bimp], writes=[Bimp])
                        kb.op("DVE", lambda e: e.tensor_tensor(rd[:, 1:2], rd[:, 0:1], gsb[:, t, h:h + 1], ALU.mult), reads=[Brd, self.Bgate], writes=[Brd])
                        kb.op("DVE", lambda e, s=s, r=r: e.tensor_scalar(occ[:, r, s, :], O_[:, 0:64], rd[:, 1:2], None, ALU.mult),
                              reads=[BOc, Brd], writes=[Bocc])
                sel_fin = []
                for s in (range(4) if "select" in parts else ()):
                    t = 4 * qt + s
                    aug, Baug = augs.next()
                    kb.op("DVE", lambda e: e.tensor_tensor(sc[:], imp[:, s, :], nfv[:, t, :], ALU.mult), reads=[Bimp, Bnfv], writes=[Bsc])
                    kb.op("DVE", lambda e: e.tensor_tensor(sc[:], sc[:], cst[:, t, :], ALU.add), reads=[Bsc, Bcst], writes=[Bsc])
                    kb.op("DVE", lambda e: e.max(m8a[:], sc[:]), reads=[Bsc], writes=[Bm8a])
                    kb.op("DVE", lambda e: e.match_replace(sc2[:], m8a[:], sc[:], -1e30), reads=[Bsc, Bm8a], writes=[Bsc2])
                    kb.op("DVE", lambda e: e.max(m8b[:], sc2[:]), reads=[Bsc2], writes=[Bm8b])
                    kb.op("DVE", lambda e: e.tensor_scalar(selm[:], sc[:], m8b[:, 7:8], None, ALU.is_ge), reads=[Bsc, Bm8b], writes=[Bselm])
                    kb.op("DVE", lambda e: e.tensor_tensor(selm[:], selm[:], v01[:, t, :], ALU.mult), reads=[Bselm, Bv01], writes=[Bselm])
                    kb.op("DVE", lambda e, aug=aug: e.tensor_scalar(aug[:, 64:128], selm[:], -1.0, -MASKV, ALU.add, ALU.mult), reads=[Bselm], writes=[Baug])

                    def fin(aug=aug, Baug=Baug, t=t, s=s):
                        pt, Bpt = ptr.next()
                        kb.op("PE", lambda e: e.transpose(pt[:, s * 128:(s + 1) * 128], aug[:], self.ident[:]), reads=[Baug, self.Bident], writes=[Bpt])
                        for r in range(4):
                            kb.op("ACT", lambda e, r=r: e.copy(QA4[64:128, r, t * 128:(t + 1) * 128], pt[64:128, s * 128:(s + 1) * 128]),
                                  reads=[Bpt], writes=[BQA4m])
                    sel_fin.append(fin)
                pending = []
                spairs = self.causal_pairs(qt)
                wpairs = []
                for kc in range(max(0, 4 * qt - 4), 4 * qt + 4):
                    subs = []
                    for s in range(4):
                        dlt = 4 * qt + s - kc
                        if dlt == 0:
                            subs.append((s, "tri"))
                        elif 1 <= dlt <= 3:
                            subs.append((s, "full"))
                        elif dlt == 4:
                            subs.append((s, "atri"))
                    if subs:
                        wpairs.append((kc, subs))
                branches = ((spairs, 128, KS, BKS, VS, BVS), (wpairs, 64, KW, BKW, VW, BVW))
                for br in ((1, 0) if "sw" in parts else ()):
                    pairs, qrows, kl, Bkl, vt, Bvt = branches[br]
                    for r in range(4):
                        h = 4 * g + r
                        OTb, BOTb = Osw.next()
                        hooks = None
                        if br == 1 and sel_fin:
                            hooks = {(4 if r == 0 else 2): sel_fin[r]}
                        self.attn_qtile(R, QA4[0:qrows, r, qt * 512:(qt + 1) * 512], BQA4,
                                        lambda kc, kl=kl, qrows=qrows: (kl[0:qrows, kc * 128:(kc + 1) * 128], 128), Bkl,
                                        lambda kc, vt=vt: vt[:, kc, :], Bvt, 65, pairs, OTb, BOTb, None, vstat=True,
                                        hooks=hooks, extra_reads=([BQA4m] if br == 0 else []))
                        def epilogue(OTb=OTb, BOTb=BOTb, br=br, r=r, h=h, qt=qt, ost=ost, Bost=Bost):
                            osb, Bosb = osbs.next()
                            self.ot_to_tok(OTb, BOTb, 65, osb, Bosb, Otk, BOtk)
                            Ob, BOb = Otk, BOtk
                            for s in range(4):
                                t = 4 * qt + s
                                O_ = Ob[:, s * 128:s * 128 + 65]
                                rd, Brd = rds.next()
                                self.epi_norm(T, O_, BOb, 64, rd[:, 0:1], Brd)
                                gcol = (br + 1) * 8 + h
                                kb.op("DVE", lambda e, rd=rd, t=t, gcol=gcol: e.tensor_tensor(rd[:, 1:2], rd[:, 0:1], gsb[:, t, gcol:gcol + 1], ALU.mult),
                                      reads=[Brd, self.Bgate], writes=[Brd])
                                if br == 1:
                                    kb.op("DVE", lambda e, O_=O_, rd=rd: e.tensor_scalar(itmp[:], O_[:, 0:64], rd[:, 1:2], None, ALU.mult),
                                          reads=[BOb, Brd], writes=[Bitmp])
                                    kb.op("DVE", lambda e, s=s: e.tensor_tensor(occ[:, r, s, :], occ[:, r, s, :], itmp[:], ALU.add),
                                          reads=[Bitmp, Bocc], writes=[Bocc])
                                else:
                                    kb.op("DVE", lambda e, s=s, O_=O_, rd=rd: e.scalar_tensor_tensor(ost[:, s, r * 64:(r + 1) * 64], O_[:, 0:64], rd[:, 1:2], occ[:, r, s, :], ALU.mult, ALU.add),
                                          reads=[BOb, Brd, Bocc], writes=[Bost])
                        if pending:
                            pending.pop()()
                        pending.append(epilogue)
                if pending:
                    pending.pop()()
                osv = self.OS[qt * 512:(qt + 1) * 512, 512 + g * 256:512 + (g + 1) * 256].rearrange("(s p) d -> p s d", p=128)
                kb.dma("POOL", osv, ost[:], reads=[Bost])
                if self.cfg.get("nsa_barrier"):
                    kb.barrier()
        T.close()
        P.close()


def _consts():
    c = {}
    c["c_ident"] = np.eye(128, dtype=np.float32)
    k = np.arange(128)[:, None]
    q = np.arange(128)[None, :]
    c["c_tri"] = (q >= k).astype(np.float32)
    c["c_atri"] = (k > q).astype(np.float32)
    key = np.arange(S)[None, :]
    c["c_e16"] = (key // 256 == np.arange(16)[:, None]).astype(np.float32)
    c["c_e64"] = (key // 64 == np.arange(64)[:, None]).astype(np.float32)
    ncmp = 255
    cs = np.arange(ncmp) * 16
    ss = np.arange(64) * 64
    ov = np.minimum(cs[:, None] + 32, ss[None, :] + 64) - np.maximum(cs[:, None], ss[None, :])
    ovp = np.zeros((256, 65), np.float32)
    ovp[:255, 0] = 1.0
    ovp[:255, 1:] = np.clip(ov, 0, None) / 32.0
    c["c_ov"] = ovp
    own = np.arange(16)[:, None]
    blk = np.arange(16)[None, :]
    c["c_t1"] = np.where(blk < own, 0.0, -1e30).astype(np.float32).reshape(1, 256)
    c["c_t2"] = (blk == own).astype(np.float32).reshape(1, 256)
    p = np.arange(128)[:, None, None]
    t = np.arange(NT)[None, :, None]
    m = np.arange(64)[None, None, :]
    cur = (t * 128 + p) // 64
    ok = m <= cur
    forced = ok & ((m == 0) | (m >= cur - 1))
    c["c_nfv"] = (ok & ~forced).astype(np.float32)
    c["c_cst"] = np.where(ok, np.where(forced, 1e4 + m, 0.0), -1e30).astype(np.float32)
    c["c_v01"] = ok.astype(np.float32)
    return c


def _core_inputs(b, inp, consts):
    f = lambda a: np.ascontiguousarray(np.asarray(a), dtype=np.float32)
    m = {}
    m["x"] = f(inp["x"][b])
    m["cT"] = f(np.asarray(inp["c"][b]).reshape(8, 128).T)
    m["posT"] = np.ascontiguousarray(np.asarray(inp["positions"][b]).reshape(NT, 128).T.astype(np.int32))
    for k in ("ada_w", "ada_b", "attn_norm", "ffn_norm", "ffn_w_gate", "ffn_w_up", "ffn_w_down"):
        m[k] = f(inp[k])
    m["sp_w_in"] = f(inp["sp_w_in"][0]); m["sp_w_out"] = f(inp["sp_w_out"][0])
    m["moba_q_norm"] = f(inp["moba_q_norm"]); m["moba_k_norm"] = f(inp["moba_k_norm"])
    m["nsa_q_norm"] = f(inp["nsa_q_norm"]); m["nsa_k_norm"] = f(inp["nsa_k_norm"][0])
    m["cmp_posT"] = f(np.asarray(inp["nsa_cmp_pos"][0]).transpose(2, 0, 1))
    m["cmp_w1"] = f(np.asarray(inp["nsa_cmp_w1"][0]).transpose(2, 0, 1, 3))
    m["cmp_w2"] = f(inp["nsa_cmp_w2"][0])
    m["diff_w_in"] = f(inp["diff_w_in"][0]); m["diff_w_out"] = f(inp["diff_w_out"][0])
    m["diff_q_norm"] = f(inp["diff_q_norm"]); m["diff_k_norm"] = f(inp["diff_k_norm"])
    m["diff_lambda"] = f(np.asarray(inp["diff_lambda"][0]).reshape(1, 256))
    m["diff_out_norm"] = f(inp["diff_out_norm"])
    m.update(consts)
    return m


_NC_CACHE = {}


def kernel(**inputs):
    consts = _consts()
    if "nc" not in _NC_CACHE:
        _NC_CACHE["nc"] = Prog({}).build()
    nc = _NC_CACHE["nc"]
    in_maps = [_core_inputs(b, inputs, consts) for b in range(8)]
    res = run_bass_kernel_spmd(nc, in_maps, core_ids=list(range(8)))
    return np.stack([np.asarray(r["out"], dtype=np.float32) for r in res.results], axis=0)
```
